# Optimizing a Trainium2 kernel written in Bass

```python
import math
import jax, jax.numpy as jnp
from jax import lax
import numpy as np

D_MODEL = 1024
BATCH = 8
SEQ = 2048
DEPTH = 2
DEC_BATCH = 128
DEC_SEQ = 4
PAST_LEN = 16384
PAGE_SIZE = 128

MLSTM_WIDTH = D_MODEL // 2
MLSTM_HEADS = 4
MLSTM_DH = MLSTM_WIDTH // MLSTM_HEADS
MLSTM_CHUNK = 64
SGU_WIDTH = D_MODEL // 4
SGU_HEADS = 4
SGU_DH = SGU_WIDTH // SGU_HEADS
SGU_CHUNK = 128
POOL_WIDTH = D_MODEL - MLSTM_WIDTH - SGU_WIDTH
POOL_WINDOWS = (2, 4, 8, 16)
POOL_GROUPS = len(POOL_WINDOWS)
POOL_DG = POOL_WIDTH // POOL_GROUPS
POOL_BUF = max(POOL_WINDOWS) - 1
PEER_HEADS = 8
PEER_NKEYS = 128
PEER_EXPERTS = PEER_NKEYS ** 2
PEER_TOPK = 16
PEER_DQ = 256
PEER_DK = PEER_DQ // 2
PEER_BLOCK = 128
ALPHA = (2 * DEPTH) ** 0.25
BETA = (8 * DEPTH) ** -0.25
LN_EPS = 1e-5
IN_COLS = 4 * MLSTM_WIDTH + 2 * MLSTM_HEADS + 2 * SGU_WIDTH + POOL_WIDTH
IN_SPLITS = [int(s) for s in np.cumsum([MLSTM_WIDTH] * 4 + [2 * MLSTM_HEADS] + [SGU_WIDTH] * 2 + [POOL_WIDTH])[:-1]]

kernel_name = 'hymba_mlstm_sgu_pool_peer_deepnorm_step'


def layer_norm(x, g, b, eps=LN_EPS):
    xf = x.astype(jnp.float32)
    mu = xf.mean(-1, keepdims=True)
    var = jnp.mean(jnp.square(xf - mu), -1, keepdims=True)
    return ((xf - mu) * lax.rsqrt(var + eps) * g + b).astype(x.dtype)


def mlstm_chunk(state, q, k, v, ig, lf):
    C, n, m = state
    L = q.shape[1]
    b = jnp.cumsum(lf, axis=1)
    causal = jnp.tril(jnp.ones((L, L), bool))
    dlog = b[:, :, None, :] - b[:, None, :, :] + ig[:, None, :, :]
    dlog = jnp.where(causal[None, :, :, None], dlog, -jnp.inf)
    inter = b + m[:, None, :]
    m_t = jnp.maximum(inter, dlog.max(axis=2))
    w_intra = jnp.exp(dlog - m_t[:, :, None, :])
    w_inter = jnp.exp(inter - m_t)
    a = w_intra * jnp.einsum('bthd,bshd->btsh', q, k)
    num = jnp.einsum('btsh,bshd->bthd', a, v) + w_inter[..., None] * jnp.einsum('bhvk,bthk->bthv', C, q)
    den = a.sum(2) + w_inter * jnp.einsum('bhk,bthk->bth', n, q)
    h = num / jnp.maximum(jnp.abs(den), jnp.exp(-m_t))[..., None]
    b_end = b[:, -1]
    dend = b_end[:, None, :] - b + ig
    m_new = jnp.maximum(b_end + m, dend.max(1))
    wc = jnp.exp(dend - m_new[:, None, :])
    dec = jnp.exp(b_end + m - m_new)
    C_new = dec[..., None, None] * C + jnp.einsum('bsh,bshv,bshk->bhvk', wc, v, k)
    n_new = dec[..., None] * n + jnp.einsum('bsh,bshk->bhk', wc, k)
    return (C_new, n_new, m_new), h


def mlstm_scan(state, q, k, v, ig, lf):
    B, T, H, Dh = q.shape
    lc = MLSTM_CHUNK if T % MLSTM_CHUNK == 0 else T
    nc = T // lc

    def to_chunks(a):
        return jnp.moveaxis(a.reshape((B, nc, lc) + a.shape[2:]), 1, 0)

    def step(carry, xs):
        return mlstm_chunk(carry, *xs)

    state, h = lax.scan(step, state, tuple(to_chunks(a) for a in (q, k, v, ig, lf)))
    return state, jnp.moveaxis(h, 0, 1).reshape(B, T, H, Dh)


def pool_mix(p_in, buf, pos0, w_pool, pool_scale):
    B, T, C = p_in.shape
    P = buf.shape[1]
    wmax = max(POOL_WINDOWS)
    xcat = jnp.concatenate([buf.astype(p_in.dtype), p_in], axis=1)
    xpad = jnp.concatenate([jnp.zeros((B, wmax, C), jnp.float32), xcat.astype(jnp.float32)], axis=1)
    cs = jnp.cumsum(xpad, axis=1)
    pos = pos0 + jnp.arange(T)
    start = wmax + P
    outs = []
    for gi, w in enumerate(POOL_WINDOWS):
        sl = slice(gi * POOL_DG, (gi + 1) * POOL_DG)
        wsum = cs[:, start:start + T, sl] - cs[:, start - w:start - w + T, sl]
        cnt = jnp.minimum(pos + 1, w).astype(jnp.float32)[None, :, None]
        outs.append(wsum / cnt - xpad[:, start:start + T, sl])
    pooled = jnp.stack(outs, axis=2)
    y = jnp.einsum('btgd,gde->btge', pooled, w_pool.astype(jnp.float32)).reshape(B, T, C) * pool_scale
    return y.astype(p_in.dtype), xcat[:, -POOL_BUF:]


def token_mixers(xm, mstate, pool_buf, pos0, w_in, b_gate, mh_g, sgu_g, sgu_b, w_s, b_s, w_pool, pool_scale, w_o):
    B, T, _ = xm.shape
    f32 = jnp.float32
    proj = jnp.einsum('btd,de->bte', xm, w_in)
    q, k, v, o, gates, u_s, v_s, p_in = jnp.split(proj, IN_SPLITS, axis=-1)
    hd = lambda a: a.reshape(B, T, MLSTM_HEADS, MLSTM_DH).astype(f32)
    gates = gates.astype(f32) + b_gate
    ig = gates[..., :MLSTM_HEADS]
    lf = jax.nn.log_sigmoid(gates[..., MLSTM_HEADS:])
    mstate, h = mlstm_scan(mstate, hd(q) * MLSTM_DH ** -0.5, hd(k), hd(v), ig, lf)
    h = layer_norm(h, mh_g.reshape(MLSTM_HEADS, MLSTM_DH), 0.0).reshape(B, T, MLSTM_WIDTH)
    y_a = (jax.nn.sigmoid(o.astype(f32)) * h).astype(xm.dtype)
    lc = min(T, SGU_CHUNK)
    vn = layer_norm(v_s.reshape(B, T, SGU_HEADS, SGU_DH), sgu_g.reshape(SGU_HEADS, SGU_DH), sgu_b.reshape(SGU_HEADS, SGU_DH))
    ws = jnp.where(jnp.tril(jnp.ones((lc, lc), bool)), w_s[:, :lc, :lc], 0.0)
    vc = vn.reshape(B, T // lc, lc, SGU_HEADS, SGU_DH)
    mix = jnp.einsum('gts,bcsgd->bctgd', ws, vc) + jnp.swapaxes(b_s[:, :lc], 0, 1)[:, :, None]
    y_b = u_s * mix.reshape(B, T, SGU_WIDTH).astype(xm.dtype)
    y_c, pool_buf = pool_mix(p_in, pool_buf, pos0, w_pool, pool_scale)
    y = jnp.einsum('bte,ed->btd', jnp.concatenate([y_a, y_b, y_c], axis=-1), w_o)
    return y, mstate, pool_buf, vn.reshape(B, T, SGU_WIDTH)


def peer(xm, w_pq, peer_keys, peer_u, peer_v):
    B, T, D = xm.shape
    n = B * T
    x2 = xm.reshape(n, D)
    q = jnp.einsum('nd,de->ne', x2, w_pq).reshape(n, PEER_HEADS, 2, PEER_DK)
    s = jnp.einsum('nhpd,hpkd->nhpk', q, peer_keys).astype(jnp.float32)
    s1, i1 = lax.top_k(s[:, :, 0], PEER_TOPK)
    s2, i2 = lax.top_k(s[:, :, 1], PEER_TOPK)
    cand = (s1[..., :, None] + s2[..., None, :]).reshape(n, PEER_HEADS, PEER_TOPK ** 2)
    cidx = (i1[..., :, None] * PEER_NKEYS + i2[..., None, :]).reshape(n, PEER_HEADS, PEER_TOPK ** 2)
    top_s, top_pos = lax.top_k(cand, PEER_TOPK)
    idx = jnp.take_along_axis(cidx, top_pos, axis=-1)
    g = jax.nn.softmax(top_s, axis=-1)
    pad = (-n) % PEER_BLOCK
    nb = (n + pad) // PEER_BLOCK
    xb = jnp.pad(x2, ((0, pad), (0, 0))).reshape(nb, PEER_BLOCK, D)
    ib = jnp.pad(idx, ((0, pad), (0, 0), (0, 0))).reshape(nb, PEER_BLOCK, PEER_HEADS, PEER_TOPK)
    gb = jnp.pad(g, ((0, pad), (0, 0), (0, 0))).reshape(nb, PEER_BLOCK, PEER_HEADS, PEER_TOPK)

    def expert_block(args):
        xk, ik, gk = args
        act = jax.nn.gelu(jnp.einsum('nd,nhkd->nhk', xk, peer_u[ik]).astype(jnp.float32), approximate=False)
        coef = (gk * act).astype(xk.dtype)
        return jnp.einsum('nhk,nhkd->nd', coef, peer_v[ik])

    out = lax.map(expert_block, (xb, ib, gb)).reshape(nb * PEER_BLOCK, D)[:n]
    return out.reshape(B, T, D)


def trunk_layer(x, c, mstate, pool_buf, pos0, w_ada, b_ada, w_in, b_gate, mh_g, sgu_g, sgu_b, w_s, b_s,
                w_pool, pool_scale, w_o, ln1_g, ln1_b, w_pq, peer_keys, peer_u, peer_v, ln2_g, ln2_b):
    ada = jnp.einsum('bd,de->be', jax.nn.silu(c), w_ada) + b_ada
    sh1, sc1, g1, sh2, sc2, g2 = jnp.split(ada[:, None, :], 6, axis=-1)
    y, mstate, pool_buf, v_rows = token_mixers(x * (1 + sc1) + sh1, mstate, pool_buf, pos0, w_in, b_gate, mh_g,
                                               sgu_g, sgu_b, w_s, b_s, w_pool, pool_scale, w_o)
    x = layer_norm(ALPHA * x + g1 * y, ln1_g, ln1_b)
    y = peer(x * (1 + sc2) + sh2, w_pq, peer_keys, peer_u, peer_v)
    x = layer_norm(ALPHA * x + g2 * y, ln2_g, ln2_b)
    return x, mstate, pool_buf, v_rows


def setup_inputs(seed: int = 0) -> dict:
    key = jax.random.key(seed)
    ks = iter(jax.random.split(key, 32))
    f32 = jnp.float32
    nrm = lambda shape, s: jax.random.normal(next(ks), shape, f32) * s
    D = D_MODEL
    H = MLSTM_HEADS
    fbias = jnp.broadcast_to(jnp.linspace(3.0, 6.0, H), (DEPTH, H))
    b_gate = jnp.concatenate([jnp.zeros((DEPTH, H), f32), fbias], axis=-1)
    return {
        'x_prompt': nrm((BATCH, SEQ, D), 1.0),
        'x_sample': nrm((DEC_BATCH, DEC_SEQ, D), 1.0),
        'state_mlstm_C': nrm((DEPTH, DEC_BATCH, H, MLSTM_DH, MLSTM_DH), 0.1),
        'state_mlstm_n': nrm((DEPTH, DEC_BATCH, H, MLSTM_DH), 0.5),
        'state_mlstm_m': nrm((DEPTH, DEC_BATCH, H), 0.5),
        'state_pool': nrm((DEPTH, DEC_BATCH, POOL_BUF, POOL_WIDTH), 1.0),
        'c_prompt': nrm((BATCH, D), 1.0),
        'c_sample': nrm((DEC_BATCH, D), 1.0),
        'w_ada': nrm((DEPTH, D, 6 * D), 0.5 * D ** -0.5),
        'b_ada': nrm((DEPTH, 6 * D), 0.01),
        'w_in': nrm((DEPTH, D, IN_COLS), D ** -0.5),
        'b_gate': b_gate + nrm((DEPTH, 2 * H), 0.1),
        'mh_g': 1.0 + nrm((DEPTH, MLSTM_WIDTH), 0.01),
        'sgu_g': 1.0 + nrm((DEPTH, SGU_WIDTH), 0.01),
        'sgu_b': nrm((DEPTH, SGU_WIDTH), 0.01),
        'w_s': nrm((DEPTH, SGU_HEADS, SGU_CHUNK, SGU_CHUNK), SGU_CHUNK ** -0.5),
        'b_s': 1.0 + nrm((DEPTH, SGU_HEADS, SGU_CHUNK), 0.01),
        'w_pool': nrm((DEPTH, POOL_GROUPS, POOL_DG, POOL_DG), POOL_DG ** -0.5),
        'pool_scale': 1.0 + nrm((DEPTH, POOL_WIDTH), 0.01),
        'w_o': nrm((DEPTH, D, D), BETA * D ** -0.5),
        'ln1_g': 1.0 + nrm((DEPTH, D), 0.01),
        'ln1_b': nrm((DEPTH, D), 0.01),
        'w_pq': nrm((DEPTH, D, PEER_HEADS * PEER_DQ), D ** -0.5),
        'peer_keys': nrm((DEPTH, PEER_HEADS, 2, PEER_NKEYS, PEER_DK), PEER_DK ** -0.5),
        'peer_u': nrm((DEPTH, PEER_EXPERTS, D), D ** -0.5),
        'peer_v': nrm((DEPTH, PEER_EXPERTS, D), BETA * PEER_HEADS ** -0.5),
        'ln2_g': 1.0 + nrm((DEPTH, D), 0.01),
        'ln2_b': nrm((DEPTH, D), 0.01),
    }


def reference(x_prompt, x_sample, state_mlstm_C, state_mlstm_n, state_mlstm_m, state_pool, c_prompt, c_sample,
              w_ada, b_ada, w_in, b_gate, mh_g, sgu_g, sgu_b, w_s, b_s, w_pool, pool_scale, w_o, ln1_g, ln1_b,
              w_pq, peer_keys, peer_u, peer_v, ln2_g, ln2_b):
    f32 = jnp.float32
    Bp = x_prompt.shape[0]
    xp, xs = x_prompt, x_sample
    pC, pn, pm, pp, sC, sn, sm, sp, sv = ([] for _ in range(9))
    for l in range(DEPTH):
        lp = (w_ada[l], b_ada[l], w_in[l], b_gate[l], mh_g[l], sgu_g[l], sgu_b[l], w_s[l], b_s[l], w_pool[l],
              pool_scale[l], w_o[l], ln1_g[l], ln1_b[l], w_pq[l], peer_keys[l], peer_u[l], peer_v[l], ln2_g[l], ln2_b[l])
        st0 = (jnp.zeros((Bp, MLSTM_HEADS, MLSTM_DH, MLSTM_DH), f32),
               jnp.zeros((Bp, MLSTM_HEADS, MLSTM_DH), f32),
               jnp.zeros((Bp, MLSTM_HEADS), f32))
        buf0 = jnp.zeros((Bp, 0, POOL_WIDTH), x_prompt.dtype)
        xp, (C, n, m), buf, _ = trunk_layer(xp, c_prompt, st0, buf0, 0, *lp)
        pC.append(C); pn.append(n); pm.append(m); pp.append(buf)
        st = (state_mlstm_C[l].astype(f32), state_mlstm_n[l].astype(f32), state_mlstm_m[l].astype(f32))
        xs, (C, n, m), buf, v_rows = trunk_layer(xs, c_sample, st, state_pool[l], PAST_LEN, *lp)
        sC.append(C); sn.append(n); sm.append(m); sp.append(buf); sv.append(v_rows)
    return (xp, xs, jnp.stack(pC), jnp.stack(pn), jnp.stack(pm), jnp.stack(pp),
            jnp.stack(sC), jnp.stack(sn), jnp.stack(sm), jnp.stack(sp), jnp.stack(sv))
```

```python
import numpy as np
from contextlib import ExitStack
import concourse.bass as bass
import concourse.mybir as mybir
from concourse.bass_utils import run_bass_kernel_spmd

F32 = mybir.dt.float32
I32 = mybir.dt.int32
U32 = mybir.dt.uint32
ALU = mybir.AluOpType
AF = mybir.ActivationFunctionType
AX = mybir.AxisListType

NCORES = 8
D = 1024
SEQ = 2048
NT = 16
SB = 16
ST = 4
SP = SB * ST
DEPTH = 2
ALPHA = (2 * DEPTH) ** 0.25
LN_EPS = 1e-5
IN_COLS = 2824
NEG = -1.0e30
WCW = 264
NEXP = 16384
SAME_ENGINE_WAITS = True


class TB:
    def __init__(self, name, sem=None):
        self.name = name
        self.last_w = None
        self.reads = []
        self.sem = sem
        self.dma_total = 0
        self.dma_dirty = False


class KB:
    ENG = ("pe", "act", "dve", "pool", "sp")

    def __init__(self, nc, stack):
        self.nc = nc
        self.stack = stack
        self.q = {e: [] for e in self.ENG}
        self.cnt = {e: 0 for e in self.ENG}
        self.esem = {e: stack.enter_context(nc.semaphore("es_" + e)) for e in self.ENG}
        self.seen = {e: {} for e in self.ENG}
        self.semobj = {}
        self._sem_owner = {}
        self.stack0 = stack
        self.phase_tbs = []
        self.sfx = ""

    def new_sem(self, name):
        return self.stack.enter_context(self.nc.semaphore(name + self.sfx))

    def buf(self, name, dma=False):
        tb = TB(name, self.new_sem("d_" + name) if dma else None)
        if dma and self.stack is not self.stack0:
            self.phase_tbs.append(tb)
        return tb

    def end_phase(self):
        for tb in self.phase_tbs:
            k = id(tb.sem)
            self._sem_owner.pop(k, None)
            self.semobj.pop(k, None)
            for e in self.ENG:
                self.seen[e].pop(k, None)
        self.phase_tbs = []

    def sb(self, name, shape, dt=F32):
        return self.stack.enter_context(self.nc.sbuf_tensor(name + self.sfx, list(shape), dt))

    def ps(self, name, shape, dt=F32):
        return self.stack.enter_context(self.nc.psum_tensor(name + self.sfx, list(shape), dt))

    def _deps(self, e, reads, writes):
        deps = {}

        def add(tok):
            if tok is None:
                return
            s, v = tok
            k = id(s)
            self.semobj[k] = s
            ow = self._sem_owner.get(k)
            if ow is not None:
                v = ow.dma_total
            if v > deps.get(k, 0):
                deps[k] = v
        for b in reads:
            add(b.last_w)
        for b in writes:
            add(b.last_w)
            for r in b.reads:
                add(r)
        out = []
        own = id(self.esem[e])
        for k, v in deps.items():
            if k == own and (e in ("pe", "sp") or not SAME_ENGINE_WAITS):
                continue
            if self.seen[e].get(k, 0) >= v:
                continue
            self.seen[e][k] = v
            out.append((self.semobj[k], v))
        return out

    def op(self, e, fn, reads=(), writes=()):
        waits = self._deps(e, reads, writes)
        for s, v in waits:
            tb = self._sem_owner.get(id(s))
            if tb is not None:
                tb.dma_dirty = True
        self.cnt[e] += 1
        tok = (self.esem[e], self.cnt[e])
        self.q[e].append((waits, _bind(fn), tok[0], 1))
        for b in reads:
            b.reads.append(tok)
        for b in writes:
            b.last_w = tok
            b.reads = []
        return tok

    def dma(self, e, fn, owner, reads=(), writes=()):
        self._sem_owner[id(owner.sem)] = owner
        waits = self._deps(e, reads, writes)
        if owner.dma_dirty and owner.dma_total > 0:
            k = id(owner.sem)
            if self.seen[e].get(k, 0) < owner.dma_total:
                self.seen[e][k] = owner.dma_total
                waits.append((owner.sem, owner.dma_total))
            owner.dma_dirty = False
        for s, v in waits:
            tb = self._sem_owner.get(id(s))
            if tb is not None and tb is not owner:
                tb.dma_dirty = True
        owner.dma_total += 16
        tok = (owner.sem, owner.dma_total)
        self.q[e].append((waits, _bind(fn), owner.sem, 16))
        for b in reads:
            b.reads.append(tok)
        for b in writes:
            b.last_w = tok
            b.reads = []
        return tok

    def barrier(self, extra=()):
        toks = [(self.esem[e], self.cnt[e]) for e in self.ENG if self.cnt[e] > 0 and e != "sp"]
        for tb in list(self._sem_owner.values()) + list(extra):
            if tb.dma_total > 0:
                toks.append((tb.sem, tb.dma_total))
        for e in self.ENG:
            waits = []
            for s, v in toks:
                k = id(s)
                if k == id(self.esem[e]):
                    continue
                if self.seen[e].get(k, 0) >= v:
                    continue
                self.seen[e][k] = v
                waits.append((s, v))
            if waits:
                self.q[e].append((waits, None, None, 0))

    def emit(self, final_waits=()):
        nc = self.nc
        engs = {"pe": "tensor", "act": "scalar", "dve": "vector", "pool": "gpsimd", "sp": "sync"}
        with nc.Block() as block:
            for e in self.ENG:
                items = self.q[e]
                fw = list(final_waits) if e == "sp" else []

                def body(eng, items=items, fw=fw):
                    for waits, fn, sem, inc in items:
                        for s, v in waits:
                            eng.wait_ge(s, v)
                        if fn is not None:
                            fn(eng).then_inc(sem, inc)
                    for s, v in fw:
                        eng.wait_ge(s, v)
                getattr(block, engs[e])(body)
        self.q = {e: [] for e in self.ENG}


class _Rec:
    def __init__(self):
        self.call = None

    def __getattr__(self, name):
        def f(*a, **k):
            self.call = (name, a, k)
            return self
        return f


def _bind(fn):
    r = _Rec()
    fn(r)
    assert r.call is not None
    name, a, k = r.call
    return lambda eng: getattr(eng, name)(*a, **k)


class Tn:
    def __init__(self, kb, name, shape, dt=F32, psum=False, dma=False):
        self.t = kb.ps(name, shape, dt) if psum else kb.sb(name, shape, dt)
        self.b = kb.buf(name, dma=dma)

    def __getitem__(self, k):
        return self.t[k]


def _consts():
    c = {}
    i128 = np.arange(128)
    c["ident"] = np.eye(128, dtype=np.float32)
    c["ones"] = np.ones((128, 128), np.float32)
    c["triu"] = (i128[:, None] <= i128[None, :]).astype(np.float32)
    c["negm"] = np.where(i128[None, :] <= i128[:, None], 0.0, NEG).astype(np.float32)
    sel = np.zeros((128, 128), np.float32); sel[127, :] = 1.0
    c["sel127"] = sel
    p = np.arange(SP); tt = p // SB; bb = p % SB
    sameb = bb[:, None] == bb[None, :]
    tri_s = (sameb & (tt[:, None] <= tt[None, :])).astype(np.float32)
    c["tri_s"] = _pad(tri_s)
    c["negm_s"] = _pad(np.where(sameb & (tt[None, :] <= tt[:, None]), 0.0, NEG).astype(np.float32))
    c["negb_s"] = _pad(np.where(sameb, 0.0, NEG).astype(np.float32))
    c["selend"] = _pad(((tt[:, None] == ST - 1) & sameb).astype(np.float32))
    oh = (bb[:, None] == np.arange(SB)[None, :]).astype(np.float32)
    c["onehotB"] = _pad(oh, cols=16)
    oh0 = ((p[:, None] == np.arange(SB)[None, :])).astype(np.float32)
    c["onehot0"] = _pad(oh0, cols=16)
    c["iota16"] = np.broadcast_to(np.arange(16, dtype=np.float32), (128, 16)).copy()
    wins = (2, 4, 8, 16)
    bc0 = np.zeros((4, 128, 128), np.float32); bc = np.zeros((4, 128, 128), np.float32)
    bp = np.zeros((4, 128, 128), np.float32)
    for g, w in enumerate(wins):
        for t in range(128):
            for j in range(w):
                s = t - j
                if s >= 0:
                    bc[g, s, t] += 1.0 / w
                    bc0[g, s, t] += 1.0 / min(t + 1, w)
                else:
                    bp[g, s + 128, t] += 1.0 / w
            bc[g, t, t] -= 1.0
            bc0[g, t, t] -= 1.0
    c["bandc0"] = bc0.transpose(1, 0, 2).reshape(128, 512)
    c["bandc"] = bc.transpose(1, 0, 2).reshape(128, 512)
    c["bandp"] = bp.transpose(1, 0, 2).reshape(128, 512)
    bsA = np.zeros((4, 128, SP), np.float32); bsB = np.zeros((4, 128, SP), np.float32)
    bsC = np.zeros((4, 128, SP), np.float32)
    for g, w in enumerate(wins):
        for t in range(ST):
            for b in range(SB):
                col = t * SB + b
                for j in range(w):
                    r = 15 + t - j
                    if r >= 15:
                        bsC[g, (r - 15) * SB + b, col] += 1.0 / w
                    elif r >= 8:
                        bsB[g, (r - 8) * SB + b, col] += 1.0 / w
                    else:
                        bsA[g, r * SB + b, col] += 1.0 / w
                bsC[g, t * SB + b, col] -= 1.0
    c["bsA"] = bsA.transpose(1, 0, 2).reshape(128, 4 * SP)
    c["bsB"] = bsB.transpose(1, 0, 2).reshape(128, 4 * SP)
    c["bsC"] = bsC.transpose(1, 0, 2).reshape(128, 4 * SP)
    return c


def _pad(a, cols=None):
    out = np.zeros((128, a.shape[1] if cols is None else cols), np.float32)
    out[: a.shape[0], : a.shape[1]] = a
    return out


_CONST_ORDER = ["ident", "ones", "triu", "negm", "sel127", "tri_s", "negm_s", "negb_s", "selend",
                "onehotB", "onehot0", "iota16", "bandc0", "bandc", "bandp", "bsA", "bsB", "bsC"]


def _const_pack():
    c = _consts()
    offs = {}
    o = 0
    arrs = []
    for k in _CONST_ORDER:
        offs[k] = (o, c[k].shape[1])
        o += c[k].shape[1]
        arrs.append(c[k])
    return np.ascontiguousarray(np.concatenate(arrs, axis=1)), offs


def build_program(n_layers=DEPTH, do_phase2=True):
    cpack, coff = _const_pack()
    NCST = cpack.shape[1]
    nc = bass.Bass("TRN2", target_bir_lowering=False)

    def din(name, shape, dt=F32):
        return nc.dram_tensor(name, list(shape), dt, kind="ExternalInput").ap()

    def dout(name, shape, dt=F32):
        return nc.dram_tensor(name, list(shape), dt, kind="ExternalOutput").ap()

    xp = din("xp", [SEQ, D]); xs = din("xs", [SP, D])
    cp = din("cp", [128, D]); cs = din("cs", [SP, D])
    sC = din("sC", [DEPTH, 4, 128, SB, 128]); snat = din("snat", [DEPTH, SB, 4, 128])
    snT = din("snT", [DEPTH, 4, 128, SB]); sm = din("sm", [DEPTH, SP, 4])
    spA = din("spA", [DEPTH, 128, 256]); spB = din("spB", [DEPTH, 112, 256])
    w_ada = din("w_ada", [DEPTH, D, 6 * D]); b_ada = din("b_ada", [DEPTH, 128, 6 * D])
    w_in = din("w_in", [DEPTH, D, IN_COLS]); b_gate = din("b_gate", [DEPTH, 128, 8])
    mh_g = din("mh_g", [DEPTH, 128, 512]); sgu_g = din("sgu_g", [DEPTH, 128, 256])
    sgu_b = din("sgu_b", [DEPTH, 128, 256]); pscale = din("pscale", [DEPTH, 128, 256])
    w_sT = din("w_sT", [DEPTH, 128, 4, 128]); b_sT = din("b_sT", [DEPTH, 128, 4])
    w_sS = din("w_sS", [DEPTH, SP, 4, SP]); b_sS = din("b_sS", [DEPTH, SP, 4])
    w_pool = din("w_pool", [DEPTH, 64, 4, 64]); w_o = din("w_o", [DEPTH, D, D])
    ln1g = din("ln1g", [DEPTH, 128, D]); ln1b = din("ln1b", [DEPTH, 128, D])
    ln2g = din("ln2g", [DEPTH, 128, D]); ln2b = din("ln2b", [DEPTH, 128, D])
    w_pq = din("w_pq", [DEPTH, D, 2048]); keysT = din("keysT", [DEPTH, 128, 16, 128])
    pu = [din("pu%d" % l, [NEXP, D]) for l in range(DEPTH)]
    pv = [din("pv%d" % l, [NEXP, D]) for l in range(DEPTH)]
    cst_d = din("cst", [128, NCST])

    yp = dout("yp", [SEQ, D]); ys = dout("ys", [SP, D])
    o_pC = dout("pC", [DEPTH, 4, 128, 128]); o_pn = dout("pn", [DEPTH, 4, 128]); o_pm = dout("pm", [DEPTH, 4])
    o_pp = dout("pp", [DEPTH, 15, 256])
    o_nC = dout("nC", [DEPTH, SB, 4, 128, 128]); o_nn = dout("nn", [DEPTH, SB, 4, 128])
    o_nm = dout("nm", [DEPTH, SB, 4]); o_np = dout("npool", [DEPTH, SB, 15, 256])
    o_nv = dout("nv", [DEPTH, SB, ST, 256])

    with ExitStack() as st0:
        kb = KB(nc, st0)
        op = kb.op
        OUT = kb.buf("outs", dma=True)

        def out_dma(dst, src, reads):
            kb.dma("sp", lambda q: q.dma_start(out=dst, in_=src), OUT, reads=reads)

        X = kb.sb("X", [128, NT + 1, D])
        XB = [kb.buf("X%d" % t) for t in range(NT + 1)]
        XL = kb.buf("xload", dma=True)
        CST = Tn(kb, "CST", [128, NCST], dma=True)

        def C(name, P=128, w=None):
            o, n = coff[name]
            return CST[:P, o:o + (n if w is None else w)]

        def Cg(name, g, P, blk, w):
            o, n = coff[name]
            return CST[:P, o + g * blk: o + g * blk + w]

        with nc.allow_non_contiguous_dma(reason="small strided state/param loads"):
            kb.dma("sp", lambda q: q.dma_start(out=CST[:, :], in_=cst_d), CST.b, writes=[CST.b])
            for t in range(NT):
                kb.dma("sp", lambda q, t=t: q.dma_start(out=X[:, t, :], in_=xp[t * 128:(t + 1) * 128, :]),
                       XL, writes=[XB[t]])
            kb.dma("sp", lambda q: q.dma_start(out=X[:SP, NT, :], in_=xs), XL, writes=[XB[NT]])

            for l in range(n_layers):
                with ExitStack() as st1:
                    kb.stack = st1
                    kb.sfx = "_a%d" % l
                    _phase1(nc, kb, l, locals())
                    kb.barrier(extra=[OUT])
                    kb.emit()
                    kb.end_phase()
                if do_phase2:
                    with ExitStack() as st2:
                        kb.stack = st2
                        kb.sfx = "_b%d" % l
                        _phase2(nc, kb, l, locals())
                        kb.barrier(extra=[OUT])
                        kb.emit()
                        kb.end_phase()
            kb.stack = st0
            kb.sfx = ""
            for t in range(NT):
                out_dma(yp[t * 128:(t + 1) * 128, :], X[:, t, :], [XB[t]])
            out_dma(ys, X[:SP, NT, :], [XB[NT]])
            kb.emit(final_waits=[(OUT.sem, OUT.dma_total)])
    return nc, cpack


def _ada(nc, kb, l, E, ADA, P, csrc, off, WCH, PM, hbuf, hT, PT, badac):
    op = kb.op
    C = E["C"]
    w_ada, b_ada = E["w_ada"], E["b_ada"]
    kb.dma("sp", lambda q: q.dma_start(out=hbuf[:P, :], in_=csrc), hbuf.b, writes=[hbuf.b])
    op("act", lambda e: e.activation(out=hbuf[:P, :], in_=hbuf[:P, :], func=AF.Silu), reads=[hbuf.b], writes=[hbuf.b])
    _transpose8(kb, E, hbuf, hT, PT, P)
    wv = w_ada[l].rearrange("(k p) n -> p k n", p=128)
    for c in range(12):
        i = c % 2
        c0 = off + c * 256
        kb.dma("sp", lambda q, i=i, c0=c0: q.dma_start(out=WCH[i][:, :, 0:256], in_=wv[:, :, c0:c0 + 256]),
               WCH[i].b, writes=[WCH[i].b])
        kb.dma("sp", lambda q, i=i, c0=c0: q.dma_start(out=badac[i][:P, :], in_=b_ada[l, :P, c0:c0 + 256]),
               badac[i].b, writes=[badac[i].b])
        for k in range(8):
            op("pe", lambda e, i=i, k=k: e.matmul(PM[i][:P, 0:256], lhsT=hT[:, k, :P], rhs=WCH[i][:, k, 0:256],
                                                   start=(k == 0), stop=(k == 7)),
               reads=[hT.b, WCH[i].b], writes=[PM[i].b] if k in (0, 7) else [])
        add1 = 1.0 if 4 <= c < 8 else 0.0
        op("dve", lambda e, i=i, c=c, add1=add1: e.scalar_tensor_tensor(
            out=ADA[:P, c * 256:(c + 1) * 256], in0=PM[i][:P, 0:256], scalar=add1, in1=badac[i][:P, :],
            op0=ALU.add, op1=ALU.add), reads=[PM[i].b, badac[i].b], writes=[ADA.b])


def _transpose8(kb, E, src, dstT, PT, P, srcb=None):
    op = kb.op
    C = E["C"]
    sb_ = src.b if srcb is None else srcb
    for half in range(2):
        for j in range(4):
            k = half * 4 + j
            op("pe", lambda e, half=half, j=j, k=k: e.transpose(
                out=PT[half][:, j * 128:j * 128 + P], in_=src[:P, k * 128:(k + 1) * 128], identity=C("ident", P, P)),
               reads=[sb_, E["CST"].b], writes=[PT[half].b])
        op("act", lambda e, half=half: e.copy(
            out=dstT[:, half * 4:half * 4 + 4, :P],
            in_=PT[half][:, :].rearrange("p (j c) -> p j c", j=4)[:, :, :P]),
           reads=[PT[half].b], writes=[dstT.b])


def _phase1(nc, kb, l, E):
    op = kb.op
    C, Cg, CST, X, XB = E["C"], E["Cg"], E["CST"], E["X"], E["XB"]
    out_dma = E["out_dma"]
    w_in, w_o = E["w_in"], E["w_o"]

    ADA = Tn(kb, "ADA1", [128, 3072]); ADAs = ADA
    WCH = [Tn(kb, "WCH%d" % i, [128, 8, WCW], dma=True) for i in range(2)]
    badac = [Tn(kb, "bada%d" % i, [128, 256], dma=True) for i in range(2)]
    H = Tn(kb, "H", [128, D], dma=True); HT = Tn(kb, "HT", [128, 8, 128])
    PROJ = Tn(kb, "PROJ", [128, IN_COLS], dma=True)
    Y = Tn(kb, "Y", [128, D])
    PRM = Tn(kb, "PRM", [128, 8 + 512 + 256 * 3 + 2 * D], dma=True)
    WS = Tn(kb, "WS", [128, 4, 128], dma=True); BS = Tn(kb, "BS", [128, 4], dma=True)
    WSs = Tn(kb, "WSs", [128, 4, SP], dma=True); BSs = Tn(kb, "BSs", [128, 4], dma=True)
    WP = Tn(kb, "WP", [64, 4, 64], dma=True)
    PT = [Tn(kb, "PT%d" % i, [128, 512], psum=True) for i in range(2)]
    PM = [Tn(kb, "PM%d" % i, [128, 512], psum=True) for i in range(2)]
    PA = Tn(kb, "PA", [128, 512], psum=True); PB = Tn(kb, "PB", [128, 512], psum=True)
    PC = Tn(kb, "PC", [128, 512], psum=True); PD = Tn(kb, "PD", [128, 512], psum=True)
    SM = Tn(kb, "SM", [128, 64])
    SMs = Tn(kb, "SMs", [128, 4], dma=True)
    MREP = Tn(kb, "MREP", [128, 4])
    CTX = Tn(kb, "CTX", [128, 4, 129], dma=True)
    DG = Tn(kb, "DG", [128, 128]); DL = Tn(kb, "DL", [128, 128]); WI = Tn(kb, "WI", [128, 128])
    AM = Tn(kb, "AM", [128, 128]); AT = Tn(kb, "AT", [128, 128])
    QT = Tn(kb, "QT", [128, 128]); KT = Tn(kb, "KT", [128, 128])
    VX = Tn(kb, "VX", [128, 129]); TOT = Tn(kb, "TOT", [128, 129]); WV = Tn(kb, "WV", [128, 129])
    HN = Tn(kb, "HN", [128, 128]); SG = Tn(kb, "SG", [128, 128]); ST6 = Tn(kb, "ST6", [128, 2, 6])
    OUTC = Tn(kb, "OUTC", [128, 128], dma=True)
    CN = Tn(kb, "CN", [128, SB, 128], dma=True); CTS = Tn(kb, "CTS", [128, SB, 129])
    ZQ = Tn(kb, "ZQ", [128, SB * SP]); RA = Tn(kb, "RA", [128, SB, 128])
    NNAT = Tn(kb, "NNAT", [SB, 4, 128], dma=True); NTH = Tn(kb, "NTH", [128, SB], dma=True)
    WCB = Tn(kb, "WCB", [128, 16]); DECD = Tn(kb, "DECD", [128, 16]); DECR = Tn(kb, "DECR", [128, 16])
    MSO = Tn(kb, "MSO", [SB, 4], dma=True)
    PREV = Tn(kb, "PREV", [128, 256]); PTT = Tn(kb, "PTT", [64, 4, 128])
    SPA = Tn(kb, "SPA", [128, 256], dma=True); SPB = Tn(kb, "SPB", [128, 256], dma=True)
    VN = Tn(kb, "VN", [128, 256], dma=True); VTMP = Tn(kb, "VTMP", [128, 256])

    o_bg, o_mh, o_sg, o_sb, o_ps, o_l1g, o_l1b = 0, 8, 520, 776, 1032, 1288, 1288 + D
    for (o, w, src) in [(o_bg, 8, E["b_gate"]), (o_mh, 512, E["mh_g"]), (o_sg, 256, E["sgu_g"]), (o_sb, 256, E["sgu_b"]),
                        (o_ps, 256, E["pscale"]), (o_l1g, D, E["ln1g"]), (o_l1b, D, E["ln1b"])]:
        kb.dma("sp", lambda q, o=o, w=w, src=src: q.dma_start(out=PRM[:, o:o + w], in_=src[l]), PRM.b, writes=[PRM.b])
    kb.dma("sp", lambda q: q.dma_start(out=WS[:, :, :], in_=E["w_sT"][l]), WS.b, writes=[WS.b])
    kb.dma("sp", lambda q: q.dma_start(out=BS[:, :], in_=E["b_sT"][l]), BS.b, writes=[BS.b])
    kb.dma("sp", lambda q: q.dma_start(out=WSs[:SP, :, :], in_=E["w_sS"][l]), WSs.b, writes=[WSs.b])
    kb.dma("sp", lambda q: q.dma_start(out=BSs[:SP, :], in_=E["b_sS"][l]), BSs.b, writes=[BSs.b])
    kb.dma("sp", lambda q: q.dma_start(out=WP[:, :, :], in_=E["w_pool"][l]), WP.b, writes=[WP.b])
    for g in range(4):
        op("dve", lambda e, g=g: e.tensor_tensor(out=WS[:, g, :], in0=WS[:, g, :], in1=C("triu"), op=ALU.mult),
           reads=[WS.b, CST.b], writes=[WS.b])
        op("dve", lambda e, g=g: e.tensor_tensor(out=WSs[:SP, g, :], in0=WSs[:SP, g, :], in1=C("tri_s", SP, SP), op=ALU.mult),
           reads=[WSs.b, CST.b], writes=[WSs.b])
    op("dve", lambda e: e.memset(CTX[:, :, :], 0.0), writes=[CTX.b])
    op("dve", lambda e: e.memset(MREP[:, :], 0.0), writes=[MREP.b])
    op("dve", lambda e: e.memset(VX[:, :], 1.0), writes=[VX.b])

    _ada(nc, kb, l, E, ADA, 128, E["cp"], 0, WCH, PM, H, HT, PT, badac)

    w_in_v = w_in[l].rearrange("(k p) n -> p k n", p=128)
    w_o_v = w_o[l].rearrange("(k p) n -> p k n", p=128)
    chunks = [(i * 256, 256) for i in range(10)] + [(2560, 264)]
    wctr = [0]

    def stream_mm(wview, c0, w, lhsT, P, evac):
        i = wctr[0] % 2
        wctr[0] += 1
        kb.dma("sp", lambda q: q.dma_start(out=WCH[i][:, :, 0:w], in_=wview[:, :, c0:c0 + w]), WCH[i].b, writes=[WCH[i].b])
        for k in range(8):
            op("pe", lambda e, k=k: e.matmul(PM[i][:P, 0:w], lhsT=lhsT[:, k, :P], rhs=WCH[i][:, k, 0:w],
                                              start=(k == 0), stop=(k == 7)),
               reads=[lhsT.b, WCH[i].b], writes=[PM[i].b] if k in (0, 7) else [])
        evac(PM[i], i)

    for t in range(NT + 1):
        is_s = (t == NT)
        P = SP if is_s else 128
        ada = ADAs if is_s else ADA
        if is_s:
            _ada(nc, kb, l, E, ADAs, SP, E["cs"], 0, WCH, PM, H, HT, PT, badac)
        xt = X[:P, t, :]
        op("dve", lambda e: e.tensor_tensor(out=H[:P, :], in0=xt, in1=ada[:P, 1024:2048], op=ALU.mult),
           reads=[XB[t], ada.b], writes=[H.b])
        op("dve", lambda e: e.tensor_tensor(out=H[:P, :], in0=H[:P, :], in1=ada[:P, 0:1024], op=ALU.add),
           reads=[H.b, ada.b], writes=[H.b])
        _transpose8(kb, E, H, HT, PT, P)
        for (c0, w) in chunks:
            stream_mm(w_in_v, c0, w, HT, P,
                      lambda pm, i, c0=c0, w=w: op("act", lambda e: e.copy(out=PROJ[:P, c0:c0 + w], in_=pm[:P, 0:w]),
                                                   reads=[pm.b], writes=[PROJ.b]))
        tri = C("tri_s", SP, SP) if is_s else C("triu")
        negm = C("negm_s", SP, SP) if is_s else C("negm")
        selE = C("selend", SP, SP) if is_s else C("sel127")
        if is_s:
            kb.dma("sp", lambda q: q.dma_start(out=SMs[:SP, :], in_=E["sm"][l]), SMs.b, writes=[SMs.b])
            kb.dma("sp", lambda q: q.dma_start(out=NNAT[:, :, :], in_=E["snat"][l]), NNAT.b, writes=[NNAT.b])
        mtok = SMs if is_s else MREP
        op("dve", lambda e: e.tensor_tensor(out=SM[:P, 0:8], in0=PROJ[:P, 2048:2056], in1=PRM[:P, o_bg:o_bg + 8], op=ALU.add),
           reads=[PROJ.b, PRM.b], writes=[SM.b])
        op("dve", lambda e: e.scalar_tensor_tensor(out=SM[:P, 8:12], in0=SM[:P, 4:8], scalar=-1.0, in1=SM[:P, 4:8], op0=ALU.mult, op1=ALU.max),
           reads=[SM.b], writes=[SM.b])
        op("act", lambda e: e.activation(out=SM[:P, 12:16], in_=SM[:P, 8:12], func=AF.Exp, scale=-1.0), reads=[SM.b], writes=[SM.b])
        op("act", lambda e: e.activation(out=SM[:P, 12:16], in_=SM[:P, 12:16], func=AF.Ln, bias=1.0, scale=1.0),
           reads=[SM.b], writes=[SM.b])
        op("dve", lambda e: e.tensor_scalar_min(out=SM[:P, 16:20], in0=SM[:P, 4:8], scalar1=0.0), reads=[SM.b], writes=[SM.b])
        op("dve", lambda e: e.tensor_tensor(out=SM[:P, 16:20], in0=SM[:P, 16:20], in1=SM[:P, 12:16], op=ALU.subtract),
           reads=[SM.b], writes=[SM.b])
        op("pe", lambda e: e.matmul(PA[:P, 0:4], lhsT=tri, rhs=SM[:P, 16:20], start=True, stop=True),
           reads=[CST.b, SM.b], writes=[PA.b])
        op("act", lambda e: e.copy(out=SM[:P, 20:24], in_=PA[:P, 0:4]), reads=[PA.b], writes=[SM.b])
        op("dve", lambda e: e.tensor_tensor(out=SM[:P, 24:28], in0=SM[:P, 0:4], in1=SM[:P, 20:24], op=ALU.subtract),
           reads=[SM.b], writes=[SM.b])
        op("pe", lambda e: e.matmul(PA[:P, 8:12], lhsT=selE, rhs=SM[:P, 20:24], start=True, stop=True),
           reads=[CST.b, SM.b], writes=[PA.b])
        op("act", lambda e: e.copy(out=SM[:P, 28:32], in_=PA[:P, 8:12]), reads=[PA.b], writes=[SM.b])

        for hh in range(4):
            qs = PROJ[:P, hh * 128:(hh + 1) * 128]
            ks = PROJ[:P, 512 + hh * 128:512 + (hh + 1) * 128]
            vs = PROJ[:P, 1024 + hh * 128:1024 + (hh + 1) * 128]
            os_ = PROJ[:P, 1536 + hh * 128:1536 + (hh + 1) * 128]
            col = lambda c, hh=hh: SM[:P, c + hh:c + hh + 1]
            S1 = lambda c: SM[:P, c:c + 1]
            if is_s:
                kb.dma("sp", lambda q, hh=hh: q.dma_start(out=CN[:, :, :], in_=E["sC"][l, hh]), CN.b, writes=[CN.b])
                kb.dma("sp", lambda q, hh=hh: q.dma_start(out=NTH[:, :], in_=E["snT"][l, hh]), NTH.b, writes=[NTH.b])
                for j in range(4):
                    pt = PT[j % 2]
                    for jj in range(4):
                        b = j * 4 + jj
                        op("pe", lambda e, b=b, jj=jj, pt=pt: e.transpose(out=pt[:, jj * 128:(jj + 1) * 128], in_=CN[:, b, :],
                                                                       identity=C("ident")),
                           reads=[CN.b, CST.b], writes=[pt.b])
                    op("act", lambda e, j=j, pt=pt: e.copy(out=CTS[:, j * 4:(j + 1) * 4, 0:128],
                                                           in_=pt[:, :].rearrange("p (j c) -> p j c", j=4)),
                       reads=[pt.b], writes=[CTS.b])
                op("dve", lambda e: e.tensor_copy(out=CTS[:, :, 128:129], in_=NTH[:, :].unsqueeze(2)), reads=[NTH.b], writes=[CTS.b])
            op("dve", lambda e, hh=hh: e.tensor_scalar(out=DG[:P, :P], in0=C("ident", P, P), scalar1=col(24), scalar2=None,
                                                       op0=ALU.mult), reads=[SM.b, CST.b], writes=[DG.b])
            op("pe", lambda e: e.matmul(PB[:P, 0:P], lhsT=C("ones", P, P), rhs=DG[:P, :P], start=True, stop=True),
               reads=[DG.b, CST.b], writes=[PB.b])
            if is_s:
                op("dve", lambda e: e.tensor_tensor(out=DL[:P, :P], in0=PB[:P, 0:P], in1=C("negb_s", SP, SP), op=ALU.add),
                   reads=[PB.b, CST.b], writes=[DL.b])
                op("dve", lambda e: e.tensor_reduce(out=S1(32), in_=DL[:P, :P], axis=AX.X, op=ALU.max), reads=[DL.b], writes=[SM.b])
            else:
                op("dve", lambda e: e.tensor_reduce(out=S1(32), in_=PB[:P, 0:P], axis=AX.X, op=ALU.max), reads=[PB.b], writes=[SM.b])
            op("dve", lambda e, hh=hh: e.scalar_tensor_tensor(out=DL[:P, :P], in0=PB[:P, 0:P], scalar=col(20), in1=negm,
                                                              op0=ALU.add, op1=ALU.add),
               reads=[PB.b, SM.b, CST.b], writes=[DL.b])
            op("dve", lambda e: e.tensor_reduce(out=S1(33), in_=DL[:P, :P], axis=AX.X, op=ALU.max), reads=[DL.b], writes=[SM.b])
            op("dve", lambda e, hh=hh: e.tensor_tensor(out=S1(34), in0=col(20), in1=mtok[:P, hh:hh + 1], op=ALU.add),
               reads=[SM.b, mtok.b], writes=[SM.b])
            op("dve", lambda e: e.tensor_tensor(out=S1(35), in0=S1(34), in1=S1(33), op=ALU.max), reads=[SM.b], writes=[SM.b])
            op("dve", lambda e: e.tensor_scalar(out=S1(36), in0=S1(35), scalar1=-1.0, scalar2=None, op0=ALU.mult),
               reads=[SM.b], writes=[SM.b])
            op("act", lambda e: e.activation(out=WI[:P, :P], in_=DL[:P, :P], func=AF.Exp, bias=S1(36), scale=1.0),
               reads=[DL.b, SM.b], writes=[WI.b])
            op("act", lambda e: e.activation(out=S1(37), in_=S1(34), func=AF.Exp, bias=S1(36), scale=1.0), reads=[SM.b], writes=[SM.b])
            op("act", lambda e: e.activation(out=S1(38), in_=S1(36), func=AF.Exp), reads=[SM.b], writes=[SM.b])
            op("pe", lambda e: e.transpose(out=PC[:, 0:P], in_=qs, identity=C("ident", P, P)), reads=[PROJ.b, CST.b], writes=[PC.b])
            op("pe", lambda e: e.transpose(out=PC[:, 128:128 + P], in_=ks, identity=C("ident", P, P)), reads=[PROJ.b, CST.b], writes=[PC.b])
            op("act", lambda e: e.mul(out=QT[:, :P], in_=PC[:, 0:P], mul=128.0 ** -0.5), reads=[PC.b], writes=[QT.b])
            op("act", lambda e: e.copy(out=KT[:, :P], in_=PC[:, 128:128 + P]), reads=[PC.b], writes=[KT.b])
            op("pe", lambda e: e.matmul(PD[:P, 0:P], lhsT=QT[:, :P], rhs=KT[:, :P], start=True, stop=True),
               reads=[QT.b, KT.b], writes=[PD.b])
            op("dve", lambda e: e.tensor_tensor(out=AM[:P, :P], in0=WI[:P, :P], in1=PD[:P, 0:P], op=ALU.mult),
               reads=[WI.b, PD.b], writes=[AM.b])
            op("pe", lambda e: e.transpose(out=PB[:P, 128:128 + P], in_=AM[:P, :P], identity=C("ident", P, P)),
               reads=[AM.b, CST.b], writes=[PB.b])
            op("act", lambda e: e.copy(out=AT[:P, :P], in_=PB[:P, 128:128 + P]), reads=[PB.b], writes=[AT.b])
            op("pool", lambda e: e.tensor_copy(out=VX[:P, 0:128], in_=vs), reads=[PROJ.b], writes=[VX.b])
            op("pe", lambda e: e.matmul(PD[:P, 128:257], lhsT=AT[:P, :P], rhs=VX[:P, :], start=True, stop=True),
               reads=[AT.b, VX.b], writes=[PD.b])
            if is_s:
                op("pool", lambda e: e.memset(ZQ[:, :], 0.0), writes=[ZQ.b])
                for b in range(SB):
                    op("pool", lambda e, b=b: e.tensor_copy(out=ZQ[:, b * SP + b:(b + 1) * SP:SB], in_=QT[:, b:SP:SB]),
                       reads=[QT.b], writes=[ZQ.b])
                for b in range(SB):
                    op("pe", lambda e, b=b: e.matmul(PC[:P, 256:385], lhsT=ZQ[:, b * SP:(b + 1) * SP], rhs=CTS[:, b, :],
                                                     start=(b == 0), stop=(b == SB - 1)),
                       reads=[ZQ.b, CTS.b], writes=[PC.b] if b in (0, SB - 1) else [])
            else:
                op("pe", lambda e, hh=hh: e.matmul(PC[:P, 256:385], lhsT=QT[:, :P], rhs=CTX[:, hh, :], start=True, stop=True),
                   reads=[QT.b, CTX.b], writes=[PC.b])
            op("act", lambda e: e.activation(out=TOT[:P, :], in_=PC[:P, 256:385], func=AF.Identity, scale=S1(37)),
               reads=[PC.b, SM.b], writes=[TOT.b])
            op("dve", lambda e: e.tensor_tensor(out=TOT[:P, :], in0=TOT[:P, :], in1=PD[:P, 128:257], op=ALU.add),
               reads=[TOT.b, PD.b], writes=[TOT.b])
            op("dve", lambda e: e.scalar_tensor_tensor(out=S1(39), in0=TOT[:P, 128:129], scalar=-1.0, in1=TOT[:P, 128:129], op0=ALU.mult, op1=ALU.max),
               reads=[TOT.b], writes=[SM.b])
            op("dve", lambda e: e.tensor_tensor(out=S1(39), in0=S1(39), in1=S1(38), op=ALU.max), reads=[SM.b], writes=[SM.b])
            op("dve", lambda e: e.reciprocal(out=S1(40), in_=S1(39)), reads=[SM.b], writes=[SM.b])
            op("dve", lambda e: e.tensor_scalar(out=HN[:P, :], in0=TOT[:P, 0:128], scalar1=S1(40), scalar2=None, op0=ALU.mult),
               reads=[TOT.b, SM.b], writes=[HN.b])
            op("dve", lambda e: e.bn_stats(out=ST6[:P, 0, :], in_=HN[:P, :]), reads=[HN.b], writes=[ST6.b])
            op("dve", lambda e: e.bn_aggr(out=SM[:P, 41:43], in_=ST6[:P, 0, :]), reads=[ST6.b], writes=[SM.b])
            op("act", lambda e: e.activation(out=S1(43), in_=S1(42), func=AF.Sqrt, bias=LN_EPS, scale=1.0), reads=[SM.b], writes=[SM.b])
            op("dve", lambda e: e.reciprocal(out=S1(44), in_=S1(43)), reads=[SM.b], writes=[SM.b])
            op("dve", lambda e: e.tensor_scalar(out=HN[:P, :], in0=HN[:P, :], scalar1=S1(41), scalar2=S1(44),
                                                op0=ALU.subtract, op1=ALU.mult), reads=[HN.b, SM.b], writes=[HN.b])
            op("pool", lambda e, hh=hh: e.tensor_tensor(out=HN[:P, :], in0=HN[:P, :],
                                                        in1=PRM[:P, o_mh + hh * 128:o_mh + (hh + 1) * 128], op=ALU.mult),
               reads=[HN.b, PRM.b], writes=[HN.b])
            op("act", lambda e: e.activation(out=SG[:P, :], in_=os_, func=AF.Sigmoid), reads=[PROJ.b], writes=[SG.b])
            op("dve", lambda e, hh=hh: e.tensor_tensor(out=Y[:P, hh * 128:(hh + 1) * 128], in0=HN[:P, :], in1=SG[:P, :], op=ALU.mult),
               reads=[HN.b, SG.b], writes=[Y.b])
            op("dve", lambda e, hh=hh: e.tensor_tensor(out=S1(45), in0=mtok[:P, hh:hh + 1], in1=S1(32), op=ALU.max),
               reads=[SM.b, mtok.b], writes=[SM.b])
            op("dve", lambda e, hh=hh: e.tensor_tensor(out=S1(45), in0=S1(45), in1=col(28), op=ALU.add), reads=[SM.b], writes=[SM.b])
            op("dve", lambda e, hh=hh: e.tensor_tensor(out=S1(46), in0=col(28), in1=S1(45), op=ALU.subtract), reads=[SM.b], writes=[SM.b])
            op("act", lambda e, hh=hh: e.activation(out=S1(47), in_=col(24), func=AF.Exp, bias=S1(46), scale=1.0),
               reads=[SM.b], writes=[SM.b])
            op("act", lambda e, hh=hh: e.activation(out=S1(48), in_=mtok[:P, hh:hh + 1], func=AF.Exp, bias=S1(46), scale=1.0),
               reads=[SM.b, mtok.b], writes=[SM.b])
            if not is_s:
                op("dve", lambda e: e.tensor_scalar(out=WV[:P, :], in0=VX[:P, :], scalar1=S1(47), scalar2=None, op0=ALU.mult),
                   reads=[VX.b, SM.b], writes=[WV.b])
                op("pe", lambda e: e.matmul(PB[:, 256:385], lhsT=ks, rhs=WV[:P, :], start=True, stop=True),
                   reads=[PROJ.b, WV.b], writes=[PB.b])
                op("dve", lambda e, hh=hh: e.scalar_tensor_tensor(out=CTX[:, hh, :], in0=CTX[:, hh, :], scalar=S1(48), in1=PB[:, 256:385],
                                                                  op0=ALU.mult, op1=ALU.add),
                   reads=[CTX.b, SM.b, PB.b], writes=[CTX.b])
                op("dve", lambda e, hh=hh: e.tensor_copy(out=MREP[:, hh:hh + 1], in_=S1(45)), reads=[SM.b], writes=[MREP.b])
                if t == NT - 1:
                    op("pe", lambda e, hh=hh: e.transpose(out=PA[:, 128:256], in_=CTX[:, hh, 0:128], identity=C("ident")),
                       reads=[CTX.b, CST.b], writes=[PA.b])
                    op("act", lambda e: e.copy(out=OUTC[:, :], in_=PA[:, 128:256]), reads=[PA.b], writes=[OUTC.b])
                    out_dma(E["o_pC"][l, hh], OUTC[:, :], [OUTC.b])
                    out_dma(E["o_pn"][l, hh].rearrange("(k o) -> k o", o=1), CTX[:, hh, 128:129], [CTX.b])
                    if hh == 3:
                        out_dma(E["o_pm"][l:l + 1, :], MREP[0:1, :], [MREP.b])
            else:
                op("dve", lambda e: e.tensor_scalar(out=WCB[:P, :], in0=C("onehotB", SP), scalar1=S1(47), scalar2=None, op0=ALU.mult),
                   reads=[SM.b, CST.b], writes=[WCB.b])
                op("dve", lambda e: e.tensor_tensor(out=RA[:P, :, :], in0=vs.unsqueeze(1).broadcast_to([P, SB, 128]),
                                                    in1=WCB[:P, :].unsqueeze(2).broadcast_to([P, SB, 128]), op=ALU.mult),
                   reads=[PROJ.b, WCB.b], writes=[RA.b])
                op("dve", lambda e: e.tensor_scalar(out=DECD[:P, :], in0=C("onehot0", SP), scalar1=S1(48), scalar2=None, op0=ALU.mult),
                   reads=[SM.b, CST.b], writes=[DECD.b])
                op("pe", lambda e: e.matmul(PA[:, 16:32], lhsT=C("ones", SP, 128), rhs=DECD[:P, :], start=True, stop=True),
                   reads=[DECD.b, CST.b], writes=[PA.b])
                op("act", lambda e: e.copy(out=DECR[:, :], in_=PA[:, 16:32]), reads=[PA.b], writes=[DECR.b])
                for b in range(SB):
                    pq = [PA, PB, PC, PD][b % 4]
                    op("pe", lambda e, b=b, pq=pq: e.matmul(pq[:, 384:512], lhsT=RA[:P, b, :], rhs=ks, start=True, stop=True),
                       reads=[RA.b, PROJ.b], writes=[pq.b])
                    op("dve", lambda e, b=b, pq=pq: e.scalar_tensor_tensor(out=CN[:, b, :], in0=CN[:, b, :], scalar=DECR[:, b:b + 1],
                                                                           in1=pq[:, 384:512], op0=ALU.mult, op1=ALU.add),
                       reads=[CN.b, DECR.b, pq.b], writes=[CN.b])
                out_dma(E["o_nC"][l, :, hh].rearrange("b v k -> v b k"), CN[:, :, :], [CN.b])
                op("pe", lambda e: e.matmul(PA[:SB, 32:160], lhsT=WCB[:P, :], rhs=ks, start=True, stop=True),
                   reads=[WCB.b, PROJ.b], writes=[PA.b])
                op("dve", lambda e, hh=hh: e.scalar_tensor_tensor(out=NNAT[:, hh, :], in0=NNAT[:, hh, :], scalar=SM[:SB, 48:49],
                                                                  in1=PA[:SB, 32:160], op0=ALU.mult, op1=ALU.add),
                   reads=[NNAT.b, SM.b, PA.b], writes=[NNAT.b])
                op("dve", lambda e, hh=hh: e.tensor_copy(out=MSO[:, hh:hh + 1], in_=SM[:SB, 45:46]), reads=[SM.b], writes=[MSO.b])
                if hh == 3:
                    out_dma(E["o_nn"][l], NNAT[:, :, :], [NNAT.b])
                    out_dma(E["o_nm"][l], MSO[:, :], [MSO.b])

        vsv = PROJ[:P, 2312:2568].rearrange("p (g d) -> p g d", g=4)
        op("dve", lambda e: e.tensor_reduce(out=SM[:P, 50:54], in_=vsv, axis=AX.X, op=ALU.add), reads=[PROJ.b], writes=[SM.b])
        op("dve", lambda e: e.tensor_scalar(out=SM[:P, 50:54], in0=SM[:P, 50:54], scalar1=1.0 / 64, scalar2=None, op0=ALU.mult),
           reads=[SM.b], writes=[SM.b])
        op("dve", lambda e: e.tensor_tensor(out=VN[:P, :].rearrange("p (g d) -> p g d", g=4), in0=vsv,
                                            in1=SM[:P, 50:54].unsqueeze(2).broadcast_to([P, 4, 64]), op=ALU.subtract),
           reads=[PROJ.b, SM.b], writes=[VN.b])
        op("pool", lambda e: e.tensor_tensor(out=VTMP[:P, :], in0=VN[:P, :], in1=VN[:P, :], op=ALU.mult), reads=[VN.b], writes=[VTMP.b])
        op("dve", lambda e: e.tensor_reduce(out=SM[:P, 54:58], in_=VTMP[:P, :].rearrange("p (g d) -> p g d", g=4), axis=AX.X, op=ALU.add),
           reads=[VTMP.b], writes=[SM.b])
        op("act", lambda e: e.activation(out=SM[:P, 54:58], in_=SM[:P, 54:58], func=AF.Sqrt, bias=LN_EPS, scale=1.0 / 64),
           reads=[SM.b], writes=[SM.b])
        op("dve", lambda e: e.reciprocal(out=SM[:P, 58:62], in_=SM[:P, 54:58]), reads=[SM.b], writes=[SM.b])
        op("dve", lambda e: e.tensor_tensor(out=VN[:P, :].rearrange("p (g d) -> p g d", g=4), in0=VN[:P, :].rearrange("p (g d) -> p g d", g=4),
                                            in1=SM[:P, 58:62].unsqueeze(2).broadcast_to([P, 4, 64]), op=ALU.mult),
           reads=[VN.b, SM.b], writes=[VN.b])
        op("pool", lambda e: e.tensor_tensor(out=VN[:P, :], in0=VN[:P, :], in1=PRM[:P, o_sg:o_sg + 256], op=ALU.mult),
           reads=[VN.b, PRM.b], writes=[VN.b])
        op("pool", lambda e: e.tensor_tensor(out=VN[:P, :], in0=VN[:P, :], in1=PRM[:P, o_sb:o_sb + 256], op=ALU.add),
           reads=[VN.b, PRM.b], writes=[VN.b])
        wsl = WSs if is_s else WS
        bsl = BSs if is_s else BS
        for g in range(4):
            op("pe", lambda e, g=g: e.matmul(PC[:P, g * 64:(g + 1) * 64], lhsT=wsl[:P, g, :P], rhs=VN[:P, g * 64:(g + 1) * 64],
                                             start=True, stop=True), reads=[wsl.b, VN.b], writes=[PC.b])
        for g in range(4):
            op("dve", lambda e, g=g: e.scalar_tensor_tensor(out=Y[:P, 512 + g * 64:512 + (g + 1) * 64], in0=PC[:P, g * 64:(g + 1) * 64],
                                                            scalar=bsl[:P, g:g + 1], in1=PROJ[:P, 2056 + g * 64:2056 + (g + 1) * 64],
                                                            op0=ALU.add, op1=ALU.mult),
               reads=[PC.b, bsl.b, PROJ.b], writes=[Y.b])
        if is_s:
            for tq in range(ST):
                out_dma(E["o_nv"][l][:, tq, :], VN[tq * SB:(tq + 1) * SB, :], [VN.b])

        pin = lambda g: PROJ[:P, 2568 + g * 64:2568 + (g + 1) * 64]
        if is_s:
            kb.dma("sp", lambda q: q.dma_start(out=SPA[:, :], in_=E["spA"][l]), SPA.b, writes=[SPA.b])
            kb.dma("sp", lambda q: q.dma_start(out=SPB[:112, :], in_=E["spB"][l]), SPB.b, writes=[SPB.b])
            for g in range(4):
                op("pe", lambda e, g=g: e.matmul(PA[:64, g * 128:g * 128 + P], lhsT=SPA[:, g * 64:(g + 1) * 64], rhs=Cg("bsA", g, 128, SP, SP),
                                                 start=True, stop=False), reads=[SPA.b, CST.b], writes=[PA.b])
                op("pe", lambda e, g=g: e.matmul(PA[:64, g * 128:g * 128 + P], lhsT=SPB[:112, g * 64:(g + 1) * 64], rhs=Cg("bsB", g, 112, SP, SP),
                                                 start=False, stop=False), reads=[SPB.b, CST.b], writes=[])
                op("pe", lambda e, g=g: e.matmul(PA[:64, g * 128:g * 128 + P], lhsT=pin(g), rhs=Cg("bsC", g, SP, SP, SP),
                                                 start=False, stop=True), reads=[PROJ.b, CST.b], writes=[PA.b])
            npv = E["o_np"][l].rearrange("b r c -> r b c")
            for r in range(4):
                out_dma(npv[r], SPA[64 + r * SB:64 + (r + 1) * SB, :], [SPA.b])
            for r in range(7):
                out_dma(npv[4 + r], SPB[r * SB:(r + 1) * SB, :], [SPB.b])
            for r in range(4):
                out_dma(npv[11 + r], PROJ[r * SB:(r + 1) * SB, 2568:2824], [PROJ.b])
        else:
            for g in range(4):
                band = Cg("bandc0" if t == 0 else "bandc", g, 128, 128, 128)
                op("pe", lambda e, g=g, band=band: e.matmul(PA[:64, g * 128:(g + 1) * 128], lhsT=pin(g), rhs=band, start=True, stop=(t == 0)),
                   reads=[PROJ.b, CST.b], writes=[PA.b])
                if t > 0:
                    op("pe", lambda e, g=g: e.matmul(PA[:64, g * 128:(g + 1) * 128], lhsT=PREV[:, g * 64:(g + 1) * 64],
                                                     rhs=Cg("bandp", g, 128, 128, 128), start=False, stop=True),
                       reads=[PREV.b, CST.b], writes=[PA.b])
            if t < NT - 1:
                op("pool", lambda e: e.tensor_copy(out=PREV[:, :], in_=PROJ[:, 2568:2824]), reads=[PROJ.b], writes=[PREV.b])
            else:
                out_dma(E["o_pp"][l], PROJ[113:128, 2568:2824], [PROJ.b])
        op("act", lambda e: e.copy(out=PTT[:, :, :P], in_=PA[:64, :].rearrange("p (g c) -> p g c", g=4)[:, :, :P]),
           reads=[PA.b], writes=[PTT.b])
        for g in range(4):
            op("pe", lambda e, g=g: e.matmul(PB[:P, g * 64:(g + 1) * 64], lhsT=PTT[:, g, :P], rhs=WP[:, g, :], start=True, stop=True),
               reads=[PTT.b, WP.b], writes=[PB.b])
        op("dve", lambda e: e.tensor_tensor(out=Y[:P, 768:1024], in0=PB[:P, 0:256], in1=PRM[:P, o_ps:o_ps + 256], op=ALU.mult),
           reads=[PB.b, PRM.b], writes=[Y.b])

        _transpose8(kb, E, Y, HT, PT, P)
        for c in range(4):
            stream_mm(w_o_v, c * 256, 256, HT, P,
                      lambda pm, i, c=c: op("dve", lambda e: e.tensor_tensor(out=H[:P, c * 256:(c + 1) * 256], in0=pm[:P, 0:256],
                                                                            in1=ada[:P, 2048 + c * 256:2048 + (c + 1) * 256], op=ALU.mult),
                                            reads=[pm.b, ada.b], writes=[H.b]))
        _resid_ln(kb, X, XB[t], t, P, H, SM, ST6, PRM, o_l1g, o_l1b)


def _resid_ln(kb, X, xb, t, P, Z, SM, ST6, PRM, og, ob):
    op = kb.op
    xt = X[:P, t, :]
    op("dve", lambda e: e.scalar_tensor_tensor(out=Z[:P, :], in0=xt, scalar=ALPHA, in1=Z[:P, :], op0=ALU.mult, op1=ALU.add),
       reads=[xb, Z.b], writes=[Z.b])
    op("dve", lambda e: e.bn_stats(out=ST6[:P, 0, :], in_=Z[:P, 0:512]), reads=[Z.b], writes=[ST6.b])
    op("dve", lambda e: e.bn_stats(out=ST6[:P, 1, :], in_=Z[:P, 512:1024]), reads=[Z.b], writes=[ST6.b])
    op("dve", lambda e: e.bn_aggr(out=SM[:P, 41:43], in_=ST6[:P, :, :].rearrange("p a b -> p (a b)")), reads=[ST6.b], writes=[SM.b])
    op("act", lambda e: e.activation(out=SM[:P, 43:44], in_=SM[:P, 42:43], func=AF.Sqrt, bias=LN_EPS, scale=1.0), reads=[SM.b], writes=[SM.b])
    op("dve", lambda e: e.reciprocal(out=SM[:P, 44:45], in_=SM[:P, 43:44]), reads=[SM.b], writes=[SM.b])
    op("dve", lambda e: e.tensor_scalar(out=Z[:P, :], in0=Z[:P, :], scalar1=SM[:P, 41:42], scalar2=SM[:P, 44:45],
                                        op0=ALU.subtract, op1=ALU.mult), reads=[Z.b, SM.b], writes=[Z.b])
    op("pool", lambda e: e.tensor_tensor(out=Z[:P, :], in0=Z[:P, :], in1=PRM[:P, og:og + D], op=ALU.mult), reads=[Z.b, PRM.b], writes=[Z.b])
    op("dve", lambda e: e.tensor_tensor(out=xt, in0=Z[:P, :], in1=PRM[:P, ob:ob + D], op=ALU.add), reads=[Z.b, PRM.b], writes=[xb])


def _phase2(nc, kb, l, E):
    op = kb.op
    C, CST, X, XB = E["C"], E["CST"], E["X"], E["XB"]
    ADA = Tn(kb, "ADA2", [128, 3072]); ADAs = ADA
    WCH = [Tn(kb, "WCHb%d" % i, [128, 8, 256], dma=True) for i in range(2)]
    badac = [Tn(kb, "badab%d" % i, [128, 256], dma=True) for i in range(2)]
    H = Tn(kb, "H2", [128, D], dma=True); HT = Tn(kb, "H2T", [128, 8, 128])
    PRM = Tn(kb, "PRM2", [128, 2 * D], dma=True)
    KTS = Tn(kb, "KTS", [128, 16, 128], dma=True)
    S0 = Tn(kb, "S0", [128, 2048]); S1_ = Tn(kb, "S1", [128, 2048]); S2 = Tn(kb, "S2", [128, 2048])
    TOPS = Tn(kb, "TOPS", [128, 16, 16]); IDXU = Tn(kb, "IDXU", [128, 16, 16], U32); IDXF = Tn(kb, "IDXF", [128, 16, 16])
    CV = Tn(kb, "CV", [128, 8, 16]); CPOS = Tn(kb, "CPOS", [128, 8, 16], U32)
    PAU = Tn(kb, "PAU", [128, 8, 16], U32); PBU = Tn(kb, "PBU", [128, 8, 16], U32)
    PAF = Tn(kb, "PAF", [128, 8, 16]); PBF = Tn(kb, "PBF", [128, 8, 16])
    I1 = Tn(kb, "I1", [128, 128]); I2 = Tn(kb, "I2", [128, 128])
    IDX = Tn(kb, "IDX", [128, 128], I32); GATE = Tn(kb, "GATE", [128, 128])
    ACTV = Tn(kb, "ACTV", [128, 128]); COEF = Tn(kb, "COEF", [128, 128])
    SM = Tn(kb, "SM2", [128, 64]); ST6 = Tn(kb, "ST62", [128, 2, 6])
    UB = [Tn(kb, "UB%d" % i, [128, 4, D], dma=True) for i in range(2)]
    ACC = Tn(kb, "ACC", [128, D])
    PT = [Tn(kb, "PTb%d" % i, [128, 512], psum=True) for i in range(2)]
    PM = [Tn(kb, "PMb%d" % i, [128, 512], psum=True) for i in range(2)]
    PQ = [Tn(kb, "PQ%d" % i, [128, 512], psum=True) for i in range(2)]
    PS = [Tn(kb, "PS%d" % i, [128, 512], psum=True) for i in range(2)]

    kb.dma("sp", lambda q: q.dma_start(out=PRM[:, 0:D], in_=E["ln2g"][l]), PRM.b, writes=[PRM.b])
    kb.dma("sp", lambda q: q.dma_start(out=PRM[:, D:2 * D], in_=E["ln2b"][l]), PRM.b, writes=[PRM.b])
    kb.dma("sp", lambda q: q.dma_start(out=KTS[:, :, :], in_=E["keysT"][l]), KTS.b, writes=[KTS.b])
    _ada(nc, kb, l, E, ADA, 128, E["cp"], 3072, WCH, PM, H, HT, PT, badac)
    wpq_v = E["w_pq"][l].rearrange("(k p) n -> p k n", p=128)
    QT = S0
    wctr = 0
    for t in range(NT + 1):
        is_s = (t == NT)
        P = SP if is_s else 128
        ada = ADAs if is_s else ADA
        if is_s:
            _ada(nc, kb, l, E, ADAs, SP, E["cs"], 3072, WCH, PM, H, HT, PT, badac)
        xt = X[:P, t, :]
        op("dve", lambda e: e.tensor_tensor(out=H[:P, :], in0=xt, in1=ada[:P, 1024:2048], op=ALU.mult), reads=[XB[t], ada.b], writes=[H.b])
        op("dve", lambda e: e.tensor_tensor(out=H[:P, :], in0=H[:P, :], in1=ada[:P, 0:1024], op=ALU.add), reads=[H.b, ada.b], writes=[H.b])
        _transpose8(kb, E, H, HT, PT, P)
        for cc in range(8):
            i = wctr % 2
            wctr += 1
            kb.dma("sp", lambda q, i=i, cc=cc: q.dma_start(out=WCH[i][:, :, :], in_=wpq_v[:, :, cc * 256:(cc + 1) * 256]),
                   WCH[i].b, writes=[WCH[i].b])
            for j in range(2):
                c = cc * 2 + j
                for k in range(8):
                    op("pe", lambda e, i=i, j=j, k=k: e.matmul(PQ[j][:, 0:P], lhsT=WCH[i][:, k, j * 128:(j + 1) * 128], rhs=HT[:, k, :P],
                                                               start=(k == 0), stop=(k == 7)),
                       reads=[WCH[i].b, HT.b], writes=[PQ[j].b] if k in (0, 7) else [])
                op("act", lambda e, j=j, c=c: e.copy(out=QT[:, c * 128:c * 128 + P], in_=PQ[j][:, 0:P]), reads=[PQ[j].b], writes=[QT.b])
        for c4 in range(4):
            ps = PS[c4 % 2]
            for j in range(4):
                c = c4 * 4 + j
                op("pe", lambda e, ps=ps, j=j, c=c: e.matmul(ps[:P, j * 128:(j + 1) * 128], lhsT=QT[:, c * 128:c * 128 + P], rhs=KTS[:, c, :],
                                                             start=True, stop=True), reads=[QT.b, KTS.b], writes=[ps.b])
            op("act", lambda e, ps=ps, c4=c4: e.copy(out=S1_[:P, c4 * 512:(c4 + 1) * 512], in_=ps[:P, :]), reads=[ps.b], writes=[S1_.b])
        for c in range(16):
            sc = S1_[:P, c * 128:(c + 1) * 128]
            wk = S2[:P, c * 128:(c + 1) * 128]
            op("dve", lambda e, c=c, sc=sc: e.max(out=TOPS[:P, c, 0:8], in_=sc), reads=[S1_.b], writes=[TOPS.b])
            op("dve", lambda e, c=c, sc=sc: e.max_index(out=IDXU[:P, c, 0:8], in_max=TOPS[:P, c, 0:8], in_values=sc),
               reads=[S1_.b, TOPS.b], writes=[IDXU.b])
            op("dve", lambda e, c=c, sc=sc, wk=wk: e.match_replace(out=wk, in_to_replace=TOPS[:P, c, 0:8], in_values=sc, imm_value=NEG),
               reads=[S1_.b, TOPS.b], writes=[S2.b])
            op("dve", lambda e, c=c, wk=wk: e.max(out=TOPS[:P, c, 8:16], in_=wk), reads=[S2.b], writes=[TOPS.b])
            op("dve", lambda e, c=c, wk=wk: e.max_index(out=IDXU[:P, c, 8:16], in_max=TOPS[:P, c, 8:16], in_values=wk),
               reads=[S2.b, TOPS.b], writes=[IDXU.b])
        op("dve", lambda e: e.tensor_copy(out=IDXF[:P, :, :], in_=IDXU[:P, :, :]), reads=[IDXU.b], writes=[IDXF.b])
        tv = TOPS[:P, :, :].rearrange("p (h two) k -> p h two k", two=2)
        CAND = S0
        op("dve", lambda e: e.tensor_tensor(out=CAND[:P, :].rearrange("p (h a b) -> p h a b", h=8, a=16),
                                            in0=tv[:, :, 0, :].unsqueeze(3).broadcast_to([P, 8, 16, 16]),
                                            in1=tv[:, :, 1, :].unsqueeze(2).broadcast_to([P, 8, 16, 16]), op=ALU.add),
           reads=[TOPS.b], writes=[S0.b])
        for h in range(8):
            cd = CAND[:P, h * 256:(h + 1) * 256]
            wk = S2[:P, h * 256:(h + 1) * 256]
            op("dve", lambda e, h=h, cd=cd: e.max(out=CV[:P, h, 0:8], in_=cd), reads=[S0.b], writes=[CV.b])
            op("dve", lambda e, h=h, cd=cd: e.max_index(out=CPOS[:P, h, 0:8], in_max=CV[:P, h, 0:8], in_values=cd),
               reads=[S0.b, CV.b], writes=[CPOS.b])
            op("dve", lambda e, h=h, cd=cd, wk=wk: e.match_replace(out=wk, in_to_replace=CV[:P, h, 0:8], in_values=cd, imm_value=NEG),
               reads=[S0.b, CV.b], writes=[S2.b])
            op("dve", lambda e, h=h, wk=wk: e.max(out=CV[:P, h, 8:16], in_=wk), reads=[S2.b], writes=[CV.b])
            op("dve", lambda e, h=h, wk=wk: e.max_index(out=CPOS[:P, h, 8:16], in_max=CV[:P, h, 8:16], in_values=wk),
               reads=[S2.b, CV.b], writes=[CPOS.b])
        op("dve", lambda e: e.tensor_single_scalar(out=PAU[:P, :, :], in_=CPOS[:P, :, :], scalar=4, op=ALU.logical_shift_right),
           reads=[CPOS.b], writes=[PAU.b])
        op("dve", lambda e: e.tensor_single_scalar(out=PBU[:P, :, :], in_=CPOS[:P, :, :], scalar=15, op=ALU.bitwise_and),
           reads=[CPOS.b], writes=[PBU.b])
        op("dve", lambda e: e.tensor_copy(out=PAF[:P, :, :], in_=PAU[:P, :, :]), reads=[PAU.b], writes=[PAF.b])
        op("dve", lambda e: e.tensor_copy(out=PBF[:P, :, :], in_=PBU[:P, :, :]), reads=[PBU.b], writes=[PBF.b])
        iv = IDXF[:P, :, :].rearrange("p (h two) k -> p h two k", two=2)
        io16 = C("iota16", P).unsqueeze(1).unsqueeze(1).broadcast_to([P, 8, 16, 16])
        for (pf, half, dst) in [(PAF, 0, I1), (PBF, 1, I2)]:
            eq = S1_[:P, :].rearrange("p (h k a) -> p h k a", h=8, k=16)
            op("dve", lambda e, pf=pf, eq=eq: e.tensor_tensor(out=eq, in0=pf[:P, :, :].unsqueeze(3).broadcast_to([P, 8, 16, 16]), in1=io16,
                                                              op=ALU.is_equal), reads=[pf.b, CST.b], writes=[S1_.b])
            op("dve", lambda e, half=half, eq=eq: e.tensor_tensor(out=eq, in0=eq, in1=iv[:, :, half, :].unsqueeze(2).broadcast_to([P, 8, 16, 16]),
                                                                  op=ALU.mult), reads=[S1_.b, IDXF.b], writes=[S1_.b])
            op("dve", lambda e, dst=dst, eq=eq: e.tensor_reduce(out=dst[:P, :].rearrange("p (h k) -> p h k", h=8), in_=eq, axis=AX.X, op=ALU.add),
               reads=[S1_.b], writes=[dst.b])
        op("dve", lambda e: e.scalar_tensor_tensor(out=I1[:P, :], in0=I1[:P, :], scalar=128.0, in1=I2[:P, :], op0=ALU.mult, op1=ALU.add),
           reads=[I1.b, I2.b], writes=[I1.b])
        op("dve", lambda e: e.tensor_copy(out=IDX[:P, :], in_=I1[:P, :]), reads=[I1.b], writes=[IDX.b])
        cvv = CV[:P, :, :]
        gv = GATE[:P, :].rearrange("p (h k) -> p h k", h=8)
        op("dve", lambda e: e.tensor_tensor(out=gv, in0=cvv, in1=CV[:P, :, 0:1].broadcast_to([P, 8, 16]), op=ALU.subtract),
           reads=[CV.b], writes=[GATE.b])
        op("act", lambda e: e.activation(out=GATE[:P, :], in_=GATE[:P, :], func=AF.Exp), reads=[GATE.b], writes=[GATE.b])
        op("dve", lambda e: e.tensor_reduce(out=SM[:P, 0:8], in_=gv, axis=AX.X, op=ALU.add), reads=[GATE.b], writes=[SM.b])
        op("dve", lambda e: e.reciprocal(out=SM[:P, 8:16], in_=SM[:P, 0:8]), reads=[SM.b], writes=[SM.b])
        op("dve", lambda e: e.tensor_tensor(out=gv, in0=gv, in1=SM[:P, 8:16].unsqueeze(2).broadcast_to([P, 8, 16]), op=ALU.mult),
           reads=[GATE.b, SM.b], writes=[GATE.b])
        JUNK = S2
        for side in range(2):
            tab = (E["pu"] if side == 0 else E["pv"])[l]
            for grp in range(32):
                ub = UB[grp % 2]
                for j in range(4):
                    s = grp * 4 + j
                    kb.dma("pool", lambda q, ub=ub, j=j, s=s, tab=tab: q.indirect_dma_start(
                        out=ub[:P, j, :], out_offset=None, in_=tab, in_offset=bass.IndirectOffsetOnAxis(ap=IDX[:P, s:s + 1], axis=0)),
                        ub.b, reads=[IDX.b], writes=[ub.b])
                for j in range(4):
                    s = grp * 4 + j
                    if side == 0:
                        op("dve", lambda e, ub=ub, j=j, s=s: e.scalar_tensor_tensor(out=JUNK[:P, 0:D], in0=ub[:P, j, :], scalar=1.0, in1=H[:P, :],
                                                                                    op0=ALU.mult, op1=ALU.mult, accum_out=ACTV[:P, s:s + 1]),
                           reads=[ub.b, H.b], writes=[S2.b, ACTV.b])
                    elif s == 0:
                        op("dve", lambda e, ub=ub, j=j, s=s: e.tensor_scalar(out=ACC[:P, :], in0=ub[:P, j, :], scalar1=COEF[:P, s:s + 1], scalar2=None,
                                                                             op0=ALU.mult), reads=[ub.b, COEF.b], writes=[ACC.b])
                    else:
                        op("dve", lambda e, ub=ub, j=j, s=s: e.scalar_tensor_tensor(out=ACC[:P, :], in0=ub[:P, j, :], scalar=COEF[:P, s:s + 1],
                                                                                    in1=ACC[:P, :], op0=ALU.mult, op1=ALU.add),
                           reads=[ub.b, COEF.b, ACC.b], writes=[ACC.b])
            if side == 0:
                op("act", lambda e: e.activation(out=COEF[:P, :], in_=ACTV[:P, :], func=AF.Gelu), reads=[ACTV.b], writes=[COEF.b])
                op("dve", lambda e: e.tensor_tensor(out=COEF[:P, :], in0=COEF[:P, :], in1=GATE[:P, :], op=ALU.mult),
                   reads=[COEF.b, GATE.b], writes=[COEF.b])
        op("dve", lambda e: e.tensor_tensor(out=ACC[:P, :], in0=ACC[:P, :], in1=ada[:P, 2048:3072], op=ALU.mult), reads=[ACC.b, ada.b], writes=[ACC.b])
        _resid_ln(kb, X, XB[t], t, P, ACC, SM, ST6, PRM, 0, D)


_CACHE = {}


def _rep(a, P=128):
    return np.ascontiguousarray(np.broadcast_to(a[:, None, :], (a.shape[0], P, a.shape[1])))


def make_in_maps(inp, cpack):
    f = lambda a: np.ascontiguousarray(np.asarray(a, dtype=np.float32))
    shared = {
        "w_ada": f(inp["w_ada"]), "b_ada": _rep(f(inp["b_ada"])), "w_in": f(inp["w_in"]), "b_gate": _rep(f(inp["b_gate"])),
        "mh_g": _rep(f(inp["mh_g"])), "sgu_g": _rep(f(inp["sgu_g"])), "sgu_b": _rep(f(inp["sgu_b"])),
        "pscale": _rep(f(inp["pool_scale"])),
        "w_sT": f(np.asarray(inp["w_s"]).transpose(0, 3, 1, 2)),
        "b_sT": f(np.asarray(inp["b_s"]).transpose(0, 2, 1)),
        "w_pool": f(np.asarray(inp["w_pool"]).transpose(0, 2, 1, 3)),
        "w_o": f(inp["w_o"]), "ln1g": _rep(f(inp["ln1_g"])), "ln1b": _rep(f(inp["ln1_b"])),
        "ln2g": _rep(f(inp["ln2_g"])), "ln2b": _rep(f(inp["ln2_b"])), "w_pq": f(inp["w_pq"]),
        "keysT": f(np.asarray(inp["peer_keys"]).transpose(0, 4, 1, 2, 3).reshape(DEPTH, 128, 16, 128)),
        "cst": cpack,
    }
    ws4 = np.asarray(inp["w_s"])[:, :, :ST, :ST]
    wsS = np.repeat(np.repeat(ws4.transpose(0, 3, 1, 2), SB, axis=1), SB, axis=3)
    shared["w_sS"] = f(wsS)
    bs4 = np.asarray(inp["b_s"])[:, :, :ST]
    shared["b_sS"] = f(np.repeat(bs4.transpose(0, 2, 1), SB, axis=1))
    for l in range(DEPTH):
        shared["pu%d" % l] = f(np.asarray(inp["peer_u"])[l])
        shared["pv%d" % l] = f(np.asarray(inp["peer_v"])[l])
    maps = []
    for c in range(NCORES):
        bs = slice(c * SB, (c + 1) * SB)
        m = dict(shared)
        m["xp"] = f(np.asarray(inp["x_prompt"])[c])
        m["xs"] = f(np.asarray(inp["x_sample"])[bs].transpose(1, 0, 2).reshape(SP, D))
        m["cp"] = f(np.broadcast_to(np.asarray(inp["c_prompt"])[c][None, :], (128, D)))
        m["cs"] = f(np.tile(np.asarray(inp["c_sample"])[bs], (ST, 1)))
        sCc = np.asarray(inp["state_mlstm_C"])[:, bs]
        m["sC"] = f(sCc.transpose(0, 2, 3, 1, 4))
        snc = np.asarray(inp["state_mlstm_n"])[:, bs]
        m["snat"] = f(snc)
        m["snT"] = f(snc.transpose(0, 2, 3, 1))
        m["sm"] = f(np.tile(np.asarray(inp["state_mlstm_m"])[:, bs], (1, ST, 1)))
        spc = np.asarray(inp["state_pool"])[:, bs].transpose(0, 2, 1, 3)
        m["spA"] = f(spc[:, 0:8].reshape(DEPTH, 128, 256))
        m["spB"] = f(spc[:, 8:15].reshape(DEPTH, 112, 256))
        maps.append(m)
    return maps


def gather_outputs(results):
    cat = lambda k, ax: np.concatenate([r[k] for r in results], axis=ax)
    yp = np.stack([r["yp"] for r in results], 0)
    ys = np.concatenate([r["ys"].reshape(ST, SB, D).transpose(1, 0, 2) for r in results], 0)
    pC = np.stack([r["pC"] for r in results], 1)
    pn = np.stack([r["pn"] for r in results], 1)
    pm = np.stack([r["pm"] for r in results], 1)
    pp = np.stack([r["pp"] for r in results], 1)
    return (yp, ys, pC, pn, pm, pp, cat("nC", 1), cat("nn", 1), cat("nm", 1), cat("npool", 1), cat("nv", 1))


def kernel(**inputs):
    if "prog" not in _CACHE:
        _CACHE["prog"] = build_program()
    nc, cpack = _CACHE["prog"]
    maps = make_in_maps(inputs, cpack)
    res = run_bass_kernel_spmd(nc, maps, core_ids=list(range(NCORES)))
    outs = gather_outputs(res.results)
    return tuple(np.ascontiguousarray(o, dtype=np.float32) for o in outs)
```

```python
import numpy as np
from contextlib import ExitStack
import concourse.bass as bass
import concourse.mybir as mybir
from concourse.bass_utils import run_bass_kernel_spmd

F32 = mybir.dt.float32
I32 = mybir.dt.int32
U32 = mybir.dt.uint32
ALU = mybir.AluOpType
AF = mybir.ActivationFunctionType
AX = mybir.AxisListType

NCORES = 8
D = 1024
SEQ = 2048
NT = 16
SB = 16
ST = 4
SP = SB * ST
DEPTH = 2
ALPHA = (2 * DEPTH) ** 0.25
LN_EPS = 1e-5
IN_COLS = 2824
NEG = -1.0e30
WCW = 264
NEXP = 16384
SAME_ENGINE_WAITS = True


class TB:
    def __init__(self, name, sem=None):
        self.name = name
        self.last_w = None
        self.reads = []
        self.sem = sem
        self.dma_total = 0
        self.dma_dirty = False


class KB:
    ENG = ("pe", "act", "dve", "pool", "sp")

    def __init__(self, nc, stack):
        self.nc = nc
        self.stack = stack
        self.q = {e: [] for e in self.ENG}
        self.cnt = {e: 0 for e in self.ENG}
        self.esem = {e: stack.enter_context(nc.semaphore("es_" + e)) for e in self.ENG}
        self.seen = {e: {} for e in self.ENG}
        self.semobj = {}
        self._sem_owner = {}
        self.stack0 = stack
        self.phase_tbs = []
        self.sfx = ""

    def new_sem(self, name):
        return self.stack.enter_context(self.nc.semaphore(name + self.sfx))

    def buf(self, name, dma=False):
        tb = TB(name, self.new_sem("d_" + name) if dma else None)
        if dma and self.stack is not self.stack0:
            self.phase_tbs.append(tb)
        return tb

    def end_phase(self):
        for tb in self.phase_tbs:
            k = id(tb.sem)
            self._sem_owner.pop(k, None)
            self.semobj.pop(k, None)
            for e in self.ENG:
                self.seen[e].pop(k, None)
        self.phase_tbs = []

    def sb(self, name, shape, dt=F32):
        return self.stack.enter_context(self.nc.sbuf_tensor(name + self.sfx, list(shape), dt))

    def ps(self, name, shape, dt=F32):
        return self.stack.enter_context(self.nc.psum_tensor(name + self.sfx, list(shape), dt))

    def _deps(self, e, reads, writes):
        deps = {}

        def add(tok):
            if tok is None:
                return
            s, v = tok
            k = id(s)
            self.semobj[k] = s
            ow = self._sem_owner.get(k)
            if ow is not None:
                v = ow.dma_total
            if v > deps.get(k, 0):
                deps[k] = v
        for b in reads:
            add(b.last_w)
        for b in writes:
            add(b.last_w)
            for r in b.reads:
                add(r)
        out = []
        own = id(self.esem[e])
        for k, v in deps.items():
            if k == own and (e in ("pe", "sp") or not SAME_ENGINE_WAITS):
                continue
            if self.seen[e].get(k, 0) >= v:
                continue
            self.seen[e][k] = v
            out.append((self.semobj[k], v))
        return out

    def op(self, e, fn, reads=(), writes=()):
        waits = self._deps(e, reads, writes)
        for s, v in waits:
            tb = self._sem_owner.get(id(s))
            if tb is not None:
                tb.dma_dirty = True
        self.cnt[e] += 1
        tok = (self.esem[e], self.cnt[e])
        self.q[e].append((waits, _bind(fn), tok[0], 1))
        for b in reads:
            b.reads.append(tok)
        for b in writes:
            b.last_w = tok
            b.reads = []
        return tok

    def dma(self, e, fn, owner, reads=(), writes=()):
        self._sem_owner[id(owner.sem)] = owner
        waits = self._deps(e, reads, writes)
        if owner.dma_dirty and owner.dma_total > 0:
            k = id(owner.sem)
            if self.seen[e].get(k, 0) < owner.dma_total:
                self.seen[e][k] = owner.dma_total
                waits.append((owner.sem, owner.dma_total))
            owner.dma_dirty = False
        for s, v in waits:
            tb = self._sem_owner.get(id(s))
            if tb is not None and tb is not owner:
                tb.dma_dirty = True
        owner.dma_total += 16
        tok = (owner.sem, owner.dma_total)
        self.q[e].append((waits, _bind(fn), owner.sem, 16))
        for b in reads:
            b.reads.append(tok)
        for b in writes:
            b.last_w = tok
            b.reads = []
        return tok

    def barrier(self, extra=()):
        toks = [(self.esem[e], self.cnt[e]) for e in self.ENG if self.cnt[e] > 0 and e != "sp"]
        for tb in list(self._sem_owner.values()) + list(extra):
            if tb.dma_total > 0:
                toks.append((tb.sem, tb.dma_total))
        for e in self.ENG:
            waits = []
            for s, v in toks:
                k = id(s)
                if k == id(self.esem[e]):
                    continue
                if self.seen[e].get(k, 0) >= v:
                    continue
                self.seen[e][k] = v
                waits.append((s, v))
            if waits:
                self.q[e].append((waits, None, None, 0))

    def emit(self, final_waits=()):
        nc = self.nc
        engs = {"pe": "tensor", "act": "scalar", "dve": "vector", "pool": "gpsimd", "sp": "sync"}
        with nc.Block() as block:
            for e in self.ENG:
                items = self.q[e]
                fw = list(final_waits) if e == "sp" else []

                def body(eng, items=items, fw=fw):
                    for waits, fn, sem, inc in items:
                        for s, v in waits:
                            eng.wait_ge(s, v)
                        if fn is not None:
                            fn(eng).then_inc(sem, inc)
                    for s, v in fw:
                        eng.wait_ge(s, v)
                getattr(block, engs[e])(body)
        self.q = {e: [] for e in self.ENG}


class _Rec:
    def __init__(self):
        self.call = None

    def __getattr__(self, name):
        def f(*a, **k):
            self.call = (name, a, k)
            return self
        return f


def _bind(fn):
    r = _Rec()
    fn(r)
    assert r.call is not None
    name, a, k = r.call
    return lambda eng: getattr(eng, name)(*a, **k)


class Tn:
    def __init__(self, kb, name, shape, dt=F32, psum=False, dma=False):
        self.t = kb.ps(name, shape, dt) if psum else kb.sb(name, shape, dt)
        self.b = kb.buf(name, dma=dma)

    def __getitem__(self, k):
        return self.t[k]


def _consts():
    c = {}
    i128 = np.arange(128)
    c["ident"] = np.eye(128, dtype=np.float32)
    c["ones"] = np.ones((128, 128), np.float32)
    c["triu"] = (i128[:, None] <= i128[None, :]).astype(np.float32)
    c["negm"] = np.where(i128[None, :] <= i128[:, None], 0.0, NEG).astype(np.float32)
    sel = np.zeros((128, 128), np.float32); sel[127, :] = 1.0
    c["sel127"] = sel
    p = np.arange(SP); tt = p // SB; bb = p % SB
    sameb = bb[:, None] == bb[None, :]
    tri_s = (sameb & (tt[:, None] <= tt[None, :])).astype(np.float32)
    c["tri_s"] = _pad(tri_s)
    c["negm_s"] = _pad(np.where(sameb & (tt[None, :] <= tt[:, None]), 0.0, NEG).astype(np.float32))
    c["negb_s"] = _pad(np.where(sameb, 0.0, NEG).astype(np.float32))
    c["selend"] = _pad(((tt[:, None] == ST - 1) & sameb).astype(np.float32))
    oh = (bb[:, None] == np.arange(SB)[None, :]).astype(np.float32)
    c["onehotB"] = _pad(oh, cols=16)
    oh0 = ((p[:, None] == np.arange(SB)[None, :])).astype(np.float32)
    c["onehot0"] = _pad(oh0, cols=16)
    c["iota16"] = np.broadcast_to(np.arange(16, dtype=np.float32), (128, 16)).copy()
    wins = (2, 4, 8, 16)
    bc0 = np.zeros((4, 128, 128), np.float32); bc = np.zeros((4, 128, 128), np.float32)
    bp = np.zeros((4, 128, 128), np.float32)
    for g, w in enumerate(wins):
        for t in range(128):
            for j in range(w):
                s = t - j
                if s >= 0:
                    bc[g, s, t] += 1.0 / w
                    bc0[g, s, t] += 1.0 / min(t + 1, w)
                else:
                    bp[g, s + 128, t] += 1.0 / w
            bc[g, t, t] -= 1.0
            bc0[g, t, t] -= 1.0
    c["bandc0"] = bc0.transpose(1, 0, 2).reshape(128, 512)
    c["bandc"] = bc.transpose(1, 0, 2).reshape(128, 512)
    c["bandp"] = bp.transpose(1, 0, 2).reshape(128, 512)
    bsA = np.zeros((4, 128, SP), np.float32); bsB = np.zeros((4, 128, SP), np.float32)
    bsC = np.zeros((4, 128, SP), np.float32)
    for g, w in enumerate(wins):
        for t in range(ST):
            for b in range(SB):
                col = t * SB + b
                for j in range(w):
                    r = 15 + t - j
                    if r >= 15:
                        bsC[g, (r - 15) * SB + b, col] += 1.0 / w
                    elif r >= 8:
                        bsB[g, (r - 8) * SB + b, col] += 1.0 / w
                    else:
                        bsA[g, r * SB + b, col] += 1.0 / w
                bsC[g, t * SB + b, col] -= 1.0
    c["bsA"] = bsA.transpose(1, 0, 2).reshape(128, 4 * SP)
    c["bsB"] = bsB.transpose(1, 0, 2).reshape(128, 4 * SP)
    c["bsC"] = bsC.transpose(1, 0, 2).reshape(128, 4 * SP)
    return c


def _pad(a, cols=None):
    out = np.zeros((128, a.shape[1] if cols is None else cols), np.float32)
    out[: a.shape[0], : a.shape[1]] = a
    return out


_CONST_ORDER = ["ident", "ones", "triu", "negm", "sel127", "tri_s", "negm_s", "negb_s", "selend",
                "onehotB", "onehot0", "iota16", "bandc0", "bandc", "bandp", "bsA", "bsB", "bsC"]


def _const_pack():
    c = _consts()
    offs = {}
    o = 0
    arrs = []
    for k in _CONST_ORDER:
        offs[k] = (o, c[k].shape[1])
        o += c[k].shape[1]
        arrs.append(c[k])
    return np.ascontiguousarray(np.concatenate(arrs, axis=1)), offs


def build_program(n_layers=DEPTH, do_phase2=True):
    cpack, coff = _const_pack()
    NCST = cpack.shape[1]
    nc = bass.Bass("TRN2", target_bir_lowering=False)

    def din(name, shape, dt=F32):
        return nc.dram_tensor(name, list(shape), dt, kind="ExternalInput").ap()

    def dout(name, shape, dt=F32):
        return nc.dram_tensor(name, list(shape), dt, kind="ExternalOutput").ap()

    xp = din("xp", [SEQ, D]); xs = din("xs", [SP, D])
    cp = din("cp", [128, D]); cs = din("cs", [SP, D])
    sC = din("sC", [DEPTH, 4, 128, SB, 128]); snat = din("snat", [DEPTH, SB, 4, 128])
    snT = din("snT", [DEPTH, 4, 128, SB]); sm = din("sm", [DEPTH, SP, 4])
    spA = din("spA", [DEPTH, 128, 256]); spB = din("spB", [DEPTH, 112, 256])
    w_ada = din("w_ada", [DEPTH, D, 6 * D]); b_ada = din("b_ada", [DEPTH, 128, 6 * D])
    w_in = din("w_in", [DEPTH, D, IN_COLS]); b_gate = din("b_gate", [DEPTH, 128, 8])
    mh_g = din("mh_g", [DEPTH, 128, 512]); sgu_g = din("sgu_g", [DEPTH, 128, 256])
    sgu_b = din("sgu_b", [DEPTH, 128, 256]); pscale = din("pscale", [DEPTH, 128, 256])
    w_sT = din("w_sT", [DEPTH, 128, 4, 128]); b_sT = din("b_sT", [DEPTH, 128, 4])
    w_sS = din("w_sS", [DEPTH, SP, 4, SP]); b_sS = din("b_sS", [DEPTH, SP, 4])
    w_pool = din("w_pool", [DEPTH, 64, 4, 64]); w_o = din("w_o", [DEPTH, D, D])
    ln1g = din("ln1g", [DEPTH, 128, D]); ln1b = din("ln1b", [DEPTH, 128, D])
    ln2g = din("ln2g", [DEPTH, 128, D]); ln2b = din("ln2b", [DEPTH, 128, D])
    w_pq = din("w_pq", [DEPTH, D, 2048]); keysT = din("keysT", [DEPTH, 128, 16, 128])
    puv = [din("puv%d" % l, [NEXP, 2 * D]) for l in range(DEPTH)]
    cst_d = din("cst", [128, NCST])

    yp = dout("yp", [SEQ, D]); ys = dout("ys", [SP, D])
    o_pC = dout("pC", [DEPTH, 4, 128, 128]); o_pn = dout("pn", [DEPTH, 4, 128]); o_pm = dout("pm", [DEPTH, 4])
    o_pp = dout("pp", [DEPTH, 15, 256])
    o_nC = dout("nC", [DEPTH, SB, 4, 128, 128]); o_nn = dout("nn", [DEPTH, SB, 4, 128])
    o_nm = dout("nm", [DEPTH, SB, 4]); o_np = dout("npool", [DEPTH, SB, 15, 256])
    o_nv = dout("nv", [DEPTH, SB, ST, 256])

    with ExitStack() as st0:
        kb = KB(nc, st0)
        op = kb.op
        OUT = kb.buf("outs", dma=True)

        def out_dma(dst, src, reads):
            kb.dma("sp", lambda q: q.dma_start(out=dst, in_=src), OUT, reads=reads)

        X = kb.sb("X", [128, NT + 1, D])
        XB = [kb.buf("X%d" % t) for t in range(NT + 1)]
        XL = kb.buf("xload", dma=True)
        CST = Tn(kb, "CST", [128, NCST], dma=True)

        def C(name, P=128, w=None):
            o, n = coff[name]
            return CST[:P, o:o + (n if w is None else w)]

        def Cg(name, g, P, blk, w):
            o, n = coff[name]
            return CST[:P, o + g * blk: o + g * blk + w]

        with nc.allow_non_contiguous_dma(reason="small strided state/param loads"):
            kb.dma("sp", lambda q: q.dma_start(out=CST[:, :], in_=cst_d), CST.b, writes=[CST.b])
            for t in range(NT):
                kb.dma("sp", lambda q, t=t: q.dma_start(out=X[:, t, :], in_=xp[t * 128:(t + 1) * 128, :]),
                       XL, writes=[XB[t]])
            kb.dma("sp", lambda q: q.dma_start(out=X[:SP, NT, :], in_=xs), XL, writes=[XB[NT]])

            for l in range(n_layers):
                with ExitStack() as st1:
                    kb.stack = st1
                    kb.sfx = "_a%d" % l
                    _phase1(nc, kb, l, locals())
                    kb.barrier(extra=[OUT])
                    kb.emit()
                    kb.end_phase()
                if do_phase2:
                    with ExitStack() as st2:
                        kb.stack = st2
                        kb.sfx = "_b%d" % l
                        _phase2(nc, kb, l, locals())
                        kb.barrier(extra=[OUT])
                        kb.emit()
                        kb.end_phase()
            kb.stack = st0
            kb.sfx = ""
            for t in range(NT):
                out_dma(yp[t * 128:(t + 1) * 128, :], X[:, t, :], [XB[t]])
            out_dma(ys, X[:SP, NT, :], [XB[NT]])
            kb.emit(final_waits=[(OUT.sem, OUT.dma_total)])
    return nc, cpack


def _ada(nc, kb, l, E, ADA, P, csrc, off, WCH, PM, hbuf, hT, PT, badac):
    op = kb.op
    C = E["C"]
    w_ada, b_ada = E["w_ada"], E["b_ada"]
    kb.dma("sp", lambda q: q.dma_start(out=hbuf[:P, :], in_=csrc), hbuf.b, writes=[hbuf.b])
    op("act", lambda e: e.activation(out=hbuf[:P, :], in_=hbuf[:P, :], func=AF.Silu), reads=[hbuf.b], writes=[hbuf.b])
    _transpose8(kb, E, hbuf, hT, PT, P)
    wv = w_ada[l].rearrange("(k p) n -> p k n", p=128)
    for c in range(12):
        i = c % 2
        c0 = off + c * 256
        kb.dma("sp", lambda q, i=i, c0=c0: q.dma_start(out=WCH[i][:, :, 0:256], in_=wv[:, :, c0:c0 + 256]),
               WCH[i].b, writes=[WCH[i].b])
        kb.dma("sp", lambda q, i=i, c0=c0: q.dma_start(out=badac[i][:P, :], in_=b_ada[l, :P, c0:c0 + 256]),
               badac[i].b, writes=[badac[i].b])
        for k in range(8):
            op("pe", lambda e, i=i, k=k: e.matmul(PM[i][:P, 0:256], lhsT=hT[:, k, :P], rhs=WCH[i][:, k, 0:256],
                                                   start=(k == 0), stop=(k == 7)),
               reads=[hT.b, WCH[i].b], writes=[PM[i].b] if k in (0, 7) else [])
        add1 = 1.0 if 4 <= c < 8 else 0.0
        op("dve", lambda e, i=i, c=c, add1=add1: e.scalar_tensor_tensor(
            out=ADA[:P, c * 256:(c + 1) * 256], in0=PM[i][:P, 0:256], scalar=add1, in1=badac[i][:P, :],
            op0=ALU.add, op1=ALU.add), reads=[PM[i].b, badac[i].b], writes=[ADA.b])


def _transpose8(kb, E, src, dstT, PT, P, srcb=None):
    op = kb.op
    C = E["C"]
    sb_ = src.b if srcb is None else srcb
    for half in range(2):
        for j in range(4):
            k = half * 4 + j
            op("pe", lambda e, half=half, j=j, k=k: e.transpose(
                out=PT[half][:, j * 128:j * 128 + P], in_=src[:P, k * 128:(k + 1) * 128], identity=C("ident", P, P)),
               reads=[sb_, E["CST"].b], writes=[PT[half].b])
        op("act", lambda e, half=half: e.copy(
            out=dstT[:, half * 4:half * 4 + 4, :P],
            in_=PT[half][:, :].rearrange("p (j c) -> p j c", j=4)[:, :, :P]),
           reads=[PT[half].b], writes=[dstT.b])


def _phase1(nc, kb, l, E):
    op = kb.op
    C, Cg, CST, X, XB = E["C"], E["Cg"], E["CST"], E["X"], E["XB"]
    out_dma = E["out_dma"]
    w_in, w_o = E["w_in"], E["w_o"]

    ADA = Tn(kb, "ADA1", [128, 3072]); ADAs = ADA
    WCH = [Tn(kb, "WCH%d" % i, [128, 8, WCW], dma=True) for i in range(2)]
    badac = [Tn(kb, "bada%d" % i, [128, 256], dma=True) for i in range(2)]
    H = Tn(kb, "H", [128, D], dma=True); HT = Tn(kb, "HT", [128, 8, 128])
    PROJ = Tn(kb, "PROJ", [128, IN_COLS], dma=True)
    Y = Tn(kb, "Y", [128, D])
    PRM = Tn(kb, "PRM", [128, 8 + 512 + 256 * 3 + 2 * D], dma=True)
    WS = Tn(kb, "WS", [128, 4, 128], dma=True); BS = Tn(kb, "BS", [128, 4], dma=True)
    WSs = Tn(kb, "WSs", [128, 4, SP], dma=True); BSs = Tn(kb, "BSs", [128, 4], dma=True)
    WP = Tn(kb, "WP", [64, 4, 64], dma=True)
    PT = [Tn(kb, "PT%d" % i, [128, 512], psum=True) for i in range(2)]
    PM = [Tn(kb, "PM%d" % i, [128, 512], psum=True) for i in range(2)]
    PA = Tn(kb, "PA", [128, 512], psum=True); PB = Tn(kb, "PB", [128, 512], psum=True)
    PC = Tn(kb, "PC", [128, 512], psum=True); PD = Tn(kb, "PD", [128, 512], psum=True)
    SM = Tn(kb, "SM", [128, 64])
    SMs = Tn(kb, "SMs", [128, 4], dma=True)
    MREP = Tn(kb, "MREP", [128, 4])
    CTX = Tn(kb, "CTX", [128, 4, 129], dma=True)
    DG = Tn(kb, "DG", [128, 128]); DL = Tn(kb, "DL", [128, 128]); WI = Tn(kb, "WI", [128, 128])
    AM = Tn(kb, "AM", [128, 128]); AT = Tn(kb, "AT", [128, 128])
    QT = Tn(kb, "QT", [128, 128]); KT = Tn(kb, "KT", [128, 128])
    VX = Tn(kb, "VX", [128, 129]); TOT = Tn(kb, "TOT", [128, 129]); WV = Tn(kb, "WV", [128, 129])
    HN = Tn(kb, "HN", [128, 128]); SG = Tn(kb, "SG", [128, 128]); ST6 = Tn(kb, "ST6", [128, 2, 6])
    OUTC = Tn(kb, "OUTC", [128, 128], dma=True)
    CN = Tn(kb, "CN", [128, SB, 128], dma=True); CTS = Tn(kb, "CTS", [128, SB, 129])
    ZQ = Tn(kb, "ZQ", [128, SB * SP]); RA = Tn(kb, "RA", [128, SB, 128])
    NNAT = Tn(kb, "NNAT", [SB, 4, 128], dma=True); NTH = Tn(kb, "NTH", [128, SB], dma=True)
    WCB = Tn(kb, "WCB", [128, 16]); DECD = Tn(kb, "DECD", [128, 16]); DECR = Tn(kb, "DECR", [128, 16])
    MSO = Tn(kb, "MSO", [SB, 4], dma=True)
    PREV = Tn(kb, "PREV", [128, 256]); PTT = Tn(kb, "PTT", [64, 4, 128])
    SPA = Tn(kb, "SPA", [128, 256], dma=True); SPB = Tn(kb, "SPB", [128, 256], dma=True)
    VN = Tn(kb, "VN", [128, 256], dma=True); VTMP = Tn(kb, "VTMP", [128, 256])

    o_bg, o_mh, o_sg, o_sb, o_ps, o_l1g, o_l1b = 0, 8, 520, 776, 1032, 1288, 1288 + D
    for (o, w, src) in [(o_bg, 8, E["b_gate"]), (o_mh, 512, E["mh_g"]), (o_sg, 256, E["sgu_g"]), (o_sb, 256, E["sgu_b"]),
                        (o_ps, 256, E["pscale"]), (o_l1g, D, E["ln1g"]), (o_l1b, D, E["ln1b"])]:
        kb.dma("sp", lambda q, o=o, w=w, src=src: q.dma_start(out=PRM[:, o:o + w], in_=src[l]), PRM.b, writes=[PRM.b])
    kb.dma("sp", lambda q: q.dma_start(out=WS[:, :, :], in_=E["w_sT"][l]), WS.b, writes=[WS.b])
    kb.dma("sp", lambda q: q.dma_start(out=BS[:, :], in_=E["b_sT"][l]), BS.b, writes=[BS.b])
    kb.dma("sp", lambda q: q.dma_start(out=WSs[:SP, :, :], in_=E["w_sS"][l]), WSs.b, writes=[WSs.b])
    kb.dma("sp", lambda q: q.dma_start(out=BSs[:SP, :], in_=E["b_sS"][l]), BSs.b, writes=[BSs.b])
    kb.dma("sp", lambda q: q.dma_start(out=WP[:, :, :], in_=E["w_pool"][l]), WP.b, writes=[WP.b])
    for g in range(4):
        op("dve", lambda e, g=g: e.tensor_tensor(out=WS[:, g, :], in0=WS[:, g, :], in1=C("triu"), op=ALU.mult),
           reads=[WS.b, CST.b], writes=[WS.b])
        op("dve", lambda e, g=g: e.tensor_tensor(out=WSs[:SP, g, :], in0=WSs[:SP, g, :], in1=C("tri_s", SP, SP), op=ALU.mult),
           reads=[WSs.b, CST.b], writes=[WSs.b])
    op("dve", lambda e: e.memset(CTX[:, :, :], 0.0), writes=[CTX.b])
    op("dve", lambda e: e.memset(MREP[:, :], 0.0), writes=[MREP.b])
    op("dve", lambda e: e.memset(VX[:, :], 1.0), writes=[VX.b])

    _ada(nc, kb, l, E, ADA, 128, E["cp"], 0, WCH, PM, H, HT, PT, badac)

    w_in_v = w_in[l].rearrange("(k p) n -> p k n", p=128)
    w_o_v = w_o[l].rearrange("(k p) n -> p k n", p=128)
    chunks = [(i * 256, 256) for i in range(10)] + [(2560, 264)]
    wctr = [0]

    def stream_mm(wview, c0, w, lhsT, P, evac):
        i = wctr[0] % 2
        wctr[0] += 1
        kb.dma("sp", lambda q: q.dma_start(out=WCH[i][:, :, 0:w], in_=wview[:, :, c0:c0 + w]), WCH[i].b, writes=[WCH[i].b])
        for k in range(8):
            op("pe", lambda e, k=k: e.matmul(PM[i][:P, 0:w], lhsT=lhsT[:, k, :P], rhs=WCH[i][:, k, 0:w],
                                              start=(k == 0), stop=(k == 7)),
               reads=[lhsT.b, WCH[i].b], writes=[PM[i].b] if k in (0, 7) else [])
        evac(PM[i], i)

    for t in range(NT + 1):
        is_s = (t == NT)
        P = SP if is_s else 128
        ada = ADAs if is_s else ADA
        if is_s:
            _ada(nc, kb, l, E, ADAs, SP, E["cs"], 0, WCH, PM, H, HT, PT, badac)
        xt = X[:P, t, :]
        op("dve", lambda e: e.tensor_tensor(out=H[:P, :], in0=xt, in1=ada[:P, 1024:2048], op=ALU.mult),
           reads=[XB[t], ada.b], writes=[H.b])
        op("dve", lambda e: e.tensor_tensor(out=H[:P, :], in0=H[:P, :], in1=ada[:P, 0:1024], op=ALU.add),
           reads=[H.b, ada.b], writes=[H.b])
        _transpose8(kb, E, H, HT, PT, P)
        for (c0, w) in chunks:
            stream_mm(w_in_v, c0, w, HT, P,
                      lambda pm, i, c0=c0, w=w: op("act", lambda e: e.copy(out=PROJ[:P, c0:c0 + w], in_=pm[:P, 0:w]),
                                                   reads=[pm.b], writes=[PROJ.b]))
        tri = C("tri_s", SP, SP) if is_s else C("triu")
        negm = C("negm_s", SP, SP) if is_s else C("negm")
        selE = C("selend", SP, SP) if is_s else C("sel127")
        if is_s:
            kb.dma("sp", lambda q: q.dma_start(out=SMs[:SP, :], in_=E["sm"][l]), SMs.b, writes=[SMs.b])
            kb.dma("sp", lambda q: q.dma_start(out=NNAT[:, :, :], in_=E["snat"][l]), NNAT.b, writes=[NNAT.b])
        mtok = SMs if is_s else MREP
        op("dve", lambda e: e.tensor_tensor(out=SM[:P, 0:8], in0=PROJ[:P, 2048:2056], in1=PRM[:P, o_bg:o_bg + 8], op=ALU.add),
           reads=[PROJ.b, PRM.b], writes=[SM.b])
        op("dve", lambda e: e.scalar_tensor_tensor(out=SM[:P, 8:12], in0=SM[:P, 4:8], scalar=-1.0, in1=SM[:P, 4:8], op0=ALU.mult, op1=ALU.max),
           reads=[SM.b], writes=[SM.b])
        op("act", lambda e: e.activation(out=SM[:P, 12:16], in_=SM[:P, 8:12], func=AF.Exp, scale=-1.0), reads=[SM.b], writes=[SM.b])
        op("act", lambda e: e.activation(out=SM[:P, 12:16], in_=SM[:P, 12:16], func=AF.Ln, bias=1.0, scale=1.0),
           reads=[SM.b], writes=[SM.b])
        op("dve", lambda e: e.tensor_scalar_min(out=SM[:P, 16:20], in0=SM[:P, 4:8], scalar1=0.0), reads=[SM.b], writes=[SM.b])
        op("dve", lambda e: e.tensor_tensor(out=SM[:P, 16:20], in0=SM[:P, 16:20], in1=SM[:P, 12:16], op=ALU.subtract),
           reads=[SM.b], writes=[SM.b])
        op("pe", lambda e: e.matmul(PA[:P, 0:4], lhsT=tri, rhs=SM[:P, 16:20], start=True, stop=True),
           reads=[CST.b, SM.b], writes=[PA.b])
        op("act", lambda e: e.copy(out=SM[:P, 20:24], in_=PA[:P, 0:4]), reads=[PA.b], writes=[SM.b])
        op("dve", lambda e: e.tensor_tensor(out=SM[:P, 24:28], in0=SM[:P, 0:4], in1=SM[:P, 20:24], op=ALU.subtract),
           reads=[SM.b], writes=[SM.b])
        op("pe", lambda e: e.matmul(PA[:P, 8:12], lhsT=selE, rhs=SM[:P, 20:24], start=True, stop=True),
           reads=[CST.b, SM.b], writes=[PA.b])
        op("act", lambda e: e.copy(out=SM[:P, 28:32], in_=PA[:P, 8:12]), reads=[PA.b], writes=[SM.b])

        for hh in range(4):
            qs = PROJ[:P, hh * 128:(hh + 1) * 128]
            ks = PROJ[:P, 512 + hh * 128:512 + (hh + 1) * 128]
            vs = PROJ[:P, 1024 + hh * 128:1024 + (hh + 1) * 128]
            os_ = PROJ[:P, 1536 + hh * 128:1536 + (hh + 1) * 128]
            col = lambda c, hh=hh: SM[:P, c + hh:c + hh + 1]
            S1 = lambda c: SM[:P, c:c + 1]
            if is_s:
                kb.dma("sp", lambda q, hh=hh: q.dma_start(out=CN[:, :, :], in_=E["sC"][l, hh]), CN.b, writes=[CN.b])
                kb.dma("sp", lambda q, hh=hh: q.dma_start(out=NTH[:, :], in_=E["snT"][l, hh]), NTH.b, writes=[NTH.b])
                for j in range(4):
                    pt = PT[j % 2]
                    for jj in range(4):
                        b = j * 4 + jj
                        op("pe", lambda e, b=b, jj=jj, pt=pt: e.transpose(out=pt[:, jj * 128:(jj + 1) * 128], in_=CN[:, b, :],
                                                                       identity=C("ident")),
                           reads=[CN.b, CST.b], writes=[pt.b])
                    op("act", lambda e, j=j, pt=pt: e.copy(out=CTS[:, j * 4:(j + 1) * 4, 0:128],
                                                           in_=pt[:, :].rearrange("p (j c) -> p j c", j=4)),
                       reads=[pt.b], writes=[CTS.b])
                op("dve", lambda e: e.tensor_copy(out=CTS[:, :, 128:129], in_=NTH[:, :].unsqueeze(2)), reads=[NTH.b], writes=[CTS.b])
            op("dve", lambda e, hh=hh: e.tensor_scalar(out=DG[:P, :P], in0=C("ident", P, P), scalar1=col(24), scalar2=None,
                                                       op0=ALU.mult), reads=[SM.b, CST.b], writes=[DG.b])
            op("pe", lambda e: e.matmul(PB[:P, 0:P], lhsT=C("ones", P, P), rhs=DG[:P, :P], start=True, stop=True),
               reads=[DG.b, CST.b], writes=[PB.b])
            if is_s:
                op("dve", lambda e: e.tensor_tensor(out=DL[:P, :P], in0=PB[:P, 0:P], in1=C("negb_s", SP, SP), op=ALU.add),
                   reads=[PB.b, CST.b], writes=[DL.b])
                op("dve", lambda e: e.tensor_reduce(out=S1(32), in_=DL[:P, :P], axis=AX.X, op=ALU.max), reads=[DL.b], writes=[SM.b])
            else:
                op("dve", lambda e: e.tensor_reduce(out=S1(32), in_=PB[:P, 0:P], axis=AX.X, op=ALU.max), reads=[PB.b], writes=[SM.b])
            op("dve", lambda e, hh=hh: e.scalar_tensor_tensor(out=DL[:P, :P], in0=PB[:P, 0:P], scalar=col(20), in1=negm,
                                                              op0=ALU.add, op1=ALU.add),
               reads=[PB.b, SM.b, CST.b], writes=[DL.b])
            op("dve", lambda e: e.tensor_reduce(out=S1(33), in_=DL[:P, :P], axis=AX.X, op=ALU.max), reads=[DL.b], writes=[SM.b])
            op("dve", lambda e, hh=hh: e.tensor_tensor(out=S1(34), in0=col(20), in1=mtok[:P, hh:hh + 1], op=ALU.add),
               reads=[SM.b, mtok.b], writes=[SM.b])
            op("dve", lambda e: e.tensor_tensor(out=S1(35), in0=S1(34), in1=S1(33), op=ALU.max), reads=[SM.b], writes=[SM.b])
            op("dve", lambda e: e.tensor_scalar(out=S1(36), in0=S1(35), scalar1=-1.0, scalar2=None, op0=ALU.mult),
               reads=[SM.b], writes=[SM.b])
            op("act", lambda e: e.activation(out=WI[:P, :P], in_=DL[:P, :P], func=AF.Exp, bias=S1(36), scale=1.0),
               reads=[DL.b, SM.b], writes=[WI.b])
            op("act", lambda e: e.activation(out=S1(37), in_=S1(34), func=AF.Exp, bias=S1(36), scale=1.0), reads=[SM.b], writes=[SM.b])
            op("act", lambda e: e.activation(out=S1(38), in_=S1(36), func=AF.Exp), reads=[SM.b], writes=[SM.b])
            op("pe", lambda e: e.transpose(out=PC[:, 0:P], in_=qs, identity=C("ident", P, P)), reads=[PROJ.b, CST.b], writes=[PC.b])
            op("pe", lambda e: e.transpose(out=PC[:, 128:128 + P], in_=ks, identity=C("ident", P, P)), reads=[PROJ.b, CST.b], writes=[PC.b])
            op("act", lambda e: e.mul(out=QT[:, :P], in_=PC[:, 0:P], mul=128.0 ** -0.5), reads=[PC.b], writes=[QT.b])
            op("act", lambda e: e.copy(out=KT[:, :P], in_=PC[:, 128:128 + P]), reads=[PC.b], writes=[KT.b])
            op("pe", lambda e: e.matmul(PD[:P, 0:P], lhsT=QT[:, :P], rhs=KT[:, :P], start=True, stop=True),
               reads=[QT.b, KT.b], writes=[PD.b])
            op("dve", lambda e: e.tensor_tensor(out=AM[:P, :P], in0=WI[:P, :P], in1=PD[:P, 0:P], op=ALU.mult),
               reads=[WI.b, PD.b], writes=[AM.b])
            op("pe", lambda e: e.transpose(out=PB[:P, 128:128 + P], in_=AM[:P, :P], identity=C("ident", P, P)),
               reads=[AM.b, CST.b], writes=[PB.b])
            op("act", lambda e: e.copy(out=AT[:P, :P], in_=PB[:P, 128:128 + P]), reads=[PB.b], writes=[AT.b])
            op("pool", lambda e: e.tensor_copy(out=VX[:P, 0:128], in_=vs), reads=[PROJ.b], writes=[VX.b])
            op("pe", lambda e: e.matmul(PD[:P, 128:257], lhsT=AT[:P, :P], rhs=VX[:P, :], start=True, stop=True),
               reads=[AT.b, VX.b], writes=[PD.b])
            if is_s:
                op("pool", lambda e: e.memset(ZQ[:, :], 0.0), writes=[ZQ.b])
                for b in range(SB):
                    op("pool", lambda e, b=b: e.tensor_copy(out=ZQ[:, b * SP + b:(b + 1) * SP:SB], in_=QT[:, b:SP:SB]),
                       reads=[QT.b], writes=[ZQ.b])
                for b in range(SB):
                    op("pe", lambda e, b=b: e.matmul(PC[:P, 256:385], lhsT=ZQ[:, b * SP:(b + 1) * SP], rhs=CTS[:, b, :],
                                                     start=(b == 0), stop=(b == SB - 1)),
                       reads=[ZQ.b, CTS.b], writes=[PC.b] if b in (0, SB - 1) else [])
            else:
                op("pe", lambda e, hh=hh: e.matmul(PC[:P, 256:385], lhsT=QT[:, :P], rhs=CTX[:, hh, :], start=True, stop=True),
                   reads=[QT.b, CTX.b], writes=[PC.b])
            op("act", lambda e: e.activation(out=TOT[:P, :], in_=PC[:P, 256:385], func=AF.Identity, scale=S1(37)),
               reads=[PC.b, SM.b], writes=[TOT.b])
            op("dve", lambda e: e.tensor_tensor(out=TOT[:P, :], in0=TOT[:P, :], in1=PD[:P, 128:257], op=ALU.add),
               reads=[TOT.b, PD.b], writes=[TOT.b])
            op("dve", lambda e: e.scalar_tensor_tensor(out=S1(39), in0=TOT[:P, 128:129], scalar=-1.0, in1=TOT[:P, 128:129], op0=ALU.mult, op1=ALU.max),
               reads=[TOT.b], writes=[SM.b])
            op("dve", lambda e: e.tensor_tensor(out=S1(39), in0=S1(39), in1=S1(38), op=ALU.max), reads=[SM.b], writes=[SM.b])
            op("dve", lambda e: e.reciprocal(out=S1(40), in_=S1(39)), reads=[SM.b], writes=[SM.b])
            op("dve", lambda e: e.tensor_scalar(out=HN[:P, :], in0=TOT[:P, 0:128], scalar1=S1(40), scalar2=None, op0=ALU.mult),
               reads=[TOT.b, SM.b], writes=[HN.b])
            op("dve", lambda e: e.bn_stats(out=ST6[:P, 0, :], in_=HN[:P, :]), reads=[HN.b], writes=[ST6.b])
            op("dve", lambda e: e.bn_aggr(out=SM[:P, 41:43], in_=ST6[:P, 0, :]), reads=[ST6.b], writes=[SM.b])
            op("act", lambda e: e.activation(out=S1(43), in_=S1(42), func=AF.Sqrt, bias=LN_EPS, scale=1.0), reads=[SM.b], writes=[SM.b])
            op("dve", lambda e: e.reciprocal(out=S1(44), in_=S1(43)), reads=[SM.b], writes=[SM.b])
            op("dve", lambda e: e.tensor_scalar(out=HN[:P, :], in0=HN[:P, :], scalar1=S1(41), scalar2=S1(44),
                                                op0=ALU.subtract, op1=ALU.mult), reads=[HN.b, SM.b], writes=[HN.b])
            op("pool", lambda e, hh=hh: e.tensor_tensor(out=HN[:P, :], in0=HN[:P, :],
                                                        in1=PRM[:P, o_mh + hh * 128:o_mh + (hh + 1) * 128], op=ALU.mult),
               reads=[HN.b, PRM.b], writes=[HN.b])
            op("act", lambda e: e.activation(out=SG[:P, :], in_=os_, func=AF.Sigmoid), reads=[PROJ.b], writes=[SG.b])
            op("dve", lambda e, hh=hh: e.tensor_tensor(out=Y[:P, hh * 128:(hh + 1) * 128], in0=HN[:P, :], in1=SG[:P, :], op=ALU.mult),
               reads=[HN.b, SG.b], writes=[Y.b])
            op("dve", lambda e, hh=hh: e.tensor_tensor(out=S1(45), in0=mtok[:P, hh:hh + 1], in1=S1(32), op=ALU.max),
               reads=[SM.b, mtok.b], writes=[SM.b])
            op("dve", lambda e, hh=hh: e.tensor_tensor(out=S1(45), in0=S1(45), in1=col(28), op=ALU.add), reads=[SM.b], writes=[SM.b])
            op("dve", lambda e, hh=hh: e.tensor_tensor(out=S1(46), in0=col(28), in1=S1(45), op=ALU.subtract), reads=[SM.b], writes=[SM.b])
            op("act", lambda e, hh=hh: e.activation(out=S1(47), in_=col(24), func=AF.Exp, bias=S1(46), scale=1.0),
               reads=[SM.b], writes=[SM.b])
            op("act", lambda e, hh=hh: e.activation(out=S1(48), in_=mtok[:P, hh:hh + 1], func=AF.Exp, bias=S1(46), scale=1.0),
               reads=[SM.b, mtok.b], writes=[SM.b])
            if not is_s:
                op("dve", lambda e: e.tensor_scalar(out=WV[:P, :], in0=VX[:P, :], scalar1=S1(47), scalar2=None, op0=ALU.mult),
                   reads=[VX.b, SM.b], writes=[WV.b])
                op("pe", lambda e: e.matmul(PB[:, 256:385], lhsT=ks, rhs=WV[:P, :], start=True, stop=True),
                   reads=[PROJ.b, WV.b], writes=[PB.b])
                op("dve", lambda e, hh=hh: e.scalar_tensor_tensor(out=CTX[:, hh, :], in0=CTX[:, hh, :], scalar=S1(48), in1=PB[:, 256:385],
                                                                  op0=ALU.mult, op1=ALU.add),
                   reads=[CTX.b, SM.b, PB.b], writes=[CTX.b])
                op("dve", lambda e, hh=hh: e.tensor_copy(out=MREP[:, hh:hh + 1], in_=S1(45)), reads=[SM.b], writes=[MREP.b])
                if t == NT - 1:
                    op("pe", lambda e, hh=hh: e.transpose(out=PA[:, 128:256], in_=CTX[:, hh, 0:128], identity=C("ident")),
                       reads=[CTX.b, CST.b], writes=[PA.b])
                    op("act", lambda e: e.copy(out=OUTC[:, :], in_=PA[:, 128:256]), reads=[PA.b], writes=[OUTC.b])
                    out_dma(E["o_pC"][l, hh], OUTC[:, :], [OUTC.b])
                    out_dma(E["o_pn"][l, hh].rearrange("(k o) -> k o", o=1), CTX[:, hh, 128:129], [CTX.b])
                    if hh == 3:
                        out_dma(E["o_pm"][l:l + 1, :], MREP[0:1, :], [MREP.b])
            else:
                op("dve", lambda e: e.tensor_scalar(out=WCB[:P, :], in0=C("onehotB", SP), scalar1=S1(47), scalar2=None, op0=ALU.mult),
                   reads=[SM.b, CST.b], writes=[WCB.b])
                op("dve", lambda e: e.tensor_tensor(out=RA[:P, :, :], in0=vs.unsqueeze(1).broadcast_to([P, SB, 128]),
                                                    in1=WCB[:P, :].unsqueeze(2).broadcast_to([P, SB, 128]), op=ALU.mult),
                   reads=[PROJ.b, WCB.b], writes=[RA.b])
                op("dve", lambda e: e.tensor_scalar(out=DECD[:P, :], in0=C("onehot0", SP), scalar1=S1(48), scalar2=None, op0=ALU.mult),
                   reads=[SM.b, CST.b], writes=[DECD.b])
                op("pe", lambda e: e.matmul(PA[:, 16:32], lhsT=C("ones", SP, 128), rhs=DECD[:P, :], start=True, stop=True),
                   reads=[DECD.b, CST.b], writes=[PA.b])
                op("act", lambda e: e.copy(out=DECR[:, :], in_=PA[:, 16:32]), reads=[PA.b], writes=[DECR.b])
                for b in range(SB):
                    pq = [PA, PB, PC, PD][b % 4]
                    op("pe", lambda e, b=b, pq=pq: e.matmul(pq[:, 384:512], lhsT=RA[:P, b, :], rhs=ks, start=True, stop=True),
                       reads=[RA.b, PROJ.b], writes=[pq.b])
                    op("dve", lambda e, b=b, pq=pq: e.scalar_tensor_tensor(out=CN[:, b, :], in0=CN[:, b, :], scalar=DECR[:, b:b + 1],
                                                                           in1=pq[:, 384:512], op0=ALU.mult, op1=ALU.add),
                       reads=[CN.b, DECR.b, pq.b], writes=[CN.b])
                out_dma(E["o_nC"][l, :, hh].rearrange("b v k -> v b k"), CN[:, :, :], [CN.b])
                op("pe", lambda e: e.matmul(PA[:SB, 32:160], lhsT=WCB[:P, :], rhs=ks, start=True, stop=True),
                   reads=[WCB.b, PROJ.b], writes=[PA.b])
                op("dve", lambda e, hh=hh: e.scalar_tensor_tensor(out=NNAT[:, hh, :], in0=NNAT[:, hh, :], scalar=SM[:SB, 48:49],
                                                                  in1=PA[:SB, 32:160], op0=ALU.mult, op1=ALU.add),
                   reads=[NNAT.b, SM.b, PA.b], writes=[NNAT.b])
                op("dve", lambda e, hh=hh: e.tensor_copy(out=MSO[:, hh:hh + 1], in_=SM[:SB, 45:46]), reads=[SM.b], writes=[MSO.b])
                if hh == 3:
                    out_dma(E["o_nn"][l], NNAT[:, :, :], [NNAT.b])
                    out_dma(E["o_nm"][l], MSO[:, :], [MSO.b])

        vsv = PROJ[:P, 2312:2568].rearrange("p (g d) -> p g d", g=4)
        op("dve", lambda e: e.tensor_reduce(out=SM[:P, 50:54], in_=vsv, axis=AX.X, op=ALU.add), reads=[PROJ.b], writes=[SM.b])
        op("dve", lambda e: e.tensor_scalar(out=SM[:P, 50:54], in0=SM[:P, 50:54], scalar1=1.0 / 64, scalar2=None, op0=ALU.mult),
           reads=[SM.b], writes=[SM.b])
        op("dve", lambda e: e.tensor_tensor(out=VN[:P, :].rearrange("p (g d) -> p g d", g=4), in0=vsv,
                                            in1=SM[:P, 50:54].unsqueeze(2).broadcast_to([P, 4, 64]), op=ALU.subtract),
           reads=[PROJ.b, SM.b], writes=[VN.b])
        op("pool", lambda e: e.tensor_tensor(out=VTMP[:P, :], in0=VN[:P, :], in1=VN[:P, :], op=ALU.mult), reads=[VN.b], writes=[VTMP.b])
        op("dve", lambda e: e.tensor_reduce(out=SM[:P, 54:58], in_=VTMP[:P, :].rearrange("p (g d) -> p g d", g=4), axis=AX.X, op=ALU.add),
           reads=[VTMP.b], writes=[SM.b])
        op("act", lambda e: e.activation(out=SM[:P, 54:58], in_=SM[:P, 54:58], func=AF.Sqrt, bias=LN_EPS, scale=1.0 / 64),
           reads=[SM.b], writes=[SM.b])
        op("dve", lambda e: e.reciprocal(out=SM[:P, 58:62], in_=SM[:P, 54:58]), reads=[SM.b], writes=[SM.b])
        op("dve", lambda e: e.tensor_tensor(out=VN[:P, :].rearrange("p (g d) -> p g d", g=4), in0=VN[:P, :].rearrange("p (g d) -> p g d", g=4),
                                            in1=SM[:P, 58:62].unsqueeze(2).broadcast_to([P, 4, 64]), op=ALU.mult),
           reads=[VN.b, SM.b], writes=[VN.b])
        op("pool", lambda e: e.tensor_tensor(out=VN[:P, :], in0=VN[:P, :], in1=PRM[:P, o_sg:o_sg + 256], op=ALU.mult),
           reads=[VN.b, PRM.b], writes=[VN.b])
        op("pool", lambda e: e.tensor_tensor(out=VN[:P, :], in0=VN[:P, :], in1=PRM[:P, o_sb:o_sb + 256], op=ALU.add),
           reads=[VN.b, PRM.b], writes=[VN.b])
        wsl = WSs if is_s else WS
        bsl = BSs if is_s else BS
        for g in range(4):
            op("pe", lambda e, g=g: e.matmul(PC[:P, g * 64:(g + 1) * 64], lhsT=wsl[:P, g, :P], rhs=VN[:P, g * 64:(g + 1) * 64],
                                             start=True, stop=True), reads=[wsl.b, VN.b], writes=[PC.b])
        for g in range(4):
            op("dve", lambda e, g=g: e.scalar_tensor_tensor(out=Y[:P, 512 + g * 64:512 + (g + 1) * 64], in0=PC[:P, g * 64:(g + 1) * 64],
                                                            scalar=bsl[:P, g:g + 1], in1=PROJ[:P, 2056 + g * 64:2056 + (g + 1) * 64],
                                                            op0=ALU.add, op1=ALU.mult),
               reads=[PC.b, bsl.b, PROJ.b], writes=[Y.b])
        if is_s:
            for tq in range(ST):
                out_dma(E["o_nv"][l][:, tq, :], VN[tq * SB:(tq + 1) * SB, :], [VN.b])

        pin = lambda g: PROJ[:P, 2568 + g * 64:2568 + (g + 1) * 64]
        if is_s:
            kb.dma("sp", lambda q: q.dma_start(out=SPA[:, :], in_=E["spA"][l]), SPA.b, writes=[SPA.b])
            kb.dma("sp", lambda q: q.dma_start(out=SPB[:112, :], in_=E["spB"][l]), SPB.b, writes=[SPB.b])
            for g in range(4):
                op("pe", lambda e, g=g: e.matmul(PA[:64, g * 128:g * 128 + P], lhsT=SPA[:, g * 64:(g + 1) * 64], rhs=Cg("bsA", g, 128, SP, SP),
                                                 start=True, stop=False), reads=[SPA.b, CST.b], writes=[PA.b])
                op("pe", lambda e, g=g: e.matmul(PA[:64, g * 128:g * 128 + P], lhsT=SPB[:112, g * 64:(g + 1) * 64], rhs=Cg("bsB", g, 112, SP, SP),
                                                 start=False, stop=False), reads=[SPB.b, CST.b], writes=[])
                op("pe", lambda e, g=g: e.matmul(PA[:64, g * 128:g * 128 + P], lhsT=pin(g), rhs=Cg("bsC", g, SP, SP, SP),
                                                 start=False, stop=True), reads=[PROJ.b, CST.b], writes=[PA.b])
            npv = E["o_np"][l].rearrange("b r c -> r b c")
            for r in range(4):
                out_dma(npv[r], SPA[64 + r * SB:64 + (r + 1) * SB, :], [SPA.b])
            for r in range(7):
                out_dma(npv[4 + r], SPB[r * SB:(r + 1) * SB, :], [SPB.b])
            for r in range(4):
                out_dma(npv[11 + r], PROJ[r * SB:(r + 1) * SB, 2568:2824], [PROJ.b])
        else:
            for g in range(4):
                band = Cg("bandc0" if t == 0 else "bandc", g, 128, 128, 128)
                op("pe", lambda e, g=g, band=band: e.matmul(PA[:64, g * 128:(g + 1) * 128], lhsT=pin(g), rhs=band, start=True, stop=(t == 0)),
                   reads=[PROJ.b, CST.b], writes=[PA.b])
                if t > 0:
                    op("pe", lambda e, g=g: e.matmul(PA[:64, g * 128:(g + 1) * 128], lhsT=PREV[:, g * 64:(g + 1) * 64],
                                                     rhs=Cg("bandp", g, 128, 128, 128), start=False, stop=True),
                       reads=[PREV.b, CST.b], writes=[PA.b])
            if t < NT - 1:
                op("pool", lambda e: e.tensor_copy(out=PREV[:, :], in_=PROJ[:, 2568:2824]), reads=[PROJ.b], writes=[PREV.b])
            else:
                out_dma(E["o_pp"][l], PROJ[113:128, 2568:2824], [PROJ.b])
        op("act", lambda e: e.copy(out=PTT[:, :, :P], in_=PA[:64, :].rearrange("p (g c) -> p g c", g=4)[:, :, :P]),
           reads=[PA.b], writes=[PTT.b])
        for g in range(4):
            op("pe", lambda e, g=g: e.matmul(PB[:P, g * 64:(g + 1) * 64], lhsT=PTT[:, g, :P], rhs=WP[:, g, :], start=True, stop=True),
               reads=[PTT.b, WP.b], writes=[PB.b])
        op("dve", lambda e: e.tensor_tensor(out=Y[:P, 768:1024], in0=PB[:P, 0:256], in1=PRM[:P, o_ps:o_ps + 256], op=ALU.mult),
           reads=[PB.b, PRM.b], writes=[Y.b])

        _transpose8(kb, E, Y, HT, PT, P)
        for c in range(4):
            stream_mm(w_o_v, c * 256, 256, HT, P,
                      lambda pm, i, c=c: op("dve", lambda e: e.tensor_tensor(out=H[:P, c * 256:(c + 1) * 256], in0=pm[:P, 0:256],
                                                                            in1=ada[:P, 2048 + c * 256:2048 + (c + 1) * 256], op=ALU.mult),
                                            reads=[pm.b, ada.b], writes=[H.b]))
        _resid_ln(kb, X, XB[t], t, P, H, SM, ST6, PRM, o_l1g, o_l1b)


def _resid_ln(kb, X, xb, t, P, Z, SM, ST6, PRM, og, ob):
    op = kb.op
    xt = X[:P, t, :]
    op("dve", lambda e: e.scalar_tensor_tensor(out=Z[:P, :], in0=xt, scalar=ALPHA, in1=Z[:P, :], op0=ALU.mult, op1=ALU.add),
       reads=[xb, Z.b], writes=[Z.b])
    op("dve", lambda e: e.bn_stats(out=ST6[:P, 0, :], in_=Z[:P, 0:512]), reads=[Z.b], writes=[ST6.b])
    op("dve", lambda e: e.bn_stats(out=ST6[:P, 1, :], in_=Z[:P, 512:1024]), reads=[Z.b], writes=[ST6.b])
    op("dve", lambda e: e.bn_aggr(out=SM[:P, 41:43], in_=ST6[:P, :, :].rearrange("p a b -> p (a b)")), reads=[ST6.b], writes=[SM.b])
    op("act", lambda e: e.activation(out=SM[:P, 43:44], in_=SM[:P, 42:43], func=AF.Sqrt, bias=LN_EPS, scale=1.0), reads=[SM.b], writes=[SM.b])
    op("dve", lambda e: e.reciprocal(out=SM[:P, 44:45], in_=SM[:P, 43:44]), reads=[SM.b], writes=[SM.b])
    op("dve", lambda e: e.tensor_scalar(out=Z[:P, :], in0=Z[:P, :], scalar1=SM[:P, 41:42], scalar2=SM[:P, 44:45],
                                        op0=ALU.subtract, op1=ALU.mult), reads=[Z.b, SM.b], writes=[Z.b])
    op("pool", lambda e: e.tensor_tensor(out=Z[:P, :], in0=Z[:P, :], in1=PRM[:P, og:og + D], op=ALU.mult), reads=[Z.b, PRM.b], writes=[Z.b])
    op("dve", lambda e: e.tensor_tensor(out=xt, in0=Z[:P, :], in1=PRM[:P, ob:ob + D], op=ALU.add), reads=[Z.b, PRM.b], writes=[xb])


def _phase2(nc, kb, l, E):
    op = kb.op
    C, CST, X, XB = E["C"], E["CST"], E["X"], E["XB"]
    ADA = Tn(kb, "ADA2", [128, 3072]); ADAs = ADA
    WCH = [Tn(kb, "WCHb%d" % i, [128, 8, 256], dma=True) for i in range(2)]
    badac = [Tn(kb, "badab%d" % i, [128, 256], dma=True) for i in range(2)]
    H = Tn(kb, "H2", [128, D], dma=True); HT = Tn(kb, "H2T", [128, 8, 128])
    PRM = Tn(kb, "PRM2", [128, 2 * D], dma=True)
    KTS = Tn(kb, "KTS", [128, 16, 128], dma=True)
    S0 = Tn(kb, "S0", [128, 2048]); S1_ = Tn(kb, "S1", [128, 2048]); S2 = Tn(kb, "S2", [128, 2048])
    TOPS = Tn(kb, "TOPS", [128, 16, 16]); IDXU = Tn(kb, "IDXU", [128, 16, 16], U32); IDXF = Tn(kb, "IDXF", [128, 16, 16])
    CV = Tn(kb, "CV", [128, 8, 16]); CPOS = Tn(kb, "CPOS", [128, 8, 16], U32)
    PAU = Tn(kb, "PAU", [128, 8, 16], U32); PBU = Tn(kb, "PBU", [128, 8, 16], U32)
    PAF = Tn(kb, "PAF", [128, 8, 16]); PBF = Tn(kb, "PBF", [128, 8, 16])
    I1 = Tn(kb, "I1", [128, 128]); I2 = Tn(kb, "I2", [128, 128])
    IDX = Tn(kb, "IDX", [128, 128], I32); GATE = Tn(kb, "GATE", [128, 128])
    ACTV = Tn(kb, "ACTV", [128, 128]); COEF = Tn(kb, "COEF", [128, 128])
    SM = Tn(kb, "SM2", [128, 64]); ST6 = Tn(kb, "ST62", [128, 2, 6])
    NB = 4
    UB = [Tn(kb, "UB%d" % i, [128, 2 * D], dma=True) for i in range(NB)]
    ACTB = [kb.buf("actv%d" % i) for i in range(NB)]
    COEFB = [kb.buf("coef%d" % i) for i in range(NB)]
    COEF2 = Tn(kb, "COEF2", [128, 128])
    ACC = Tn(kb, "ACC", [128, D])
    PT = [Tn(kb, "PTb%d" % i, [128, 512], psum=True) for i in range(2)]
    PM = [Tn(kb, "PMb%d" % i, [128, 512], psum=True) for i in range(2)]
    PQ = [Tn(kb, "PQ%d" % i, [128, 512], psum=True) for i in range(2)]
    PS = [Tn(kb, "PS%d" % i, [128, 512], psum=True) for i in range(2)]

    kb.dma("sp", lambda q: q.dma_start(out=PRM[:, 0:D], in_=E["ln2g"][l]), PRM.b, writes=[PRM.b])
    kb.dma("sp", lambda q: q.dma_start(out=PRM[:, D:2 * D], in_=E["ln2b"][l]), PRM.b, writes=[PRM.b])
    kb.dma("sp", lambda q: q.dma_start(out=KTS[:, :, :], in_=E["keysT"][l]), KTS.b, writes=[KTS.b])
    _ada(nc, kb, l, E, ADA, 128, E["cp"], 3072, WCH, PM, H, HT, PT, badac)
    wpq_v = E["w_pq"][l].rearrange("(k p) n -> p k n", p=128)
    QT = S0
    wctr = 0
    for t in range(NT + 1):
        is_s = (t == NT)
        P = SP if is_s else 128
        ada = ADAs if is_s else ADA
        if is_s:
            _ada(nc, kb, l, E, ADAs, SP, E["cs"], 3072, WCH, PM, H, HT, PT, badac)
        xt = X[:P, t, :]
        op("dve", lambda e: e.tensor_tensor(out=H[:P, :], in0=xt, in1=ada[:P, 1024:2048], op=ALU.mult), reads=[XB[t], ada.b], writes=[H.b])
        op("dve", lambda e: e.tensor_tensor(out=H[:P, :], in0=H[:P, :], in1=ada[:P, 0:1024], op=ALU.add), reads=[H.b, ada.b], writes=[H.b])
        _transpose8(kb, E, H, HT, PT, P)
        for cc in range(8):
            i = wctr % 2
            wctr += 1
            kb.dma("sp", lambda q, i=i, cc=cc: q.dma_start(out=WCH[i][:, :, :], in_=wpq_v[:, :, cc * 256:(cc + 1) * 256]),
                   WCH[i].b, writes=[WCH[i].b])
            for j in range(2):
                c = cc * 2 + j
                for k in range(8):
                    op("pe", lambda e, i=i, j=j, k=k: e.matmul(PQ[j][:, 0:P], lhsT=WCH[i][:, k, j * 128:(j + 1) * 128], rhs=HT[:, k, :P],
                                                               start=(k == 0), stop=(k == 7)),
                       reads=[WCH[i].b, HT.b], writes=[PQ[j].b] if k in (0, 7) else [])
                op("act", lambda e, j=j, c=c: e.copy(out=QT[:, c * 128:c * 128 + P], in_=PQ[j][:, 0:P]), reads=[PQ[j].b], writes=[QT.b])
        for c4 in range(4):
            ps = PS[c4 % 2]
            for j in range(4):
                c = c4 * 4 + j
                op("pe", lambda e, ps=ps, j=j, c=c: e.matmul(ps[:P, j * 128:(j + 1) * 128], lhsT=QT[:, c * 128:c * 128 + P], rhs=KTS[:, c, :],
                                                             start=True, stop=True), reads=[QT.b, KTS.b], writes=[ps.b])
            op("act", lambda e, ps=ps, c4=c4: e.copy(out=S1_[:P, c4 * 512:(c4 + 1) * 512], in_=ps[:P, :]), reads=[ps.b], writes=[S1_.b])
        for c in range(16):
            sc = S1_[:P, c * 128:(c + 1) * 128]
            wk = S2[:P, c * 128:(c + 1) * 128]
            op("dve", lambda e, c=c, sc=sc: e.max(out=TOPS[:P, c, 0:8], in_=sc), reads=[S1_.b], writes=[TOPS.b])
            op("dve", lambda e, c=c, sc=sc: e.max_index(out=IDXU[:P, c, 0:8], in_max=TOPS[:P, c, 0:8], in_values=sc),
               reads=[S1_.b, TOPS.b], writes=[IDXU.b])
            op("dve", lambda e, c=c, sc=sc, wk=wk: e.match_replace(out=wk, in_to_replace=TOPS[:P, c, 0:8], in_values=sc, imm_value=NEG),
               reads=[S1_.b, TOPS.b], writes=[S2.b])
            op("dve", lambda e, c=c, wk=wk: e.max(out=TOPS[:P, c, 8:16], in_=wk), reads=[S2.b], writes=[TOPS.b])
            op("dve", lambda e, c=c, wk=wk: e.max_index(out=IDXU[:P, c, 8:16], in_max=TOPS[:P, c, 8:16], in_values=wk),
               reads=[S2.b, TOPS.b], writes=[IDXU.b])
        op("dve", lambda e: e.tensor_copy(out=IDXF[:P, :, :], in_=IDXU[:P, :, :]), reads=[IDXU.b], writes=[IDXF.b])
        tv = TOPS[:P, :, :].rearrange("p (h two) k -> p h two k", two=2)
        CAND = S0
        op("dve", lambda e: e.tensor_tensor(out=CAND[:P, :].rearrange("p (h a b) -> p h a b", h=8, a=16),
                                            in0=tv[:, :, 0, :].unsqueeze(3).broadcast_to([P, 8, 16, 16]),
                                            in1=tv[:, :, 1, :].unsqueeze(2).broadcast_to([P, 8, 16, 16]), op=ALU.add),
           reads=[TOPS.b], writes=[S0.b])
        for h in range(8):
            cd = CAND[:P, h * 256:(h + 1) * 256]
            wk = S2[:P, h * 256:(h + 1) * 256]
            op("dve", lambda e, h=h, cd=cd: e.max(out=CV[:P, h, 0:8], in_=cd), reads=[S0.b], writes=[CV.b])
            op("dve", lambda e, h=h, cd=cd: e.max_index(out=CPOS[:P, h, 0:8], in_max=CV[:P, h, 0:8], in_values=cd),
               reads=[S0.b, CV.b], writes=[CPOS.b])
            op("dve", lambda e, h=h, cd=cd, wk=wk: e.match_replace(out=wk, in_to_replace=CV[:P, h, 0:8], in_values=cd, imm_value=NEG),
               reads=[S0.b, CV.b], writes=[S2.b])
            op("dve", lambda e, h=h, wk=wk: e.max(out=CV[:P, h, 8:16], in_=wk), reads=[S2.b], writes=[CV.b])
            op("dve", lambda e, h=h, wk=wk: e.max_index(out=CPOS[:P, h, 8:16], in_max=CV[:P, h, 8:16], in_values=wk),
               reads=[S2.b, CV.b], writes=[CPOS.b])
        op("dve", lambda e: e.tensor_single_scalar(out=PAU[:P, :, :], in_=CPOS[:P, :, :], scalar=4, op=ALU.logical_shift_right),
           reads=[CPOS.b], writes=[PAU.b])
        op("dve", lambda e: e.tensor_single_scalar(out=PBU[:P, :, :], in_=CPOS[:P, :, :], scalar=15, op=ALU.bitwise_and),
           reads=[CPOS.b], writes=[PBU.b])
        op("dve", lambda e: e.tensor_copy(out=PAF[:P, :, :], in_=PAU[:P, :, :]), reads=[PAU.b], writes=[PAF.b])
        op("dve", lambda e: e.tensor_copy(out=PBF[:P, :, :], in_=PBU[:P, :, :]), reads=[PBU.b], writes=[PBF.b])
        iv = IDXF[:P, :, :].rearrange("p (h two) k -> p h two k", two=2)
        io16 = C("iota16", P).unsqueeze(1).unsqueeze(1).broadcast_to([P, 8, 16, 16])
        for (pf, half, dst) in [(PAF, 0, I1), (PBF, 1, I2)]:
            eq = S1_[:P, :].rearrange("p (h k a) -> p h k a", h=8, k=16)
            op("dve", lambda e, pf=pf, eq=eq: e.tensor_tensor(out=eq, in0=pf[:P, :, :].unsqueeze(3).broadcast_to([P, 8, 16, 16]), in1=io16,
                                                              op=ALU.is_equal), reads=[pf.b, CST.b], writes=[S1_.b])
            op("dve", lambda e, half=half, eq=eq: e.tensor_tensor(out=eq, in0=eq, in1=iv[:, :, half, :].unsqueeze(2).broadcast_to([P, 8, 16, 16]),
                                                                  op=ALU.mult), reads=[S1_.b, IDXF.b], writes=[S1_.b])
            op("dve", lambda e, dst=dst, eq=eq: e.tensor_reduce(out=dst[:P, :].rearrange("p (h k) -> p h k", h=8), in_=eq, axis=AX.X, op=ALU.add),
               reads=[S1_.b], writes=[dst.b])
        op("dve", lambda e: e.scalar_tensor_tensor(out=I1[:P, :], in0=I1[:P, :], scalar=128.0, in1=I2[:P, :], op0=ALU.mult, op1=ALU.add),
           reads=[I1.b, I2.b], writes=[I1.b])
        op("dve", lambda e: e.tensor_copy(out=IDX[:P, :], in_=I1[:P, :]), reads=[I1.b], writes=[IDX.b])
        cvv = CV[:P, :, :]
        gv = GATE[:P, :].rearrange("p (h k) -> p h k", h=8)
        op("dve", lambda e: e.tensor_tensor(out=gv, in0=cvv, in1=CV[:P, :, 0:1].broadcast_to([P, 8, 16]), op=ALU.subtract),
           reads=[CV.b], writes=[GATE.b])
        op("act", lambda e: e.activation(out=GATE[:P, :], in_=GATE[:P, :], func=AF.Exp), reads=[GATE.b], writes=[GATE.b])
        op("dve", lambda e: e.tensor_reduce(out=SM[:P, 0:8], in_=gv, axis=AX.X, op=ALU.add), reads=[GATE.b], writes=[SM.b])
        op("dve", lambda e: e.reciprocal(out=SM[:P, 8:16], in_=SM[:P, 0:8]), reads=[SM.b], writes=[SM.b])
        op("dve", lambda e: e.tensor_tensor(out=gv, in0=gv, in1=SM[:P, 8:16].unsqueeze(2).broadcast_to([P, 8, 16]), op=ALU.mult),
           reads=[GATE.b, SM.b], writes=[GATE.b])
        JUNK = S2
        tab = E["puv"][l]

        def axpy(s):
            b = s % NB
            op("dve", lambda e: e.tensor_tensor(out=COEF2[:P, s:s + 1], in0=COEF[:P, s:s + 1], in1=GATE[:P, s:s + 1], op=ALU.mult),
               reads=[COEFB[b], GATE.b], writes=[COEF2.b])
            if s == 0:
                op("dve", lambda e: e.tensor_scalar(out=ACC[:P, :], in0=UB[b][:P, D:2 * D], scalar1=COEF2[:P, s:s + 1], scalar2=None, op0=ALU.mult),
                   reads=[UB[b].b, COEF2.b], writes=[ACC.b])
            else:
                op("dve", lambda e: e.scalar_tensor_tensor(out=ACC[:P, :], in0=UB[b][:P, D:2 * D], scalar=COEF2[:P, s:s + 1], in1=ACC[:P, :],
                                                           op0=ALU.mult, op1=ALU.add), reads=[UB[b].b, COEF2.b, ACC.b], writes=[ACC.b])

        for s_ in range(128):
            b = s_ % NB
            kb.dma("pool", lambda q: q.indirect_dma_start(out=UB[b][:P, :], out_offset=None, in_=tab,
                                                          in_offset=bass.IndirectOffsetOnAxis(ap=IDX[:P, s_:s_ + 1], axis=0)),
                   UB[b].b, reads=[IDX.b], writes=[UB[b].b])
            op("dve", lambda e: e.scalar_tensor_tensor(out=JUNK[:P, 0:D], in0=UB[b][:P, 0:D], scalar=1.0, in1=H[:P, :],
                                                       op0=ALU.mult, op1=ALU.mult, accum_out=ACTV[:P, s_:s_ + 1]),
               reads=[UB[b].b, H.b], writes=[S2.b, ACTB[b]])
            op("act", lambda e: e.activation(out=COEF[:P, s_:s_ + 1], in_=ACTV[:P, s_:s_ + 1], func=AF.Gelu), reads=[ACTB[b]], writes=[COEFB[b]])
            if s_ >= 1:
                axpy(s_ - 1)
        axpy(127)
        op("dve", lambda e: e.tensor_tensor(out=ACC[:P, :], in0=ACC[:P, :], in1=ada[:P, 2048:3072], op=ALU.mult), reads=[ACC.b, ada.b], writes=[ACC.b])
        _resid_ln(kb, X, XB[t], t, P, ACC, SM, ST6, PRM, 0, D)


_CACHE = {}


def _rep(a, P=128):
    return np.ascontiguousarray(np.broadcast_to(a[:, None, :], (a.shape[0], P, a.shape[1])))


def make_in_maps(inp, cpack):
    f = lambda a: np.ascontiguousarray(np.asarray(a, dtype=np.float32))
    shared = {
        "w_ada": f(inp["w_ada"]), "b_ada": _rep(f(inp["b_ada"])), "w_in": f(inp["w_in"]), "b_gate": _rep(f(inp["b_gate"])),
        "mh_g": _rep(f(inp["mh_g"])), "sgu_g": _rep(f(inp["sgu_g"])), "sgu_b": _rep(f(inp["sgu_b"])),
        "pscale": _rep(f(inp["pool_scale"])),
        "w_sT": f(np.asarray(inp["w_s"]).transpose(0, 3, 1, 2)),
        "b_sT": f(np.asarray(inp["b_s"]).transpose(0, 2, 1)),
        "w_pool": f(np.asarray(inp["w_pool"]).transpose(0, 2, 1, 3)),
        "w_o": f(inp["w_o"]), "ln1g": _rep(f(inp["ln1_g"])), "ln1b": _rep(f(inp["ln1_b"])),
        "ln2g": _rep(f(inp["ln2_g"])), "ln2b": _rep(f(inp["ln2_b"])), "w_pq": f(inp["w_pq"]),
        "keysT": f(np.asarray(inp["peer_keys"]).transpose(0, 4, 1, 2, 3).reshape(DEPTH, 128, 16, 128)),
        "cst": cpack,
    }
    ws4 = np.asarray(inp["w_s"])[:, :, :ST, :ST]
    wsS = np.repeat(np.repeat(ws4.transpose(0, 3, 1, 2), SB, axis=1), SB, axis=3)
    shared["w_sS"] = f(wsS)
    bs4 = np.asarray(inp["b_s"])[:, :, :ST]
    shared["b_sS"] = f(np.repeat(bs4.transpose(0, 2, 1), SB, axis=1))
    for l in range(DEPTH):
        shared["puv%d" % l] = np.ascontiguousarray(
            np.concatenate([np.asarray(inp["peer_u"])[l], np.asarray(inp["peer_v"])[l]], axis=1), dtype=np.float32)
    maps = []
    for c in range(NCORES):
        bs = slice(c * SB, (c + 1) * SB)
        m = dict(shared)
        m["xp"] = f(np.asarray(inp["x_prompt"])[c])
        m["xs"] = f(np.asarray(inp["x_sample"])[bs].transpose(1, 0, 2).reshape(SP, D))
        m["cp"] = f(np.broadcast_to(np.asarray(inp["c_prompt"])[c][None, :], (128, D)))
        m["cs"] = f(np.tile(np.asarray(inp["c_sample"])[bs], (ST, 1)))
        sCc = np.asarray(inp["state_mlstm_C"])[:, bs]
        m["sC"] = f(sCc.transpose(0, 2, 3, 1, 4))
        snc = np.asarray(inp["state_mlstm_n"])[:, bs]
        m["snat"] = f(snc)
        m["snT"] = f(snc.transpose(0, 2, 3, 1))
        m["sm"] = f(np.tile(np.asarray(inp["state_mlstm_m"])[:, bs], (1, ST, 1)))
        spc = np.asarray(inp["state_pool"])[:, bs].transpose(0, 2, 1, 3)
        m["spA"] = f(spc[:, 0:8].reshape(DEPTH, 128, 256))
        m["spB"] = f(spc[:, 8:15].reshape(DEPTH, 112, 256))
        maps.append(m)
    return maps


def gather_outputs(results):
    cat = lambda k, ax: np.concatenate([r[k] for r in results], axis=ax)
    yp = np.stack([r["yp"] for r in results], 0)
    ys = np.concatenate([r["ys"].reshape(ST, SB, D).transpose(1, 0, 2) for r in results], 0)
    pC = np.stack([r["pC"] for r in results], 1)
    pn = np.stack([r["pn"] for r in results], 1)
    pm = np.stack([r["pm"] for r in results], 1)
    pp = np.stack([r["pp"] for r in results], 1)
    return (yp, ys, pC, pn, pm, pp, cat("nC", 1), cat("nn", 1), cat("nm", 1), cat("npool", 1), cat("nv", 1))


def kernel(**inputs):
    if "prog" not in _CACHE:
        _CACHE["prog"] = build_program()
    nc, cpack = _CACHE["prog"]
    maps = make_in_maps(inputs, cpack)
    res = run_bass_kernel_spmd(nc, maps, core_ids=list(range(NCORES)))
    outs = gather_outputs(res.results)
    return tuple(np.ascontiguousarray(o, dtype=np.float32) for o in outs)
```

```python
import numpy as np
from contextlib import ExitStack
import concourse.bass as bass
import concourse.mybir as mybir
from concourse.bass_utils import run_bass_kernel_spmd

F32 = mybir.dt.float32
I32 = mybir.dt.int32
U32 = mybir.dt.uint32
ALU = mybir.AluOpType
AF = mybir.ActivationFunctionType
AX = mybir.AxisListType

NCORES = 8
D = 1024
SEQ = 2048
NT = 16
SB = 16
ST = 4
SP = SB * ST
DEPTH = 2
ALPHA = (2 * DEPTH) ** 0.25
LN_EPS = 1e-5
IN_COLS = 2824
NEG = -1.0e30
WCW = 264
NEXP = 16384
SAME_ENGINE_WAITS = True


class TB:
    def __init__(self, name, sem=None):
        self.name = name
        self.last_w = None
        self.reads = []
        self.sem = sem
        self.dma_total = 0
        self.dma_dirty = False


class KB:
    ENG = ("pe", "act", "dve", "pool", "sp")

    def __init__(self, nc, stack):
        self.nc = nc
        self.stack = stack
        self.q = {e: [] for e in self.ENG}
        self.cnt = {e: 0 for e in self.ENG}
        self.esem = {e: stack.enter_context(nc.semaphore("es_" + e)) for e in self.ENG}
        self.seen = {e: {} for e in self.ENG}
        self.semobj = {}
        self._sem_owner = {}
        self.stack0 = stack
        self.phase_tbs = []
        self.sfx = ""

    def new_sem(self, name):
        return self.stack.enter_context(self.nc.semaphore(name + self.sfx))

    def buf(self, name, dma=False):
        tb = TB(name, self.new_sem("d_" + name) if dma else None)
        if dma and self.stack is not self.stack0:
            self.phase_tbs.append(tb)
        return tb

    def end_phase(self):
        for tb in self.phase_tbs:
            k = id(tb.sem)
            self._sem_owner.pop(k, None)
            self.semobj.pop(k, None)
            for e in self.ENG:
                self.seen[e].pop(k, None)
        self.phase_tbs = []

    def sb(self, name, shape, dt=F32):
        return self.stack.enter_context(self.nc.sbuf_tensor(name + self.sfx, list(shape), dt))

    def ps(self, name, shape, dt=F32):
        return self.stack.enter_context(self.nc.psum_tensor(name + self.sfx, list(shape), dt))

    def _deps(self, e, reads, writes):
        deps = {}

        def add(tok):
            if tok is None:
                return
            s, v = tok
            k = id(s)
            self.semobj[k] = s
            ow = self._sem_owner.get(k)
            if ow is not None:
                v = ow.dma_total
            if v > deps.get(k, 0):
                deps[k] = v
        for b in reads:
            add(b.last_w)
        for b in writes:
            add(b.last_w)
            for r in b.reads:
                add(r)
        out = []
        own = id(self.esem[e])
        for k, v in deps.items():
            if k == own and (e in ("pe", "sp") or not SAME_ENGINE_WAITS):
                continue
            if self.seen[e].get(k, 0) >= v:
                continue
            self.seen[e][k] = v
            out.append((self.semobj[k], v))
        return out

    def op(self, e, fn, reads=(), writes=(), bound=False):
        waits = self._deps(e, reads, writes)
        for s, v in waits:
            tb = self._sem_owner.get(id(s))
            if tb is not None:
                tb.dma_dirty = True
        self.cnt[e] += 1
        tok = (self.esem[e], self.cnt[e])
        self.q[e].append((waits, fn if bound else _bind(fn), tok[0], 1))
        for b in reads:
            b.reads.append(tok)
        for b in writes:
            b.last_w = tok
            b.reads = []
        return tok

    def dma(self, e, fn, owner, reads=(), writes=(), bound=False):
        self._sem_owner[id(owner.sem)] = owner
        waits = self._deps(e, reads, writes)
        if owner.dma_dirty and owner.dma_total > 0:
            k = id(owner.sem)
            if self.seen[e].get(k, 0) < owner.dma_total:
                self.seen[e][k] = owner.dma_total
                waits.append((owner.sem, owner.dma_total))
            owner.dma_dirty = False
        for s, v in waits:
            tb = self._sem_owner.get(id(s))
            if tb is not None and tb is not owner:
                tb.dma_dirty = True
        owner.dma_total += 16
        tok = (owner.sem, owner.dma_total)
        self.q[e].append((waits, fn if bound else _bind(fn), owner.sem, 16))
        for b in reads:
            b.reads.append(tok)
        for b in writes:
            b.last_w = tok
            b.reads = []
        return tok

    def barrier(self, extra=()):
        toks = [(self.esem[e], self.cnt[e]) for e in self.ENG if self.cnt[e] > 0 and e != "sp"]
        for tb in list(self._sem_owner.values()) + list(extra):
            if tb.dma_total > 0:
                toks.append((tb.sem, tb.dma_total))
        for e in self.ENG:
            waits = []
            for s, v in toks:
                k = id(s)
                if k == id(self.esem[e]):
                    continue
                if self.seen[e].get(k, 0) >= v:
                    continue
                self.seen[e][k] = v
                waits.append((s, v))
            if waits:
                self.q[e].append((waits, None, None, 0))

    def emit(self, final_waits=()):
        nc = self.nc
        engs = {"pe": "tensor", "act": "scalar", "dve": "vector", "pool": "gpsimd", "sp": "sync"}
        with nc.Block() as block:
            for e in self.ENG:
                items = self.q[e]
                fw = list(final_waits) if e == "sp" else []

                def body(eng, items=items, fw=fw):
                    for waits, fn, sem, inc in items:
                        for s, v in waits:
                            eng.wait_ge(s, v)
                        if fn is not None:
                            fn(eng).then_inc(sem, inc)
                    for s, v in fw:
                        eng.wait_ge(s, v)
                getattr(block, engs[e])(body)
        self.q = {e: [] for e in self.ENG}


class _Rec:
    def __init__(self):
        self.call = None

    def __getattr__(self, name):
        def f(*a, **k):
            self.call = (name, a, k)
            return self
        return f


def _bind(fn):
    r = _Rec()
    fn(r)
    assert r.call is not None
    name, a, k = r.call
    return lambda eng: getattr(eng, name)(*a, **k)


class Tn:
    def __init__(self, kb, name, shape, dt=F32, psum=False, dma=False):
        self.t = kb.ps(name, shape, dt) if psum else kb.sb(name, shape, dt)
        self.b = kb.buf(name, dma=dma)

    def __getitem__(self, k):
        return self.t[k]


def _consts():
    c = {}
    i128 = np.arange(128)
    c["ident"] = np.eye(128, dtype=np.float32)
    c["ones"] = np.ones((128, 128), np.float32)
    c["triu"] = (i128[:, None] <= i128[None, :]).astype(np.float32)
    c["negm"] = np.where(i128[None, :] <= i128[:, None], 0.0, NEG).astype(np.float32)
    sel = np.zeros((128, 128), np.float32); sel[127, :] = 1.0
    c["sel127"] = sel
    p = np.arange(SP); tt = p // SB; bb = p % SB
    sameb = bb[:, None] == bb[None, :]
    tri_s = (sameb & (tt[:, None] <= tt[None, :])).astype(np.float32)
    c["tri_s"] = _pad(tri_s)
    c["negm_s"] = _pad(np.where(sameb & (tt[None, :] <= tt[:, None]), 0.0, NEG).astype(np.float32))
    c["negb_s"] = _pad(np.where(sameb, 0.0, NEG).astype(np.float32))
    c["selend"] = _pad(((tt[:, None] == ST - 1) & sameb).astype(np.float32))
    oh = (bb[:, None] == np.arange(SB)[None, :]).astype(np.float32)
    c["onehotB"] = _pad(oh, cols=16)
    oh0 = ((p[:, None] == np.arange(SB)[None, :])).astype(np.float32)
    c["onehot0"] = _pad(oh0, cols=16)
    c["iota16"] = np.broadcast_to(np.arange(16, dtype=np.float32), (128, 16)).copy()
    wins = (2, 4, 8, 16)
    bc0 = np.zeros((4, 128, 128), np.float32); bc = np.zeros((4, 128, 128), np.float32)
    bp = np.zeros((4, 128, 128), np.float32)
    for g, w in enumerate(wins):
        for t in range(128):
            for j in range(w):
                s = t - j
                if s >= 0:
                    bc[g, s, t] += 1.0 / w
                    bc0[g, s, t] += 1.0 / min(t + 1, w)
                else:
                    bp[g, s + 128, t] += 1.0 / w
            bc[g, t, t] -= 1.0
            bc0[g, t, t] -= 1.0
    c["bandc0"] = bc0.transpose(1, 0, 2).reshape(128, 512)
    c["bandc"] = bc.transpose(1, 0, 2).reshape(128, 512)
    c["bandp"] = bp.transpose(1, 0, 2).reshape(128, 512)
    bsA = np.zeros((4, 128, SP), np.float32); bsB = np.zeros((4, 128, SP), np.float32)
    bsC = np.zeros((4, 128, SP), np.float32)
    for g, w in enumerate(wins):
        for t in range(ST):
            for b in range(SB):
                col = t * SB + b
                for j in range(w):
                    r = 15 + t - j
                    if r >= 15:
                        bsC[g, (r - 15) * SB + b, col] += 1.0 / w
                    elif r >= 8:
                        bsB[g, (r - 8) * SB + b, col] += 1.0 / w
                    else:
                        bsA[g, r * SB + b, col] += 1.0 / w
                bsC[g, t * SB + b, col] -= 1.0
    c["bsA"] = bsA.transpose(1, 0, 2).reshape(128, 4 * SP)
    c["bsB"] = bsB.transpose(1, 0, 2).reshape(128, 4 * SP)
    c["bsC"] = bsC.transpose(1, 0, 2).reshape(128, 4 * SP)
    return c


def _pad(a, cols=None):
    out = np.zeros((128, a.shape[1] if cols is None else cols), np.float32)
    out[: a.shape[0], : a.shape[1]] = a
    return out


_CONST_ORDER = ["ident", "ones", "triu", "negm", "sel127", "tri_s", "negm_s", "negb_s", "selend",
                "onehotB", "onehot0", "iota16", "bandc0", "bandc", "bandp", "bsA", "bsB", "bsC"]


def _const_pack():
    c = _consts()
    offs = {}
    o = 0
    arrs = []
    for k in _CONST_ORDER:
        offs[k] = (o, c[k].shape[1])
        o += c[k].shape[1]
        arrs.append(c[k])
    return np.ascontiguousarray(np.concatenate(arrs, axis=1)), offs


def build_program(n_layers=DEPTH, do_phase2=True):
    cpack, coff = _const_pack()
    NCST = cpack.shape[1]
    nc = bass.Bass("TRN2", target_bir_lowering=False)

    def din(name, shape, dt=F32):
        return nc.dram_tensor(name, list(shape), dt, kind="ExternalInput").ap()

    def dout(name, shape, dt=F32):
        return nc.dram_tensor(name, list(shape), dt, kind="ExternalOutput").ap()

    xp = din("xp", [SEQ, D]); xs = din("xs", [SP, D])
    cp = din("cp", [128, D]); cs = din("cs", [SP, D])
    sC = din("sC", [DEPTH, 4, 128, SB, 128]); snat = din("snat", [DEPTH, SB, 4, 128])
    snT = din("snT", [DEPTH, 4, 128, SB]); sm = din("sm", [DEPTH, SP, 4])
    spA = din("spA", [DEPTH, 128, 256]); spB = din("spB", [DEPTH, 112, 256])
    w_ada = din("w_ada", [DEPTH, D, 6 * D]); b_ada = din("b_ada", [DEPTH, 128, 6 * D])
    w_in = din("w_in", [DEPTH, D, IN_COLS]); b_gate = din("b_gate", [DEPTH, 128, 8])
    mh_g = din("mh_g", [DEPTH, 128, 512]); sgu_g = din("sgu_g", [DEPTH, 128, 256])
    sgu_b = din("sgu_b", [DEPTH, 128, 256]); pscale = din("pscale", [DEPTH, 128, 256])
    w_sT = din("w_sT", [DEPTH, 128, 4, 128]); b_sT = din("b_sT", [DEPTH, 128, 4])
    w_sS = din("w_sS", [DEPTH, SP, 4, SP]); b_sS = din("b_sS", [DEPTH, SP, 4])
    w_pool = din("w_pool", [DEPTH, 64, 4, 64]); w_o = din("w_o", [DEPTH, D, D])
    ln1g = din("ln1g", [DEPTH, 128, D]); ln1b = din("ln1b", [DEPTH, 128, D])
    ln2g = din("ln2g", [DEPTH, 128, D]); ln2b = din("ln2b", [DEPTH, 128, D])
    w_pq = din("w_pq", [DEPTH, D, 2048]); keysT = din("keysT", [DEPTH, 128, 16, 128])
    puv = [din("puv%d" % l, [NEXP, 2 * D]) for l in range(DEPTH)]
    cst_d = din("cst", [128, NCST])

    yp = dout("yp", [SEQ, D]); ys = dout("ys", [SP, D])
    o_pC = dout("pC", [DEPTH, 4, 128, 128]); o_pn = dout("pn", [DEPTH, 4, 128]); o_pm = dout("pm", [DEPTH, 4])
    o_pp = dout("pp", [DEPTH, 15, 256])
    o_nC = dout("nC", [DEPTH, SB, 4, 128, 128]); o_nn = dout("nn", [DEPTH, SB, 4, 128])
    o_nm = dout("nm", [DEPTH, SB, 4]); o_np = dout("npool", [DEPTH, SB, 15, 256])
    o_nv = dout("nv", [DEPTH, SB, ST, 256])

    with ExitStack() as st0:
        kb = KB(nc, st0)
        op = kb.op
        OUT = kb.buf("outs", dma=True)

        def out_dma(dst, src, reads):
            kb.dma("sp", lambda q: q.dma_start(out=dst, in_=src), OUT, reads=reads)

        X = kb.sb("X", [128, NT + 1, D])
        XB = [kb.buf("X%d" % t) for t in range(NT + 1)]
        XL = kb.buf("xload", dma=True)
        CST = Tn(kb, "CST", [128, NCST], dma=True)

        def C(name, P=128, w=None):
            o, n = coff[name]
            return CST[:P, o:o + (n if w is None else w)]

        def Cg(name, g, P, blk, w):
            o, n = coff[name]
            return CST[:P, o + g * blk: o + g * blk + w]

        with nc.allow_non_contiguous_dma(reason="small strided state/param loads"):
            kb.dma("sp", lambda q: q.dma_start(out=CST[:, :], in_=cst_d), CST.b, writes=[CST.b])
            for t in range(NT):
                kb.dma("sp", lambda q, t=t: q.dma_start(out=X[:, t, :], in_=xp[t * 128:(t + 1) * 128, :]),
                       XL, writes=[XB[t]])
            kb.dma("sp", lambda q: q.dma_start(out=X[:SP, NT, :], in_=xs), XL, writes=[XB[NT]])

            for l in range(n_layers):
                with ExitStack() as st1:
                    kb.stack = st1
                    kb.sfx = "_a%d" % l
                    _phase1(nc, kb, l, locals())
                    kb.barrier(extra=[OUT])
                    kb.emit()
                    kb.end_phase()
                if do_phase2:
                    with ExitStack() as st2:
                        kb.stack = st2
                        kb.sfx = "_b%d" % l
                        _phase2(nc, kb, l, locals())
                        kb.barrier(extra=[OUT])
                        kb.emit()
                        kb.end_phase()
            kb.stack = st0
            kb.sfx = ""
            for t in range(NT):
                out_dma(yp[t * 128:(t + 1) * 128, :], X[:, t, :], [XB[t]])
            out_dma(ys, X[:SP, NT, :], [XB[NT]])
            kb.emit(final_waits=[(OUT.sem, OUT.dma_total)])
    return nc, cpack


def _ada(nc, kb, l, E, ADA, P, csrc, off, WCH, PM, hbuf, hT, PT, badac):
    op = kb.op
    C = E["C"]
    w_ada, b_ada = E["w_ada"], E["b_ada"]
    kb.dma("sp", lambda q: q.dma_start(out=hbuf[:P, :], in_=csrc), hbuf.b, writes=[hbuf.b])
    op("act", lambda e: e.activation(out=hbuf[:P, :], in_=hbuf[:P, :], func=AF.Silu), reads=[hbuf.b], writes=[hbuf.b])
    _transpose8(kb, E, hbuf, hT, PT, P)
    wv = w_ada[l].rearrange("(k p) n -> p k n", p=128)
    for c in range(12):
        i = c % 2
        c0 = off + c * 256
        wch = WCH[c % len(WCH)]
        kb.dma("sp", lambda q, c0=c0: q.dma_start(out=wch[:, :, 0:256], in_=wv[:, :, c0:c0 + 256]), wch.b, writes=[wch.b])
        kb.dma("sp", lambda q, i=i, c0=c0: q.dma_start(out=badac[i][:P, :], in_=b_ada[l, :P, c0:c0 + 256]),
               badac[i].b, writes=[badac[i].b])
        for k in range(8):
            op("pe", lambda e, i=i, k=k: e.matmul(PM[i][:P, 0:256], lhsT=hT[:, k, :P], rhs=wch[:, k, 0:256],
                                                   start=(k == 0), stop=(k == 7)),
               reads=[hT.b, wch.b], writes=[PM[i].b] if k in (0, 7) else [])
        add1 = 1.0 if 4 <= c < 8 else 0.0
        op("dve", lambda e, i=i, c=c, add1=add1: e.scalar_tensor_tensor(
            out=ADA[:P, c * 256:(c + 1) * 256], in0=PM[i][:P, 0:256], scalar=add1, in1=badac[i][:P, :],
            op0=ALU.add, op1=ALU.add), reads=[PM[i].b, badac[i].b], writes=[ADA.b])


def _transpose8(kb, E, src, dstT, PT, P, srcb=None, op=None):
    op = kb.op if op is None else op
    C = E["C"]
    sb_ = src.b if srcb is None else srcb
    for half in range(2):
        for j in range(4):
            k = half * 4 + j
            op("pe", lambda e, half=half, j=j, k=k: e.transpose(
                out=PT[half][:, j * 128:j * 128 + P], in_=src[:P, k * 128:(k + 1) * 128], identity=C("ident", P, P)),
               reads=[sb_, E["CST"].b], writes=[PT[half].b])
        op("act", lambda e, half=half: e.copy(
            out=dstT[:, half * 4:half * 4 + 4, :P],
            in_=PT[half][:, :].rearrange("p (j c) -> p j c", j=4)[:, :, :P]),
           reads=[PT[half].b], writes=[dstT.b])


def _phase1(nc, kb, l, E):
    op = kb.op
    C, Cg, CST, X, XB = E["C"], E["Cg"], E["CST"], E["X"], E["XB"]
    out_dma = E["out_dma"]
    w_in, w_o = E["w_in"], E["w_o"]

    ADA = Tn(kb, "ADA1", [128, 3072]); ADAs = ADA
    WCH = [Tn(kb, "WCH%d" % i, [128, 8, WCW], dma=True) for i in range(2)]
    badac = [Tn(kb, "bada%d" % i, [128, 256], dma=True) for i in range(2)]
    H = Tn(kb, "H", [128, D], dma=True); HT = Tn(kb, "HT", [128, 8, 128])
    PROJ = Tn(kb, "PROJ", [128, IN_COLS], dma=True)
    Y = Tn(kb, "Y", [128, D])
    PRM = Tn(kb, "PRM", [128, 8 + 512 + 256 * 3 + 2 * D], dma=True)
    WS = Tn(kb, "WS", [128, 4, 128], dma=True); BS = Tn(kb, "BS", [128, 4], dma=True)
    WSs = Tn(kb, "WSs", [128, 4, SP], dma=True); BSs = Tn(kb, "BSs", [128, 4], dma=True)
    WP = Tn(kb, "WP", [64, 4, 64], dma=True)
    PT = [Tn(kb, "PT%d" % i, [128, 512], psum=True) for i in range(2)]
    PM = [Tn(kb, "PM%d" % i, [128, 512], psum=True) for i in range(2)]
    PA = Tn(kb, "PA", [128, 512], psum=True); PB = Tn(kb, "PB", [128, 512], psum=True)
    PC = Tn(kb, "PC", [128, 512], psum=True); PD = Tn(kb, "PD", [128, 512], psum=True)
    SM = Tn(kb, "SM", [128, 64])
    SMs = Tn(kb, "SMs", [128, 4], dma=True)
    MREP = Tn(kb, "MREP", [128, 4])
    CTX = Tn(kb, "CTX", [128, 4, 129], dma=True)
    DG = Tn(kb, "DG", [128, 128]); DL = Tn(kb, "DL", [128, 128]); WI = Tn(kb, "WI", [128, 128])
    AM = Tn(kb, "AM", [128, 128]); AT = Tn(kb, "AT", [128, 128])
    QT = Tn(kb, "QT", [128, 128]); KT = Tn(kb, "KT", [128, 128])
    VX = Tn(kb, "VX", [128, 129]); TOT = Tn(kb, "TOT", [128, 129]); WV = Tn(kb, "WV", [128, 129])
    HN = Tn(kb, "HN", [128, 128]); SG = Tn(kb, "SG", [128, 128]); ST6 = Tn(kb, "ST6", [128, 2, 6])
    OUTC = Tn(kb, "OUTC", [128, 128], dma=True)
    CN = Tn(kb, "CN", [128, SB, 128], dma=True); CTS = Tn(kb, "CTS", [128, SB, 129])
    ZQ = Tn(kb, "ZQ", [128, SB * SP]); RA = Tn(kb, "RA", [128, SB, 128])
    NNAT = Tn(kb, "NNAT", [SB, 4, 128], dma=True); NTH = Tn(kb, "NTH", [128, SB], dma=True)
    WCB = Tn(kb, "WCB", [128, 16]); DECD = Tn(kb, "DECD", [128, 16]); DECR = Tn(kb, "DECR", [128, 16])
    MSO = Tn(kb, "MSO", [SB, 4], dma=True)
    PREV = Tn(kb, "PREV", [128, 256]); PTT = Tn(kb, "PTT", [64, 4, 128])
    SPA = Tn(kb, "SPA", [128, 256], dma=True); SPB = Tn(kb, "SPB", [128, 256], dma=True)
    VN = Tn(kb, "VN", [128, 256], dma=True); VTMP = Tn(kb, "VTMP", [128, 256])

    o_bg, o_mh, o_sg, o_sb, o_ps, o_l1g, o_l1b = 0, 8, 520, 776, 1032, 1288, 1288 + D
    for (o, w, src) in [(o_bg, 8, E["b_gate"]), (o_mh, 512, E["mh_g"]), (o_sg, 256, E["sgu_g"]), (o_sb, 256, E["sgu_b"]),
                        (o_ps, 256, E["pscale"]), (o_l1g, D, E["ln1g"]), (o_l1b, D, E["ln1b"])]:
        kb.dma("sp", lambda q, o=o, w=w, src=src: q.dma_start(out=PRM[:, o:o + w], in_=src[l]), PRM.b, writes=[PRM.b])
    kb.dma("sp", lambda q: q.dma_start(out=WS[:, :, :], in_=E["w_sT"][l]), WS.b, writes=[WS.b])
    kb.dma("sp", lambda q: q.dma_start(out=BS[:, :], in_=E["b_sT"][l]), BS.b, writes=[BS.b])
    kb.dma("sp", lambda q: q.dma_start(out=WSs[:SP, :, :], in_=E["w_sS"][l]), WSs.b, writes=[WSs.b])
    kb.dma("sp", lambda q: q.dma_start(out=BSs[:SP, :], in_=E["b_sS"][l]), BSs.b, writes=[BSs.b])
    kb.dma("sp", lambda q: q.dma_start(out=WP[:, :, :], in_=E["w_pool"][l]), WP.b, writes=[WP.b])
    for g in range(4):
        op("dve", lambda e, g=g: e.tensor_tensor(out=WS[:, g, :], in0=WS[:, g, :], in1=C("triu"), op=ALU.mult),
           reads=[WS.b, CST.b], writes=[WS.b])
        op("dve", lambda e, g=g: e.tensor_tensor(out=WSs[:SP, g, :], in0=WSs[:SP, g, :], in1=C("tri_s", SP, SP), op=ALU.mult),
           reads=[WSs.b, CST.b], writes=[WSs.b])
    op("dve", lambda e: e.memset(CTX[:, :, :], 0.0), writes=[CTX.b])
    op("dve", lambda e: e.memset(MREP[:, :], 0.0), writes=[MREP.b])
    op("dve", lambda e: e.memset(VX[:, :], 1.0), writes=[VX.b])

    _ada(nc, kb, l, E, ADA, 128, E["cp"], 0, WCH, PM, H, HT, PT, badac)

    w_in_v = w_in[l].rearrange("(k p) n -> p k n", p=128)
    w_o_v = w_o[l].rearrange("(k p) n -> p k n", p=128)
    chunks = [(i * 256, 256) for i in range(10)] + [(2560, 264)]
    wctr = [0]

    def stream_mm(wview, c0, w, lhsT, P, evac):
        i = wctr[0] % 2
        wctr[0] += 1
        kb.dma("sp", lambda q: q.dma_start(out=WCH[i][:, :, 0:w], in_=wview[:, :, c0:c0 + w]), WCH[i].b, writes=[WCH[i].b])
        for k in range(8):
            op("pe", lambda e, k=k: e.matmul(PM[i][:P, 0:w], lhsT=lhsT[:, k, :P], rhs=WCH[i][:, k, 0:w],
                                              start=(k == 0), stop=(k == 7)),
               reads=[lhsT.b, WCH[i].b], writes=[PM[i].b] if k in (0, 7) else [])
        evac(PM[i], i)

    for t in range(NT + 1):
        is_s = (t == NT)
        P = SP if is_s else 128
        ada = ADAs if is_s else ADA
        if is_s:
            _ada(nc, kb, l, E, ADAs, SP, E["cs"], 0, WCH, PM, H, HT, PT, badac)
        xt = X[:P, t, :]
        op("dve", lambda e: e.tensor_tensor(out=H[:P, :], in0=xt, in1=ada[:P, 1024:2048], op=ALU.mult),
           reads=[XB[t], ada.b], writes=[H.b])
        op("dve", lambda e: e.tensor_tensor(out=H[:P, :], in0=H[:P, :], in1=ada[:P, 0:1024], op=ALU.add),
           reads=[H.b, ada.b], writes=[H.b])
        _transpose8(kb, E, H, HT, PT, P)
        for (c0, w) in chunks:
            stream_mm(w_in_v, c0, w, HT, P,
                      lambda pm, i, c0=c0, w=w: op("act", lambda e: e.copy(out=PROJ[:P, c0:c0 + w], in_=pm[:P, 0:w]),
                                                   reads=[pm.b], writes=[PROJ.b]))
        tri = C("tri_s", SP, SP) if is_s else C("triu")
        negm = C("negm_s", SP, SP) if is_s else C("negm")
        selE = C("selend", SP, SP) if is_s else C("sel127")
        if is_s:
            kb.dma("sp", lambda q: q.dma_start(out=SMs[:SP, :], in_=E["sm"][l]), SMs.b, writes=[SMs.b])
            kb.dma("sp", lambda q: q.dma_start(out=NNAT[:, :, :], in_=E["snat"][l]), NNAT.b, writes=[NNAT.b])
        mtok = SMs if is_s else MREP
        op("dve", lambda e: e.tensor_tensor(out=SM[:P, 0:8], in0=PROJ[:P, 2048:2056], in1=PRM[:P, o_bg:o_bg + 8], op=ALU.add),
           reads=[PROJ.b, PRM.b], writes=[SM.b])
        op("dve", lambda e: e.scalar_tensor_tensor(out=SM[:P, 8:12], in0=SM[:P, 4:8], scalar=-1.0, in1=SM[:P, 4:8], op0=ALU.mult, op1=ALU.max),
           reads=[SM.b], writes=[SM.b])
        op("act", lambda e: e.activation(out=SM[:P, 12:16], in_=SM[:P, 8:12], func=AF.Exp, scale=-1.0), reads=[SM.b], writes=[SM.b])
        op("act", lambda e: e.activation(out=SM[:P, 12:16], in_=SM[:P, 12:16], func=AF.Ln, bias=1.0, scale=1.0),
           reads=[SM.b], writes=[SM.b])
        op("dve", lambda e: e.tensor_scalar_min(out=SM[:P, 16:20], in0=SM[:P, 4:8], scalar1=0.0), reads=[SM.b], writes=[SM.b])
        op("dve", lambda e: e.tensor_tensor(out=SM[:P, 16:20], in0=SM[:P, 16:20], in1=SM[:P, 12:16], op=ALU.subtract),
           reads=[SM.b], writes=[SM.b])
        op("pe", lambda e: e.matmul(PA[:P, 0:4], lhsT=tri, rhs=SM[:P, 16:20], start=True, stop=True),
           reads=[CST.b, SM.b], writes=[PA.b])
        op("act", lambda e: e.copy(out=SM[:P, 20:24], in_=PA[:P, 0:4]), reads=[PA.b], writes=[SM.b])
        op("dve", lambda e: e.tensor_tensor(out=SM[:P, 24:28], in0=SM[:P, 0:4], in1=SM[:P, 20:24], op=ALU.subtract),
           reads=[SM.b], writes=[SM.b])
        op("pe", lambda e: e.matmul(PA[:P, 8:12], lhsT=selE, rhs=SM[:P, 20:24], start=True, stop=True),
           reads=[CST.b, SM.b], writes=[PA.b])
        op("act", lambda e: e.copy(out=SM[:P, 28:32], in_=PA[:P, 8:12]), reads=[PA.b], writes=[SM.b])

        for hh in range(4):
            qs = PROJ[:P, hh * 128:(hh + 1) * 128]
            ks = PROJ[:P, 512 + hh * 128:512 + (hh + 1) * 128]
            vs = PROJ[:P, 1024 + hh * 128:1024 + (hh + 1) * 128]
            os_ = PROJ[:P, 1536 + hh * 128:1536 + (hh + 1) * 128]
            col = lambda c, hh=hh: SM[:P, c + hh:c + hh + 1]
            S1 = lambda c: SM[:P, c:c + 1]
            if is_s:
                kb.dma("sp", lambda q, hh=hh: q.dma_start(out=CN[:, :, :], in_=E["sC"][l, hh]), CN.b, writes=[CN.b])
                kb.dma("sp", lambda q, hh=hh: q.dma_start(out=NTH[:, :], in_=E["snT"][l, hh]), NTH.b, writes=[NTH.b])
                for j in range(4):
                    pt = PT[j % 2]
                    for jj in range(4):
                        b = j * 4 + jj
                        op("pe", lambda e, b=b, jj=jj, pt=pt: e.transpose(out=pt[:, jj * 128:(jj + 1) * 128], in_=CN[:, b, :],
                                                                       identity=C("ident")),
                           reads=[CN.b, CST.b], writes=[pt.b])
                    op("act", lambda e, j=j, pt=pt: e.copy(out=CTS[:, j * 4:(j + 1) * 4, 0:128],
                                                           in_=pt[:, :].rearrange("p (j c) -> p j c", j=4)),
                       reads=[pt.b], writes=[CTS.b])
                op("dve", lambda e: e.tensor_copy(out=CTS[:, :, 128:129], in_=NTH[:, :].unsqueeze(2)), reads=[NTH.b], writes=[CTS.b])
            op("dve", lambda e, hh=hh: e.tensor_scalar(out=DG[:P, :P], in0=C("ident", P, P), scalar1=col(24), scalar2=None,
                                                       op0=ALU.mult), reads=[SM.b, CST.b], writes=[DG.b])
            op("pe", lambda e: e.matmul(PB[:P, 0:P], lhsT=C("ones", P, P), rhs=DG[:P, :P], start=True, stop=True),
               reads=[DG.b, CST.b], writes=[PB.b])
            if is_s:
                op("dve", lambda e: e.tensor_tensor(out=DL[:P, :P], in0=PB[:P, 0:P], in1=C("negb_s", SP, SP), op=ALU.add),
                   reads=[PB.b, CST.b], writes=[DL.b])
                op("dve", lambda e: e.tensor_reduce(out=S1(32), in_=DL[:P, :P], axis=AX.X, op=ALU.max), reads=[DL.b], writes=[SM.b])
            else:
                op("dve", lambda e: e.tensor_reduce(out=S1(32), in_=PB[:P, 0:P], axis=AX.X, op=ALU.max), reads=[PB.b], writes=[SM.b])
            op("dve", lambda e, hh=hh: e.scalar_tensor_tensor(out=DL[:P, :P], in0=PB[:P, 0:P], scalar=col(20), in1=negm,
                                                              op0=ALU.add, op1=ALU.add),
               reads=[PB.b, SM.b, CST.b], writes=[DL.b])
            op("dve", lambda e: e.tensor_reduce(out=S1(33), in_=DL[:P, :P], axis=AX.X, op=ALU.max), reads=[DL.b], writes=[SM.b])
            op("dve", lambda e, hh=hh: e.tensor_tensor(out=S1(34), in0=col(20), in1=mtok[:P, hh:hh + 1], op=ALU.add),
               reads=[SM.b, mtok.b], writes=[SM.b])
            op("dve", lambda e: e.tensor_tensor(out=S1(35), in0=S1(34), in1=S1(33), op=ALU.max), reads=[SM.b], writes=[SM.b])
            op("dve", lambda e: e.tensor_scalar(out=S1(36), in0=S1(35), scalar1=-1.0, scalar2=None, op0=ALU.mult),
               reads=[SM.b], writes=[SM.b])
            op("act", lambda e: e.activation(out=WI[:P, :P], in_=DL[:P, :P], func=AF.Exp, bias=S1(36), scale=1.0),
               reads=[DL.b, SM.b], writes=[WI.b])
            op("act", lambda e: e.activation(out=S1(37), in_=S1(34), func=AF.Exp, bias=S1(36), scale=1.0), reads=[SM.b], writes=[SM.b])
            op("act", lambda e: e.activation(out=S1(38), in_=S1(36), func=AF.Exp), reads=[SM.b], writes=[SM.b])
            op("pe", lambda e: e.transpose(out=PC[:, 0:P], in_=qs, identity=C("ident", P, P)), reads=[PROJ.b, CST.b], writes=[PC.b])
            op("pe", lambda e: e.transpose(out=PC[:, 128:128 + P], in_=ks, identity=C("ident", P, P)), reads=[PROJ.b, CST.b], writes=[PC.b])
            op("act", lambda e: e.mul(out=QT[:, :P], in_=PC[:, 0:P], mul=128.0 ** -0.5), reads=[PC.b], writes=[QT.b])
            op("act", lambda e: e.copy(out=KT[:, :P], in_=PC[:, 128:128 + P]), reads=[PC.b], writes=[KT.b])
            op("pe", lambda e: e.matmul(PD[:P, 0:P], lhsT=QT[:, :P], rhs=KT[:, :P], start=True, stop=True),
               reads=[QT.b, KT.b], writes=[PD.b])
            op("dve", lambda e: e.tensor_tensor(out=AM[:P, :P], in0=WI[:P, :P], in1=PD[:P, 0:P], op=ALU.mult),
               reads=[WI.b, PD.b], writes=[AM.b])
            op("pe", lambda e: e.transpose(out=PB[:P, 128:128 + P], in_=AM[:P, :P], identity=C("ident", P, P)),
               reads=[AM.b, CST.b], writes=[PB.b])
            op("act", lambda e: e.copy(out=AT[:P, :P], in_=PB[:P, 128:128 + P]), reads=[PB.b], writes=[AT.b])
            op("pool", lambda e: e.tensor_copy(out=VX[:P, 0:128], in_=vs), reads=[PROJ.b], writes=[VX.b])
            op("pe", lambda e: e.matmul(PD[:P, 128:257], lhsT=AT[:P, :P], rhs=VX[:P, :], start=True, stop=True),
               reads=[AT.b, VX.b], writes=[PD.b])
            if is_s:
                op("pool", lambda e: e.memset(ZQ[:, :], 0.0), writes=[ZQ.b])
                for b in range(SB):
                    op("pool", lambda e, b=b: e.tensor_copy(out=ZQ[:, b * SP + b:(b + 1) * SP:SB], in_=QT[:, b:SP:SB]),
                       reads=[QT.b], writes=[ZQ.b])
                for b in range(SB):
                    op("pe", lambda e, b=b: e.matmul(PC[:P, 256:385], lhsT=ZQ[:, b * SP:(b + 1) * SP], rhs=CTS[:, b, :],
                                                     start=(b == 0), stop=(b == SB - 1)),
                       reads=[ZQ.b, CTS.b], writes=[PC.b] if b in (0, SB - 1) else [])
            else:
                op("pe", lambda e, hh=hh: e.matmul(PC[:P, 256:385], lhsT=QT[:, :P], rhs=CTX[:, hh, :], start=True, stop=True),
                   reads=[QT.b, CTX.b], writes=[PC.b])
            op("act", lambda e: e.activation(out=TOT[:P, :], in_=PC[:P, 256:385], func=AF.Identity, scale=S1(37)),
               reads=[PC.b, SM.b], writes=[TOT.b])
            op("dve", lambda e: e.tensor_tensor(out=TOT[:P, :], in0=TOT[:P, :], in1=PD[:P, 128:257], op=ALU.add),
               reads=[TOT.b, PD.b], writes=[TOT.b])
            op("dve", lambda e: e.scalar_tensor_tensor(out=S1(39), in0=TOT[:P, 128:129], scalar=-1.0, in1=TOT[:P, 128:129], op0=ALU.mult, op1=ALU.max),
               reads=[TOT.b], writes=[SM.b])
            op("dve", lambda e: e.tensor_tensor(out=S1(39), in0=S1(39), in1=S1(38), op=ALU.max), reads=[SM.b], writes=[SM.b])
            op("dve", lambda e: e.reciprocal(out=S1(40), in_=S1(39)), reads=[SM.b], writes=[SM.b])
            op("dve", lambda e: e.tensor_scalar(out=HN[:P, :], in0=TOT[:P, 0:128], scalar1=S1(40), scalar2=None, op0=ALU.mult),
               reads=[TOT.b, SM.b], writes=[HN.b])
            op("dve", lambda e: e.bn_stats(out=ST6[:P, 0, :], in_=HN[:P, :]), reads=[HN.b], writes=[ST6.b])
            op("dve", lambda e: e.bn_aggr(out=SM[:P, 41:43], in_=ST6[:P, 0, :]), reads=[ST6.b], writes=[SM.b])
            op("act", lambda e: e.activation(out=S1(43), in_=S1(42), func=AF.Sqrt, bias=LN_EPS, scale=1.0), reads=[SM.b], writes=[SM.b])
            op("dve", lambda e: e.reciprocal(out=S1(44), in_=S1(43)), reads=[SM.b], writes=[SM.b])
            op("dve", lambda e: e.tensor_scalar(out=HN[:P, :], in0=HN[:P, :], scalar1=S1(41), scalar2=S1(44),
                                                op0=ALU.subtract, op1=ALU.mult), reads=[HN.b, SM.b], writes=[HN.b])
            op("pool", lambda e, hh=hh: e.tensor_tensor(out=HN[:P, :], in0=HN[:P, :],
                                                        in1=PRM[:P, o_mh + hh * 128:o_mh + (hh + 1) * 128], op=ALU.mult),
               reads=[HN.b, PRM.b], writes=[HN.b])
            op("act", lambda e: e.activation(out=SG[:P, :], in_=os_, func=AF.Sigmoid), reads=[PROJ.b], writes=[SG.b])
            op("dve", lambda e, hh=hh: e.tensor_tensor(out=Y[:P, hh * 128:(hh + 1) * 128], in0=HN[:P, :], in1=SG[:P, :], op=ALU.mult),
               reads=[HN.b, SG.b], writes=[Y.b])
            op("dve", lambda e, hh=hh: e.tensor_tensor(out=S1(45), in0=mtok[:P, hh:hh + 1], in1=S1(32), op=ALU.max),
               reads=[SM.b, mtok.b], writes=[SM.b])
            op("dve", lambda e, hh=hh: e.tensor_tensor(out=S1(45), in0=S1(45), in1=col(28), op=ALU.add), reads=[SM.b], writes=[SM.b])
            op("dve", lambda e, hh=hh: e.tensor_tensor(out=S1(46), in0=col(28), in1=S1(45), op=ALU.subtract), reads=[SM.b], writes=[SM.b])
            op("act", lambda e, hh=hh: e.activation(out=S1(47), in_=col(24), func=AF.Exp, bias=S1(46), scale=1.0),
               reads=[SM.b], writes=[SM.b])
            op("act", lambda e, hh=hh: e.activation(out=S1(48), in_=mtok[:P, hh:hh + 1], func=AF.Exp, bias=S1(46), scale=1.0),
               reads=[SM.b, mtok.b], writes=[SM.b])
            if not is_s:
                op("dve", lambda e: e.tensor_scalar(out=WV[:P, :], in0=VX[:P, :], scalar1=S1(47), scalar2=None, op0=ALU.mult),
                   reads=[VX.b, SM.b], writes=[WV.b])
                op("pe", lambda e: e.matmul(PB[:, 256:385], lhsT=ks, rhs=WV[:P, :], start=True, stop=True),
                   reads=[PROJ.b, WV.b], writes=[PB.b])
                op("dve", lambda e, hh=hh: e.scalar_tensor_tensor(out=CTX[:, hh, :], in0=CTX[:, hh, :], scalar=S1(48), in1=PB[:, 256:385],
                                                                  op0=ALU.mult, op1=ALU.add),
                   reads=[CTX.b, SM.b, PB.b], writes=[CTX.b])
                op("dve", lambda e, hh=hh: e.tensor_copy(out=MREP[:, hh:hh + 1], in_=S1(45)), reads=[SM.b], writes=[MREP.b])
                if t == NT - 1:
                    op("pe", lambda e, hh=hh: e.transpose(out=PA[:, 128:256], in_=CTX[:, hh, 0:128], identity=C("ident")),
                       reads=[CTX.b, CST.b], writes=[PA.b])
                    op("act", lambda e: e.copy(out=OUTC[:, :], in_=PA[:, 128:256]), reads=[PA.b], writes=[OUTC.b])
                    out_dma(E["o_pC"][l, hh], OUTC[:, :], [OUTC.b])
                    out_dma(E["o_pn"][l, hh].rearrange("(k o) -> k o", o=1), CTX[:, hh, 128:129], [CTX.b])
                    if hh == 3:
                        out_dma(E["o_pm"][l:l + 1, :], MREP[0:1, :], [MREP.b])
            else:
                op("dve", lambda e: e.tensor_scalar(out=WCB[:P, :], in0=C("onehotB", SP), scalar1=S1(47), scalar2=None, op0=ALU.mult),
                   reads=[SM.b, CST.b], writes=[WCB.b])
                op("dve", lambda e: e.tensor_tensor(out=RA[:P, :, :], in0=vs.unsqueeze(1).broadcast_to([P, SB, 128]),
                                                    in1=WCB[:P, :].unsqueeze(2).broadcast_to([P, SB, 128]), op=ALU.mult),
                   reads=[PROJ.b, WCB.b], writes=[RA.b])
                op("dve", lambda e: e.tensor_scalar(out=DECD[:P, :], in0=C("onehot0", SP), scalar1=S1(48), scalar2=None, op0=ALU.mult),
                   reads=[SM.b, CST.b], writes=[DECD.b])
                op("pe", lambda e: e.matmul(PA[:, 16:32], lhsT=C("ones", SP, 128), rhs=DECD[:P, :], start=True, stop=True),
                   reads=[DECD.b, CST.b], writes=[PA.b])
                op("act", lambda e: e.copy(out=DECR[:, :], in_=PA[:, 16:32]), reads=[PA.b], writes=[DECR.b])
                for b in range(SB):
                    pq = [PA, PB, PC, PD][b % 4]
                    op("pe", lambda e, b=b, pq=pq: e.matmul(pq[:, 384:512], lhsT=RA[:P, b, :], rhs=ks, start=True, stop=True),
                       reads=[RA.b, PROJ.b], writes=[pq.b])
                    op("dve", lambda e, b=b, pq=pq: e.scalar_tensor_tensor(out=CN[:, b, :], in0=CN[:, b, :], scalar=DECR[:, b:b + 1],
                                                                           in1=pq[:, 384:512], op0=ALU.mult, op1=ALU.add),
                       reads=[CN.b, DECR.b, pq.b], writes=[CN.b])
                out_dma(E["o_nC"][l, :, hh].rearrange("b v k -> v b k"), CN[:, :, :], [CN.b])
                op("pe", lambda e: e.matmul(PA[:SB, 32:160], lhsT=WCB[:P, :], rhs=ks, start=True, stop=True),
                   reads=[WCB.b, PROJ.b], writes=[PA.b])
                op("dve", lambda e, hh=hh: e.scalar_tensor_tensor(out=NNAT[:, hh, :], in0=NNAT[:, hh, :], scalar=SM[:SB, 48:49],
                                                                  in1=PA[:SB, 32:160], op0=ALU.mult, op1=ALU.add),
                   reads=[NNAT.b, SM.b, PA.b], writes=[NNAT.b])
                op("dve", lambda e, hh=hh: e.tensor_copy(out=MSO[:, hh:hh + 1], in_=SM[:SB, 45:46]), reads=[SM.b], writes=[MSO.b])
                if hh == 3:
                    out_dma(E["o_nn"][l], NNAT[:, :, :], [NNAT.b])
                    out_dma(E["o_nm"][l], MSO[:, :], [MSO.b])

        vsv = PROJ[:P, 2312:2568].rearrange("p (g d) -> p g d", g=4)
        op("dve", lambda e: e.tensor_reduce(out=SM[:P, 50:54], in_=vsv, axis=AX.X, op=ALU.add), reads=[PROJ.b], writes=[SM.b])
        op("dve", lambda e: e.tensor_scalar(out=SM[:P, 50:54], in0=SM[:P, 50:54], scalar1=1.0 / 64, scalar2=None, op0=ALU.mult),
           reads=[SM.b], writes=[SM.b])
        op("dve", lambda e: e.tensor_tensor(out=VN[:P, :].rearrange("p (g d) -> p g d", g=4), in0=vsv,
                                            in1=SM[:P, 50:54].unsqueeze(2).broadcast_to([P, 4, 64]), op=ALU.subtract),
           reads=[PROJ.b, SM.b], writes=[VN.b])
        op("pool", lambda e: e.tensor_tensor(out=VTMP[:P, :], in0=VN[:P, :], in1=VN[:P, :], op=ALU.mult), reads=[VN.b], writes=[VTMP.b])
        op("dve", lambda e: e.tensor_reduce(out=SM[:P, 54:58], in_=VTMP[:P, :].rearrange("p (g d) -> p g d", g=4), axis=AX.X, op=ALU.add),
           reads=[VTMP.b], writes=[SM.b])
        op("act", lambda e: e.activation(out=SM[:P, 54:58], in_=SM[:P, 54:58], func=AF.Sqrt, bias=LN_EPS, scale=1.0 / 64),
           reads=[SM.b], writes=[SM.b])
        op("dve", lambda e: e.reciprocal(out=SM[:P, 58:62], in_=SM[:P, 54:58]), reads=[SM.b], writes=[SM.b])
        op("dve", lambda e: e.tensor_tensor(out=VN[:P, :].rearrange("p (g d) -> p g d", g=4), in0=VN[:P, :].rearrange("p (g d) -> p g d", g=4),
                                            in1=SM[:P, 58:62].unsqueeze(2).broadcast_to([P, 4, 64]), op=ALU.mult),
           reads=[VN.b, SM.b], writes=[VN.b])
        op("pool", lambda e: e.tensor_tensor(out=VN[:P, :], in0=VN[:P, :], in1=PRM[:P, o_sg:o_sg + 256], op=ALU.mult),
           reads=[VN.b, PRM.b], writes=[VN.b])
        op("pool", lambda e: e.tensor_tensor(out=VN[:P, :], in0=VN[:P, :], in1=PRM[:P, o_sb:o_sb + 256], op=ALU.add),
           reads=[VN.b, PRM.b], writes=[VN.b])
        wsl = WSs if is_s else WS
        bsl = BSs if is_s else BS
        for g in range(4):
            op("pe", lambda e, g=g: e.matmul(PC[:P, g * 64:(g + 1) * 64], lhsT=wsl[:P, g, :P], rhs=VN[:P, g * 64:(g + 1) * 64],
                                             start=True, stop=True), reads=[wsl.b, VN.b], writes=[PC.b])
        for g in range(4):
            op("dve", lambda e, g=g: e.scalar_tensor_tensor(out=Y[:P, 512 + g * 64:512 + (g + 1) * 64], in0=PC[:P, g * 64:(g + 1) * 64],
                                                            scalar=bsl[:P, g:g + 1], in1=PROJ[:P, 2056 + g * 64:2056 + (g + 1) * 64],
                                                            op0=ALU.add, op1=ALU.mult),
               reads=[PC.b, bsl.b, PROJ.b], writes=[Y.b])
        if is_s:
            for tq in range(ST):
                out_dma(E["o_nv"][l][:, tq, :], VN[tq * SB:(tq + 1) * SB, :], [VN.b])

        pin = lambda g: PROJ[:P, 2568 + g * 64:2568 + (g + 1) * 64]
        if is_s:
            kb.dma("sp", lambda q: q.dma_start(out=SPA[:, :], in_=E["spA"][l]), SPA.b, writes=[SPA.b])
            kb.dma("sp", lambda q: q.dma_start(out=SPB[:112, :], in_=E["spB"][l]), SPB.b, writes=[SPB.b])
            for g in range(4):
                op("pe", lambda e, g=g: e.matmul(PA[:64, g * 128:g * 128 + P], lhsT=SPA[:, g * 64:(g + 1) * 64], rhs=Cg("bsA", g, 128, SP, SP),
                                                 start=True, stop=False), reads=[SPA.b, CST.b], writes=[PA.b])
                op("pe", lambda e, g=g: e.matmul(PA[:64, g * 128:g * 128 + P], lhsT=SPB[:112, g * 64:(g + 1) * 64], rhs=Cg("bsB", g, 112, SP, SP),
                                                 start=False, stop=False), reads=[SPB.b, CST.b], writes=[])
                op("pe", lambda e, g=g: e.matmul(PA[:64, g * 128:g * 128 + P], lhsT=pin(g), rhs=Cg("bsC", g, SP, SP, SP),
                                                 start=False, stop=True), reads=[PROJ.b, CST.b], writes=[PA.b])
            npv = E["o_np"][l].rearrange("b r c -> r b c")
            for r in range(4):
                out_dma(npv[r], SPA[64 + r * SB:64 + (r + 1) * SB, :], [SPA.b])
            for r in range(7):
                out_dma(npv[4 + r], SPB[r * SB:(r + 1) * SB, :], [SPB.b])
            for r in range(4):
                out_dma(npv[11 + r], PROJ[r * SB:(r + 1) * SB, 2568:2824], [PROJ.b])
        else:
            for g in range(4):
                band = Cg("bandc0" if t == 0 else "bandc", g, 128, 128, 128)
                op("pe", lambda e, g=g, band=band: e.matmul(PA[:64, g * 128:(g + 1) * 128], lhsT=pin(g), rhs=band, start=True, stop=(t == 0)),
                   reads=[PROJ.b, CST.b], writes=[PA.b])
                if t > 0:
                    op("pe", lambda e, g=g: e.matmul(PA[:64, g * 128:(g + 1) * 128], lhsT=PREV[:, g * 64:(g + 1) * 64],
                                                     rhs=Cg("bandp", g, 128, 128, 128), start=False, stop=True),
                       reads=[PREV.b, CST.b], writes=[PA.b])
            if t < NT - 1:
                op("pool", lambda e: e.tensor_copy(out=PREV[:, :], in_=PROJ[:, 2568:2824]), reads=[PROJ.b], writes=[PREV.b])
            else:
                out_dma(E["o_pp"][l], PROJ[113:128, 2568:2824], [PROJ.b])
        op("act", lambda e: e.copy(out=PTT[:, :, :P], in_=PA[:64, :].rearrange("p (g c) -> p g c", g=4)[:, :, :P]),
           reads=[PA.b], writes=[PTT.b])
        for g in range(4):
            op("pe", lambda e, g=g: e.matmul(PB[:P, g * 64:(g + 1) * 64], lhsT=PTT[:, g, :P], rhs=WP[:, g, :], start=True, stop=True),
               reads=[PTT.b, WP.b], writes=[PB.b])
        op("dve", lambda e: e.tensor_tensor(out=Y[:P, 768:1024], in0=PB[:P, 0:256], in1=PRM[:P, o_ps:o_ps + 256], op=ALU.mult),
           reads=[PB.b, PRM.b], writes=[Y.b])

        _transpose8(kb, E, Y, HT, PT, P)
        for c in range(4):
            stream_mm(w_o_v, c * 256, 256, HT, P,
                      lambda pm, i, c=c: op("dve", lambda e: e.tensor_tensor(out=H[:P, c * 256:(c + 1) * 256], in0=pm[:P, 0:256],
                                                                            in1=ada[:P, 2048 + c * 256:2048 + (c + 1) * 256], op=ALU.mult),
                                            reads=[pm.b, ada.b], writes=[H.b]))
        _resid_ln(kb, X, XB[t], t, P, H, SM, ST6, PRM, o_l1g, o_l1b)


def _resid_ln(kb, X, xb, t, P, Z, SM, ST6, PRM, og, ob):
    op = kb.op
    xt = X[:P, t, :]
    op("dve", lambda e: e.scalar_tensor_tensor(out=Z[:P, :], in0=xt, scalar=ALPHA, in1=Z[:P, :], op0=ALU.mult, op1=ALU.add),
       reads=[xb, Z.b], writes=[Z.b])
    op("dve", lambda e: e.bn_stats(out=ST6[:P, 0, :], in_=Z[:P, 0:512]), reads=[Z.b], writes=[ST6.b])
    op("dve", lambda e: e.bn_stats(out=ST6[:P, 1, :], in_=Z[:P, 512:1024]), reads=[Z.b], writes=[ST6.b])
    op("dve", lambda e: e.bn_aggr(out=SM[:P, 41:43], in_=ST6[:P, :, :].rearrange("p a b -> p (a b)")), reads=[ST6.b], writes=[SM.b])
    op("act", lambda e: e.activation(out=SM[:P, 43:44], in_=SM[:P, 42:43], func=AF.Sqrt, bias=LN_EPS, scale=1.0), reads=[SM.b], writes=[SM.b])
    op("dve", lambda e: e.reciprocal(out=SM[:P, 44:45], in_=SM[:P, 43:44]), reads=[SM.b], writes=[SM.b])
    op("dve", lambda e: e.tensor_scalar(out=Z[:P, :], in0=Z[:P, :], scalar1=SM[:P, 41:42], scalar2=SM[:P, 44:45],
                                        op0=ALU.subtract, op1=ALU.mult), reads=[Z.b, SM.b], writes=[Z.b])
    op("pool", lambda e: e.tensor_tensor(out=Z[:P, :], in0=Z[:P, :], in1=PRM[:P, og:og + D], op=ALU.mult), reads=[Z.b, PRM.b], writes=[Z.b])
    op("dve", lambda e: e.tensor_tensor(out=xt, in0=Z[:P, :], in1=PRM[:P, ob:ob + D], op=ALU.add), reads=[Z.b, PRM.b], writes=[xb])


def _phase2(nc, kb, l, E):
    C, CST, X, XB = E["C"], E["CST"], E["X"], E["XB"]
    ADA = Tn(kb, "ADA2", [128, 3072]); ADAs = ADA
    WCH = [Tn(kb, "WCHb%d" % i, [128, 8, 128], dma=True) for i in range(2)]
    badac = [Tn(kb, "badab%d" % i, [128, 256], dma=True) for i in range(2)]
    Hs = [Tn(kb, "H2_%d" % i, [128, D], dma=True) for i in range(2)]
    HT = Tn(kb, "H2T", [128, 8, 128])
    PRM = Tn(kb, "PRM2", [128, 2 * D], dma=True)
    KTS = Tn(kb, "KTS", [128, 16, 128], dma=True)
    S0 = Tn(kb, "S0", [128, 2048]); S1_ = Tn(kb, "S1", [128, 2048]); S2 = Tn(kb, "S2", [128, 2048])
    JUNK = Tn(kb, "JUNK", [128, D])
    TOPS = Tn(kb, "TOPS", [128, 16, 16]); IDXU = Tn(kb, "IDXU", [128, 16, 16], U32); IDXF = Tn(kb, "IDXF", [128, 16, 16])
    CV = Tn(kb, "CV", [128, 8, 16]); CPOS = Tn(kb, "CPOS", [128, 8, 16], U32)
    PAU = Tn(kb, "PAU", [128, 8, 16], U32); PBU = Tn(kb, "PBU", [128, 8, 16], U32)
    PAF = Tn(kb, "PAF", [128, 8, 16]); PBF = Tn(kb, "PBF", [128, 8, 16])
    I1 = Tn(kb, "I1", [128, 128]); I2 = Tn(kb, "I2", [128, 128])
    IDXs = [Tn(kb, "IDX%d" % i, [128, 128], I32) for i in range(2)]
    GATEs = [Tn(kb, "GATE%d" % i, [128, 128]) for i in range(2)]
    ACTV = Tn(kb, "ACTV", [128, 128]); COEF = Tn(kb, "COEF", [128, 128])
    SMf = Tn(kb, "SM2f", [128, 16]); SM = Tn(kb, "SM2", [128, 64]); ST6 = Tn(kb, "ST62", [128, 2, 6])
    NB = 4
    UB = [Tn(kb, "UB%d" % i, [128, 2 * D], dma=True) for i in range(NB)]
    ACTB = [kb.buf("actv%d" % i) for i in range(NB)]
    COEFB = [kb.buf("coef%d" % i) for i in range(NB)]
    COEF2 = Tn(kb, "COEF2", [128, 128])
    ACC = Tn(kb, "ACC", [128, D])

    class _View:
        def __init__(self, ap, b):
            self.t = ap
            self.b = b

        def __getitem__(self, k):
            return self.t[k]
    WCA = [_View(UB[0][:, :].rearrange("p (k c) -> p k c", k=8), UB[0].b)]
    PT = [Tn(kb, "PTb%d" % i, [128, 512], psum=True) for i in range(2)]
    PM = [Tn(kb, "PMb%d" % i, [128, 512], psum=True) for i in range(2)]
    PQ = [Tn(kb, "PQ%d" % i, [128, 512], psum=True) for i in range(2)]
    PS = [Tn(kb, "PS%d" % i, [128, 512], psum=True) for i in range(2)]

    kb.dma("sp", lambda q: q.dma_start(out=PRM[:, 0:D], in_=E["ln2g"][l]), PRM.b, writes=[PRM.b])
    kb.dma("sp", lambda q: q.dma_start(out=PRM[:, D:2 * D], in_=E["ln2b"][l]), PRM.b, writes=[PRM.b])
    kb.dma("sp", lambda q: q.dma_start(out=KTS[:, :, :], in_=E["keysT"][l]), KTS.b, writes=[KTS.b])
    wpq_v = E["w_pq"][l].rearrange("(k p) n -> p k n", p=128)
    tab = E["puv"][l]
    wctr = [0]

    def make_front(t, sset):
        items = []

        def op(e, fn, reads=(), writes=()):
            items.append(("op", e, _bind(fn), None, tuple(reads), tuple(writes)))

        def dma(e, fn, owner, reads=(), writes=()):
            items.append(("dma", e, _bind(fn), owner, tuple(reads), tuple(writes)))
        is_s = (t == NT)
        P = SP if is_s else 128
        ada = ADAs if is_s else ADA
        H = Hs[sset]; IDX = IDXs[sset]; GATE = GATEs[sset]
        QT = S0
        xt = X[:P, t, :]
        op("dve", lambda e: e.tensor_tensor(out=H[:P, :], in0=xt, in1=ada[:P, 1024:2048], op=ALU.mult), reads=[XB[t], ada.b], writes=[H.b])
        op("dve", lambda e: e.tensor_tensor(out=H[:P, :], in0=H[:P, :], in1=ada[:P, 0:1024], op=ALU.add), reads=[H.b, ada.b], writes=[H.b])
        _transpose8(kb, E, H, HT, PT, P, op=op)
        for c in range(16):
            i = wctr[0] % 2
            wctr[0] += 1
            j = c % 2
            dma("sp", lambda q: q.dma_start(out=WCH[i][:, :, :], in_=wpq_v[:, :, c * 128:(c + 1) * 128]), WCH[i].b, writes=[WCH[i].b])
            for k in range(8):
                op("pe", lambda e: e.matmul(PQ[j][:, 0:P], lhsT=WCH[i][:, k, :], rhs=HT[:, k, :P], start=(k == 0), stop=(k == 7)),
                   reads=[WCH[i].b, HT.b], writes=[PQ[j].b] if k in (0, 7) else [])
            op("act", lambda e: e.copy(out=QT[:, c * 128:c * 128 + P], in_=PQ[j][:, 0:P]), reads=[PQ[j].b], writes=[QT.b])
        for c4 in range(4):
            ps = PS[c4 % 2]
            for j in range(4):
                c = c4 * 4 + j
                op("pe", lambda e: e.matmul(ps[:P, j * 128:(j + 1) * 128], lhsT=QT[:, c * 128:c * 128 + P], rhs=KTS[:, c, :], start=True, stop=True),
                   reads=[QT.b, KTS.b], writes=[ps.b])
            op("act", lambda e: e.copy(out=S1_[:P, c4 * 512:(c4 + 1) * 512], in_=ps[:P, :]), reads=[ps.b], writes=[S1_.b])
        for c in range(16):
            sc = S1_[:P, c * 128:(c + 1) * 128]
            wk = S2[:P, c * 128:(c + 1) * 128]
            op("dve", lambda e: e.max(out=TOPS[:P, c, 0:8], in_=sc), reads=[S1_.b], writes=[TOPS.b])
            op("dve", lambda e: e.max_index(out=IDXU[:P, c, 0:8], in_max=TOPS[:P, c, 0:8], in_values=sc), reads=[S1_.b, TOPS.b], writes=[IDXU.b])
            op("dve", lambda e: e.match_replace(out=wk, in_to_replace=TOPS[:P, c, 0:8], in_values=sc, imm_value=NEG), reads=[S1_.b, TOPS.b], writes=[S2.b])
            op("dve", lambda e: e.max(out=TOPS[:P, c, 8:16], in_=wk), reads=[S2.b], writes=[TOPS.b])
            op("dve", lambda e: e.max_index(out=IDXU[:P, c, 8:16], in_max=TOPS[:P, c, 8:16], in_values=wk), reads=[S2.b, TOPS.b], writes=[IDXU.b])
        op("dve", lambda e: e.tensor_copy(out=IDXF[:P, :, :], in_=IDXU[:P, :, :]), reads=[IDXU.b], writes=[IDXF.b])
        tv = TOPS[:P, :, :].rearrange("p (h two) k -> p h two k", two=2)
        CAND = S0
        op("dve", lambda e: e.tensor_tensor(out=CAND[:P, :].rearrange("p (h a b) -> p h a b", h=8, a=16),
                                            in0=tv[:, :, 0, :].unsqueeze(3).broadcast_to([P, 8, 16, 16]),
                                            in1=tv[:, :, 1, :].unsqueeze(2).broadcast_to([P, 8, 16, 16]), op=ALU.add),
           reads=[TOPS.b], writes=[S0.b])
        for h in range(8):
            cd = CAND[:P, h * 256:(h + 1) * 256]
            wk = S2[:P, h * 256:(h + 1) * 256]
            op("dve", lambda e: e.max(out=CV[:P, h, 0:8], in_=cd), reads=[S0.b], writes=[CV.b])
            op("dve", lambda e: e.max_index(out=CPOS[:P, h, 0:8], in_max=CV[:P, h, 0:8], in_values=cd), reads=[S0.b, CV.b], writes=[CPOS.b])
            op("dve", lambda e: e.match_replace(out=wk, in_to_replace=CV[:P, h, 0:8], in_values=cd, imm_value=NEG), reads=[S0.b, CV.b], writes=[S2.b])
            op("dve", lambda e: e.max(out=CV[:P, h, 8:16], in_=wk), reads=[S2.b], writes=[CV.b])
            op("dve", lambda e: e.max_index(out=CPOS[:P, h, 8:16], in_max=CV[:P, h, 8:16], in_values=wk), reads=[S2.b, CV.b], writes=[CPOS.b])
        op("dve", lambda e: e.tensor_single_scalar(out=PAU[:P, :, :], in_=CPOS[:P, :, :], scalar=4, op=ALU.logical_shift_right), reads=[CPOS.b], writes=[PAU.b])
        op("dve", lambda e: e.tensor_single_scalar(out=PBU[:P, :, :], in_=CPOS[:P, :, :], scalar=15, op=ALU.bitwise_and), reads=[CPOS.b], writes=[PBU.b])
        op("dve", lambda e: e.tensor_copy(out=PAF[:P, :, :], in_=PAU[:P, :, :]), reads=[PAU.b], writes=[PAF.b])
        op("dve", lambda e: e.tensor_copy(out=PBF[:P, :, :], in_=PBU[:P, :, :]), reads=[PBU.b], writes=[PBF.b])
        iv = IDXF[:P, :, :].rearrange("p (h two) k -> p h two k", two=2)
        io16 = C("iota16", P).unsqueeze(1).unsqueeze(1).broadcast_to([P, 8, 16, 16])
        for (pf, half, dst) in [(PAF, 0, I1), (PBF, 1, I2)]:
            eq = S1_[:P, :].rearrange("p (h k a) -> p h k a", h=8, k=16)
            op("dve", lambda e: e.tensor_tensor(out=eq, in0=pf[:P, :, :].unsqueeze(3).broadcast_to([P, 8, 16, 16]), in1=io16, op=ALU.is_equal),
               reads=[pf.b, CST.b], writes=[S1_.b])
            op("dve", lambda e: e.tensor_tensor(out=eq, in0=eq, in1=iv[:, :, half, :].unsqueeze(2).broadcast_to([P, 8, 16, 16]), op=ALU.mult),
               reads=[S1_.b, IDXF.b], writes=[S1_.b])
            op("dve", lambda e: e.tensor_reduce(out=dst[:P, :].rearrange("p (h k) -> p h k", h=8), in_=eq, axis=AX.X, op=ALU.add),
               reads=[S1_.b], writes=[dst.b])
        op("dve", lambda e: e.scalar_tensor_tensor(out=I1[:P, :], in0=I1[:P, :], scalar=128.0, in1=I2[:P, :], op0=ALU.mult, op1=ALU.add),
           reads=[I1.b, I2.b], writes=[I1.b])
        op("dve", lambda e: e.tensor_copy(out=IDX[:P, :], in_=I1[:P, :]), reads=[I1.b], writes=[IDX.b])
        gv = GATE[:P, :].rearrange("p (h k) -> p h k", h=8)
        op("dve", lambda e: e.tensor_tensor(out=gv, in0=CV[:P, :, :], in1=CV[:P, :, 0:1].broadcast_to([P, 8, 16]), op=ALU.subtract),
           reads=[CV.b], writes=[GATE.b])
        op("act", lambda e: e.activation(out=GATE[:P, :], in_=GATE[:P, :], func=AF.Exp), reads=[GATE.b], writes=[GATE.b])
        op("dve", lambda e: e.tensor_reduce(out=SMf[:P, 0:8], in_=gv, axis=AX.X, op=ALU.add), reads=[GATE.b], writes=[SMf.b])
        op("dve", lambda e: e.reciprocal(out=SMf[:P, 8:16], in_=SMf[:P, 0:8]), reads=[SMf.b], writes=[SMf.b])
        op("dve", lambda e: e.tensor_tensor(out=gv, in0=gv, in1=SMf[:P, 8:16].unsqueeze(2).broadcast_to([P, 8, 16]), op=ALU.mult),
           reads=[GATE.b, SMf.b], writes=[GATE.b])
        return items

    def run_items(items, n=None):
        n = len(items) if n is None else min(n, len(items))
        for _ in range(n):
            kind, e, fn, owner, reads, writes = items.pop(0)
            if kind == "op":
                kb.op(e, fn, reads=reads, writes=writes, bound=True)
            else:
                kb.dma(e, fn, owner, reads=reads, writes=writes, bound=True)

    def back(t, sset, nxt):
        op = kb.op
        is_s = (t == NT)
        P = SP if is_s else 128
        ada = ADAs if is_s else ADA
        H = Hs[sset]; IDX = IDXs[sset]; GATE = GATEs[sset]
        per = 0 if not nxt else (len(nxt) + 119) // 120

        def axpy(s):
            b = s % NB
            op("dve", lambda e: e.tensor_tensor(out=COEF2[:P, s:s + 1], in0=COEF[:P, s:s + 1], in1=GATE[:P, s:s + 1], op=ALU.mult),
               reads=[COEFB[b], GATE.b], writes=[COEF2.b])
            if s == 0:
                op("dve", lambda e: e.tensor_scalar(out=ACC[:P, :], in0=UB[b][:P, D:2 * D], scalar1=COEF2[:P, s:s + 1], scalar2=None, op0=ALU.mult),
                   reads=[UB[b].b, COEF2.b], writes=[ACC.b])
            else:
                op("dve", lambda e: e.scalar_tensor_tensor(out=ACC[:P, :], in0=UB[b][:P, D:2 * D], scalar=COEF2[:P, s:s + 1], in1=ACC[:P, :],
                                                           op0=ALU.mult, op1=ALU.add), reads=[UB[b].b, COEF2.b, ACC.b], writes=[ACC.b])

        for s_ in range(128):
            b = s_ % NB
            kb.dma("pool", lambda q: q.indirect_dma_start(out=UB[b][:P, :], out_offset=None, in_=tab,
                                                          in_offset=bass.IndirectOffsetOnAxis(ap=IDX[:P, s_:s_ + 1], axis=0)),
                   UB[b].b, reads=[IDX.b], writes=[UB[b].b])
            op("dve", lambda e: e.scalar_tensor_tensor(out=JUNK[:P, 0:D], in0=UB[b][:P, 0:D], scalar=1.0, in1=H[:P, :],
                                                       op0=ALU.mult, op1=ALU.mult, accum_out=ACTV[:P, s_:s_ + 1]),
               reads=[UB[b].b, H.b], writes=[JUNK.b, ACTB[b]])
            op("act", lambda e: e.activation(out=COEF[:P, s_:s_ + 1], in_=ACTV[:P, s_:s_ + 1], func=AF.Gelu), reads=[ACTB[b]], writes=[COEFB[b]])
            if s_ >= 1:
                axpy(s_ - 1)
            if nxt:
                run_items(nxt, per)
        axpy(127)
        if nxt:
            run_items(nxt)
        op("dve", lambda e: e.tensor_tensor(out=ACC[:P, :], in0=ACC[:P, :], in1=ada[:P, 2048:3072], op=ALU.mult), reads=[ACC.b, ada.b], writes=[ACC.b])
        _resid_ln(kb, X, XB[t], t, P, ACC, SM, ST6, PRM, 0, D)

    _ada(nc, kb, l, E, ADA, 128, E["cp"], 3072, WCA, PM, Hs[0], HT, PT, badac)
    run_items(make_front(0, 0))
    for t in range(NT + 1):
        nxt = make_front(t + 1, (t + 1) % 2) if t + 1 < NT else None
        back(t, t % 2, nxt)
        if t + 1 == NT:
            _ada(nc, kb, l, E, ADAs, SP, E["cs"], 3072, WCA, PM, Hs[NT % 2], HT, PT, badac)
            run_items(make_front(NT, NT % 2))


_CACHE = {}


def _rep(a, P=128):
    return np.ascontiguousarray(np.broadcast_to(a[:, None, :], (a.shape[0], P, a.shape[1])))


def make_in_maps(inp, cpack):
    f = lambda a: np.ascontiguousarray(np.asarray(a, dtype=np.float32))
    shared = {
        "w_ada": f(inp["w_ada"]), "b_ada": _rep(f(inp["b_ada"])), "w_in": f(inp["w_in"]), "b_gate": _rep(f(inp["b_gate"])),
        "mh_g": _rep(f(inp["mh_g"])), "sgu_g": _rep(f(inp["sgu_g"])), "sgu_b": _rep(f(inp["sgu_b"])),
        "pscale": _rep(f(inp["pool_scale"])),
        "w_sT": f(np.asarray(inp["w_s"]).transpose(0, 3, 1, 2)),
        "b_sT": f(np.asarray(inp["b_s"]).transpose(0, 2, 1)),
        "w_pool": f(np.asarray(inp["w_pool"]).transpose(0, 2, 1, 3)),
        "w_o": f(inp["w_o"]), "ln1g": _rep(f(inp["ln1_g"])), "ln1b": _rep(f(inp["ln1_b"])),
        "ln2g": _rep(f(inp["ln2_g"])), "ln2b": _rep(f(inp["ln2_b"])), "w_pq": f(inp["w_pq"]),
        "keysT": f(np.asarray(inp["peer_keys"]).transpose(0, 4, 1, 2, 3).reshape(DEPTH, 128, 16, 128)),
        "cst": cpack,
    }
    ws4 = np.asarray(inp["w_s"])[:, :, :ST, :ST]
    wsS = np.repeat(np.repeat(ws4.transpose(0, 3, 1, 2), SB, axis=1), SB, axis=3)
    shared["w_sS"] = f(wsS)
    bs4 = np.asarray(inp["b_s"])[:, :, :ST]
    shared["b_sS"] = f(np.repeat(bs4.transpose(0, 2, 1), SB, axis=1))
    for l in range(DEPTH):
        shared["puv%d" % l] = np.ascontiguousarray(
            np.concatenate([np.asarray(inp["peer_u"])[l], np.asarray(inp["peer_v"])[l]], axis=1), dtype=np.float32)
    maps = []
    for c in range(NCORES):
        bs = slice(c * SB, (c + 1) * SB)
        m = dict(shared)
        m["xp"] = f(np.asarray(inp["x_prompt"])[c])
        m["xs"] = f(np.asarray(inp["x_sample"])[bs].transpose(1, 0, 2).reshape(SP, D))
        m["cp"] = f(np.broadcast_to(np.asarray(inp["c_prompt"])[c][None, :], (128, D)))
        m["cs"] = f(np.tile(np.asarray(inp["c_sample"])[bs], (ST, 1)))
        sCc = np.asarray(inp["state_mlstm_C"])[:, bs]
        m["sC"] = f(sCc.transpose(0, 2, 3, 1, 4))
        snc = np.asarray(inp["state_mlstm_n"])[:, bs]
        m["snat"] = f(snc)
        m["snT"] = f(snc.transpose(0, 2, 3, 1))
        m["sm"] = f(np.tile(np.asarray(inp["state_mlstm_m"])[:, bs], (1, ST, 1)))
        spc = np.asarray(inp["state_pool"])[:, bs].transpose(0, 2, 1, 3)
        m["spA"] = f(spc[:, 0:8].reshape(DEPTH, 128, 256))
        m["spB"] = f(spc[:, 8:15].reshape(DEPTH, 112, 256))
        maps.append(m)
    return maps


def gather_outputs(results):
    cat = lambda k, ax: np.concatenate([r[k] for r in results], axis=ax)
    yp = np.stack([r["yp"] for r in results], 0)
    ys = np.concatenate([r["ys"].reshape(ST, SB, D).transpose(1, 0, 2) for r in results], 0)
    pC = np.stack([r["pC"] for r in results], 1)
    pn = np.stack([r["pn"] for r in results], 1)
    pm = np.stack([r["pm"] for r in results], 1)
    pp = np.stack([r["pp"] for r in results], 1)
    return (yp, ys, pC, pn, pm, pp, cat("nC", 1), cat("nn", 1), cat("nm", 1), cat("npool", 1), cat("nv", 1))


def kernel(**inputs):
    if "prog" not in _CACHE:
        _CACHE["prog"] = build_program()
    nc, cpack = _CACHE["prog"]
    maps = make_in_maps(inputs, cpack)
    res = run_bass_kernel_spmd(nc, maps, core_ids=list(range(NCORES)))
    outs = gather_outputs(res.results)
    return tuple(np.ascontiguousarray(o, dtype=np.float32) for o in outs)
```

```python
import numpy as np
from contextlib import ExitStack
import concourse.bass as bass
import concourse.mybir as mybir
from concourse.bass_utils import run_bass_kernel_spmd

F32 = mybir.dt.float32
I32 = mybir.dt.int32
U32 = mybir.dt.uint32
F32R = mybir.dt.float32r
ALU = mybir.AluOpType
AF = mybir.ActivationFunctionType
AX = mybir.AxisListType

NCORES = 8
D = 1024
SEQ = 2048
NT = 16
SB = 16
ST = 4
SP = SB * ST
DEPTH = 2
ALPHA = (2 * DEPTH) ** 0.25
LN_EPS = 1e-5
IN_COLS = 2824
NEG = -1.0e30
WCW = 192
NEXP = 16384
SAME_ENGINE_WAITS = True
NBUF = 6


class TB:
    def __init__(self, name, sem=None):
        self.name = name
        self.last_w = None
        self.reads = []
        self.sem = sem
        self.dma_total = 0
        self.dma_dirty = False


class KB:
    ENG = ("pe", "act", "dve", "pool", "sp")

    def __init__(self, nc, stack):
        self.nc = nc
        self.stack = stack
        self.q = {e: [] for e in self.ENG}
        self.cnt = {e: 0 for e in self.ENG}
        self.esem = {e: stack.enter_context(nc.semaphore("es_" + e)) for e in self.ENG}
        self.seen = {e: {} for e in self.ENG}
        self.semobj = {}
        self._sem_owner = {}
        self.stack0 = stack
        self.phase_tbs = []
        self.sfx = ""

    def new_sem(self, name):
        return self.stack.enter_context(self.nc.semaphore(name + self.sfx))

    def buf(self, name, dma=False):
        tb = TB(name, self.new_sem("d_" + name) if dma else None)
        if dma and self.stack is not self.stack0:
            self.phase_tbs.append(tb)
        return tb

    def end_phase(self):
        for tb in self.phase_tbs:
            k = id(tb.sem)
            self._sem_owner.pop(k, None)
            self.semobj.pop(k, None)
            for e in self.ENG:
                self.seen[e].pop(k, None)
        self.phase_tbs = []

    def sb(self, name, shape, dt=F32):
        return self.stack.enter_context(self.nc.sbuf_tensor(name + self.sfx, list(shape), dt))

    def ps(self, name, shape, dt=F32):
        return self.stack.enter_context(self.nc.psum_tensor(name + self.sfx, list(shape), dt))

    def _deps(self, e, reads, writes):
        deps = {}

        def add(tok):
            if tok is None:
                return
            s, v = tok
            k = id(s)
            self.semobj[k] = s
            ow = self._sem_owner.get(k)
            if ow is not None:
                v = ow.dma_total
            if v > deps.get(k, 0):
                deps[k] = v
        for b in reads:
            add(b.last_w)
        for b in writes:
            add(b.last_w)
            for r in b.reads:
                add(r)
        out = []
        own = id(self.esem[e])
        for k, v in deps.items():
            if k == own and (e in ("pe", "sp") or not SAME_ENGINE_WAITS):
                continue
            if self.seen[e].get(k, 0) >= v:
                continue
            self.seen[e][k] = v
            out.append((self.semobj[k], v))
        return out

    def op(self, e, fn, reads=(), writes=(), bound=False):
        waits = self._deps(e, reads, writes)
        for s, v in waits:
            tb = self._sem_owner.get(id(s))
            if tb is not None:
                tb.dma_dirty = True
        self.cnt[e] += 1
        tok = (self.esem[e], self.cnt[e])
        self.q[e].append((waits, fn if bound else _bind(fn), tok[0], 1))
        for b in reads:
            b.reads.append(tok)
        for b in writes:
            b.last_w = tok
            b.reads = []
        return tok

    def dma(self, e, fn, owner, reads=(), writes=(), bound=False):
        self._sem_owner[id(owner.sem)] = owner
        waits = self._deps(e, reads, writes)
        if owner.dma_dirty and owner.dma_total > 0:
            k = id(owner.sem)
            if self.seen[e].get(k, 0) < owner.dma_total:
                self.seen[e][k] = owner.dma_total
                waits.append((owner.sem, owner.dma_total))
            owner.dma_dirty = False
        for s, v in waits:
            tb = self._sem_owner.get(id(s))
            if tb is not None and tb is not owner:
                tb.dma_dirty = True
        owner.dma_total += 16
        tok = (owner.sem, owner.dma_total)
        self.q[e].append((waits, fn if bound else _bind(fn), owner.sem, 16))
        for b in reads:
            b.reads.append(tok)
        for b in writes:
            b.last_w = tok
            b.reads = []
        return tok

    def barrier(self, extra=()):
        toks = [(self.esem[e], self.cnt[e]) for e in self.ENG if self.cnt[e] > 0 and e != "sp"]
        for tb in list(self._sem_owner.values()) + list(extra):
            if tb.dma_total > 0:
                toks.append((tb.sem, tb.dma_total))
        for e in self.ENG:
            waits = []
            for s, v in toks:
                k = id(s)
                if k == id(self.esem[e]):
                    continue
                if self.seen[e].get(k, 0) >= v:
                    continue
                self.seen[e][k] = v
                waits.append((s, v))
            if waits:
                self.q[e].append((waits, None, None, 0))

    def emit(self, final_waits=()):
        nc = self.nc
        engs = {"pe": "tensor", "act": "scalar", "dve": "vector", "pool": "gpsimd", "sp": "sync"}
        with nc.Block() as block:
            for e in self.ENG:
                items = self.q[e]
                fw = list(final_waits) if e == "sp" else []

                def body(eng, items=items, fw=fw):
                    for waits, fn, sem, inc in items:
                        for s, v in waits:
                            eng.wait_ge(s, v)
                        if fn is not None:
                            fn(eng).then_inc(sem, inc)
                    for s, v in fw:
                        eng.wait_ge(s, v)
                getattr(block, engs[e])(body)
        self.q = {e: [] for e in self.ENG}


class _Rec:
    def __init__(self):
        self.call = None

    def __getattr__(self, name):
        def f(*a, **k):
            self.call = (name, a, k)
            return self
        return f


def _bind(fn):
    r = _Rec()
    fn(r)
    assert r.call is not None
    name, a, k = r.call
    return lambda eng: getattr(eng, name)(*a, **k)


class Tn:
    def __init__(self, kb, name, shape, dt=F32, psum=False, dma=False):
        self.t = kb.ps(name, shape, dt) if psum else kb.sb(name, shape, dt)
        self.b = kb.buf(name, dma=dma)

    def __getitem__(self, k):
        return self.t[k]


def _consts():
    c = {}
    i128 = np.arange(128)
    c["ident"] = np.eye(128, dtype=np.float32)
    c["ones"] = np.ones((128, 128), np.float32)
    c["triu"] = (i128[:, None] <= i128[None, :]).astype(np.float32)
    c["negm"] = np.where(i128[None, :] <= i128[:, None], 0.0, NEG).astype(np.float32)
    sel = np.zeros((128, 128), np.float32); sel[127, :] = 1.0
    c["sel127"] = sel
    p = np.arange(SP); tt = p // SB; bb = p % SB
    sameb = bb[:, None] == bb[None, :]
    tri_s = (sameb & (tt[:, None] <= tt[None, :])).astype(np.float32)
    c["tri_s"] = _pad(tri_s)
    c["negm_s"] = _pad(np.where(sameb & (tt[None, :] <= tt[:, None]), 0.0, NEG).astype(np.float32))
    c["negb_s"] = _pad(np.where(sameb, 0.0, NEG).astype(np.float32))
    c["selend"] = _pad(((tt[:, None] == ST - 1) & sameb).astype(np.float32))
    oh = (bb[:, None] == np.arange(SB)[None, :]).astype(np.float32)
    c["onehotB"] = _pad(oh, cols=16)
    oh0 = ((p[:, None] == np.arange(SB)[None, :])).astype(np.float32)
    c["onehot0"] = _pad(oh0, cols=16)
    c["iota16"] = np.broadcast_to(np.arange(16, dtype=np.float32), (128, 16)).copy()
    wins = (2, 4, 8, 16)
    bc0 = np.zeros((4, 128, 128), np.float32); bc = np.zeros((4, 128, 128), np.float32)
    bp = np.zeros((4, 128, 128), np.float32)
    for g, w in enumerate(wins):
        for t in range(128):
            for j in range(w):
                s = t - j
                if s >= 0:
                    bc[g, s, t] += 1.0 / w
                    bc0[g, s, t] += 1.0 / min(t + 1, w)
                else:
                    bp[g, s + 128, t] += 1.0 / w
            bc[g, t, t] -= 1.0
            bc0[g, t, t] -= 1.0
    c["bandc0"] = bc0.transpose(1, 0, 2).reshape(128, 512)
    c["bandc"] = bc.transpose(1, 0, 2).reshape(128, 512)
    c["bandp"] = bp.transpose(1, 0, 2).reshape(128, 512)
    bsA = np.zeros((4, 128, SP), np.float32); bsB = np.zeros((4, 128, SP), np.float32)
    bsC = np.zeros((4, 128, SP), np.float32)
    for g, w in enumerate(wins):
        for t in range(ST):
            for b in range(SB):
                col = t * SB + b
                for j in range(w):
                    r = 15 + t - j
                    if r >= 15:
                        bsC[g, (r - 15) * SB + b, col] += 1.0 / w
                    elif r >= 8:
                        bsB[g, (r - 8) * SB + b, col] += 1.0 / w
                    else:
                        bsA[g, r * SB + b, col] += 1.0 / w
                bsC[g, t * SB + b, col] -= 1.0
    c["bsA"] = bsA.transpose(1, 0, 2).reshape(128, 4 * SP)
    c["bsB"] = bsB.transpose(1, 0, 2).reshape(128, 4 * SP)
    c["bsC"] = bsC.transpose(1, 0, 2).reshape(128, 4 * SP)
    return c


def _pad(a, cols=None):
    out = np.zeros((128, a.shape[1] if cols is None else cols), np.float32)
    out[: a.shape[0], : a.shape[1]] = a
    return out


_CONST_G = ["ident", "ones", "iota16"]
_CONST_1 = ["triu", "negm", "sel127", "tri_s", "negm_s", "negb_s", "selend",
            "onehotB", "onehot0", "bandc0", "bandc", "bandp", "bsA", "bsB", "bsC"]


def _const_pack():
    c = _consts()
    packs = []
    for order in (_CONST_G, _CONST_1):
        offs = {}
        o = 0
        arrs = []
        for k in order:
            offs[k] = (o, c[k].shape[1])
            o += c[k].shape[1]
            arrs.append(c[k])
        packs.append((np.ascontiguousarray(np.concatenate(arrs, axis=1)), offs))
    return packs


def build_program(n_layers=DEPTH, do_phase2=True):
    (cpack, coff), (cpack1, coff1) = _const_pack()
    NCST = cpack.shape[1]
    NCST1 = cpack1.shape[1]
    nc = bass.Bass("TRN2", target_bir_lowering=False)

    def din(name, shape, dt=F32):
        return nc.dram_tensor(name, list(shape), dt, kind="ExternalInput").ap()

    def dout(name, shape, dt=F32):
        return nc.dram_tensor(name, list(shape), dt, kind="ExternalOutput").ap()

    xp = din("xp", [SEQ, D]); xs = din("xs", [SP, D])
    cp = din("cp", [128, D]); cs = din("cs", [SP, D])
    sC = din("sC", [DEPTH, 4, 128, SB, 128]); snat = din("snat", [DEPTH, SB, 4, 128])
    snT = din("snT", [DEPTH, 4, 128, SB]); sm = din("sm", [DEPTH, SP, 4])
    spA = din("spA", [DEPTH, 128, 256]); spB = din("spB", [DEPTH, 112, 256])
    w_ada = din("w_ada", [DEPTH, (6 * D) // WCW, 128, 8 * WCW]); b_ada = din("b_ada", [DEPTH, 128, 6 * D])
    w_in = din("w_in", [DEPTH, (IN_COLS + WCW - 1) // WCW, 128, 8 * WCW]); b_gate = din("b_gate", [DEPTH, 128, 8])
    mh_g = din("mh_g", [DEPTH, 128, 512]); sgu_g = din("sgu_g", [DEPTH, 128, 256])
    sgu_b = din("sgu_b", [DEPTH, 128, 256]); pscale = din("pscale", [DEPTH, 128, 256])
    w_sT = din("w_sT", [DEPTH, 128, 4, 128]); b_sT = din("b_sT", [DEPTH, 128, 4])
    w_sS = din("w_sS", [DEPTH, SP, 4, SP]); b_sS = din("b_sS", [DEPTH, SP, 4])
    w_pool = din("w_pool", [DEPTH, 64, 4, 64]); w_o = din("w_o", [DEPTH, (D + WCW - 1) // WCW, 128, 8 * WCW])
    ln1g = din("ln1g", [DEPTH, 128, D]); ln1b = din("ln1b", [DEPTH, 128, D])
    ln2g = din("ln2g", [DEPTH, 128, D]); ln2b = din("ln2b", [DEPTH, 128, D])
    w_pq = din("w_pq", [DEPTH, 16, 128, 8 * 128]); keysT = din("keysT", [DEPTH, 128, 16, 128])
    puv = [din("puv%d" % l, [NEXP, 2 * D]) for l in range(DEPTH)]
    cst_d = din("cst", [128, NCST])
    cst1_d = din("cst1", [128, NCST1])

    yp = dout("yp", [SEQ, D]); ys = dout("ys", [SP, D])
    o_pC = dout("pC", [DEPTH, 4, 128, 128]); o_pn = dout("pn", [DEPTH, 4, 128]); o_pm = dout("pm", [DEPTH, 4])
    o_pp = dout("pp", [DEPTH, 15, 256])
    o_nC = dout("nC", [DEPTH, SB, 4, 128, 128]); o_nn = dout("nn", [DEPTH, SB, 4, 128])
    o_nm = dout("nm", [DEPTH, SB, 4]); o_np = dout("npool", [DEPTH, SB, 15, 256])
    o_nv = dout("nv", [DEPTH, SB, ST, 256])

    with ExitStack() as st0:
        kb = KB(nc, st0)
        op = kb.op
        OUT = kb.buf("outs", dma=True)

        def out_dma(dst, src, reads):
            kb.dma("sp", lambda q: q.dma_start(out=dst, in_=src), OUT, reads=reads)

        X = kb.sb("X", [128, NT + 1, D])
        XB = [kb.buf("X%d" % t) for t in range(NT + 1)]
        XL = kb.buf("xload", dma=True)
        CST = Tn(kb, "CST", [128, NCST], dma=True)
        EPSB = Tn(kb, "EPSB", [128, 1])
        kb.op("dve", lambda e: e.memset(EPSB[:, :], LN_EPS), writes=[EPSB.b])

        def C(name, P=128, w=None):
            if name in coff:
                o, n = coff[name]
                return CST[:P, o:o + (n if w is None else w)]
            o, n = coff1[name]
            return kb.cst1[:P, o:o + (n if w is None else w)]

        def Cg(name, g, P, blk, w):
            o, n = coff1[name]
            return kb.cst1[:P, o + g * blk: o + g * blk + w]

        with nc.allow_non_contiguous_dma(reason="small strided state/param loads"):
            kb.dma("sp", lambda q: q.dma_start(out=CST[:, :], in_=cst_d), CST.b, writes=[CST.b])
            for t in range(NT):
                kb.dma("sp", lambda q, t=t: q.dma_start(out=X[:, t, :], in_=xp[t * 128:(t + 1) * 128, :]),
                       XL, writes=[XB[t]])
            kb.dma("sp", lambda q: q.dma_start(out=X[:SP, NT, :], in_=xs), XL, writes=[XB[NT]])

            for l in range(n_layers):
                with ExitStack() as st1:
                    kb.stack = st1
                    kb.sfx = "_a%d" % l
                    _phase1(nc, kb, l, locals())
                    kb.barrier(extra=[OUT])
                    kb.emit()
                    kb.end_phase()
                if do_phase2:
                    with ExitStack() as st2:
                        kb.stack = st2
                        kb.sfx = "_b%d" % l
                        _phase2(nc, kb, l, locals())
                        kb.barrier(extra=[OUT])
                        kb.emit()
                        kb.end_phase()
            kb.stack = st0
            kb.sfx = ""
            for t in range(NT):
                out_dma(yp[t * 128:(t + 1) * 128, :], X[:, t, :], [XB[t]])
            out_dma(ys, X[:SP, NT, :], [XB[NT]])
            kb.emit(final_waits=[(OUT.sem, OUT.dma_total)])
    return nc, (cpack, cpack1)


def _ada(nc, kb, l, E, ADA, P, csrc, off, WCH, PM, hbuf, hT, PT, badac, WCR=None):
    op = kb.op
    C = E["C"]
    w_ada, b_ada = E["w_ada"], E["b_ada"]
    kb.dma("sp", lambda q: q.dma_start(out=hbuf[:P, :], in_=csrc), hbuf.b, writes=[hbuf.b])
    op("act", lambda e: e.activation(out=hbuf[:P, :], in_=hbuf[:P, :], func=AF.Silu), reads=[hbuf.b], writes=[hbuf.b])
    _transpose8(kb, E, hbuf, hT, PT, P)
    r32 = (hT.t.dtype == F32R)
    for c in range(3072 // WCW):
        i = c % 2
        c0 = off + c * WCW
        wch = WCH[c % len(WCH)]
        kb.dma("sp", lambda q: q.dma_start(out=wch[:, :, 0:WCW], in_=w_ada[l, c0 // WCW].rearrange("p (k j) -> p k j", k=8)), wch.b, writes=[wch.b])
        kb.dma("sp", lambda q: q.dma_start(out=badac[i][:P, 0:WCW], in_=b_ada[l, :P, c0:c0 + WCW]), badac[i].b, writes=[badac[i].b])
        wsrc = WCR[c % 2] if r32 else wch
        if r32:
            op("act", lambda e: e.copy(out=wsrc[:, :, 0:WCW], in_=wch[:, :, 0:WCW]), reads=[wch.b], writes=[wsrc.b])
        for k in range(8):
            if r32:
                op("pe", lambda e: e.matmul(PM[i][:, 0:WCW], lhsT=hT[:, k, :], rhs=wsrc[:, k, 0:WCW], start=(k == 0), stop=(k == 7)),
                   reads=[hT.b, wsrc.b], writes=[PM[i].b] if k in (0, 7) else [])
            else:
                op("pe", lambda e: e.matmul(PM[i][:P, 0:WCW], lhsT=hT[:, k, :P], rhs=wch[:, k, 0:WCW], start=(k == 0), stop=(k == 7)),
                   reads=[hT.b, wch.b], writes=[PM[i].b] if k in (0, 7) else [])
        op("dve", lambda e: e.tensor_tensor(out=ADA[:P, c * WCW:(c + 1) * WCW], in0=PM[i][:P, 0:WCW], in1=badac[i][:P, 0:WCW], op=ALU.add),
           reads=[PM[i].b, badac[i].b], writes=[ADA.b])
    op("dve", lambda e: e.tensor_scalar_add(out=ADA[:P, 1024:2048], in0=ADA[:P, 1024:2048], scalar1=1.0), reads=[ADA.b], writes=[ADA.b])


def _transpose8(kb, E, src, dstT, PT, P, srcb=None, op=None):
    op = kb.op if op is None else op
    C = E["C"]
    sb_ = src.b if srcb is None else srcb
    for half in range(2):
        for j in range(4):
            k = half * 4 + j
            op("pe", lambda e, half=half, j=j, k=k: e.transpose(
                out=PT[half][:, j * 128:j * 128 + P], in_=src[:P, k * 128:(k + 1) * 128], identity=C("ident", P, P)),
               reads=[sb_, E["CST"].b], writes=[PT[half].b])
        op("act", lambda e, half=half: e.copy(
            out=dstT[:, half * 4:half * 4 + 4, :P],
            in_=PT[half][:, :].rearrange("p (j c) -> p j c", j=4)[:, :, :P]),
           reads=[PT[half].b], writes=[dstT.b])


def _phase1(nc, kb, l, E):
    op = kb.op
    C, Cg, CST, X, XB = E["C"], E["Cg"], E["CST"], E["X"], E["XB"]
    EPSB = E["EPSB"]
    out_dma = E["out_dma"]
    w_in, w_o = E["w_in"], E["w_o"]

    kb.cst1 = kb.sb("CST1", [128, E["NCST1"]])
    kb.dma("sp", lambda q: q.dma_start(out=kb.cst1[:, :], in_=E["cst1_d"]), CST.b, writes=[CST.b])
    ADA = Tn(kb, "ADA1", [128, 3072]); ADAs = ADA
    WCH = [Tn(kb, "WCH%d" % i, [128, 8, WCW], dma=True) for i in range(2)]
    WCR = [Tn(kb, "WCR%d" % i, [128, 8, WCW], F32R) for i in range(2)]
    badac = [Tn(kb, "bada%d" % i, [128, 256], dma=True) for i in range(2)]
    H = Tn(kb, "H", [128, D], dma=True); HT = Tn(kb, "HT", [128, 8, 128], F32R)
    PROJ = Tn(kb, "PROJ", [128, IN_COLS], dma=True)
    Y = Tn(kb, "Y", [128, D])
    PRM = Tn(kb, "PRM", [128, 8 + 512 + 256 * 3 + 2 * D], dma=True)
    WS = Tn(kb, "WS", [128, 4, 128], dma=True); BS = Tn(kb, "BS", [128, 4], dma=True)
    WSs = Tn(kb, "WSs", [128, 4, SP], dma=True); BSs = Tn(kb, "BSs", [128, 4], dma=True)
    WP = Tn(kb, "WP", [64, 4, 64], dma=True)
    PT = [Tn(kb, "PT%d" % i, [128, 512], psum=True) for i in range(2)]
    PM = [Tn(kb, "PM%d" % i, [128, 512], psum=True) for i in range(2)]
    PA = Tn(kb, "PA", [128, 512], psum=True); PB = Tn(kb, "PB", [128, 512], psum=True)
    PC = Tn(kb, "PC", [128, 512], psum=True); PD = Tn(kb, "PD", [128, 512], psum=True)
    SM = Tn(kb, "SM", [128, 64])
    SMs = Tn(kb, "SMs", [128, 4], dma=True)
    MREP = Tn(kb, "MREP", [128, 4])
    CTX = Tn(kb, "CTX", [128, 4, 129], dma=True)
    DG = Tn(kb, "DG", [128, 128]); DL = Tn(kb, "DL", [128, 128]); WI = Tn(kb, "WI", [128, 128])
    AM = Tn(kb, "AM", [128, 128]); AT = Tn(kb, "AT", [128, 128])
    QT = Tn(kb, "QT", [128, 128]); KT = Tn(kb, "KT", [128, 128])
    VX = Tn(kb, "VX", [128, 129]); TOT = Tn(kb, "TOT", [128, 129]); WV = Tn(kb, "WV", [128, 129])
    HN = Tn(kb, "HN", [128, 128]); SG = Tn(kb, "SG", [128, 128]); ST6 = Tn(kb, "ST6", [128, 2, 6])
    OUTC = Tn(kb, "OUTC", [128, 128], dma=True)
    CN = Tn(kb, "CN", [128, SB, 128], dma=True); CTS = Tn(kb, "CTS", [128, SB, 129])
    RA = Tn(kb, "RA", [128, SB, 128])

    class _V2:
        def __init__(self, ap, b):
            self.t = ap
            self.b = b

        def __getitem__(self, k):
            return self.t[k]
    ZQ = _V2(RA[:, :, :].rearrange("p a b -> p (a b)")[:, 0:SB * SP], RA.b)
    NNAT = Tn(kb, "NNAT", [SB, 4, 128], dma=True); NTH = Tn(kb, "NTH", [128, SB], dma=True)
    WCB = Tn(kb, "WCB", [128, 16]); DECD = Tn(kb, "DECD", [128, 16]); DECR = Tn(kb, "DECR", [128, 16])
    MSO = Tn(kb, "MSO", [SB, 4], dma=True)
    PREV = Tn(kb, "PREV", [128, 256]); PTT = Tn(kb, "PTT", [64, 4, 128])
    SPA = Tn(kb, "SPA", [128, 256], dma=True); SPB = Tn(kb, "SPB", [128, 256], dma=True)
    VN = Tn(kb, "VN", [128, 256], dma=True); VTMP = Tn(kb, "VTMP", [128, 256])

    o_bg, o_mh, o_sg, o_sb, o_ps, o_l1g, o_l1b = 0, 8, 520, 776, 1032, 1288, 1288 + D
    for (o, w, src) in [(o_bg, 8, E["b_gate"]), (o_mh, 512, E["mh_g"]), (o_sg, 256, E["sgu_g"]), (o_sb, 256, E["sgu_b"]),
                        (o_ps, 256, E["pscale"]), (o_l1g, D, E["ln1g"]), (o_l1b, D, E["ln1b"])]:
        kb.dma("sp", lambda q, o=o, w=w, src=src: q.dma_start(out=PRM[:, o:o + w], in_=src[l]), PRM.b, writes=[PRM.b])
    kb.dma("sp", lambda q: q.dma_start(out=WS[:, :, :], in_=E["w_sT"][l]), WS.b, writes=[WS.b])
    kb.dma("sp", lambda q: q.dma_start(out=BS[:, :], in_=E["b_sT"][l]), BS.b, writes=[BS.b])
    kb.dma("sp", lambda q: q.dma_start(out=WSs[:SP, :, :], in_=E["w_sS"][l]), WSs.b, writes=[WSs.b])
    kb.dma("sp", lambda q: q.dma_start(out=BSs[:SP, :], in_=E["b_sS"][l]), BSs.b, writes=[BSs.b])
    kb.dma("sp", lambda q: q.dma_start(out=WP[:, :, :], in_=E["w_pool"][l]), WP.b, writes=[WP.b])
    for g in range(4):
        op("dve", lambda e, g=g: e.tensor_tensor(out=WS[:, g, :], in0=WS[:, g, :], in1=C("triu"), op=ALU.mult),
           reads=[WS.b, CST.b], writes=[WS.b])
        op("dve", lambda e, g=g: e.tensor_tensor(out=WSs[:SP, g, :], in0=WSs[:SP, g, :], in1=C("tri_s", SP, SP), op=ALU.mult),
           reads=[WSs.b, CST.b], writes=[WSs.b])
    op("dve", lambda e: e.memset(CTX[:, :, :], 0.0), writes=[CTX.b])
    op("dve", lambda e: e.memset(MREP[:, :], 0.0), writes=[MREP.b])
    op("dve", lambda e: e.memset(VX[:, :], 1.0), writes=[VX.b])

    _ada(nc, kb, l, E, ADA, 128, E["cp"], 0, WCH, PM, H, HT, PT, badac, WCR)

    w_in_v = w_in[l]
    w_o_v = w_o[l]
    def mk_chunks(n):
        return [(c0, min(WCW, n - c0)) for c0 in range(0, n, WCW)]
    chunks = mk_chunks(IN_COLS)
    wctr = [0]

    def stream_mm(wview, c0, w, lhsT, P, evac):
        i = wctr[0] % 2
        wctr[0] += 1
        kb.dma("sp", lambda q: q.dma_start(out=WCH[i][:, :, :], in_=wview[c0 // WCW].rearrange("p (k j) -> p k j", k=8)), WCH[i].b, writes=[WCH[i].b])
        wr = WCR[i]
        if wctr[0] % 3 == 0:
            op("dve", lambda e: e.tensor_copy(out=wr[:, :, 0:w], in_=WCH[i][:, :, 0:w]), reads=[WCH[i].b], writes=[wr.b])
        else:
            op("act", lambda e: e.copy(out=wr[:, :, 0:w], in_=WCH[i][:, :, 0:w]), reads=[WCH[i].b], writes=[wr.b])
        for k in range(8):
            op("pe", lambda e, k=k: e.matmul(PM[i][:, 0:w], lhsT=lhsT[:, k, :], rhs=wr[:, k, 0:w],
                                              start=(k == 0), stop=(k == 7)),
               reads=[lhsT.b, wr.b], writes=[PM[i].b] if k in (0, 7) else [])
        evac(PM[i], i)

    for t in range(NT + 1):
        is_s = (t == NT)
        P = SP if is_s else 128
        ada = ADAs if is_s else ADA
        if is_s:
            _ada(nc, kb, l, E, ADAs, SP, E["cs"], 0, WCH, PM, H, HT, PT, badac, WCR)
        xt = X[:P, t, :]
        op("dve", lambda e: e.tensor_tensor(out=H[:P, :], in0=xt, in1=ada[:P, 1024:2048], op=ALU.mult),
           reads=[XB[t], ada.b], writes=[H.b])
        op("dve", lambda e: e.tensor_tensor(out=H[:P, :], in0=H[:P, :], in1=ada[:P, 0:1024], op=ALU.add),
           reads=[H.b, ada.b], writes=[H.b])
        _transpose8(kb, E, H, HT, PT, P)
        for (c0, w) in chunks:
            stream_mm(w_in_v, c0, w, HT, P,
                      lambda pm, i, c0=c0, w=w: op("act", lambda e: e.copy(out=PROJ[:P, c0:c0 + w], in_=pm[:P, 0:w]),
                                                   reads=[pm.b], writes=[PROJ.b]))
        tri = C("tri_s", SP, SP) if is_s else C("triu")
        negm = C("negm_s", SP, SP) if is_s else C("negm")
        selE = C("selend", SP, SP) if is_s else C("sel127")
        if is_s:
            kb.dma("sp", lambda q: q.dma_start(out=SMs[:SP, :], in_=E["sm"][l]), SMs.b, writes=[SMs.b])
            kb.dma("sp", lambda q: q.dma_start(out=NNAT[:, :, :], in_=E["snat"][l]), NNAT.b, writes=[NNAT.b])
        mtok = SMs if is_s else MREP
        op("dve", lambda e: e.tensor_tensor(out=SM[:P, 0:8], in0=PROJ[:P, 2048:2056], in1=PRM[:P, o_bg:o_bg + 8], op=ALU.add),
           reads=[PROJ.b, PRM.b], writes=[SM.b])
        op("dve", lambda e: e.scalar_tensor_tensor(out=SM[:P, 8:12], in0=SM[:P, 4:8], scalar=-1.0, in1=SM[:P, 4:8], op0=ALU.mult, op1=ALU.max),
           reads=[SM.b], writes=[SM.b])
        op("act", lambda e: e.activation(out=SM[:P, 12:16], in_=SM[:P, 8:12], func=AF.Exp, scale=-1.0), reads=[SM.b], writes=[SM.b])
        op("act", lambda e: e.activation(out=SM[:P, 12:16], in_=SM[:P, 12:16], func=AF.Ln, bias=1.0, scale=1.0),
           reads=[SM.b], writes=[SM.b])
        op("dve", lambda e: e.tensor_scalar_min(out=SM[:P, 16:20], in0=SM[:P, 4:8], scalar1=0.0), reads=[SM.b], writes=[SM.b])
        op("dve", lambda e: e.tensor_tensor(out=SM[:P, 16:20], in0=SM[:P, 16:20], in1=SM[:P, 12:16], op=ALU.subtract),
           reads=[SM.b], writes=[SM.b])
        op("pe", lambda e: e.matmul(PA[:P, 0:4], lhsT=tri, rhs=SM[:P, 16:20], start=True, stop=True),
           reads=[CST.b, SM.b], writes=[PA.b])
        op("act", lambda e: e.copy(out=SM[:P, 20:24], in_=PA[:P, 0:4]), reads=[PA.b], writes=[SM.b])
        op("dve", lambda e: e.tensor_tensor(out=SM[:P, 24:28], in0=SM[:P, 0:4], in1=SM[:P, 20:24], op=ALU.subtract),
           reads=[SM.b], writes=[SM.b])
        op("pe", lambda e: e.matmul(PA[:P, 8:12], lhsT=selE, rhs=SM[:P, 20:24], start=True, stop=True),
           reads=[CST.b, SM.b], writes=[PA.b])
        op("act", lambda e: e.copy(out=SM[:P, 28:32], in_=PA[:P, 8:12]), reads=[PA.b], writes=[SM.b])

        for hh in range(4):
            qs = PROJ[:P, hh * 128:(hh + 1) * 128]
            ks = PROJ[:P, 512 + hh * 128:512 + (hh + 1) * 128]
            vs = PROJ[:P, 1024 + hh * 128:1024 + (hh + 1) * 128]
            os_ = PROJ[:P, 1536 + hh * 128:1536 + (hh + 1) * 128]
            col = lambda c, hh=hh: SM[:P, c + hh:c + hh + 1]
            S1 = lambda c: SM[:P, c:c + 1]
            if is_s:
                kb.dma("sp", lambda q, hh=hh: q.dma_start(out=CN[:, :, :], in_=E["sC"][l, hh]), CN.b, writes=[CN.b])
                kb.dma("sp", lambda q, hh=hh: q.dma_start(out=NTH[:, :], in_=E["snT"][l, hh]), NTH.b, writes=[NTH.b])
                for j in range(4):
                    pt = PT[j % 2]
                    for jj in range(4):
                        b = j * 4 + jj
                        op("pe", lambda e, b=b, jj=jj, pt=pt: e.transpose(out=pt[:, jj * 128:(jj + 1) * 128], in_=CN[:, b, :],
                                                                       identity=C("ident")),
                           reads=[CN.b, CST.b], writes=[pt.b])
                    op("act", lambda e, j=j, pt=pt: e.copy(out=CTS[:, j * 4:(j + 1) * 4, 0:128],
                                                           in_=pt[:, :].rearrange("p (j c) -> p j c", j=4)),
                       reads=[pt.b], writes=[CTS.b])
                op("dve", lambda e: e.tensor_copy(out=CTS[:, :, 128:129], in_=NTH[:, :].unsqueeze(2)), reads=[NTH.b], writes=[CTS.b])
            op("dve", lambda e, hh=hh: e.tensor_scalar(out=DG[:P, :P], in0=C("ident", P, P), scalar1=col(24), scalar2=None,
                                                       op0=ALU.mult), reads=[SM.b, CST.b], writes=[DG.b])
            op("pe", lambda e: e.matmul(PB[:P, 0:P], lhsT=C("ones", P, P), rhs=DG[:P, :P], start=True, stop=True),
               reads=[DG.b, CST.b], writes=[PB.b])
            if is_s:
                op("dve", lambda e: e.tensor_tensor(out=DL[:P, :P], in0=PB[:P, 0:P], in1=C("negb_s", SP, SP), op=ALU.add),
                   reads=[PB.b, CST.b], writes=[DL.b])
                op("dve", lambda e: e.tensor_reduce(out=S1(32), in_=DL[:P, :P], axis=AX.X, op=ALU.max), reads=[DL.b], writes=[SM.b])
            else:
                op("dve", lambda e: e.tensor_reduce(out=S1(32), in_=PB[:P, 0:P], axis=AX.X, op=ALU.max), reads=[PB.b], writes=[SM.b])
            op("dve", lambda e, hh=hh: e.scalar_tensor_tensor(out=DL[:P, :P], in0=PB[:P, 0:P], scalar=col(20), in1=negm,
                                                              op0=ALU.add, op1=ALU.add),
               reads=[PB.b, SM.b, CST.b], writes=[DL.b])
            op("dve", lambda e: e.tensor_reduce(out=S1(33), in_=DL[:P, :P], axis=AX.X, op=ALU.max), reads=[DL.b], writes=[SM.b])
            op("dve", lambda e, hh=hh: e.tensor_tensor(out=S1(34), in0=col(20), in1=mtok[:P, hh:hh + 1], op=ALU.add),
               reads=[SM.b, mtok.b], writes=[SM.b])
            op("dve", lambda e: e.tensor_tensor(out=S1(35), in0=S1(34), in1=S1(33), op=ALU.max), reads=[SM.b], writes=[SM.b])
            op("dve", lambda e: e.tensor_scalar(out=S1(36), in0=S1(35), scalar1=-1.0, scalar2=None, op0=ALU.mult),
               reads=[SM.b], writes=[SM.b])
            op("act", lambda e: e.activation(out=WI[:P, :P], in_=DL[:P, :P], func=AF.Exp, bias=S1(36), scale=1.0),
               reads=[DL.b, SM.b], writes=[WI.b])
            op("act", lambda e: e.activation(out=S1(37), in_=S1(34), func=AF.Exp, bias=S1(36), scale=1.0), reads=[SM.b], writes=[SM.b])
            op("act", lambda e: e.activation(out=S1(38), in_=S1(36), func=AF.Exp), reads=[SM.b], writes=[SM.b])
            op("pe", lambda e: e.transpose(out=PC[:, 0:P], in_=qs, identity=C("ident", P, P)), reads=[PROJ.b, CST.b], writes=[PC.b])
            op("pe", lambda e: e.transpose(out=PC[:, 128:128 + P], in_=ks, identity=C("ident", P, P)), reads=[PROJ.b, CST.b], writes=[PC.b])
            op("act", lambda e: e.mul(out=QT[:, :P], in_=PC[:, 0:P], mul=128.0 ** -0.5), reads=[PC.b], writes=[QT.b])
            op("act", lambda e: e.copy(out=KT[:, :P], in_=PC[:, 128:128 + P]), reads=[PC.b], writes=[KT.b])
            op("pe", lambda e: e.matmul(PD[:P, 0:P], lhsT=QT[:, :P], rhs=KT[:, :P], start=True, stop=True),
               reads=[QT.b, KT.b], writes=[PD.b])
            op("dve", lambda e: e.tensor_tensor(out=AM[:P, :P], in0=WI[:P, :P], in1=PD[:P, 0:P], op=ALU.mult),
               reads=[WI.b, PD.b], writes=[AM.b])
            op("pe", lambda e: e.transpose(out=PB[:P, 128:128 + P], in_=AM[:P, :P], identity=C("ident", P, P)),
               reads=[AM.b, CST.b], writes=[PB.b])
            op("act", lambda e: e.copy(out=AT[:P, :P], in_=PB[:P, 128:128 + P]), reads=[PB.b], writes=[AT.b])
            op("pool", lambda e: e.tensor_copy(out=VX[:P, 0:128], in_=vs), reads=[PROJ.b], writes=[VX.b])
            op("pe", lambda e: e.matmul(PD[:P, 128:257], lhsT=AT[:P, :P], rhs=VX[:P, :], start=True, stop=True),
               reads=[AT.b, VX.b], writes=[PD.b])
            if is_s:
                op("pool", lambda e: e.memset(ZQ[:, :], 0.0), writes=[ZQ.b])
                for b in range(SB):
                    op("pool", lambda e, b=b: e.tensor_copy(out=ZQ[:, b * SP + b:(b + 1) * SP:SB], in_=QT[:, b:SP:SB]),
                       reads=[QT.b], writes=[ZQ.b])
                for b in range(SB):
                    op("pe", lambda e, b=b: e.matmul(PC[:P, 256:385], lhsT=ZQ[:, b * SP:(b + 1) * SP], rhs=CTS[:, b, :],
                                                     start=(b == 0), stop=(b == SB - 1)),
                       reads=[ZQ.b, CTS.b], writes=[PC.b] if b in (0, SB - 1) else [])
            else:
                op("pe", lambda e, hh=hh: e.matmul(PC[:P, 256:385], lhsT=QT[:, :P], rhs=CTX[:, hh, :], start=True, stop=True),
                   reads=[QT.b, CTX.b], writes=[PC.b])
            op("act", lambda e: e.activation(out=TOT[:P, :], in_=PC[:P, 256:385], func=AF.Identity, scale=S1(37)),
               reads=[PC.b, SM.b], writes=[TOT.b])
            op("dve", lambda e: e.tensor_tensor(out=TOT[:P, :], in0=TOT[:P, :], in1=PD[:P, 128:257], op=ALU.add),
               reads=[TOT.b, PD.b], writes=[TOT.b])
            op("dve", lambda e: e.scalar_tensor_tensor(out=S1(39), in0=TOT[:P, 128:129], scalar=-1.0, in1=TOT[:P, 128:129], op0=ALU.mult, op1=ALU.max),
               reads=[TOT.b], writes=[SM.b])
            op("dve", lambda e: e.tensor_tensor(out=S1(39), in0=S1(39), in1=S1(38), op=ALU.max), reads=[SM.b], writes=[SM.b])
            op("dve", lambda e: e.reciprocal(out=S1(40), in_=S1(39)), reads=[SM.b], writes=[SM.b])
            op("dve", lambda e: e.tensor_scalar(out=HN[:P, :], in0=TOT[:P, 0:128], scalar1=S1(40), scalar2=None, op0=ALU.mult),
               reads=[TOT.b, SM.b], writes=[HN.b])
            op("dve", lambda e: e.bn_stats(out=ST6[:P, 0, :], in_=HN[:P, :]), reads=[HN.b], writes=[ST6.b])
            op("dve", lambda e: e.bn_aggr(out=SM[:P, 41:43], in_=ST6[:P, 0, :]), reads=[ST6.b], writes=[SM.b])
            op("act", lambda e: e.activation(out=S1(43), in_=S1(42), func=AF.Ln, bias=EPSB[:P, 0:1], scale=1.0), reads=[SM.b, EPSB.b], writes=[SM.b])
            op("act", lambda e: e.activation(out=S1(44), in_=S1(43), func=AF.Exp, scale=-0.5), reads=[SM.b], writes=[SM.b])
            op("dve", lambda e: e.tensor_scalar(out=HN[:P, :], in0=HN[:P, :], scalar1=S1(41), scalar2=S1(44),
                                                op0=ALU.subtract, op1=ALU.mult), reads=[HN.b, SM.b], writes=[HN.b])
            op("dve", lambda e, hh=hh: e.tensor_tensor(out=HN[:P, :], in0=HN[:P, :],
                                                       in1=PRM[:P, o_mh + hh * 128:o_mh + (hh + 1) * 128], op=ALU.mult),
               reads=[HN.b, PRM.b], writes=[HN.b])
            op("act", lambda e: e.activation(out=SG[:P, :], in_=os_, func=AF.Exp, scale=-1.0), reads=[PROJ.b], writes=[SG.b])
            op("dve", lambda e: e.tensor_scalar_add(out=SG[:P, :], in0=SG[:P, :], scalar1=1.0), reads=[SG.b], writes=[SG.b])
            op("dve", lambda e: e.reciprocal(out=SG[:P, :], in_=SG[:P, :]), reads=[SG.b], writes=[SG.b])
            op("dve", lambda e, hh=hh: e.tensor_tensor(out=Y[:P, hh * 128:(hh + 1) * 128], in0=HN[:P, :], in1=SG[:P, :], op=ALU.mult),
               reads=[HN.b, SG.b], writes=[Y.b])
            op("dve", lambda e, hh=hh: e.tensor_tensor(out=S1(45), in0=mtok[:P, hh:hh + 1], in1=S1(32), op=ALU.max),
               reads=[SM.b, mtok.b], writes=[SM.b])
            op("dve", lambda e, hh=hh: e.tensor_tensor(out=S1(45), in0=S1(45), in1=col(28), op=ALU.add), reads=[SM.b], writes=[SM.b])
            op("dve", lambda e, hh=hh: e.tensor_tensor(out=S1(46), in0=col(28), in1=S1(45), op=ALU.subtract), reads=[SM.b], writes=[SM.b])
            op("act", lambda e, hh=hh: e.activation(out=S1(47), in_=col(24), func=AF.Exp, bias=S1(46), scale=1.0),
               reads=[SM.b], writes=[SM.b])
            op("act", lambda e, hh=hh: e.activation(out=S1(48), in_=mtok[:P, hh:hh + 1], func=AF.Exp, bias=S1(46), scale=1.0),
               reads=[SM.b, mtok.b], writes=[SM.b])
            if not is_s:
                op("dve", lambda e: e.tensor_scalar(out=WV[:P, :], in0=VX[:P, :], scalar1=S1(47), scalar2=None, op0=ALU.mult),
                   reads=[VX.b, SM.b], writes=[WV.b])
                op("pe", lambda e: e.matmul(PB[:, 256:385], lhsT=ks, rhs=WV[:P, :], start=True, stop=True),
                   reads=[PROJ.b, WV.b], writes=[PB.b])
                op("dve", lambda e, hh=hh: e.scalar_tensor_tensor(out=CTX[:, hh, :], in0=CTX[:, hh, :], scalar=S1(48), in1=PB[:, 256:385],
                                                                  op0=ALU.mult, op1=ALU.add),
                   reads=[CTX.b, SM.b, PB.b], writes=[CTX.b])
                op("dve", lambda e, hh=hh: e.tensor_copy(out=MREP[:, hh:hh + 1], in_=S1(45)), reads=[SM.b], writes=[MREP.b])
                if t == NT - 1:
                    op("pe", lambda e, hh=hh: e.transpose(out=PA[:, 128:256], in_=CTX[:, hh, 0:128], identity=C("ident")),
                       reads=[CTX.b, CST.b], writes=[PA.b])
                    op("act", lambda e: e.copy(out=OUTC[:, :], in_=PA[:, 128:256]), reads=[PA.b], writes=[OUTC.b])
                    out_dma(E["o_pC"][l, hh], OUTC[:, :], [OUTC.b])
                    out_dma(E["o_pn"][l, hh].rearrange("(k o) -> k o", o=1), CTX[:, hh, 128:129], [CTX.b])
                    if hh == 3:
                        out_dma(E["o_pm"][l:l + 1, :], MREP[0:1, :], [MREP.b])
            else:
                op("dve", lambda e: e.tensor_scalar(out=WCB[:P, :], in0=C("onehotB", SP), scalar1=S1(47), scalar2=None, op0=ALU.mult),
                   reads=[SM.b, CST.b], writes=[WCB.b])
                op("dve", lambda e: e.tensor_tensor(out=RA[:P, :, :], in0=vs.unsqueeze(1).broadcast_to([P, SB, 128]),
                                                    in1=WCB[:P, :].unsqueeze(2).broadcast_to([P, SB, 128]), op=ALU.mult),
                   reads=[PROJ.b, WCB.b], writes=[RA.b])
                op("dve", lambda e: e.tensor_scalar(out=DECD[:P, :], in0=C("onehot0", SP), scalar1=S1(48), scalar2=None, op0=ALU.mult),
                   reads=[SM.b, CST.b], writes=[DECD.b])
                op("pe", lambda e: e.matmul(PA[:, 16:32], lhsT=C("ones", SP, 128), rhs=DECD[:P, :], start=True, stop=True),
                   reads=[DECD.b, CST.b], writes=[PA.b])
                op("act", lambda e: e.copy(out=DECR[:, :], in_=PA[:, 16:32]), reads=[PA.b], writes=[DECR.b])
                for b in range(SB):
                    pq = [PA, PB, PC, PD][b % 4]
                    op("pe", lambda e, b=b, pq=pq: e.matmul(pq[:, 384:512], lhsT=RA[:P, b, :], rhs=ks, start=True, stop=True),
                       reads=[RA.b, PROJ.b], writes=[pq.b])
                    op("dve", lambda e, b=b, pq=pq: e.scalar_tensor_tensor(out=CN[:, b, :], in0=CN[:, b, :], scalar=DECR[:, b:b + 1],
                                                                           in1=pq[:, 384:512], op0=ALU.mult, op1=ALU.add),
                       reads=[CN.b, DECR.b, pq.b], writes=[CN.b])
                out_dma(E["o_nC"][l, :, hh].rearrange("b v k -> v b k"), CN[:, :, :], [CN.b])
                op("pe", lambda e: e.matmul(PA[:SB, 32:160], lhsT=WCB[:P, :], rhs=ks, start=True, stop=True),
                   reads=[WCB.b, PROJ.b], writes=[PA.b])
                op("dve", lambda e, hh=hh: e.scalar_tensor_tensor(out=NNAT[:, hh, :], in0=NNAT[:, hh, :], scalar=SM[:SB, 48:49],
                                                                  in1=PA[:SB, 32:160], op0=ALU.mult, op1=ALU.add),
                   reads=[NNAT.b, SM.b, PA.b], writes=[NNAT.b])
                op("dve", lambda e, hh=hh: e.tensor_copy(out=MSO[:, hh:hh + 1], in_=SM[:SB, 45:46]), reads=[SM.b], writes=[MSO.b])
                if hh == 3:
                    out_dma(E["o_nn"][l], NNAT[:, :, :], [NNAT.b])
                    out_dma(E["o_nm"][l], MSO[:, :], [MSO.b])

        vsv = PROJ[:P, 2312:2568].rearrange("p (g d) -> p g d", g=4)
        op("dve", lambda e: e.tensor_reduce(out=SM[:P, 50:54], in_=vsv, axis=AX.X, op=ALU.add), reads=[PROJ.b], writes=[SM.b])
        op("dve", lambda e: e.tensor_scalar(out=SM[:P, 50:54], in0=SM[:P, 50:54], scalar1=1.0 / 64, scalar2=None, op0=ALU.mult),
           reads=[SM.b], writes=[SM.b])
        op("dve", lambda e: e.tensor_tensor(out=VN[:P, :].rearrange("p (g d) -> p g d", g=4), in0=vsv,
                                            in1=SM[:P, 50:54].unsqueeze(2).broadcast_to([P, 4, 64]), op=ALU.subtract),
           reads=[PROJ.b, SM.b], writes=[VN.b])
        op("pool", lambda e: e.tensor_tensor(out=VTMP[:P, :], in0=VN[:P, :], in1=VN[:P, :], op=ALU.mult), reads=[VN.b], writes=[VTMP.b])
        op("dve", lambda e: e.tensor_reduce(out=SM[:P, 54:58], in_=VTMP[:P, :].rearrange("p (g d) -> p g d", g=4), axis=AX.X, op=ALU.add),
           reads=[VTMP.b], writes=[SM.b])
        op("act", lambda e: e.activation(out=SM[:P, 54:58], in_=SM[:P, 54:58], func=AF.Ln, bias=EPSB[:P, 0:1], scale=1.0 / 64),
           reads=[SM.b, EPSB.b], writes=[SM.b])
        op("act", lambda e: e.activation(out=SM[:P, 58:62], in_=SM[:P, 54:58], func=AF.Exp, scale=-0.5), reads=[SM.b], writes=[SM.b])
        op("dve", lambda e: e.tensor_tensor(out=VN[:P, :].rearrange("p (g d) -> p g d", g=4), in0=VN[:P, :].rearrange("p (g d) -> p g d", g=4),
                                            in1=SM[:P, 58:62].unsqueeze(2).broadcast_to([P, 4, 64]), op=ALU.mult),
           reads=[VN.b, SM.b], writes=[VN.b])
        op("pool", lambda e: e.tensor_tensor(out=VN[:P, :], in0=VN[:P, :], in1=PRM[:P, o_sg:o_sg + 256], op=ALU.mult),
           reads=[VN.b, PRM.b], writes=[VN.b])
        op("pool", lambda e: e.tensor_tensor(out=VN[:P, :], in0=VN[:P, :], in1=PRM[:P, o_sb:o_sb + 256], op=ALU.add),
           reads=[VN.b, PRM.b], writes=[VN.b])
        wsl = WSs if is_s else WS
        bsl = BSs if is_s else BS
        for g in range(4):
            op("pe", lambda e, g=g: e.matmul(PC[:P, g * 64:(g + 1) * 64], lhsT=wsl[:P, g, :P], rhs=VN[:P, g * 64:(g + 1) * 64],
                                             start=True, stop=True), reads=[wsl.b, VN.b], writes=[PC.b])
        for g in range(4):
            op("dve", lambda e, g=g: e.scalar_tensor_tensor(out=Y[:P, 512 + g * 64:512 + (g + 1) * 64], in0=PC[:P, g * 64:(g + 1) * 64],
                                                            scalar=bsl[:P, g:g + 1], in1=PROJ[:P, 2056 + g * 64:2056 + (g + 1) * 64],
                                                            op0=ALU.add, op1=ALU.mult),
               reads=[PC.b, bsl.b, PROJ.b], writes=[Y.b])
        if is_s:
            for tq in range(ST):
                out_dma(E["o_nv"][l][:, tq, :], VN[tq * SB:(tq + 1) * SB, :], [VN.b])

        pin = lambda g: PROJ[:P, 2568 + g * 64:2568 + (g + 1) * 64]
        if is_s:
            kb.dma("sp", lambda q: q.dma_start(out=SPA[:, :], in_=E["spA"][l]), SPA.b, writes=[SPA.b])
            kb.dma("sp", lambda q: q.dma_start(out=SPB[:112, :], in_=E["spB"][l]), SPB.b, writes=[SPB.b])
            for g in range(4):
                op("pe", lambda e, g=g: e.matmul(PA[:64, g * 128:g * 128 + P], lhsT=SPA[:, g * 64:(g + 1) * 64], rhs=Cg("bsA", g, 128, SP, SP),
                                                 start=True, stop=False), reads=[SPA.b, CST.b], writes=[PA.b])
                op("pe", lambda e, g=g: e.matmul(PA[:64, g * 128:g * 128 + P], lhsT=SPB[:112, g * 64:(g + 1) * 64], rhs=Cg("bsB", g, 112, SP, SP),
                                                 start=False, stop=False), reads=[SPB.b, CST.b], writes=[])
                op("pe", lambda e, g=g: e.matmul(PA[:64, g * 128:g * 128 + P], lhsT=pin(g), rhs=Cg("bsC", g, SP, SP, SP),
                                                 start=False, stop=True), reads=[PROJ.b, CST.b], writes=[PA.b])
            npv = E["o_np"][l].rearrange("b r c -> r b c")
            for r in range(4):
                out_dma(npv[r], SPA[64 + r * SB:64 + (r + 1) * SB, :], [SPA.b])
            for r in range(7):
                out_dma(npv[4 + r], SPB[r * SB:(r + 1) * SB, :], [SPB.b])
            for r in range(4):
                out_dma(npv[11 + r], PROJ[r * SB:(r + 1) * SB, 2568:2824], [PROJ.b])
        else:
            for g in range(4):
                band = Cg("bandc0" if t == 0 else "bandc", g, 128, 128, 128)
                op("pe", lambda e, g=g, band=band: e.matmul(PA[:64, g * 128:(g + 1) * 128], lhsT=pin(g), rhs=band, start=True, stop=(t == 0)),
                   reads=[PROJ.b, CST.b], writes=[PA.b])
                if t > 0:
                    op("pe", lambda e, g=g: e.matmul(PA[:64, g * 128:(g + 1) * 128], lhsT=PREV[:, g * 64:(g + 1) * 64],
                                                     rhs=Cg("bandp", g, 128, 128, 128), start=False, stop=True),
                       reads=[PREV.b, CST.b], writes=[PA.b])
            if t < NT - 1:
                op("pool", lambda e: e.tensor_copy(out=PREV[:, :], in_=PROJ[:, 2568:2824]), reads=[PROJ.b], writes=[PREV.b])
            else:
                out_dma(E["o_pp"][l], PROJ[113:128, 2568:2824], [PROJ.b])
        op("act", lambda e: e.copy(out=PTT[:, :, :P], in_=PA[:64, :].rearrange("p (g c) -> p g c", g=4)[:, :, :P]),
           reads=[PA.b], writes=[PTT.b])
        for g in range(4):
            op("pe", lambda e, g=g: e.matmul(PB[:P, g * 64:(g + 1) * 64], lhsT=PTT[:, g, :P], rhs=WP[:, g, :], start=True, stop=True),
               reads=[PTT.b, WP.b], writes=[PB.b])
        op("dve", lambda e: e.tensor_tensor(out=Y[:P, 768:1024], in0=PB[:P, 0:256], in1=PRM[:P, o_ps:o_ps + 256], op=ALU.mult),
           reads=[PB.b, PRM.b], writes=[Y.b])

        _transpose8(kb, E, Y, HT, PT, P)
        for (c0, w) in mk_chunks(D):
            stream_mm(w_o_v, c0, w, HT, P,
                      lambda pm, i, c0=c0, w=w: op("dve", lambda e: e.tensor_tensor(out=H[:P, c0:c0 + w], in0=pm[:P, 0:w],
                                                                                    in1=ada[:P, 2048 + c0:2048 + c0 + w], op=ALU.mult),
                                                   reads=[pm.b, ada.b], writes=[H.b]))
        _resid_ln(kb, X, XB[t], t, P, H, SM, ST6, PRM, o_l1g, o_l1b, EPSB)


def _resid_ln(kb, X, xb, t, P, Z, SM, ST6, PRM, og, ob, EPSB):
    op = kb.op
    xt = X[:P, t, :]
    op("dve", lambda e: e.scalar_tensor_tensor(out=Z[:P, :], in0=xt, scalar=ALPHA, in1=Z[:P, :], op0=ALU.mult, op1=ALU.add),
       reads=[xb, Z.b], writes=[Z.b])
    op("dve", lambda e: e.bn_stats(out=ST6[:P, 0, :], in_=Z[:P, 0:512]), reads=[Z.b], writes=[ST6.b])
    op("dve", lambda e: e.bn_stats(out=ST6[:P, 1, :], in_=Z[:P, 512:1024]), reads=[Z.b], writes=[ST6.b])
    op("dve", lambda e: e.bn_aggr(out=SM[:P, 41:43], in_=ST6[:P, :, :].rearrange("p a b -> p (a b)")), reads=[ST6.b], writes=[SM.b])
    op("act", lambda e: e.activation(out=SM[:P, 43:44], in_=SM[:P, 42:43], func=AF.Ln, bias=EPSB[:P, 0:1], scale=1.0), reads=[SM.b, EPSB.b], writes=[SM.b])
    op("act", lambda e: e.activation(out=SM[:P, 44:45], in_=SM[:P, 43:44], func=AF.Exp, scale=-0.5), reads=[SM.b], writes=[SM.b])
    op("dve", lambda e: e.tensor_scalar(out=Z[:P, :], in0=Z[:P, :], scalar1=SM[:P, 41:42], scalar2=SM[:P, 44:45],
                                        op0=ALU.subtract, op1=ALU.mult), reads=[Z.b, SM.b], writes=[Z.b])
    op("pool", lambda e: e.tensor_tensor(out=Z[:P, :], in0=Z[:P, :], in1=PRM[:P, og:og + D], op=ALU.mult), reads=[Z.b, PRM.b], writes=[Z.b])
    op("dve", lambda e: e.tensor_tensor(out=xt, in0=Z[:P, :], in1=PRM[:P, ob:ob + D], op=ALU.add), reads=[Z.b, PRM.b], writes=[xb])


def _phase2(nc, kb, l, E):
    C, CST, X, XB = E["C"], E["CST"], E["X"], E["XB"]
    ADA = Tn(kb, "ADA2", [128, 3072]); ADAs = ADA
    WCH = [Tn(kb, "WCHb%d" % i, [128, 8, 128], dma=True) for i in range(2)]
    badac = [Tn(kb, "badab%d" % i, [128, 256], dma=True) for i in range(2)]
    Hs = [Tn(kb, "H2_%d" % i, [128, D], dma=True) for i in range(2)]
    HT = Tn(kb, "H2T", [128, 8, 128])
    PRM = Tn(kb, "PRM2", [128, 2 * D], dma=True)
    KTS = Tn(kb, "KTS", [128, 16, 128], dma=True)
    S0 = Tn(kb, "S0", [128, 2048]); S1_ = Tn(kb, "S1", [128, 2048]); S2 = Tn(kb, "S2", [128, 2048])
    TOPS = Tn(kb, "TOPS", [128, 16, 16]); IDXU = Tn(kb, "IDXU", [128, 16, 16], U32); IDXF = Tn(kb, "IDXF", [128, 16, 16])
    CV = Tn(kb, "CV", [128, 8, 16]); CPOS = Tn(kb, "CPOS", [128, 8, 16], U32)
    PAU = Tn(kb, "PAU", [128, 8, 16], U32); PBU = Tn(kb, "PBU", [128, 8, 16], U32)
    PAF = Tn(kb, "PAF", [128, 8, 16]); PBF = Tn(kb, "PBF", [128, 8, 16])
    I1 = Tn(kb, "I1", [128, 128]); I2 = Tn(kb, "I2", [128, 128])
    IDXs = [Tn(kb, "IDX%d" % i, [128, 128], I32) for i in range(2)]
    GATEs = [Tn(kb, "GATE%d" % i, [128, 128]) for i in range(2)]
    ACTV = Tn(kb, "ACTV", [128, 128]); COEF = Tn(kb, "COEF", [128, 128])
    SMf = Tn(kb, "SM2f", [128, 16]); SM = Tn(kb, "SM2", [128, 64]); ST6 = Tn(kb, "ST62", [128, 2, 6])
    NB = NBUF
    UB = [Tn(kb, "UB%d" % i, [128, 2 * D], dma=True) for i in range(NB)]
    ACTB = [kb.buf("actv%d" % i) for i in range(NB)]
    COEFB = [kb.buf("coef%d" % i) for i in range(NB)]
    COEF2 = Tn(kb, "COEF2", [128, 128])
    ACC = Tn(kb, "ACC", [128, D])

    class _View:
        def __init__(self, ap, b):
            self.t = ap
            self.b = b

        def __getitem__(self, k):
            return self.t[k]
    WCA = [_View(UB[0][:, :].rearrange("p (k c) -> p k c", k=8), UB[0].b)]
    PT = [Tn(kb, "PTb%d" % i, [128, 512], psum=True) for i in range(2)]
    PQ = [Tn(kb, "PQ%d" % i, [128, 512], psum=True) for i in range(2)]
    PM = PQ
    JUNK = Tn(kb, "JUNKP", [128, 2, 512], psum=True)
    PS = [Tn(kb, "PS%d" % i, [128, 512], psum=True) for i in range(2)]

    kb.dma("sp", lambda q: q.dma_start(out=PRM[:, 0:D], in_=E["ln2g"][l]), PRM.b, writes=[PRM.b])
    kb.dma("sp", lambda q: q.dma_start(out=PRM[:, D:2 * D], in_=E["ln2b"][l]), PRM.b, writes=[PRM.b])
    kb.dma("sp", lambda q: q.dma_start(out=KTS[:, :, :], in_=E["keysT"][l]), KTS.b, writes=[KTS.b])
    wpq_v = E["w_pq"][l]
    tab = E["puv"][l]
    wctr = [0]

    def make_front(t, sset):
        items = []

        def op(e, fn, reads=(), writes=()):
            items.append(("op", e, _bind(fn), None, tuple(reads), tuple(writes)))

        def dma(e, fn, owner, reads=(), writes=()):
            items.append(("dma", e, _bind(fn), owner, tuple(reads), tuple(writes)))
        is_s = (t == NT)
        P = SP if is_s else 128
        ada = ADAs if is_s else ADA
        H = Hs[sset]; IDX = IDXs[sset]; GATE = GATEs[sset]
        QT = S0
        xt = X[:P, t, :]
        op("dve", lambda e: e.tensor_tensor(out=H[:P, :], in0=xt, in1=ada[:P, 1024:2048], op=ALU.mult), reads=[XB[t], ada.b], writes=[H.b])
        op("dve", lambda e: e.tensor_tensor(out=H[:P, :], in0=H[:P, :], in1=ada[:P, 0:1024], op=ALU.add), reads=[H.b, ada.b], writes=[H.b])
        _transpose8(kb, E, H, HT, PT, P, op=op)
        for c in range(16):
            i = wctr[0] % 2
            wctr[0] += 1
            j = c % 2
            dma("sp", lambda q: q.dma_start(out=WCH[i][:, :, :], in_=wpq_v[c].rearrange("p (k j) -> p k j", k=8)), WCH[i].b, writes=[WCH[i].b])
            for k in range(8):
                op("pe", lambda e: e.matmul(PQ[j][:, 0:P], lhsT=WCH[i][:, k, :], rhs=HT[:, k, :P], start=(k == 0), stop=(k == 7)),
                   reads=[WCH[i].b, HT.b], writes=[PQ[j].b] if k in (0, 7) else [])
            op("act", lambda e: e.copy(out=QT[:, c * 128:c * 128 + P], in_=PQ[j][:, 0:P]), reads=[PQ[j].b], writes=[QT.b])
        for c4 in range(4):
            ps = PS[c4 % 2]
            for j in range(4):
                c = c4 * 4 + j
                op("pe", lambda e: e.matmul(ps[:P, j * 128:(j + 1) * 128], lhsT=QT[:, c * 128:c * 128 + P], rhs=KTS[:, c, :], start=True, stop=True),
                   reads=[QT.b, KTS.b], writes=[ps.b])
            op("act", lambda e: e.copy(out=S1_[:P, c4 * 512:(c4 + 1) * 512], in_=ps[:P, :]), reads=[ps.b], writes=[S1_.b])
        for c in range(16):
            sc = S1_[:P, c * 128:(c + 1) * 128]
            wk = S2[:P, c * 128:(c + 1) * 128]
            op("dve", lambda e: e.max(out=TOPS[:P, c, 0:8], in_=sc), reads=[S1_.b], writes=[TOPS.b])
            op("dve", lambda e: e.max_index(out=IDXU[:P, c, 0:8], in_max=TOPS[:P, c, 0:8], in_values=sc), reads=[S1_.b, TOPS.b], writes=[IDXU.b])
            op("dve", lambda e: e.match_replace(out=wk, in_to_replace=TOPS[:P, c, 0:8], in_values=sc, imm_value=NEG), reads=[S1_.b, TOPS.b], writes=[S2.b])
            op("dve", lambda e: e.max(out=TOPS[:P, c, 8:16], in_=wk), reads=[S2.b], writes=[TOPS.b])
            op("dve", lambda e: e.max_index(out=IDXU[:P, c, 8:16], in_max=TOPS[:P, c, 8:16], in_values=wk), reads=[S2.b, TOPS.b], writes=[IDXU.b])
        op("dve", lambda e: e.tensor_copy(out=IDXF[:P, :, :], in_=IDXU[:P, :, :]), reads=[IDXU.b], writes=[IDXF.b])
        tv = TOPS[:P, :, :].rearrange("p (h two) k -> p h two k", two=2)
        CAND = S0
        op("dve", lambda e: e.tensor_tensor(out=CAND[:P, :].rearrange("p (h a b) -> p h a b", h=8, a=16),
                                            in0=tv[:, :, 0, :].unsqueeze(3).broadcast_to([P, 8, 16, 16]),
                                            in1=tv[:, :, 1, :].unsqueeze(2).broadcast_to([P, 8, 16, 16]), op=ALU.add),
           reads=[TOPS.b], writes=[S0.b])
        for h in range(8):
            cd = CAND[:P, h * 256:(h + 1) * 256]
            wk = S2[:P, h * 256:(h + 1) * 256]
            op("dve", lambda e: e.max(out=CV[:P, h, 0:8], in_=cd), reads=[S0.b], writes=[CV.b])
            op("dve", lambda e: e.max_index(out=CPOS[:P, h, 0:8], in_max=CV[:P, h, 0:8], in_values=cd), reads=[S0.b, CV.b], writes=[CPOS.b])
            op("dve", lambda e: e.match_replace(out=wk, in_to_replace=CV[:P, h, 0:8], in_values=cd, imm_value=NEG), reads=[S0.b, CV.b], writes=[S2.b])
            op("dve", lambda e: e.max(out=CV[:P, h, 8:16], in_=wk), reads=[S2.b], writes=[CV.b])
            op("dve", lambda e: e.max_index(out=CPOS[:P, h, 8:16], in_max=CV[:P, h, 8:16], in_values=wk), reads=[S2.b, CV.b], writes=[CPOS.b])
        op("dve", lambda e: e.tensor_single_scalar(out=PAU[:P, :, :], in_=CPOS[:P, :, :], scalar=4, op=ALU.logical_shift_right), reads=[CPOS.b], writes=[PAU.b])
        op("dve", lambda e: e.tensor_single_scalar(out=PBU[:P, :, :], in_=CPOS[:P, :, :], scalar=15, op=ALU.bitwise_and), reads=[CPOS.b], writes=[PBU.b])
        op("dve", lambda e: e.tensor_copy(out=PAF[:P, :, :], in_=PAU[:P, :, :]), reads=[PAU.b], writes=[PAF.b])
        op("dve", lambda e: e.tensor_copy(out=PBF[:P, :, :], in_=PBU[:P, :, :]), reads=[PBU.b], writes=[PBF.b])
        iv = IDXF[:P, :, :].rearrange("p (h two) k -> p h two k", two=2)
        io16 = C("iota16", P).unsqueeze(1).unsqueeze(1).broadcast_to([P, 8, 16, 16])
        for (pf, half, dst) in [(PAF, 0, I1), (PBF, 1, I2)]:
            eq = S1_[:P, :].rearrange("p (h k a) -> p h k a", h=8, k=16)
            op("dve", lambda e: e.tensor_tensor(out=eq, in0=pf[:P, :, :].unsqueeze(3).broadcast_to([P, 8, 16, 16]), in1=io16, op=ALU.is_equal),
               reads=[pf.b, CST.b], writes=[S1_.b])
            op("dve", lambda e: e.tensor_tensor(out=eq, in0=eq, in1=iv[:, :, half, :].unsqueeze(2).broadcast_to([P, 8, 16, 16]), op=ALU.mult),
               reads=[S1_.b, IDXF.b], writes=[S1_.b])
            op("dve", lambda e: e.tensor_reduce(out=dst[:P, :].rearrange("p (h k) -> p h k", h=8), in_=eq, axis=AX.X, op=ALU.add),
               reads=[S1_.b], writes=[dst.b])
        op("dve", lambda e: e.scalar_tensor_tensor(out=I1[:P, :], in0=I1[:P, :], scalar=128.0, in1=I2[:P, :], op0=ALU.mult, op1=ALU.add),
           reads=[I1.b, I2.b], writes=[I1.b])
        op("dve", lambda e: e.tensor_copy(out=IDX[:P, :], in_=I1[:P, :]), reads=[I1.b], writes=[IDX.b])
        gv = GATE[:P, :].rearrange("p (h k) -> p h k", h=8)
        op("dve", lambda e: e.tensor_tensor(out=gv, in0=CV[:P, :, :], in1=CV[:P, :, 0:1].broadcast_to([P, 8, 16]), op=ALU.subtract),
           reads=[CV.b], writes=[GATE.b])
        op("act", lambda e: e.activation(out=GATE[:P, :], in_=GATE[:P, :], func=AF.Exp), reads=[GATE.b], writes=[GATE.b])
        op("dve", lambda e: e.tensor_reduce(out=SMf[:P, 0:8], in_=gv, axis=AX.X, op=ALU.add), reads=[GATE.b], writes=[SMf.b])
        op("dve", lambda e: e.reciprocal(out=SMf[:P, 8:16], in_=SMf[:P, 0:8]), reads=[SMf.b], writes=[SMf.b])
        op("dve", lambda e: e.tensor_tensor(out=gv, in0=gv, in1=SMf[:P, 8:16].unsqueeze(2).broadcast_to([P, 8, 16]), op=ALU.mult),
           reads=[GATE.b, SMf.b], writes=[GATE.b])
        return items

    def run_items(items, n=None):
        n = len(items) if n is None else min(n, len(items))
        for _ in range(n):
            kind, e, fn, owner, reads, writes = items.pop(0)
            if kind == "op":
                kb.op(e, fn, reads=reads, writes=writes, bound=True)
            else:
                kb.dma(e, fn, owner, reads=reads, writes=writes, bound=True)

    def back(t, sset, nxt):
        op = kb.op
        is_s = (t == NT)
        P = SP if is_s else 128
        ada = ADAs if is_s else ADA
        H = Hs[sset]; IDX = IDXs[sset]; GATE = GATEs[sset]
        per = 0 if not nxt else (len(nxt) + 119) // 120

        def axpy(s):
            b = s % NB
            op("dve", lambda e: e.tensor_tensor(out=COEF2[:P, s:s + 1], in0=COEF[:P, s:s + 1], in1=GATE[:P, s:s + 1], op=ALU.mult),
               reads=[COEFB[b], GATE.b], writes=[COEF2.b])
            if s == 0:
                op("dve", lambda e: e.tensor_scalar(out=ACC[:P, :], in0=UB[b][:P, D:2 * D], scalar1=COEF2[:P, s:s + 1], scalar2=None, op0=ALU.mult),
                   reads=[UB[b].b, COEF2.b], writes=[ACC.b])
            else:
                op("dve", lambda e: e.scalar_tensor_tensor(out=ACC[:P, :], in0=UB[b][:P, D:2 * D], scalar=COEF2[:P, s:s + 1], in1=ACC[:P, :],
                                                           op0=ALU.mult, op1=ALU.add), reads=[UB[b].b, COEF2.b, ACC.b], writes=[ACC.b])

        for s_ in range(128):
            b = s_ % NB
            kb.dma("pool", lambda q: q.indirect_dma_start(out=UB[b][:P, :], out_offset=None, in_=tab,
                                                          in_offset=bass.IndirectOffsetOnAxis(ap=IDX[:P, s_:s_ + 1], axis=0)),
                   UB[b].b, reads=[IDX.b], writes=[UB[b].b])
            op("dve", lambda e: e.scalar_tensor_tensor(out=JUNK[:P, :, :].rearrange("p a b -> p (a b)"), in0=UB[b][:P, 0:D], scalar=1.0, in1=H[:P, :],
                                                       op0=ALU.mult, op1=ALU.mult, accum_out=ACTV[:P, s_:s_ + 1]),
               reads=[UB[b].b, H.b], writes=[JUNK.b, ACTB[b]])
            op("act", lambda e: e.activation(out=COEF[:P, s_:s_ + 1], in_=ACTV[:P, s_:s_ + 1], func=AF.Gelu), reads=[ACTB[b]], writes=[COEFB[b]])
            if s_ >= 1:
                axpy(s_ - 1)
            if nxt:
                run_items(nxt, per)
        axpy(127)
        if nxt:
            run_items(nxt)
        op("dve", lambda e: e.tensor_tensor(out=ACC[:P, :], in0=ACC[:P, :], in1=ada[:P, 2048:3072], op=ALU.mult), reads=[ACC.b, ada.b], writes=[ACC.b])
        _resid_ln(kb, X, XB[t], t, P, ACC, SM, ST6, PRM, 0, D, E["EPSB"])

    _ada(nc, kb, l, E, ADA, 128, E["cp"], 3072, WCA, PM, Hs[0], HT, PT, badac)
    run_items(make_front(0, 0))
    for t in range(NT + 1):
        nxt = make_front(t + 1, (t + 1) % 2) if t + 1 < NT else None
        back(t, t % 2, nxt)
        if t + 1 == NT:
            _ada(nc, kb, l, E, ADAs, SP, E["cs"], 3072, WCA, PM, Hs[NT % 2], HT, PT, badac)
            run_items(make_front(NT, NT % 2))


_CACHE = {}


def _chunked(w, cw):
    L, K, n = w.shape
    nch = (n + cw - 1) // cw
    wp = np.zeros((L, K, nch * cw), np.float32)
    wp[:, :, :n] = w
    wp = wp.reshape(L, 8, 128, nch, cw).transpose(0, 3, 2, 1, 4)
    return np.ascontiguousarray(wp.reshape(L, nch, 128, 8 * cw))


def _rep(a, P=128):
    return np.ascontiguousarray(np.broadcast_to(a[:, None, :], (a.shape[0], P, a.shape[1])))


def make_in_maps(inp, cpack):
    f = lambda a: np.ascontiguousarray(np.asarray(a, dtype=np.float32))
    shared = {
        "w_ada": _chunked(f(inp["w_ada"]), WCW), "b_ada": _rep(f(inp["b_ada"])), "w_in": _chunked(f(inp["w_in"]), WCW),
        "b_gate": _rep(f(inp["b_gate"])),
        "mh_g": _rep(f(inp["mh_g"])), "sgu_g": _rep(f(inp["sgu_g"])), "sgu_b": _rep(f(inp["sgu_b"])),
        "pscale": _rep(f(inp["pool_scale"])),
        "w_sT": f(np.asarray(inp["w_s"]).transpose(0, 3, 1, 2)),
        "b_sT": f(np.asarray(inp["b_s"]).transpose(0, 2, 1)),
        "w_pool": f(np.asarray(inp["w_pool"]).transpose(0, 2, 1, 3)),
        "w_o": _chunked(f(inp["w_o"]), WCW), "ln1g": _rep(f(inp["ln1_g"])), "ln1b": _rep(f(inp["ln1_b"])),
        "ln2g": _rep(f(inp["ln2_g"])), "ln2b": _rep(f(inp["ln2_b"])), "w_pq": _chunked(f(inp["w_pq"]), 128),
        "keysT": f(np.asarray(inp["peer_keys"]).transpose(0, 4, 1, 2, 3).reshape(DEPTH, 128, 16, 128)),
        "cst": cpack[0], "cst1": cpack[1],
    }
    ws4 = np.asarray(inp["w_s"])[:, :, :ST, :ST]
    wsS = np.repeat(np.repeat(ws4.transpose(0, 3, 1, 2), SB, axis=1), SB, axis=3)
    shared["w_sS"] = f(wsS)
    bs4 = np.asarray(inp["b_s"])[:, :, :ST]
    shared["b_sS"] = f(np.repeat(bs4.transpose(0, 2, 1), SB, axis=1))
    for l in range(DEPTH):
        shared["puv%d" % l] = np.ascontiguousarray(
            np.concatenate([np.asarray(inp["peer_u"])[l], np.asarray(inp["peer_v"])[l]], axis=1), dtype=np.float32)
    maps = []
    for c in range(NCORES):
        bs = slice(c * SB, (c + 1) * SB)
        m = dict(shared)
        m["xp"] = f(np.asarray(inp["x_prompt"])[c])
        m["xs"] = f(np.asarray(inp["x_sample"])[bs].transpose(1, 0, 2).reshape(SP, D))
        m["cp"] = f(np.broadcast_to(np.asarray(inp["c_prompt"])[c][None, :], (128, D)))
        m["cs"] = f(np.tile(np.asarray(inp["c_sample"])[bs], (ST, 1)))
        sCc = np.asarray(inp["state_mlstm_C"])[:, bs]
        m["sC"] = f(sCc.transpose(0, 2, 3, 1, 4))
        snc = np.asarray(inp["state_mlstm_n"])[:, bs]
        m["snat"] = f(snc)
        m["snT"] = f(snc.transpose(0, 2, 3, 1))
        m["sm"] = f(np.tile(np.asarray(inp["state_mlstm_m"])[:, bs], (1, ST, 1)))
        spc = np.asarray(inp["state_pool"])[:, bs].transpose(0, 2, 1, 3)
        m["spA"] = f(spc[:, 0:8].reshape(DEPTH, 128, 256))
        m["spB"] = f(spc[:, 8:15].reshape(DEPTH, 112, 256))
        maps.append(m)
    return maps


def gather_outputs(results):
    cat = lambda k, ax: np.concatenate([r[k] for r in results], axis=ax)
    yp = np.stack([r["yp"] for r in results], 0)
    ys = np.concatenate([r["ys"].reshape(ST, SB, D).transpose(1, 0, 2) for r in results], 0)
    pC = np.stack([r["pC"] for r in results], 1)
    pn = np.stack([r["pn"] for r in results], 1)
    pm = np.stack([r["pm"] for r in results], 1)
    pp = np.stack([r["pp"] for r in results], 1)
    return (yp, ys, pC, pn, pm, pp, cat("nC", 1), cat("nn", 1), cat("nm", 1), cat("npool", 1), cat("nv", 1))


def kernel(**inputs):
    if "prog" not in _CACHE:
        _CACHE["prog"] = build_program()
    nc, cpack = _CACHE["prog"]
    maps = make_in_maps(inputs, cpack)
    res = run_bass_kernel_spmd(nc, maps, core_ids=list(range(NCORES)))
    outs = gather_outputs(res.results)
    return tuple(np.ascontiguousarray(o, dtype=np.float32) for o in outs)
```

```python
import numpy as np
from contextlib import ExitStack
import concourse.bass as bass
import concourse.mybir as mybir
from concourse.bass_utils import run_bass_kernel_spmd

F32 = mybir.dt.float32
I32 = mybir.dt.int32
U32 = mybir.dt.uint32
F32R = mybir.dt.float32r
ALU = mybir.AluOpType
AF = mybir.ActivationFunctionType
AX = mybir.AxisListType

NCORES = 8
D = 1024
SEQ = 2048
NT = 16
SB = 16
ST = 4
SP = SB * ST
DEPTH = 2
ALPHA = (2 * DEPTH) ** 0.25
LN_EPS = 1e-5
IN_COLS = 2824
NEG = -1.0e30
WCW = 192
NEXP = 16384
SAME_ENGINE_WAITS = True
NBUF = 6


class TB:
    def __init__(self, name, sem=None):
        self.name = name
        self.last_w = None
        self.reads = []
        self.sem = sem
        self.dma_total = 0
        self.dma_dirty = False


class KB:
    ENG = ("pe", "act", "dve", "pool", "sp")

    def __init__(self, nc, stack):
        self.nc = nc
        self.stack = stack
        self.q = {e: [] for e in self.ENG}
        self.cnt = {e: 0 for e in self.ENG}
        self.esem = {e: stack.enter_context(nc.semaphore("es_" + e)) for e in self.ENG}
        self.seen = {e: {} for e in self.ENG}
        self.semobj = {}
        self._sem_owner = {}
        self.stack0 = stack
        self.phase_tbs = []
        self.sfx = ""

    def new_sem(self, name):
        return self.stack.enter_context(self.nc.semaphore(name + self.sfx))

    def buf(self, name, dma=False):
        tb = TB(name, self.new_sem("d_" + name) if dma else None)
        if dma and self.stack is not self.stack0:
            self.phase_tbs.append(tb)
        return tb

    def end_phase(self):
        for tb in self.phase_tbs:
            k = id(tb.sem)
            self._sem_owner.pop(k, None)
            self.semobj.pop(k, None)
            for e in self.ENG:
                self.seen[e].pop(k, None)
        self.phase_tbs = []

    def sb(self, name, shape, dt=F32):
        return self.stack.enter_context(self.nc.sbuf_tensor(name + self.sfx, list(shape), dt))

    def ps(self, name, shape, dt=F32):
        return self.stack.enter_context(self.nc.psum_tensor(name + self.sfx, list(shape), dt))

    def _deps(self, e, reads, writes):
        deps = {}

        def add(tok):
            if tok is None:
                return
            s, v = tok
            k = id(s)
            self.semobj[k] = s
            ow = self._sem_owner.get(k)
            if ow is not None:
                v = ow.dma_total
            if v > deps.get(k, 0):
                deps[k] = v
        for b in reads:
            add(b.last_w)
        for b in writes:
            add(b.last_w)
            for r in b.reads:
                add(r)
        out = []
        own = id(self.esem[e])
        for k, v in deps.items():
            if k == own and (e in ("pe", "sp") or not SAME_ENGINE_WAITS):
                continue
            if self.seen[e].get(k, 0) >= v:
                continue
            self.seen[e][k] = v
            out.append((self.semobj[k], v))
        return out

    def op(self, e, fn, reads=(), writes=(), bound=False):
        waits = self._deps(e, reads, writes)
        for s, v in waits:
            tb = self._sem_owner.get(id(s))
            if tb is not None:
                tb.dma_dirty = True
        self.cnt[e] += 1
        tok = (self.esem[e], self.cnt[e])
        self.q[e].append((waits, fn if bound else _bind(fn), tok[0], 1))
        for b in reads:
            b.reads.append(tok)
        for b in writes:
            b.last_w = tok
            b.reads = []
        return tok

    def dma(self, e, fn, owner, reads=(), writes=(), bound=False):
        self._sem_owner[id(owner.sem)] = owner
        waits = self._deps(e, reads, writes)
        if owner.dma_dirty and owner.dma_total > 0:
            k = id(owner.sem)
            if self.seen[e].get(k, 0) < owner.dma_total:
                self.seen[e][k] = owner.dma_total
                waits.append((owner.sem, owner.dma_total))
            owner.dma_dirty = False
        for s, v in waits:
            tb = self._sem_owner.get(id(s))
            if tb is not None and tb is not owner:
                tb.dma_dirty = True
        owner.dma_total += 16
        tok = (owner.sem, owner.dma_total)
        self.q[e].append((waits, fn if bound else _bind(fn), owner.sem, 16))
        for b in reads:
            b.reads.append(tok)
        for b in writes:
            b.last_w = tok
            b.reads = []
        return tok

    def barrier(self, extra=()):
        toks = [(self.esem[e], self.cnt[e]) for e in self.ENG if self.cnt[e] > 0 and e != "sp"]
        for tb in list(self._sem_owner.values()) + list(extra):
            if tb.dma_total > 0:
                toks.append((tb.sem, tb.dma_total))
        for e in self.ENG:
            waits = []
            for s, v in toks:
                k = id(s)
                if k == id(self.esem[e]):
                    continue
                if self.seen[e].get(k, 0) >= v:
                    continue
                self.seen[e][k] = v
                waits.append((s, v))
            if waits:
                self.q[e].append((waits, None, None, 0))

    def emit(self, final_waits=()):
        nc = self.nc
        engs = {"pe": "tensor", "act": "scalar", "dve": "vector", "pool": "gpsimd", "sp": "sync"}
        with nc.Block() as block:
            for e in self.ENG:
                items = self.q[e]
                fw = list(final_waits) if e == "sp" else []

                def body(eng, items=items, fw=fw):
                    for waits, fn, sem, inc in items:
                        for s, v in waits:
                            eng.wait_ge(s, v)
                        if fn is not None:
                            fn(eng).then_inc(sem, inc)
                    for s, v in fw:
                        eng.wait_ge(s, v)
                getattr(block, engs[e])(body)
        self.q = {e: [] for e in self.ENG}


class _Rec:
    def __init__(self):
        self.call = None

    def __getattr__(self, name):
        def f(*a, **k):
            self.call = (name, a, k)
            return self
        return f


def _bind(fn):
    r = _Rec()
    fn(r)
    assert r.call is not None
    name, a, k = r.call
    return lambda eng: getattr(eng, name)(*a, **k)


class Tn:
    def __init__(self, kb, name, shape, dt=F32, psum=False, dma=False):
        self.t = kb.ps(name, shape, dt) if psum else kb.sb(name, shape, dt)
        self.b = kb.buf(name, dma=dma)

    def __getitem__(self, k):
        return self.t[k]


def _consts():
    c = {}
    i128 = np.arange(128)
    c["ident"] = np.eye(128, dtype=np.float32)
    c["ones"] = np.ones((128, 128), np.float32)
    c["triu"] = (i128[:, None] <= i128[None, :]).astype(np.float32)
    c["negm"] = np.where(i128[None, :] <= i128[:, None], 0.0, NEG).astype(np.float32)
    sel = np.zeros((128, 128), np.float32); sel[127, :] = 1.0
    c["sel127"] = sel
    p = np.arange(SP); tt = p // SB; bb = p % SB
    sameb = bb[:, None] == bb[None, :]
    tri_s = (sameb & (tt[:, None] <= tt[None, :])).astype(np.float32)
    c["tri_s"] = _pad(tri_s)
    c["negm_s"] = _pad(np.where(sameb & (tt[None, :] <= tt[:, None]), 0.0, NEG).astype(np.float32))
    c["negb_s"] = _pad(np.where(sameb, 0.0, NEG).astype(np.float32))
    c["selend"] = _pad(((tt[:, None] == ST - 1) & sameb).astype(np.float32))
    oh = (bb[:, None] == np.arange(SB)[None, :]).astype(np.float32)
    c["onehotB"] = _pad(oh, cols=16)
    oh0 = ((p[:, None] == np.arange(SB)[None, :])).astype(np.float32)
    c["onehot0"] = _pad(oh0, cols=16)
    c["iota16"] = np.broadcast_to(np.arange(16, dtype=np.float32), (128, 16)).copy()
    wins = (2, 4, 8, 16)
    bc0 = np.zeros((4, 128, 128), np.float32); bc = np.zeros((4, 128, 128), np.float32)
    bp = np.zeros((4, 128, 128), np.float32)
    for g, w in enumerate(wins):
        for t in range(128):
            for j in range(w):
                s = t - j
                if s >= 0:
                    bc[g, s, t] += 1.0 / w
                    bc0[g, s, t] += 1.0 / min(t + 1, w)
                else:
                    bp[g, s + 128, t] += 1.0 / w
            bc[g, t, t] -= 1.0
            bc0[g, t, t] -= 1.0
    c["bandc0"] = bc0.transpose(1, 0, 2).reshape(128, 512)
    c["bandc"] = bc.transpose(1, 0, 2).reshape(128, 512)
    c["bandp"] = bp.transpose(1, 0, 2).reshape(128, 512)
    bsA = np.zeros((4, 128, SP), np.float32); bsB = np.zeros((4, 128, SP), np.float32)
    bsC = np.zeros((4, 128, SP), np.float32)
    for g, w in enumerate(wins):
        for t in range(ST):
            for b in range(SB):
                col = t * SB + b
                for j in range(w):
                    r = 15 + t - j
                    if r >= 15:
                        bsC[g, (r - 15) * SB + b, col] += 1.0 / w
                    elif r >= 8:
                        bsB[g, (r - 8) * SB + b, col] += 1.0 / w
                    else:
                        bsA[g, r * SB + b, col] += 1.0 / w
                bsC[g, t * SB + b, col] -= 1.0
    c["bsA"] = bsA.transpose(1, 0, 2).reshape(128, 4 * SP)
    c["bsB"] = bsB.transpose(1, 0, 2).reshape(128, 4 * SP)
    c["bsC"] = bsC.transpose(1, 0, 2).reshape(128, 4 * SP)
    return c


def _pad(a, cols=None):
    out = np.zeros((128, a.shape[1] if cols is None else cols), np.float32)
    out[: a.shape[0], : a.shape[1]] = a
    return out


_CONST_G = ["ident", "ones", "iota16"]
_CONST_1 = ["triu", "negm", "sel127", "tri_s", "negm_s", "negb_s", "selend",
            "onehotB", "onehot0", "bandc0", "bandc", "bandp", "bsA", "bsB", "bsC"]


def _const_pack():
    c = _consts()
    packs = []
    for order in (_CONST_G, _CONST_1):
        offs = {}
        o = 0
        arrs = []
        for k in order:
            offs[k] = (o, c[k].shape[1])
            o += c[k].shape[1]
            arrs.append(c[k])
        packs.append((np.ascontiguousarray(np.concatenate(arrs, axis=1)), offs))
    return packs


def build_program(n_layers=DEPTH, do_phase2=True):
    (cpack, coff), (cpack1, coff1) = _const_pack()
    NCST = cpack.shape[1]
    NCST1 = cpack1.shape[1]
    nc = bass.Bass("TRN2", target_bir_lowering=False)

    def din(name, shape, dt=F32):
        return nc.dram_tensor(name, list(shape), dt, kind="ExternalInput").ap()

    def dout(name, shape, dt=F32):
        return nc.dram_tensor(name, list(shape), dt, kind="ExternalOutput").ap()

    xp = din("xp", [SEQ, D]); xs = din("xs", [SP, D])
    cp = din("cp", [128, D]); cs = din("cs", [SP, D])
    sC = din("sC", [DEPTH, 4, 128, SB, 128]); snat = din("snat", [DEPTH, SB, 4, 128])
    snT = din("snT", [DEPTH, 4, 128, SB]); sm = din("sm", [DEPTH, SP, 4])
    spA = din("spA", [DEPTH, 128, 256]); spB = din("spB", [DEPTH, 112, 256])
    w_ada = din("w_ada", [DEPTH, (6 * D) // WCW, 128, 8 * WCW]); b_ada = din("b_ada", [DEPTH, 128, 6 * D])
    w_in = din("w_in", [DEPTH, (IN_COLS + WCW - 1) // WCW, 128, 8 * WCW]); b_gate = din("b_gate", [DEPTH, 128, 8])
    mh_g = din("mh_g", [DEPTH, 128, 512]); sgu_g = din("sgu_g", [DEPTH, 128, 256])
    sgu_b = din("sgu_b", [DEPTH, 128, 256]); pscale = din("pscale", [DEPTH, 128, 256])
    w_sT = din("w_sT", [DEPTH, 128, 4, 128]); b_sT = din("b_sT", [DEPTH, 128, 4])
    w_sS = din("w_sS", [DEPTH, SP, 4, SP]); b_sS = din("b_sS", [DEPTH, SP, 4])
    w_pool = din("w_pool", [DEPTH, 64, 4, 64]); w_o = din("w_o", [DEPTH, (D + WCW - 1) // WCW, 128, 8 * WCW])
    ln1g = din("ln1g", [DEPTH, 128, D]); ln1b = din("ln1b", [DEPTH, 128, D])
    ln2g = din("ln2g", [DEPTH, 128, D]); ln2b = din("ln2b", [DEPTH, 128, D])
    w_pq = din("w_pq", [DEPTH, 16, 128, 8 * 128]); keysT = din("keysT", [DEPTH, 128, 16, 128])
    puv = [din("puv%d" % l, [NEXP, 2 * D]) for l in range(DEPTH)]
    cst_d = din("cst", [128, NCST])
    cst1_d = din("cst1", [128, NCST1])

    yp = dout("yp", [SEQ, D]); ys = dout("ys", [SP, D])
    o_pC = dout("pC", [DEPTH, 4, 128, 128]); o_pn = dout("pn", [DEPTH, 4, 128]); o_pm = dout("pm", [DEPTH, 4])
    o_pp = dout("pp", [DEPTH, 15, 256])
    o_nC = dout("nC", [DEPTH, SB, 4, 128, 128]); o_nn = dout("nn", [DEPTH, SB, 4, 128])
    o_nm = dout("nm", [DEPTH, SB, 4]); o_np = dout("npool", [DEPTH, SB, 15, 256])
    o_nv = dout("nv", [DEPTH, SB, ST, 256])

    with ExitStack() as st0:
        kb = KB(nc, st0)
        op = kb.op
        OUT = kb.buf("outs", dma=True)

        def out_dma(dst, src, reads):
            kb.dma("sp", lambda q: q.dma_start(out=dst, in_=src), OUT, reads=reads)

        X = kb.sb("X", [128, NT + 1, D])
        XB = [kb.buf("X%d" % t) for t in range(NT + 1)]
        XL = kb.buf("xload", dma=True)
        CST = Tn(kb, "CST", [128, NCST], dma=True)
        EPSB = Tn(kb, "EPSB", [128, 1])
        kb.op("dve", lambda e: e.memset(EPSB[:, :], LN_EPS), writes=[EPSB.b])

        def C(name, P=128, w=None):
            if name in coff:
                o, n = coff[name]
                return CST[:P, o:o + (n if w is None else w)]
            o, n = coff1[name]
            return kb.cst1[:P, o:o + (n if w is None else w)]

        def Cg(name, g, P, blk, w):
            o, n = coff1[name]
            return kb.cst1[:P, o + g * blk: o + g * blk + w]

        with nc.allow_non_contiguous_dma(reason="small strided state/param loads"):
            kb.dma("sp", lambda q: q.dma_start(out=CST[:, :], in_=cst_d), CST.b, writes=[CST.b])
            for t in range(NT):
                kb.dma("sp", lambda q, t=t: q.dma_start(out=X[:, t, :], in_=xp[t * 128:(t + 1) * 128, :]),
                       XL, writes=[XB[t]])
            kb.dma("sp", lambda q: q.dma_start(out=X[:SP, NT, :], in_=xs), XL, writes=[XB[NT]])

            for l in range(n_layers):
                with ExitStack() as st1:
                    kb.stack = st1
                    kb.sfx = "_a%d" % l
                    _phase1(nc, kb, l, locals())
                    kb.barrier(extra=[OUT])
                    kb.emit()
                    kb.end_phase()
                if do_phase2:
                    with ExitStack() as st2:
                        kb.stack = st2
                        kb.sfx = "_b%d" % l
                        _phase2(nc, kb, l, locals())
                        kb.barrier(extra=[OUT])
                        kb.emit()
                        kb.end_phase()
            kb.stack = st0
            kb.sfx = ""
            for t in range(NT):
                out_dma(yp[t * 128:(t + 1) * 128, :], X[:, t, :], [XB[t]])
            out_dma(ys, X[:SP, NT, :], [XB[NT]])
            kb.emit(final_waits=[(OUT.sem, OUT.dma_total)])
    return nc, (cpack, cpack1)


def _ada(nc, kb, l, E, ADA, P, csrc, off, WCH, PM, hbuf, hT, PT, badac, WCR=None):
    op = kb.op
    C = E["C"]
    w_ada, b_ada = E["w_ada"], E["b_ada"]
    kb.dma("sp", lambda q: q.dma_start(out=hbuf[:P, :], in_=csrc), hbuf.b, writes=[hbuf.b])
    op("act", lambda e: e.activation(out=hbuf[:P, :], in_=hbuf[:P, :], func=AF.Silu), reads=[hbuf.b], writes=[hbuf.b])
    _transpose8(kb, E, hbuf, hT, PT, P)
    r32 = (hT.t.dtype == F32R)
    for c in range(3072 // WCW):
        i = c % 2
        c0 = off + c * WCW
        wch = WCH[c % len(WCH)]
        kb.dma("sp", lambda q: q.dma_start(out=wch[:, :, 0:WCW], in_=w_ada[l, c0 // WCW].rearrange("p (k j) -> p k j", k=8)), wch.b, writes=[wch.b])
        kb.dma("sp", lambda q: q.dma_start(out=badac[i][:P, 0:WCW], in_=b_ada[l, :P, c0:c0 + WCW]), badac[i].b, writes=[badac[i].b])
        wsrc = WCR[c % 2] if r32 else wch
        if r32:
            op("act", lambda e: e.copy(out=wsrc[:, :, 0:WCW], in_=wch[:, :, 0:WCW]), reads=[wch.b], writes=[wsrc.b])
        for k in range(8):
            if r32:
                op("pe", lambda e: e.matmul(PM[i][:, 0:WCW], lhsT=hT[:, k, :], rhs=wsrc[:, k, 0:WCW], start=(k == 0), stop=(k == 7)),
                   reads=[hT.b, wsrc.b], writes=[PM[i].b] if k in (0, 7) else [])
            else:
                op("pe", lambda e: e.matmul(PM[i][:P, 0:WCW], lhsT=hT[:, k, :P], rhs=wch[:, k, 0:WCW], start=(k == 0), stop=(k == 7)),
                   reads=[hT.b, wch.b], writes=[PM[i].b] if k in (0, 7) else [])
        op("dve", lambda e: e.tensor_tensor(out=ADA[:P, c * WCW:(c + 1) * WCW], in0=PM[i][:P, 0:WCW], in1=badac[i][:P, 0:WCW], op=ALU.add),
           reads=[PM[i].b, badac[i].b], writes=[ADA.b])
    op("dve", lambda e: e.tensor_scalar_add(out=ADA[:P, 1024:2048], in0=ADA[:P, 1024:2048], scalar1=1.0), reads=[ADA.b], writes=[ADA.b])


def _transpose8(kb, E, src, dstT, PT, P, srcb=None, op=None):
    op = kb.op if op is None else op
    C = E["C"]
    sb_ = src.b if srcb is None else srcb
    for half in range(2):
        for j in range(4):
            k = half * 4 + j
            op("pe", lambda e, half=half, j=j, k=k: e.transpose(
                out=PT[half][:, j * 128:j * 128 + P], in_=src[:P, k * 128:(k + 1) * 128], identity=C("ident", P, P)),
               reads=[sb_, E["CST"].b], writes=[PT[half].b])
        op("act", lambda e, half=half: e.copy(
            out=dstT[:, half * 4:half * 4 + 4, :P],
            in_=PT[half][:, :].rearrange("p (j c) -> p j c", j=4)[:, :, :P]),
           reads=[PT[half].b], writes=[dstT.b])


def _phase1(nc, kb, l, E):
    op = kb.op
    C, Cg, CST, X, XB = E["C"], E["Cg"], E["CST"], E["X"], E["XB"]
    EPSB = E["EPSB"]
    out_dma = E["out_dma"]
    w_in, w_o = E["w_in"], E["w_o"]

    kb.cst1 = kb.sb("CST1", [128, E["NCST1"]])
    kb.dma("sp", lambda q: q.dma_start(out=kb.cst1[:, :], in_=E["cst1_d"]), CST.b, writes=[CST.b])
    ADA = Tn(kb, "ADA1", [128, 3072]); ADAs = ADA
    WCH = [Tn(kb, "WCH%d" % i, [128, 8, WCW], dma=True) for i in range(2)]
    WCR = [Tn(kb, "WCR%d" % i, [128, 8, WCW], F32R) for i in range(2)]
    badac = [Tn(kb, "bada%d" % i, [128, 256], dma=True) for i in range(2)]
    H = Tn(kb, "H", [128, D], dma=True); HT = Tn(kb, "HT", [128, 8, 128], F32R)
    PROJ = Tn(kb, "PROJ", [128, IN_COLS], dma=True)
    Y = Tn(kb, "Y", [128, D])
    PRM = Tn(kb, "PRM", [128, 8 + 512 + 256 * 3 + 2 * D], dma=True)
    WS = Tn(kb, "WS", [128, 4, 128], dma=True); BS = Tn(kb, "BS", [128, 4], dma=True)
    WSs = Tn(kb, "WSs", [128, 4, SP], dma=True); BSs = Tn(kb, "BSs", [128, 4], dma=True)
    WP = Tn(kb, "WP", [64, 4, 64], dma=True)
    PT = [Tn(kb, "PT%d" % i, [128, 512], psum=True) for i in range(2)]
    PM = [Tn(kb, "PM%d" % i, [128, 512], psum=True) for i in range(2)]
    PA = Tn(kb, "PA", [128, 512], psum=True); PB = Tn(kb, "PB", [128, 512], psum=True)
    PC = Tn(kb, "PC", [128, 512], psum=True); PD = Tn(kb, "PD", [128, 512], psum=True)
    SM = Tn(kb, "SM", [128, 64])
    SMs = Tn(kb, "SMs", [128, 4], dma=True)
    MREP = Tn(kb, "MREP", [128, 4])
    CTX = Tn(kb, "CTX", [128, 4, 129], dma=True)
    DG = Tn(kb, "DG", [128, 128]); DL = Tn(kb, "DL", [128, 128]); WI = Tn(kb, "WI", [128, 128])
    AM = Tn(kb, "AM", [128, 128]); AT = Tn(kb, "AT", [128, 128])
    QT = Tn(kb, "QT", [128, 128]); KT = Tn(kb, "KT", [128, 128])
    VX = Tn(kb, "VX", [128, 129]); TOT = Tn(kb, "TOT", [128, 129]); WV = Tn(kb, "WV", [128, 129])
    HN = Tn(kb, "HN", [128, 128]); SG = Tn(kb, "SG", [128, 128]); ST6 = Tn(kb, "ST6", [128, 2, 6])
    OUTC = Tn(kb, "OUTC", [128, 128], dma=True)
    CN = Tn(kb, "CN", [128, SB, 128], dma=True); CTS = Tn(kb, "CTS", [128, SB, 129])
    RA = Tn(kb, "RA", [128, SB, 128])

    class _V2:
        def __init__(self, ap, b):
            self.t = ap
            self.b = b

        def __getitem__(self, k):
            return self.t[k]
    ZQ = _V2(RA[:, :, :].rearrange("p a b -> p (a b)")[:, 0:SB * SP], RA.b)
    NNAT = Tn(kb, "NNAT", [SB, 4, 128], dma=True); NTH = Tn(kb, "NTH", [128, SB], dma=True)
    WCB = Tn(kb, "WCB", [128, 16]); DECD = Tn(kb, "DECD", [128, 16]); DECR = Tn(kb, "DECR", [128, 16])
    MSO = Tn(kb, "MSO", [SB, 4], dma=True)
    PREV = Tn(kb, "PREV", [128, 256]); PTT = Tn(kb, "PTT", [64, 4, 128])
    SPA = Tn(kb, "SPA", [128, 256], dma=True); SPB = Tn(kb, "SPB", [128, 256], dma=True)
    VN = Tn(kb, "VN", [128, 256], dma=True); VTMP = Tn(kb, "VTMP", [128, 256])

    o_bg, o_mh, o_sg, o_sb, o_ps, o_l1g, o_l1b = 0, 8, 520, 776, 1032, 1288, 1288 + D
    for (o, w, src) in [(o_bg, 8, E["b_gate"]), (o_mh, 512, E["mh_g"]), (o_sg, 256, E["sgu_g"]), (o_sb, 256, E["sgu_b"]),
                        (o_ps, 256, E["pscale"]), (o_l1g, D, E["ln1g"]), (o_l1b, D, E["ln1b"])]:
        kb.dma("sp", lambda q, o=o, w=w, src=src: q.dma_start(out=PRM[:, o:o + w], in_=src[l]), PRM.b, writes=[PRM.b])
    kb.dma("sp", lambda q: q.dma_start(out=WS[:, :, :], in_=E["w_sT"][l]), WS.b, writes=[WS.b])
    kb.dma("sp", lambda q: q.dma_start(out=BS[:, :], in_=E["b_sT"][l]), BS.b, writes=[BS.b])
    kb.dma("sp", lambda q: q.dma_start(out=WSs[:SP, :, :], in_=E["w_sS"][l]), WSs.b, writes=[WSs.b])
    kb.dma("sp", lambda q: q.dma_start(out=BSs[:SP, :], in_=E["b_sS"][l]), BSs.b, writes=[BSs.b])
    kb.dma("sp", lambda q: q.dma_start(out=WP[:, :, :], in_=E["w_pool"][l]), WP.b, writes=[WP.b])
    for g in range(4):
        op("dve", lambda e, g=g: e.tensor_tensor(out=WS[:, g, :], in0=WS[:, g, :], in1=C("triu"), op=ALU.mult),
           reads=[WS.b, CST.b], writes=[WS.b])
        op("dve", lambda e, g=g: e.tensor_tensor(out=WSs[:SP, g, :], in0=WSs[:SP, g, :], in1=C("tri_s", SP, SP), op=ALU.mult),
           reads=[WSs.b, CST.b], writes=[WSs.b])
    op("dve", lambda e: e.memset(CTX[:, :, :], 0.0), writes=[CTX.b])
    op("dve", lambda e: e.memset(MREP[:, :], 0.0), writes=[MREP.b])
    op("dve", lambda e: e.memset(VX[:, :], 1.0), writes=[VX.b])

    _ada(nc, kb, l, E, ADA, 128, E["cp"], 0, WCH, PM, H, HT, PT, badac, WCR)

    w_in_v = w_in[l]
    w_o_v = w_o[l]
    def mk_chunks(n):
        return [(c0, min(WCW, n - c0)) for c0 in range(0, n, WCW)]
    chunks = mk_chunks(IN_COLS)
    wctr = [0]

    def stream_mm(wview, c0, w, lhsT, P, evac):
        i = wctr[0] % 2
        wctr[0] += 1
        kb.dma("sp", lambda q: q.dma_start(out=WCH[i][:, :, :], in_=wview[c0 // WCW].rearrange("p (k j) -> p k j", k=8)), WCH[i].b, writes=[WCH[i].b])
        wr = WCR[i]
        if wctr[0] % 3 == 0:
            op("dve", lambda e: e.tensor_copy(out=wr[:, :, 0:w], in_=WCH[i][:, :, 0:w]), reads=[WCH[i].b], writes=[wr.b])
        else:
            op("act", lambda e: e.copy(out=wr[:, :, 0:w], in_=WCH[i][:, :, 0:w]), reads=[WCH[i].b], writes=[wr.b])
        for k in range(8):
            op("pe", lambda e, k=k: e.matmul(PM[i][:, 0:w], lhsT=lhsT[:, k, :], rhs=wr[:, k, 0:w],
                                              start=(k == 0), stop=(k == 7)),
               reads=[lhsT.b, wr.b], writes=[PM[i].b] if k in (0, 7) else [])
        evac(PM[i], i)

    for t in range(NT + 1):
        is_s = (t == NT)
        P = SP if is_s else 128
        ada = ADAs if is_s else ADA
        if is_s:
            _ada(nc, kb, l, E, ADAs, SP, E["cs"], 0, WCH, PM, H, HT, PT, badac, WCR)
        xt = X[:P, t, :]
        op("dve", lambda e: e.tensor_tensor(out=H[:P, :], in0=xt, in1=ada[:P, 1024:2048], op=ALU.mult),
           reads=[XB[t], ada.b], writes=[H.b])
        op("dve", lambda e: e.tensor_tensor(out=H[:P, :], in0=H[:P, :], in1=ada[:P, 0:1024], op=ALU.add),
           reads=[H.b, ada.b], writes=[H.b])
        _transpose8(kb, E, H, HT, PT, P)
        for (c0, w) in chunks:
            stream_mm(w_in_v, c0, w, HT, P,
                      lambda pm, i, c0=c0, w=w: op("act", lambda e: e.copy(out=PROJ[:P, c0:c0 + w], in_=pm[:P, 0:w]),
                                                   reads=[pm.b], writes=[PROJ.b]))
        tri = C("tri_s", SP, SP) if is_s else C("triu")
        negm = C("negm_s", SP, SP) if is_s else C("negm")
        selE = C("selend", SP, SP) if is_s else C("sel127")
        if is_s:
            kb.dma("sp", lambda q: q.dma_start(out=SMs[:SP, :], in_=E["sm"][l]), SMs.b, writes=[SMs.b])
            kb.dma("sp", lambda q: q.dma_start(out=NNAT[:, :, :], in_=E["snat"][l]), NNAT.b, writes=[NNAT.b])
        mtok = SMs if is_s else MREP
        op("dve", lambda e: e.tensor_tensor(out=SM[:P, 0:8], in0=PROJ[:P, 2048:2056], in1=PRM[:P, o_bg:o_bg + 8], op=ALU.add),
           reads=[PROJ.b, PRM.b], writes=[SM.b])
        op("dve", lambda e: e.scalar_tensor_tensor(out=SM[:P, 8:12], in0=SM[:P, 4:8], scalar=-1.0, in1=SM[:P, 4:8], op0=ALU.mult, op1=ALU.max),
           reads=[SM.b], writes=[SM.b])
        op("act", lambda e: e.activation(out=SM[:P, 12:16], in_=SM[:P, 8:12], func=AF.Exp, scale=-1.0), reads=[SM.b], writes=[SM.b])
        op("act", lambda e: e.activation(out=SM[:P, 12:16], in_=SM[:P, 12:16], func=AF.Ln, bias=1.0, scale=1.0),
           reads=[SM.b], writes=[SM.b])
        op("dve", lambda e: e.tensor_scalar_min(out=SM[:P, 16:20], in0=SM[:P, 4:8], scalar1=0.0), reads=[SM.b], writes=[SM.b])
        op("dve", lambda e: e.tensor_tensor(out=SM[:P, 16:20], in0=SM[:P, 16:20], in1=SM[:P, 12:16], op=ALU.subtract),
           reads=[SM.b], writes=[SM.b])
        op("pe", lambda e: e.matmul(PA[:P, 0:4], lhsT=tri, rhs=SM[:P, 16:20], start=True, stop=True),
           reads=[CST.b, SM.b], writes=[PA.b])
        op("act", lambda e: e.copy(out=SM[:P, 20:24], in_=PA[:P, 0:4]), reads=[PA.b], writes=[SM.b])
        op("dve", lambda e: e.tensor_tensor(out=SM[:P, 24:28], in0=SM[:P, 0:4], in1=SM[:P, 20:24], op=ALU.subtract),
           reads=[SM.b], writes=[SM.b])
        op("pe", lambda e: e.matmul(PA[:P, 8:12], lhsT=selE, rhs=SM[:P, 20:24], start=True, stop=True),
           reads=[CST.b, SM.b], writes=[PA.b])
        op("act", lambda e: e.copy(out=SM[:P, 28:32], in_=PA[:P, 8:12]), reads=[PA.b], writes=[SM.b])

        for hh in range(4):
            qs = PROJ[:P, hh * 128:(hh + 1) * 128]
            ks = PROJ[:P, 512 + hh * 128:512 + (hh + 1) * 128]
            vs = PROJ[:P, 1024 + hh * 128:1024 + (hh + 1) * 128]
            os_ = PROJ[:P, 1536 + hh * 128:1536 + (hh + 1) * 128]
            col = lambda c, hh=hh: SM[:P, c + hh:c + hh + 1]
            S1 = lambda c: SM[:P, c:c + 1]
            if is_s:
                kb.dma("sp", lambda q, hh=hh: q.dma_start(out=CN[:, :, :], in_=E["sC"][l, hh]), CN.b, writes=[CN.b])
                kb.dma("sp", lambda q, hh=hh: q.dma_start(out=NTH[:, :], in_=E["snT"][l, hh]), NTH.b, writes=[NTH.b])
                for j in range(4):
                    pt = PT[j % 2]
                    for jj in range(4):
                        b = j * 4 + jj
                        op("pe", lambda e, b=b, jj=jj, pt=pt: e.transpose(out=pt[:, jj * 128:(jj + 1) * 128], in_=CN[:, b, :],
                                                                       identity=C("ident")),
                           reads=[CN.b, CST.b], writes=[pt.b])
                    op("act", lambda e, j=j, pt=pt: e.copy(out=CTS[:, j * 4:(j + 1) * 4, 0:128],
                                                           in_=pt[:, :].rearrange("p (j c) -> p j c", j=4)),
                       reads=[pt.b], writes=[CTS.b])
                op("dve", lambda e: e.tensor_copy(out=CTS[:, :, 128:129], in_=NTH[:, :].unsqueeze(2)), reads=[NTH.b], writes=[CTS.b])
            op("dve", lambda e, hh=hh: e.tensor_scalar(out=DG[:P, :P], in0=C("ident", P, P), scalar1=col(24), scalar2=None,
                                                       op0=ALU.mult), reads=[SM.b, CST.b], writes=[DG.b])
            op("pe", lambda e: e.matmul(PB[:P, 0:P], lhsT=C("ones", P, P), rhs=DG[:P, :P], start=True, stop=True),
               reads=[DG.b, CST.b], writes=[PB.b])
            if is_s:
                op("dve", lambda e: e.tensor_tensor(out=DL[:P, :P], in0=PB[:P, 0:P], in1=C("negb_s", SP, SP), op=ALU.add),
                   reads=[PB.b, CST.b], writes=[DL.b])
                op("dve", lambda e: e.tensor_reduce(out=S1(32), in_=DL[:P, :P], axis=AX.X, op=ALU.max), reads=[DL.b], writes=[SM.b])
            else:
                op("dve", lambda e: e.tensor_reduce(out=S1(32), in_=PB[:P, 0:P], axis=AX.X, op=ALU.max), reads=[PB.b], writes=[SM.b])
            op("dve", lambda e, hh=hh: e.scalar_tensor_tensor(out=DL[:P, :P], in0=PB[:P, 0:P], scalar=col(20), in1=negm,
                                                              op0=ALU.add, op1=ALU.add),
               reads=[PB.b, SM.b, CST.b], writes=[DL.b])
            op("dve", lambda e: e.tensor_reduce(out=S1(33), in_=DL[:P, :P], axis=AX.X, op=ALU.max), reads=[DL.b], writes=[SM.b])
            op("dve", lambda e, hh=hh: e.tensor_tensor(out=S1(34), in0=col(20), in1=mtok[:P, hh:hh + 1], op=ALU.add),
               reads=[SM.b, mtok.b], writes=[SM.b])
            op("dve", lambda e: e.tensor_tensor(out=S1(35), in0=S1(34), in1=S1(33), op=ALU.max), reads=[SM.b], writes=[SM.b])
            op("dve", lambda e: e.tensor_scalar(out=S1(36), in0=S1(35), scalar1=-1.0, scalar2=None, op0=ALU.mult),
               reads=[SM.b], writes=[SM.b])
            op("act", lambda e: e.activation(out=WI[:P, :P], in_=DL[:P, :P], func=AF.Exp, bias=S1(36), scale=1.0),
               reads=[DL.b, SM.b], writes=[WI.b])
            op("act", lambda e: e.activation(out=S1(37), in_=S1(34), func=AF.Exp, bias=S1(36), scale=1.0), reads=[SM.b], writes=[SM.b])
            op("act", lambda e: e.activation(out=S1(38), in_=S1(36), func=AF.Exp), reads=[SM.b], writes=[SM.b])
            op("pe", lambda e: e.transpose(out=PC[:, 0:P], in_=qs, identity=C("ident", P, P)), reads=[PROJ.b, CST.b], writes=[PC.b])
            op("pe", lambda e: e.transpose(out=PC[:, 128:128 + P], in_=ks, identity=C("ident", P, P)), reads=[PROJ.b, CST.b], writes=[PC.b])
            op("act", lambda e: e.mul(out=QT[:, :P], in_=PC[:, 0:P], mul=128.0 ** -0.5), reads=[PC.b], writes=[QT.b])
            op("act", lambda e: e.copy(out=KT[:, :P], in_=PC[:, 128:128 + P]), reads=[PC.b], writes=[KT.b])
            op("pe", lambda e: e.matmul(PD[:P, 0:P], lhsT=QT[:, :P], rhs=KT[:, :P], start=True, stop=True),
               reads=[QT.b, KT.b], writes=[PD.b])
            op("dve", lambda e: e.tensor_tensor(out=AM[:P, :P], in0=WI[:P, :P], in1=PD[:P, 0:P], op=ALU.mult),
               reads=[WI.b, PD.b], writes=[AM.b])
            op("pe", lambda e: e.transpose(out=PB[:P, 128:128 + P], in_=AM[:P, :P], identity=C("ident", P, P)),
               reads=[AM.b, CST.b], writes=[PB.b])
            op("act", lambda e: e.copy(out=AT[:P, :P], in_=PB[:P, 128:128 + P]), reads=[PB.b], writes=[AT.b])
            op("pool", lambda e: e.tensor_copy(out=VX[:P, 0:128], in_=vs), reads=[PROJ.b], writes=[VX.b])
            op("pe", lambda e: e.matmul(PD[:P, 128:257], lhsT=AT[:P, :P], rhs=VX[:P, :], start=True, stop=True),
               reads=[AT.b, VX.b], writes=[PD.b])
            if is_s:
                op("pool", lambda e: e.memset(ZQ[:, :], 0.0), writes=[ZQ.b])
                for b in range(SB):
                    op("pool", lambda e, b=b: e.tensor_copy(out=ZQ[:, b * SP + b:(b + 1) * SP:SB], in_=QT[:, b:SP:SB]),
                       reads=[QT.b], writes=[ZQ.b])
                for b in range(SB):
                    op("pe", lambda e, b=b: e.matmul(PC[:P, 256:385], lhsT=ZQ[:, b * SP:(b + 1) * SP], rhs=CTS[:, b, :],
                                                     start=(b == 0), stop=(b == SB - 1)),
                       reads=[ZQ.b, CTS.b], writes=[PC.b] if b in (0, SB - 1) else [])
            else:
                op("pe", lambda e, hh=hh: e.matmul(PC[:P, 256:385], lhsT=QT[:, :P], rhs=CTX[:, hh, :], start=True, stop=True),
                   reads=[QT.b, CTX.b], writes=[PC.b])
            op("act", lambda e: e.activation(out=TOT[:P, :], in_=PC[:P, 256:385], func=AF.Identity, scale=S1(37)),
               reads=[PC.b, SM.b], writes=[TOT.b])
            op("dve", lambda e: e.tensor_tensor(out=TOT[:P, :], in0=TOT[:P, :], in1=PD[:P, 128:257], op=ALU.add),
               reads=[TOT.b, PD.b], writes=[TOT.b])
            op("dve", lambda e: e.scalar_tensor_tensor(out=S1(39), in0=TOT[:P, 128:129], scalar=-1.0, in1=TOT[:P, 128:129], op0=ALU.mult, op1=ALU.max),
               reads=[TOT.b], writes=[SM.b])
            op("dve", lambda e: e.tensor_tensor(out=S1(39), in0=S1(39), in1=S1(38), op=ALU.max), reads=[SM.b], writes=[SM.b])
            op("dve", lambda e: e.reciprocal(out=S1(40), in_=S1(39)), reads=[SM.b], writes=[SM.b])
            op("dve", lambda e: e.tensor_scalar(out=HN[:P, :], in0=TOT[:P, 0:128], scalar1=S1(40), scalar2=None, op0=ALU.mult),
               reads=[TOT.b, SM.b], writes=[HN.b])
            op("dve", lambda e: e.bn_stats(out=ST6[:P, 0, :], in_=HN[:P, :]), reads=[HN.b], writes=[ST6.b])
            op("dve", lambda e: e.bn_aggr(out=SM[:P, 41:43], in_=ST6[:P, 0, :]), reads=[ST6.b], writes=[SM.b])
            op("act", lambda e: e.activation(out=S1(43), in_=S1(42), func=AF.Ln, bias=EPSB[:P, 0:1], scale=1.0), reads=[SM.b, EPSB.b], writes=[SM.b])
            op("act", lambda e: e.activation(out=S1(44), in_=S1(43), func=AF.Exp, scale=-0.5), reads=[SM.b], writes=[SM.b])
            op("dve", lambda e: e.tensor_scalar(out=HN[:P, :], in0=HN[:P, :], scalar1=S1(41), scalar2=S1(44),
                                                op0=ALU.subtract, op1=ALU.mult), reads=[HN.b, SM.b], writes=[HN.b])
            op("dve", lambda e, hh=hh: e.tensor_tensor(out=HN[:P, :], in0=HN[:P, :],
                                                       in1=PRM[:P, o_mh + hh * 128:o_mh + (hh + 1) * 128], op=ALU.mult),
               reads=[HN.b, PRM.b], writes=[HN.b])
            op("act", lambda e: e.activation(out=SG[:P, :], in_=os_, func=AF.Exp, scale=-1.0), reads=[PROJ.b], writes=[SG.b])
            op("dve", lambda e: e.tensor_scalar_add(out=SG[:P, :], in0=SG[:P, :], scalar1=1.0), reads=[SG.b], writes=[SG.b])
            op("dve", lambda e: e.reciprocal(out=SG[:P, :], in_=SG[:P, :]), reads=[SG.b], writes=[SG.b])
            op("dve", lambda e, hh=hh: e.tensor_tensor(out=Y[:P, hh * 128:(hh + 1) * 128], in0=HN[:P, :], in1=SG[:P, :], op=ALU.mult),
               reads=[HN.b, SG.b], writes=[Y.b])
            op("dve", lambda e, hh=hh: e.tensor_tensor(out=S1(45), in0=mtok[:P, hh:hh + 1], in1=S1(32), op=ALU.max),
               reads=[SM.b, mtok.b], writes=[SM.b])
            op("dve", lambda e, hh=hh: e.tensor_tensor(out=S1(45), in0=S1(45), in1=col(28), op=ALU.add), reads=[SM.b], writes=[SM.b])
            op("dve", lambda e, hh=hh: e.tensor_tensor(out=S1(46), in0=col(28), in1=S1(45), op=ALU.subtract), reads=[SM.b], writes=[SM.b])
            op("act", lambda e, hh=hh: e.activation(out=S1(47), in_=col(24), func=AF.Exp, bias=S1(46), scale=1.0),
               reads=[SM.b], writes=[SM.b])
            op("act", lambda e, hh=hh: e.activation(out=S1(48), in_=mtok[:P, hh:hh + 1], func=AF.Exp, bias=S1(46), scale=1.0),
               reads=[SM.b, mtok.b], writes=[SM.b])
            if not is_s:
                op("dve", lambda e: e.tensor_scalar(out=WV[:P, :], in0=VX[:P, :], scalar1=S1(47), scalar2=None, op0=ALU.mult),
                   reads=[VX.b, SM.b], writes=[WV.b])
                op("pe", lambda e: e.matmul(PB[:, 256:385], lhsT=ks, rhs=WV[:P, :], start=True, stop=True),
                   reads=[PROJ.b, WV.b], writes=[PB.b])
                op("dve", lambda e, hh=hh: e.scalar_tensor_tensor(out=CTX[:, hh, :], in0=CTX[:, hh, :], scalar=S1(48), in1=PB[:, 256:385],
                                                                  op0=ALU.mult, op1=ALU.add),
                   reads=[CTX.b, SM.b, PB.b], writes=[CTX.b])
                op("dve", lambda e, hh=hh: e.tensor_copy(out=MREP[:, hh:hh + 1], in_=S1(45)), reads=[SM.b], writes=[MREP.b])
                if t == NT - 1:
                    op("pe", lambda e, hh=hh: e.transpose(out=PA[:, 128:256], in_=CTX[:, hh, 0:128], identity=C("ident")),
                       reads=[CTX.b, CST.b], writes=[PA.b])
                    op("act", lambda e: e.copy(out=OUTC[:, :], in_=PA[:, 128:256]), reads=[PA.b], writes=[OUTC.b])
                    out_dma(E["o_pC"][l, hh], OUTC[:, :], [OUTC.b])
                    out_dma(E["o_pn"][l, hh].rearrange("(k o) -> k o", o=1), CTX[:, hh, 128:129], [CTX.b])
                    if hh == 3:
                        out_dma(E["o_pm"][l:l + 1, :], MREP[0:1, :], [MREP.b])
            else:
                op("dve", lambda e: e.tensor_scalar(out=WCB[:P, :], in0=C("onehotB", SP), scalar1=S1(47), scalar2=None, op0=ALU.mult),
                   reads=[SM.b, CST.b], writes=[WCB.b])
                op("dve", lambda e: e.tensor_tensor(out=RA[:P, :, :], in0=vs.unsqueeze(1).broadcast_to([P, SB, 128]),
                                                    in1=WCB[:P, :].unsqueeze(2).broadcast_to([P, SB, 128]), op=ALU.mult),
                   reads=[PROJ.b, WCB.b], writes=[RA.b])
                op("dve", lambda e: e.tensor_scalar(out=DECD[:P, :], in0=C("onehot0", SP), scalar1=S1(48), scalar2=None, op0=ALU.mult),
                   reads=[SM.b, CST.b], writes=[DECD.b])
                op("pe", lambda e: e.matmul(PA[:, 16:32], lhsT=C("ones", SP, 128), rhs=DECD[:P, :], start=True, stop=True),
                   reads=[DECD.b, CST.b], writes=[PA.b])
                op("act", lambda e: e.copy(out=DECR[:, :], in_=PA[:, 16:32]), reads=[PA.b], writes=[DECR.b])
                for b in range(SB):
                    pq = [PA, PB, PC, PD][b % 4]
                    op("pe", lambda e, b=b, pq=pq: e.matmul(pq[:, 384:512], lhsT=RA[:P, b, :], rhs=ks, start=True, stop=True),
                       reads=[RA.b, PROJ.b], writes=[pq.b])
                    op("dve", lambda e, b=b, pq=pq: e.scalar_tensor_tensor(out=CN[:, b, :], in0=CN[:, b, :], scalar=DECR[:, b:b + 1],
                                                                           in1=pq[:, 384:512], op0=ALU.mult, op1=ALU.add),
                       reads=[CN.b, DECR.b, pq.b], writes=[CN.b])
                out_dma(E["o_nC"][l, :, hh].rearrange("b v k -> v b k"), CN[:, :, :], [CN.b])
                op("pe", lambda e: e.matmul(PA[:SB, 32:160], lhsT=WCB[:P, :], rhs=ks, start=True, stop=True),
                   reads=[WCB.b, PROJ.b], writes=[PA.b])
                op("dve", lambda e, hh=hh: e.scalar_tensor_tensor(out=NNAT[:, hh, :], in0=NNAT[:, hh, :], scalar=SM[:SB, 48:49],
                                                                  in1=PA[:SB, 32:160], op0=ALU.mult, op1=ALU.add),
                   reads=[NNAT.b, SM.b, PA.b], writes=[NNAT.b])
                op("dve", lambda e, hh=hh: e.tensor_copy(out=MSO[:, hh:hh + 1], in_=SM[:SB, 45:46]), reads=[SM.b], writes=[MSO.b])
                if hh == 3:
                    out_dma(E["o_nn"][l], NNAT[:, :, :], [NNAT.b])
                    out_dma(E["o_nm"][l], MSO[:, :], [MSO.b])

        vsv = PROJ[:P, 2312:2568].rearrange("p (g d) -> p g d", g=4)
        op("dve", lambda e: e.tensor_reduce(out=SM[:P, 50:54], in_=vsv, axis=AX.X, op=ALU.add), reads=[PROJ.b], writes=[SM.b])
        op("dve", lambda e: e.tensor_scalar(out=SM[:P, 50:54], in0=SM[:P, 50:54], scalar1=1.0 / 64, scalar2=None, op0=ALU.mult),
           reads=[SM.b], writes=[SM.b])
        op("dve", lambda e: e.tensor_tensor(out=VN[:P, :].rearrange("p (g d) -> p g d", g=4), in0=vsv,
                                            in1=SM[:P, 50:54].unsqueeze(2).broadcast_to([P, 4, 64]), op=ALU.subtract),
           reads=[PROJ.b, SM.b], writes=[VN.b])
        op("pool", lambda e: e.tensor_tensor(out=VTMP[:P, :], in0=VN[:P, :], in1=VN[:P, :], op=ALU.mult), reads=[VN.b], writes=[VTMP.b])
        op("dve", lambda e: e.tensor_reduce(out=SM[:P, 54:58], in_=VTMP[:P, :].rearrange("p (g d) -> p g d", g=4), axis=AX.X, op=ALU.add),
           reads=[VTMP.b], writes=[SM.b])
        op("act", lambda e: e.activation(out=SM[:P, 54:58], in_=SM[:P, 54:58], func=AF.Ln, bias=EPSB[:P, 0:1], scale=1.0 / 64),
           reads=[SM.b, EPSB.b], writes=[SM.b])
        op("act", lambda e: e.activation(out=SM[:P, 58:62], in_=SM[:P, 54:58], func=AF.Exp, scale=-0.5), reads=[SM.b], writes=[SM.b])
        op("dve", lambda e: e.tensor_tensor(out=VN[:P, :].rearrange("p (g d) -> p g d", g=4), in0=VN[:P, :].rearrange("p (g d) -> p g d", g=4),
                                            in1=SM[:P, 58:62].unsqueeze(2).broadcast_to([P, 4, 64]), op=ALU.mult),
           reads=[VN.b, SM.b], writes=[VN.b])
        op("pool", lambda e: e.tensor_tensor(out=VN[:P, :], in0=VN[:P, :], in1=PRM[:P, o_sg:o_sg + 256], op=ALU.mult),
           reads=[VN.b, PRM.b], writes=[VN.b])
        op("pool", lambda e: e.tensor_tensor(out=VN[:P, :], in0=VN[:P, :], in1=PRM[:P, o_sb:o_sb + 256], op=ALU.add),
           reads=[VN.b, PRM.b], writes=[VN.b])
        wsl = WSs if is_s else WS
        bsl = BSs if is_s else BS
        for g in range(4):
            op("pe", lambda e, g=g: e.matmul(PC[:P, g * 64:(g + 1) * 64], lhsT=wsl[:P, g, :P], rhs=VN[:P, g * 64:(g + 1) * 64],
                                             start=True, stop=True), reads=[wsl.b, VN.b], writes=[PC.b])
        for g in range(4):
            op("dve", lambda e, g=g: e.scalar_tensor_tensor(out=Y[:P, 512 + g * 64:512 + (g + 1) * 64], in0=PC[:P, g * 64:(g + 1) * 64],
                                                            scalar=bsl[:P, g:g + 1], in1=PROJ[:P, 2056 + g * 64:2056 + (g + 1) * 64],
                                                            op0=ALU.add, op1=ALU.mult),
               reads=[PC.b, bsl.b, PROJ.b], writes=[Y.b])
        if is_s:
            for tq in range(ST):
                out_dma(E["o_nv"][l][:, tq, :], VN[tq * SB:(tq + 1) * SB, :], [VN.b])

        pin = lambda g: PROJ[:P, 2568 + g * 64:2568 + (g + 1) * 64]
        if is_s:
            kb.dma("sp", lambda q: q.dma_start(out=SPA[:, :], in_=E["spA"][l]), SPA.b, writes=[SPA.b])
            kb.dma("sp", lambda q: q.dma_start(out=SPB[:112, :], in_=E["spB"][l]), SPB.b, writes=[SPB.b])
            for g in range(4):
                op("pe", lambda e, g=g: e.matmul(PA[:64, g * 128:g * 128 + P], lhsT=SPA[:, g * 64:(g + 1) * 64], rhs=Cg("bsA", g, 128, SP, SP),
                                                 start=True, stop=False), reads=[SPA.b, CST.b], writes=[PA.b])
                op("pe", lambda e, g=g: e.matmul(PA[:64, g * 128:g * 128 + P], lhsT=SPB[:112, g * 64:(g + 1) * 64], rhs=Cg("bsB", g, 112, SP, SP),
                                                 start=False, stop=False), reads=[SPB.b, CST.b], writes=[])
                op("pe", lambda e, g=g: e.matmul(PA[:64, g * 128:g * 128 + P], lhsT=pin(g), rhs=Cg("bsC", g, SP, SP, SP),
                                                 start=False, stop=True), reads=[PROJ.b, CST.b], writes=[PA.b])
            npv = E["o_np"][l].rearrange("b r c -> r b c")
            for r in range(4):
                out_dma(npv[r], SPA[64 + r * SB:64 + (r + 1) * SB, :], [SPA.b])
            for r in range(7):
                out_dma(npv[4 + r], SPB[r * SB:(r + 1) * SB, :], [SPB.b])
            for r in range(4):
                out_dma(npv[11 + r], PROJ[r * SB:(r + 1) * SB, 2568:2824], [PROJ.b])
        else:
            for g in range(4):
                band = Cg("bandc0" if t == 0 else "bandc", g, 128, 128, 128)
                op("pe", lambda e, g=g, band=band: e.matmul(PA[:64, g * 128:(g + 1) * 128], lhsT=pin(g), rhs=band, start=True, stop=(t == 0)),
                   reads=[PROJ.b, CST.b], writes=[PA.b])
                if t > 0:
                    op("pe", lambda e, g=g: e.matmul(PA[:64, g * 128:(g + 1) * 128], lhsT=PREV[:, g * 64:(g + 1) * 64],
                                                     rhs=Cg("bandp", g, 128, 128, 128), start=False, stop=True),
                       reads=[PREV.b, CST.b], writes=[PA.b])
            if t < NT - 1:
                op("pool", lambda e: e.tensor_copy(out=PREV[:, :], in_=PROJ[:, 2568:2824]), reads=[PROJ.b], writes=[PREV.b])
            else:
                out_dma(E["o_pp"][l], PROJ[113:128, 2568:2824], [PROJ.b])
        op("act", lambda e: e.copy(out=PTT[:, :, :P], in_=PA[:64, :].rearrange("p (g c) -> p g c", g=4)[:, :, :P]),
           reads=[PA.b], writes=[PTT.b])
        for g in range(4):
            op("pe", lambda e, g=g: e.matmul(PB[:P, g * 64:(g + 1) * 64], lhsT=PTT[:, g, :P], rhs=WP[:, g, :], start=True, stop=True),
               reads=[PTT.b, WP.b], writes=[PB.b])
        op("dve", lambda e: e.tensor_tensor(out=Y[:P, 768:1024], in0=PB[:P, 0:256], in1=PRM[:P, o_ps:o_ps + 256], op=ALU.mult),
           reads=[PB.b, PRM.b], writes=[Y.b])

        _transpose8(kb, E, Y, HT, PT, P)
        for (c0, w) in mk_chunks(D):
            stream_mm(w_o_v, c0, w, HT, P,
                      lambda pm, i, c0=c0, w=w: op("dve", lambda e: e.tensor_tensor(out=H[:P, c0:c0 + w], in0=pm[:P, 0:w],
                                                                                    in1=ada[:P, 2048 + c0:2048 + c0 + w], op=ALU.mult),
                                                   reads=[pm.b, ada.b], writes=[H.b]))
        _resid_ln(kb, X, XB[t], t, P, H, SM, ST6, PRM, o_l1g, o_l1b, EPSB)


def _resid_ln(kb, X, xb, t, P, Z, SM, ST6, PRM, og, ob, EPSB):
    op = kb.op
    xt = X[:P, t, :]
    op("dve", lambda e: e.scalar_tensor_tensor(out=Z[:P, :], in0=xt, scalar=ALPHA, in1=Z[:P, :], op0=ALU.mult, op1=ALU.add),
       reads=[xb, Z.b], writes=[Z.b])
    op("dve", lambda e: e.bn_stats(out=ST6[:P, 0, :], in_=Z[:P, 0:512]), reads=[Z.b], writes=[ST6.b])
    op("dve", lambda e: e.bn_stats(out=ST6[:P, 1, :], in_=Z[:P, 512:1024]), reads=[Z.b], writes=[ST6.b])
    op("dve", lambda e: e.bn_aggr(out=SM[:P, 41:43], in_=ST6[:P, :, :].rearrange("p a b -> p (a b)")), reads=[ST6.b], writes=[SM.b])
    op("act", lambda e: e.activation(out=SM[:P, 43:44], in_=SM[:P, 42:43], func=AF.Ln, bias=EPSB[:P, 0:1], scale=1.0), reads=[SM.b, EPSB.b], writes=[SM.b])
    op("act", lambda e: e.activation(out=SM[:P, 44:45], in_=SM[:P, 43:44], func=AF.Exp, scale=-0.5), reads=[SM.b], writes=[SM.b])
    op("dve", lambda e: e.tensor_scalar(out=Z[:P, :], in0=Z[:P, :], scalar1=SM[:P, 41:42], scalar2=SM[:P, 44:45],
                                        op0=ALU.subtract, op1=ALU.mult), reads=[Z.b, SM.b], writes=[Z.b])
    op("pool", lambda e: e.tensor_tensor(out=Z[:P, :], in0=Z[:P, :], in1=PRM[:P, og:og + D], op=ALU.mult), reads=[Z.b, PRM.b], writes=[Z.b])
    op("dve", lambda e: e.tensor_tensor(out=xt, in0=Z[:P, :], in1=PRM[:P, ob:ob + D], op=ALU.add), reads=[Z.b, PRM.b], writes=[xb])


def _phase2(nc, kb, l, E):
    C, CST, X, XB = E["C"], E["CST"], E["X"], E["XB"]
    ADA = Tn(kb, "ADA2", [128, 3072]); ADAs = ADA
    WCH = [Tn(kb, "WCHb%d" % i, [128, 8, 128], dma=True) for i in range(2)]
    badac = [Tn(kb, "badab%d" % i, [128, 256], dma=True) for i in range(2)]
    Hs = [Tn(kb, "H2_%d" % i, [128, D], dma=True) for i in range(2)]
    HT = Tn(kb, "H2T", [128, 8, 128])
    PRM = Tn(kb, "PRM2", [128, 2 * D], dma=True)
    KTS = Tn(kb, "KTS", [128, 16, 128], dma=True)
    S0 = Tn(kb, "S0", [128, 2048]); S1_ = Tn(kb, "S1", [128, 2048]); S2 = Tn(kb, "S2", [128, 256])
    TOPS = Tn(kb, "TOPS", [128, 16, 16]); IDXU = Tn(kb, "IDXU", [128, 16, 16], U32); IDXF = Tn(kb, "IDXF", [128, 16, 16])
    CV = Tn(kb, "CV", [128, 8, 16]); CPOS = Tn(kb, "CPOS", [128, 8, 16], U32)
    PAU = Tn(kb, "PAU", [128, 8, 16], U32); PBU = Tn(kb, "PBU", [128, 8, 16], U32)
    PAF = Tn(kb, "PAF", [128, 8, 16]); PBF = Tn(kb, "PBF", [128, 8, 16])
    I1 = Tn(kb, "I1", [128, 128]); I2 = Tn(kb, "I2", [128, 128])
    IDXs = [Tn(kb, "IDX%d" % i, [128, 128], I32) for i in range(2)]
    GATEs = [Tn(kb, "GATE%d" % i, [128, 128]) for i in range(2)]
    ACTV = Tn(kb, "ACTV", [128, 128]); COEF = Tn(kb, "COEF", [128, 128])
    SMf = Tn(kb, "SM2f", [128, 16]); SM = Tn(kb, "SM2", [128, 64]); ST6 = Tn(kb, "ST62", [128, 2, 6])
    NB = NBUF
    UB = [Tn(kb, "UB%d" % i, [128, 2 * D], dma=True) for i in range(NB)]
    ACTB = [kb.buf("actv%d" % i) for i in range(NB)]
    COEFB = [kb.buf("coef%d" % i) for i in range(NB)]
    COEF2 = Tn(kb, "COEF2", [128, 128])

    class _View:
        def __init__(self, ap, b):
            self.t = ap
            self.b = b

        def __getitem__(self, k):
            return self.t[k]
    WCA = [_View(UB[0][:, :].rearrange("p (k c) -> p k c", k=8), UB[0].b)]
    PT = [Tn(kb, "PTb%d" % i, [128, 512], psum=True) for i in range(2)]
    PQ = [Tn(kb, "PQ%d" % i, [128, 512], psum=True) for i in range(2)]
    PM = PQ
    ACCP = [Tn(kb, "ACCP%d" % i, [128, 512], psum=True) for i in range(2)]
    JUNK = Tn(kb, "JUNK", [128, D])
    ACC = JUNK
    TMPS = [Tn(kb, "TMPS%d" % i, [128, D], F32R) for i in range(2)]
    IDR = Tn(kb, "IDR", [128, 128], F32R)
    PS = [Tn(kb, "PS%d" % i, [128, 512], psum=True) for i in range(2)]

    kb.dma("sp", lambda q: q.dma_start(out=PRM[:, 0:D], in_=E["ln2g"][l]), PRM.b, writes=[PRM.b])
    kb.dma("sp", lambda q: q.dma_start(out=PRM[:, D:2 * D], in_=E["ln2b"][l]), PRM.b, writes=[PRM.b])
    kb.dma("sp", lambda q: q.dma_start(out=KTS[:, :, :], in_=E["keysT"][l]), KTS.b, writes=[KTS.b])
    wpq_v = E["w_pq"][l]
    kb.op("act", lambda e: e.copy(out=IDR[:, :], in_=C("ident")), reads=[CST.b], writes=[IDR.b])
    tab = E["puv"][l]
    wctr = [0]

    def make_front(t, sset):
        items = []

        def op(e, fn, reads=(), writes=()):
            items.append(("op", e, _bind(fn), None, tuple(reads), tuple(writes)))

        def dma(e, fn, owner, reads=(), writes=()):
            items.append(("dma", e, _bind(fn), owner, tuple(reads), tuple(writes)))
        is_s = (t == NT)
        P = SP if is_s else 128
        ada = ADAs if is_s else ADA
        H = Hs[sset]; IDX = IDXs[sset]; GATE = GATEs[sset]
        QT = S0
        xt = X[:P, t, :]
        op("dve", lambda e: e.tensor_tensor(out=H[:P, :], in0=xt, in1=ada[:P, 1024:2048], op=ALU.mult), reads=[XB[t], ada.b], writes=[H.b])
        op("dve", lambda e: e.tensor_tensor(out=H[:P, :], in0=H[:P, :], in1=ada[:P, 0:1024], op=ALU.add), reads=[H.b, ada.b], writes=[H.b])
        _transpose8(kb, E, H, HT, PT, P, op=op)
        def wdma(c):
            i = c % 2
            dma("sp", lambda q: q.dma_start(out=WCH[i][:, :, :], in_=wpq_v[c].rearrange("p (k j) -> p k j", k=8)), WCH[i].b, writes=[WCH[i].b])
        wdma(0)
        for c in range(16):
            i = c % 2
            j = c % 2
            if c + 1 < 16:
                wdma(c + 1)
            for k in range(8):
                op("pe", lambda e: e.matmul(PQ[j][:, 0:P], lhsT=WCH[i][:, k, :], rhs=HT[:, k, :P], start=(k == 0), stop=(k == 7)),
                   reads=[WCH[i].b, HT.b], writes=[PQ[j].b] if k in (0, 7) else [])
            op("act", lambda e: e.copy(out=QT[:, c * 128:c * 128 + P], in_=PQ[j][:, 0:P]), reads=[PQ[j].b], writes=[QT.b])
        for c4 in range(4):
            ps = PS[c4 % 2]
            for j in range(4):
                c = c4 * 4 + j
                op("pe", lambda e: e.matmul(ps[:P, j * 128:(j + 1) * 128], lhsT=QT[:, c * 128:c * 128 + P], rhs=KTS[:, c, :], start=True, stop=True),
                   reads=[QT.b, KTS.b], writes=[ps.b])
            op("act", lambda e: e.copy(out=S1_[:P, c4 * 512:(c4 + 1) * 512], in_=ps[:P, :]), reads=[ps.b], writes=[S1_.b])
        for c in range(16):
            sc = S1_[:P, c * 128:(c + 1) * 128]
            wk = S2[:P, 0:128]
            op("dve", lambda e: e.max(out=TOPS[:P, c, 0:8], in_=sc), reads=[S1_.b], writes=[TOPS.b])
            op("dve", lambda e: e.max_index(out=IDXU[:P, c, 0:8], in_max=TOPS[:P, c, 0:8], in_values=sc), reads=[S1_.b, TOPS.b], writes=[IDXU.b])
            op("dve", lambda e: e.match_replace(out=wk, in_to_replace=TOPS[:P, c, 0:8], in_values=sc, imm_value=NEG), reads=[S1_.b, TOPS.b], writes=[S2.b])
            op("dve", lambda e: e.max(out=TOPS[:P, c, 8:16], in_=wk), reads=[S2.b], writes=[TOPS.b])
            op("dve", lambda e: e.max_index(out=IDXU[:P, c, 8:16], in_max=TOPS[:P, c, 8:16], in_values=wk), reads=[S2.b, TOPS.b], writes=[IDXU.b])
        op("dve", lambda e: e.tensor_copy(out=IDXF[:P, :, :], in_=IDXU[:P, :, :]), reads=[IDXU.b], writes=[IDXF.b])
        tv = TOPS[:P, :, :].rearrange("p (h two) k -> p h two k", two=2)
        CAND = S0
        op("dve", lambda e: e.tensor_tensor(out=CAND[:P, :].rearrange("p (h a b) -> p h a b", h=8, a=16),
                                            in0=tv[:, :, 0, :].unsqueeze(3).broadcast_to([P, 8, 16, 16]),
                                            in1=tv[:, :, 1, :].unsqueeze(2).broadcast_to([P, 8, 16, 16]), op=ALU.add),
           reads=[TOPS.b], writes=[S0.b])
        for h in range(8):
            cd = CAND[:P, h * 256:(h + 1) * 256]
            wk = S2[:P, 0:256]
            op("dve", lambda e: e.max(out=CV[:P, h, 0:8], in_=cd), reads=[S0.b], writes=[CV.b])
            op("dve", lambda e: e.max_index(out=CPOS[:P, h, 0:8], in_max=CV[:P, h, 0:8], in_values=cd), reads=[S0.b, CV.b], writes=[CPOS.b])
            op("dve", lambda e: e.match_replace(out=wk, in_to_replace=CV[:P, h, 0:8], in_values=cd, imm_value=NEG), reads=[S0.b, CV.b], writes=[S2.b])
            op("dve", lambda e: e.max(out=CV[:P, h, 8:16], in_=wk), reads=[S2.b], writes=[CV.b])
            op("dve", lambda e: e.max_index(out=CPOS[:P, h, 8:16], in_max=CV[:P, h, 8:16], in_values=wk), reads=[S2.b, CV.b], writes=[CPOS.b])
        op("dve", lambda e: e.tensor_single_scalar(out=PAU[:P, :, :], in_=CPOS[:P, :, :], scalar=4, op=ALU.logical_shift_right), reads=[CPOS.b], writes=[PAU.b])
        op("dve", lambda e: e.tensor_single_scalar(out=PBU[:P, :, :], in_=CPOS[:P, :, :], scalar=15, op=ALU.bitwise_and), reads=[CPOS.b], writes=[PBU.b])
        op("dve", lambda e: e.tensor_copy(out=PAF[:P, :, :], in_=PAU[:P, :, :]), reads=[PAU.b], writes=[PAF.b])
        op("dve", lambda e: e.tensor_copy(out=PBF[:P, :, :], in_=PBU[:P, :, :]), reads=[PBU.b], writes=[PBF.b])
        iv = IDXF[:P, :, :].rearrange("p (h two) k -> p h two k", two=2)
        io16 = C("iota16", P).unsqueeze(1).unsqueeze(1).broadcast_to([P, 8, 16, 16])
        for (pf, half, dst) in [(PAF, 0, I1), (PBF, 1, I2)]:
            eq = S1_[:P, :].rearrange("p (h k a) -> p h k a", h=8, k=16)
            op("dve", lambda e: e.tensor_tensor(out=eq, in0=pf[:P, :, :].unsqueeze(3).broadcast_to([P, 8, 16, 16]), in1=io16, op=ALU.is_equal),
               reads=[pf.b, CST.b], writes=[S1_.b])
            op("dve", lambda e: e.tensor_tensor(out=eq, in0=eq, in1=iv[:, :, half, :].unsqueeze(2).broadcast_to([P, 8, 16, 16]), op=ALU.mult),
               reads=[S1_.b, IDXF.b], writes=[S1_.b])
            op("dve", lambda e: e.tensor_reduce(out=dst[:P, :].rearrange("p (h k) -> p h k", h=8), in_=eq, axis=AX.X, op=ALU.add),
               reads=[S1_.b], writes=[dst.b])
        op("dve", lambda e: e.scalar_tensor_tensor(out=I1[:P, :], in0=I1[:P, :], scalar=128.0, in1=I2[:P, :], op0=ALU.mult, op1=ALU.add),
           reads=[I1.b, I2.b], writes=[I1.b])
        op("dve", lambda e: e.tensor_copy(out=IDX[:P, :], in_=I1[:P, :]), reads=[I1.b], writes=[IDX.b])
        gv = GATE[:P, :].rearrange("p (h k) -> p h k", h=8)
        op("dve", lambda e: e.tensor_tensor(out=gv, in0=CV[:P, :, :], in1=CV[:P, :, 0:1].broadcast_to([P, 8, 16]), op=ALU.subtract),
           reads=[CV.b], writes=[GATE.b])
        op("act", lambda e: e.activation(out=GATE[:P, :], in_=GATE[:P, :], func=AF.Exp), reads=[GATE.b], writes=[GATE.b])
        op("dve", lambda e: e.tensor_reduce(out=SMf[:P, 0:8], in_=gv, axis=AX.X, op=ALU.add), reads=[GATE.b], writes=[SMf.b])
        op("dve", lambda e: e.reciprocal(out=SMf[:P, 8:16], in_=SMf[:P, 0:8]), reads=[SMf.b], writes=[SMf.b])
        op("dve", lambda e: e.tensor_tensor(out=gv, in0=gv, in1=SMf[:P, 8:16].unsqueeze(2).broadcast_to([P, 8, 16]), op=ALU.mult),
           reads=[GATE.b, SMf.b], writes=[GATE.b])
        return items

    def run_items(items, n=None):
        n = len(items) if n is None else min(n, len(items))
        for _ in range(n):
            kind, e, fn, owner, reads, writes = items.pop(0)
            if kind == "op":
                kb.op(e, fn, reads=reads, writes=writes, bound=True)
            else:
                kb.dma(e, fn, owner, reads=reads, writes=writes, bound=True)

    def back(t, sset, nxt):
        op = kb.op
        is_s = (t == NT)
        P = SP if is_s else 128
        ada = ADAs if is_s else ADA
        H = Hs[sset]; IDX = IDXs[sset]; GATE = GATEs[sset]
        per = 0 if not nxt else (len(nxt) + 119) // 120

        def axpy(s):
            b = s % NB
            tm = TMPS[s % 2]
            op("dve", lambda e: e.tensor_tensor(out=COEF2[:P, s:s + 1], in0=COEF[:P, s:s + 1], in1=GATE[:P, s:s + 1], op=ALU.mult),
               reads=[COEFB[b], GATE.b], writes=[COEF2.b])
            op("act", lambda e: e.activation(out=tm[:P, :], in_=UB[b][:P, D:2 * D], func=AF.Identity, scale=COEF2[:P, s:s + 1]),
               reads=[UB[b].b, COEF2.b], writes=[tm.b])
            for hf in range(2):
                op("pe", lambda e: e.matmul(ACCP[hf][:, :], lhsT=IDR[:, :], rhs=tm[:, hf * 512:(hf + 1) * 512], start=(s == 0), stop=(s == 127)),
                   reads=[IDR.b, tm.b], writes=[ACCP[hf].b] if s in (0, 127) else [])

        for s_ in range(128):
            b = s_ % NB
            kb.dma("pool", lambda q: q.indirect_dma_start(out=UB[b][:P, :], out_offset=None, in_=tab,
                                                          in_offset=bass.IndirectOffsetOnAxis(ap=IDX[:P, s_:s_ + 1], axis=0)),
                   UB[b].b, reads=[IDX.b], writes=[UB[b].b])
            op("dve", lambda e: e.scalar_tensor_tensor(out=JUNK[:P, :], in0=UB[b][:P, 0:D], scalar=1.0, in1=H[:P, :],
                                                       op0=ALU.mult, op1=ALU.mult, accum_out=ACTV[:P, s_:s_ + 1]),
               reads=[UB[b].b, H.b], writes=[JUNK.b, ACTB[b]])
            op("act", lambda e: e.activation(out=COEF[:P, s_:s_ + 1], in_=ACTV[:P, s_:s_ + 1], func=AF.Gelu), reads=[ACTB[b]], writes=[COEFB[b]])
            if s_ >= 1:
                axpy(s_ - 1)
            if nxt:
                run_items(nxt, per)
        axpy(127)
        if nxt:
            run_items(nxt)
        for hf in range(2):
            op("dve", lambda e: e.tensor_tensor(out=ACC[:P, hf * 512:(hf + 1) * 512], in0=ACCP[hf][:P, :], in1=ada[:P, 2048 + hf * 512:2048 + (hf + 1) * 512],
                                                op=ALU.mult), reads=[ACCP[hf].b, ada.b], writes=[ACC.b])
        _resid_ln(kb, X, XB[t], t, P, ACC, SM, ST6, PRM, 0, D, E["EPSB"])

    _ada(nc, kb, l, E, ADA, 128, E["cp"], 3072, WCA, PM, Hs[0], HT, PT, badac)
    run_items(make_front(0, 0))
    for t in range(NT + 1):
        nxt = make_front(t + 1, (t + 1) % 2) if t + 1 < NT else None
        back(t, t % 2, nxt)
        if t + 1 == NT:
            _ada(nc, kb, l, E, ADAs, SP, E["cs"], 3072, WCA, PM, Hs[NT % 2], HT, PT, badac)
            run_items(make_front(NT, NT % 2))


_CACHE = {}


def _chunked(w, cw):
    L, K, n = w.shape
    nch = (n + cw - 1) // cw
    wp = np.zeros((L, K, nch * cw), np.float32)
    wp[:, :, :n] = w
    wp = wp.reshape(L, 8, 128, nch, cw).transpose(0, 3, 2, 1, 4)
    return np.ascontiguousarray(wp.reshape(L, nch, 128, 8 * cw))


def _rep(a, P=128):
    return np.ascontiguousarray(np.broadcast_to(a[:, None, :], (a.shape[0], P, a.shape[1])))


def make_in_maps(inp, cpack):
    f = lambda a: np.ascontiguousarray(np.asarray(a, dtype=np.float32))
    shared = {
        "w_ada": _chunked(f(inp["w_ada"]), WCW), "b_ada": _rep(f(inp["b_ada"])), "w_in": _chunked(f(inp["w_in"]), WCW),
        "b_gate": _rep(f(inp["b_gate"])),
        "mh_g": _rep(f(inp["mh_g"])), "sgu_g": _rep(f(inp["sgu_g"])), "sgu_b": _rep(f(inp["sgu_b"])),
        "pscale": _rep(f(inp["pool_scale"])),
        "w_sT": f(np.asarray(inp["w_s"]).transpose(0, 3, 1, 2)),
        "b_sT": f(np.asarray(inp["b_s"]).transpose(0, 2, 1)),
        "w_pool": f(np.asarray(inp["w_pool"]).transpose(0, 2, 1, 3)),
        "w_o": _chunked(f(inp["w_o"]), WCW), "ln1g": _rep(f(inp["ln1_g"])), "ln1b": _rep(f(inp["ln1_b"])),
        "ln2g": _rep(f(inp["ln2_g"])), "ln2b": _rep(f(inp["ln2_b"])), "w_pq": _chunked(f(inp["w_pq"]), 128),
        "keysT": f(np.asarray(inp["peer_keys"]).transpose(0, 4, 1, 2, 3).reshape(DEPTH, 128, 16, 128)),
        "cst": cpack[0], "cst1": cpack[1],
    }
    ws4 = np.asarray(inp["w_s"])[:, :, :ST, :ST]
    wsS = np.repeat(np.repeat(ws4.transpose(0, 3, 1, 2), SB, axis=1), SB, axis=3)
    shared["w_sS"] = f(wsS)
    bs4 = np.asarray(inp["b_s"])[:, :, :ST]
    shared["b_sS"] = f(np.repeat(bs4.transpose(0, 2, 1), SB, axis=1))
    for l in range(DEPTH):
        shared["puv%d" % l] = np.ascontiguousarray(
            np.concatenate([np.asarray(inp["peer_u"])[l], np.asarray(inp["peer_v"])[l]], axis=1), dtype=np.float32)
    maps = []
    for c in range(NCORES):
        bs = slice(c * SB, (c + 1) * SB)
        m = dict(shared)
        m["xp"] = f(np.asarray(inp["x_prompt"])[c])
        m["xs"] = f(np.asarray(inp["x_sample"])[bs].transpose(1, 0, 2).reshape(SP, D))
        m["cp"] = f(np.broadcast_to(np.asarray(inp["c_prompt"])[c][None, :], (128, D)))
        m["cs"] = f(np.tile(np.asarray(inp["c_sample"])[bs], (ST, 1)))
        sCc = np.asarray(inp["state_mlstm_C"])[:, bs]
        m["sC"] = f(sCc.transpose(0, 2, 3, 1, 4))
        snc = np.asarray(inp["state_mlstm_n"])[:, bs]
        m["snat"] = f(snc)
        m["snT"] = f(snc.transpose(0, 2, 3, 1))
        m["sm"] = f(np.tile(np.asarray(inp["state_mlstm_m"])[:, bs], (1, ST, 1)))
        spc = np.asarray(inp["state_pool"])[:, bs].transpose(0, 2, 1, 3)
        m["spA"] = f(spc[:, 0:8].reshape(DEPTH, 128, 256))
        m["spB"] = f(spc[:, 8:15].reshape(DEPTH, 112, 256))
        maps.append(m)
    return maps


def gather_outputs(results):
    cat = lambda k, ax: np.concatenate([r[k] for r in results], axis=ax)
    yp = np.stack([r["yp"] for r in results], 0)
    ys = np.concatenate([r["ys"].reshape(ST, SB, D).transpose(1, 0, 2) for r in results], 0)
    pC = np.stack([r["pC"] for r in results], 1)
    pn = np.stack([r["pn"] for r in results], 1)
    pm = np.stack([r["pm"] for r in results], 1)
    pp = np.stack([r["pp"] for r in results], 1)
    return (yp, ys, pC, pn, pm, pp, cat("nC", 1), cat("nn", 1), cat("nm", 1), cat("npool", 1), cat("nv", 1))


def kernel(**inputs):
    if "prog" not in _CACHE:
        _CACHE["prog"] = build_program()
    nc, cpack = _CACHE["prog"]
    maps = make_in_maps(inputs, cpack)
    res = run_bass_kernel_spmd(nc, maps, core_ids=list(range(NCORES)))
    outs = gather_outputs(res.results)
    return tuple(np.ascontiguousarray(o, dtype=np.float32) for o in outs)
```

```python
import numpy as np
from contextlib import ExitStack
import concourse.bass as bass
import concourse.mybir as mybir
from concourse.bass_utils import run_bass_kernel_spmd

F32 = mybir.dt.float32
I32 = mybir.dt.int32
U32 = mybir.dt.uint32
F32R = mybir.dt.float32r
BF16 = mybir.dt.bfloat16
ALU = mybir.AluOpType
AF = mybir.ActivationFunctionType
AX = mybir.AxisListType

NCORES = 8
D = 1024
SEQ = 2048
NT = 16
SB = 16
ST = 4
SP = SB * ST
DEPTH = 2
ALPHA = (2 * DEPTH) ** 0.25
LN_EPS = 1e-5
IN_COLS = 2824
NEG = -1.0e30
WCW = 192
NEXP = 16384
SAME_ENGINE_WAITS = True
NBUF = 8


class TB:
    def __init__(self, name, sem=None):
        self.name = name
        self.last_w = None
        self.reads = []
        self.sem = sem
        self.dma_total = 0
        self.dma_dirty = False


class KB:
    ENG = ("pe", "act", "dve", "pool", "sp")

    def __init__(self, nc, stack):
        self.nc = nc
        self.stack = stack
        self.q = {e: [] for e in self.ENG}
        self.cnt = {e: 0 for e in self.ENG}
        self.esem = {e: stack.enter_context(nc.semaphore("es_" + e)) for e in self.ENG}
        self.seen = {e: {} for e in self.ENG}
        self.semobj = {}
        self._sem_owner = {}
        self.stack0 = stack
        self.phase_tbs = []
        self.sfx = ""

    def new_sem(self, name):
        return self.stack.enter_context(self.nc.semaphore(name + self.sfx))

    def buf(self, name, dma=False):
        tb = TB(name, self.new_sem("d_" + name) if dma else None)
        if dma and self.stack is not self.stack0:
            self.phase_tbs.append(tb)
        return tb

    def end_phase(self):
        for tb in self.phase_tbs:
            k = id(tb.sem)
            self._sem_owner.pop(k, None)
            self.semobj.pop(k, None)
            for e in self.ENG:
                self.seen[e].pop(k, None)
        self.phase_tbs = []

    def sb(self, name, shape, dt=F32):
        return self.stack.enter_context(self.nc.sbuf_tensor(name + self.sfx, list(shape), dt))

    def ps(self, name, shape, dt=F32):
        return self.stack.enter_context(self.nc.psum_tensor(name + self.sfx, list(shape), dt))

    def _deps(self, e, reads, writes):
        deps = {}

        def add(tok):
            if tok is None:
                return
            s, v = tok
            k = id(s)
            self.semobj[k] = s
            ow = self._sem_owner.get(k)
            if ow is not None:
                v = ow.dma_total
            if v > deps.get(k, 0):
                deps[k] = v
        for b in reads:
            add(b.last_w)
        for b in writes:
            add(b.last_w)
            for r in b.reads:
                add(r)
        out = []
        own = id(self.esem[e])
        for k, v in deps.items():
            if k == own and (e in ("pe", "sp") or not SAME_ENGINE_WAITS):
                continue
            if self.seen[e].get(k, 0) >= v:
                continue
            self.seen[e][k] = v
            out.append((self.semobj[k], v))
        return out

    def op(self, e, fn, reads=(), writes=(), bound=False):
        waits = self._deps(e, reads, writes)
        for s, v in waits:
            tb = self._sem_owner.get(id(s))
            if tb is not None:
                tb.dma_dirty = True
        self.cnt[e] += 1
        tok = (self.esem[e], self.cnt[e])
        self.q[e].append((waits, fn if bound else _bind(fn), tok[0], 1))
        for b in reads:
            b.reads.append(tok)
        for b in writes:
            b.last_w = tok
            b.reads = []
        return tok

    def dma(self, e, fn, owner, reads=(), writes=(), bound=False):
        self._sem_owner[id(owner.sem)] = owner
        waits = self._deps(e, reads, writes)
        if owner.dma_dirty and owner.dma_total > 0:
            k = id(owner.sem)
            if self.seen[e].get(k, 0) < owner.dma_total:
                self.seen[e][k] = owner.dma_total
                waits.append((owner.sem, owner.dma_total))
            owner.dma_dirty = False
        for s, v in waits:
            tb = self._sem_owner.get(id(s))
            if tb is not None and tb is not owner:
                tb.dma_dirty = True
        owner.dma_total += 16
        tok = (owner.sem, owner.dma_total)
        self.q[e].append((waits, fn if bound else _bind(fn), owner.sem, 16))
        for b in reads:
            b.reads.append(tok)
        for b in writes:
            b.last_w = tok
            b.reads = []
        return tok

    def barrier(self, extra=()):
        toks = [(self.esem[e], self.cnt[e]) for e in self.ENG if self.cnt[e] > 0 and e != "sp"]
        for tb in list(self._sem_owner.values()) + list(extra):
            if tb.dma_total > 0:
                toks.append((tb.sem, tb.dma_total))
        for e in self.ENG:
            waits = []
            for s, v in toks:
                k = id(s)
                if k == id(self.esem[e]):
                    continue
                if self.seen[e].get(k, 0) >= v:
                    continue
                self.seen[e][k] = v
                waits.append((s, v))
            if waits:
                self.q[e].append((waits, None, None, 0))

    def emit(self, final_waits=()):
        nc = self.nc
        engs = {"pe": "tensor", "act": "scalar", "dve": "vector", "pool": "gpsimd", "sp": "sync"}
        with nc.Block() as block:
            for e in self.ENG:
                items = self.q[e]
                fw = list(final_waits) if e == "sp" else []

                def body(eng, items=items, fw=fw):
                    for waits, fn, sem, inc in items:
                        for s, v in waits:
                            eng.wait_ge(s, v)
                        if fn is not None:
                            fn(eng).then_inc(sem, inc)
                    for s, v in fw:
                        eng.wait_ge(s, v)
                getattr(block, engs[e])(body)
        self.q = {e: [] for e in self.ENG}


class _Rec:
    def __init__(self):
        self.call = None

    def __getattr__(self, name):
        def f(*a, **k):
            self.call = (name, a, k)
            return self
        return f


def _bind(fn):
    r = _Rec()
    fn(r)
    assert r.call is not None
    name, a, k = r.call
    return lambda eng: getattr(eng, name)(*a, **k)


class Tn:
    def __init__(self, kb, name, shape, dt=F32, psum=False, dma=False):
        self.t = kb.ps(name, shape, dt) if psum else kb.sb(name, shape, dt)
        self.b = kb.buf(name, dma=dma)

    def __getitem__(self, k):
        return self.t[k]


def _consts():
    c = {}
    i128 = np.arange(128)
    c["ident"] = np.eye(128, dtype=np.float32)
    c["ones"] = np.ones((128, 128), np.float32)
    c["triu"] = (i128[:, None] <= i128[None, :]).astype(np.float32)
    c["negm"] = np.where(i128[None, :] <= i128[:, None], 0.0, NEG).astype(np.float32)
    sel = np.zeros((128, 128), np.float32); sel[127, :] = 1.0
    c["sel127"] = sel
    p = np.arange(SP); tt = p // SB; bb = p % SB
    sameb = bb[:, None] == bb[None, :]
    tri_s = (sameb & (tt[:, None] <= tt[None, :])).astype(np.float32)
    c["tri_s"] = _pad(tri_s)
    c["negm_s"] = _pad(np.where(sameb & (tt[None, :] <= tt[:, None]), 0.0, NEG).astype(np.float32))
    c["negb_s"] = _pad(np.where(sameb, 0.0, NEG).astype(np.float32))
    c["selend"] = _pad(((tt[:, None] == ST - 1) & sameb).astype(np.float32))
    oh = (bb[:, None] == np.arange(SB)[None, :]).astype(np.float32)
    c["onehotB"] = _pad(oh, cols=16)
    oh0 = ((p[:, None] == np.arange(SB)[None, :])).astype(np.float32)
    c["onehot0"] = _pad(oh0, cols=16)
    c["iota16"] = np.broadcast_to(np.arange(16, dtype=np.float32), (128, 16)).copy()
    wins = (2, 4, 8, 16)
    bc0 = np.zeros((4, 128, 128), np.float32); bc = np.zeros((4, 128, 128), np.float32)
    bp = np.zeros((4, 128, 128), np.float32)
    for g, w in enumerate(wins):
        for t in range(128):
            for j in range(w):
                s = t - j
                if s >= 0:
                    bc[g, s, t] += 1.0 / w
                    bc0[g, s, t] += 1.0 / min(t + 1, w)
                else:
                    bp[g, s + 128, t] += 1.0 / w
            bc[g, t, t] -= 1.0
            bc0[g, t, t] -= 1.0
    c["bandc0"] = bc0.transpose(1, 0, 2).reshape(128, 512)
    c["bandc"] = bc.transpose(1, 0, 2).reshape(128, 512)
    c["bandp"] = bp.transpose(1, 0, 2).reshape(128, 512)
    bsA = np.zeros((4, 128, SP), np.float32); bsB = np.zeros((4, 128, SP), np.float32)
    bsC = np.zeros((4, 128, SP), np.float32)
    for g, w in enumerate(wins):
        for t in range(ST):
            for b in range(SB):
                col = t * SB + b
                for j in range(w):
                    r = 15 + t - j
                    if r >= 15:
                        bsC[g, (r - 15) * SB + b, col] += 1.0 / w
                    elif r >= 8:
                        bsB[g, (r - 8) * SB + b, col] += 1.0 / w
                    else:
                        bsA[g, r * SB + b, col] += 1.0 / w
                bsC[g, t * SB + b, col] -= 1.0
    c["bsA"] = bsA.transpose(1, 0, 2).reshape(128, 4 * SP)
    c["bsB"] = bsB.transpose(1, 0, 2).reshape(128, 4 * SP)
    c["bsC"] = bsC.transpose(1, 0, 2).reshape(128, 4 * SP)
    return c


def _pad(a, cols=None):
    out = np.zeros((128, a.shape[1] if cols is None else cols), np.float32)
    out[: a.shape[0], : a.shape[1]] = a
    return out


_CONST_G = ["ident", "ones", "iota16"]
_CONST_1 = ["triu", "negm", "sel127", "tri_s", "negm_s", "negb_s", "selend",
            "onehotB", "onehot0", "bandc0", "bandc", "bandp", "bsA", "bsB", "bsC"]


def _const_pack():
    c = _consts()
    packs = []
    for order in (_CONST_G, _CONST_1):
        offs = {}
        o = 0
        arrs = []
        for k in order:
            offs[k] = (o, c[k].shape[1])
            o += c[k].shape[1]
            arrs.append(c[k])
        packs.append((np.ascontiguousarray(np.concatenate(arrs, axis=1)), offs))
    return packs


def build_program(n_layers=DEPTH, do_phase2=True):
    (cpack, coff), (cpack1, coff1) = _const_pack()
    NCST = cpack.shape[1]
    NCST1 = cpack1.shape[1]
    nc = bass.Bass("TRN2", target_bir_lowering=False)

    def din(name, shape, dt=F32):
        return nc.dram_tensor(name, list(shape), dt, kind="ExternalInput").ap()

    def dout(name, shape, dt=F32):
        return nc.dram_tensor(name, list(shape), dt, kind="ExternalOutput").ap()

    xp = din("xp", [SEQ, D]); xs = din("xs", [SP, D])
    cp = din("cp", [128, D]); cs = din("cs", [SP, D])
    sC = din("sC", [DEPTH, 4, 128, SB, 128]); snat = din("snat", [DEPTH, SB, 4, 128])
    snT = din("snT", [DEPTH, 4, 128, SB]); sm = din("sm", [DEPTH, SP, 4])
    spA = din("spA", [DEPTH, 128, 256]); spB = din("spB", [DEPTH, 112, 256])
    w_ada = din("w_ada", [DEPTH, (6 * D) // WCW, 128, 8 * WCW]); b_ada = din("b_ada", [DEPTH, 128, 6 * D])
    w_in = din("w_in", [DEPTH, (IN_COLS + WCW - 1) // WCW, 128, 8 * WCW]); b_gate = din("b_gate", [DEPTH, 128, 8])
    mh_g = din("mh_g", [DEPTH, 128, 512]); sgu_g = din("sgu_g", [DEPTH, 128, 256])
    sgu_b = din("sgu_b", [DEPTH, 128, 256]); pscale = din("pscale", [DEPTH, 128, 256])
    w_sT = din("w_sT", [DEPTH, 128, 4, 128]); b_sT = din("b_sT", [DEPTH, 128, 4])
    w_sS = din("w_sS", [DEPTH, SP, 4, SP]); b_sS = din("b_sS", [DEPTH, SP, 4])
    w_pool = din("w_pool", [DEPTH, 64, 4, 64]); w_o = din("w_o", [DEPTH, (D + WCW - 1) // WCW, 128, 8 * WCW])
    ln1g = din("ln1g", [DEPTH, 128, D]); ln1b = din("ln1b", [DEPTH, 128, D])
    ln2g = din("ln2g", [DEPTH, 128, D]); ln2b = din("ln2b", [DEPTH, 128, D])
    w_pq = din("w_pq", [DEPTH, 16, 128, 8 * 128]); keysT = din("keysT", [DEPTH, 128, 16, 128])
    puv = [din("puv%d" % l, [NEXP, 2 * D]) for l in range(DEPTH)]
    puvb = [nc.dram_tensor("puvb%d" % l, [NEXP, 2 * D], BF16, kind="Internal").ap() for l in range(DEPTH)]
    cst_d = din("cst", [128, NCST])
    cst1_d = din("cst1", [128, NCST1])

    yp = dout("yp", [SEQ, D]); ys = dout("ys", [SP, D])
    o_pC = dout("pC", [DEPTH, 4, 128, 128]); o_pn = dout("pn", [DEPTH, 4, 128]); o_pm = dout("pm", [DEPTH, 4])
    o_pp = dout("pp", [DEPTH, 15, 256])
    o_nC = dout("nC", [DEPTH, SB, 4, 128, 128]); o_nn = dout("nn", [DEPTH, SB, 4, 128])
    o_nm = dout("nm", [DEPTH, SB, 4]); o_np = dout("npool", [DEPTH, SB, 15, 256])
    o_nv = dout("nv", [DEPTH, SB, ST, 256])

    with ExitStack() as st0:
        kb = KB(nc, st0)
        op = kb.op
        OUT = kb.buf("outs", dma=True)

        def out_dma(dst, src, reads):
            kb.dma("sp", lambda q: q.dma_start(out=dst, in_=src), OUT, reads=reads)

        X = kb.sb("X", [128, NT + 1, D])
        XB = [kb.buf("X%d" % t) for t in range(NT + 1)]
        XL = kb.buf("xload", dma=True)
        CST = Tn(kb, "CST", [128, NCST], dma=True)
        EPSB = Tn(kb, "EPSB", [128, 1])
        kb.op("dve", lambda e: e.memset(EPSB[:, :], LN_EPS), writes=[EPSB.b])

        def C(name, P=128, w=None):
            if name in coff:
                o, n = coff[name]
                return CST[:P, o:o + (n if w is None else w)]
            o, n = coff1[name]
            return kb.cst1[:P, o:o + (n if w is None else w)]

        def Cg(name, g, P, blk, w):
            o, n = coff1[name]
            return kb.cst1[:P, o + g * blk: o + g * blk + w]

        with nc.allow_non_contiguous_dma(reason="small strided state/param loads"):
            kb.dma("sp", lambda q: q.dma_start(out=CST[:, :], in_=cst_d), CST.b, writes=[CST.b])
            for t in range(NT):
                kb.dma("sp", lambda q, t=t: q.dma_start(out=X[:, t, :], in_=xp[t * 128:(t + 1) * 128, :]),
                       XL, writes=[XB[t]])
            kb.dma("sp", lambda q: q.dma_start(out=X[:SP, NT, :], in_=xs), XL, writes=[XB[NT]])

            puvb_ = puvb
            for l in range(n_layers):
                with ExitStack() as st1:
                    kb.stack = st1
                    kb.sfx = "_a%d" % l
                    _phase1(nc, kb, l, locals())
                    kb.barrier(extra=[OUT])
                    kb.emit()
                    kb.end_phase()
                if do_phase2:
                    with ExitStack() as st2:
                        kb.stack = st2
                        kb.sfx = "_b%d" % l
                        _phase2(nc, kb, l, locals())
                        kb.barrier(extra=[OUT])
                        kb.emit()
                        kb.end_phase()
            kb.stack = st0
            kb.sfx = ""
            for t in range(NT):
                out_dma(yp[t * 128:(t + 1) * 128, :], X[:, t, :], [XB[t]])
            out_dma(ys, X[:SP, NT, :], [XB[NT]])
            kb.emit(final_waits=[(OUT.sem, OUT.dma_total)])
    return nc, (cpack, cpack1)


def _ada(nc, kb, l, E, ADA, P, csrc, off, WCH, PM, hbuf, hT, PT, badac, WCR=None):
    op = kb.op
    C = E["C"]
    w_ada, b_ada = E["w_ada"], E["b_ada"]
    kb.dma("sp", lambda q: q.dma_start(out=hbuf[:P, :], in_=csrc), hbuf.b, writes=[hbuf.b])
    op("act", lambda e: e.activation(out=hbuf[:P, :], in_=hbuf[:P, :], func=AF.Silu), reads=[hbuf.b], writes=[hbuf.b])
    _transpose8(kb, E, hbuf, hT, PT, P)
    r32 = (hT.t.dtype == F32R)
    for c in range(3072 // WCW):
        i = c % 2
        c0 = off + c * WCW
        wch = WCH[c % len(WCH)]
        kb.dma("sp", lambda q: q.dma_start(out=wch[:, :, 0:WCW], in_=w_ada[l, c0 // WCW].rearrange("p (k j) -> p k j", k=8)), wch.b, writes=[wch.b])
        kb.dma("sp", lambda q: q.dma_start(out=badac[i][:P, 0:WCW], in_=b_ada[l, :P, c0:c0 + WCW]), badac[i].b, writes=[badac[i].b])
        wsrc = WCR[c % 2] if r32 else wch
        if r32:
            op("act", lambda e: e.copy(out=wsrc[:, :, 0:WCW], in_=wch[:, :, 0:WCW]), reads=[wch.b], writes=[wsrc.b])
        for k in range(8):
            if r32:
                op("pe", lambda e: e.matmul(PM[i][:, 0:WCW], lhsT=hT[:, k, :], rhs=wsrc[:, k, 0:WCW], start=(k == 0), stop=(k == 7)),
                   reads=[hT.b, wsrc.b], writes=[PM[i].b] if k in (0, 7) else [])
            else:
                op("pe", lambda e: e.matmul(PM[i][:P, 0:WCW], lhsT=hT[:, k, :P], rhs=wch[:, k, 0:WCW], start=(k == 0), stop=(k == 7)),
                   reads=[hT.b, wch.b], writes=[PM[i].b] if k in (0, 7) else [])
        op("dve", lambda e: e.tensor_tensor(out=ADA[:P, c * WCW:(c + 1) * WCW], in0=PM[i][:P, 0:WCW], in1=badac[i][:P, 0:WCW], op=ALU.add),
           reads=[PM[i].b, badac[i].b], writes=[ADA.b])
    op("dve", lambda e: e.tensor_scalar_add(out=ADA[:P, 1024:2048], in0=ADA[:P, 1024:2048], scalar1=1.0), reads=[ADA.b], writes=[ADA.b])


def _transpose8(kb, E, src, dstT, PT, P, srcb=None, op=None):
    op = kb.op if op is None else op
    C = E["C"]
    sb_ = src.b if srcb is None else srcb
    for half in range(2):
        for j in range(4):
            k = half * 4 + j
            op("pe", lambda e, half=half, j=j, k=k: e.transpose(
                out=PT[half][:, j * 128:j * 128 + P], in_=src[:P, k * 128:(k + 1) * 128], identity=C("ident", P, P)),
               reads=[sb_, E["CST"].b], writes=[PT[half].b])
        op("act", lambda e, half=half: e.copy(
            out=dstT[:, half * 4:half * 4 + 4, :P],
            in_=PT[half][:, :].rearrange("p (j c) -> p j c", j=4)[:, :, :P]),
           reads=[PT[half].b], writes=[dstT.b])


def _phase1(nc, kb, l, E):
    op = kb.op
    C, Cg, CST, X, XB = E["C"], E["Cg"], E["CST"], E["X"], E["XB"]
    EPSB = E["EPSB"]
    out_dma = E["out_dma"]
    w_in, w_o = E["w_in"], E["w_o"]

    kb.cst1 = kb.sb("CST1", [128, E["NCST1"]])
    kb.dma("sp", lambda q: q.dma_start(out=kb.cst1[:, :], in_=E["cst1_d"]), CST.b, writes=[CST.b])
    ADA = Tn(kb, "ADA1", [128, 3072]); ADAs = ADA
    WCH = [Tn(kb, "WCH%d" % i, [128, 8, WCW], dma=True) for i in range(2)]
    WCR = [Tn(kb, "WCR%d" % i, [128, 8, WCW], F32R) for i in range(2)]
    badac = [Tn(kb, "bada%d" % i, [128, 256], dma=True) for i in range(2)]
    H = Tn(kb, "H", [128, D], dma=True); HT = Tn(kb, "HT", [128, 8, 128], F32R)
    PROJ = Tn(kb, "PROJ", [128, IN_COLS], dma=True)
    Y = Tn(kb, "Y", [128, D])
    PRM = Tn(kb, "PRM", [128, 8 + 512 + 256 * 3 + 2 * D], dma=True)
    WS = Tn(kb, "WS", [128, 4, 128], dma=True); BS = Tn(kb, "BS", [128, 4], dma=True)
    WSs = Tn(kb, "WSs", [128, 4, SP], dma=True); BSs = Tn(kb, "BSs", [128, 4], dma=True)
    WP = Tn(kb, "WP", [64, 4, 64], dma=True)
    PT = [Tn(kb, "PT%d" % i, [128, 512], psum=True) for i in range(2)]
    PM = [Tn(kb, "PM%d" % i, [128, 512], psum=True) for i in range(2)]
    PA = Tn(kb, "PA", [128, 512], psum=True); PB = Tn(kb, "PB", [128, 512], psum=True)
    PC = Tn(kb, "PC", [128, 512], psum=True); PD = Tn(kb, "PD", [128, 512], psum=True)
    SM = Tn(kb, "SM", [128, 64])
    SMs = Tn(kb, "SMs", [128, 4], dma=True)
    MREP = Tn(kb, "MREP", [128, 4])
    CTX = Tn(kb, "CTX", [128, 4, 129], dma=True)
    DG = Tn(kb, "DG", [128, 128]); DL = Tn(kb, "DL", [128, 128]); WI = Tn(kb, "WI", [128, 128])
    AM = Tn(kb, "AM", [128, 128]); AT = Tn(kb, "AT", [128, 128])
    QT = Tn(kb, "QT", [128, 128]); KT = Tn(kb, "KT", [128, 128])
    VX = Tn(kb, "VX", [128, 129]); TOT = Tn(kb, "TOT", [128, 129]); WV = Tn(kb, "WV", [128, 129])
    HN = Tn(kb, "HN", [128, 128]); SG = Tn(kb, "SG", [128, 128]); ST6 = Tn(kb, "ST6", [128, 2, 6])
    OUTC = Tn(kb, "OUTC", [128, 128], dma=True)
    CN = Tn(kb, "CN", [128, SB, 128], dma=True); CTS = Tn(kb, "CTS", [128, SB, 129])
    RA = Tn(kb, "RA", [128, SB, 128])

    class _V2:
        def __init__(self, ap, b):
            self.t = ap
            self.b = b

        def __getitem__(self, k):
            return self.t[k]
    ZQ = _V2(RA[:, :, :].rearrange("p a b -> p (a b)")[:, 0:SB * SP], RA.b)
    NNAT = Tn(kb, "NNAT", [SB, 4, 128], dma=True); NTH = Tn(kb, "NTH", [128, SB], dma=True)
    WCB = Tn(kb, "WCB", [128, 16]); DECD = Tn(kb, "DECD", [128, 16]); DECR = Tn(kb, "DECR", [128, 16])
    MSO = Tn(kb, "MSO", [SB, 4], dma=True)
    PREV = Tn(kb, "PREV", [128, 256]); PTT = Tn(kb, "PTT", [64, 4, 128])
    SPA = Tn(kb, "SPA", [128, 256], dma=True); SPB = Tn(kb, "SPB", [128, 256], dma=True)
    VN = Tn(kb, "VN", [128, 256], dma=True); VTMP = Tn(kb, "VTMP", [128, 256])

    o_bg, o_mh, o_sg, o_sb, o_ps, o_l1g, o_l1b = 0, 8, 520, 776, 1032, 1288, 1288 + D
    for (o, w, src) in [(o_bg, 8, E["b_gate"]), (o_mh, 512, E["mh_g"]), (o_sg, 256, E["sgu_g"]), (o_sb, 256, E["sgu_b"]),
                        (o_ps, 256, E["pscale"]), (o_l1g, D, E["ln1g"]), (o_l1b, D, E["ln1b"])]:
        kb.dma("sp", lambda q, o=o, w=w, src=src: q.dma_start(out=PRM[:, o:o + w], in_=src[l]), PRM.b, writes=[PRM.b])
    kb.dma("sp", lambda q: q.dma_start(out=WS[:, :, :], in_=E["w_sT"][l]), WS.b, writes=[WS.b])
    kb.dma("sp", lambda q: q.dma_start(out=BS[:, :], in_=E["b_sT"][l]), BS.b, writes=[BS.b])
    kb.dma("sp", lambda q: q.dma_start(out=WSs[:SP, :, :], in_=E["w_sS"][l]), WSs.b, writes=[WSs.b])
    kb.dma("sp", lambda q: q.dma_start(out=BSs[:SP, :], in_=E["b_sS"][l]), BSs.b, writes=[BSs.b])
    kb.dma("sp", lambda q: q.dma_start(out=WP[:, :, :], in_=E["w_pool"][l]), WP.b, writes=[WP.b])
    for g in range(4):
        op("dve", lambda e, g=g: e.tensor_tensor(out=WS[:, g, :], in0=WS[:, g, :], in1=C("triu"), op=ALU.mult),
           reads=[WS.b, CST.b], writes=[WS.b])
        op("dve", lambda e, g=g: e.tensor_tensor(out=WSs[:SP, g, :], in0=WSs[:SP, g, :], in1=C("tri_s", SP, SP), op=ALU.mult),
           reads=[WSs.b, CST.b], writes=[WSs.b])
    op("dve", lambda e: e.memset(CTX[:, :, :], 0.0), writes=[CTX.b])
    op("dve", lambda e: e.memset(MREP[:, :], 0.0), writes=[MREP.b])
    op("dve", lambda e: e.memset(VX[:, :], 1.0), writes=[VX.b])

    _ada(nc, kb, l, E, ADA, 128, E["cp"], 0, WCH, PM, H, HT, PT, badac, WCR)

    w_in_v = w_in[l]
    w_o_v = w_o[l]
    def mk_chunks(n):
        return [(c0, min(WCW, n - c0)) for c0 in range(0, n, WCW)]
    chunks = mk_chunks(IN_COLS)
    wctr = [0]

    def stream_mm(wview, c0, w, lhsT, P, evac):
        i = wctr[0] % 2
        wctr[0] += 1
        kb.dma("sp", lambda q: q.dma_start(out=WCH[i][:, :, :], in_=wview[c0 // WCW].rearrange("p (k j) -> p k j", k=8)), WCH[i].b, writes=[WCH[i].b])
        wr = WCR[i]
        if wctr[0] % 3 == 0:
            op("dve", lambda e: e.tensor_copy(out=wr[:, :, 0:w], in_=WCH[i][:, :, 0:w]), reads=[WCH[i].b], writes=[wr.b])
        else:
            op("act", lambda e: e.copy(out=wr[:, :, 0:w], in_=WCH[i][:, :, 0:w]), reads=[WCH[i].b], writes=[wr.b])
        for k in range(8):
            op("pe", lambda e, k=k: e.matmul(PM[i][:, 0:w], lhsT=lhsT[:, k, :], rhs=wr[:, k, 0:w],
                                              start=(k == 0), stop=(k == 7)),
               reads=[lhsT.b, wr.b], writes=[PM[i].b] if k in (0, 7) else [])
        evac(PM[i], i)

    for t in range(NT + 1):
        is_s = (t == NT)
        P = SP if is_s else 128
        ada = ADAs if is_s else ADA
        if is_s:
            _ada(nc, kb, l, E, ADAs, SP, E["cs"], 0, WCH, PM, H, HT, PT, badac, WCR)
        xt = X[:P, t, :]
        op("dve", lambda e: e.tensor_tensor(out=H[:P, :], in0=xt, in1=ada[:P, 1024:2048], op=ALU.mult),
           reads=[XB[t], ada.b], writes=[H.b])
        op("dve", lambda e: e.tensor_tensor(out=H[:P, :], in0=H[:P, :], in1=ada[:P, 0:1024], op=ALU.add),
           reads=[H.b, ada.b], writes=[H.b])
        _transpose8(kb, E, H, HT, PT, P)
        for (c0, w) in chunks:
            stream_mm(w_in_v, c0, w, HT, P,
                      lambda pm, i, c0=c0, w=w: op("act", lambda e: e.copy(out=PROJ[:P, c0:c0 + w], in_=pm[:P, 0:w]),
                                                   reads=[pm.b], writes=[PROJ.b]))
        tri = C("tri_s", SP, SP) if is_s else C("triu")
        negm = C("negm_s", SP, SP) if is_s else C("negm")
        selE = C("selend", SP, SP) if is_s else C("sel127")
        if is_s:
            kb.dma("sp", lambda q: q.dma_start(out=SMs[:SP, :], in_=E["sm"][l]), SMs.b, writes=[SMs.b])
            kb.dma("sp", lambda q: q.dma_start(out=NNAT[:, :, :], in_=E["snat"][l]), NNAT.b, writes=[NNAT.b])
        mtok = SMs if is_s else MREP
        op("dve", lambda e: e.tensor_tensor(out=SM[:P, 0:8], in0=PROJ[:P, 2048:2056], in1=PRM[:P, o_bg:o_bg + 8], op=ALU.add),
           reads=[PROJ.b, PRM.b], writes=[SM.b])
        op("dve", lambda e: e.scalar_tensor_tensor(out=SM[:P, 8:12], in0=SM[:P, 4:8], scalar=-1.0, in1=SM[:P, 4:8], op0=ALU.mult, op1=ALU.max),
           reads=[SM.b], writes=[SM.b])
        op("act", lambda e: e.activation(out=SM[:P, 12:16], in_=SM[:P, 8:12], func=AF.Exp, scale=-1.0), reads=[SM.b], writes=[SM.b])
        op("act", lambda e: e.activation(out=SM[:P, 12:16], in_=SM[:P, 12:16], func=AF.Ln, bias=1.0, scale=1.0),
           reads=[SM.b], writes=[SM.b])
        op("dve", lambda e: e.tensor_scalar_min(out=SM[:P, 16:20], in0=SM[:P, 4:8], scalar1=0.0), reads=[SM.b], writes=[SM.b])
        op("dve", lambda e: e.tensor_tensor(out=SM[:P, 16:20], in0=SM[:P, 16:20], in1=SM[:P, 12:16], op=ALU.subtract),
           reads=[SM.b], writes=[SM.b])
        op("pe", lambda e: e.matmul(PA[:P, 0:4], lhsT=tri, rhs=SM[:P, 16:20], start=True, stop=True),
           reads=[CST.b, SM.b], writes=[PA.b])
        op("act", lambda e: e.copy(out=SM[:P, 20:24], in_=PA[:P, 0:4]), reads=[PA.b], writes=[SM.b])
        op("dve", lambda e: e.tensor_tensor(out=SM[:P, 24:28], in0=SM[:P, 0:4], in1=SM[:P, 20:24], op=ALU.subtract),
           reads=[SM.b], writes=[SM.b])
        op("pe", lambda e: e.matmul(PA[:P, 8:12], lhsT=selE, rhs=SM[:P, 20:24], start=True, stop=True),
           reads=[CST.b, SM.b], writes=[PA.b])
        op("act", lambda e: e.copy(out=SM[:P, 28:32], in_=PA[:P, 8:12]), reads=[PA.b], writes=[SM.b])

        for hh in range(4):
            qs = PROJ[:P, hh * 128:(hh + 1) * 128]
            ks = PROJ[:P, 512 + hh * 128:512 + (hh + 1) * 128]
            vs = PROJ[:P, 1024 + hh * 128:1024 + (hh + 1) * 128]
            os_ = PROJ[:P, 1536 + hh * 128:1536 + (hh + 1) * 128]
            col = lambda c, hh=hh: SM[:P, c + hh:c + hh + 1]
            S1 = lambda c: SM[:P, c:c + 1]
            if is_s:
                kb.dma("sp", lambda q, hh=hh: q.dma_start(out=CN[:, :, :], in_=E["sC"][l, hh]), CN.b, writes=[CN.b])
                kb.dma("sp", lambda q, hh=hh: q.dma_start(out=NTH[:, :], in_=E["snT"][l, hh]), NTH.b, writes=[NTH.b])
                for j in range(4):
                    pt = PT[j % 2]
                    for jj in range(4):
                        b = j * 4 + jj
                        op("pe", lambda e, b=b, jj=jj, pt=pt: e.transpose(out=pt[:, jj * 128:(jj + 1) * 128], in_=CN[:, b, :],
                                                                       identity=C("ident")),
                           reads=[CN.b, CST.b], writes=[pt.b])
                    op("act", lambda e, j=j, pt=pt: e.copy(out=CTS[:, j * 4:(j + 1) * 4, 0:128],
                                                           in_=pt[:, :].rearrange("p (j c) -> p j c", j=4)),
                       reads=[pt.b], writes=[CTS.b])
                op("dve", lambda e: e.tensor_copy(out=CTS[:, :, 128:129], in_=NTH[:, :].unsqueeze(2)), reads=[NTH.b], writes=[CTS.b])
            op("dve", lambda e, hh=hh: e.tensor_scalar(out=DG[:P, :P], in0=C("ident", P, P), scalar1=col(24), scalar2=None,
                                                       op0=ALU.mult), reads=[SM.b, CST.b], writes=[DG.b])
            op("pe", lambda e: e.matmul(PB[:P, 0:P], lhsT=C("ones", P, P), rhs=DG[:P, :P], start=True, stop=True),
               reads=[DG.b, CST.b], writes=[PB.b])
            if is_s:
                op("dve", lambda e: e.tensor_tensor(out=DL[:P, :P], in0=PB[:P, 0:P], in1=C("negb_s", SP, SP), op=ALU.add),
                   reads=[PB.b, CST.b], writes=[DL.b])
                op("dve", lambda e: e.tensor_reduce(out=S1(32), in_=DL[:P, :P], axis=AX.X, op=ALU.max), reads=[DL.b], writes=[SM.b])
            else:
                op("dve", lambda e: e.tensor_reduce(out=S1(32), in_=PB[:P, 0:P], axis=AX.X, op=ALU.max), reads=[PB.b], writes=[SM.b])
            op("dve", lambda e, hh=hh: e.scalar_tensor_tensor(out=DL[:P, :P], in0=PB[:P, 0:P], scalar=col(20), in1=negm,
                                                              op0=ALU.add, op1=ALU.add),
               reads=[PB.b, SM.b, CST.b], writes=[DL.b])
            op("dve", lambda e: e.tensor_reduce(out=S1(33), in_=DL[:P, :P], axis=AX.X, op=ALU.max), reads=[DL.b], writes=[SM.b])
            op("dve", lambda e, hh=hh: e.tensor_tensor(out=S1(34), in0=col(20), in1=mtok[:P, hh:hh + 1], op=ALU.add),
               reads=[SM.b, mtok.b], writes=[SM.b])
            op("dve", lambda e: e.tensor_tensor(out=S1(35), in0=S1(34), in1=S1(33), op=ALU.max), reads=[SM.b], writes=[SM.b])
            op("dve", lambda e: e.tensor_scalar(out=S1(36), in0=S1(35), scalar1=-1.0, scalar2=None, op0=ALU.mult),
               reads=[SM.b], writes=[SM.b])
            op("act", lambda e: e.activation(out=WI[:P, :P], in_=DL[:P, :P], func=AF.Exp, bias=S1(36), scale=1.0),
               reads=[DL.b, SM.b], writes=[WI.b])
            op("act", lambda e: e.activation(out=S1(37), in_=S1(34), func=AF.Exp, bias=S1(36), scale=1.0), reads=[SM.b], writes=[SM.b])
            op("act", lambda e: e.activation(out=S1(38), in_=S1(36), func=AF.Exp), reads=[SM.b], writes=[SM.b])
            op("pe", lambda e: e.transpose(out=PC[:, 0:P], in_=qs, identity=C("ident", P, P)), reads=[PROJ.b, CST.b], writes=[PC.b])
            op("pe", lambda e: e.transpose(out=PC[:, 128:128 + P], in_=ks, identity=C("ident", P, P)), reads=[PROJ.b, CST.b], writes=[PC.b])
            op("act", lambda e: e.mul(out=QT[:, :P], in_=PC[:, 0:P], mul=128.0 ** -0.5), reads=[PC.b], writes=[QT.b])
            op("act", lambda e: e.copy(out=KT[:, :P], in_=PC[:, 128:128 + P]), reads=[PC.b], writes=[KT.b])
            op("pe", lambda e: e.matmul(PD[:P, 0:P], lhsT=QT[:, :P], rhs=KT[:, :P], start=True, stop=True),
               reads=[QT.b, KT.b], writes=[PD.b])
            op("dve", lambda e: e.tensor_tensor(out=AM[:P, :P], in0=WI[:P, :P], in1=PD[:P, 0:P], op=ALU.mult),
               reads=[WI.b, PD.b], writes=[AM.b])
            op("pe", lambda e: e.transpose(out=PB[:P, 128:128 + P], in_=AM[:P, :P], identity=C("ident", P, P)),
               reads=[AM.b, CST.b], writes=[PB.b])
            op("act", lambda e: e.copy(out=AT[:P, :P], in_=PB[:P, 128:128 + P]), reads=[PB.b], writes=[AT.b])
            op("pool", lambda e: e.tensor_copy(out=VX[:P, 0:128], in_=vs), reads=[PROJ.b], writes=[VX.b])
            op("pe", lambda e: e.matmul(PD[:P, 128:257], lhsT=AT[:P, :P], rhs=VX[:P, :], start=True, stop=True),
               reads=[AT.b, VX.b], writes=[PD.b])
            if is_s:
                op("pool", lambda e: e.memset(ZQ[:, :], 0.0), writes=[ZQ.b])
                for b in range(SB):
                    op("pool", lambda e, b=b: e.tensor_copy(out=ZQ[:, b * SP + b:(b + 1) * SP:SB], in_=QT[:, b:SP:SB]),
                       reads=[QT.b], writes=[ZQ.b])
                for b in range(SB):
                    op("pe", lambda e, b=b: e.matmul(PC[:P, 256:385], lhsT=ZQ[:, b * SP:(b + 1) * SP], rhs=CTS[:, b, :],
                                                     start=(b == 0), stop=(b == SB - 1)),
                       reads=[ZQ.b, CTS.b], writes=[PC.b] if b in (0, SB - 1) else [])
            else:
                op("pe", lambda e, hh=hh: e.matmul(PC[:P, 256:385], lhsT=QT[:, :P], rhs=CTX[:, hh, :], start=True, stop=True),
                   reads=[QT.b, CTX.b], writes=[PC.b])
            op("act", lambda e: e.activation(out=TOT[:P, :], in_=PC[:P, 256:385], func=AF.Identity, scale=S1(37)),
               reads=[PC.b, SM.b], writes=[TOT.b])
            op("dve", lambda e: e.tensor_tensor(out=TOT[:P, :], in0=TOT[:P, :], in1=PD[:P, 128:257], op=ALU.add),
               reads=[TOT.b, PD.b], writes=[TOT.b])
            op("dve", lambda e: e.scalar_tensor_tensor(out=S1(39), in0=TOT[:P, 128:129], scalar=-1.0, in1=TOT[:P, 128:129], op0=ALU.mult, op1=ALU.max),
               reads=[TOT.b], writes=[SM.b])
            op("dve", lambda e: e.tensor_tensor(out=S1(39), in0=S1(39), in1=S1(38), op=ALU.max), reads=[SM.b], writes=[SM.b])
            op("dve", lambda e: e.reciprocal(out=S1(40), in_=S1(39)), reads=[SM.b], writes=[SM.b])
            op("dve", lambda e: e.tensor_scalar(out=HN[:P, :], in0=TOT[:P, 0:128], scalar1=S1(40), scalar2=None, op0=ALU.mult),
               reads=[TOT.b, SM.b], writes=[HN.b])
            op("dve", lambda e: e.bn_stats(out=ST6[:P, 0, :], in_=HN[:P, :]), reads=[HN.b], writes=[ST6.b])
            op("dve", lambda e: e.bn_aggr(out=SM[:P, 41:43], in_=ST6[:P, 0, :]), reads=[ST6.b], writes=[SM.b])
            op("act", lambda e: e.activation(out=S1(43), in_=S1(42), func=AF.Ln, bias=EPSB[:P, 0:1], scale=1.0), reads=[SM.b, EPSB.b], writes=[SM.b])
            op("act", lambda e: e.activation(out=S1(44), in_=S1(43), func=AF.Exp, scale=-0.5), reads=[SM.b], writes=[SM.b])
            op("dve", lambda e: e.tensor_scalar(out=HN[:P, :], in0=HN[:P, :], scalar1=S1(41), scalar2=S1(44),
                                                op0=ALU.subtract, op1=ALU.mult), reads=[HN.b, SM.b], writes=[HN.b])
            op("dve", lambda e, hh=hh: e.tensor_tensor(out=HN[:P, :], in0=HN[:P, :],
                                                       in1=PRM[:P, o_mh + hh * 128:o_mh + (hh + 1) * 128], op=ALU.mult),
               reads=[HN.b, PRM.b], writes=[HN.b])
            op("act", lambda e: e.activation(out=SG[:P, :], in_=os_, func=AF.Exp, scale=-1.0), reads=[PROJ.b], writes=[SG.b])
            op("dve", lambda e: e.tensor_scalar_add(out=SG[:P, :], in0=SG[:P, :], scalar1=1.0), reads=[SG.b], writes=[SG.b])
            op("dve", lambda e: e.reciprocal(out=SG[:P, :], in_=SG[:P, :]), reads=[SG.b], writes=[SG.b])
            op("dve", lambda e, hh=hh: e.tensor_tensor(out=Y[:P, hh * 128:(hh + 1) * 128], in0=HN[:P, :], in1=SG[:P, :], op=ALU.mult),
               reads=[HN.b, SG.b], writes=[Y.b])
            op("dve", lambda e, hh=hh: e.tensor_tensor(out=S1(45), in0=mtok[:P, hh:hh + 1], in1=S1(32), op=ALU.max),
               reads=[SM.b, mtok.b], writes=[SM.b])
            op("dve", lambda e, hh=hh: e.tensor_tensor(out=S1(45), in0=S1(45), in1=col(28), op=ALU.add), reads=[SM.b], writes=[SM.b])
            op("dve", lambda e, hh=hh: e.tensor_tensor(out=S1(46), in0=col(28), in1=S1(45), op=ALU.subtract), reads=[SM.b], writes=[SM.b])
            op("act", lambda e, hh=hh: e.activation(out=S1(47), in_=col(24), func=AF.Exp, bias=S1(46), scale=1.0),
               reads=[SM.b], writes=[SM.b])
            op("act", lambda e, hh=hh: e.activation(out=S1(48), in_=mtok[:P, hh:hh + 1], func=AF.Exp, bias=S1(46), scale=1.0),
               reads=[SM.b, mtok.b], writes=[SM.b])
            if not is_s:
                op("dve", lambda e: e.tensor_scalar(out=WV[:P, :], in0=VX[:P, :], scalar1=S1(47), scalar2=None, op0=ALU.mult),
                   reads=[VX.b, SM.b], writes=[WV.b])
                op("pe", lambda e: e.matmul(PB[:, 256:385], lhsT=ks, rhs=WV[:P, :], start=True, stop=True),
                   reads=[PROJ.b, WV.b], writes=[PB.b])
                op("dve", lambda e, hh=hh: e.scalar_tensor_tensor(out=CTX[:, hh, :], in0=CTX[:, hh, :], scalar=S1(48), in1=PB[:, 256:385],
                                                                  op0=ALU.mult, op1=ALU.add),
                   reads=[CTX.b, SM.b, PB.b], writes=[CTX.b])
                op("dve", lambda e, hh=hh: e.tensor_copy(out=MREP[:, hh:hh + 1], in_=S1(45)), reads=[SM.b], writes=[MREP.b])
                if t == NT - 1:
                    op("pe", lambda e, hh=hh: e.transpose(out=PA[:, 128:256], in_=CTX[:, hh, 0:128], identity=C("ident")),
                       reads=[CTX.b, CST.b], writes=[PA.b])
                    op("act", lambda e: e.copy(out=OUTC[:, :], in_=PA[:, 128:256]), reads=[PA.b], writes=[OUTC.b])
                    out_dma(E["o_pC"][l, hh], OUTC[:, :], [OUTC.b])
                    out_dma(E["o_pn"][l, hh].rearrange("(k o) -> k o", o=1), CTX[:, hh, 128:129], [CTX.b])
                    if hh == 3:
                        out_dma(E["o_pm"][l:l + 1, :], MREP[0:1, :], [MREP.b])
            else:
                op("dve", lambda e: e.tensor_scalar(out=WCB[:P, :], in0=C("onehotB", SP), scalar1=S1(47), scalar2=None, op0=ALU.mult),
                   reads=[SM.b, CST.b], writes=[WCB.b])
                op("dve", lambda e: e.tensor_tensor(out=RA[:P, :, :], in0=vs.unsqueeze(1).broadcast_to([P, SB, 128]),
                                                    in1=WCB[:P, :].unsqueeze(2).broadcast_to([P, SB, 128]), op=ALU.mult),
                   reads=[PROJ.b, WCB.b], writes=[RA.b])
                op("dve", lambda e: e.tensor_scalar(out=DECD[:P, :], in0=C("onehot0", SP), scalar1=S1(48), scalar2=None, op0=ALU.mult),
                   reads=[SM.b, CST.b], writes=[DECD.b])
                op("pe", lambda e: e.matmul(PA[:, 16:32], lhsT=C("ones", SP, 128), rhs=DECD[:P, :], start=True, stop=True),
                   reads=[DECD.b, CST.b], writes=[PA.b])
                op("act", lambda e: e.copy(out=DECR[:, :], in_=PA[:, 16:32]), reads=[PA.b], writes=[DECR.b])
                for b in range(SB):
                    pq = [PA, PB, PC, PD][b % 4]
                    op("pe", lambda e, b=b, pq=pq: e.matmul(pq[:, 384:512], lhsT=RA[:P, b, :], rhs=ks, start=True, stop=True),
                       reads=[RA.b, PROJ.b], writes=[pq.b])
                    op("dve", lambda e, b=b, pq=pq: e.scalar_tensor_tensor(out=CN[:, b, :], in0=CN[:, b, :], scalar=DECR[:, b:b + 1],
                                                                           in1=pq[:, 384:512], op0=ALU.mult, op1=ALU.add),
                       reads=[CN.b, DECR.b, pq.b], writes=[CN.b])
                out_dma(E["o_nC"][l, :, hh].rearrange("b v k -> v b k"), CN[:, :, :], [CN.b])
                op("pe", lambda e: e.matmul(PA[:SB, 32:160], lhsT=WCB[:P, :], rhs=ks, start=True, stop=True),
                   reads=[WCB.b, PROJ.b], writes=[PA.b])
                op("dve", lambda e, hh=hh: e.scalar_tensor_tensor(out=NNAT[:, hh, :], in0=NNAT[:, hh, :], scalar=SM[:SB, 48:49],
                                                                  in1=PA[:SB, 32:160], op0=ALU.mult, op1=ALU.add),
                   reads=[NNAT.b, SM.b, PA.b], writes=[NNAT.b])
                op("dve", lambda e, hh=hh: e.tensor_copy(out=MSO[:, hh:hh + 1], in_=SM[:SB, 45:46]), reads=[SM.b], writes=[MSO.b])
                if hh == 3:
                    out_dma(E["o_nn"][l], NNAT[:, :, :], [NNAT.b])
                    out_dma(E["o_nm"][l], MSO[:, :], [MSO.b])

        vsv = PROJ[:P, 2312:2568].rearrange("p (g d) -> p g d", g=4)
        op("dve", lambda e: e.tensor_reduce(out=SM[:P, 50:54], in_=vsv, axis=AX.X, op=ALU.add), reads=[PROJ.b], writes=[SM.b])
        op("dve", lambda e: e.tensor_scalar(out=SM[:P, 50:54], in0=SM[:P, 50:54], scalar1=1.0 / 64, scalar2=None, op0=ALU.mult),
           reads=[SM.b], writes=[SM.b])
        op("dve", lambda e: e.tensor_tensor(out=VN[:P, :].rearrange("p (g d) -> p g d", g=4), in0=vsv,
                                            in1=SM[:P, 50:54].unsqueeze(2).broadcast_to([P, 4, 64]), op=ALU.subtract),
           reads=[PROJ.b, SM.b], writes=[VN.b])
        op("pool", lambda e: e.tensor_tensor(out=VTMP[:P, :], in0=VN[:P, :], in1=VN[:P, :], op=ALU.mult), reads=[VN.b], writes=[VTMP.b])
        op("dve", lambda e: e.tensor_reduce(out=SM[:P, 54:58], in_=VTMP[:P, :].rearrange("p (g d) -> p g d", g=4), axis=AX.X, op=ALU.add),
           reads=[VTMP.b], writes=[SM.b])
        op("act", lambda e: e.activation(out=SM[:P, 54:58], in_=SM[:P, 54:58], func=AF.Ln, bias=EPSB[:P, 0:1], scale=1.0 / 64),
           reads=[SM.b, EPSB.b], writes=[SM.b])
        op("act", lambda e: e.activation(out=SM[:P, 58:62], in_=SM[:P, 54:58], func=AF.Exp, scale=-0.5), reads=[SM.b], writes=[SM.b])
        op("dve", lambda e: e.tensor_tensor(out=VN[:P, :].rearrange("p (g d) -> p g d", g=4), in0=VN[:P, :].rearrange("p (g d) -> p g d", g=4),
                                            in1=SM[:P, 58:62].unsqueeze(2).broadcast_to([P, 4, 64]), op=ALU.mult),
           reads=[VN.b, SM.b], writes=[VN.b])
        op("pool", lambda e: e.tensor_tensor(out=VN[:P, :], in0=VN[:P, :], in1=PRM[:P, o_sg:o_sg + 256], op=ALU.mult),
           reads=[VN.b, PRM.b], writes=[VN.b])
        op("pool", lambda e: e.tensor_tensor(out=VN[:P, :], in0=VN[:P, :], in1=PRM[:P, o_sb:o_sb + 256], op=ALU.add),
           reads=[VN.b, PRM.b], writes=[VN.b])
        wsl = WSs if is_s else WS
        bsl = BSs if is_s else BS
        for g in range(4):
            op("pe", lambda e, g=g: e.matmul(PC[:P, g * 64:(g + 1) * 64], lhsT=wsl[:P, g, :P], rhs=VN[:P, g * 64:(g + 1) * 64],
                                             start=True, stop=True), reads=[wsl.b, VN.b], writes=[PC.b])
        for g in range(4):
            op("dve", lambda e, g=g: e.scalar_tensor_tensor(out=Y[:P, 512 + g * 64:512 + (g + 1) * 64], in0=PC[:P, g * 64:(g + 1) * 64],
                                                            scalar=bsl[:P, g:g + 1], in1=PROJ[:P, 2056 + g * 64:2056 + (g + 1) * 64],
                                                            op0=ALU.add, op1=ALU.mult),
               reads=[PC.b, bsl.b, PROJ.b], writes=[Y.b])
        if is_s:
            for tq in range(ST):
                out_dma(E["o_nv"][l][:, tq, :], VN[tq * SB:(tq + 1) * SB, :], [VN.b])

        pin = lambda g: PROJ[:P, 2568 + g * 64:2568 + (g + 1) * 64]
        if is_s:
            kb.dma("sp", lambda q: q.dma_start(out=SPA[:, :], in_=E["spA"][l]), SPA.b, writes=[SPA.b])
            kb.dma("sp", lambda q: q.dma_start(out=SPB[:112, :], in_=E["spB"][l]), SPB.b, writes=[SPB.b])
            for g in range(4):
                op("pe", lambda e, g=g: e.matmul(PA[:64, g * 128:g * 128 + P], lhsT=SPA[:, g * 64:(g + 1) * 64], rhs=Cg("bsA", g, 128, SP, SP),
                                                 start=True, stop=False), reads=[SPA.b, CST.b], writes=[PA.b])
                op("pe", lambda e, g=g: e.matmul(PA[:64, g * 128:g * 128 + P], lhsT=SPB[:112, g * 64:(g + 1) * 64], rhs=Cg("bsB", g, 112, SP, SP),
                                                 start=False, stop=False), reads=[SPB.b, CST.b], writes=[])
                op("pe", lambda e, g=g: e.matmul(PA[:64, g * 128:g * 128 + P], lhsT=pin(g), rhs=Cg("bsC", g, SP, SP, SP),
                                                 start=False, stop=True), reads=[PROJ.b, CST.b], writes=[PA.b])
            npv = E["o_np"][l].rearrange("b r c -> r b c")
            for r in range(4):
                out_dma(npv[r], SPA[64 + r * SB:64 + (r + 1) * SB, :], [SPA.b])
            for r in range(7):
                out_dma(npv[4 + r], SPB[r * SB:(r + 1) * SB, :], [SPB.b])
            for r in range(4):
                out_dma(npv[11 + r], PROJ[r * SB:(r + 1) * SB, 2568:2824], [PROJ.b])
        else:
            for g in range(4):
                band = Cg("bandc0" if t == 0 else "bandc", g, 128, 128, 128)
                op("pe", lambda e, g=g, band=band: e.matmul(PA[:64, g * 128:(g + 1) * 128], lhsT=pin(g), rhs=band, start=True, stop=(t == 0)),
                   reads=[PROJ.b, CST.b], writes=[PA.b])
                if t > 0:
                    op("pe", lambda e, g=g: e.matmul(PA[:64, g * 128:(g + 1) * 128], lhsT=PREV[:, g * 64:(g + 1) * 64],
                                                     rhs=Cg("bandp", g, 128, 128, 128), start=False, stop=True),
                       reads=[PREV.b, CST.b], writes=[PA.b])
            if t < NT - 1:
                op("pool", lambda e: e.tensor_copy(out=PREV[:, :], in_=PROJ[:, 2568:2824]), reads=[PROJ.b], writes=[PREV.b])
            else:
                out_dma(E["o_pp"][l], PROJ[113:128, 2568:2824], [PROJ.b])
        op("act", lambda e: e.copy(out=PTT[:, :, :P], in_=PA[:64, :].rearrange("p (g c) -> p g c", g=4)[:, :, :P]),
           reads=[PA.b], writes=[PTT.b])
        for g in range(4):
            op("pe", lambda e, g=g: e.matmul(PB[:P, g * 64:(g + 1) * 64], lhsT=PTT[:, g, :P], rhs=WP[:, g, :], start=True, stop=True),
               reads=[PTT.b, WP.b], writes=[PB.b])
        op("dve", lambda e: e.tensor_tensor(out=Y[:P, 768:1024], in0=PB[:P, 0:256], in1=PRM[:P, o_ps:o_ps + 256], op=ALU.mult),
           reads=[PB.b, PRM.b], writes=[Y.b])

        _transpose8(kb, E, Y, HT, PT, P)
        for (c0, w) in mk_chunks(D):
            stream_mm(w_o_v, c0, w, HT, P,
                      lambda pm, i, c0=c0, w=w: op("dve", lambda e: e.tensor_tensor(out=H[:P, c0:c0 + w], in0=pm[:P, 0:w],
                                                                                    in1=ada[:P, 2048 + c0:2048 + c0 + w], op=ALU.mult),
                                                   reads=[pm.b, ada.b], writes=[H.b]))
        _resid_ln(kb, X, XB[t], t, P, H, SM, ST6, PRM, o_l1g, o_l1b, EPSB)


def _resid_ln(kb, X, xb, t, P, Z, SM, ST6, PRM, og, ob, EPSB):
    op = kb.op
    xt = X[:P, t, :]
    op("dve", lambda e: e.scalar_tensor_tensor(out=Z[:P, :], in0=xt, scalar=ALPHA, in1=Z[:P, :], op0=ALU.mult, op1=ALU.add),
       reads=[xb, Z.b], writes=[Z.b])
    op("dve", lambda e: e.bn_stats(out=ST6[:P, 0, :], in_=Z[:P, 0:512]), reads=[Z.b], writes=[ST6.b])
    op("dve", lambda e: e.bn_stats(out=ST6[:P, 1, :], in_=Z[:P, 512:1024]), reads=[Z.b], writes=[ST6.b])
    op("dve", lambda e: e.bn_aggr(out=SM[:P, 41:43], in_=ST6[:P, :, :].rearrange("p a b -> p (a b)")), reads=[ST6.b], writes=[SM.b])
    op("act", lambda e: e.activation(out=SM[:P, 43:44], in_=SM[:P, 42:43], func=AF.Ln, bias=EPSB[:P, 0:1], scale=1.0), reads=[SM.b, EPSB.b], writes=[SM.b])
    op("act", lambda e: e.activation(out=SM[:P, 44:45], in_=SM[:P, 43:44], func=AF.Exp, scale=-0.5), reads=[SM.b], writes=[SM.b])
    op("dve", lambda e: e.tensor_scalar(out=Z[:P, :], in0=Z[:P, :], scalar1=SM[:P, 41:42], scalar2=SM[:P, 44:45],
                                        op0=ALU.subtract, op1=ALU.mult), reads=[Z.b, SM.b], writes=[Z.b])
    op("pool", lambda e: e.tensor_tensor(out=Z[:P, :], in0=Z[:P, :], in1=PRM[:P, og:og + D], op=ALU.mult), reads=[Z.b, PRM.b], writes=[Z.b])
    op("dve", lambda e: e.tensor_tensor(out=xt, in0=Z[:P, :], in1=PRM[:P, ob:ob + D], op=ALU.add), reads=[Z.b, PRM.b], writes=[xb])


def _phase2(nc, kb, l, E):
    C, CST, X, XB = E["C"], E["CST"], E["X"], E["XB"]
    ADA = Tn(kb, "ADA2", [128, 3072]); ADAs = ADA
    WCH = [Tn(kb, "WCHb%d" % i, [128, 8, 128], dma=True) for i in range(2)]
    badac = [Tn(kb, "badab%d" % i, [128, 256], dma=True) for i in range(2)]
    Hs = [Tn(kb, "H2_%d" % i, [128, D], dma=True) for i in range(2)]
    HT = Tn(kb, "H2T", [128, 8, 128])
    PRM = Tn(kb, "PRM2", [128, 2 * D], dma=True)
    KTS = Tn(kb, "KTS", [128, 16, 128], dma=True)
    S0 = Tn(kb, "S0", [128, 2048]); S1_ = Tn(kb, "S1", [128, 2048]); S2 = Tn(kb, "S2", [128, 256])
    TOPS = Tn(kb, "TOPS", [128, 16, 16]); IDXU = Tn(kb, "IDXU", [128, 16, 16], U32); IDXF = Tn(kb, "IDXF", [128, 16, 16])
    CV = Tn(kb, "CV", [128, 8, 16]); CPOS = Tn(kb, "CPOS", [128, 8, 16], U32)
    PAU = Tn(kb, "PAU", [128, 8, 16], U32); PBU = Tn(kb, "PBU", [128, 8, 16], U32)
    PAF = Tn(kb, "PAF", [128, 8, 16]); PBF = Tn(kb, "PBF", [128, 8, 16])
    I1 = Tn(kb, "I1", [128, 128]); I2 = Tn(kb, "I2", [128, 128])
    IDXs = [Tn(kb, "IDX%d" % i, [128, 128], I32) for i in range(2)]
    GATEs = [Tn(kb, "GATE%d" % i, [128, 128]) for i in range(2)]
    ACTV = Tn(kb, "ACTV", [128, 128]); COEF = Tn(kb, "COEF", [128, 128])
    SMf = Tn(kb, "SM2f", [128, 16]); SM = Tn(kb, "SM2", [128, 64]); ST6 = Tn(kb, "ST62", [128, 2, 6])
    NB = NBUF
    UB = [Tn(kb, "UB%d" % i, [128, 2 * D], BF16, dma=True) for i in range(NB)]
    CB = [Tn(kb, "CB%d" % i, [128, 2 * D], BF16, dma=True) for i in range(2)]
    WCAB = Tn(kb, "WCAB", [128, 8, 256], dma=True)
    PUVB = kb.buf("puvb", dma=True)
    ACTB = [kb.buf("actv%d" % i) for i in range(NB)]
    COEFB = [kb.buf("coef%d" % i) for i in range(NB)]
    COEF2 = Tn(kb, "COEF2", [128, 128])

    class _View:
        def __init__(self, ap, b):
            self.t = ap
            self.b = b

        def __getitem__(self, k):
            return self.t[k]
    WCA = [WCAB]
    PT = [Tn(kb, "PTb%d" % i, [128, 512], psum=True) for i in range(2)]
    PQ = [Tn(kb, "PQ%d" % i, [128, 512], psum=True) for i in range(2)]
    PM = PQ
    ACCP = [Tn(kb, "ACCP%d" % i, [128, 512], psum=True) for i in range(2)]
    JUNK = Tn(kb, "JUNK", [128, D])
    ACC = JUNK
    DGB = [Tn(kb, "DGB%d" % i, [128, 128], BF16) for i in range(3)]
    PS = [Tn(kb, "PS%d" % i, [128, 512], psum=True) for i in range(2)]

    kb.dma("sp", lambda q: q.dma_start(out=PRM[:, 0:D], in_=E["ln2g"][l]), PRM.b, writes=[PRM.b])
    kb.dma("sp", lambda q: q.dma_start(out=PRM[:, D:2 * D], in_=E["ln2b"][l]), PRM.b, writes=[PRM.b])
    kb.dma("sp", lambda q: q.dma_start(out=KTS[:, :, :], in_=E["keysT"][l]), KTS.b, writes=[KTS.b])
    wpq_v = E["w_pq"][l]

    tab32 = E["puv"][l]
    tab = E["puvb"][l]
    stg = [S0, S1_]
    for blk in range(NEXP // 128):
        sg = stg[blk % 2]
        cb = CB[blk % 2]
        kb.dma("sp", lambda q: q.dma_start(out=sg[:, :], in_=tab32[blk * 128:(blk + 1) * 128, :]), WCAB.b, writes=[sg.b])
        if blk % 2 == 0:
            kb.op("act", lambda e: e.copy(out=cb[:, :], in_=sg[:, :]), reads=[sg.b], writes=[cb.b])
        else:
            kb.op("dve", lambda e: e.tensor_copy(out=cb[:, :], in_=sg[:, :]), reads=[sg.b], writes=[cb.b])
        kb.dma("sp", lambda q: q.dma_start(out=tab[blk * 128:(blk + 1) * 128, :], in_=cb[:, :]), PUVB, reads=[cb.b], writes=[PUVB])
    wctr = [0]

    def make_front(t, sset):
        items = []

        def op(e, fn, reads=(), writes=()):
            items.append(("op", e, _bind(fn), None, tuple(reads), tuple(writes)))

        def dma(e, fn, owner, reads=(), writes=()):
            items.append(("dma", e, _bind(fn), owner, tuple(reads), tuple(writes)))
        is_s = (t == NT)
        P = SP if is_s else 128
        ada = ADAs if is_s else ADA
        H = Hs[sset]; IDX = IDXs[sset]; GATE = GATEs[sset]
        QT = S0
        xt = X[:P, t, :]
        op("dve", lambda e: e.tensor_tensor(out=H[:P, :], in0=xt, in1=ada[:P, 1024:2048], op=ALU.mult), reads=[XB[t], ada.b], writes=[H.b])
        op("dve", lambda e: e.tensor_tensor(out=H[:P, :], in0=H[:P, :], in1=ada[:P, 0:1024], op=ALU.add), reads=[H.b, ada.b], writes=[H.b])
        _transpose8(kb, E, H, HT, PT, P, op=op)
        def wdma(c):
            i = c % 2
            dma("sp", lambda q: q.dma_start(out=WCH[i][:, :, :], in_=wpq_v[c].rearrange("p (k j) -> p k j", k=8)), WCH[i].b, writes=[WCH[i].b])
        wdma(0)
        for c in range(16):
            i = c % 2
            j = c % 2
            if c + 1 < 16:
                wdma(c + 1)
            for k in range(8):
                op("pe", lambda e: e.matmul(PQ[j][:, 0:P], lhsT=WCH[i][:, k, :], rhs=HT[:, k, :P], start=(k == 0), stop=(k == 7)),
                   reads=[WCH[i].b, HT.b], writes=[PQ[j].b] if k in (0, 7) else [])
            op("act", lambda e: e.copy(out=QT[:, c * 128:c * 128 + P], in_=PQ[j][:, 0:P]), reads=[PQ[j].b], writes=[QT.b])
        for c4 in range(4):
            ps = PS[c4 % 2]
            for j in range(4):
                c = c4 * 4 + j
                op("pe", lambda e: e.matmul(ps[:P, j * 128:(j + 1) * 128], lhsT=QT[:, c * 128:c * 128 + P], rhs=KTS[:, c, :], start=True, stop=True),
                   reads=[QT.b, KTS.b], writes=[ps.b])
            op("act", lambda e: e.copy(out=S1_[:P, c4 * 512:(c4 + 1) * 512], in_=ps[:P, :]), reads=[ps.b], writes=[S1_.b])
        for c in range(16):
            sc = S1_[:P, c * 128:(c + 1) * 128]
            wk = S2[:P, 0:128]
            op("dve", lambda e: e.max(out=TOPS[:P, c, 0:8], in_=sc), reads=[S1_.b], writes=[TOPS.b])
            op("dve", lambda e: e.max_index(out=IDXU[:P, c, 0:8], in_max=TOPS[:P, c, 0:8], in_values=sc), reads=[S1_.b, TOPS.b], writes=[IDXU.b])
            op("dve", lambda e: e.match_replace(out=wk, in_to_replace=TOPS[:P, c, 0:8], in_values=sc, imm_value=NEG), reads=[S1_.b, TOPS.b], writes=[S2.b])
            op("dve", lambda e: e.max(out=TOPS[:P, c, 8:16], in_=wk), reads=[S2.b], writes=[TOPS.b])
            op("dve", lambda e: e.max_index(out=IDXU[:P, c, 8:16], in_max=TOPS[:P, c, 8:16], in_values=wk), reads=[S2.b, TOPS.b], writes=[IDXU.b])
        op("dve", lambda e: e.tensor_copy(out=IDXF[:P, :, :], in_=IDXU[:P, :, :]), reads=[IDXU.b], writes=[IDXF.b])
        tv = TOPS[:P, :, :].rearrange("p (h two) k -> p h two k", two=2)
        CAND = S0
        op("dve", lambda e: e.tensor_tensor(out=CAND[:P, :].rearrange("p (h a b) -> p h a b", h=8, a=16),
                                            in0=tv[:, :, 0, :].unsqueeze(3).broadcast_to([P, 8, 16, 16]),
                                            in1=tv[:, :, 1, :].unsqueeze(2).broadcast_to([P, 8, 16, 16]), op=ALU.add),
           reads=[TOPS.b], writes=[S0.b])
        for h in range(8):
            cd = CAND[:P, h * 256:(h + 1) * 256]
            wk = S2[:P, 0:256]
            op("dve", lambda e: e.max(out=CV[:P, h, 0:8], in_=cd), reads=[S0.b], writes=[CV.b])
            op("dve", lambda e: e.max_index(out=CPOS[:P, h, 0:8], in_max=CV[:P, h, 0:8], in_values=cd), reads=[S0.b, CV.b], writes=[CPOS.b])
            op("dve", lambda e: e.match_replace(out=wk, in_to_replace=CV[:P, h, 0:8], in_values=cd, imm_value=NEG), reads=[S0.b, CV.b], writes=[S2.b])
            op("dve", lambda e: e.max(out=CV[:P, h, 8:16], in_=wk), reads=[S2.b], writes=[CV.b])
            op("dve", lambda e: e.max_index(out=CPOS[:P, h, 8:16], in_max=CV[:P, h, 8:16], in_values=wk), reads=[S2.b, CV.b], writes=[CPOS.b])
        op("dve", lambda e: e.tensor_single_scalar(out=PAU[:P, :, :], in_=CPOS[:P, :, :], scalar=4, op=ALU.logical_shift_right), reads=[CPOS.b], writes=[PAU.b])
        op("dve", lambda e: e.tensor_single_scalar(out=PBU[:P, :, :], in_=CPOS[:P, :, :], scalar=15, op=ALU.bitwise_and), reads=[CPOS.b], writes=[PBU.b])
        op("dve", lambda e: e.tensor_copy(out=PAF[:P, :, :], in_=PAU[:P, :, :]), reads=[PAU.b], writes=[PAF.b])
        op("dve", lambda e: e.tensor_copy(out=PBF[:P, :, :], in_=PBU[:P, :, :]), reads=[PBU.b], writes=[PBF.b])
        iv = IDXF[:P, :, :].rearrange("p (h two) k -> p h two k", two=2)
        io16 = C("iota16", P).unsqueeze(1).unsqueeze(1).broadcast_to([P, 8, 16, 16])
        for (pf, half, dst) in [(PAF, 0, I1), (PBF, 1, I2)]:
            eq = S1_[:P, :].rearrange("p (h k a) -> p h k a", h=8, k=16)
            op("dve", lambda e: e.tensor_tensor(out=eq, in0=pf[:P, :, :].unsqueeze(3).broadcast_to([P, 8, 16, 16]), in1=io16, op=ALU.is_equal),
               reads=[pf.b, CST.b], writes=[S1_.b])
            op("dve", lambda e: e.tensor_tensor(out=eq, in0=eq, in1=iv[:, :, half, :].unsqueeze(2).broadcast_to([P, 8, 16, 16]), op=ALU.mult),
               reads=[S1_.b, IDXF.b], writes=[S1_.b])
            op("dve", lambda e: e.tensor_reduce(out=dst[:P, :].rearrange("p (h k) -> p h k", h=8), in_=eq, axis=AX.X, op=ALU.add),
               reads=[S1_.b], writes=[dst.b])
        op("dve", lambda e: e.scalar_tensor_tensor(out=I1[:P, :], in0=I1[:P, :], scalar=128.0, in1=I2[:P, :], op0=ALU.mult, op1=ALU.add),
           reads=[I1.b, I2.b], writes=[I1.b])
        op("dve", lambda e: e.tensor_copy(out=IDX[:P, :], in_=I1[:P, :]), reads=[I1.b], writes=[IDX.b])
        gv = GATE[:P, :].rearrange("p (h k) -> p h k", h=8)
        op("dve", lambda e: e.tensor_tensor(out=gv, in0=CV[:P, :, :], in1=CV[:P, :, 0:1].broadcast_to([P, 8, 16]), op=ALU.subtract),
           reads=[CV.b], writes=[GATE.b])
        op("act", lambda e: e.activation(out=GATE[:P, :], in_=GATE[:P, :], func=AF.Exp), reads=[GATE.b], writes=[GATE.b])
        op("dve", lambda e: e.tensor_reduce(out=SMf[:P, 0:8], in_=gv, axis=AX.X, op=ALU.add), reads=[GATE.b], writes=[SMf.b])
        op("dve", lambda e: e.reciprocal(out=SMf[:P, 8:16], in_=SMf[:P, 0:8]), reads=[SMf.b], writes=[SMf.b])
        op("dve", lambda e: e.tensor_tensor(out=gv, in0=gv, in1=SMf[:P, 8:16].unsqueeze(2).broadcast_to([P, 8, 16]), op=ALU.mult),
           reads=[GATE.b, SMf.b], writes=[GATE.b])
        return items

    def run_items(items, n=None):
        n = len(items) if n is None else min(n, len(items))
        for _ in range(n):
            kind, e, fn, owner, reads, writes = items.pop(0)
            if kind == "op":
                kb.op(e, fn, reads=reads, writes=writes, bound=True)
            else:
                kb.dma(e, fn, owner, reads=reads, writes=writes, bound=True)

    def back(t, sset, nxt):
        op = kb.op
        is_s = (t == NT)
        P = SP if is_s else 128
        ada = ADAs if is_s else ADA
        H = Hs[sset]; IDX = IDXs[sset]; GATE = GATEs[sset]
        per = 0 if not nxt else (len(nxt) + 119) // 120

        def axpy(s):
            b = s % NB
            dg = DGB[s % 3]
            op("act", lambda e: e.activation(out=COEF2[:P, s:s + 1], in_=COEF[:P, s:s + 1], func=AF.Identity, scale=GATE[:P, s:s + 1]),
               reads=[COEFB[b], GATE.b], writes=[COEF2.b])
            op("act", lambda e: e.activation(out=dg[:P, :P], in_=C("ident", P, P), func=AF.Identity, scale=COEF2[:P, s:s + 1]),
               reads=[COEF2.b, CST.b], writes=[dg.b])
            for hf in range(2):
                op("pe", lambda e: e.matmul(ACCP[hf][:P, :], lhsT=dg[:P, :P], rhs=UB[b][:P, D + hf * 512:D + (hf + 1) * 512], start=(s == 0), stop=(s == 127)),
                   reads=[dg.b, UB[b].b], writes=[ACCP[hf].b] if s in (0, 127) else [])

        for s_ in range(128):
            b = s_ % NB
            kb.dma("pool", lambda q: q.indirect_dma_start(out=UB[b][:P, :], out_offset=None, in_=tab,
                                                          in_offset=bass.IndirectOffsetOnAxis(ap=IDX[:P, s_:s_ + 1], axis=0)),
                   UB[b].b, reads=[IDX.b, PUVB], writes=[UB[b].b])
            op("dve", lambda e: e.scalar_tensor_tensor(out=JUNK[:P, :], in0=UB[b][:P, 0:D], scalar=1.0, in1=H[:P, :],
                                                       op0=ALU.mult, op1=ALU.mult, accum_out=ACTV[:P, s_:s_ + 1]),
               reads=[UB[b].b, H.b], writes=[JUNK.b, ACTB[b]])
            op("act", lambda e: e.activation(out=COEF[:P, s_:s_ + 1], in_=ACTV[:P, s_:s_ + 1], func=AF.Gelu), reads=[ACTB[b]], writes=[COEFB[b]])
            if s_ >= 1:
                axpy(s_ - 1)
            if nxt:
                run_items(nxt, per)
        axpy(127)
        if nxt:
            run_items(nxt)
        for hf in range(2):
            op("dve", lambda e: e.tensor_tensor(out=ACC[:P, hf * 512:(hf + 1) * 512], in0=ACCP[hf][:P, :], in1=ada[:P, 2048 + hf * 512:2048 + (hf + 1) * 512],
                                                op=ALU.mult), reads=[ACCP[hf].b, ada.b], writes=[ACC.b])
        _resid_ln(kb, X, XB[t], t, P, ACC, SM, ST6, PRM, 0, D, E["EPSB"])

    _ada(nc, kb, l, E, ADA, 128, E["cp"], 3072, WCA, PM, Hs[0], HT, PT, badac)
    run_items(make_front(0, 0))
    for t in range(NT + 1):
        nxt = make_front(t + 1, (t + 1) % 2) if t + 1 < NT else None
        back(t, t % 2, nxt)
        if t + 1 == NT:
            _ada(nc, kb, l, E, ADAs, SP, E["cs"], 3072, WCA, PM, Hs[NT % 2], HT, PT, badac)
            run_items(make_front(NT, NT % 2))


_CACHE = {}


def _chunked(w, cw):
    L, K, n = w.shape
    nch = (n + cw - 1) // cw
    wp = np.zeros((L, K, nch * cw), np.float32)
    wp[:, :, :n] = w
    wp = wp.reshape(L, 8, 128, nch, cw).transpose(0, 3, 2, 1, 4)
    return np.ascontiguousarray(wp.reshape(L, nch, 128, 8 * cw))


def _rep(a, P=128):
    return np.ascontiguousarray(np.broadcast_to(a[:, None, :], (a.shape[0], P, a.shape[1])))


def make_in_maps(inp, cpack):
    f = lambda a: np.ascontiguousarray(np.asarray(a, dtype=np.float32))
    shared = {
        "w_ada": _chunked(f(inp["w_ada"]), WCW), "b_ada": _rep(f(inp["b_ada"])), "w_in": _chunked(f(inp["w_in"]), WCW),
        "b_gate": _rep(f(inp["b_gate"])),
        "mh_g": _rep(f(inp["mh_g"])), "sgu_g": _rep(f(inp["sgu_g"])), "sgu_b": _rep(f(inp["sgu_b"])),
        "pscale": _rep(f(inp["pool_scale"])),
        "w_sT": f(np.asarray(inp["w_s"]).transpose(0, 3, 1, 2)),
        "b_sT": f(np.asarray(inp["b_s"]).transpose(0, 2, 1)),
        "w_pool": f(np.asarray(inp["w_pool"]).transpose(0, 2, 1, 3)),
        "w_o": _chunked(f(inp["w_o"]), WCW), "ln1g": _rep(f(inp["ln1_g"])), "ln1b": _rep(f(inp["ln1_b"])),
        "ln2g": _rep(f(inp["ln2_g"])), "ln2b": _rep(f(inp["ln2_b"])), "w_pq": _chunked(f(inp["w_pq"]), 128),
        "keysT": f(np.asarray(inp["peer_keys"]).transpose(0, 4, 1, 2, 3).reshape(DEPTH, 128, 16, 128)),
        "cst": cpack[0], "cst1": cpack[1],
    }
    ws4 = np.asarray(inp["w_s"])[:, :, :ST, :ST]
    wsS = np.repeat(np.repeat(ws4.transpose(0, 3, 1, 2), SB, axis=1), SB, axis=3)
    shared["w_sS"] = f(wsS)
    bs4 = np.asarray(inp["b_s"])[:, :, :ST]
    shared["b_sS"] = f(np.repeat(bs4.transpose(0, 2, 1), SB, axis=1))
    for l in range(DEPTH):
        shared["puv%d" % l] = np.ascontiguousarray(
            np.concatenate([np.asarray(inp["peer_u"])[l], np.asarray(inp["peer_v"])[l]], axis=1), dtype=np.float32)
    maps = []
    for c in range(NCORES):
        bs = slice(c * SB, (c + 1) * SB)
        m = dict(shared)
        m["xp"] = f(np.asarray(inp["x_prompt"])[c])
        m["xs"] = f(np.asarray(inp["x_sample"])[bs].transpose(1, 0, 2).reshape(SP, D))
        m["cp"] = f(np.broadcast_to(np.asarray(inp["c_prompt"])[c][None, :], (128, D)))
        m["cs"] = f(np.tile(np.asarray(inp["c_sample"])[bs], (ST, 1)))
        sCc = np.asarray(inp["state_mlstm_C"])[:, bs]
        m["sC"] = f(sCc.transpose(0, 2, 3, 1, 4))
        snc = np.asarray(inp["state_mlstm_n"])[:, bs]
        m["snat"] = f(snc)
        m["snT"] = f(snc.transpose(0, 2, 3, 1))
        m["sm"] = f(np.tile(np.asarray(inp["state_mlstm_m"])[:, bs], (1, ST, 1)))
        spc = np.asarray(inp["state_pool"])[:, bs].transpose(0, 2, 1, 3)
        m["spA"] = f(spc[:, 0:8].reshape(DEPTH, 128, 256))
        m["spB"] = f(spc[:, 8:15].reshape(DEPTH, 112, 256))
        maps.append(m)
    return maps


def gather_outputs(results):
    cat = lambda k, ax: np.concatenate([r[k] for r in results], axis=ax)
    yp = np.stack([r["yp"] for r in results], 0)
    ys = np.concatenate([r["ys"].reshape(ST, SB, D).transpose(1, 0, 2) for r in results], 0)
    pC = np.stack([r["pC"] for r in results], 1)
    pn = np.stack([r["pn"] for r in results], 1)
    pm = np.stack([r["pm"] for r in results], 1)
    pp = np.stack([r["pp"] for r in results], 1)
    return (yp, ys, pC, pn, pm, pp, cat("nC", 1), cat("nn", 1), cat("nm", 1), cat("npool", 1), cat("nv", 1))


def kernel(**inputs):
    if "prog" not in _CACHE:
        _CACHE["prog"] = build_program()
    nc, cpack = _CACHE["prog"]
    maps = make_in_maps(inputs, cpack)
    res = run_bass_kernel_spmd(nc, maps, core_ids=list(range(NCORES)))
    outs = gather_outputs(res.results)
    return tuple(np.ascontiguousarray(o, dtype=np.float32) for o in outs)
```

```python
import numpy as np
from contextlib import ExitStack
import concourse.bass as bass
import concourse.mybir as mybir
from concourse.bass_utils import run_bass_kernel_spmd

F32 = mybir.dt.float32
I32 = mybir.dt.int32
U32 = mybir.dt.uint32
F32R = mybir.dt.float32r
BF16 = mybir.dt.bfloat16
ALU = mybir.AluOpType
AF = mybir.ActivationFunctionType
AX = mybir.AxisListType

NCORES = 8
D = 1024
SEQ = 2048
NT = 16
SB = 16
ST = 4
SP = SB * ST
DEPTH = 2
ALPHA = (2 * DEPTH) ** 0.25
LN_EPS = 1e-5
IN_COLS = 2824
NEG = -1.0e30
WCW = 192
NEXP = 16384
SAME_ENGINE_WAITS = True
NBUF = 8


class TB:
    def __init__(self, name, sem=None):
        self.name = name
        self.last_w = None
        self.reads = []
        self.sem = sem
        self.dma_total = 0
        self.dma_dirty = False


class KB:
    ENG = ("pe", "act", "dve", "pool", "sp")

    def __init__(self, nc, stack):
        self.nc = nc
        self.stack = stack
        self.q = {e: [] for e in self.ENG}
        self.cnt = {e: 0 for e in self.ENG}
        self.esem = {e: stack.enter_context(nc.semaphore("es_" + e)) for e in self.ENG}
        self.seen = {e: {} for e in self.ENG}
        self.semobj = {}
        self._sem_owner = {}
        self.stack0 = stack
        self.phase_tbs = []
        self.sem_pool = []
        self.nsem = 0
        self.sfx = ""

    def new_sem(self, name):
        return self.stack.enter_context(self.nc.semaphore(name + self.sfx))

    def buf(self, name, dma=False):
        if not dma:
            return TB(name)
        if self.sem_pool:
            sem, val = self.sem_pool.pop()
        else:
            sem, val = self.stack0.enter_context(self.nc.semaphore("dsem%d" % self.nsem)), 0
            self.nsem += 1
        tb = TB(name, sem)
        tb.dma_total = val
        if self.stack is not self.stack0:
            self.phase_tbs.append(tb)
        return tb

    def end_phase(self):
        for tb in self.phase_tbs:
            self._sem_owner.pop(id(tb.sem), None)
            self.sem_pool.append((tb.sem, tb.dma_total))
        self.phase_tbs = []

    def sb(self, name, shape, dt=F32):
        return self.stack.enter_context(self.nc.sbuf_tensor(name + self.sfx, list(shape), dt))

    def ps(self, name, shape, dt=F32):
        return self.stack.enter_context(self.nc.psum_tensor(name + self.sfx, list(shape), dt))

    def _deps(self, e, reads, writes):
        deps = {}

        def add(tok):
            if tok is None:
                return
            s, v = tok
            k = id(s)
            self.semobj[k] = s
            ow = self._sem_owner.get(k)
            if ow is not None:
                v = ow.dma_total
            if v > deps.get(k, 0):
                deps[k] = v
        for b in reads:
            add(b.last_w)
        for b in writes:
            add(b.last_w)
            for r in b.reads:
                add(r)
        out = []
        own = id(self.esem[e])
        for k, v in deps.items():
            if k == own and (e in ("pe", "sp") or not SAME_ENGINE_WAITS):
                continue
            if self.seen[e].get(k, 0) >= v:
                continue
            self.seen[e][k] = v
            out.append((self.semobj[k], v))
        return out

    def op(self, e, fn, reads=(), writes=(), bound=False):
        waits = self._deps(e, reads, writes)
        for s, v in waits:
            tb = self._sem_owner.get(id(s))
            if tb is not None:
                tb.dma_dirty = True
        self.cnt[e] += 1
        tok = (self.esem[e], self.cnt[e])
        self.q[e].append((waits, fn if bound else _bind(fn), tok[0], 1))
        for b in reads:
            b.reads.append(tok)
        for b in writes:
            b.last_w = tok
            b.reads = []
        return tok

    def dma(self, e, fn, owner, reads=(), writes=(), bound=False):
        self._sem_owner[id(owner.sem)] = owner
        waits = self._deps(e, reads, writes)
        if owner.dma_dirty and owner.dma_total > 0:
            k = id(owner.sem)
            if self.seen[e].get(k, 0) < owner.dma_total:
                self.seen[e][k] = owner.dma_total
                waits.append((owner.sem, owner.dma_total))
            owner.dma_dirty = False
        for s, v in waits:
            tb = self._sem_owner.get(id(s))
            if tb is not None and tb is not owner:
                tb.dma_dirty = True
        owner.dma_total += 16
        tok = (owner.sem, owner.dma_total)
        self.q[e].append((waits, fn if bound else _bind(fn), owner.sem, 16))
        for b in reads:
            b.reads.append(tok)
        for b in writes:
            b.last_w = tok
            b.reads = []
        return tok

    def barrier(self, extra=()):
        toks = [(self.esem[e], self.cnt[e]) for e in self.ENG if self.cnt[e] > 0 and e != "sp"]
        for tb in list(self._sem_owner.values()) + list(extra):
            if tb.dma_total > 0:
                toks.append((tb.sem, tb.dma_total))
        for e in self.ENG:
            waits = []
            for s, v in toks:
                k = id(s)
                if k == id(self.esem[e]):
                    continue
                if self.seen[e].get(k, 0) >= v:
                    continue
                self.seen[e][k] = v
                waits.append((s, v))
            if waits:
                self.q[e].append((waits, None, None, 0))

    def emit(self, final_waits=()):
        nc = self.nc
        engs = {"pe": "tensor", "act": "scalar", "dve": "vector", "pool": "gpsimd", "sp": "sync"}
        with nc.Block() as block:
            for e in self.ENG:
                items = self.q[e]
                fw = list(final_waits) if e == "sp" else []

                def body(eng, items=items, fw=fw):
                    for waits, fn, sem, inc in items:
                        for s, v in waits:
                            eng.wait_ge(s, v)
                        if fn is not None:
                            fn(eng).then_inc(sem, inc)
                    for s, v in fw:
                        eng.wait_ge(s, v)
                getattr(block, engs[e])(body)
        self.q = {e: [] for e in self.ENG}


class _Rec:
    def __init__(self):
        self.call = None

    def __getattr__(self, name):
        def f(*a, **k):
            self.call = (name, a, k)
            return self
        return f


def _bind(fn):
    r = _Rec()
    fn(r)
    assert r.call is not None
    name, a, k = r.call
    return lambda eng: getattr(eng, name)(*a, **k)


class Tn:
    def __init__(self, kb, name, shape, dt=F32, psum=False, dma=False):
        self.t = kb.ps(name, shape, dt) if psum else kb.sb(name, shape, dt)
        self.b = kb.buf(name, dma=dma)

    def __getitem__(self, k):
        return self.t[k]


def _consts():
    c = {}
    i128 = np.arange(128)
    c["ident"] = np.eye(128, dtype=np.float32)
    c["ones"] = np.ones((128, 128), np.float32)
    c["triu"] = (i128[:, None] <= i128[None, :]).astype(np.float32)
    c["negm"] = np.where(i128[None, :] <= i128[:, None], 0.0, NEG).astype(np.float32)
    sel = np.zeros((128, 128), np.float32); sel[127, :] = 1.0
    c["sel127"] = sel
    p = np.arange(SP); tt = p // SB; bb = p % SB
    sameb = bb[:, None] == bb[None, :]
    tri_s = (sameb & (tt[:, None] <= tt[None, :])).astype(np.float32)
    c["tri_s"] = _pad(tri_s)
    c["negm_s"] = _pad(np.where(sameb & (tt[None, :] <= tt[:, None]), 0.0, NEG).astype(np.float32))
    c["negb_s"] = _pad(np.where(sameb, 0.0, NEG).astype(np.float32))
    c["selend"] = _pad(((tt[:, None] == ST - 1) & sameb).astype(np.float32))
    oh = (bb[:, None] == np.arange(SB)[None, :]).astype(np.float32)
    c["onehotB"] = _pad(oh, cols=16)
    oh0 = ((p[:, None] == np.arange(SB)[None, :])).astype(np.float32)
    c["onehot0"] = _pad(oh0, cols=16)
    c["iota16"] = np.broadcast_to(np.arange(16, dtype=np.float32), (128, 16)).copy()
    wins = (2, 4, 8, 16)
    bc0 = np.zeros((4, 128, 128), np.float32); bc = np.zeros((4, 128, 128), np.float32)
    bp = np.zeros((4, 128, 128), np.float32)
    for g, w in enumerate(wins):
        for t in range(128):
            for j in range(w):
                s = t - j
                if s >= 0:
                    bc[g, s, t] += 1.0 / w
                    bc0[g, s, t] += 1.0 / min(t + 1, w)
                else:
                    bp[g, s + 128, t] += 1.0 / w
            bc[g, t, t] -= 1.0
            bc0[g, t, t] -= 1.0
    c["bandc0"] = bc0.transpose(1, 0, 2).reshape(128, 512)
    c["bandc"] = bc.transpose(1, 0, 2).reshape(128, 512)
    c["bandp"] = bp.transpose(1, 0, 2).reshape(128, 512)
    bsA = np.zeros((4, 128, SP), np.float32); bsB = np.zeros((4, 128, SP), np.float32)
    bsC = np.zeros((4, 128, SP), np.float32)
    for g, w in enumerate(wins):
        for t in range(ST):
            for b in range(SB):
                col = t * SB + b
                for j in range(w):
                    r = 15 + t - j
                    if r >= 15:
                        bsC[g, (r - 15) * SB + b, col] += 1.0 / w
                    elif r >= 8:
                        bsB[g, (r - 8) * SB + b, col] += 1.0 / w
                    else:
                        bsA[g, r * SB + b, col] += 1.0 / w
                bsC[g, t * SB + b, col] -= 1.0
    c["bsA"] = bsA.transpose(1, 0, 2).reshape(128, 4 * SP)
    c["bsB"] = bsB.transpose(1, 0, 2).reshape(128, 4 * SP)
    c["bsC"] = bsC.transpose(1, 0, 2).reshape(128, 4 * SP)
    return c


def _pad(a, cols=None):
    out = np.zeros((128, a.shape[1] if cols is None else cols), np.float32)
    out[: a.shape[0], : a.shape[1]] = a
    return out


_CONST_G = ["ident", "ones", "iota16"]
_CONST_1 = ["triu", "negm", "sel127", "tri_s", "negm_s", "negb_s", "selend",
            "onehotB", "onehot0", "bandc0", "bandc", "bandp", "bsA", "bsB", "bsC"]


def _const_pack():
    c = _consts()
    packs = []
    for order in (_CONST_G, _CONST_1):
        offs = {}
        o = 0
        arrs = []
        for k in order:
            offs[k] = (o, c[k].shape[1])
            o += c[k].shape[1]
            arrs.append(c[k])
        packs.append((np.ascontiguousarray(np.concatenate(arrs, axis=1)), offs))
    return packs


def build_program(n_layers=DEPTH, do_phase2=True):
    (cpack, coff), (cpack1, coff1) = _const_pack()
    NCST = cpack.shape[1]
    NCST1 = cpack1.shape[1]
    nc = bass.Bass("TRN2", target_bir_lowering=False)

    def din(name, shape, dt=F32):
        return nc.dram_tensor(name, list(shape), dt, kind="ExternalInput").ap()

    def dout(name, shape, dt=F32):
        return nc.dram_tensor(name, list(shape), dt, kind="ExternalOutput").ap()

    xp = din("xp", [SEQ, D]); xs = din("xs", [SP, D])
    cp = din("cp", [128, D]); cs = din("cs", [SP, D])
    sC = din("sC", [DEPTH, 4, 128, SB, 128]); snat = din("snat", [DEPTH, SB, 4, 128])
    snT = din("snT", [DEPTH, 4, 128, SB]); sm = din("sm", [DEPTH, SP, 4])
    spA = din("spA", [DEPTH, 128, 256]); spB = din("spB", [DEPTH, 112, 256])
    w_ada = din("w_ada", [DEPTH, (6 * D) // WCW, 128, 8 * WCW]); b_ada = din("b_ada", [DEPTH, 128, 6 * D])
    w_in = din("w_in", [DEPTH, (IN_COLS + WCW - 1) // WCW, 128, 8 * WCW]); b_gate = din("b_gate", [DEPTH, 128, 8])
    mh_g = din("mh_g", [DEPTH, 128, 512]); sgu_g = din("sgu_g", [DEPTH, 128, 256])
    sgu_b = din("sgu_b", [DEPTH, 128, 256]); pscale = din("pscale", [DEPTH, 128, 256])
    w_sT = din("w_sT", [DEPTH, 128, 4, 128]); b_sT = din("b_sT", [DEPTH, 128, 4])
    w_sS = din("w_sS", [DEPTH, SP, 4, SP]); b_sS = din("b_sS", [DEPTH, SP, 4])
    w_pool = din("w_pool", [DEPTH, 64, 4, 64]); w_o = din("w_o", [DEPTH, (D + WCW - 1) // WCW, 128, 8 * WCW])
    ln1g = din("ln1g", [DEPTH, 128, D]); ln1b = din("ln1b", [DEPTH, 128, D])
    ln2g = din("ln2g", [DEPTH, 128, D]); ln2b = din("ln2b", [DEPTH, 128, D])
    w_pq = din("w_pq", [DEPTH, 16, 128, 8 * 128]); keysT = din("keysT", [DEPTH, 128, 16, 128])
    puv = [din("puv%d" % l, [NEXP, 2 * D]) for l in range(DEPTH)]
    puvb = [nc.dram_tensor("puvb%d" % l, [NEXP, 2 * D], BF16, kind="Internal").ap() for l in range(DEPTH)]
    cst_d = din("cst", [128, NCST])
    cst1_d = din("cst1", [128, NCST1])

    yp = dout("yp", [SEQ, D]); ys = dout("ys", [SP, D])
    o_pC = dout("pC", [DEPTH, 4, 128, 128]); o_pn = dout("pn", [DEPTH, 4, 128]); o_pm = dout("pm", [DEPTH, 4])
    o_pp = dout("pp", [DEPTH, 15, 256])
    o_nC = dout("nC", [DEPTH, SB, 4, 128, 128]); o_nn = dout("nn", [DEPTH, SB, 4, 128])
    o_nm = dout("nm", [DEPTH, SB, 4]); o_np = dout("npool", [DEPTH, SB, 15, 256])
    o_nv = dout("nv", [DEPTH, SB, ST, 256])

    with ExitStack() as st0:
        kb = KB(nc, st0)
        op = kb.op
        OUT = kb.buf("outs", dma=True)

        def out_dma(dst, src, reads):
            kb.dma("sp", lambda q: q.dma_start(out=dst, in_=src), OUT, reads=reads)

        X = kb.sb("X", [128, NT + 1, D])
        XB = [kb.buf("X%d" % t) for t in range(NT + 1)]
        XL = kb.buf("xload", dma=True)
        CST = Tn(kb, "CST", [128, NCST], dma=True)
        EPSB = Tn(kb, "EPSB", [128, 1])
        kb.op("dve", lambda e: e.memset(EPSB[:, :], LN_EPS), writes=[EPSB.b])

        def C(name, P=128, w=None):
            if name in coff:
                o, n = coff[name]
                return CST[:P, o:o + (n if w is None else w)]
            o, n = coff1[name]
            return kb.cst1[:P, o:o + (n if w is None else w)]

        def Cg(name, g, P, blk, w):
            o, n = coff1[name]
            return kb.cst1[:P, o + g * blk: o + g * blk + w]

        with nc.allow_non_contiguous_dma(reason="small strided state/param loads"):
            kb.dma("sp", lambda q: q.dma_start(out=CST[:, :], in_=cst_d), CST.b, writes=[CST.b])
            for t in range(NT):
                kb.dma("sp", lambda q, t=t: q.dma_start(out=X[:, t, :], in_=xp[t * 128:(t + 1) * 128, :]),
                       XL, writes=[XB[t]])
            kb.dma("sp", lambda q: q.dma_start(out=X[:SP, NT, :], in_=xs), XL, writes=[XB[NT]])

            puvb_ = puvb
            for l in range(n_layers):
                for part in ("p", "s"):
                    with ExitStack() as st1:
                        kb.stack = st1
                        kb.sfx = "_a%s%d" % (part, l)
                        _phase1(nc, kb, l, locals(), part)
                        kb.barrier(extra=[OUT])
                        kb.emit()
                        kb.end_phase()
                if do_phase2:
                    with ExitStack() as st2:
                        kb.stack = st2
                        kb.sfx = "_b%d" % l
                        _phase2(nc, kb, l, locals())
                        kb.barrier(extra=[OUT])
                        kb.emit()
                        kb.end_phase()
            kb.stack = st0
            kb.sfx = ""
            for t in range(NT):
                out_dma(yp[t * 128:(t + 1) * 128, :], X[:, t, :], [XB[t]])
            out_dma(ys, X[:SP, NT, :], [XB[NT]])
            kb.emit(final_waits=[(OUT.sem, OUT.dma_total)])
    return nc, (cpack, cpack1)


def _ada(nc, kb, l, E, ADA, P, csrc, off, WCH, PM, hbuf, hT, PT, badac, WCR=None):
    op = kb.op
    C = E["C"]
    w_ada, b_ada = E["w_ada"], E["b_ada"]
    kb.dma("sp", lambda q: q.dma_start(out=hbuf[:P, :], in_=csrc), hbuf.b, writes=[hbuf.b])
    op("act", lambda e: e.activation(out=hbuf[:P, :], in_=hbuf[:P, :], func=AF.Silu), reads=[hbuf.b], writes=[hbuf.b])
    _transpose8(kb, E, hbuf, hT, PT, P)
    r32 = (hT.t.dtype == F32R)
    for c in range(3072 // WCW):
        i = c % 2
        c0 = off + c * WCW
        wch = WCH[c % len(WCH)]
        kb.dma("sp", lambda q: q.dma_start(out=wch[:, :, 0:WCW], in_=w_ada[l, c0 // WCW].rearrange("p (k j) -> p k j", k=8)), wch.b, writes=[wch.b])
        kb.dma("sp", lambda q: q.dma_start(out=badac[i][:P, 0:WCW], in_=b_ada[l, :P, c0:c0 + WCW]), badac[i].b, writes=[badac[i].b])
        wsrc = WCR[c % 2] if r32 else wch
        if r32:
            op("act", lambda e: e.copy(out=wsrc[:, :, 0:WCW], in_=wch[:, :, 0:WCW]), reads=[wch.b], writes=[wsrc.b])
        for k in range(8):
            if r32:
                op("pe", lambda e: e.matmul(PM[i][:, 0:WCW], lhsT=hT[:, k, :], rhs=wsrc[:, k, 0:WCW], start=(k == 0), stop=(k == 7)),
                   reads=[hT.b, wsrc.b], writes=[PM[i].b] if k in (0, 7) else [])
            else:
                op("pe", lambda e: e.matmul(PM[i][:P, 0:WCW], lhsT=hT[:, k, :P], rhs=wch[:, k, 0:WCW], start=(k == 0), stop=(k == 7)),
                   reads=[hT.b, wch.b], writes=[PM[i].b] if k in (0, 7) else [])
        op("dve", lambda e: e.tensor_tensor(out=ADA[:P, c * WCW:(c + 1) * WCW], in0=PM[i][:P, 0:WCW], in1=badac[i][:P, 0:WCW], op=ALU.add),
           reads=[PM[i].b, badac[i].b], writes=[ADA.b])
    op("dve", lambda e: e.tensor_scalar_add(out=ADA[:P, 1024:2048], in0=ADA[:P, 1024:2048], scalar1=1.0), reads=[ADA.b], writes=[ADA.b])


def _transpose8(kb, E, src, dstT, PT, P, srcb=None, op=None):
    op = kb.op if op is None else op
    C = E["C"]
    sb_ = src.b if srcb is None else srcb
    for half in range(2):
        for j in range(4):
            k = half * 4 + j
            op("pe", lambda e, half=half, j=j, k=k: e.transpose(
                out=PT[half][:, j * 128:j * 128 + P], in_=src[:P, k * 128:(k + 1) * 128], identity=C("ident", P, P)),
               reads=[sb_, E["CST"].b], writes=[PT[half].b])
        op("act", lambda e, half=half: e.copy(
            out=dstT[:, half * 4:half * 4 + 4, :P],
            in_=PT[half][:, :].rearrange("p (j c) -> p j c", j=4)[:, :, :P]),
           reads=[PT[half].b], writes=[dstT.b])


def _phase1(nc, kb, l, E, part):
    isS = (part == "s")
    tiles = [NT] if isS else list(range(NT))
    cur = [None]

    def run_items(items, n=None):
        n = len(items) if n is None else min(n, len(items))
        for _ in range(n):
            kind, e, fn, owner, reads, writes = items.pop(0)
            if kind == "op":
                kb.op(e, fn, reads=reads, writes=writes, bound=True)
            else:
                kb.dma(e, fn, owner, reads=reads, writes=writes, bound=True)

    def op(e, fn, reads=(), writes=()):
        tok = kb.op(e, fn, reads=reads, writes=writes)
        if cur[0]:
            run_items(cur[0], 1)
        return tok
    C, Cg, CST, X, XB = E["C"], E["Cg"], E["CST"], E["X"], E["XB"]
    EPSB = E["EPSB"]
    out_dma = E["out_dma"]
    w_in, w_o = E["w_in"], E["w_o"]

    kb.cst1 = kb.sb("CST1", [128, E["NCST1"]])
    kb.dma("sp", lambda q: q.dma_start(out=kb.cst1[:, :], in_=E["cst1_d"]), CST.b, writes=[CST.b])
    ADA = Tn(kb, "ADA1", [128, 3072]); ADAs = ADA
    WCH = [Tn(kb, "WCH%d" % i, [128, 8, WCW], dma=True) for i in range(2)]
    WCR = [Tn(kb, "WCR%d" % i, [128, 8, WCW], F32R) for i in range(2)]
    badac = [Tn(kb, "bada%d" % i, [128, 256], dma=True) for i in range(2)]
    nbuf = 1 if isS else 2
    Hs = [Tn(kb, "H%d" % i, [128, D], dma=True) for i in range(nbuf)]
    HTs = [Tn(kb, "HT%d" % i, [128, 8, 128], F32R) for i in range(nbuf)]
    PROJs = [Tn(kb, "PROJ%d" % i, [128, IN_COLS], dma=True) for i in range(nbuf)]
    H, HT, PROJ = Hs[0], HTs[0], PROJs[0]
    Y = Tn(kb, "Y", [128, D])
    PRM = Tn(kb, "PRM", [128, 8 + 512 + 256 * 3 + 2 * D], dma=True)
    WS = BS = WSs = BSs = None
    if isS:
        WSs = Tn(kb, "WSs", [128, 4, SP], dma=True); BSs = Tn(kb, "BSs", [128, 4], dma=True)
    else:
        WS = Tn(kb, "WS", [128, 4, 128], dma=True); BS = Tn(kb, "BS", [128, 4], dma=True)
    WP = Tn(kb, "WP", [64, 4, 64], dma=True)
    PT = [Tn(kb, "PT%d" % i, [128, 512], psum=True) for i in range(2)]
    PM = [Tn(kb, "PM%d" % i, [128, 512], psum=True) for i in range(2)]
    PA = Tn(kb, "PA", [128, 512], psum=True); PB = Tn(kb, "PB", [128, 512], psum=True)
    PC = Tn(kb, "PC", [128, 512], psum=True); PD = Tn(kb, "PD", [128, 512], psum=True)
    SM = Tn(kb, "SM", [128, 64])
    SMs = MREP = CTX = None
    if isS:
        SMs = Tn(kb, "SMs", [128, 4], dma=True)
    else:
        MREP = Tn(kb, "MREP", [128, 4])
        CTX = Tn(kb, "CTX", [128, 4, 129], dma=True)
    DG = Tn(kb, "DG", [128, 128]); DL = Tn(kb, "DL", [128, 128]); WI = Tn(kb, "WI", [128, 128])
    AM = Tn(kb, "AM", [128, 128]); AT = Tn(kb, "AT", [128, 128])
    QT = Tn(kb, "QT", [128, 128]); KT = Tn(kb, "KT", [128, 128])
    VX = Tn(kb, "VX", [128, 129]); TOT = Tn(kb, "TOT", [128, 129]); WV = Tn(kb, "WV", [128, 129])
    HN = Tn(kb, "HN", [128, 128]); SG = Tn(kb, "SG", [128, 128]); ST6 = Tn(kb, "ST6", [128, 2, 6])
    OUTC = None if isS else Tn(kb, "OUTC", [128, 128], dma=True)
    CN = CTS = RA = ZQ = NNAT = NTH = WCB = DECD = DECR = MSO = SPA = SPB = PREV = None
    if isS:
        CN = Tn(kb, "CN", [128, SB, 128], dma=True); CTS = Tn(kb, "CTS", [128, SB, 129])
        RA = Tn(kb, "RA", [128, SB, 128])

    class _V2:
        def __init__(self, ap, b):
            self.t = ap
            self.b = b

        def __getitem__(self, k):
            return self.t[k]
    if isS:
        ZQ = _V2(RA[:, :, :].rearrange("p a b -> p (a b)")[:, 0:SB * SP], RA.b)
        NNAT = Tn(kb, "NNAT", [SB, 4, 128], dma=True); NTH = Tn(kb, "NTH", [128, SB], dma=True)
        WCB = Tn(kb, "WCB", [128, 16]); DECD = Tn(kb, "DECD", [128, 16]); DECR = Tn(kb, "DECR", [128, 16])
        MSO = Tn(kb, "MSO", [SB, 4], dma=True)
        SPA = Tn(kb, "SPA", [128, 256], dma=True); SPB = Tn(kb, "SPB", [128, 256], dma=True)
    else:
        PREV = Tn(kb, "PREV", [128, 256])
    PTT = Tn(kb, "PTT", [64, 4, 128])
    VN = Tn(kb, "VN", [128, 256], dma=True); VTMP = Tn(kb, "VTMP", [128, 256])

    o_bg, o_mh, o_sg, o_sb, o_ps, o_l1g, o_l1b = 0, 8, 520, 776, 1032, 1288, 1288 + D
    for (o, w, src) in [(o_bg, 8, E["b_gate"]), (o_mh, 512, E["mh_g"]), (o_sg, 256, E["sgu_g"]), (o_sb, 256, E["sgu_b"]),
                        (o_ps, 256, E["pscale"]), (o_l1g, D, E["ln1g"]), (o_l1b, D, E["ln1b"])]:
        kb.dma("sp", lambda q, o=o, w=w, src=src: q.dma_start(out=PRM[:, o:o + w], in_=src[l]), PRM.b, writes=[PRM.b])
    kb.dma("sp", lambda q: q.dma_start(out=WP[:, :, :], in_=E["w_pool"][l]), WP.b, writes=[WP.b])
    if isS:
        kb.dma("sp", lambda q: q.dma_start(out=WSs[:SP, :, :], in_=E["w_sS"][l]), WSs.b, writes=[WSs.b])
        kb.dma("sp", lambda q: q.dma_start(out=BSs[:SP, :], in_=E["b_sS"][l]), BSs.b, writes=[BSs.b])
    else:
        kb.dma("sp", lambda q: q.dma_start(out=WS[:, :, :], in_=E["w_sT"][l]), WS.b, writes=[WS.b])
        kb.dma("sp", lambda q: q.dma_start(out=BS[:, :], in_=E["b_sT"][l]), BS.b, writes=[BS.b])
    for g in range(4):
        if isS:
            op("dve", lambda e, g=g: e.tensor_tensor(out=WSs[:SP, g, :], in0=WSs[:SP, g, :], in1=C("tri_s", SP, SP), op=ALU.mult),
               reads=[WSs.b, CST.b], writes=[WSs.b])
        else:
            op("dve", lambda e, g=g: e.tensor_tensor(out=WS[:, g, :], in0=WS[:, g, :], in1=C("triu"), op=ALU.mult),
               reads=[WS.b, CST.b], writes=[WS.b])
    if not isS:
        op("dve", lambda e: e.memset(CTX[:, :, :], 0.0), writes=[CTX.b])
        op("dve", lambda e: e.memset(MREP[:, :], 0.0), writes=[MREP.b])
    op("dve", lambda e: e.memset(VX[:, :], 1.0), writes=[VX.b])

    if isS:
        _ada(nc, kb, l, E, ADA, SP, E["cs"], 0, WCH, PM, H, HT, PT, badac, WCR)
    else:
        _ada(nc, kb, l, E, ADA, 128, E["cp"], 0, WCH, PM, H, HT, PT, badac, WCR)

    w_in_v = w_in[l]
    w_o_v = w_o[l]
    def mk_chunks(n):
        return [(c0, min(WCW, n - c0)) for c0 in range(0, n, WCW)]
    chunks = mk_chunks(IN_COLS)
    wctr = [0]

    def stream_mm(wview, c0, w, lhsT, P, evac, op=op, dma=kb.dma):
        i = wctr[0] % 2
        wctr[0] += 1
        dma("sp", lambda q: q.dma_start(out=WCH[i][:, :, :], in_=wview[c0 // WCW].rearrange("p (k j) -> p k j", k=8)), WCH[i].b, writes=[WCH[i].b])
        wr = WCR[i]
        if wctr[0] % 3 == 0:
            op("dve", lambda e: e.tensor_copy(out=wr[:, :, 0:w], in_=WCH[i][:, :, 0:w]), reads=[WCH[i].b], writes=[wr.b])
        else:
            op("act", lambda e: e.copy(out=wr[:, :, 0:w], in_=WCH[i][:, :, 0:w]), reads=[WCH[i].b], writes=[wr.b])
        for k in range(8):
            op("pe", lambda e, k=k: e.matmul(PM[i][:, 0:w], lhsT=lhsT[:, k, :], rhs=wr[:, k, 0:w],
                                              start=(k == 0), stop=(k == 7)),
               reads=[lhsT.b, wr.b], writes=[PM[i].b] if k in (0, 7) else [])
        evac(PM[i], i)

    def make_A(t):
        items = []

        def iop(e, fn, reads=(), writes=()):
            items.append(("op", e, _bind(fn), None, tuple(reads), tuple(writes)))

        def idma(e, fn, owner, reads=(), writes=()):
            items.append(("dma", e, _bind(fn), owner, tuple(reads), tuple(writes)))
        P = SP if isS else 128
        H, HT, PROJ = Hs[t % nbuf], HTs[t % nbuf], PROJs[t % nbuf]
        xt = X[:P, t, :]
        iop("dve", lambda e: e.tensor_tensor(out=H[:P, :], in0=xt, in1=ADA[:P, 1024:2048], op=ALU.mult), reads=[XB[t], ADA.b], writes=[H.b])
        iop("dve", lambda e: e.tensor_tensor(out=H[:P, :], in0=H[:P, :], in1=ADA[:P, 0:1024], op=ALU.add), reads=[H.b, ADA.b], writes=[H.b])
        _transpose8(kb, E, H, HT, PT, P, op=iop)
        for (c0, w) in chunks:
            stream_mm(w_in_v, c0, w, HT, P,
                      lambda pm, i, c0=c0, w=w: iop("act", lambda e: e.copy(out=PROJ[:P, c0:c0 + w], in_=pm[:P, 0:w]),
                                                    reads=[pm.b], writes=[PROJ.b]), op=iop, dma=idma)
        return items

    run_items(make_A(tiles[0]))
    for ti, t in enumerate(tiles):
        is_s = isS
        P = SP if is_s else 128
        ada = ADA
        H, HT, PROJ = Hs[t % nbuf], HTs[t % nbuf], PROJs[t % nbuf]
        xt = X[:P, t, :]
        nxtA = make_A(tiles[ti + 1]) if ti + 1 < len(tiles) else None
        cur[0] = nxtA
        tri = C("tri_s", SP, SP) if is_s else C("triu")
        negm = C("negm_s", SP, SP) if is_s else C("negm")
        selE = C("selend", SP, SP) if is_s else C("sel127")
        if is_s:
            kb.dma("sp", lambda q: q.dma_start(out=SMs[:SP, :], in_=E["sm"][l]), SMs.b, writes=[SMs.b])
            kb.dma("sp", lambda q: q.dma_start(out=NNAT[:, :, :], in_=E["snat"][l]), NNAT.b, writes=[NNAT.b])
        mtok = SMs if is_s else MREP
        op("dve", lambda e: e.tensor_tensor(out=SM[:P, 0:8], in0=PROJ[:P, 2048:2056], in1=PRM[:P, o_bg:o_bg + 8], op=ALU.add),
           reads=[PROJ.b, PRM.b], writes=[SM.b])
        op("dve", lambda e: e.scalar_tensor_tensor(out=SM[:P, 8:12], in0=SM[:P, 4:8], scalar=-1.0, in1=SM[:P, 4:8], op0=ALU.mult, op1=ALU.max),
           reads=[SM.b], writes=[SM.b])
        op("act", lambda e: e.activation(out=SM[:P, 12:16], in_=SM[:P, 8:12], func=AF.Exp, scale=-1.0), reads=[SM.b], writes=[SM.b])
        op("act", lambda e: e.activation(out=SM[:P, 12:16], in_=SM[:P, 12:16], func=AF.Ln, bias=1.0, scale=1.0),
           reads=[SM.b], writes=[SM.b])
        op("dve", lambda e: e.tensor_scalar_min(out=SM[:P, 16:20], in0=SM[:P, 4:8], scalar1=0.0), reads=[SM.b], writes=[SM.b])
        op("dve", lambda e: e.tensor_tensor(out=SM[:P, 16:20], in0=SM[:P, 16:20], in1=SM[:P, 12:16], op=ALU.subtract),
           reads=[SM.b], writes=[SM.b])
        op("pe", lambda e: e.matmul(PA[:P, 0:4], lhsT=tri, rhs=SM[:P, 16:20], start=True, stop=True),
           reads=[CST.b, SM.b], writes=[PA.b])
        op("act", lambda e: e.copy(out=SM[:P, 20:24], in_=PA[:P, 0:4]), reads=[PA.b], writes=[SM.b])
        op("dve", lambda e: e.tensor_tensor(out=SM[:P, 24:28], in0=SM[:P, 0:4], in1=SM[:P, 20:24], op=ALU.subtract),
           reads=[SM.b], writes=[SM.b])
        op("pe", lambda e: e.matmul(PA[:P, 8:12], lhsT=selE, rhs=SM[:P, 20:24], start=True, stop=True),
           reads=[CST.b, SM.b], writes=[PA.b])
        op("act", lambda e: e.copy(out=SM[:P, 28:32], in_=PA[:P, 8:12]), reads=[PA.b], writes=[SM.b])

        for hh in range(4):
            qs = PROJ[:P, hh * 128:(hh + 1) * 128]
            ks = PROJ[:P, 512 + hh * 128:512 + (hh + 1) * 128]
            vs = PROJ[:P, 1024 + hh * 128:1024 + (hh + 1) * 128]
            os_ = PROJ[:P, 1536 + hh * 128:1536 + (hh + 1) * 128]
            col = lambda c, hh=hh: SM[:P, c + hh:c + hh + 1]
            S1 = lambda c: SM[:P, c:c + 1]
            if is_s:
                kb.dma("sp", lambda q, hh=hh: q.dma_start(out=CN[:, :, :], in_=E["sC"][l, hh]), CN.b, writes=[CN.b])
                kb.dma("sp", lambda q, hh=hh: q.dma_start(out=NTH[:, :], in_=E["snT"][l, hh]), NTH.b, writes=[NTH.b])
                for j in range(4):
                    pt = PT[j % 2]
                    for jj in range(4):
                        b = j * 4 + jj
                        op("pe", lambda e, b=b, jj=jj, pt=pt: e.transpose(out=pt[:, jj * 128:(jj + 1) * 128], in_=CN[:, b, :],
                                                                       identity=C("ident")),
                           reads=[CN.b, CST.b], writes=[pt.b])
                    op("act", lambda e, j=j, pt=pt: e.copy(out=CTS[:, j * 4:(j + 1) * 4, 0:128],
                                                           in_=pt[:, :].rearrange("p (j c) -> p j c", j=4)),
                       reads=[pt.b], writes=[CTS.b])
                op("dve", lambda e: e.tensor_copy(out=CTS[:, :, 128:129], in_=NTH[:, :].unsqueeze(2)), reads=[NTH.b], writes=[CTS.b])
            op("dve", lambda e, hh=hh: e.tensor_scalar(out=DG[:P, :P], in0=C("ident", P, P), scalar1=col(24), scalar2=None,
                                                       op0=ALU.mult), reads=[SM.b, CST.b], writes=[DG.b])
            op("pe", lambda e: e.matmul(PB[:P, 0:P], lhsT=C("ones", P, P), rhs=DG[:P, :P], start=True, stop=True),
               reads=[DG.b, CST.b], writes=[PB.b])
            if is_s:
                op("dve", lambda e: e.tensor_tensor(out=DL[:P, :P], in0=PB[:P, 0:P], in1=C("negb_s", SP, SP), op=ALU.add),
                   reads=[PB.b, CST.b], writes=[DL.b])
                op("dve", lambda e: e.tensor_reduce(out=S1(32), in_=DL[:P, :P], axis=AX.X, op=ALU.max), reads=[DL.b], writes=[SM.b])
            else:
                op("dve", lambda e: e.tensor_reduce(out=S1(32), in_=PB[:P, 0:P], axis=AX.X, op=ALU.max), reads=[PB.b], writes=[SM.b])
            op("dve", lambda e, hh=hh: e.scalar_tensor_tensor(out=DL[:P, :P], in0=PB[:P, 0:P], scalar=col(20), in1=negm,
                                                              op0=ALU.add, op1=ALU.add),
               reads=[PB.b, SM.b, CST.b], writes=[DL.b])
            op("dve", lambda e: e.tensor_reduce(out=S1(33), in_=DL[:P, :P], axis=AX.X, op=ALU.max), reads=[DL.b], writes=[SM.b])
            op("dve", lambda e, hh=hh: e.tensor_tensor(out=S1(34), in0=col(20), in1=mtok[:P, hh:hh + 1], op=ALU.add),
               reads=[SM.b, mtok.b], writes=[SM.b])
            op("dve", lambda e: e.tensor_tensor(out=S1(35), in0=S1(34), in1=S1(33), op=ALU.max), reads=[SM.b], writes=[SM.b])
            op("dve", lambda e: e.tensor_scalar(out=S1(36), in0=S1(35), scalar1=-1.0, scalar2=None, op0=ALU.mult),
               reads=[SM.b], writes=[SM.b])
            op("act", lambda e: e.activation(out=WI[:P, :P], in_=DL[:P, :P], func=AF.Exp, bias=S1(36), scale=1.0),
               reads=[DL.b, SM.b], writes=[WI.b])
            op("act", lambda e: e.activation(out=S1(37), in_=S1(34), func=AF.Exp, bias=S1(36), scale=1.0), reads=[SM.b], writes=[SM.b])
            op("act", lambda e: e.activation(out=S1(38), in_=S1(36), func=AF.Exp), reads=[SM.b], writes=[SM.b])
            op("pe", lambda e: e.transpose(out=PC[:, 0:P], in_=qs, identity=C("ident", P, P)), reads=[PROJ.b, CST.b], writes=[PC.b])
            op("pe", lambda e: e.transpose(out=PC[:, 128:128 + P], in_=ks, identity=C("ident", P, P)), reads=[PROJ.b, CST.b], writes=[PC.b])
            op("act", lambda e: e.mul(out=QT[:, :P], in_=PC[:, 0:P], mul=128.0 ** -0.5), reads=[PC.b], writes=[QT.b])
            op("act", lambda e: e.copy(out=KT[:, :P], in_=PC[:, 128:128 + P]), reads=[PC.b], writes=[KT.b])
            op("pe", lambda e: e.matmul(PD[:P, 0:P], lhsT=QT[:, :P], rhs=KT[:, :P], start=True, stop=True),
               reads=[QT.b, KT.b], writes=[PD.b])
            op("dve", lambda e: e.tensor_tensor(out=AM[:P, :P], in0=WI[:P, :P], in1=PD[:P, 0:P], op=ALU.mult),
               reads=[WI.b, PD.b], writes=[AM.b])
            op("pe", lambda e: e.transpose(out=PB[:P, 128:128 + P], in_=AM[:P, :P], identity=C("ident", P, P)),
               reads=[AM.b, CST.b], writes=[PB.b])
            op("act", lambda e: e.copy(out=AT[:P, :P], in_=PB[:P, 128:128 + P]), reads=[PB.b], writes=[AT.b])
            op("pool", lambda e: e.tensor_copy(out=VX[:P, 0:128], in_=vs), reads=[PROJ.b], writes=[VX.b])
            op("pe", lambda e: e.matmul(PD[:P, 128:257], lhsT=AT[:P, :P], rhs=VX[:P, :], start=True, stop=True),
               reads=[AT.b, VX.b], writes=[PD.b])
            if is_s:
                op("pool", lambda e: e.memset(ZQ[:, :], 0.0), writes=[ZQ.b])
                for b in range(SB):
                    op("pool", lambda e, b=b: e.tensor_copy(out=ZQ[:, b * SP + b:(b + 1) * SP:SB], in_=QT[:, b:SP:SB]),
                       reads=[QT.b], writes=[ZQ.b])
                for b in range(SB):
                    op("pe", lambda e, b=b: e.matmul(PC[:P, 256:385], lhsT=ZQ[:, b * SP:(b + 1) * SP], rhs=CTS[:, b, :],
                                                     start=(b == 0), stop=(b == SB - 1)),
                       reads=[ZQ.b, CTS.b], writes=[PC.b] if b in (0, SB - 1) else [])
            else:
                op("pe", lambda e, hh=hh: e.matmul(PC[:P, 256:385], lhsT=QT[:, :P], rhs=CTX[:, hh, :], start=True, stop=True),
                   reads=[QT.b, CTX.b], writes=[PC.b])
            op("act", lambda e: e.activation(out=TOT[:P, :], in_=PC[:P, 256:385], func=AF.Identity, scale=S1(37)),
               reads=[PC.b, SM.b], writes=[TOT.b])
            op("dve", lambda e: e.tensor_tensor(out=TOT[:P, :], in0=TOT[:P, :], in1=PD[:P, 128:257], op=ALU.add),
               reads=[TOT.b, PD.b], writes=[TOT.b])
            op("dve", lambda e: e.scalar_tensor_tensor(out=S1(39), in0=TOT[:P, 128:129], scalar=-1.0, in1=TOT[:P, 128:129], op0=ALU.mult, op1=ALU.max),
               reads=[TOT.b], writes=[SM.b])
            op("dve", lambda e: e.tensor_tensor(out=S1(39), in0=S1(39), in1=S1(38), op=ALU.max), reads=[SM.b], writes=[SM.b])
            op("dve", lambda e: e.reciprocal(out=S1(40), in_=S1(39)), reads=[SM.b], writes=[SM.b])
            op("dve", lambda e: e.tensor_scalar(out=HN[:P, :], in0=TOT[:P, 0:128], scalar1=S1(40), scalar2=None, op0=ALU.mult),
               reads=[TOT.b, SM.b], writes=[HN.b])
            op("dve", lambda e: e.bn_stats(out=ST6[:P, 0, :], in_=HN[:P, :]), reads=[HN.b], writes=[ST6.b])
            op("dve", lambda e: e.bn_aggr(out=SM[:P, 41:43], in_=ST6[:P, 0, :]), reads=[ST6.b], writes=[SM.b])
            op("act", lambda e: e.activation(out=S1(43), in_=S1(42), func=AF.Ln, bias=EPSB[:P, 0:1], scale=1.0), reads=[SM.b, EPSB.b], writes=[SM.b])
            op("act", lambda e: e.activation(out=S1(44), in_=S1(43), func=AF.Exp, scale=-0.5), reads=[SM.b], writes=[SM.b])
            op("dve", lambda e: e.tensor_scalar(out=HN[:P, :], in0=HN[:P, :], scalar1=S1(41), scalar2=S1(44),
                                                op0=ALU.subtract, op1=ALU.mult), reads=[HN.b, SM.b], writes=[HN.b])
            op("dve", lambda e, hh=hh: e.tensor_tensor(out=HN[:P, :], in0=HN[:P, :],
                                                       in1=PRM[:P, o_mh + hh * 128:o_mh + (hh + 1) * 128], op=ALU.mult),
               reads=[HN.b, PRM.b], writes=[HN.b])
            op("act", lambda e: e.activation(out=SG[:P, :], in_=os_, func=AF.Exp, scale=-1.0), reads=[PROJ.b], writes=[SG.b])
            op("dve", lambda e: e.tensor_scalar_add(out=SG[:P, :], in0=SG[:P, :], scalar1=1.0), reads=[SG.b], writes=[SG.b])
            op("dve", lambda e: e.reciprocal(out=SG[:P, :], in_=SG[:P, :]), reads=[SG.b], writes=[SG.b])
            op("dve", lambda e, hh=hh: e.tensor_tensor(out=Y[:P, hh * 128:(hh + 1) * 128], in0=HN[:P, :], in1=SG[:P, :], op=ALU.mult),
               reads=[HN.b, SG.b], writes=[Y.b])
            op("dve", lambda e, hh=hh: e.tensor_tensor(out=S1(45), in0=mtok[:P, hh:hh + 1], in1=S1(32), op=ALU.max),
               reads=[SM.b, mtok.b], writes=[SM.b])
            op("dve", lambda e, hh=hh: e.tensor_tensor(out=S1(45), in0=S1(45), in1=col(28), op=ALU.add), reads=[SM.b], writes=[SM.b])
            op("dve", lambda e, hh=hh: e.tensor_tensor(out=S1(46), in0=col(28), in1=S1(45), op=ALU.subtract), reads=[SM.b], writes=[SM.b])
            op("act", lambda e, hh=hh: e.activation(out=S1(47), in_=col(24), func=AF.Exp, bias=S1(46), scale=1.0),
               reads=[SM.b], writes=[SM.b])
            op("act", lambda e, hh=hh: e.activation(out=S1(48), in_=mtok[:P, hh:hh + 1], func=AF.Exp, bias=S1(46), scale=1.0),
               reads=[SM.b, mtok.b], writes=[SM.b])
            if not is_s:
                op("dve", lambda e: e.tensor_scalar(out=WV[:P, :], in0=VX[:P, :], scalar1=S1(47), scalar2=None, op0=ALU.mult),
                   reads=[VX.b, SM.b], writes=[WV.b])
                op("pe", lambda e: e.matmul(PB[:, 256:385], lhsT=ks, rhs=WV[:P, :], start=True, stop=True),
                   reads=[PROJ.b, WV.b], writes=[PB.b])
                op("dve", lambda e, hh=hh: e.scalar_tensor_tensor(out=CTX[:, hh, :], in0=CTX[:, hh, :], scalar=S1(48), in1=PB[:, 256:385],
                                                                  op0=ALU.mult, op1=ALU.add),
                   reads=[CTX.b, SM.b, PB.b], writes=[CTX.b])
                op("dve", lambda e, hh=hh: e.tensor_copy(out=MREP[:, hh:hh + 1], in_=S1(45)), reads=[SM.b], writes=[MREP.b])
                if t == NT - 1:
                    op("pe", lambda e, hh=hh: e.transpose(out=PA[:, 128:256], in_=CTX[:, hh, 0:128], identity=C("ident")),
                       reads=[CTX.b, CST.b], writes=[PA.b])
                    op("act", lambda e: e.copy(out=OUTC[:, :], in_=PA[:, 128:256]), reads=[PA.b], writes=[OUTC.b])
                    out_dma(E["o_pC"][l, hh], OUTC[:, :], [OUTC.b])
                    out_dma(E["o_pn"][l, hh].rearrange("(k o) -> k o", o=1), CTX[:, hh, 128:129], [CTX.b])
                    if hh == 3:
                        out_dma(E["o_pm"][l:l + 1, :], MREP[0:1, :], [MREP.b])
            else:
                op("dve", lambda e: e.tensor_scalar(out=WCB[:P, :], in0=C("onehotB", SP), scalar1=S1(47), scalar2=None, op0=ALU.mult),
                   reads=[SM.b, CST.b], writes=[WCB.b])
                op("dve", lambda e: e.tensor_tensor(out=RA[:P, :, :], in0=vs.unsqueeze(1).broadcast_to([P, SB, 128]),
                                                    in1=WCB[:P, :].unsqueeze(2).broadcast_to([P, SB, 128]), op=ALU.mult),
                   reads=[PROJ.b, WCB.b], writes=[RA.b])
                op("dve", lambda e: e.tensor_scalar(out=DECD[:P, :], in0=C("onehot0", SP), scalar1=S1(48), scalar2=None, op0=ALU.mult),
                   reads=[SM.b, CST.b], writes=[DECD.b])
                op("pe", lambda e: e.matmul(PA[:, 16:32], lhsT=C("ones", SP, 128), rhs=DECD[:P, :], start=True, stop=True),
                   reads=[DECD.b, CST.b], writes=[PA.b])
                op("act", lambda e: e.copy(out=DECR[:, :], in_=PA[:, 16:32]), reads=[PA.b], writes=[DECR.b])
                for b in range(SB):
                    pq = [PA, PB, PC, PD][b % 4]
                    op("pe", lambda e, b=b, pq=pq: e.matmul(pq[:, 384:512], lhsT=RA[:P, b, :], rhs=ks, start=True, stop=True),
                       reads=[RA.b, PROJ.b], writes=[pq.b])
                    op("dve", lambda e, b=b, pq=pq: e.scalar_tensor_tensor(out=CN[:, b, :], in0=CN[:, b, :], scalar=DECR[:, b:b + 1],
                                                                           in1=pq[:, 384:512], op0=ALU.mult, op1=ALU.add),
                       reads=[CN.b, DECR.b, pq.b], writes=[CN.b])
                out_dma(E["o_nC"][l, :, hh].rearrange("b v k -> v b k"), CN[:, :, :], [CN.b])
                op("pe", lambda e: e.matmul(PA[:SB, 32:160], lhsT=WCB[:P, :], rhs=ks, start=True, stop=True),
                   reads=[WCB.b, PROJ.b], writes=[PA.b])
                op("dve", lambda e, hh=hh: e.scalar_tensor_tensor(out=NNAT[:, hh, :], in0=NNAT[:, hh, :], scalar=SM[:SB, 48:49],
                                                                  in1=PA[:SB, 32:160], op0=ALU.mult, op1=ALU.add),
                   reads=[NNAT.b, SM.b, PA.b], writes=[NNAT.b])
                op("dve", lambda e, hh=hh: e.tensor_copy(out=MSO[:, hh:hh + 1], in_=SM[:SB, 45:46]), reads=[SM.b], writes=[MSO.b])
                if hh == 3:
                    out_dma(E["o_nn"][l], NNAT[:, :, :], [NNAT.b])
                    out_dma(E["o_nm"][l], MSO[:, :], [MSO.b])

        vsv = PROJ[:P, 2312:2568].rearrange("p (g d) -> p g d", g=4)
        op("dve", lambda e: e.tensor_reduce(out=SM[:P, 50:54], in_=vsv, axis=AX.X, op=ALU.add), reads=[PROJ.b], writes=[SM.b])
        op("dve", lambda e: e.tensor_scalar(out=SM[:P, 50:54], in0=SM[:P, 50:54], scalar1=1.0 / 64, scalar2=None, op0=ALU.mult),
           reads=[SM.b], writes=[SM.b])
        op("dve", lambda e: e.tensor_tensor(out=VN[:P, :].rearrange("p (g d) -> p g d", g=4), in0=vsv,
                                            in1=SM[:P, 50:54].unsqueeze(2).broadcast_to([P, 4, 64]), op=ALU.subtract),
           reads=[PROJ.b, SM.b], writes=[VN.b])
        op("pool", lambda e: e.tensor_tensor(out=VTMP[:P, :], in0=VN[:P, :], in1=VN[:P, :], op=ALU.mult), reads=[VN.b], writes=[VTMP.b])
        op("dve", lambda e: e.tensor_reduce(out=SM[:P, 54:58], in_=VTMP[:P, :].rearrange("p (g d) -> p g d", g=4), axis=AX.X, op=ALU.add),
           reads=[VTMP.b], writes=[SM.b])
        op("act", lambda e: e.activation(out=SM[:P, 54:58], in_=SM[:P, 54:58], func=AF.Ln, bias=EPSB[:P, 0:1], scale=1.0 / 64),
           reads=[SM.b, EPSB.b], writes=[SM.b])
        op("act", lambda e: e.activation(out=SM[:P, 58:62], in_=SM[:P, 54:58], func=AF.Exp, scale=-0.5), reads=[SM.b], writes=[SM.b])
        op("dve", lambda e: e.tensor_tensor(out=VN[:P, :].rearrange("p (g d) -> p g d", g=4), in0=VN[:P, :].rearrange("p (g d) -> p g d", g=4),
                                            in1=SM[:P, 58:62].unsqueeze(2).broadcast_to([P, 4, 64]), op=ALU.mult),
           reads=[VN.b, SM.b], writes=[VN.b])
        op("pool", lambda e: e.tensor_tensor(out=VN[:P, :], in0=VN[:P, :], in1=PRM[:P, o_sg:o_sg + 256], op=ALU.mult),
           reads=[VN.b, PRM.b], writes=[VN.b])
        op("pool", lambda e: e.tensor_tensor(out=VN[:P, :], in0=VN[:P, :], in1=PRM[:P, o_sb:o_sb + 256], op=ALU.add),
           reads=[VN.b, PRM.b], writes=[VN.b])
        wsl = WSs if is_s else WS
        bsl = BSs if is_s else BS
        for g in range(4):
            op("pe", lambda e, g=g: e.matmul(PC[:P, g * 64:(g + 1) * 64], lhsT=wsl[:P, g, :P], rhs=VN[:P, g * 64:(g + 1) * 64],
                                             start=True, stop=True), reads=[wsl.b, VN.b], writes=[PC.b])
        for g in range(4):
            op("dve", lambda e, g=g: e.scalar_tensor_tensor(out=Y[:P, 512 + g * 64:512 + (g + 1) * 64], in0=PC[:P, g * 64:(g + 1) * 64],
                                                            scalar=bsl[:P, g:g + 1], in1=PROJ[:P, 2056 + g * 64:2056 + (g + 1) * 64],
                                                            op0=ALU.add, op1=ALU.mult),
               reads=[PC.b, bsl.b, PROJ.b], writes=[Y.b])
        if is_s:
            for tq in range(ST):
                out_dma(E["o_nv"][l][:, tq, :], VN[tq * SB:(tq + 1) * SB, :], [VN.b])

        pin = lambda g: PROJ[:P, 2568 + g * 64:2568 + (g + 1) * 64]
        if is_s:
            kb.dma("sp", lambda q: q.dma_start(out=SPA[:, :], in_=E["spA"][l]), SPA.b, writes=[SPA.b])
            kb.dma("sp", lambda q: q.dma_start(out=SPB[:112, :], in_=E["spB"][l]), SPB.b, writes=[SPB.b])
            for g in range(4):
                op("pe", lambda e, g=g: e.matmul(PA[:64, g * 128:g * 128 + P], lhsT=SPA[:, g * 64:(g + 1) * 64], rhs=Cg("bsA", g, 128, SP, SP),
                                                 start=True, stop=False), reads=[SPA.b, CST.b], writes=[PA.b])
                op("pe", lambda e, g=g: e.matmul(PA[:64, g * 128:g * 128 + P], lhsT=SPB[:112, g * 64:(g + 1) * 64], rhs=Cg("bsB", g, 112, SP, SP),
                                                 start=False, stop=False), reads=[SPB.b, CST.b], writes=[])
                op("pe", lambda e, g=g: e.matmul(PA[:64, g * 128:g * 128 + P], lhsT=pin(g), rhs=Cg("bsC", g, SP, SP, SP),
                                                 start=False, stop=True), reads=[PROJ.b, CST.b], writes=[PA.b])
            npv = E["o_np"][l].rearrange("b r c -> r b c")
            for r in range(4):
                out_dma(npv[r], SPA[64 + r * SB:64 + (r + 1) * SB, :], [SPA.b])
            for r in range(7):
                out_dma(npv[4 + r], SPB[r * SB:(r + 1) * SB, :], [SPB.b])
            for r in range(4):
                out_dma(npv[11 + r], PROJ[r * SB:(r + 1) * SB, 2568:2824], [PROJ.b])
        else:
            for g in range(4):
                band = Cg("bandc0" if t == 0 else "bandc", g, 128, 128, 128)
                op("pe", lambda e, g=g, band=band: e.matmul(PA[:64, g * 128:(g + 1) * 128], lhsT=pin(g), rhs=band, start=True, stop=(t == 0)),
                   reads=[PROJ.b, CST.b], writes=[PA.b])
                if t > 0:
                    op("pe", lambda e, g=g: e.matmul(PA[:64, g * 128:(g + 1) * 128], lhsT=PREV[:, g * 64:(g + 1) * 64],
                                                     rhs=Cg("bandp", g, 128, 128, 128), start=False, stop=True),
                       reads=[PREV.b, CST.b], writes=[PA.b])
            if t < NT - 1:
                op("pool", lambda e: e.tensor_copy(out=PREV[:, :], in_=PROJ[:, 2568:2824]), reads=[PROJ.b], writes=[PREV.b])
            else:
                out_dma(E["o_pp"][l], PROJ[113:128, 2568:2824], [PROJ.b])
        op("act", lambda e: e.copy(out=PTT[:, :, :P], in_=PA[:64, :].rearrange("p (g c) -> p g c", g=4)[:, :, :P]),
           reads=[PA.b], writes=[PTT.b])
        for g in range(4):
            op("pe", lambda e, g=g: e.matmul(PB[:P, g * 64:(g + 1) * 64], lhsT=PTT[:, g, :P], rhs=WP[:, g, :], start=True, stop=True),
               reads=[PTT.b, WP.b], writes=[PB.b])
        op("dve", lambda e: e.tensor_tensor(out=Y[:P, 768:1024], in0=PB[:P, 0:256], in1=PRM[:P, o_ps:o_ps + 256], op=ALU.mult),
           reads=[PB.b, PRM.b], writes=[Y.b])

        _transpose8(kb, E, Y, HT, PT, P, op=op)
        for (c0, w) in mk_chunks(D):
            stream_mm(w_o_v, c0, w, HT, P,
                      lambda pm, i, c0=c0, w=w: op("dve", lambda e: e.tensor_tensor(out=H[:P, c0:c0 + w], in0=pm[:P, 0:w],
                                                                                    in1=ada[:P, 2048 + c0:2048 + c0 + w], op=ALU.mult),
                                                   reads=[pm.b, ada.b], writes=[H.b]))
        _resid_ln(kb, X, XB[t], t, P, H, SM, ST6, PRM, o_l1g, o_l1b, EPSB)
        cur[0] = None
        if nxtA:
            run_items(nxtA)


def _resid_ln(kb, X, xb, t, P, Z, SM, ST6, PRM, og, ob, EPSB):
    op = kb.op
    xt = X[:P, t, :]
    op("dve", lambda e: e.scalar_tensor_tensor(out=Z[:P, :], in0=xt, scalar=ALPHA, in1=Z[:P, :], op0=ALU.mult, op1=ALU.add),
       reads=[xb, Z.b], writes=[Z.b])
    op("dve", lambda e: e.bn_stats(out=ST6[:P, 0, :], in_=Z[:P, 0:512]), reads=[Z.b], writes=[ST6.b])
    op("dve", lambda e: e.bn_stats(out=ST6[:P, 1, :], in_=Z[:P, 512:1024]), reads=[Z.b], writes=[ST6.b])
    op("dve", lambda e: e.bn_aggr(out=SM[:P, 41:43], in_=ST6[:P, :, :].rearrange("p a b -> p (a b)")), reads=[ST6.b], writes=[SM.b])
    op("act", lambda e: e.activation(out=SM[:P, 43:44], in_=SM[:P, 42:43], func=AF.Ln, bias=EPSB[:P, 0:1], scale=1.0), reads=[SM.b, EPSB.b], writes=[SM.b])
    op("act", lambda e: e.activation(out=SM[:P, 44:45], in_=SM[:P, 43:44], func=AF.Exp, scale=-0.5), reads=[SM.b], writes=[SM.b])
    op("dve", lambda e: e.tensor_scalar(out=Z[:P, :], in0=Z[:P, :], scalar1=SM[:P, 41:42], scalar2=SM[:P, 44:45],
                                        op0=ALU.subtract, op1=ALU.mult), reads=[Z.b, SM.b], writes=[Z.b])
    op("pool", lambda e: e.tensor_tensor(out=Z[:P, :], in0=Z[:P, :], in1=PRM[:P, og:og + D], op=ALU.mult), reads=[Z.b, PRM.b], writes=[Z.b])
    op("dve", lambda e: e.tensor_tensor(out=xt, in0=Z[:P, :], in1=PRM[:P, ob:ob + D], op=ALU.add), reads=[Z.b, PRM.b], writes=[xb])


def _phase2(nc, kb, l, E):
    C, CST, X, XB = E["C"], E["CST"], E["X"], E["XB"]
    ADA = Tn(kb, "ADA2", [128, 3072]); ADAs = ADA
    WCH = [Tn(kb, "WCHb%d" % i, [128, 8, 128], dma=True) for i in range(2)]
    badac = [Tn(kb, "badab%d" % i, [128, 256], dma=True) for i in range(2)]
    Hs = [Tn(kb, "H2_%d" % i, [128, D], dma=True) for i in range(2)]
    HT = Tn(kb, "H2T", [128, 8, 128])
    PRM = Tn(kb, "PRM2", [128, 2 * D], dma=True)
    KTS = Tn(kb, "KTS", [128, 16, 128], dma=True)
    S0 = Tn(kb, "S0", [128, 2048]); S1_ = Tn(kb, "S1", [128, 2048]); S2 = Tn(kb, "S2", [128, 256])
    TOPS = Tn(kb, "TOPS", [128, 16, 16]); IDXU = Tn(kb, "IDXU", [128, 16, 16], U32); IDXF = Tn(kb, "IDXF", [128, 16, 16])
    CV = Tn(kb, "CV", [128, 8, 16]); CPOS = Tn(kb, "CPOS", [128, 8, 16], U32)
    PAU = Tn(kb, "PAU", [128, 8, 16], U32); PBU = Tn(kb, "PBU", [128, 8, 16], U32)
    PAF = Tn(kb, "PAF", [128, 8, 16]); PBF = Tn(kb, "PBF", [128, 8, 16])
    I1 = Tn(kb, "I1", [128, 128]); I2 = Tn(kb, "I2", [128, 128])
    IDXs = [Tn(kb, "IDX%d" % i, [128, 128], I32) for i in range(2)]
    GATEs = [Tn(kb, "GATE%d" % i, [128, 128]) for i in range(2)]
    ACTV = Tn(kb, "ACTV", [128, 128]); COEF = Tn(kb, "COEF", [128, 128])
    SMf = Tn(kb, "SM2f", [128, 16]); SM = Tn(kb, "SM2", [128, 64]); ST6 = Tn(kb, "ST62", [128, 2, 6])
    NB = NBUF
    UB = [Tn(kb, "UB%d" % i, [128, 2 * D], BF16, dma=True) for i in range(NB)]
    CB = [Tn(kb, "CB%d" % i, [128, 2 * D], BF16, dma=True) for i in range(2)]
    WCAB = Tn(kb, "WCAB", [128, 8, 256], dma=True)
    PUVB = kb.buf("puvb", dma=True)
    ACTB = [kb.buf("actv%d" % i) for i in range(NB)]
    COEFB = [kb.buf("coef%d" % i) for i in range(NB)]
    COEF2 = Tn(kb, "COEF2", [128, 128])

    class _View:
        def __init__(self, ap, b):
            self.t = ap
            self.b = b

        def __getitem__(self, k):
            return self.t[k]
    WCA = [WCAB]
    PT = [Tn(kb, "PTb%d" % i, [128, 512], psum=True) for i in range(2)]
    PQ = [Tn(kb, "PQ%d" % i, [128, 512], psum=True) for i in range(2)]
    PM = PQ
    ACCP = [Tn(kb, "ACCP%d" % i, [128, 512], psum=True) for i in range(2)]
    JUNK = Tn(kb, "JUNK", [128, D])
    ACC = JUNK
    DGB = [Tn(kb, "DGB%d" % i, [128, 128], BF16) for i in range(3)]
    PS = [Tn(kb, "PS%d" % i, [128, 512], psum=True) for i in range(2)]

    kb.dma("sp", lambda q: q.dma_start(out=PRM[:, 0:D], in_=E["ln2g"][l]), PRM.b, writes=[PRM.b])
    kb.dma("sp", lambda q: q.dma_start(out=PRM[:, D:2 * D], in_=E["ln2b"][l]), PRM.b, writes=[PRM.b])
    kb.dma("sp", lambda q: q.dma_start(out=KTS[:, :, :], in_=E["keysT"][l]), KTS.b, writes=[KTS.b])
    wpq_v = E["w_pq"][l]

    tab32 = E["puv"][l]
    tab = E["puvb"][l]
    stg = [S0, S1_]
    for blk in range(NEXP // 128):
        sg = stg[blk % 2]
        cb = CB[blk % 2]
        kb.dma("sp", lambda q: q.dma_start(out=sg[:, :], in_=tab32[blk * 128:(blk + 1) * 128, :]), WCAB.b, writes=[sg.b])
        if blk % 2 == 0:
            kb.op("act", lambda e: e.copy(out=cb[:, :], in_=sg[:, :]), reads=[sg.b], writes=[cb.b])
        else:
            kb.op("dve", lambda e: e.tensor_copy(out=cb[:, :], in_=sg[:, :]), reads=[sg.b], writes=[cb.b])
        kb.dma("sp", lambda q: q.dma_start(out=tab[blk * 128:(blk + 1) * 128, :], in_=cb[:, :]), PUVB, reads=[cb.b], writes=[PUVB])
    wctr = [0]

    def make_front(t, sset):
        items = []

        def op(e, fn, reads=(), writes=()):
            items.append(("op", e, _bind(fn), None, tuple(reads), tuple(writes)))

        def dma(e, fn, owner, reads=(), writes=()):
            items.append(("dma", e, _bind(fn), owner, tuple(reads), tuple(writes)))
        is_s = (t == NT)
        P = SP if is_s else 128
        ada = ADAs if is_s else ADA
        H = Hs[sset]; IDX = IDXs[sset]; GATE = GATEs[sset]
        QT = S0
        xt = X[:P, t, :]
        op("dve", lambda e: e.tensor_tensor(out=H[:P, :], in0=xt, in1=ada[:P, 1024:2048], op=ALU.mult), reads=[XB[t], ada.b], writes=[H.b])
        op("dve", lambda e: e.tensor_tensor(out=H[:P, :], in0=H[:P, :], in1=ada[:P, 0:1024], op=ALU.add), reads=[H.b, ada.b], writes=[H.b])
        _transpose8(kb, E, H, HT, PT, P, op=op)
        def wdma(c):
            i = c % 2
            dma("sp", lambda q: q.dma_start(out=WCH[i][:, :, :], in_=wpq_v[c].rearrange("p (k j) -> p k j", k=8)), WCH[i].b, writes=[WCH[i].b])
        wdma(0)
        for c in range(16):
            i = c % 2
            j = c % 2
            if c + 1 < 16:
                wdma(c + 1)
            for k in range(8):
                op("pe", lambda e: e.matmul(PQ[j][:, 0:P], lhsT=WCH[i][:, k, :], rhs=HT[:, k, :P], start=(k == 0), stop=(k == 7)),
                   reads=[WCH[i].b, HT.b], writes=[PQ[j].b] if k in (0, 7) else [])
            op("act", lambda e: e.copy(out=QT[:, c * 128:c * 128 + P], in_=PQ[j][:, 0:P]), reads=[PQ[j].b], writes=[QT.b])
        for c4 in range(4):
            ps = PS[c4 % 2]
            for j in range(4):
                c = c4 * 4 + j
                op("pe", lambda e: e.matmul(ps[:P, j * 128:(j + 1) * 128], lhsT=QT[:, c * 128:c * 128 + P], rhs=KTS[:, c, :], start=True, stop=True),
                   reads=[QT.b, KTS.b], writes=[ps.b])
            op("act", lambda e: e.copy(out=S1_[:P, c4 * 512:(c4 + 1) * 512], in_=ps[:P, :]), reads=[ps.b], writes=[S1_.b])
        for c in range(16):
            sc = S1_[:P, c * 128:(c + 1) * 128]
            wk = S2[:P, 0:128]
            op("dve", lambda e: e.max(out=TOPS[:P, c, 0:8], in_=sc), reads=[S1_.b], writes=[TOPS.b])
            op("dve", lambda e: e.max_index(out=IDXU[:P, c, 0:8], in_max=TOPS[:P, c, 0:8], in_values=sc), reads=[S1_.b, TOPS.b], writes=[IDXU.b])
            op("dve", lambda e: e.match_replace(out=wk, in_to_replace=TOPS[:P, c, 0:8], in_values=sc, imm_value=NEG), reads=[S1_.b, TOPS.b], writes=[S2.b])
            op("dve", lambda e: e.max(out=TOPS[:P, c, 8:16], in_=wk), reads=[S2.b], writes=[TOPS.b])
            op("dve", lambda e: e.max_index(out=IDXU[:P, c, 8:16], in_max=TOPS[:P, c, 8:16], in_values=wk), reads=[S2.b, TOPS.b], writes=[IDXU.b])
        op("dve", lambda e: e.tensor_copy(out=IDXF[:P, :, :], in_=IDXU[:P, :, :]), reads=[IDXU.b], writes=[IDXF.b])
        tv = TOPS[:P, :, :].rearrange("p (h two) k -> p h two k", two=2)
        CAND = S0
        op("dve", lambda e: e.tensor_tensor(out=CAND[:P, :].rearrange("p (h a b) -> p h a b", h=8, a=16),
                                            in0=tv[:, :, 0, :].unsqueeze(3).broadcast_to([P, 8, 16, 16]),
                                            in1=tv[:, :, 1, :].unsqueeze(2).broadcast_to([P, 8, 16, 16]), op=ALU.add),
           reads=[TOPS.b], writes=[S0.b])
        for h in range(8):
            cd = CAND[:P, h * 256:(h + 1) * 256]
            wk = S2[:P, 0:256]
            op("dve", lambda e: e.max(out=CV[:P, h, 0:8], in_=cd), reads=[S0.b], writes=[CV.b])
            op("dve", lambda e: e.max_index(out=CPOS[:P, h, 0:8], in_max=CV[:P, h, 0:8], in_values=cd), reads=[S0.b, CV.b], writes=[CPOS.b])
            op("dve", lambda e: e.match_replace(out=wk, in_to_replace=CV[:P, h, 0:8], in_values=cd, imm_value=NEG), reads=[S0.b, CV.b], writes=[S2.b])
            op("dve", lambda e: e.max(out=CV[:P, h, 8:16], in_=wk), reads=[S2.b], writes=[CV.b])
            op("dve", lambda e: e.max_index(out=CPOS[:P, h, 8:16], in_max=CV[:P, h, 8:16], in_values=wk), reads=[S2.b, CV.b], writes=[CPOS.b])
        op("dve", lambda e: e.tensor_single_scalar(out=PAU[:P, :, :], in_=CPOS[:P, :, :], scalar=4, op=ALU.logical_shift_right), reads=[CPOS.b], writes=[PAU.b])
        op("dve", lambda e: e.tensor_single_scalar(out=PBU[:P, :, :], in_=CPOS[:P, :, :], scalar=15, op=ALU.bitwise_and), reads=[CPOS.b], writes=[PBU.b])
        op("dve", lambda e: e.tensor_copy(out=PAF[:P, :, :], in_=PAU[:P, :, :]), reads=[PAU.b], writes=[PAF.b])
        op("dve", lambda e: e.tensor_copy(out=PBF[:P, :, :], in_=PBU[:P, :, :]), reads=[PBU.b], writes=[PBF.b])
        iv = IDXF[:P, :, :].rearrange("p (h two) k -> p h two k", two=2)
        io16 = C("iota16", P).unsqueeze(1).unsqueeze(1).broadcast_to([P, 8, 16, 16])
        for (pf, half, dst) in [(PAF, 0, I1), (PBF, 1, I2)]:
            eq = S1_[:P, :].rearrange("p (h k a) -> p h k a", h=8, k=16)
            op("dve", lambda e: e.tensor_tensor(out=eq, in0=pf[:P, :, :].unsqueeze(3).broadcast_to([P, 8, 16, 16]), in1=io16, op=ALU.is_equal),
               reads=[pf.b, CST.b], writes=[S1_.b])
            op("dve", lambda e: e.tensor_tensor(out=eq, in0=eq, in1=iv[:, :, half, :].unsqueeze(2).broadcast_to([P, 8, 16, 16]), op=ALU.mult),
               reads=[S1_.b, IDXF.b], writes=[S1_.b])
            op("dve", lambda e: e.tensor_reduce(out=dst[:P, :].rearrange("p (h k) -> p h k", h=8), in_=eq, axis=AX.X, op=ALU.add),
               reads=[S1_.b], writes=[dst.b])
        op("dve", lambda e: e.scalar_tensor_tensor(out=I1[:P, :], in0=I1[:P, :], scalar=128.0, in1=I2[:P, :], op0=ALU.mult, op1=ALU.add),
           reads=[I1.b, I2.b], writes=[I1.b])
        op("dve", lambda e: e.tensor_copy(out=IDX[:P, :], in_=I1[:P, :]), reads=[I1.b], writes=[IDX.b])
        gv = GATE[:P, :].rearrange("p (h k) -> p h k", h=8)
        op("dve", lambda e: e.tensor_tensor(out=gv, in0=CV[:P, :, :], in1=CV[:P, :, 0:1].broadcast_to([P, 8, 16]), op=ALU.subtract),
           reads=[CV.b], writes=[GATE.b])
        op("act", lambda e: e.activation(out=GATE[:P, :], in_=GATE[:P, :], func=AF.Exp), reads=[GATE.b], writes=[GATE.b])
        op("dve", lambda e: e.tensor_reduce(out=SMf[:P, 0:8], in_=gv, axis=AX.X, op=ALU.add), reads=[GATE.b], writes=[SMf.b])
        op("dve", lambda e: e.reciprocal(out=SMf[:P, 8:16], in_=SMf[:P, 0:8]), reads=[SMf.b], writes=[SMf.b])
        op("dve", lambda e: e.tensor_tensor(out=gv, in0=gv, in1=SMf[:P, 8:16].unsqueeze(2).broadcast_to([P, 8, 16]), op=ALU.mult),
           reads=[GATE.b, SMf.b], writes=[GATE.b])
        return items

    def run_items(items, n=None):
        n = len(items) if n is None else min(n, len(items))
        for _ in range(n):
            kind, e, fn, owner, reads, writes = items.pop(0)
            if kind == "op":
                kb.op(e, fn, reads=reads, writes=writes, bound=True)
            else:
                kb.dma(e, fn, owner, reads=reads, writes=writes, bound=True)

    def back(t, sset, nxt):
        op = kb.op
        is_s = (t == NT)
        P = SP if is_s else 128
        ada = ADAs if is_s else ADA
        H = Hs[sset]; IDX = IDXs[sset]; GATE = GATEs[sset]
        per = 0 if not nxt else (len(nxt) + 119) // 120

        def axpy(s):
            b = s % NB
            dg = DGB[s % 3]
            op("act", lambda e: e.activation(out=COEF2[:P, s:s + 1], in_=COEF[:P, s:s + 1], func=AF.Identity, scale=GATE[:P, s:s + 1]),
               reads=[COEFB[b], GATE.b], writes=[COEF2.b])
            op("act", lambda e: e.activation(out=dg[:P, :P], in_=C("ident", P, P), func=AF.Identity, scale=COEF2[:P, s:s + 1]),
               reads=[COEF2.b, CST.b], writes=[dg.b])
            for hf in range(2):
                op("pe", lambda e: e.matmul(ACCP[hf][:P, :], lhsT=dg[:P, :P], rhs=UB[b][:P, D + hf * 512:D + (hf + 1) * 512], start=(s == 0), stop=(s == 127)),
                   reads=[dg.b, UB[b].b], writes=[ACCP[hf].b] if s in (0, 127) else [])

        for s_ in range(128):
            b = s_ % NB
            kb.dma("pool", lambda q: q.indirect_dma_start(out=UB[b][:P, :], out_offset=None, in_=tab,
                                                          in_offset=bass.IndirectOffsetOnAxis(ap=IDX[:P, s_:s_ + 1], axis=0)),
                   UB[b].b, reads=[IDX.b, PUVB], writes=[UB[b].b])
            op("dve", lambda e: e.scalar_tensor_tensor(out=JUNK[:P, :], in0=UB[b][:P, 0:D], scalar=1.0, in1=H[:P, :],
                                                       op0=ALU.mult, op1=ALU.mult, accum_out=ACTV[:P, s_:s_ + 1]),
               reads=[UB[b].b, H.b], writes=[JUNK.b, ACTB[b]])
            op("act", lambda e: e.activation(out=COEF[:P, s_:s_ + 1], in_=ACTV[:P, s_:s_ + 1], func=AF.Gelu), reads=[ACTB[b]], writes=[COEFB[b]])
            if s_ >= 1:
                axpy(s_ - 1)
            if nxt:
                run_items(nxt, per)
        axpy(127)
        if nxt:
            run_items(nxt)
        for hf in range(2):
            op("dve", lambda e: e.tensor_tensor(out=ACC[:P, hf * 512:(hf + 1) * 512], in0=ACCP[hf][:P, :], in1=ada[:P, 2048 + hf * 512:2048 + (hf + 1) * 512],
                                                op=ALU.mult), reads=[ACCP[hf].b, ada.b], writes=[ACC.b])
        _resid_ln(kb, X, XB[t], t, P, ACC, SM, ST6, PRM, 0, D, E["EPSB"])

    _ada(nc, kb, l, E, ADA, 128, E["cp"], 3072, WCA, PM, Hs[0], HT, PT, badac)
    run_items(make_front(0, 0))
    for t in range(NT + 1):
        nxt = make_front(t + 1, (t + 1) % 2) if t + 1 < NT else None
        back(t, t % 2, nxt)
        if t + 1 == NT:
            _ada(nc, kb, l, E, ADAs, SP, E["cs"], 3072, WCA, PM, Hs[NT % 2], HT, PT, badac)
            run_items(make_front(NT, NT % 2))


_CACHE = {}


def _chunked(w, cw):
    L, K, n = w.shape
    nch = (n + cw - 1) // cw
    wp = np.zeros((L, K, nch * cw), np.float32)
    wp[:, :, :n] = w
    wp = wp.reshape(L, 8, 128, nch, cw).transpose(0, 3, 2, 1, 4)
    return np.ascontiguousarray(wp.reshape(L, nch, 128, 8 * cw))


def _rep(a, P=128):
    return np.ascontiguousarray(np.broadcast_to(a[:, None, :], (a.shape[0], P, a.shape[1])))


def make_in_maps(inp, cpack):
    f = lambda a: np.ascontiguousarray(np.asarray(a, dtype=np.float32))
    shared = {
        "w_ada": _chunked(f(inp["w_ada"]), WCW), "b_ada": _rep(f(inp["b_ada"])), "w_in": _chunked(f(inp["w_in"]), WCW),
        "b_gate": _rep(f(inp["b_gate"])),
        "mh_g": _rep(f(inp["mh_g"])), "sgu_g": _rep(f(inp["sgu_g"])), "sgu_b": _rep(f(inp["sgu_b"])),
        "pscale": _rep(f(inp["pool_scale"])),
        "w_sT": f(np.asarray(inp["w_s"]).transpose(0, 3, 1, 2)),
        "b_sT": f(np.asarray(inp["b_s"]).transpose(0, 2, 1)),
        "w_pool": f(np.asarray(inp["w_pool"]).transpose(0, 2, 1, 3)),
        "w_o": _chunked(f(inp["w_o"]), WCW), "ln1g": _rep(f(inp["ln1_g"])), "ln1b": _rep(f(inp["ln1_b"])),
        "ln2g": _rep(f(inp["ln2_g"])), "ln2b": _rep(f(inp["ln2_b"])), "w_pq": _chunked(f(inp["w_pq"]), 128),
        "keysT": f(np.asarray(inp["peer_keys"]).transpose(0, 4, 1, 2, 3).reshape(DEPTH, 128, 16, 128)),
        "cst": cpack[0], "cst1": cpack[1],
    }
    ws4 = np.asarray(inp["w_s"])[:, :, :ST, :ST]
    wsS = np.repeat(np.repeat(ws4.transpose(0, 3, 1, 2), SB, axis=1), SB, axis=3)
    shared["w_sS"] = f(wsS)
    bs4 = np.asarray(inp["b_s"])[:, :, :ST]
    shared["b_sS"] = f(np.repeat(bs4.transpose(0, 2, 1), SB, axis=1))
    for l in range(DEPTH):
        shared["puv%d" % l] = np.ascontiguousarray(
            np.concatenate([np.asarray(inp["peer_u"])[l], np.asarray(inp["peer_v"])[l]], axis=1), dtype=np.float32)
    maps = []
    for c in range(NCORES):
        bs = slice(c * SB, (c + 1) * SB)
        m = dict(shared)
        m["xp"] = f(np.asarray(inp["x_prompt"])[c])
        m["xs"] = f(np.asarray(inp["x_sample"])[bs].transpose(1, 0, 2).reshape(SP, D))
        m["cp"] = f(np.broadcast_to(np.asarray(inp["c_prompt"])[c][None, :], (128, D)))
        m["cs"] = f(np.tile(np.asarray(inp["c_sample"])[bs], (ST, 1)))
        sCc = np.asarray(inp["state_mlstm_C"])[:, bs]
        m["sC"] = f(sCc.transpose(0, 2, 3, 1, 4))
        snc = np.asarray(inp["state_mlstm_n"])[:, bs]
        m["snat"] = f(snc)
        m["snT"] = f(snc.transpose(0, 2, 3, 1))
        m["sm"] = f(np.tile(np.asarray(inp["state_mlstm_m"])[:, bs], (1, ST, 1)))
        spc = np.asarray(inp["state_pool"])[:, bs].transpose(0, 2, 1, 3)
        m["spA"] = f(spc[:, 0:8].reshape(DEPTH, 128, 256))
        m["spB"] = f(spc[:, 8:15].reshape(DEPTH, 112, 256))
        maps.append(m)
    return maps


def gather_outputs(results):
    cat = lambda k, ax: np.concatenate([r[k] for r in results], axis=ax)
    yp = np.stack([r["yp"] for r in results], 0)
    ys = np.concatenate([r["ys"].reshape(ST, SB, D).transpose(1, 0, 2) for r in results], 0)
    pC = np.stack([r["pC"] for r in results], 1)
    pn = np.stack([r["pn"] for r in results], 1)
    pm = np.stack([r["pm"] for r in results], 1)
    pp = np.stack([r["pp"] for r in results], 1)
    return (yp, ys, pC, pn, pm, pp, cat("nC", 1), cat("nn", 1), cat("nm", 1), cat("npool", 1), cat("nv", 1))


def kernel(**inputs):
    if "prog" not in _CACHE:
        _CACHE["prog"] = build_program()
    nc, cpack = _CACHE["prog"]
    maps = make_in_maps(inputs, cpack)
    res = run_bass_kernel_spmd(nc, maps, core_ids=list(range(NCORES)))
    outs = gather_outputs(res.results)
    return tuple(np.ascontiguousarray(o, dtype=np.float32) for o in outs)
```

```python
import numpy as np
from contextlib import ExitStack
import concourse.bass as bass
import concourse.mybir as mybir
from concourse.bass_utils import run_bass_kernel_spmd

F32 = mybir.dt.float32
I32 = mybir.dt.int32
U32 = mybir.dt.uint32
F32R = mybir.dt.float32r
BF16 = mybir.dt.bfloat16
ALU = mybir.AluOpType
AF = mybir.ActivationFunctionType
AX = mybir.AxisListType

NCORES = 8
D = 1024
SEQ = 2048
NT = 16
SB = 16
ST = 4
SP = SB * ST
DEPTH = 2
ALPHA = (2 * DEPTH) ** 0.25
LN_EPS = 1e-5
IN_COLS = 2824
NEG = -1.0e30
WCW = 192
NEXP = 16384
SAME_ENGINE_WAITS = True
NBUF = 8


class TB:
    def __init__(self, name, sem=None):
        self.name = name
        self.last_w = None
        self.reads = []
        self.sem = sem
        self.dma_total = 0
        self.dma_dirty = False


class KB:
    ENG = ("pe", "act", "dve", "pool", "sp")

    def __init__(self, nc, stack):
        self.nc = nc
        self.stack = stack
        self.q = {e: [] for e in self.ENG}
        self.cnt = {e: 0 for e in self.ENG}
        self.esem = {e: stack.enter_context(nc.semaphore("es_" + e)) for e in self.ENG}
        self.seen = {e: {} for e in self.ENG}
        self.semobj = {}
        self._sem_owner = {}
        self.stack0 = stack
        self.phase_tbs = []
        self.sem_pool = []
        self.nsem = 0
        self.sfx = ""

    def new_sem(self, name):
        return self.stack.enter_context(self.nc.semaphore(name + self.sfx))

    def buf(self, name, dma=False):
        if not dma:
            return TB(name)
        if self.sem_pool:
            sem, val = self.sem_pool.pop()
        else:
            sem, val = self.stack0.enter_context(self.nc.semaphore("dsem%d" % self.nsem)), 0
            self.nsem += 1
        tb = TB(name, sem)
        tb.dma_total = val
        if self.stack is not self.stack0:
            self.phase_tbs.append(tb)
        return tb

    def end_phase(self):
        for tb in self.phase_tbs:
            self._sem_owner.pop(id(tb.sem), None)
            self.sem_pool.append((tb.sem, tb.dma_total))
        self.phase_tbs = []

    def sb(self, name, shape, dt=F32):
        return self.stack.enter_context(self.nc.sbuf_tensor(name + self.sfx, list(shape), dt))

    def ps(self, name, shape, dt=F32):
        return self.stack.enter_context(self.nc.psum_tensor(name + self.sfx, list(shape), dt))

    def _deps(self, e, reads, writes):
        deps = {}

        def add(tok):
            if tok is None:
                return
            s, v = tok
            k = id(s)
            self.semobj[k] = s
            ow = self._sem_owner.get(k)
            if ow is not None:
                v = ow.dma_total
            if v > deps.get(k, 0):
                deps[k] = v
        for b in reads:
            add(b.last_w)
        for b in writes:
            add(b.last_w)
            for r in b.reads:
                add(r)
        out = []
        own = id(self.esem[e])
        for k, v in deps.items():
            if k == own and (e in ("pe", "sp") or not SAME_ENGINE_WAITS):
                continue
            if self.seen[e].get(k, 0) >= v:
                continue
            self.seen[e][k] = v
            out.append((self.semobj[k], v))
        return out

    def op(self, e, fn, reads=(), writes=(), bound=False):
        waits = self._deps(e, reads, writes)
        for s, v in waits:
            tb = self._sem_owner.get(id(s))
            if tb is not None:
                tb.dma_dirty = True
        self.cnt[e] += 1
        tok = (self.esem[e], self.cnt[e])
        self.q[e].append((waits, fn if bound else _bind(fn), tok[0], 1))
        for b in reads:
            b.reads.append(tok)
        for b in writes:
            b.last_w = tok
            b.reads = []
        return tok

    def dma(self, e, fn, owner, reads=(), writes=(), bound=False):
        self._sem_owner[id(owner.sem)] = owner
        waits = self._deps(e, reads, writes)
        if owner.dma_dirty and owner.dma_total > 0:
            k = id(owner.sem)
            if self.seen[e].get(k, 0) < owner.dma_total:
                self.seen[e][k] = owner.dma_total
                waits.append((owner.sem, owner.dma_total))
            owner.dma_dirty = False
        for s, v in waits:
            tb = self._sem_owner.get(id(s))
            if tb is not None and tb is not owner:
                tb.dma_dirty = True
        owner.dma_total += 16
        tok = (owner.sem, owner.dma_total)
        self.q[e].append((waits, fn if bound else _bind(fn), owner.sem, 16))
        for b in reads:
            b.reads.append(tok)
        for b in writes:
            b.last_w = tok
            b.reads = []
        return tok

    def barrier(self, extra=()):
        toks = [(self.esem[e], self.cnt[e]) for e in self.ENG if self.cnt[e] > 0 and e != "sp"]
        for tb in list(self._sem_owner.values()) + list(extra):
            if tb.dma_total > 0:
                toks.append((tb.sem, tb.dma_total))
        for e in self.ENG:
            waits = []
            for s, v in toks:
                k = id(s)
                if k == id(self.esem[e]):
                    continue
                if self.seen[e].get(k, 0) >= v:
                    continue
                self.seen[e][k] = v
                waits.append((s, v))
            if waits:
                self.q[e].append((waits, None, None, 0))

    def emit(self, final_waits=()):
        nc = self.nc
        engs = {"pe": "tensor", "act": "scalar", "dve": "vector", "pool": "gpsimd", "sp": "sync"}
        with nc.Block() as block:
            for e in self.ENG:
                items = self.q[e]
                fw = list(final_waits) if e == "sp" else []

                def body(eng, items=items, fw=fw):
                    for waits, fn, sem, inc in items:
                        for s, v in waits:
                            eng.wait_ge(s, v)
                        if fn is not None:
                            fn(eng).then_inc(sem, inc)
                    for s, v in fw:
                        eng.wait_ge(s, v)
                getattr(block, engs[e])(body)
        self.q = {e: [] for e in self.ENG}


class _Rec:
    def __init__(self):
        self.call = None

    def __getattr__(self, name):
        def f(*a, **k):
            self.call = (name, a, k)
            return self
        return f


def _bind(fn):
    r = _Rec()
    fn(r)
    assert r.call is not None
    name, a, k = r.call
    return lambda eng: getattr(eng, name)(*a, **k)


class Tn:
    def __init__(self, kb, name, shape, dt=F32, psum=False, dma=False):
        self.t = kb.ps(name, shape, dt) if psum else kb.sb(name, shape, dt)
        self.b = kb.buf(name, dma=dma)

    def __getitem__(self, k):
        return self.t[k]


def _consts():
    c = {}
    i128 = np.arange(128)
    c["ident"] = np.eye(128, dtype=np.float32)
    c["ones"] = np.ones((128, 128), np.float32)
    c["triu"] = (i128[:, None] <= i128[None, :]).astype(np.float32)
    c["negm"] = np.where(i128[None, :] <= i128[:, None], 0.0, NEG).astype(np.float32)
    sel = np.zeros((128, 128), np.float32); sel[127, :] = 1.0
    c["sel127"] = sel
    p = np.arange(SP); tt = p // SB; bb = p % SB
    sameb = bb[:, None] == bb[None, :]
    tri_s = (sameb & (tt[:, None] <= tt[None, :])).astype(np.float32)
    c["tri_s"] = _pad(tri_s)
    c["negm_s"] = _pad(np.where(sameb & (tt[None, :] <= tt[:, None]), 0.0, NEG).astype(np.float32))
    c["negb_s"] = _pad(np.where(sameb, 0.0, NEG).astype(np.float32))
    c["selend"] = _pad(((tt[:, None] == ST - 1) & sameb).astype(np.float32))
    oh = (bb[:, None] == np.arange(SB)[None, :]).astype(np.float32)
    c["onehotB"] = _pad(oh, cols=16)
    oh0 = ((p[:, None] == np.arange(SB)[None, :])).astype(np.float32)
    c["onehot0"] = _pad(oh0, cols=16)
    c["iota16"] = np.broadcast_to(np.arange(16, dtype=np.float32), (128, 16)).copy()
    wins = (2, 4, 8, 16)
    bc0 = np.zeros((4, 128, 128), np.float32); bc = np.zeros((4, 128, 128), np.float32)
    bp = np.zeros((4, 128, 128), np.float32)
    for g, w in enumerate(wins):
        for t in range(128):
            for j in range(w):
                s = t - j
                if s >= 0:
                    bc[g, s, t] += 1.0 / w
                    bc0[g, s, t] += 1.0 / min(t + 1, w)
                else:
                    bp[g, s + 128, t] += 1.0 / w
            bc[g, t, t] -= 1.0
            bc0[g, t, t] -= 1.0
    c["bandc0"] = bc0.transpose(1, 0, 2).reshape(128, 512)
    c["bandc"] = bc.transpose(1, 0, 2).reshape(128, 512)
    c["bandp"] = bp.transpose(1, 0, 2).reshape(128, 512)
    bsA = np.zeros((4, 128, SP), np.float32); bsB = np.zeros((4, 128, SP), np.float32)
    bsC = np.zeros((4, 128, SP), np.float32)
    for g, w in enumerate(wins):
        for t in range(ST):
            for b in range(SB):
                col = t * SB + b
                for j in range(w):
                    r = 15 + t - j
                    if r >= 15:
                        bsC[g, (r - 15) * SB + b, col] += 1.0 / w
                    elif r >= 8:
                        bsB[g, (r - 8) * SB + b, col] += 1.0 / w
                    else:
                        bsA[g, r * SB + b, col] += 1.0 / w
                bsC[g, t * SB + b, col] -= 1.0
    c["bsA"] = bsA.transpose(1, 0, 2).reshape(128, 4 * SP)
    c["bsB"] = bsB.transpose(1, 0, 2).reshape(128, 4 * SP)
    c["bsC"] = bsC.transpose(1, 0, 2).reshape(128, 4 * SP)
    return c


def _pad(a, cols=None):
    out = np.zeros((128, a.shape[1] if cols is None else cols), np.float32)
    out[: a.shape[0], : a.shape[1]] = a
    return out


_CONST_G = ["ident", "ones", "iota16"]
_CONST_1 = ["triu", "negm", "sel127", "tri_s", "negm_s", "negb_s", "selend",
            "onehotB", "onehot0", "bandc0", "bandc", "bandp", "bsA", "bsB", "bsC"]


def _const_pack():
    c = _consts()
    packs = []
    for order in (_CONST_G, _CONST_1):
        offs = {}
        o = 0
        arrs = []
        for k in order:
            offs[k] = (o, c[k].shape[1])
            o += c[k].shape[1]
            arrs.append(c[k])
        packs.append((np.ascontiguousarray(np.concatenate(arrs, axis=1)), offs))
    return packs


def build_program(n_layers=DEPTH, do_phase2=True):
    (cpack, coff), (cpack1, coff1) = _const_pack()
    NCST = cpack.shape[1]
    NCST1 = cpack1.shape[1]
    nc = bass.Bass("TRN2", target_bir_lowering=False)

    def din(name, shape, dt=F32):
        return nc.dram_tensor(name, list(shape), dt, kind="ExternalInput").ap()

    def dout(name, shape, dt=F32):
        return nc.dram_tensor(name, list(shape), dt, kind="ExternalOutput").ap()

    xp = din("xp", [SEQ, D]); xs = din("xs", [SP, D])
    cp = din("cp", [128, D]); cs = din("cs", [SP, D])
    sC = din("sC", [DEPTH, 4, 128, SB, 128]); snat = din("snat", [DEPTH, SB, 4, 128])
    snT = din("snT", [DEPTH, 4, 128, SB]); sm = din("sm", [DEPTH, SP, 4])
    spA = din("spA", [DEPTH, 128, 256]); spB = din("spB", [DEPTH, 112, 256])
    w_ada = din("w_ada", [DEPTH, (6 * D) // WCW, 128, 8 * WCW]); b_ada = din("b_ada", [DEPTH, 128, 6 * D])
    w_in = din("w_in", [DEPTH, (IN_COLS + WCW - 1) // WCW, 128, 8 * WCW]); b_gate = din("b_gate", [DEPTH, 128, 8])
    mh_g = din("mh_g", [DEPTH, 128, 512]); sgu_g = din("sgu_g", [DEPTH, 128, 256])
    sgu_b = din("sgu_b", [DEPTH, 128, 256]); pscale = din("pscale", [DEPTH, 128, 256])
    w_sT = din("w_sT", [DEPTH, 128, 4, 128]); b_sT = din("b_sT", [DEPTH, 128, 4])
    w_sS = din("w_sS", [DEPTH, SP, 4, SP]); b_sS = din("b_sS", [DEPTH, SP, 4])
    w_pool = din("w_pool", [DEPTH, 64, 4, 64]); w_o = din("w_o", [DEPTH, (D + WCW - 1) // WCW, 128, 8 * WCW])
    ln1g = din("ln1g", [DEPTH, 128, D]); ln1b = din("ln1b", [DEPTH, 128, D])
    ln2g = din("ln2g", [DEPTH, 128, D]); ln2b = din("ln2b", [DEPTH, 128, D])
    w_pq = din("w_pq", [DEPTH, 16, 128, 8 * 128]); keysT = din("keysT", [DEPTH, 128, 16, 128])
    puv = [din("puv%d" % l, [NEXP, 2 * D]) for l in range(DEPTH)]
    puvb = [nc.dram_tensor("puvb%d" % l, [NEXP, 2 * D], BF16, kind="Internal").ap() for l in range(DEPTH)]
    cst_d = din("cst", [128, NCST])
    cst1_d = din("cst1", [128, NCST1])

    yp = dout("yp", [SEQ, D]); ys = dout("ys", [SP, D])
    o_pC = dout("pC", [DEPTH, 4, 128, 128]); o_pn = dout("pn", [DEPTH, 4, 128]); o_pm = dout("pm", [DEPTH, 4])
    o_pp = dout("pp", [DEPTH, 15, 256])
    o_nC = dout("nC", [DEPTH, SB, 4, 128, 128]); o_nn = dout("nn", [DEPTH, SB, 4, 128])
    o_nm = dout("nm", [DEPTH, SB, 4]); o_np = dout("npool", [DEPTH, SB, 15, 256])
    o_nv = dout("nv", [DEPTH, SB, ST, 256])

    with ExitStack() as st0:
        kb = KB(nc, st0)
        op = kb.op
        OUT = kb.buf("outs", dma=True)

        def out_dma(dst, src, reads):
            kb.dma("sp", lambda q: q.dma_start(out=dst, in_=src), OUT, reads=reads)

        X = kb.sb("X", [128, NT + 1, D])
        XB = [kb.buf("X%d" % t) for t in range(NT + 1)]
        XL = kb.buf("xload", dma=True)
        CST = Tn(kb, "CST", [128, NCST], dma=True)
        PUVBT = [kb.buf("puvb%d" % i, dma=True) for i in range(DEPTH)]
        EPSB = Tn(kb, "EPSB", [128, 1])
        kb.op("dve", lambda e: e.memset(EPSB[:, :], LN_EPS), writes=[EPSB.b])

        def C(name, P=128, w=None):
            if name in coff:
                o, n = coff[name]
                return CST[:P, o:o + (n if w is None else w)]
            o, n = coff1[name]
            return kb.cst1[:P, o:o + (n if w is None else w)]

        def Cg(name, g, P, blk, w):
            o, n = coff1[name]
            return kb.cst1[:P, o + g * blk: o + g * blk + w]

        with nc.allow_non_contiguous_dma(reason="small strided state/param loads"):
            kb.dma("sp", lambda q: q.dma_start(out=CST[:, :], in_=cst_d), CST.b, writes=[CST.b])
            for t in range(NT):
                kb.dma("sp", lambda q, t=t: q.dma_start(out=X[:, t, :], in_=xp[t * 128:(t + 1) * 128, :]),
                       XL, writes=[XB[t]])
            kb.dma("sp", lambda q: q.dma_start(out=X[:SP, NT, :], in_=xs), XL, writes=[XB[NT]])

            puvb_ = puvb
            for l in range(n_layers):
                for part in ("p", "s"):
                    with ExitStack() as st1:
                        kb.stack = st1
                        kb.sfx = "_a%s%d" % (part, l)
                        _phase1(nc, kb, l, locals(), part)
                        kb.barrier(extra=[OUT])
                        kb.emit()
                        kb.end_phase()
                if do_phase2:
                    with ExitStack() as st2:
                        kb.stack = st2
                        kb.sfx = "_b%d" % l
                        _phase2(nc, kb, l, locals())
                        kb.barrier(extra=[OUT])
                        kb.emit()
                        kb.end_phase()
            kb.stack = st0
            kb.sfx = ""
            for t in range(NT):
                out_dma(yp[t * 128:(t + 1) * 128, :], X[:, t, :], [XB[t]])
            out_dma(ys, X[:SP, NT, :], [XB[NT]])
            kb.emit(final_waits=[(OUT.sem, OUT.dma_total)])
    return nc, (cpack, cpack1)


def _ada(nc, kb, l, E, ADA, P, csrc, off, WCH, PM, hbuf, hT, PT, badac, WCR=None):
    op = kb.op
    C = E["C"]
    w_ada, b_ada = E["w_ada"], E["b_ada"]
    kb.dma("sp", lambda q: q.dma_start(out=hbuf[:P, :], in_=csrc), hbuf.b, writes=[hbuf.b])
    op("act", lambda e: e.activation(out=hbuf[:P, :], in_=hbuf[:P, :], func=AF.Silu), reads=[hbuf.b], writes=[hbuf.b])
    _transpose8(kb, E, hbuf, hT, PT, P)
    r32 = (hT.t.dtype == F32R)
    for c in range(3072 // WCW):
        i = c % 2
        c0 = off + c * WCW
        wch = WCH[c % len(WCH)]
        kb.dma("sp", lambda q: q.dma_start(out=wch[:, :, 0:WCW], in_=w_ada[l, c0 // WCW].rearrange("p (k j) -> p k j", k=8)), wch.b, writes=[wch.b])
        kb.dma("sp", lambda q: q.dma_start(out=badac[i][:P, 0:WCW], in_=b_ada[l, :P, c0:c0 + WCW]), badac[i].b, writes=[badac[i].b])
        wsrc = WCR[c % 2] if r32 else wch
        if r32:
            op("act", lambda e: e.copy(out=wsrc[:, :, 0:WCW], in_=wch[:, :, 0:WCW]), reads=[wch.b], writes=[wsrc.b])
        for k in range(8):
            if r32:
                op("pe", lambda e: e.matmul(PM[i][:, 0:WCW], lhsT=hT[:, k, :], rhs=wsrc[:, k, 0:WCW], start=(k == 0), stop=(k == 7)),
                   reads=[hT.b, wsrc.b], writes=[PM[i].b] if k in (0, 7) else [])
            else:
                op("pe", lambda e: e.matmul(PM[i][:P, 0:WCW], lhsT=hT[:, k, :P], rhs=wch[:, k, 0:WCW], start=(k == 0), stop=(k == 7)),
                   reads=[hT.b, wch.b], writes=[PM[i].b] if k in (0, 7) else [])
        op("dve", lambda e: e.tensor_tensor(out=ADA[:P, c * WCW:(c + 1) * WCW], in0=PM[i][:P, 0:WCW], in1=badac[i][:P, 0:WCW], op=ALU.add),
           reads=[PM[i].b, badac[i].b], writes=[ADA.b])
    op("dve", lambda e: e.tensor_scalar_add(out=ADA[:P, 1024:2048], in0=ADA[:P, 1024:2048], scalar1=1.0), reads=[ADA.b], writes=[ADA.b])


def _transpose8(kb, E, src, dstT, PT, P, srcb=None, op=None):
    op = kb.op if op is None else op
    C = E["C"]
    sb_ = src.b if srcb is None else srcb
    for half in range(2):
        for j in range(4):
            k = half * 4 + j
            op("pe", lambda e, half=half, j=j, k=k: e.transpose(
                out=PT[half][:, j * 128:j * 128 + P], in_=src[:P, k * 128:(k + 1) * 128], identity=C("ident", P, P)),
               reads=[sb_, E["CST"].b], writes=[PT[half].b])
        op("act", lambda e, half=half: e.copy(
            out=dstT[:, half * 4:half * 4 + 4, :P],
            in_=PT[half][:, :].rearrange("p (j c) -> p j c", j=4)[:, :, :P]),
           reads=[PT[half].b], writes=[dstT.b])


def _phase1(nc, kb, l, E, part):
    isS = (part == "s")
    tiles = [NT] if isS else list(range(NT))
    cur = [None]
    cnt = [0]
    convq = [None]

    def run_items(items, n=None):
        n = len(items) if n is None else min(n, len(items))
        for _ in range(n):
            kind, e, fn, owner, reads, writes = items.pop(0)
            if kind == "op":
                kb.op(e, fn, reads=reads, writes=writes, bound=True)
            else:
                kb.dma(e, fn, owner, reads=reads, writes=writes, bound=True)

    def op(e, fn, reads=(), writes=()):
        tok = kb.op(e, fn, reads=reads, writes=writes)
        if cur[0]:
            run_items(cur[0], 1)
        cnt[0] += 1
        if convq[0] and cnt[0] % 16 == 0:
            run_items(convq[0], 1)
        return tok
    C, Cg, CST, X, XB = E["C"], E["Cg"], E["CST"], E["X"], E["XB"]
    EPSB = E["EPSB"]
    out_dma = E["out_dma"]
    w_in, w_o = E["w_in"], E["w_o"]

    kb.cst1 = kb.sb("CST1", [128, E["NCST1"]])
    kb.dma("sp", lambda q: q.dma_start(out=kb.cst1[:, :], in_=E["cst1_d"]), CST.b, writes=[CST.b])
    ADA = Tn(kb, "ADA1", [128, 3072]); ADAs = ADA
    WCH = [Tn(kb, "WCH%d" % i, [128, 8, WCW], dma=True) for i in range(2)]
    WCR = [Tn(kb, "WCR%d" % i, [128, 8, WCW], F32R) for i in range(2)]
    badac = [Tn(kb, "bada%d" % i, [128, 256], dma=True) for i in range(2)]
    nbuf = 1 if isS else 2
    Hs = [Tn(kb, "H%d" % i, [128, D], dma=True) for i in range(nbuf)]
    HTs = [Tn(kb, "HT%d" % i, [128, 8, 128], F32R) for i in range(nbuf)]
    PROJs = [Tn(kb, "PROJ%d" % i, [128, IN_COLS], dma=True) for i in range(nbuf)]
    H, HT, PROJ = Hs[0], HTs[0], PROJs[0]
    Y = Tn(kb, "Y", [128, D])
    PRM = Tn(kb, "PRM", [128, 8 + 512 + 256 * 3 + 2 * D], dma=True)
    WS = BS = WSs = BSs = None
    if isS:
        WSs = Tn(kb, "WSs", [128, 4, SP], dma=True); BSs = Tn(kb, "BSs", [128, 4], dma=True)
    else:
        WS = Tn(kb, "WS", [128, 4, 128], dma=True); BS = Tn(kb, "BS", [128, 4], dma=True)
    WP = Tn(kb, "WP", [64, 4, 64], dma=True)
    PT = [Tn(kb, "PT%d" % i, [128, 512], psum=True) for i in range(2)]
    PM = [Tn(kb, "PM%d" % i, [128, 512], psum=True) for i in range(2)]
    PA = Tn(kb, "PA", [128, 512], psum=True); PB = Tn(kb, "PB", [128, 512], psum=True)
    PC = Tn(kb, "PC", [128, 512], psum=True); PD = Tn(kb, "PD", [128, 512], psum=True)
    SM = Tn(kb, "SM", [128, 64])
    SMs = MREP = CTX = None
    if isS:
        SMs = Tn(kb, "SMs", [128, 4], dma=True)
    else:
        MREP = Tn(kb, "MREP", [128, 4])
        CTX = Tn(kb, "CTX", [128, 4, 129], dma=True)
    DG = Tn(kb, "DG", [128, 128]); DL = Tn(kb, "DL", [128, 128]); WI = Tn(kb, "WI", [128, 128])
    AM = Tn(kb, "AM", [128, 128]); AT = Tn(kb, "AT", [128, 128])
    QT = Tn(kb, "QT", [128, 128]); KT = Tn(kb, "KT", [128, 128])
    VX = Tn(kb, "VX", [128, 129]); TOT = Tn(kb, "TOT", [128, 129]); WV = Tn(kb, "WV", [128, 129])
    HN = Tn(kb, "HN", [128, 128]); SG = Tn(kb, "SG", [128, 128]); ST6 = Tn(kb, "ST6", [128, 2, 6])
    OUTC = None if isS else Tn(kb, "OUTC", [128, 128], dma=True)
    CN = CTS = RA = ZQ = NNAT = NTH = WCB = DECD = DECR = MSO = SPA = SPB = PREV = None
    if isS:
        CN = Tn(kb, "CN", [128, SB, 128], dma=True); CTS = Tn(kb, "CTS", [128, SB, 129])
        RA = Tn(kb, "RA", [128, SB, 128])

    class _V2:
        def __init__(self, ap, b):
            self.t = ap
            self.b = b

        def __getitem__(self, k):
            return self.t[k]
    if isS:
        ZQ = _V2(RA[:, :, :].rearrange("p a b -> p (a b)")[:, 0:SB * SP], RA.b)
        NNAT = Tn(kb, "NNAT", [SB, 4, 128], dma=True); NTH = Tn(kb, "NTH", [128, SB], dma=True)
        WCB = Tn(kb, "WCB", [128, 16]); DECD = Tn(kb, "DECD", [128, 16]); DECR = Tn(kb, "DECR", [128, 16])
        MSO = Tn(kb, "MSO", [SB, 4], dma=True)
        SPA = Tn(kb, "SPA", [128, 256], dma=True); SPB = Tn(kb, "SPB", [128, 256], dma=True)
    else:
        PREV = Tn(kb, "PREV", [128, 256])
    PTT = Tn(kb, "PTT", [64, 4, 128])
    VN = Tn(kb, "VN", [128, 256], dma=True); VTMP = Tn(kb, "VTMP", [128, 256])

    o_bg, o_mh, o_sg, o_sb, o_ps, o_l1g, o_l1b = 0, 8, 520, 776, 1032, 1288, 1288 + D
    for (o, w, src) in [(o_bg, 8, E["b_gate"]), (o_mh, 512, E["mh_g"]), (o_sg, 256, E["sgu_g"]), (o_sb, 256, E["sgu_b"]),
                        (o_ps, 256, E["pscale"]), (o_l1g, D, E["ln1g"]), (o_l1b, D, E["ln1b"])]:
        kb.dma("sp", lambda q, o=o, w=w, src=src: q.dma_start(out=PRM[:, o:o + w], in_=src[l]), PRM.b, writes=[PRM.b])
    kb.dma("sp", lambda q: q.dma_start(out=WP[:, :, :], in_=E["w_pool"][l]), WP.b, writes=[WP.b])
    if isS:
        kb.dma("sp", lambda q: q.dma_start(out=WSs[:SP, :, :], in_=E["w_sS"][l]), WSs.b, writes=[WSs.b])
        kb.dma("sp", lambda q: q.dma_start(out=BSs[:SP, :], in_=E["b_sS"][l]), BSs.b, writes=[BSs.b])
    else:
        kb.dma("sp", lambda q: q.dma_start(out=WS[:, :, :], in_=E["w_sT"][l]), WS.b, writes=[WS.b])
        kb.dma("sp", lambda q: q.dma_start(out=BS[:, :], in_=E["b_sT"][l]), BS.b, writes=[BS.b])
    for g in range(4):
        if isS:
            op("dve", lambda e, g=g: e.tensor_tensor(out=WSs[:SP, g, :], in0=WSs[:SP, g, :], in1=C("tri_s", SP, SP), op=ALU.mult),
               reads=[WSs.b, CST.b], writes=[WSs.b])
        else:
            op("dve", lambda e, g=g: e.tensor_tensor(out=WS[:, g, :], in0=WS[:, g, :], in1=C("triu"), op=ALU.mult),
               reads=[WS.b, CST.b], writes=[WS.b])
    if not isS:
        op("dve", lambda e: e.memset(CTX[:, :, :], 0.0), writes=[CTX.b])
        op("dve", lambda e: e.memset(MREP[:, :], 0.0), writes=[MREP.b])
    op("dve", lambda e: e.memset(VX[:, :], 1.0), writes=[VX.b])

    if isS:
        _ada(nc, kb, l, E, ADA, SP, E["cs"], 0, WCH, PM, H, HT, PT, badac, WCR)
    else:
        _ada(nc, kb, l, E, ADA, 128, E["cp"], 0, WCH, PM, H, HT, PT, badac, WCR)

    w_in_v = w_in[l]
    w_o_v = w_o[l]
    def mk_chunks(n):
        return [(c0, min(WCW, n - c0)) for c0 in range(0, n, WCW)]
    chunks = mk_chunks(IN_COLS)
    wctr = [0]

    def stream_mm(wview, c0, w, lhsT, P, evac, op=op, dma=kb.dma):
        i = wctr[0] % 2
        wctr[0] += 1
        dma("sp", lambda q: q.dma_start(out=WCH[i][:, :, :], in_=wview[c0 // WCW].rearrange("p (k j) -> p k j", k=8)), WCH[i].b, writes=[WCH[i].b])
        wr = WCR[i]
        if wctr[0] % 3 == 0:
            op("dve", lambda e: e.tensor_copy(out=wr[:, :, 0:w], in_=WCH[i][:, :, 0:w]), reads=[WCH[i].b], writes=[wr.b])
        else:
            op("act", lambda e: e.copy(out=wr[:, :, 0:w], in_=WCH[i][:, :, 0:w]), reads=[WCH[i].b], writes=[wr.b])
        for k in range(8):
            op("pe", lambda e, k=k: e.matmul(PM[i][:, 0:w], lhsT=lhsT[:, k, :], rhs=wr[:, k, 0:w],
                                              start=(k == 0), stop=(k == 7)),
               reads=[lhsT.b, wr.b], writes=[PM[i].b] if k in (0, 7) else [])
        evac(PM[i], i)

    def make_A(t):
        items = []

        def iop(e, fn, reads=(), writes=()):
            items.append(("op", e, _bind(fn), None, tuple(reads), tuple(writes)))

        def idma(e, fn, owner, reads=(), writes=()):
            items.append(("dma", e, _bind(fn), owner, tuple(reads), tuple(writes)))
        P = SP if isS else 128
        H, HT, PROJ = Hs[t % nbuf], HTs[t % nbuf], PROJs[t % nbuf]
        xt = X[:P, t, :]
        iop("dve", lambda e: e.tensor_tensor(out=H[:P, :], in0=xt, in1=ADA[:P, 1024:2048], op=ALU.mult), reads=[XB[t], ADA.b], writes=[H.b])
        iop("dve", lambda e: e.tensor_tensor(out=H[:P, :], in0=H[:P, :], in1=ADA[:P, 0:1024], op=ALU.add), reads=[H.b, ADA.b], writes=[H.b])
        _transpose8(kb, E, H, HT, PT, P, op=iop)
        for (c0, w) in chunks:
            stream_mm(w_in_v, c0, w, HT, P,
                      lambda pm, i, c0=c0, w=w: iop("act", lambda e: e.copy(out=PROJ[:P, c0:c0 + w], in_=pm[:P, 0:w]),
                                                    reads=[pm.b], writes=[PROJ.b]), op=iop, dma=idma)
        return items

    conv = []
    if not isS:
        CBs = [Tn(kb, "CB%d" % i, [128, 2 * D], BF16, dma=True) for i in range(2)]
        tab32 = E["puv"][l]
        tabb = E["puvb"][l]
        PUVB = E["PUVBT"][l]
        for blk in range(NEXP // 128):
            cb = CBs[blk % 2]
            conv.append(("dma", "pool", _bind(lambda q: q.dma_start(out=cb[:, :], in_=tab32[blk * 128:(blk + 1) * 128, :])), cb.b, (), (cb.b,)))
            conv.append(("dma", "sp", _bind(lambda q: q.dma_start(out=tabb[blk * 128:(blk + 1) * 128, :], in_=cb[:, :])), PUVB, (cb.b,), (PUVB,)))
    convq[0] = conv
    run_items(make_A(tiles[0]))
    for ti, t in enumerate(tiles):
        is_s = isS
        P = SP if is_s else 128
        ada = ADA
        H, HT, PROJ = Hs[t % nbuf], HTs[t % nbuf], PROJs[t % nbuf]
        xt = X[:P, t, :]
        nxtA = make_A(tiles[ti + 1]) if ti + 1 < len(tiles) else None
        cur[0] = nxtA
        tri = C("tri_s", SP, SP) if is_s else C("triu")
        negm = C("negm_s", SP, SP) if is_s else C("negm")
        selE = C("selend", SP, SP) if is_s else C("sel127")
        if is_s:
            kb.dma("sp", lambda q: q.dma_start(out=SMs[:SP, :], in_=E["sm"][l]), SMs.b, writes=[SMs.b])
            kb.dma("sp", lambda q: q.dma_start(out=NNAT[:, :, :], in_=E["snat"][l]), NNAT.b, writes=[NNAT.b])
        mtok = SMs if is_s else MREP
        op("dve", lambda e: e.tensor_tensor(out=SM[:P, 0:8], in0=PROJ[:P, 2048:2056], in1=PRM[:P, o_bg:o_bg + 8], op=ALU.add),
           reads=[PROJ.b, PRM.b], writes=[SM.b])
        op("dve", lambda e: e.scalar_tensor_tensor(out=SM[:P, 8:12], in0=SM[:P, 4:8], scalar=-1.0, in1=SM[:P, 4:8], op0=ALU.mult, op1=ALU.max),
           reads=[SM.b], writes=[SM.b])
        op("act", lambda e: e.activation(out=SM[:P, 12:16], in_=SM[:P, 8:12], func=AF.Exp, scale=-1.0), reads=[SM.b], writes=[SM.b])
        op("act", lambda e: e.activation(out=SM[:P, 12:16], in_=SM[:P, 12:16], func=AF.Ln, bias=1.0, scale=1.0),
           reads=[SM.b], writes=[SM.b])
        op("dve", lambda e: e.tensor_scalar_min(out=SM[:P, 16:20], in0=SM[:P, 4:8], scalar1=0.0), reads=[SM.b], writes=[SM.b])
        op("dve", lambda e: e.tensor_tensor(out=SM[:P, 16:20], in0=SM[:P, 16:20], in1=SM[:P, 12:16], op=ALU.subtract),
           reads=[SM.b], writes=[SM.b])
        op("pe", lambda e: e.matmul(PA[:P, 0:4], lhsT=tri, rhs=SM[:P, 16:20], start=True, stop=True),
           reads=[CST.b, SM.b], writes=[PA.b])
        op("act", lambda e: e.copy(out=SM[:P, 20:24], in_=PA[:P, 0:4]), reads=[PA.b], writes=[SM.b])
        op("dve", lambda e: e.tensor_tensor(out=SM[:P, 24:28], in0=SM[:P, 0:4], in1=SM[:P, 20:24], op=ALU.subtract),
           reads=[SM.b], writes=[SM.b])
        op("pe", lambda e: e.matmul(PA[:P, 8:12], lhsT=selE, rhs=SM[:P, 20:24], start=True, stop=True),
           reads=[CST.b, SM.b], writes=[PA.b])
        op("act", lambda e: e.copy(out=SM[:P, 28:32], in_=PA[:P, 8:12]), reads=[PA.b], writes=[SM.b])

        for hh in range(4):
            qs = PROJ[:P, hh * 128:(hh + 1) * 128]
            ks = PROJ[:P, 512 + hh * 128:512 + (hh + 1) * 128]
            vs = PROJ[:P, 1024 + hh * 128:1024 + (hh + 1) * 128]
            os_ = PROJ[:P, 1536 + hh * 128:1536 + (hh + 1) * 128]
            col = lambda c, hh=hh: SM[:P, c + hh:c + hh + 1]
            S1 = lambda c: SM[:P, c:c + 1]
            if is_s:
                kb.dma("sp", lambda q, hh=hh: q.dma_start(out=CN[:, :, :], in_=E["sC"][l, hh]), CN.b, writes=[CN.b])
                kb.dma("sp", lambda q, hh=hh: q.dma_start(out=NTH[:, :], in_=E["snT"][l, hh]), NTH.b, writes=[NTH.b])
                for j in range(4):
                    pt = PT[j % 2]
                    for jj in range(4):
                        b = j * 4 + jj
                        op("pe", lambda e, b=b, jj=jj, pt=pt: e.transpose(out=pt[:, jj * 128:(jj + 1) * 128], in_=CN[:, b, :],
                                                                       identity=C("ident")),
                           reads=[CN.b, CST.b], writes=[pt.b])
                    op("act", lambda e, j=j, pt=pt: e.copy(out=CTS[:, j * 4:(j + 1) * 4, 0:128],
                                                           in_=pt[:, :].rearrange("p (j c) -> p j c", j=4)),
                       reads=[pt.b], writes=[CTS.b])
                op("dve", lambda e: e.tensor_copy(out=CTS[:, :, 128:129], in_=NTH[:, :].unsqueeze(2)), reads=[NTH.b], writes=[CTS.b])
            op("dve", lambda e, hh=hh: e.tensor_scalar(out=DG[:P, :P], in0=C("ident", P, P), scalar1=col(24), scalar2=None,
                                                       op0=ALU.mult), reads=[SM.b, CST.b], writes=[DG.b])
            op("pe", lambda e: e.matmul(PB[:P, 0:P], lhsT=C("ones", P, P), rhs=DG[:P, :P], start=True, stop=True),
               reads=[DG.b, CST.b], writes=[PB.b])
            if is_s:
                op("dve", lambda e: e.tensor_tensor(out=DL[:P, :P], in0=PB[:P, 0:P], in1=C("negb_s", SP, SP), op=ALU.add),
                   reads=[PB.b, CST.b], writes=[DL.b])
                op("dve", lambda e: e.tensor_reduce(out=S1(32), in_=DL[:P, :P], axis=AX.X, op=ALU.max), reads=[DL.b], writes=[SM.b])
            else:
                op("dve", lambda e: e.tensor_reduce(out=S1(32), in_=PB[:P, 0:P], axis=AX.X, op=ALU.max), reads=[PB.b], writes=[SM.b])
            op("dve", lambda e, hh=hh: e.scalar_tensor_tensor(out=DL[:P, :P], in0=PB[:P, 0:P], scalar=col(20), in1=negm,
                                                              op0=ALU.add, op1=ALU.add),
               reads=[PB.b, SM.b, CST.b], writes=[DL.b])
            op("dve", lambda e: e.tensor_reduce(out=S1(33), in_=DL[:P, :P], axis=AX.X, op=ALU.max), reads=[DL.b], writes=[SM.b])
            op("dve", lambda e, hh=hh: e.tensor_tensor(out=S1(34), in0=col(20), in1=mtok[:P, hh:hh + 1], op=ALU.add),
               reads=[SM.b, mtok.b], writes=[SM.b])
            op("dve", lambda e: e.tensor_tensor(out=S1(35), in0=S1(34), in1=S1(33), op=ALU.max), reads=[SM.b], writes=[SM.b])
            op("dve", lambda e: e.tensor_scalar(out=S1(36), in0=S1(35), scalar1=-1.0, scalar2=None, op0=ALU.mult),
               reads=[SM.b], writes=[SM.b])
            op("act", lambda e: e.activation(out=WI[:P, :P], in_=DL[:P, :P], func=AF.Exp, bias=S1(36), scale=1.0),
               reads=[DL.b, SM.b], writes=[WI.b])
            op("act", lambda e: e.activation(out=S1(37), in_=S1(34), func=AF.Exp, bias=S1(36), scale=1.0), reads=[SM.b], writes=[SM.b])
            op("act", lambda e: e.activation(out=S1(38), in_=S1(36), func=AF.Exp), reads=[SM.b], writes=[SM.b])
            op("pe", lambda e: e.transpose(out=PC[:, 0:P], in_=qs, identity=C("ident", P, P)), reads=[PROJ.b, CST.b], writes=[PC.b])
            op("pe", lambda e: e.transpose(out=PC[:, 128:128 + P], in_=ks, identity=C("ident", P, P)), reads=[PROJ.b, CST.b], writes=[PC.b])
            op("act", lambda e: e.mul(out=QT[:, :P], in_=PC[:, 0:P], mul=128.0 ** -0.5), reads=[PC.b], writes=[QT.b])
            op("act", lambda e: e.copy(out=KT[:, :P], in_=PC[:, 128:128 + P]), reads=[PC.b], writes=[KT.b])
            op("pe", lambda e: e.matmul(PD[:P, 0:P], lhsT=QT[:, :P], rhs=KT[:, :P], start=True, stop=True),
               reads=[QT.b, KT.b], writes=[PD.b])
            op("dve", lambda e: e.tensor_tensor(out=AM[:P, :P], in0=WI[:P, :P], in1=PD[:P, 0:P], op=ALU.mult),
               reads=[WI.b, PD.b], writes=[AM.b])
            op("pe", lambda e: e.transpose(out=PB[:P, 128:128 + P], in_=AM[:P, :P], identity=C("ident", P, P)),
               reads=[AM.b, CST.b], writes=[PB.b])
            op("act", lambda e: e.copy(out=AT[:P, :P], in_=PB[:P, 128:128 + P]), reads=[PB.b], writes=[AT.b])
            op("pool", lambda e: e.tensor_copy(out=VX[:P, 0:128], in_=vs), reads=[PROJ.b], writes=[VX.b])
            op("pe", lambda e: e.matmul(PD[:P, 128:257], lhsT=AT[:P, :P], rhs=VX[:P, :], start=True, stop=True),
               reads=[AT.b, VX.b], writes=[PD.b])
            if is_s:
                op("pool", lambda e: e.memset(ZQ[:, :], 0.0), writes=[ZQ.b])
                for b in range(SB):
                    op("pool", lambda e, b=b: e.tensor_copy(out=ZQ[:, b * SP + b:(b + 1) * SP:SB], in_=QT[:, b:SP:SB]),
                       reads=[QT.b], writes=[ZQ.b])
                for b in range(SB):
                    op("pe", lambda e, b=b: e.matmul(PC[:P, 256:385], lhsT=ZQ[:, b * SP:(b + 1) * SP], rhs=CTS[:, b, :],
                                                     start=(b == 0), stop=(b == SB - 1)),
                       reads=[ZQ.b, CTS.b], writes=[PC.b] if b in (0, SB - 1) else [])
            else:
                op("pe", lambda e, hh=hh: e.matmul(PC[:P, 256:385], lhsT=QT[:, :P], rhs=CTX[:, hh, :], start=True, stop=True),
                   reads=[QT.b, CTX.b], writes=[PC.b])
            op("act", lambda e: e.activation(out=TOT[:P, :], in_=PC[:P, 256:385], func=AF.Identity, scale=S1(37)),
               reads=[PC.b, SM.b], writes=[TOT.b])
            op("dve", lambda e: e.tensor_tensor(out=TOT[:P, :], in0=TOT[:P, :], in1=PD[:P, 128:257], op=ALU.add),
               reads=[TOT.b, PD.b], writes=[TOT.b])
            op("dve", lambda e: e.scalar_tensor_tensor(out=S1(39), in0=TOT[:P, 128:129], scalar=-1.0, in1=TOT[:P, 128:129], op0=ALU.mult, op1=ALU.max),
               reads=[TOT.b], writes=[SM.b])
            op("dve", lambda e: e.tensor_tensor(out=S1(39), in0=S1(39), in1=S1(38), op=ALU.max), reads=[SM.b], writes=[SM.b])
            op("dve", lambda e: e.reciprocal(out=S1(40), in_=S1(39)), reads=[SM.b], writes=[SM.b])
            op("dve", lambda e: e.tensor_scalar(out=HN[:P, :], in0=TOT[:P, 0:128], scalar1=S1(40), scalar2=None, op0=ALU.mult),
               reads=[TOT.b, SM.b], writes=[HN.b])
            op("dve", lambda e: e.bn_stats(out=ST6[:P, 0, :], in_=HN[:P, :]), reads=[HN.b], writes=[ST6.b])
            op("dve", lambda e: e.bn_aggr(out=SM[:P, 41:43], in_=ST6[:P, 0, :]), reads=[ST6.b], writes=[SM.b])
            op("act", lambda e: e.activation(out=S1(43), in_=S1(42), func=AF.Ln, bias=EPSB[:P, 0:1], scale=1.0), reads=[SM.b, EPSB.b], writes=[SM.b])
            op("act", lambda e: e.activation(out=S1(44), in_=S1(43), func=AF.Exp, scale=-0.5), reads=[SM.b], writes=[SM.b])
            op("dve", lambda e: e.tensor_scalar(out=HN[:P, :], in0=HN[:P, :], scalar1=S1(41), scalar2=S1(44),
                                                op0=ALU.subtract, op1=ALU.mult), reads=[HN.b, SM.b], writes=[HN.b])
            op("dve", lambda e, hh=hh: e.tensor_tensor(out=HN[:P, :], in0=HN[:P, :],
                                                       in1=PRM[:P, o_mh + hh * 128:o_mh + (hh + 1) * 128], op=ALU.mult),
               reads=[HN.b, PRM.b], writes=[HN.b])
            op("act", lambda e: e.activation(out=SG[:P, :], in_=os_, func=AF.Exp, scale=-1.0), reads=[PROJ.b], writes=[SG.b])
            op("dve", lambda e: e.tensor_scalar_add(out=SG[:P, :], in0=SG[:P, :], scalar1=1.0), reads=[SG.b], writes=[SG.b])
            op("dve", lambda e: e.reciprocal(out=SG[:P, :], in_=SG[:P, :]), reads=[SG.b], writes=[SG.b])
            op("dve", lambda e, hh=hh: e.tensor_tensor(out=Y[:P, hh * 128:(hh + 1) * 128], in0=HN[:P, :], in1=SG[:P, :], op=ALU.mult),
               reads=[HN.b, SG.b], writes=[Y.b])
            op("dve", lambda e, hh=hh: e.tensor_tensor(out=S1(45), in0=mtok[:P, hh:hh + 1], in1=S1(32), op=ALU.max),
               reads=[SM.b, mtok.b], writes=[SM.b])
            op("dve", lambda e, hh=hh: e.tensor_tensor(out=S1(45), in0=S1(45), in1=col(28), op=ALU.add), reads=[SM.b], writes=[SM.b])
            op("dve", lambda e, hh=hh: e.tensor_tensor(out=S1(46), in0=col(28), in1=S1(45), op=ALU.subtract), reads=[SM.b], writes=[SM.b])
            op("act", lambda e, hh=hh: e.activation(out=S1(47), in_=col(24), func=AF.Exp, bias=S1(46), scale=1.0),
               reads=[SM.b], writes=[SM.b])
            op("act", lambda e, hh=hh: e.activation(out=S1(48), in_=mtok[:P, hh:hh + 1], func=AF.Exp, bias=S1(46), scale=1.0),
               reads=[SM.b, mtok.b], writes=[SM.b])
            if not is_s:
                op("dve", lambda e: e.tensor_scalar(out=WV[:P, :], in0=VX[:P, :], scalar1=S1(47), scalar2=None, op0=ALU.mult),
                   reads=[VX.b, SM.b], writes=[WV.b])
                op("pe", lambda e: e.matmul(PB[:, 256:385], lhsT=ks, rhs=WV[:P, :], start=True, stop=True),
                   reads=[PROJ.b, WV.b], writes=[PB.b])
                op("dve", lambda e, hh=hh: e.scalar_tensor_tensor(out=CTX[:, hh, :], in0=CTX[:, hh, :], scalar=S1(48), in1=PB[:, 256:385],
                                                                  op0=ALU.mult, op1=ALU.add),
                   reads=[CTX.b, SM.b, PB.b], writes=[CTX.b])
                op("dve", lambda e, hh=hh: e.tensor_copy(out=MREP[:, hh:hh + 1], in_=S1(45)), reads=[SM.b], writes=[MREP.b])
                if t == NT - 1:
                    op("pe", lambda e, hh=hh: e.transpose(out=PA[:, 128:256], in_=CTX[:, hh, 0:128], identity=C("ident")),
                       reads=[CTX.b, CST.b], writes=[PA.b])
                    op("act", lambda e: e.copy(out=OUTC[:, :], in_=PA[:, 128:256]), reads=[PA.b], writes=[OUTC.b])
                    out_dma(E["o_pC"][l, hh], OUTC[:, :], [OUTC.b])
                    out_dma(E["o_pn"][l, hh].rearrange("(k o) -> k o", o=1), CTX[:, hh, 128:129], [CTX.b])
                    if hh == 3:
                        out_dma(E["o_pm"][l:l + 1, :], MREP[0:1, :], [MREP.b])
            else:
                op("dve", lambda e: e.tensor_scalar(out=WCB[:P, :], in0=C("onehotB", SP), scalar1=S1(47), scalar2=None, op0=ALU.mult),
                   reads=[SM.b, CST.b], writes=[WCB.b])
                op("dve", lambda e: e.tensor_tensor(out=RA[:P, :, :], in0=vs.unsqueeze(1).broadcast_to([P, SB, 128]),
                                                    in1=WCB[:P, :].unsqueeze(2).broadcast_to([P, SB, 128]), op=ALU.mult),
                   reads=[PROJ.b, WCB.b], writes=[RA.b])
                op("dve", lambda e: e.tensor_scalar(out=DECD[:P, :], in0=C("onehot0", SP), scalar1=S1(48), scalar2=None, op0=ALU.mult),
                   reads=[SM.b, CST.b], writes=[DECD.b])
                op("pe", lambda e: e.matmul(PA[:, 16:32], lhsT=C("ones", SP, 128), rhs=DECD[:P, :], start=True, stop=True),
                   reads=[DECD.b, CST.b], writes=[PA.b])
                op("act", lambda e: e.copy(out=DECR[:, :], in_=PA[:, 16:32]), reads=[PA.b], writes=[DECR.b])
                for b in range(SB):
                    pq = [PA, PB, PC, PD][b % 4]
                    op("pe", lambda e, b=b, pq=pq: e.matmul(pq[:, 384:512], lhsT=RA[:P, b, :], rhs=ks, start=True, stop=True),
                       reads=[RA.b, PROJ.b], writes=[pq.b])
                    op("dve", lambda e, b=b, pq=pq: e.scalar_tensor_tensor(out=CN[:, b, :], in0=CN[:, b, :], scalar=DECR[:, b:b + 1],
                                                                           in1=pq[:, 384:512], op0=ALU.mult, op1=ALU.add),
                       reads=[CN.b, DECR.b, pq.b], writes=[CN.b])
                out_dma(E["o_nC"][l, :, hh].rearrange("b v k -> v b k"), CN[:, :, :], [CN.b])
                op("pe", lambda e: e.matmul(PA[:SB, 32:160], lhsT=WCB[:P, :], rhs=ks, start=True, stop=True),
                   reads=[WCB.b, PROJ.b], writes=[PA.b])
                op("dve", lambda e, hh=hh: e.scalar_tensor_tensor(out=NNAT[:, hh, :], in0=NNAT[:, hh, :], scalar=SM[:SB, 48:49],
                                                                  in1=PA[:SB, 32:160], op0=ALU.mult, op1=ALU.add),
                   reads=[NNAT.b, SM.b, PA.b], writes=[NNAT.b])
                op("dve", lambda e, hh=hh: e.tensor_copy(out=MSO[:, hh:hh + 1], in_=SM[:SB, 45:46]), reads=[SM.b], writes=[MSO.b])
                if hh == 3:
                    out_dma(E["o_nn"][l], NNAT[:, :, :], [NNAT.b])
                    out_dma(E["o_nm"][l], MSO[:, :], [MSO.b])

        vsv = PROJ[:P, 2312:2568].rearrange("p (g d) -> p g d", g=4)
        op("dve", lambda e: e.tensor_reduce(out=SM[:P, 50:54], in_=vsv, axis=AX.X, op=ALU.add), reads=[PROJ.b], writes=[SM.b])
        op("dve", lambda e: e.tensor_scalar(out=SM[:P, 50:54], in0=SM[:P, 50:54], scalar1=1.0 / 64, scalar2=None, op0=ALU.mult),
           reads=[SM.b], writes=[SM.b])
        op("dve", lambda e: e.tensor_tensor(out=VN[:P, :].rearrange("p (g d) -> p g d", g=4), in0=vsv,
                                            in1=SM[:P, 50:54].unsqueeze(2).broadcast_to([P, 4, 64]), op=ALU.subtract),
           reads=[PROJ.b, SM.b], writes=[VN.b])
        op("pool", lambda e: e.tensor_tensor(out=VTMP[:P, :], in0=VN[:P, :], in1=VN[:P, :], op=ALU.mult), reads=[VN.b], writes=[VTMP.b])
        op("dve", lambda e: e.tensor_reduce(out=SM[:P, 54:58], in_=VTMP[:P, :].rearrange("p (g d) -> p g d", g=4), axis=AX.X, op=ALU.add),
           reads=[VTMP.b], writes=[SM.b])
        op("act", lambda e: e.activation(out=SM[:P, 54:58], in_=SM[:P, 54:58], func=AF.Ln, bias=EPSB[:P, 0:1], scale=1.0 / 64),
           reads=[SM.b, EPSB.b], writes=[SM.b])
        op("act", lambda e: e.activation(out=SM[:P, 58:62], in_=SM[:P, 54:58], func=AF.Exp, scale=-0.5), reads=[SM.b], writes=[SM.b])
        op("dve", lambda e: e.tensor_tensor(out=VN[:P, :].rearrange("p (g d) -> p g d", g=4), in0=VN[:P, :].rearrange("p (g d) -> p g d", g=4),
                                            in1=SM[:P, 58:62].unsqueeze(2).broadcast_to([P, 4, 64]), op=ALU.mult),
           reads=[VN.b, SM.b], writes=[VN.b])
        op("pool", lambda e: e.tensor_tensor(out=VN[:P, :], in0=VN[:P, :], in1=PRM[:P, o_sg:o_sg + 256], op=ALU.mult),
           reads=[VN.b, PRM.b], writes=[VN.b])
        op("pool", lambda e: e.tensor_tensor(out=VN[:P, :], in0=VN[:P, :], in1=PRM[:P, o_sb:o_sb + 256], op=ALU.add),
           reads=[VN.b, PRM.b], writes=[VN.b])
        wsl = WSs if is_s else WS
        bsl = BSs if is_s else BS
        for g in range(4):
            op("pe", lambda e, g=g: e.matmul(PC[:P, g * 64:(g + 1) * 64], lhsT=wsl[:P, g, :P], rhs=VN[:P, g * 64:(g + 1) * 64],
                                             start=True, stop=True), reads=[wsl.b, VN.b], writes=[PC.b])
        for g in range(4):
            op("dve", lambda e, g=g: e.scalar_tensor_tensor(out=Y[:P, 512 + g * 64:512 + (g + 1) * 64], in0=PC[:P, g * 64:(g + 1) * 64],
                                                            scalar=bsl[:P, g:g + 1], in1=PROJ[:P, 2056 + g * 64:2056 + (g + 1) * 64],
                                                            op0=ALU.add, op1=ALU.mult),
               reads=[PC.b, bsl.b, PROJ.b], writes=[Y.b])
        if is_s:
            for tq in range(ST):
                out_dma(E["o_nv"][l][:, tq, :], VN[tq * SB:(tq + 1) * SB, :], [VN.b])

        pin = lambda g: PROJ[:P, 2568 + g * 64:2568 + (g + 1) * 64]
        if is_s:
            kb.dma("sp", lambda q: q.dma_start(out=SPA[:, :], in_=E["spA"][l]), SPA.b, writes=[SPA.b])
            kb.dma("sp", lambda q: q.dma_start(out=SPB[:112, :], in_=E["spB"][l]), SPB.b, writes=[SPB.b])
            for g in range(4):
                op("pe", lambda e, g=g: e.matmul(PA[:64, g * 128:g * 128 + P], lhsT=SPA[:, g * 64:(g + 1) * 64], rhs=Cg("bsA", g, 128, SP, SP),
                                                 start=True, stop=False), reads=[SPA.b, CST.b], writes=[PA.b])
                op("pe", lambda e, g=g: e.matmul(PA[:64, g * 128:g * 128 + P], lhsT=SPB[:112, g * 64:(g + 1) * 64], rhs=Cg("bsB", g, 112, SP, SP),
                                                 start=False, stop=False), reads=[SPB.b, CST.b], writes=[])
                op("pe", lambda e, g=g: e.matmul(PA[:64, g * 128:g * 128 + P], lhsT=pin(g), rhs=Cg("bsC", g, SP, SP, SP),
                                                 start=False, stop=True), reads=[PROJ.b, CST.b], writes=[PA.b])
            npv = E["o_np"][l].rearrange("b r c -> r b c")
            for r in range(4):
                out_dma(npv[r], SPA[64 + r * SB:64 + (r + 1) * SB, :], [SPA.b])
            for r in range(7):
                out_dma(npv[4 + r], SPB[r * SB:(r + 1) * SB, :], [SPB.b])
            for r in range(4):
                out_dma(npv[11 + r], PROJ[r * SB:(r + 1) * SB, 2568:2824], [PROJ.b])
        else:
            for g in range(4):
                band = Cg("bandc0" if t == 0 else "bandc", g, 128, 128, 128)
                op("pe", lambda e, g=g, band=band: e.matmul(PA[:64, g * 128:(g + 1) * 128], lhsT=pin(g), rhs=band, start=True, stop=(t == 0)),
                   reads=[PROJ.b, CST.b], writes=[PA.b])
                if t > 0:
                    op("pe", lambda e, g=g: e.matmul(PA[:64, g * 128:(g + 1) * 128], lhsT=PREV[:, g * 64:(g + 1) * 64],
                                                     rhs=Cg("bandp", g, 128, 128, 128), start=False, stop=True),
                       reads=[PREV.b, CST.b], writes=[PA.b])
            if t < NT - 1:
                op("pool", lambda e: e.tensor_copy(out=PREV[:, :], in_=PROJ[:, 2568:2824]), reads=[PROJ.b], writes=[PREV.b])
            else:
                out_dma(E["o_pp"][l], PROJ[113:128, 2568:2824], [PROJ.b])
        op("act", lambda e: e.copy(out=PTT[:, :, :P], in_=PA[:64, :].rearrange("p (g c) -> p g c", g=4)[:, :, :P]),
           reads=[PA.b], writes=[PTT.b])
        for g in range(4):
            op("pe", lambda e, g=g: e.matmul(PB[:P, g * 64:(g + 1) * 64], lhsT=PTT[:, g, :P], rhs=WP[:, g, :], start=True, stop=True),
               reads=[PTT.b, WP.b], writes=[PB.b])
        op("dve", lambda e: e.tensor_tensor(out=Y[:P, 768:1024], in0=PB[:P, 0:256], in1=PRM[:P, o_ps:o_ps + 256], op=ALU.mult),
           reads=[PB.b, PRM.b], writes=[Y.b])

        _transpose8(kb, E, Y, HT, PT, P, op=op)
        for (c0, w) in mk_chunks(D):
            stream_mm(w_o_v, c0, w, HT, P,
                      lambda pm, i, c0=c0, w=w: op("dve", lambda e: e.tensor_tensor(out=H[:P, c0:c0 + w], in0=pm[:P, 0:w],
                                                                                    in1=ada[:P, 2048 + c0:2048 + c0 + w], op=ALU.mult),
                                                   reads=[pm.b, ada.b], writes=[H.b]))
        _resid_ln(kb, X, XB[t], t, P, H, SM, ST6, PRM, o_l1g, o_l1b, EPSB)
        cur[0] = None
        if nxtA:
            run_items(nxtA)
        if ti == len(tiles) - 1 and conv:
            run_items(conv)


def _resid_ln(kb, X, xb, t, P, Z, SM, ST6, PRM, og, ob, EPSB):
    op = kb.op
    xt = X[:P, t, :]
    op("dve", lambda e: e.scalar_tensor_tensor(out=Z[:P, :], in0=xt, scalar=ALPHA, in1=Z[:P, :], op0=ALU.mult, op1=ALU.add),
       reads=[xb, Z.b], writes=[Z.b])
    op("dve", lambda e: e.bn_stats(out=ST6[:P, 0, :], in_=Z[:P, 0:512]), reads=[Z.b], writes=[ST6.b])
    op("dve", lambda e: e.bn_stats(out=ST6[:P, 1, :], in_=Z[:P, 512:1024]), reads=[Z.b], writes=[ST6.b])
    op("dve", lambda e: e.bn_aggr(out=SM[:P, 41:43], in_=ST6[:P, :, :].rearrange("p a b -> p (a b)")), reads=[ST6.b], writes=[SM.b])
    op("act", lambda e: e.activation(out=SM[:P, 43:44], in_=SM[:P, 42:43], func=AF.Ln, bias=EPSB[:P, 0:1], scale=1.0), reads=[SM.b, EPSB.b], writes=[SM.b])
    op("act", lambda e: e.activation(out=SM[:P, 44:45], in_=SM[:P, 43:44], func=AF.Exp, scale=-0.5), reads=[SM.b], writes=[SM.b])
    op("dve", lambda e: e.tensor_scalar(out=Z[:P, :], in0=Z[:P, :], scalar1=SM[:P, 41:42], scalar2=SM[:P, 44:45],
                                        op0=ALU.subtract, op1=ALU.mult), reads=[Z.b, SM.b], writes=[Z.b])
    op("pool", lambda e: e.tensor_tensor(out=Z[:P, :], in0=Z[:P, :], in1=PRM[:P, og:og + D], op=ALU.mult), reads=[Z.b, PRM.b], writes=[Z.b])
    op("dve", lambda e: e.tensor_tensor(out=xt, in0=Z[:P, :], in1=PRM[:P, ob:ob + D], op=ALU.add), reads=[Z.b, PRM.b], writes=[xb])


def _phase2(nc, kb, l, E):
    C, CST, X, XB = E["C"], E["CST"], E["X"], E["XB"]
    ADA = Tn(kb, "ADA2", [128, 3072]); ADAs = ADA
    WCH = [Tn(kb, "WCHb%d" % i, [128, 8, 128], dma=True) for i in range(2)]
    badac = [Tn(kb, "badab%d" % i, [128, 256], dma=True) for i in range(2)]
    Hs = [Tn(kb, "H2_%d" % i, [128, D], dma=True) for i in range(2)]
    HT = Tn(kb, "H2T", [128, 8, 128])
    PRM = Tn(kb, "PRM2", [128, 2 * D], dma=True)
    KTS = Tn(kb, "KTS", [128, 16, 128], dma=True)
    S0 = Tn(kb, "S0", [128, 2048]); S1_ = Tn(kb, "S1", [128, 2048]); S2 = Tn(kb, "S2", [128, 256])
    TOPS = Tn(kb, "TOPS", [128, 16, 16]); IDXU = Tn(kb, "IDXU", [128, 16, 16], U32); IDXF = Tn(kb, "IDXF", [128, 16, 16])
    CV = Tn(kb, "CV", [128, 8, 16]); CPOS = Tn(kb, "CPOS", [128, 8, 16], U32)
    PAU = Tn(kb, "PAU", [128, 8, 16], U32); PBU = Tn(kb, "PBU", [128, 8, 16], U32)
    PAF = Tn(kb, "PAF", [128, 8, 16]); PBF = Tn(kb, "PBF", [128, 8, 16])
    I1 = Tn(kb, "I1", [128, 128]); I2 = Tn(kb, "I2", [128, 128])
    IDXs = [Tn(kb, "IDX%d" % i, [128, 128], I32) for i in range(2)]
    GATEs = [Tn(kb, "GATE%d" % i, [128, 128]) for i in range(2)]
    ACTV = Tn(kb, "ACTV", [128, 128]); COEF = Tn(kb, "COEF", [128, 128])
    SMf = Tn(kb, "SM2f", [128, 16]); SM = Tn(kb, "SM2", [128, 64]); ST6 = Tn(kb, "ST62", [128, 2, 6])
    NB = NBUF
    UB = [Tn(kb, "UB%d" % i, [128, 2 * D], BF16, dma=True) for i in range(NB)]
    WCAB = Tn(kb, "WCAB", [128, 8, 256], dma=True)
    ACTB = [kb.buf("actv%d" % i) for i in range(NB)]
    COEFB = [kb.buf("coef%d" % i) for i in range(NB)]
    COEF2 = Tn(kb, "COEF2", [128, 128])

    class _View:
        def __init__(self, ap, b):
            self.t = ap
            self.b = b

        def __getitem__(self, k):
            return self.t[k]
    WCA = [WCAB]
    PT = [Tn(kb, "PTb%d" % i, [128, 512], psum=True) for i in range(2)]
    PQ = [Tn(kb, "PQ%d" % i, [128, 512], psum=True) for i in range(2)]
    PM = PQ
    ACCP = [Tn(kb, "ACCP%d" % i, [128, 512], psum=True) for i in range(2)]
    JUNK = Tn(kb, "JUNK", [128, D])
    ACC = JUNK
    DGB = [Tn(kb, "DGB%d" % i, [128, 128], BF16) for i in range(3)]
    PS = [Tn(kb, "PS%d" % i, [128, 512], psum=True) for i in range(2)]

    kb.dma("sp", lambda q: q.dma_start(out=PRM[:, 0:D], in_=E["ln2g"][l]), PRM.b, writes=[PRM.b])
    kb.dma("sp", lambda q: q.dma_start(out=PRM[:, D:2 * D], in_=E["ln2b"][l]), PRM.b, writes=[PRM.b])
    kb.dma("sp", lambda q: q.dma_start(out=KTS[:, :, :], in_=E["keysT"][l]), KTS.b, writes=[KTS.b])
    wpq_v = E["w_pq"][l]

    tab = E["puvb"][l]
    PUVB = E["PUVBT"][l]
    wctr = [0]

    def make_front(t, sset):
        items = []

        def op(e, fn, reads=(), writes=()):
            items.append(("op", e, _bind(fn), None, tuple(reads), tuple(writes)))

        def dma(e, fn, owner, reads=(), writes=()):
            items.append(("dma", e, _bind(fn), owner, tuple(reads), tuple(writes)))
        is_s = (t == NT)
        P = SP if is_s else 128
        ada = ADAs if is_s else ADA
        H = Hs[sset]; IDX = IDXs[sset]; GATE = GATEs[sset]
        QT = S0
        xt = X[:P, t, :]
        op("dve", lambda e: e.tensor_tensor(out=H[:P, :], in0=xt, in1=ada[:P, 1024:2048], op=ALU.mult), reads=[XB[t], ada.b], writes=[H.b])
        op("dve", lambda e: e.tensor_tensor(out=H[:P, :], in0=H[:P, :], in1=ada[:P, 0:1024], op=ALU.add), reads=[H.b, ada.b], writes=[H.b])
        _transpose8(kb, E, H, HT, PT, P, op=op)
        def wdma(c):
            i = c % 2
            dma("sp", lambda q: q.dma_start(out=WCH[i][:, :, :], in_=wpq_v[c].rearrange("p (k j) -> p k j", k=8)), WCH[i].b, writes=[WCH[i].b])
        wdma(0)
        for c in range(16):
            i = c % 2
            j = c % 2
            if c + 1 < 16:
                wdma(c + 1)
            for k in range(8):
                op("pe", lambda e: e.matmul(PQ[j][:, 0:P], lhsT=WCH[i][:, k, :], rhs=HT[:, k, :P], start=(k == 0), stop=(k == 7)),
                   reads=[WCH[i].b, HT.b], writes=[PQ[j].b] if k in (0, 7) else [])
            op("act", lambda e: e.copy(out=QT[:, c * 128:c * 128 + P], in_=PQ[j][:, 0:P]), reads=[PQ[j].b], writes=[QT.b])
        for c4 in range(4):
            ps = PS[c4 % 2]
            for j in range(4):
                c = c4 * 4 + j
                op("pe", lambda e: e.matmul(ps[:P, j * 128:(j + 1) * 128], lhsT=QT[:, c * 128:c * 128 + P], rhs=KTS[:, c, :], start=True, stop=True),
                   reads=[QT.b, KTS.b], writes=[ps.b])
            op("act", lambda e: e.copy(out=S1_[:P, c4 * 512:(c4 + 1) * 512], in_=ps[:P, :]), reads=[ps.b], writes=[S1_.b])
        for c in range(16):
            sc = S1_[:P, c * 128:(c + 1) * 128]
            wk = S2[:P, 0:128]
            op("dve", lambda e: e.max(out=TOPS[:P, c, 0:8], in_=sc), reads=[S1_.b], writes=[TOPS.b])
            op("dve", lambda e: e.max_index(out=IDXU[:P, c, 0:8], in_max=TOPS[:P, c, 0:8], in_values=sc), reads=[S1_.b, TOPS.b], writes=[IDXU.b])
            op("dve", lambda e: e.match_replace(out=wk, in_to_replace=TOPS[:P, c, 0:8], in_values=sc, imm_value=NEG), reads=[S1_.b, TOPS.b], writes=[S2.b])
            op("dve", lambda e: e.max(out=TOPS[:P, c, 8:16], in_=wk), reads=[S2.b], writes=[TOPS.b])
            op("dve", lambda e: e.max_index(out=IDXU[:P, c, 8:16], in_max=TOPS[:P, c, 8:16], in_values=wk), reads=[S2.b, TOPS.b], writes=[IDXU.b])
        op("dve", lambda e: e.tensor_copy(out=IDXF[:P, :, :], in_=IDXU[:P, :, :]), reads=[IDXU.b], writes=[IDXF.b])
        tv = TOPS[:P, :, :].rearrange("p (h two) k -> p h two k", two=2)
        CAND = S0
        op("dve", lambda e: e.tensor_tensor(out=CAND[:P, :].rearrange("p (h a b) -> p h a b", h=8, a=16),
                                            in0=tv[:, :, 0, :].unsqueeze(3).broadcast_to([P, 8, 16, 16]),
                                            in1=tv[:, :, 1, :].unsqueeze(2).broadcast_to([P, 8, 16, 16]), op=ALU.add),
           reads=[TOPS.b], writes=[S0.b])
        for h in range(8):
            cd = CAND[:P, h * 256:(h + 1) * 256]
            wk = S2[:P, 0:256]
            op("dve", lambda e: e.max(out=CV[:P, h, 0:8], in_=cd), reads=[S0.b], writes=[CV.b])
            op("dve", lambda e: e.max_index(out=CPOS[:P, h, 0:8], in_max=CV[:P, h, 0:8], in_values=cd), reads=[S0.b, CV.b], writes=[CPOS.b])
            op("dve", lambda e: e.match_replace(out=wk, in_to_replace=CV[:P, h, 0:8], in_values=cd, imm_value=NEG), reads=[S0.b, CV.b], writes=[S2.b])
            op("dve", lambda e: e.max(out=CV[:P, h, 8:16], in_=wk), reads=[S2.b], writes=[CV.b])
            op("dve", lambda e: e.max_index(out=CPOS[:P, h, 8:16], in_max=CV[:P, h, 8:16], in_values=wk), reads=[S2.b, CV.b], writes=[CPOS.b])
        op("dve", lambda e: e.tensor_single_scalar(out=PAU[:P, :, :], in_=CPOS[:P, :, :], scalar=4, op=ALU.logical_shift_right), reads=[CPOS.b], writes=[PAU.b])
        op("dve", lambda e: e.tensor_single_scalar(out=PBU[:P, :, :], in_=CPOS[:P, :, :], scalar=15, op=ALU.bitwise_and), reads=[CPOS.b], writes=[PBU.b])
        op("dve", lambda e: e.tensor_copy(out=PAF[:P, :, :], in_=PAU[:P, :, :]), reads=[PAU.b], writes=[PAF.b])
        op("dve", lambda e: e.tensor_copy(out=PBF[:P, :, :], in_=PBU[:P, :, :]), reads=[PBU.b], writes=[PBF.b])
        iv = IDXF[:P, :, :].rearrange("p (h two) k -> p h two k", two=2)
        io16 = C("iota16", P).unsqueeze(1).unsqueeze(1).broadcast_to([P, 8, 16, 16])
        for (pf, half, dst) in [(PAF, 0, I1), (PBF, 1, I2)]:
            eq = S1_[:P, :].rearrange("p (h k a) -> p h k a", h=8, k=16)
            op("dve", lambda e: e.tensor_tensor(out=eq, in0=pf[:P, :, :].unsqueeze(3).broadcast_to([P, 8, 16, 16]), in1=io16, op=ALU.is_equal),
               reads=[pf.b, CST.b], writes=[S1_.b])
            op("dve", lambda e: e.tensor_tensor(out=eq, in0=eq, in1=iv[:, :, half, :].unsqueeze(2).broadcast_to([P, 8, 16, 16]), op=ALU.mult),
               reads=[S1_.b, IDXF.b], writes=[S1_.b])
            op("dve", lambda e: e.tensor_reduce(out=dst[:P, :].rearrange("p (h k) -> p h k", h=8), in_=eq, axis=AX.X, op=ALU.add),
               reads=[S1_.b], writes=[dst.b])
        op("dve", lambda e: e.scalar_tensor_tensor(out=I1[:P, :], in0=I1[:P, :], scalar=128.0, in1=I2[:P, :], op0=ALU.mult, op1=ALU.add),
           reads=[I1.b, I2.b], writes=[I1.b])
        op("dve", lambda e: e.tensor_copy(out=IDX[:P, :], in_=I1[:P, :]), reads=[I1.b], writes=[IDX.b])
        gv = GATE[:P, :].rearrange("p (h k) -> p h k", h=8)
        op("dve", lambda e: e.tensor_tensor(out=gv, in0=CV[:P, :, :], in1=CV[:P, :, 0:1].broadcast_to([P, 8, 16]), op=ALU.subtract),
           reads=[CV.b], writes=[GATE.b])
        op("act", lambda e: e.activation(out=GATE[:P, :], in_=GATE[:P, :], func=AF.Exp), reads=[GATE.b], writes=[GATE.b])
        op("dve", lambda e: e.tensor_reduce(out=SMf[:P, 0:8], in_=gv, axis=AX.X, op=ALU.add), reads=[GATE.b], writes=[SMf.b])
        op("dve", lambda e: e.reciprocal(out=SMf[:P, 8:16], in_=SMf[:P, 0:8]), reads=[SMf.b], writes=[SMf.b])
        op("dve", lambda e: e.tensor_tensor(out=gv, in0=gv, in1=SMf[:P, 8:16].unsqueeze(2).broadcast_to([P, 8, 16]), op=ALU.mult),
           reads=[GATE.b, SMf.b], writes=[GATE.b])
        return items

    def run_items(items, n=None):
        n = len(items) if n is None else min(n, len(items))
        for _ in range(n):
            kind, e, fn, owner, reads, writes = items.pop(0)
            if kind == "op":
                kb.op(e, fn, reads=reads, writes=writes, bound=True)
            else:
                kb.dma(e, fn, owner, reads=reads, writes=writes, bound=True)

    def back(t, sset, nxt):
        op = kb.op
        is_s = (t == NT)
        P = SP if is_s else 128
        ada = ADAs if is_s else ADA
        H = Hs[sset]; IDX = IDXs[sset]; GATE = GATEs[sset]
        per = 0 if not nxt else (len(nxt) + 119) // 120

        def axpy(s):
            b = s % NB
            dg = DGB[s % 3]
            op("act", lambda e: e.activation(out=COEF2[:P, s:s + 1], in_=COEF[:P, s:s + 1], func=AF.Identity, scale=GATE[:P, s:s + 1]),
               reads=[COEFB[b], GATE.b], writes=[COEF2.b])
            op("act", lambda e: e.activation(out=dg[:P, :P], in_=C("ident", P, P), func=AF.Identity, scale=COEF2[:P, s:s + 1]),
               reads=[COEF2.b, CST.b], writes=[dg.b])
            for hf in range(2):
                op("pe", lambda e: e.matmul(ACCP[hf][:P, :], lhsT=dg[:P, :P], rhs=UB[b][:P, D + hf * 512:D + (hf + 1) * 512], start=(s == 0), stop=(s == 127)),
                   reads=[dg.b, UB[b].b], writes=[ACCP[hf].b] if s in (0, 127) else [])

        for s_ in range(128):
            b = s_ % NB
            kb.dma("pool", lambda q: q.indirect_dma_start(out=UB[b][:P, :], out_offset=None, in_=tab,
                                                          in_offset=bass.IndirectOffsetOnAxis(ap=IDX[:P, s_:s_ + 1], axis=0)),
                   UB[b].b, reads=[IDX.b, PUVB], writes=[UB[b].b])
            op("dve", lambda e: e.scalar_tensor_tensor(out=JUNK[:P, :], in0=UB[b][:P, 0:D], scalar=1.0, in1=H[:P, :],
                                                       op0=ALU.mult, op1=ALU.mult, accum_out=ACTV[:P, s_:s_ + 1]),
               reads=[UB[b].b, H.b], writes=[JUNK.b, ACTB[b]])
            op("act", lambda e: e.activation(out=COEF[:P, s_:s_ + 1], in_=ACTV[:P, s_:s_ + 1], func=AF.Gelu), reads=[ACTB[b]], writes=[COEFB[b]])
            if s_ >= 1:
                axpy(s_ - 1)
            if nxt:
                run_items(nxt, per)
        axpy(127)
        if nxt:
            run_items(nxt)
        for hf in range(2):
            op("dve", lambda e: e.tensor_tensor(out=ACC[:P, hf * 512:(hf + 1) * 512], in0=ACCP[hf][:P, :], in1=ada[:P, 2048 + hf * 512:2048 + (hf + 1) * 512],
                                                op=ALU.mult), reads=[ACCP[hf].b, ada.b], writes=[ACC.b])
        _resid_ln(kb, X, XB[t], t, P, ACC, SM, ST6, PRM, 0, D, E["EPSB"])

    _ada(nc, kb, l, E, ADA, 128, E["cp"], 3072, WCA, PM, Hs[0], HT, PT, badac)
    run_items(make_front(0, 0))
    for t in range(NT + 1):
        nxt = make_front(t + 1, (t + 1) % 2) if t + 1 < NT else None
        back(t, t % 2, nxt)
        if t + 1 == NT:
            _ada(nc, kb, l, E, ADAs, SP, E["cs"], 3072, WCA, PM, Hs[NT % 2], HT, PT, badac)
            run_items(make_front(NT, NT % 2))


_CACHE = {}


def _chunked(w, cw):
    L, K, n = w.shape
    nch = (n + cw - 1) // cw
    wp = np.zeros((L, K, nch * cw), np.float32)
    wp[:, :, :n] = w
    wp = wp.reshape(L, 8, 128, nch, cw).transpose(0, 3, 2, 1, 4)
    return np.ascontiguousarray(wp.reshape(L, nch, 128, 8 * cw))


def _rep(a, P=128):
    return np.ascontiguousarray(np.broadcast_to(a[:, None, :], (a.shape[0], P, a.shape[1])))


def make_in_maps(inp, cpack):
    f = lambda a: np.ascontiguousarray(np.asarray(a, dtype=np.float32))
    shared = {
        "w_ada": _chunked(f(inp["w_ada"]), WCW), "b_ada": _rep(f(inp["b_ada"])), "w_in": _chunked(f(inp["w_in"]), WCW),
        "b_gate": _rep(f(inp["b_gate"])),
        "mh_g": _rep(f(inp["mh_g"])), "sgu_g": _rep(f(inp["sgu_g"])), "sgu_b": _rep(f(inp["sgu_b"])),
        "pscale": _rep(f(inp["pool_scale"])),
        "w_sT": f(np.asarray(inp["w_s"]).transpose(0, 3, 1, 2)),
        "b_sT": f(np.asarray(inp["b_s"]).transpose(0, 2, 1)),
        "w_pool": f(np.asarray(inp["w_pool"]).transpose(0, 2, 1, 3)),
        "w_o": _chunked(f(inp["w_o"]), WCW), "ln1g": _rep(f(inp["ln1_g"])), "ln1b": _rep(f(inp["ln1_b"])),
        "ln2g": _rep(f(inp["ln2_g"])), "ln2b": _rep(f(inp["ln2_b"])), "w_pq": _chunked(f(inp["w_pq"]), 128),
        "keysT": f(np.asarray(inp["peer_keys"]).transpose(0, 4, 1, 2, 3).reshape(DEPTH, 128, 16, 128)),
        "cst": cpack[0], "cst1": cpack[1],
    }
    ws4 = np.asarray(inp["w_s"])[:, :, :ST, :ST]
    wsS = np.repeat(np.repeat(ws4.transpose(0, 3, 1, 2), SB, axis=1), SB, axis=3)
    shared["w_sS"] = f(wsS)
    bs4 = np.asarray(inp["b_s"])[:, :, :ST]
    shared["b_sS"] = f(np.repeat(bs4.transpose(0, 2, 1), SB, axis=1))
    for l in range(DEPTH):
        shared["puv%d" % l] = np.ascontiguousarray(
            np.concatenate([np.asarray(inp["peer_u"])[l], np.asarray(inp["peer_v"])[l]], axis=1), dtype=np.float32)
    maps = []
    for c in range(NCORES):
        bs = slice(c * SB, (c + 1) * SB)
        m = dict(shared)
        m["xp"] = f(np.asarray(inp["x_prompt"])[c])
        m["xs"] = f(np.asarray(inp["x_sample"])[bs].transpose(1, 0, 2).reshape(SP, D))
        m["cp"] = f(np.broadcast_to(np.asarray(inp["c_prompt"])[c][None, :], (128, D)))
        m["cs"] = f(np.tile(np.asarray(inp["c_sample"])[bs], (ST, 1)))
        sCc = np.asarray(inp["state_mlstm_C"])[:, bs]
        m["sC"] = f(sCc.transpose(0, 2, 3, 1, 4))
        snc = np.asarray(inp["state_mlstm_n"])[:, bs]
        m["snat"] = f(snc)
        m["snT"] = f(snc.transpose(0, 2, 3, 1))
        m["sm"] = f(np.tile(np.asarray(inp["state_mlstm_m"])[:, bs], (1, ST, 1)))
        spc = np.asarray(inp["state_pool"])[:, bs].transpose(0, 2, 1, 3)
        m["spA"] = f(spc[:, 0:8].reshape(DEPTH, 128, 256))
        m["spB"] = f(spc[:, 8:15].reshape(DEPTH, 112, 256))
        maps.append(m)
    return maps


def gather_outputs(results):
    cat = lambda k, ax: np.concatenate([r[k] for r in results], axis=ax)
    yp = np.stack([r["yp"] for r in results], 0)
    ys = np.concatenate([r["ys"].reshape(ST, SB, D).transpose(1, 0, 2) for r in results], 0)
    pC = np.stack([r["pC"] for r in results], 1)
    pn = np.stack([r["pn"] for r in results], 1)
    pm = np.stack([r["pm"] for r in results], 1)
    pp = np.stack([r["pp"] for r in results], 1)
    return (yp, ys, pC, pn, pm, pp, cat("nC", 1), cat("nn", 1), cat("nm", 1), cat("npool", 1), cat("nv", 1))


def kernel(**inputs):
    if "prog" not in _CACHE:
        _CACHE["prog"] = build_program()
    nc, cpack = _CACHE["prog"]
    maps = make_in_maps(inputs, cpack)
    res = run_bass_kernel_spmd(nc, maps, core_ids=list(range(NCORES)))
    outs = gather_outputs(res.results)
    return tuple(np.ascontiguousarray(o, dtype=np.float32) for o in outs)
```

```python
import numpy as np
from contextlib import ExitStack
import concourse.bass as bass
import concourse.mybir as mybir
from concourse.bass_utils import run_bass_kernel_spmd

F32 = mybir.dt.float32
I32 = mybir.dt.int32
U32 = mybir.dt.uint32
F32R = mybir.dt.float32r
BF16 = mybir.dt.bfloat16
ALU = mybir.AluOpType
AF = mybir.ActivationFunctionType
AX = mybir.AxisListType

NCORES = 8
D = 1024
SEQ = 2048
NT = 16
SB = 16
ST = 4
SP = SB * ST
DEPTH = 2
ALPHA = (2 * DEPTH) ** 0.25
LN_EPS = 1e-5
IN_COLS = 2824
NEG = -1.0e30
WCW = 192
NEXP = 16384
SAME_ENGINE_WAITS = True
NBUF = 8


class TB:
    def __init__(self, name, sem=None):
        self.name = name
        self.last_w = None
        self.reads = []
        self.sem = sem
        self.dma_total = 0
        self.dma_dirty = False


class KB:
    ENG = ("pe", "act", "dve", "pool", "sp")

    def __init__(self, nc, stack):
        self.nc = nc
        self.stack = stack
        self.q = {e: [] for e in self.ENG}
        self.cnt = {e: 0 for e in self.ENG}
        self.esem = {e: stack.enter_context(nc.semaphore("es_" + e)) for e in self.ENG}
        self.seen = {e: {} for e in self.ENG}
        self.semobj = {}
        self._sem_owner = {}
        self.stack0 = stack
        self.phase_tbs = []
        self.sem_pool = []
        self.nsem = 0
        self.sfx = ""

    def new_sem(self, name):
        return self.stack.enter_context(self.nc.semaphore(name + self.sfx))

    def buf(self, name, dma=False):
        if not dma:
            return TB(name)
        if self.sem_pool:
            sem, val = self.sem_pool.pop()
        else:
            sem, val = self.stack0.enter_context(self.nc.semaphore("dsem%d" % self.nsem)), 0
            self.nsem += 1
        tb = TB(name, sem)
        tb.dma_total = val
        if self.stack is not self.stack0:
            self.phase_tbs.append(tb)
        return tb

    def end_phase(self):
        for tb in self.phase_tbs:
            self._sem_owner.pop(id(tb.sem), None)
            self.sem_pool.append((tb.sem, tb.dma_total))
        self.phase_tbs = []

    def sb(self, name, shape, dt=F32):
        return self.stack.enter_context(self.nc.sbuf_tensor(name + self.sfx, list(shape), dt))

    def ps(self, name, shape, dt=F32):
        return self.stack.enter_context(self.nc.psum_tensor(name + self.sfx, list(shape), dt))

    def _deps(self, e, reads, writes):
        deps = {}

        def add(tok):
            if tok is None:
                return
            s, v = tok
            k = id(s)
            self.semobj[k] = s
            ow = self._sem_owner.get(k)
            if ow is not None:
                v = ow.dma_total
            if v > deps.get(k, 0):
                deps[k] = v
        for b in reads:
            add(b.last_w)
        for b in writes:
            add(b.last_w)
            for r in b.reads:
                add(r)
        out = []
        own = id(self.esem[e])
        for k, v in deps.items():
            if k == own and (e in ("pe", "sp") or not SAME_ENGINE_WAITS):
                continue
            if self.seen[e].get(k, 0) >= v:
                continue
            self.seen[e][k] = v
            out.append((self.semobj[k], v))
        return out

    def op(self, e, fn, reads=(), writes=(), bound=False):
        waits = self._deps(e, reads, writes)
        for s, v in waits:
            tb = self._sem_owner.get(id(s))
            if tb is not None:
                tb.dma_dirty = True
        self.cnt[e] += 1
        tok = (self.esem[e], self.cnt[e])
        self.q[e].append((waits, fn if bound else _bind(fn), tok[0], 1))
        for b in reads:
            b.reads.append(tok)
        for b in writes:
            b.last_w = tok
            b.reads = []
        return tok

    def dma(self, e, fn, owner, reads=(), writes=(), bound=False):
        self._sem_owner[id(owner.sem)] = owner
        waits = self._deps(e, reads, writes)
        if owner.dma_dirty and owner.dma_total > 0:
            k = id(owner.sem)
            if self.seen[e].get(k, 0) < owner.dma_total:
                self.seen[e][k] = owner.dma_total
                waits.append((owner.sem, owner.dma_total))
            owner.dma_dirty = False
        for s, v in waits:
            tb = self._sem_owner.get(id(s))
            if tb is not None and tb is not owner:
                tb.dma_dirty = True
        owner.dma_total += 16
        tok = (owner.sem, owner.dma_total)
        self.q[e].append((waits, fn if bound else _bind(fn), owner.sem, 16))
        for b in reads:
            b.reads.append(tok)
        for b in writes:
            b.last_w = tok
            b.reads = []
        return tok

    def barrier(self, extra=()):
        toks = [(self.esem[e], self.cnt[e]) for e in self.ENG if self.cnt[e] > 0 and e != "sp"]
        for tb in list(self._sem_owner.values()) + list(extra):
            if tb.dma_total > 0:
                toks.append((tb.sem, tb.dma_total))
        for e in self.ENG:
            waits = []
            for s, v in toks:
                k = id(s)
                if k == id(self.esem[e]):
                    continue
                if self.seen[e].get(k, 0) >= v:
                    continue
                self.seen[e][k] = v
                waits.append((s, v))
            if waits:
                self.q[e].append((waits, None, None, 0))

    def emit(self, final_waits=()):
        nc = self.nc
        engs = {"pe": "tensor", "act": "scalar", "dve": "vector", "pool": "gpsimd", "sp": "sync"}
        with nc.Block() as block:
            for e in self.ENG:
                items = self.q[e]
                fw = list(final_waits) if e == "sp" else []

                def body(eng, items=items, fw=fw):
                    for waits, fn, sem, inc in items:
                        for s, v in waits:
                            eng.wait_ge(s, v)
                        if fn is not None:
                            fn(eng).then_inc(sem, inc)
                    for s, v in fw:
                        eng.wait_ge(s, v)
                getattr(block, engs[e])(body)
        self.q = {e: [] for e in self.ENG}


class _Rec:
    def __init__(self):
        self.call = None

    def __getattr__(self, name):
        def f(*a, **k):
            self.call = (name, a, k)
            return self
        return f


def _bind(fn):
    r = _Rec()
    fn(r)
    assert r.call is not None
    name, a, k = r.call
    return lambda eng: getattr(eng, name)(*a, **k)


class Tn:
    def __init__(self, kb, name, shape, dt=F32, psum=False, dma=False):
        self.t = kb.ps(name, shape, dt) if psum else kb.sb(name, shape, dt)
        self.b = kb.buf(name, dma=dma)

    def __getitem__(self, k):
        return self.t[k]


def _consts():
    c = {}
    i128 = np.arange(128)
    c["ident"] = np.eye(128, dtype=np.float32)
    c["ones"] = np.ones((128, 128), np.float32)
    c["triu"] = (i128[:, None] <= i128[None, :]).astype(np.float32)
    c["negm"] = np.where(i128[None, :] <= i128[:, None], 0.0, NEG).astype(np.float32)
    sel = np.zeros((128, 128), np.float32); sel[127, :] = 1.0
    c["sel127"] = sel
    p = np.arange(SP); tt = p // SB; bb = p % SB
    sameb = bb[:, None] == bb[None, :]
    tri_s = (sameb & (tt[:, None] <= tt[None, :])).astype(np.float32)
    c["tri_s"] = _pad(tri_s)
    c["negm_s"] = _pad(np.where(sameb & (tt[None, :] <= tt[:, None]), 0.0, NEG).astype(np.float32))
    c["negb_s"] = _pad(np.where(sameb, 0.0, NEG).astype(np.float32))
    c["selend"] = _pad(((tt[:, None] == ST - 1) & sameb).astype(np.float32))
    oh = (bb[:, None] == np.arange(SB)[None, :]).astype(np.float32)
    c["onehotB"] = _pad(oh, cols=16)
    oh0 = ((p[:, None] == np.arange(SB)[None, :])).astype(np.float32)
    c["onehot0"] = _pad(oh0, cols=16)
    c["iota16"] = np.broadcast_to(np.arange(16, dtype=np.float32), (128, 16)).copy()
    wins = (2, 4, 8, 16)
    bc0 = np.zeros((4, 128, 128), np.float32); bc = np.zeros((4, 128, 128), np.float32)
    bp = np.zeros((4, 128, 128), np.float32)
    for g, w in enumerate(wins):
        for t in range(128):
            for j in range(w):
                s = t - j
                if s >= 0:
                    bc[g, s, t] += 1.0 / w
                    bc0[g, s, t] += 1.0 / min(t + 1, w)
                else:
                    bp[g, s + 128, t] += 1.0 / w
            bc[g, t, t] -= 1.0
            bc0[g, t, t] -= 1.0
    c["bandc0"] = bc0.transpose(1, 0, 2).reshape(128, 512)
    c["bandc"] = bc.transpose(1, 0, 2).reshape(128, 512)
    c["bandp"] = bp.transpose(1, 0, 2).reshape(128, 512)
    bsA = np.zeros((4, 128, SP), np.float32); bsB = np.zeros((4, 128, SP), np.float32)
    bsC = np.zeros((4, 128, SP), np.float32)
    for g, w in enumerate(wins):
        for t in range(ST):
            for b in range(SB):
                col = t * SB + b
                for j in range(w):
                    r = 15 + t - j
                    if r >= 15:
                        bsC[g, (r - 15) * SB + b, col] += 1.0 / w
                    elif r >= 8:
                        bsB[g, (r - 8) * SB + b, col] += 1.0 / w
                    else:
                        bsA[g, r * SB + b, col] += 1.0 / w
                bsC[g, t * SB + b, col] -= 1.0
    c["bsA"] = bsA.transpose(1, 0, 2).reshape(128, 4 * SP)
    c["bsB"] = bsB.transpose(1, 0, 2).reshape(128, 4 * SP)
    c["bsC"] = bsC.transpose(1, 0, 2).reshape(128, 4 * SP)
    return c


def _pad(a, cols=None):
    out = np.zeros((128, a.shape[1] if cols is None else cols), np.float32)
    out[: a.shape[0], : a.shape[1]] = a
    return out


_CONST_G = ["ident", "ones", "iota16"]
_CONST_1 = ["triu", "negm", "sel127", "tri_s", "negm_s", "negb_s", "selend",
            "onehotB", "onehot0", "bandc0", "bandc", "bandp", "bsA", "bsB", "bsC"]


def _const_pack():
    c = _consts()
    packs = []
    for order in (_CONST_G, _CONST_1):
        offs = {}
        o = 0
        arrs = []
        for k in order:
            offs[k] = (o, c[k].shape[1])
            o += c[k].shape[1]
            arrs.append(c[k])
        packs.append((np.ascontiguousarray(np.concatenate(arrs, axis=1)), offs))
    return packs


def build_program(n_layers=DEPTH, do_phase2=True):
    (cpack, coff), (cpack1, coff1) = _const_pack()
    NCST = cpack.shape[1]
    NCST1 = cpack1.shape[1]
    nc = bass.Bass("TRN2", target_bir_lowering=False)

    def din(name, shape, dt=F32):
        return nc.dram_tensor(name, list(shape), dt, kind="ExternalInput").ap()

    def dout(name, shape, dt=F32):
        return nc.dram_tensor(name, list(shape), dt, kind="ExternalOutput").ap()

    xp = din("xp", [SEQ, D]); xs = din("xs", [SP, D])
    cp = din("cp", [128, D]); cs = din("cs", [SP, D])
    sC = din("sC", [DEPTH, 4, 128, SB, 128]); snat = din("snat", [DEPTH, SB, 4, 128])
    snT = din("snT", [DEPTH, 4, 128, SB]); sm = din("sm", [DEPTH, SP, 4])
    spA = din("spA", [DEPTH, 128, 256]); spB = din("spB", [DEPTH, 112, 256])
    w_ada = din("w_ada", [DEPTH, (6 * D) // WCW, 128, 8 * WCW]); b_ada = din("b_ada", [DEPTH, 128, 6 * D])
    w_in = din("w_in", [DEPTH, (IN_COLS + WCW - 1) // WCW, 128, 8 * WCW]); b_gate = din("b_gate", [DEPTH, 128, 8])
    mh_g = din("mh_g", [DEPTH, 128, 512]); sgu_g = din("sgu_g", [DEPTH, 128, 256])
    sgu_b = din("sgu_b", [DEPTH, 128, 256]); pscale = din("pscale", [DEPTH, 128, 256])
    w_sT = din("w_sT", [DEPTH, 128, 4, 128]); b_sT = din("b_sT", [DEPTH, 128, 4])
    w_sS = din("w_sS", [DEPTH, SP, 4, SP]); b_sS = din("b_sS", [DEPTH, SP, 4])
    w_pool = din("w_pool", [DEPTH, 64, 4, 64]); w_o = din("w_o", [DEPTH, (D + WCW - 1) // WCW, 128, 8 * WCW])
    ln1g = din("ln1g", [DEPTH, 128, D]); ln1b = din("ln1b", [DEPTH, 128, D])
    ln2g = din("ln2g", [DEPTH, 128, D]); ln2b = din("ln2b", [DEPTH, 128, D])
    w_pq = din("w_pq", [DEPTH, 16, 128, 8 * 128]); keysT = din("keysT", [DEPTH, 128, 16, 128])
    puv = [din("puv%d" % l, [NEXP, 2 * D]) for l in range(DEPTH)]
    puvb = [nc.dram_tensor("puvb%d" % l, [NEXP, 2 * D], BF16, kind="Internal").ap() for l in range(DEPTH)]
    cst_d = din("cst", [128, NCST])
    cst1_d = din("cst1", [128, NCST1])

    yp = dout("yp", [SEQ, D]); ys = dout("ys", [SP, D])
    o_pC = dout("pC", [DEPTH, 4, 128, 128]); o_pn = dout("pn", [DEPTH, 4, 128]); o_pm = dout("pm", [DEPTH, 4])
    o_pp = dout("pp", [DEPTH, 15, 256])
    o_nC = dout("nC", [DEPTH, SB, 4, 128, 128]); o_nn = dout("nn", [DEPTH, SB, 4, 128])
    o_nm = dout("nm", [DEPTH, SB, 4]); o_np = dout("npool", [DEPTH, SB, 15, 256])
    o_nv = dout("nv", [DEPTH, SB, ST, 256])

    with ExitStack() as st0:
        kb = KB(nc, st0)
        op = kb.op
        OUT = kb.buf("outs", dma=True)

        def out_dma(dst, src, reads):
            kb.dma("sp", lambda q: q.dma_start(out=dst, in_=src), OUT, reads=reads)

        X = kb.sb("X", [128, NT + 1, D])
        XB = [kb.buf("X%d" % t) for t in range(NT + 1)]
        XL = kb.buf("xload", dma=True)
        CST = Tn(kb, "CST", [128, NCST], dma=True)
        PUVBT = [kb.buf("puvb%d" % i, dma=True) for i in range(DEPTH)]
        EPSB = Tn(kb, "EPSB", [128, 1])
        kb.op("dve", lambda e: e.memset(EPSB[:, :], LN_EPS), writes=[EPSB.b])

        def C(name, P=128, w=None):
            if name in coff:
                o, n = coff[name]
                return CST[:P, o:o + (n if w is None else w)]
            o, n = coff1[name]
            return kb.cst1[:P, o:o + (n if w is None else w)]

        def Cg(name, g, P, blk, w):
            o, n = coff1[name]
            return kb.cst1[:P, o + g * blk: o + g * blk + w]

        with nc.allow_non_contiguous_dma(reason="small strided state/param loads"):
            kb.dma("sp", lambda q: q.dma_start(out=CST[:, :], in_=cst_d), CST.b, writes=[CST.b])
            for t in range(NT):
                kb.dma("sp", lambda q, t=t: q.dma_start(out=X[:, t, :], in_=xp[t * 128:(t + 1) * 128, :]),
                       XL, writes=[XB[t]])
            kb.dma("sp", lambda q: q.dma_start(out=X[:SP, NT, :], in_=xs), XL, writes=[XB[NT]])

            puvb_ = puvb
            for l in range(n_layers):
                for part in ("p", "s"):
                    with ExitStack() as st1:
                        kb.stack = st1
                        kb.sfx = "_a%s%d" % (part, l)
                        _phase1(nc, kb, l, locals(), part)
                        kb.barrier(extra=[OUT])
                        kb.emit()
                        kb.end_phase()
                if do_phase2:
                    with ExitStack() as st2:
                        kb.stack = st2
                        kb.sfx = "_b%d" % l
                        _phase2(nc, kb, l, locals())
                        kb.barrier(extra=[OUT])
                        kb.emit()
                        kb.end_phase()
            kb.stack = st0
            kb.sfx = ""
            for t in range(NT):
                out_dma(yp[t * 128:(t + 1) * 128, :], X[:, t, :], [XB[t]])
            out_dma(ys, X[:SP, NT, :], [XB[NT]])
            kb.emit(final_waits=[(OUT.sem, OUT.dma_total)])
    return nc, (cpack, cpack1)


def _ada(nc, kb, l, E, ADA, P, csrc, off, WCH, PM, hbuf, hT, PT, badac, WCR=None):
    op = kb.op
    C = E["C"]
    w_ada, b_ada = E["w_ada"], E["b_ada"]
    kb.dma("sp", lambda q: q.dma_start(out=hbuf[:P, :], in_=csrc), hbuf.b, writes=[hbuf.b])
    op("act", lambda e: e.activation(out=hbuf[:P, :], in_=hbuf[:P, :], func=AF.Silu), reads=[hbuf.b], writes=[hbuf.b])
    _transpose8(kb, E, hbuf, hT, PT, P)
    r32 = (hT.t.dtype == F32R)
    for c in range(3072 // WCW):
        i = c % 2
        c0 = off + c * WCW
        wch = WCH[c % len(WCH)]
        kb.dma("sp", lambda q: q.dma_start(out=wch[:, :, 0:WCW], in_=w_ada[l, c0 // WCW].rearrange("p (k j) -> p k j", k=8)), wch.b, writes=[wch.b])
        kb.dma("sp", lambda q: q.dma_start(out=badac[i][:P, 0:WCW], in_=b_ada[l, :P, c0:c0 + WCW]), badac[i].b, writes=[badac[i].b])
        wsrc = WCR[c % 2] if r32 else wch
        if r32:
            op("act", lambda e: e.copy(out=wsrc[:, :, 0:WCW], in_=wch[:, :, 0:WCW]), reads=[wch.b], writes=[wsrc.b])
        for k in range(8):
            if r32:
                op("pe", lambda e: e.matmul(PM[i][:, 0:WCW], lhsT=hT[:, k, :], rhs=wsrc[:, k, 0:WCW], start=(k == 0), stop=(k == 7)),
                   reads=[hT.b, wsrc.b], writes=[PM[i].b] if k in (0, 7) else [])
            else:
                op("pe", lambda e: e.matmul(PM[i][:P, 0:WCW], lhsT=hT[:, k, :P], rhs=wch[:, k, 0:WCW], start=(k == 0), stop=(k == 7)),
                   reads=[hT.b, wch.b], writes=[PM[i].b] if k in (0, 7) else [])
        op("dve", lambda e: e.tensor_tensor(out=ADA[:P, c * WCW:(c + 1) * WCW], in0=PM[i][:P, 0:WCW], in1=badac[i][:P, 0:WCW], op=ALU.add),
           reads=[PM[i].b, badac[i].b], writes=[ADA.b])
    op("dve", lambda e: e.tensor_scalar_add(out=ADA[:P, 1024:2048], in0=ADA[:P, 1024:2048], scalar1=1.0), reads=[ADA.b], writes=[ADA.b])


def _transpose8(kb, E, src, dstT, PT, P, srcb=None, op=None):
    op = kb.op if op is None else op
    C = E["C"]
    sb_ = src.b if srcb is None else srcb
    for half in range(2):
        for j in range(4):
            k = half * 4 + j
            op("pe", lambda e, half=half, j=j, k=k: e.transpose(
                out=PT[half][:, j * 128:j * 128 + P], in_=src[:P, k * 128:(k + 1) * 128], identity=C("ident", P, P)),
               reads=[sb_, E["CST"].b], writes=[PT[half].b])
        op("act", lambda e, half=half: e.copy(
            out=dstT[:, half * 4:half * 4 + 4, :P],
            in_=PT[half][:, :].rearrange("p (j c) -> p j c", j=4)[:, :, :P]),
           reads=[PT[half].b], writes=[dstT.b])


def _phase1(nc, kb, l, E, part):
    isS = (part == "s")
    tiles = [NT] if isS else list(range(NT))
    cur = [None]
    cnt = [0]
    convq = [None]

    def run_items(items, n=None):
        n = len(items) if n is None else min(n, len(items))
        for _ in range(n):
            kind, e, fn, owner, reads, writes = items.pop(0)
            if kind == "op":
                kb.op(e, fn, reads=reads, writes=writes, bound=True)
            else:
                kb.dma(e, fn, owner, reads=reads, writes=writes, bound=True)

    def op(e, fn, reads=(), writes=()):
        tok = kb.op(e, fn, reads=reads, writes=writes)
        if cur[0]:
            run_items(cur[0], 1)
        cnt[0] += 1
        if convq[0] and cnt[0] % 16 == 0:
            run_items(convq[0], 1)
        return tok
    C, Cg, CST, X, XB = E["C"], E["Cg"], E["CST"], E["X"], E["XB"]
    EPSB = E["EPSB"]
    out_dma = E["out_dma"]
    w_in, w_o = E["w_in"], E["w_o"]

    kb.cst1 = kb.sb("CST1", [128, E["NCST1"]])
    kb.dma("sp", lambda q: q.dma_start(out=kb.cst1[:, :], in_=E["cst1_d"]), CST.b, writes=[CST.b])
    ADA = Tn(kb, "ADA1", [128, 3072]); ADAs = ADA
    WCH = [Tn(kb, "WCH%d" % i, [128, 8, WCW], dma=True) for i in range(2)]
    WCR = [Tn(kb, "WCR%d" % i, [128, 8, WCW], F32R) for i in range(2)]
    badac = [Tn(kb, "bada%d" % i, [128, 256], dma=True) for i in range(2)]
    nbuf = 1 if isS else 2
    Hs = [Tn(kb, "H%d" % i, [128, D], dma=True) for i in range(nbuf)]
    HTs = [Tn(kb, "HT%d" % i, [128, 8, 128], F32R) for i in range(nbuf)]
    PROJs = [Tn(kb, "PROJ%d" % i, [128, IN_COLS], dma=True) for i in range(nbuf)]
    H, HT, PROJ = Hs[0], HTs[0], PROJs[0]
    Y = Tn(kb, "Y", [128, D])
    PRM = Tn(kb, "PRM", [128, 8 + 512 + 256 * 3 + 2 * D], dma=True)
    WS = BS = WSs = BSs = None
    if isS:
        WSs = Tn(kb, "WSs", [128, 4, SP], dma=True); BSs = Tn(kb, "BSs", [128, 4], dma=True)
    else:
        WS = Tn(kb, "WS", [128, 4, 128], dma=True); BS = Tn(kb, "BS", [128, 4], dma=True)
    WP = Tn(kb, "WP", [64, 4, 64], dma=True)
    PT = [Tn(kb, "PT%d" % i, [128, 512], psum=True) for i in range(2)]
    PM = [Tn(kb, "PM%d" % i, [128, 512], psum=True) for i in range(2)]
    PA = Tn(kb, "PA", [128, 512], psum=True); PB = Tn(kb, "PB", [128, 512], psum=True)
    PC = Tn(kb, "PC", [128, 512], psum=True); PD = Tn(kb, "PD", [128, 512], psum=True)
    SM = Tn(kb, "SM", [128, 64])
    SMs = MREP = CTX = None
    if isS:
        SMs = Tn(kb, "SMs", [128, 4], dma=True)
    else:
        MREP = Tn(kb, "MREP", [128, 4])
        CTX = Tn(kb, "CTX", [128, 4, 129], dma=True)
    DG = Tn(kb, "DG", [128, 128]); DL = Tn(kb, "DL", [128, 128]); WI = Tn(kb, "WI", [128, 128])
    AM = Tn(kb, "AM", [128, 128]); AT = Tn(kb, "AT", [128, 128])
    QT = Tn(kb, "QT", [128, 128]); KT = Tn(kb, "KT", [128, 128])
    VX = Tn(kb, "VX", [128, 129]); TOT = Tn(kb, "TOT", [128, 129]); WV = Tn(kb, "WV", [128, 129])
    HN = Tn(kb, "HN", [128, 128]); SG = Tn(kb, "SG", [128, 128]); ST6 = Tn(kb, "ST6", [128, 2, 6])
    OUTC = None if isS else Tn(kb, "OUTC", [128, 128], dma=True)
    CN = CTS = RA = ZQ = NNAT = NTH = WCB = DECD = DECR = MSO = SPA = SPB = PREV = None
    if isS:
        CN = Tn(kb, "CN", [128, SB, 128], dma=True); CTS = Tn(kb, "CTS", [128, SB, 129])
        RA = Tn(kb, "RA", [128, SB, 128])

    class _V2:
        def __init__(self, ap, b):
            self.t = ap
            self.b = b

        def __getitem__(self, k):
            return self.t[k]
    if isS:
        ZQ = _V2(RA[:, :, :].rearrange("p a b -> p (a b)")[:, 0:SB * SP], RA.b)
        NNAT = Tn(kb, "NNAT", [SB, 4, 128], dma=True); NTH = Tn(kb, "NTH", [128, SB], dma=True)
        WCB = Tn(kb, "WCB", [128, 16]); DECD = Tn(kb, "DECD", [128, 16]); DECR = Tn(kb, "DECR", [128, 16])
        MSO = Tn(kb, "MSO", [SB, 4], dma=True)
        SPA = Tn(kb, "SPA", [128, 256], dma=True); SPB = Tn(kb, "SPB", [128, 256], dma=True)
    else:
        PREV = Tn(kb, "PREV", [128, 256])
    PTT = Tn(kb, "PTT", [64, 4, 128])
    VN = Tn(kb, "VN", [128, 256], dma=True); VTMP = Tn(kb, "VTMP", [128, 256])

    o_bg, o_mh, o_sg, o_sb, o_ps, o_l1g, o_l1b = 0, 8, 520, 776, 1032, 1288, 1288 + D
    for (o, w, src) in [(o_bg, 8, E["b_gate"]), (o_mh, 512, E["mh_g"]), (o_sg, 256, E["sgu_g"]), (o_sb, 256, E["sgu_b"]),
                        (o_ps, 256, E["pscale"]), (o_l1g, D, E["ln1g"]), (o_l1b, D, E["ln1b"])]:
        kb.dma("sp", lambda q, o=o, w=w, src=src: q.dma_start(out=PRM[:, o:o + w], in_=src[l]), PRM.b, writes=[PRM.b])
    kb.dma("sp", lambda q: q.dma_start(out=WP[:, :, :], in_=E["w_pool"][l]), WP.b, writes=[WP.b])
    if isS:
        kb.dma("sp", lambda q: q.dma_start(out=WSs[:SP, :, :], in_=E["w_sS"][l]), WSs.b, writes=[WSs.b])
        kb.dma("sp", lambda q: q.dma_start(out=BSs[:SP, :], in_=E["b_sS"][l]), BSs.b, writes=[BSs.b])
    else:
        kb.dma("sp", lambda q: q.dma_start(out=WS[:, :, :], in_=E["w_sT"][l]), WS.b, writes=[WS.b])
        kb.dma("sp", lambda q: q.dma_start(out=BS[:, :], in_=E["b_sT"][l]), BS.b, writes=[BS.b])
    for g in range(4):
        if isS:
            op("dve", lambda e, g=g: e.tensor_tensor(out=WSs[:SP, g, :], in0=WSs[:SP, g, :], in1=C("tri_s", SP, SP), op=ALU.mult),
               reads=[WSs.b, CST.b], writes=[WSs.b])
        else:
            op("dve", lambda e, g=g: e.tensor_tensor(out=WS[:, g, :], in0=WS[:, g, :], in1=C("triu"), op=ALU.mult),
               reads=[WS.b, CST.b], writes=[WS.b])
    if not isS:
        op("dve", lambda e: e.memset(CTX[:, :, :], 0.0), writes=[CTX.b])
        op("dve", lambda e: e.memset(MREP[:, :], 0.0), writes=[MREP.b])
    op("dve", lambda e: e.memset(VX[:, :], 1.0), writes=[VX.b])

    if isS:
        _ada(nc, kb, l, E, ADA, SP, E["cs"], 0, WCH, PM, H, HT, PT, badac, WCR)
    else:
        _ada(nc, kb, l, E, ADA, 128, E["cp"], 0, WCH, PM, H, HT, PT, badac, WCR)

    w_in_v = w_in[l]
    w_o_v = w_o[l]
    def mk_chunks(n):
        return [(c0, min(WCW, n - c0)) for c0 in range(0, n, WCW)]
    chunks = mk_chunks(IN_COLS)
    wctr = [0]

    def stream_mm(wview, c0, w, lhsT, P, evac, op=op, dma=kb.dma):
        i = wctr[0] % 2
        wctr[0] += 1
        dma("sp", lambda q: q.dma_start(out=WCH[i][:, :, :], in_=wview[c0 // WCW].rearrange("p (k j) -> p k j", k=8)), WCH[i].b, writes=[WCH[i].b])
        wr = WCR[i]
        if wctr[0] % 3 == 0:
            op("dve", lambda e: e.tensor_copy(out=wr[:, :, 0:w], in_=WCH[i][:, :, 0:w]), reads=[WCH[i].b], writes=[wr.b])
        else:
            op("act", lambda e: e.copy(out=wr[:, :, 0:w], in_=WCH[i][:, :, 0:w]), reads=[WCH[i].b], writes=[wr.b])
        for k in range(8):
            op("pe", lambda e, k=k: e.matmul(PM[i][:, 0:w], lhsT=lhsT[:, k, :], rhs=wr[:, k, 0:w],
                                              start=(k == 0), stop=(k == 7)),
               reads=[lhsT.b, wr.b], writes=[PM[i].b] if k in (0, 7) else [])
        evac(PM[i], i)

    def make_A(t):
        items = []

        def iop(e, fn, reads=(), writes=()):
            items.append(("op", e, _bind(fn), None, tuple(reads), tuple(writes)))

        def idma(e, fn, owner, reads=(), writes=()):
            items.append(("dma", e, _bind(fn), owner, tuple(reads), tuple(writes)))
        P = SP if isS else 128
        H, HT, PROJ = Hs[t % nbuf], HTs[t % nbuf], PROJs[t % nbuf]
        xt = X[:P, t, :]
        iop("dve", lambda e: e.tensor_tensor(out=H[:P, :], in0=xt, in1=ADA[:P, 1024:2048], op=ALU.mult), reads=[XB[t], ADA.b], writes=[H.b])
        iop("dve", lambda e: e.tensor_tensor(out=H[:P, :], in0=H[:P, :], in1=ADA[:P, 0:1024], op=ALU.add), reads=[H.b, ADA.b], writes=[H.b])
        _transpose8(kb, E, H, HT, PT, P, op=iop)
        for (c0, w) in chunks:
            stream_mm(w_in_v, c0, w, HT, P,
                      lambda pm, i, c0=c0, w=w: iop("act", lambda e: e.copy(out=PROJ[:P, c0:c0 + w], in_=pm[:P, 0:w]),
                                                    reads=[pm.b], writes=[PROJ.b]), op=iop, dma=idma)
        return items

    conv = []
    if not isS:
        CBs = [Tn(kb, "CB%d" % i, [128, 2 * D], BF16, dma=True) for i in range(2)]
        tab32 = E["puv"][l]
        tabb = E["puvb"][l]
        PUVB = E["PUVBT"][l]
        for blk in range(NEXP // 128):
            cb = CBs[blk % 2]
            conv.append(("dma", "pool", _bind(lambda q: q.dma_start(out=cb[:, :], in_=tab32[blk * 128:(blk + 1) * 128, :])), cb.b, (), (cb.b,)))
            conv.append(("dma", "sp", _bind(lambda q: q.dma_start(out=tabb[blk * 128:(blk + 1) * 128, :], in_=cb[:, :])), PUVB, (cb.b,), (PUVB,)))
    convq[0] = conv
    run_items(make_A(tiles[0]))
    for ti, t in enumerate(tiles):
        is_s = isS
        P = SP if is_s else 128
        ada = ADA
        H, HT, PROJ = Hs[t % nbuf], HTs[t % nbuf], PROJs[t % nbuf]
        xt = X[:P, t, :]
        nxtA = make_A(tiles[ti + 1]) if ti + 1 < len(tiles) else None
        cur[0] = nxtA
        tri = C("tri_s", SP, SP) if is_s else C("triu")
        negm = C("negm_s", SP, SP) if is_s else C("negm")
        selE = C("selend", SP, SP) if is_s else C("sel127")
        if is_s:
            kb.dma("sp", lambda q: q.dma_start(out=SMs[:SP, :], in_=E["sm"][l]), SMs.b, writes=[SMs.b])
            kb.dma("sp", lambda q: q.dma_start(out=NNAT[:, :, :], in_=E["snat"][l]), NNAT.b, writes=[NNAT.b])
        mtok = SMs if is_s else MREP
        op("dve", lambda e: e.tensor_tensor(out=SM[:P, 0:8], in0=PROJ[:P, 2048:2056], in1=PRM[:P, o_bg:o_bg + 8], op=ALU.add),
           reads=[PROJ.b, PRM.b], writes=[SM.b])
        op("dve", lambda e: e.scalar_tensor_tensor(out=SM[:P, 8:12], in0=SM[:P, 4:8], scalar=-1.0, in1=SM[:P, 4:8], op0=ALU.mult, op1=ALU.max),
           reads=[SM.b], writes=[SM.b])
        op("act", lambda e: e.activation(out=SM[:P, 12:16], in_=SM[:P, 8:12], func=AF.Exp, scale=-1.0), reads=[SM.b], writes=[SM.b])
        op("act", lambda e: e.activation(out=SM[:P, 12:16], in_=SM[:P, 12:16], func=AF.Ln, bias=1.0, scale=1.0),
           reads=[SM.b], writes=[SM.b])
        op("dve", lambda e: e.tensor_scalar_min(out=SM[:P, 16:20], in0=SM[:P, 4:8], scalar1=0.0), reads=[SM.b], writes=[SM.b])
        op("dve", lambda e: e.tensor_tensor(out=SM[:P, 16:20], in0=SM[:P, 16:20], in1=SM[:P, 12:16], op=ALU.subtract),
           reads=[SM.b], writes=[SM.b])
        op("pe", lambda e: e.matmul(PA[:P, 0:4], lhsT=tri, rhs=SM[:P, 16:20], start=True, stop=True),
           reads=[CST.b, SM.b], writes=[PA.b])
        op("act", lambda e: e.copy(out=SM[:P, 20:24], in_=PA[:P, 0:4]), reads=[PA.b], writes=[SM.b])
        op("dve", lambda e: e.tensor_tensor(out=SM[:P, 24:28], in0=SM[:P, 0:4], in1=SM[:P, 20:24], op=ALU.subtract),
           reads=[SM.b], writes=[SM.b])
        op("pe", lambda e: e.matmul(PA[:P, 8:12], lhsT=selE, rhs=SM[:P, 20:24], start=True, stop=True),
           reads=[CST.b, SM.b], writes=[PA.b])
        op("act", lambda e: e.copy(out=SM[:P, 28:32], in_=PA[:P, 8:12]), reads=[PA.b], writes=[SM.b])

        for hh in range(4):
            qs = PROJ[:P, hh * 128:(hh + 1) * 128]
            ks = PROJ[:P, 512 + hh * 128:512 + (hh + 1) * 128]
            vs = PROJ[:P, 1024 + hh * 128:1024 + (hh + 1) * 128]
            os_ = PROJ[:P, 1536 + hh * 128:1536 + (hh + 1) * 128]
            col = lambda c, hh=hh: SM[:P, c + hh:c + hh + 1]
            S1 = lambda c: SM[:P, c:c + 1]
            if is_s:
                kb.dma("sp", lambda q, hh=hh: q.dma_start(out=CN[:, :, :], in_=E["sC"][l, hh]), CN.b, writes=[CN.b])
                kb.dma("sp", lambda q, hh=hh: q.dma_start(out=NTH[:, :], in_=E["snT"][l, hh]), NTH.b, writes=[NTH.b])
                for j in range(4):
                    pt = PT[j % 2]
                    for jj in range(4):
                        b = j * 4 + jj
                        op("pe", lambda e, b=b, jj=jj, pt=pt: e.transpose(out=pt[:, jj * 128:(jj + 1) * 128], in_=CN[:, b, :],
                                                                       identity=C("ident")),
                           reads=[CN.b, CST.b], writes=[pt.b])
                    op("act", lambda e, j=j, pt=pt: e.copy(out=CTS[:, j * 4:(j + 1) * 4, 0:128],
                                                           in_=pt[:, :].rearrange("p (j c) -> p j c", j=4)),
                       reads=[pt.b], writes=[CTS.b])
                op("dve", lambda e: e.tensor_copy(out=CTS[:, :, 128:129], in_=NTH[:, :].unsqueeze(2)), reads=[NTH.b], writes=[CTS.b])
            op("dve", lambda e, hh=hh: e.tensor_scalar(out=DG[:P, :P], in0=C("ident", P, P), scalar1=col(24), scalar2=None,
                                                       op0=ALU.mult), reads=[SM.b, CST.b], writes=[DG.b])
            op("pe", lambda e: e.matmul(PB[:P, 0:P], lhsT=C("ones", P, P), rhs=DG[:P, :P], start=True, stop=True),
               reads=[DG.b, CST.b], writes=[PB.b])
            if is_s:
                op("dve", lambda e: e.tensor_tensor(out=DL[:P, :P], in0=PB[:P, 0:P], in1=C("negb_s", SP, SP), op=ALU.add),
                   reads=[PB.b, CST.b], writes=[DL.b])
                op("dve", lambda e: e.tensor_reduce(out=S1(32), in_=DL[:P, :P], axis=AX.X, op=ALU.max), reads=[DL.b], writes=[SM.b])
            else:
                op("dve", lambda e: e.tensor_reduce(out=S1(32), in_=PB[:P, 0:P], axis=AX.X, op=ALU.max), reads=[PB.b], writes=[SM.b])
            op("dve", lambda e, hh=hh: e.scalar_tensor_tensor(out=DL[:P, :P], in0=PB[:P, 0:P], scalar=col(20), in1=negm,
                                                              op0=ALU.add, op1=ALU.add),
               reads=[PB.b, SM.b, CST.b], writes=[DL.b])
            op("dve", lambda e: e.tensor_reduce(out=S1(33), in_=DL[:P, :P], axis=AX.X, op=ALU.max), reads=[DL.b], writes=[SM.b])
            op("dve", lambda e, hh=hh: e.tensor_tensor(out=S1(34), in0=col(20), in1=mtok[:P, hh:hh + 1], op=ALU.add),
               reads=[SM.b, mtok.b], writes=[SM.b])
            op("dve", lambda e: e.tensor_tensor(out=S1(35), in0=S1(34), in1=S1(33), op=ALU.max), reads=[SM.b], writes=[SM.b])
            op("dve", lambda e: e.tensor_scalar(out=S1(36), in0=S1(35), scalar1=-1.0, scalar2=None, op0=ALU.mult),
               reads=[SM.b], writes=[SM.b])
            op("act", lambda e: e.activation(out=WI[:P, :P], in_=DL[:P, :P], func=AF.Exp, bias=S1(36), scale=1.0),
               reads=[DL.b, SM.b], writes=[WI.b])
            op("act", lambda e: e.activation(out=S1(37), in_=S1(34), func=AF.Exp, bias=S1(36), scale=1.0), reads=[SM.b], writes=[SM.b])
            op("act", lambda e: e.activation(out=S1(38), in_=S1(36), func=AF.Exp), reads=[SM.b], writes=[SM.b])
            op("pe", lambda e: e.transpose(out=PC[:, 0:P], in_=qs, identity=C("ident", P, P)), reads=[PROJ.b, CST.b], writes=[PC.b])
            op("pe", lambda e: e.transpose(out=PC[:, 128:128 + P], in_=ks, identity=C("ident", P, P)), reads=[PROJ.b, CST.b], writes=[PC.b])
            op("act", lambda e: e.mul(out=QT[:, :P], in_=PC[:, 0:P], mul=128.0 ** -0.5), reads=[PC.b], writes=[QT.b])
            op("act", lambda e: e.copy(out=KT[:, :P], in_=PC[:, 128:128 + P]), reads=[PC.b], writes=[KT.b])
            op("pe", lambda e: e.matmul(PD[:P, 0:P], lhsT=QT[:, :P], rhs=KT[:, :P], start=True, stop=True),
               reads=[QT.b, KT.b], writes=[PD.b])
            op("dve", lambda e: e.tensor_tensor(out=AM[:P, :P], in0=WI[:P, :P], in1=PD[:P, 0:P], op=ALU.mult),
               reads=[WI.b, PD.b], writes=[AM.b])
            op("pe", lambda e: e.transpose(out=PB[:P, 128:128 + P], in_=AM[:P, :P], identity=C("ident", P, P)),
               reads=[AM.b, CST.b], writes=[PB.b])
            op("act", lambda e: e.copy(out=AT[:P, :P], in_=PB[:P, 128:128 + P]), reads=[PB.b], writes=[AT.b])
            op("pool", lambda e: e.tensor_copy(out=VX[:P, 0:128], in_=vs), reads=[PROJ.b], writes=[VX.b])
            op("pe", lambda e: e.matmul(PD[:P, 128:257], lhsT=AT[:P, :P], rhs=VX[:P, :], start=True, stop=True),
               reads=[AT.b, VX.b], writes=[PD.b])
            if is_s:
                op("pool", lambda e: e.memset(ZQ[:, :], 0.0), writes=[ZQ.b])
                for b in range(SB):
                    op("pool", lambda e, b=b: e.tensor_copy(out=ZQ[:, b * SP + b:(b + 1) * SP:SB], in_=QT[:, b:SP:SB]),
                       reads=[QT.b], writes=[ZQ.b])
                for b in range(SB):
                    op("pe", lambda e, b=b: e.matmul(PC[:P, 256:385], lhsT=ZQ[:, b * SP:(b + 1) * SP], rhs=CTS[:, b, :],
                                                     start=(b == 0), stop=(b == SB - 1)),
                       reads=[ZQ.b, CTS.b], writes=[PC.b] if b in (0, SB - 1) else [])
            else:
                op("pe", lambda e, hh=hh: e.matmul(PC[:P, 256:385], lhsT=QT[:, :P], rhs=CTX[:, hh, :], start=True, stop=True),
                   reads=[QT.b, CTX.b], writes=[PC.b])
            op("act", lambda e: e.activation(out=TOT[:P, :], in_=PC[:P, 256:385], func=AF.Identity, scale=S1(37)),
               reads=[PC.b, SM.b], writes=[TOT.b])
            op("dve", lambda e: e.tensor_tensor(out=TOT[:P, :], in0=TOT[:P, :], in1=PD[:P, 128:257], op=ALU.add),
               reads=[TOT.b, PD.b], writes=[TOT.b])
            op("dve", lambda e: e.scalar_tensor_tensor(out=S1(39), in0=TOT[:P, 128:129], scalar=-1.0, in1=TOT[:P, 128:129], op0=ALU.mult, op1=ALU.max),
               reads=[TOT.b], writes=[SM.b])
            op("dve", lambda e: e.tensor_tensor(out=S1(39), in0=S1(39), in1=S1(38), op=ALU.max), reads=[SM.b], writes=[SM.b])
            op("dve", lambda e: e.reciprocal(out=S1(40), in_=S1(39)), reads=[SM.b], writes=[SM.b])
            op("dve", lambda e: e.tensor_scalar(out=HN[:P, :], in0=TOT[:P, 0:128], scalar1=S1(40), scalar2=None, op0=ALU.mult),
               reads=[TOT.b, SM.b], writes=[HN.b])
            op("dve", lambda e: e.bn_stats(out=ST6[:P, 0, :], in_=HN[:P, :]), reads=[HN.b], writes=[ST6.b])
            op("dve", lambda e: e.bn_aggr(out=SM[:P, 41:43], in_=ST6[:P, 0, :]), reads=[ST6.b], writes=[SM.b])
            op("act", lambda e: e.activation(out=S1(43), in_=S1(42), func=AF.Ln, bias=EPSB[:P, 0:1], scale=1.0), reads=[SM.b, EPSB.b], writes=[SM.b])
            op("act", lambda e: e.activation(out=S1(44), in_=S1(43), func=AF.Exp, scale=-0.5), reads=[SM.b], writes=[SM.b])
            op("dve", lambda e: e.tensor_scalar(out=HN[:P, :], in0=HN[:P, :], scalar1=S1(41), scalar2=S1(44),
                                                op0=ALU.subtract, op1=ALU.mult), reads=[HN.b, SM.b], writes=[HN.b])
            op("dve", lambda e, hh=hh: e.tensor_tensor(out=HN[:P, :], in0=HN[:P, :],
                                                       in1=PRM[:P, o_mh + hh * 128:o_mh + (hh + 1) * 128], op=ALU.mult),
               reads=[HN.b, PRM.b], writes=[HN.b])
            op("act", lambda e: e.activation(out=SG[:P, :], in_=os_, func=AF.Exp, scale=-1.0), reads=[PROJ.b], writes=[SG.b])
            op("dve", lambda e: e.tensor_scalar_add(out=SG[:P, :], in0=SG[:P, :], scalar1=1.0), reads=[SG.b], writes=[SG.b])
            op("dve", lambda e: e.reciprocal(out=SG[:P, :], in_=SG[:P, :]), reads=[SG.b], writes=[SG.b])
            op("dve", lambda e, hh=hh: e.tensor_tensor(out=Y[:P, hh * 128:(hh + 1) * 128], in0=HN[:P, :], in1=SG[:P, :], op=ALU.mult),
               reads=[HN.b, SG.b], writes=[Y.b])
            op("dve", lambda e, hh=hh: e.tensor_tensor(out=S1(45), in0=mtok[:P, hh:hh + 1], in1=S1(32), op=ALU.max),
               reads=[SM.b, mtok.b], writes=[SM.b])
            op("dve", lambda e, hh=hh: e.tensor_tensor(out=S1(45), in0=S1(45), in1=col(28), op=ALU.add), reads=[SM.b], writes=[SM.b])
            op("dve", lambda e, hh=hh: e.tensor_tensor(out=S1(46), in0=col(28), in1=S1(45), op=ALU.subtract), reads=[SM.b], writes=[SM.b])
            op("act", lambda e, hh=hh: e.activation(out=S1(47), in_=col(24), func=AF.Exp, bias=S1(46), scale=1.0),
               reads=[SM.b], writes=[SM.b])
            op("act", lambda e, hh=hh: e.activation(out=S1(48), in_=mtok[:P, hh:hh + 1], func=AF.Exp, bias=S1(46), scale=1.0),
               reads=[SM.b, mtok.b], writes=[SM.b])
            if not is_s:
                op("dve", lambda e: e.tensor_scalar(out=WV[:P, :], in0=VX[:P, :], scalar1=S1(47), scalar2=None, op0=ALU.mult),
                   reads=[VX.b, SM.b], writes=[WV.b])
                op("pe", lambda e: e.matmul(PB[:, 256:385], lhsT=ks, rhs=WV[:P, :], start=True, stop=True),
                   reads=[PROJ.b, WV.b], writes=[PB.b])
                op("dve", lambda e, hh=hh: e.scalar_tensor_tensor(out=CTX[:, hh, :], in0=CTX[:, hh, :], scalar=S1(48), in1=PB[:, 256:385],
                                                                  op0=ALU.mult, op1=ALU.add),
                   reads=[CTX.b, SM.b, PB.b], writes=[CTX.b])
                op("dve", lambda e, hh=hh: e.tensor_copy(out=MREP[:, hh:hh + 1], in_=S1(45)), reads=[SM.b], writes=[MREP.b])
                if t == NT - 1:
                    op("pe", lambda e, hh=hh: e.transpose(out=PA[:, 128:256], in_=CTX[:, hh, 0:128], identity=C("ident")),
                       reads=[CTX.b, CST.b], writes=[PA.b])
                    op("act", lambda e: e.copy(out=OUTC[:, :], in_=PA[:, 128:256]), reads=[PA.b], writes=[OUTC.b])
                    out_dma(E["o_pC"][l, hh], OUTC[:, :], [OUTC.b])
                    out_dma(E["o_pn"][l, hh].rearrange("(k o) -> k o", o=1), CTX[:, hh, 128:129], [CTX.b])
                    if hh == 3:
                        out_dma(E["o_pm"][l:l + 1, :], MREP[0:1, :], [MREP.b])
            else:
                op("dve", lambda e: e.tensor_scalar(out=WCB[:P, :], in0=C("onehotB", SP), scalar1=S1(47), scalar2=None, op0=ALU.mult),
                   reads=[SM.b, CST.b], writes=[WCB.b])
                op("dve", lambda e: e.tensor_tensor(out=RA[:P, :, :], in0=vs.unsqueeze(1).broadcast_to([P, SB, 128]),
                                                    in1=WCB[:P, :].unsqueeze(2).broadcast_to([P, SB, 128]), op=ALU.mult),
                   reads=[PROJ.b, WCB.b], writes=[RA.b])
                op("dve", lambda e: e.tensor_scalar(out=DECD[:P, :], in0=C("onehot0", SP), scalar1=S1(48), scalar2=None, op0=ALU.mult),
                   reads=[SM.b, CST.b], writes=[DECD.b])
                op("pe", lambda e: e.matmul(PA[:, 16:32], lhsT=C("ones", SP, 128), rhs=DECD[:P, :], start=True, stop=True),
                   reads=[DECD.b, CST.b], writes=[PA.b])
                op("act", lambda e: e.copy(out=DECR[:, :], in_=PA[:, 16:32]), reads=[PA.b], writes=[DECR.b])
                for b in range(SB):
                    pq = [PA, PB, PC, PD][b % 4]
                    op("pe", lambda e, b=b, pq=pq: e.matmul(pq[:, 384:512], lhsT=RA[:P, b, :], rhs=ks, start=True, stop=True),
                       reads=[RA.b, PROJ.b], writes=[pq.b])
                    op("dve", lambda e, b=b, pq=pq: e.scalar_tensor_tensor(out=CN[:, b, :], in0=CN[:, b, :], scalar=DECR[:, b:b + 1],
                                                                           in1=pq[:, 384:512], op0=ALU.mult, op1=ALU.add),
                       reads=[CN.b, DECR.b, pq.b], writes=[CN.b])
                out_dma(E["o_nC"][l, :, hh].rearrange("b v k -> v b k"), CN[:, :, :], [CN.b])
                op("pe", lambda e: e.matmul(PA[:SB, 32:160], lhsT=WCB[:P, :], rhs=ks, start=True, stop=True),
                   reads=[WCB.b, PROJ.b], writes=[PA.b])
                op("dve", lambda e, hh=hh: e.scalar_tensor_tensor(out=NNAT[:, hh, :], in0=NNAT[:, hh, :], scalar=SM[:SB, 48:49],
                                                                  in1=PA[:SB, 32:160], op0=ALU.mult, op1=ALU.add),
                   reads=[NNAT.b, SM.b, PA.b], writes=[NNAT.b])
                op("dve", lambda e, hh=hh: e.tensor_copy(out=MSO[:, hh:hh + 1], in_=SM[:SB, 45:46]), reads=[SM.b], writes=[MSO.b])
                if hh == 3:
                    out_dma(E["o_nn"][l], NNAT[:, :, :], [NNAT.b])
                    out_dma(E["o_nm"][l], MSO[:, :], [MSO.b])

        vsv = PROJ[:P, 2312:2568].rearrange("p (g d) -> p g d", g=4)
        op("dve", lambda e: e.tensor_reduce(out=SM[:P, 50:54], in_=vsv, axis=AX.X, op=ALU.add), reads=[PROJ.b], writes=[SM.b])
        op("dve", lambda e: e.tensor_scalar(out=SM[:P, 50:54], in0=SM[:P, 50:54], scalar1=1.0 / 64, scalar2=None, op0=ALU.mult),
           reads=[SM.b], writes=[SM.b])
        op("dve", lambda e: e.tensor_tensor(out=VN[:P, :].rearrange("p (g d) -> p g d", g=4), in0=vsv,
                                            in1=SM[:P, 50:54].unsqueeze(2).broadcast_to([P, 4, 64]), op=ALU.subtract),
           reads=[PROJ.b, SM.b], writes=[VN.b])
        op("pool", lambda e: e.tensor_tensor(out=VTMP[:P, :], in0=VN[:P, :], in1=VN[:P, :], op=ALU.mult), reads=[VN.b], writes=[VTMP.b])
        op("dve", lambda e: e.tensor_reduce(out=SM[:P, 54:58], in_=VTMP[:P, :].rearrange("p (g d) -> p g d", g=4), axis=AX.X, op=ALU.add),
           reads=[VTMP.b], writes=[SM.b])
        op("act", lambda e: e.activation(out=SM[:P, 54:58], in_=SM[:P, 54:58], func=AF.Ln, bias=EPSB[:P, 0:1], scale=1.0 / 64),
           reads=[SM.b, EPSB.b], writes=[SM.b])
        op("act", lambda e: e.activation(out=SM[:P, 58:62], in_=SM[:P, 54:58], func=AF.Exp, scale=-0.5), reads=[SM.b], writes=[SM.b])
        op("dve", lambda e: e.tensor_tensor(out=VN[:P, :].rearrange("p (g d) -> p g d", g=4), in0=VN[:P, :].rearrange("p (g d) -> p g d", g=4),
                                            in1=SM[:P, 58:62].unsqueeze(2).broadcast_to([P, 4, 64]), op=ALU.mult),
           reads=[VN.b, SM.b], writes=[VN.b])
        op("pool", lambda e: e.tensor_tensor(out=VN[:P, :], in0=VN[:P, :], in1=PRM[:P, o_sg:o_sg + 256], op=ALU.mult),
           reads=[VN.b, PRM.b], writes=[VN.b])
        op("pool", lambda e: e.tensor_tensor(out=VN[:P, :], in0=VN[:P, :], in1=PRM[:P, o_sb:o_sb + 256], op=ALU.add),
           reads=[VN.b, PRM.b], writes=[VN.b])
        wsl = WSs if is_s else WS
        bsl = BSs if is_s else BS
        for g in range(4):
            op("pe", lambda e, g=g: e.matmul(PC[:P, g * 64:(g + 1) * 64], lhsT=wsl[:P, g, :P], rhs=VN[:P, g * 64:(g + 1) * 64],
                                             start=True, stop=True), reads=[wsl.b, VN.b], writes=[PC.b])
        for g in range(4):
            op("dve", lambda e, g=g: e.scalar_tensor_tensor(out=Y[:P, 512 + g * 64:512 + (g + 1) * 64], in0=PC[:P, g * 64:(g + 1) * 64],
                                                            scalar=bsl[:P, g:g + 1], in1=PROJ[:P, 2056 + g * 64:2056 + (g + 1) * 64],
                                                            op0=ALU.add, op1=ALU.mult),
               reads=[PC.b, bsl.b, PROJ.b], writes=[Y.b])
        if is_s:
            for tq in range(ST):
                out_dma(E["o_nv"][l][:, tq, :], VN[tq * SB:(tq + 1) * SB, :], [VN.b])

        pin = lambda g: PROJ[:P, 2568 + g * 64:2568 + (g + 1) * 64]
        if is_s:
            kb.dma("sp", lambda q: q.dma_start(out=SPA[:, :], in_=E["spA"][l]), SPA.b, writes=[SPA.b])
            kb.dma("sp", lambda q: q.dma_start(out=SPB[:112, :], in_=E["spB"][l]), SPB.b, writes=[SPB.b])
            for g in range(4):
                op("pe", lambda e, g=g: e.matmul(PA[:64, g * 128:g * 128 + P], lhsT=SPA[:, g * 64:(g + 1) * 64], rhs=Cg("bsA", g, 128, SP, SP),
                                                 start=True, stop=False), reads=[SPA.b, CST.b], writes=[PA.b])
                op("pe", lambda e, g=g: e.matmul(PA[:64, g * 128:g * 128 + P], lhsT=SPB[:112, g * 64:(g + 1) * 64], rhs=Cg("bsB", g, 112, SP, SP),
                                                 start=False, stop=False), reads=[SPB.b, CST.b], writes=[])
                op("pe", lambda e, g=g: e.matmul(PA[:64, g * 128:g * 128 + P], lhsT=pin(g), rhs=Cg("bsC", g, SP, SP, SP),
                                                 start=False, stop=True), reads=[PROJ.b, CST.b], writes=[PA.b])
            npv = E["o_np"][l].rearrange("b r c -> r b c")
            for r in range(4):
                out_dma(npv[r], SPA[64 + r * SB:64 + (r + 1) * SB, :], [SPA.b])
            for r in range(7):
                out_dma(npv[4 + r], SPB[r * SB:(r + 1) * SB, :], [SPB.b])
            for r in range(4):
                out_dma(npv[11 + r], PROJ[r * SB:(r + 1) * SB, 2568:2824], [PROJ.b])
        else:
            for g in range(4):
                band = Cg("bandc0" if t == 0 else "bandc", g, 128, 128, 128)
                op("pe", lambda e, g=g, band=band: e.matmul(PA[:64, g * 128:(g + 1) * 128], lhsT=pin(g), rhs=band, start=True, stop=(t == 0)),
                   reads=[PROJ.b, CST.b], writes=[PA.b])
                if t > 0:
                    op("pe", lambda e, g=g: e.matmul(PA[:64, g * 128:(g + 1) * 128], lhsT=PREV[:, g * 64:(g + 1) * 64],
                                                     rhs=Cg("bandp", g, 128, 128, 128), start=False, stop=True),
                       reads=[PREV.b, CST.b], writes=[PA.b])
            if t < NT - 1:
                op("pool", lambda e: e.tensor_copy(out=PREV[:, :], in_=PROJ[:, 2568:2824]), reads=[PROJ.b], writes=[PREV.b])
            else:
                out_dma(E["o_pp"][l], PROJ[113:128, 2568:2824], [PROJ.b])
        op("act", lambda e: e.copy(out=PTT[:, :, :P], in_=PA[:64, :].rearrange("p (g c) -> p g c", g=4)[:, :, :P]),
           reads=[PA.b], writes=[PTT.b])
        for g in range(4):
            op("pe", lambda e, g=g: e.matmul(PB[:P, g * 64:(g + 1) * 64], lhsT=PTT[:, g, :P], rhs=WP[:, g, :], start=True, stop=True),
               reads=[PTT.b, WP.b], writes=[PB.b])
        op("dve", lambda e: e.tensor_tensor(out=Y[:P, 768:1024], in0=PB[:P, 0:256], in1=PRM[:P, o_ps:o_ps + 256], op=ALU.mult),
           reads=[PB.b, PRM.b], writes=[Y.b])

        _transpose8(kb, E, Y, HT, PT, P, op=op)
        for (c0, w) in mk_chunks(D):
            stream_mm(w_o_v, c0, w, HT, P,
                      lambda pm, i, c0=c0, w=w: op("dve", lambda e: e.tensor_tensor(out=H[:P, c0:c0 + w], in0=pm[:P, 0:w],
                                                                                    in1=ada[:P, 2048 + c0:2048 + c0 + w], op=ALU.mult),
                                                   reads=[pm.b, ada.b], writes=[H.b]))
        _resid_ln(kb, X, XB[t], t, P, H, SM, ST6, PRM, o_l1g, o_l1b, EPSB)
        cur[0] = None
        if nxtA:
            run_items(nxtA)
        if ti == len(tiles) - 1 and conv:
            run_items(conv)


def _resid_ln(kb, X, xb, t, P, Z, SM, ST6, PRM, og, ob, EPSB):
    op = kb.op
    xt = X[:P, t, :]
    op("dve", lambda e: e.scalar_tensor_tensor(out=Z[:P, :], in0=xt, scalar=ALPHA, in1=Z[:P, :], op0=ALU.mult, op1=ALU.add),
       reads=[xb, Z.b], writes=[Z.b])
    op("dve", lambda e: e.bn_stats(out=ST6[:P, 0, :], in_=Z[:P, 0:512]), reads=[Z.b], writes=[ST6.b])
    op("dve", lambda e: e.bn_stats(out=ST6[:P, 1, :], in_=Z[:P, 512:1024]), reads=[Z.b], writes=[ST6.b])
    op("dve", lambda e: e.bn_aggr(out=SM[:P, 41:43], in_=ST6[:P, :, :].rearrange("p a b -> p (a b)")), reads=[ST6.b], writes=[SM.b])
    op("act", lambda e: e.activation(out=SM[:P, 43:44], in_=SM[:P, 42:43], func=AF.Ln, bias=EPSB[:P, 0:1], scale=1.0), reads=[SM.b, EPSB.b], writes=[SM.b])
    op("act", lambda e: e.activation(out=SM[:P, 44:45], in_=SM[:P, 43:44], func=AF.Exp, scale=-0.5), reads=[SM.b], writes=[SM.b])
    op("dve", lambda e: e.tensor_scalar(out=Z[:P, :], in0=Z[:P, :], scalar1=SM[:P, 41:42], scalar2=SM[:P, 44:45],
                                        op0=ALU.subtract, op1=ALU.mult), reads=[Z.b, SM.b], writes=[Z.b])
    op("pool", lambda e: e.tensor_tensor(out=Z[:P, :], in0=Z[:P, :], in1=PRM[:P, og:og + D], op=ALU.mult), reads=[Z.b, PRM.b], writes=[Z.b])
    op("dve", lambda e: e.tensor_tensor(out=xt, in0=Z[:P, :], in1=PRM[:P, ob:ob + D], op=ALU.add), reads=[Z.b, PRM.b], writes=[xb])


def _phase2(nc, kb, l, E):
    C, CST, X, XB = E["C"], E["CST"], E["X"], E["XB"]
    ADA = Tn(kb, "ADA2", [128, 3072]); ADAs = Tn(kb, "ADA2s", [128, 3072])
    WCH = [Tn(kb, "WCHb%d" % i, [128, 8, 128], dma=True) for i in range(2)]
    badac = [Tn(kb, "badab%d" % i, [128, 256], dma=True) for i in range(2)]
    Hs = [Tn(kb, "H2_%d" % i, [128, D], dma=True) for i in range(2)]
    HT = Tn(kb, "H2T", [128, 8, 128])
    PRM = Tn(kb, "PRM2", [128, 2 * D], dma=True)
    KTS = Tn(kb, "KTS", [128, 16, 128], dma=True)
    S0 = Tn(kb, "S0", [128, 2048]); S1_ = Tn(kb, "S1", [128, 2048]); S2 = Tn(kb, "S2", [128, 256])
    TOPS = Tn(kb, "TOPS", [128, 16, 16]); IDXU = Tn(kb, "IDXU", [128, 16, 16], U32); IDXF = Tn(kb, "IDXF", [128, 16, 16])
    CV = Tn(kb, "CV", [128, 8, 16]); CPOS = Tn(kb, "CPOS", [128, 8, 16], U32)
    PAU = Tn(kb, "PAU", [128, 8, 16], U32); PBU = Tn(kb, "PBU", [128, 8, 16], U32)
    PAF = Tn(kb, "PAF", [128, 8, 16]); PBF = Tn(kb, "PBF", [128, 8, 16])
    I1 = Tn(kb, "I1", [128, 128]); I2 = Tn(kb, "I2", [128, 128])
    IDXs = [Tn(kb, "IDX%d" % i, [128, 128], I32) for i in range(2)]
    GATEs = [Tn(kb, "GATE%d" % i, [128, 128]) for i in range(2)]
    ACTV = Tn(kb, "ACTV", [128, 128]); COEF = Tn(kb, "COEF", [128, 128])
    SMf = Tn(kb, "SM2f", [128, 16]); SM = Tn(kb, "SM2", [128, 64]); ST6 = Tn(kb, "ST62", [128, 2, 6])
    NB = NBUF
    UB = [Tn(kb, "UB%d" % i, [128, 2 * D], BF16, dma=True) for i in range(NB)]
    WCAB = Tn(kb, "WCAB", [128, 8, WCW], dma=True)
    ACTB = [kb.buf("actv%d" % i) for i in range(NB)]
    COEFB = [kb.buf("coef%d" % i) for i in range(NB)]
    COEF2 = Tn(kb, "COEF2", [128, 128])

    class _View:
        def __init__(self, ap, b):
            self.t = ap
            self.b = b

        def __getitem__(self, k):
            return self.t[k]
    WCA = [WCAB]
    PT = [Tn(kb, "PTb%d" % i, [128, 512], psum=True) for i in range(2)]
    PQ = [Tn(kb, "PQ%d" % i, [128, 512], psum=True) for i in range(2)]
    PM = PQ
    ACCP = [Tn(kb, "ACCP%d" % i, [128, 512], psum=True) for i in range(2)]
    JUNK = Tn(kb, "JUNK", [128, D])
    ACC = JUNK
    DGB = [Tn(kb, "DGB%d" % i, [128, 128], BF16) for i in range(3)]
    PS = [Tn(kb, "PS%d" % i, [128, 512], psum=True) for i in range(2)]

    kb.dma("sp", lambda q: q.dma_start(out=PRM[:, 0:D], in_=E["ln2g"][l]), PRM.b, writes=[PRM.b])
    kb.dma("sp", lambda q: q.dma_start(out=PRM[:, D:2 * D], in_=E["ln2b"][l]), PRM.b, writes=[PRM.b])
    kb.dma("sp", lambda q: q.dma_start(out=KTS[:, :, :], in_=E["keysT"][l]), KTS.b, writes=[KTS.b])
    wpq_v = E["w_pq"][l]

    tab = E["puvb"][l]
    PUVB = E["PUVBT"][l]
    wctr = [0]

    def make_front(t, sset):
        items = []

        def op(e, fn, reads=(), writes=()):
            items.append(("op", e, _bind(fn), None, tuple(reads), tuple(writes)))

        def dma(e, fn, owner, reads=(), writes=()):
            items.append(("dma", e, _bind(fn), owner, tuple(reads), tuple(writes)))
        is_s = (t == NT)
        P = SP if is_s else 128
        ada = ADAs if is_s else ADA
        H = Hs[sset]; IDX = IDXs[sset]; GATE = GATEs[sset]
        QT = S0
        xt = X[:P, t, :]
        op("dve", lambda e: e.tensor_tensor(out=H[:P, :], in0=xt, in1=ada[:P, 1024:2048], op=ALU.mult), reads=[XB[t], ada.b], writes=[H.b])
        op("dve", lambda e: e.tensor_tensor(out=H[:P, :], in0=H[:P, :], in1=ada[:P, 0:1024], op=ALU.add), reads=[H.b, ada.b], writes=[H.b])
        _transpose8(kb, E, H, HT, PT, P, op=op)
        def wdma(c):
            i = c % 2
            dma("sp", lambda q: q.dma_start(out=WCH[i][:, :, :], in_=wpq_v[c].rearrange("p (k j) -> p k j", k=8)), WCH[i].b, writes=[WCH[i].b])
        wdma(0)
        for c in range(16):
            i = c % 2
            j = c % 2
            if c + 1 < 16:
                wdma(c + 1)
            for k in range(8):
                op("pe", lambda e: e.matmul(PQ[j][:, 0:P], lhsT=WCH[i][:, k, :], rhs=HT[:, k, :P], start=(k == 0), stop=(k == 7)),
                   reads=[WCH[i].b, HT.b], writes=[PQ[j].b] if k in (0, 7) else [])
            op("act", lambda e: e.copy(out=QT[:, c * 128:c * 128 + P], in_=PQ[j][:, 0:P]), reads=[PQ[j].b], writes=[QT.b])
        for c4 in range(4):
            ps = PS[c4 % 2]
            for j in range(4):
                c = c4 * 4 + j
                op("pe", lambda e: e.matmul(ps[:P, j * 128:(j + 1) * 128], lhsT=QT[:, c * 128:c * 128 + P], rhs=KTS[:, c, :], start=True, stop=True),
                   reads=[QT.b, KTS.b], writes=[ps.b])
            op("act", lambda e: e.copy(out=S1_[:P, c4 * 512:(c4 + 1) * 512], in_=ps[:P, :]), reads=[ps.b], writes=[S1_.b])
        for c in range(16):
            sc = S1_[:P, c * 128:(c + 1) * 128]
            wk = S2[:P, 0:128]
            op("dve", lambda e: e.max(out=TOPS[:P, c, 0:8], in_=sc), reads=[S1_.b], writes=[TOPS.b])
            op("dve", lambda e: e.max_index(out=IDXU[:P, c, 0:8], in_max=TOPS[:P, c, 0:8], in_values=sc), reads=[S1_.b, TOPS.b], writes=[IDXU.b])
            op("dve", lambda e: e.match_replace(out=wk, in_to_replace=TOPS[:P, c, 0:8], in_values=sc, imm_value=NEG), reads=[S1_.b, TOPS.b], writes=[S2.b])
            op("dve", lambda e: e.max(out=TOPS[:P, c, 8:16], in_=wk), reads=[S2.b], writes=[TOPS.b])
            op("dve", lambda e: e.max_index(out=IDXU[:P, c, 8:16], in_max=TOPS[:P, c, 8:16], in_values=wk), reads=[S2.b, TOPS.b], writes=[IDXU.b])
        op("dve", lambda e: e.tensor_copy(out=IDXF[:P, :, :], in_=IDXU[:P, :, :]), reads=[IDXU.b], writes=[IDXF.b])
        tv = TOPS[:P, :, :].rearrange("p (h two) k -> p h two k", two=2)
        CAND = S0
        op("dve", lambda e: e.tensor_tensor(out=CAND[:P, :].rearrange("p (h a b) -> p h a b", h=8, a=16),
                                            in0=tv[:, :, 0, :].unsqueeze(3).broadcast_to([P, 8, 16, 16]),
                                            in1=tv[:, :, 1, :].unsqueeze(2).broadcast_to([P, 8, 16, 16]), op=ALU.add),
           reads=[TOPS.b], writes=[S0.b])
        for h in range(8):
            cd = CAND[:P, h * 256:(h + 1) * 256]
            wk = S2[:P, 0:256]
            op("dve", lambda e: e.max(out=CV[:P, h, 0:8], in_=cd), reads=[S0.b], writes=[CV.b])
            op("dve", lambda e: e.max_index(out=CPOS[:P, h, 0:8], in_max=CV[:P, h, 0:8], in_values=cd), reads=[S0.b, CV.b], writes=[CPOS.b])
            op("dve", lambda e: e.match_replace(out=wk, in_to_replace=CV[:P, h, 0:8], in_values=cd, imm_value=NEG), reads=[S0.b, CV.b], writes=[S2.b])
            op("dve", lambda e: e.max(out=CV[:P, h, 8:16], in_=wk), reads=[S2.b], writes=[CV.b])
            op("dve", lambda e: e.max_index(out=CPOS[:P, h, 8:16], in_max=CV[:P, h, 8:16], in_values=wk), reads=[S2.b, CV.b], writes=[CPOS.b])
        op("dve", lambda e: e.tensor_single_scalar(out=PAU[:P, :, :], in_=CPOS[:P, :, :], scalar=4, op=ALU.logical_shift_right), reads=[CPOS.b], writes=[PAU.b])
        op("dve", lambda e: e.tensor_single_scalar(out=PBU[:P, :, :], in_=CPOS[:P, :, :], scalar=15, op=ALU.bitwise_and), reads=[CPOS.b], writes=[PBU.b])
        op("dve", lambda e: e.tensor_copy(out=PAF[:P, :, :], in_=PAU[:P, :, :]), reads=[PAU.b], writes=[PAF.b])
        op("dve", lambda e: e.tensor_copy(out=PBF[:P, :, :], in_=PBU[:P, :, :]), reads=[PBU.b], writes=[PBF.b])
        iv = IDXF[:P, :, :].rearrange("p (h two) k -> p h two k", two=2)
        io16 = C("iota16", P).unsqueeze(1).unsqueeze(1).broadcast_to([P, 8, 16, 16])
        for (pf, half, dst) in [(PAF, 0, I1), (PBF, 1, I2)]:
            eq = S1_[:P, :].rearrange("p (h k a) -> p h k a", h=8, k=16)
            op("dve", lambda e: e.tensor_tensor(out=eq, in0=pf[:P, :, :].unsqueeze(3).broadcast_to([P, 8, 16, 16]), in1=io16, op=ALU.is_equal),
               reads=[pf.b, CST.b], writes=[S1_.b])
            op("dve", lambda e: e.tensor_tensor(out=eq, in0=eq, in1=iv[:, :, half, :].unsqueeze(2).broadcast_to([P, 8, 16, 16]), op=ALU.mult),
               reads=[S1_.b, IDXF.b], writes=[S1_.b])
            op("dve", lambda e: e.tensor_reduce(out=dst[:P, :].rearrange("p (h k) -> p h k", h=8), in_=eq, axis=AX.X, op=ALU.add),
               reads=[S1_.b], writes=[dst.b])
        op("dve", lambda e: e.scalar_tensor_tensor(out=I1[:P, :], in0=I1[:P, :], scalar=128.0, in1=I2[:P, :], op0=ALU.mult, op1=ALU.add),
           reads=[I1.b, I2.b], writes=[I1.b])
        op("dve", lambda e: e.tensor_copy(out=IDX[:P, :], in_=I1[:P, :]), reads=[I1.b], writes=[IDX.b])
        gv = GATE[:P, :].rearrange("p (h k) -> p h k", h=8)
        op("dve", lambda e: e.tensor_tensor(out=gv, in0=CV[:P, :, :], in1=CV[:P, :, 0:1].broadcast_to([P, 8, 16]), op=ALU.subtract),
           reads=[CV.b], writes=[GATE.b])
        op("act", lambda e: e.activation(out=GATE[:P, :], in_=GATE[:P, :], func=AF.Exp), reads=[GATE.b], writes=[GATE.b])
        op("dve", lambda e: e.tensor_reduce(out=SMf[:P, 0:8], in_=gv, axis=AX.X, op=ALU.add), reads=[GATE.b], writes=[SMf.b])
        op("dve", lambda e: e.reciprocal(out=SMf[:P, 8:16], in_=SMf[:P, 0:8]), reads=[SMf.b], writes=[SMf.b])
        op("dve", lambda e: e.tensor_tensor(out=gv, in0=gv, in1=SMf[:P, 8:16].unsqueeze(2).broadcast_to([P, 8, 16]), op=ALU.mult),
           reads=[GATE.b, SMf.b], writes=[GATE.b])
        return items

    def run_items(items, n=None):
        n = len(items) if n is None else min(n, len(items))
        for _ in range(n):
            kind, e, fn, owner, reads, writes = items.pop(0)
            if kind == "op":
                kb.op(e, fn, reads=reads, writes=writes, bound=True)
            else:
                kb.dma(e, fn, owner, reads=reads, writes=writes, bound=True)

    def back(t, sset, nxt):
        op = kb.op
        is_s = (t == NT)
        P = SP if is_s else 128
        ada = ADAs if is_s else ADA
        H = Hs[sset]; IDX = IDXs[sset]; GATE = GATEs[sset]
        per = 0 if not nxt else (len(nxt) + 119) // 120

        def axpy(s):
            b = s % NB
            dg = DGB[s % 3]
            op("act", lambda e: e.activation(out=COEF2[:P, s:s + 1], in_=COEF[:P, s:s + 1], func=AF.Identity, scale=GATE[:P, s:s + 1]),
               reads=[COEFB[b], GATE.b], writes=[COEF2.b])
            op("act", lambda e: e.activation(out=dg[:P, :P], in_=C("ident", P, P), func=AF.Identity, scale=COEF2[:P, s:s + 1]),
               reads=[COEF2.b, CST.b], writes=[dg.b])
            for hf in range(2):
                op("pe", lambda e: e.matmul(ACCP[hf][:P, :], lhsT=dg[:P, :P], rhs=UB[b][:P, D + hf * 512:D + (hf + 1) * 512], start=(s == 0), stop=(s == 127)),
                   reads=[dg.b, UB[b].b], writes=[ACCP[hf].b] if s in (0, 127) else [])

        for s_ in range(128):
            b = s_ % NB
            kb.dma("pool", lambda q: q.indirect_dma_start(out=UB[b][:P, :], out_offset=None, in_=tab,
                                                          in_offset=bass.IndirectOffsetOnAxis(ap=IDX[:P, s_:s_ + 1], axis=0)),
                   UB[b].b, reads=[IDX.b, PUVB], writes=[UB[b].b])
            op("dve", lambda e: e.scalar_tensor_tensor(out=JUNK[:P, :], in0=UB[b][:P, 0:D], scalar=1.0, in1=H[:P, :],
                                                       op0=ALU.mult, op1=ALU.mult, accum_out=ACTV[:P, s_:s_ + 1]),
               reads=[UB[b].b, H.b], writes=[JUNK.b, ACTB[b]])
            op("act", lambda e: e.activation(out=COEF[:P, s_:s_ + 1], in_=ACTV[:P, s_:s_ + 1], func=AF.Gelu), reads=[ACTB[b]], writes=[COEFB[b]])
            if s_ >= 1:
                axpy(s_ - 1)
            if nxt:
                run_items(nxt, per)
        axpy(127)
        if nxt:
            run_items(nxt)
        for hf in range(2):
            op("dve", lambda e: e.tensor_tensor(out=ACC[:P, hf * 512:(hf + 1) * 512], in0=ACCP[hf][:P, :], in1=ada[:P, 2048 + hf * 512:2048 + (hf + 1) * 512],
                                                op=ALU.mult), reads=[ACCP[hf].b, ada.b], writes=[ACC.b])
        _resid_ln(kb, X, XB[t], t, P, ACC, SM, ST6, PRM, 0, D, E["EPSB"])

    _ada(nc, kb, l, E, ADA, 128, E["cp"], 3072, WCA, PM, Hs[0], HT, PT, badac)
    _ada(nc, kb, l, E, ADAs, SP, E["cs"], 3072, WCA, PM, Hs[0], HT, PT, badac)
    run_items(make_front(0, 0))
    for t in range(NT + 1):
        nxt = make_front(t + 1, (t + 1) % 2) if t + 1 <= NT else None
        back(t, t % 2, nxt)


_CACHE = {}


def _chunked(w, cw):
    L, K, n = w.shape
    nch = (n + cw - 1) // cw
    wp = np.zeros((L, K, nch * cw), np.float32)
    wp[:, :, :n] = w
    wp = wp.reshape(L, 8, 128, nch, cw).transpose(0, 3, 2, 1, 4)
    return np.ascontiguousarray(wp.reshape(L, nch, 128, 8 * cw))


def _rep(a, P=128):
    return np.ascontiguousarray(np.broadcast_to(a[:, None, :], (a.shape[0], P, a.shape[1])))


def make_in_maps(inp, cpack):
    f = lambda a: np.ascontiguousarray(np.asarray(a, dtype=np.float32))
    shared = {
        "w_ada": _chunked(f(inp["w_ada"]), WCW), "b_ada": _rep(f(inp["b_ada"])), "w_in": _chunked(f(inp["w_in"]), WCW),
        "b_gate": _rep(f(inp["b_gate"])),
        "mh_g": _rep(f(inp["mh_g"])), "sgu_g": _rep(f(inp["sgu_g"])), "sgu_b": _rep(f(inp["sgu_b"])),
        "pscale": _rep(f(inp["pool_scale"])),
        "w_sT": f(np.asarray(inp["w_s"]).transpose(0, 3, 1, 2)),
        "b_sT": f(np.asarray(inp["b_s"]).transpose(0, 2, 1)),
        "w_pool": f(np.asarray(inp["w_pool"]).transpose(0, 2, 1, 3)),
        "w_o": _chunked(f(inp["w_o"]), WCW), "ln1g": _rep(f(inp["ln1_g"])), "ln1b": _rep(f(inp["ln1_b"])),
        "ln2g": _rep(f(inp["ln2_g"])), "ln2b": _rep(f(inp["ln2_b"])), "w_pq": _chunked(f(inp["w_pq"]), 128),
        "keysT": f(np.asarray(inp["peer_keys"]).transpose(0, 4, 1, 2, 3).reshape(DEPTH, 128, 16, 128)),
        "cst": cpack[0], "cst1": cpack[1],
    }
    ws4 = np.asarray(inp["w_s"])[:, :, :ST, :ST]
    wsS = np.repeat(np.repeat(ws4.transpose(0, 3, 1, 2), SB, axis=1), SB, axis=3)
    shared["w_sS"] = f(wsS)
    bs4 = np.asarray(inp["b_s"])[:, :, :ST]
    shared["b_sS"] = f(np.repeat(bs4.transpose(0, 2, 1), SB, axis=1))
    for l in range(DEPTH):
        shared["puv%d" % l] = np.ascontiguousarray(
            np.concatenate([np.asarray(inp["peer_u"])[l], np.asarray(inp["peer_v"])[l]], axis=1), dtype=np.float32)
    maps = []
    for c in range(NCORES):
        bs = slice(c * SB, (c + 1) * SB)
        m = dict(shared)
        m["xp"] = f(np.asarray(inp["x_prompt"])[c])
        m["xs"] = f(np.asarray(inp["x_sample"])[bs].transpose(1, 0, 2).reshape(SP, D))
        m["cp"] = f(np.broadcast_to(np.asarray(inp["c_prompt"])[c][None, :], (128, D)))
        m["cs"] = f(np.tile(np.asarray(inp["c_sample"])[bs], (ST, 1)))
        sCc = np.asarray(inp["state_mlstm_C"])[:, bs]
        m["sC"] = f(sCc.transpose(0, 2, 3, 1, 4))
        snc = np.asarray(inp["state_mlstm_n"])[:, bs]
        m["snat"] = f(snc)
        m["snT"] = f(snc.transpose(0, 2, 3, 1))
        m["sm"] = f(np.tile(np.asarray(inp["state_mlstm_m"])[:, bs], (1, ST, 1)))
        spc = np.asarray(inp["state_pool"])[:, bs].transpose(0, 2, 1, 3)
        m["spA"] = f(spc[:, 0:8].reshape(DEPTH, 128, 256))
        m["spB"] = f(spc[:, 8:15].reshape(DEPTH, 112, 256))
        maps.append(m)
    return maps


def gather_outputs(results):
    cat = lambda k, ax: np.concatenate([r[k] for r in results], axis=ax)
    yp = np.stack([r["yp"] for r in results], 0)
    ys = np.concatenate([r["ys"].reshape(ST, SB, D).transpose(1, 0, 2) for r in results], 0)
    pC = np.stack([r["pC"] for r in results], 1)
    pn = np.stack([r["pn"] for r in results], 1)
    pm = np.stack([r["pm"] for r in results], 1)
    pp = np.stack([r["pp"] for r in results], 1)
    return (yp, ys, pC, pn, pm, pp, cat("nC", 1), cat("nn", 1), cat("nm", 1), cat("npool", 1), cat("nv", 1))


def kernel(**inputs):
    if "prog" not in _CACHE:
        _CACHE["prog"] = build_program()
    nc, cpack = _CACHE["prog"]
    maps = make_in_maps(inputs, cpack)
    res = run_bass_kernel_spmd(nc, maps, core_ids=list(range(NCORES)))
    outs = gather_outputs(res.results)
    return tuple(np.ascontiguousarray(o, dtype=np.float32) for o in outs)
```

```python
import numpy as np
from contextlib import ExitStack
import concourse.bass as bass
import concourse.mybir as mybir
from concourse.bass_utils import run_bass_kernel_spmd

F32 = mybir.dt.float32
I32 = mybir.dt.int32
U32 = mybir.dt.uint32
F32R = mybir.dt.float32r
BF16 = mybir.dt.bfloat16
ALU = mybir.AluOpType
AF = mybir.ActivationFunctionType
AX = mybir.AxisListType

NCORES = 8
D = 1024
SEQ = 2048
NT = 16
SB = 16
ST = 4
SP = SB * ST
DEPTH = 2
ALPHA = (2 * DEPTH) ** 0.25
LN_EPS = 1e-5
IN_COLS = 2824
NEG = -1.0e30
WCW = 192
NEXP = 16384
SAME_ENGINE_WAITS = True
NBUF = 12


class TB:
    def __init__(self, name, sem=None):
        self.name = name
        self.last_w = None
        self.reads = []
        self.sem = sem
        self.dma_total = 0
        self.dma_dirty = False


class KB:
    ENG = ("pe", "act", "dve", "pool", "sp")

    def __init__(self, nc, stack):
        self.nc = nc
        self.stack = stack
        self.q = {e: [] for e in self.ENG}
        self.cnt = {e: 0 for e in self.ENG}
        self.esem = {e: stack.enter_context(nc.semaphore("es_" + e)) for e in self.ENG}
        self.seen = {e: {} for e in self.ENG}
        self.semobj = {}
        self._sem_owner = {}
        self.stack0 = stack
        self.phase_tbs = []
        self.sem_pool = []
        self.nsem = 0
        self.sfx = ""

    def new_sem(self, name):
        return self.stack.enter_context(self.nc.semaphore(name + self.sfx))

    def buf(self, name, dma=False):
        if not dma:
            return TB(name)
        if self.sem_pool:
            sem, val = self.sem_pool.pop()
        else:
            sem, val = self.stack0.enter_context(self.nc.semaphore("dsem%d" % self.nsem)), 0
            self.nsem += 1
        tb = TB(name, sem)
        tb.dma_total = val
        if self.stack is not self.stack0:
            self.phase_tbs.append(tb)
        return tb

    def end_phase(self):
        for tb in self.phase_tbs:
            self._sem_owner.pop(id(tb.sem), None)
            self.sem_pool.append((tb.sem, tb.dma_total))
        self.phase_tbs = []

    def sb(self, name, shape, dt=F32):
        return self.stack.enter_context(self.nc.sbuf_tensor(name + self.sfx, list(shape), dt))

    def ps(self, name, shape, dt=F32):
        return self.stack.enter_context(self.nc.psum_tensor(name + self.sfx, list(shape), dt))

    def _deps(self, e, reads, writes):
        deps = {}

        def add(tok):
            if tok is None:
                return
            s, v = tok
            k = id(s)
            self.semobj[k] = s
            ow = self._sem_owner.get(k)
            if ow is not None:
                v = ow.dma_total
            if v > deps.get(k, 0):
                deps[k] = v
        for b in reads:
            add(b.last_w)
        for b in writes:
            add(b.last_w)
            for r in b.reads:
                add(r)
        out = []
        own = id(self.esem[e])
        for k, v in deps.items():
            if k == own and (e in ("pe", "sp") or not SAME_ENGINE_WAITS):
                continue
            if self.seen[e].get(k, 0) >= v:
                continue
            self.seen[e][k] = v
            out.append((self.semobj[k], v))
        return out

    def op(self, e, fn, reads=(), writes=(), bound=False):
        waits = self._deps(e, reads, writes)
        for s, v in waits:
            tb = self._sem_owner.get(id(s))
            if tb is not None:
                tb.dma_dirty = True
        self.cnt[e] += 1
        tok = (self.esem[e], self.cnt[e])
        self.q[e].append((waits, fn if bound else _bind(fn), tok[0], 1))
        for b in reads:
            b.reads.append(tok)
        for b in writes:
            b.last_w = tok
            b.reads = []
        return tok

    def dma(self, e, fn, owner, reads=(), writes=(), bound=False):
        self._sem_owner[id(owner.sem)] = owner
        waits = self._deps(e, reads, writes)
        if owner.dma_dirty and owner.dma_total > 0:
            k = id(owner.sem)
            if self.seen[e].get(k, 0) < owner.dma_total:
                self.seen[e][k] = owner.dma_total
                waits.append((owner.sem, owner.dma_total))
            owner.dma_dirty = False
        for s, v in waits:
            tb = self._sem_owner.get(id(s))
            if tb is not None and tb is not owner:
                tb.dma_dirty = True
        owner.dma_total += 16
        tok = (owner.sem, owner.dma_total)
        self.q[e].append((waits, fn if bound else _bind(fn), owner.sem, 16))
        for b in reads:
            b.reads.append(tok)
        for b in writes:
            b.last_w = tok
            b.reads = []
        return tok

    def barrier(self, extra=()):
        toks = [(self.esem[e], self.cnt[e]) for e in self.ENG if self.cnt[e] > 0 and e != "sp"]
        for tb in list(self._sem_owner.values()) + list(extra):
            if tb.dma_total > 0:
                toks.append((tb.sem, tb.dma_total))
        for e in self.ENG:
            waits = []
            for s, v in toks:
                k = id(s)
                if k == id(self.esem[e]):
                    continue
                if self.seen[e].get(k, 0) >= v:
                    continue
                self.seen[e][k] = v
                waits.append((s, v))
            if waits:
                self.q[e].append((waits, None, None, 0))

    def emit(self, final_waits=()):
        nc = self.nc
        engs = {"pe": "tensor", "act": "scalar", "dve": "vector", "pool": "gpsimd", "sp": "sync"}
        with nc.Block() as block:
            for e in self.ENG:
                items = self.q[e]
                fw = list(final_waits) if e == "sp" else []

                def body(eng, items=items, fw=fw):
                    for waits, fn, sem, inc in items:
                        for s, v in waits:
                            eng.wait_ge(s, v)
                        if fn is not None:
                            fn(eng).then_inc(sem, inc)
                    for s, v in fw:
                        eng.wait_ge(s, v)
                getattr(block, engs[e])(body)
        self.q = {e: [] for e in self.ENG}


class _Rec:
    def __init__(self):
        self.call = None

    def __getattr__(self, name):
        def f(*a, **k):
            self.call = (name, a, k)
            return self
        return f


def _bind(fn):
    r = _Rec()
    fn(r)
    assert r.call is not None
    name, a, k = r.call
    return lambda eng: getattr(eng, name)(*a, **k)


class Tn:
    def __init__(self, kb, name, shape, dt=F32, psum=False, dma=False):
        self.t = kb.ps(name, shape, dt) if psum else kb.sb(name, shape, dt)
        self.b = kb.buf(name, dma=dma)

    def __getitem__(self, k):
        return self.t[k]


def _consts():
    c = {}
    i128 = np.arange(128)
    c["ident"] = np.eye(128, dtype=np.float32)
    c["ones"] = np.ones((128, 128), np.float32)
    c["triu"] = (i128[:, None] <= i128[None, :]).astype(np.float32)
    c["negm"] = np.where(i128[None, :] <= i128[:, None], 0.0, NEG).astype(np.float32)
    sel = np.zeros((128, 128), np.float32); sel[127, :] = 1.0
    c["sel127"] = sel
    p = np.arange(SP); tt = p // SB; bb = p % SB
    sameb = bb[:, None] == bb[None, :]
    tri_s = (sameb & (tt[:, None] <= tt[None, :])).astype(np.float32)
    c["tri_s"] = _pad(tri_s)
    c["negm_s"] = _pad(np.where(sameb & (tt[None, :] <= tt[:, None]), 0.0, NEG).astype(np.float32))
    c["negb_s"] = _pad(np.where(sameb, 0.0, NEG).astype(np.float32))
    c["selend"] = _pad(((tt[:, None] == ST - 1) & sameb).astype(np.float32))
    oh = (bb[:, None] == np.arange(SB)[None, :]).astype(np.float32)
    c["onehotB"] = _pad(oh, cols=16)
    oh0 = ((p[:, None] == np.arange(SB)[None, :])).astype(np.float32)
    c["onehot0"] = _pad(oh0, cols=16)
    c["iota16"] = np.broadcast_to(np.arange(16, dtype=np.float32), (128, 16)).copy()
    wins = (2, 4, 8, 16)
    bc0 = np.zeros((4, 128, 128), np.float32); bc = np.zeros((4, 128, 128), np.float32)
    bp = np.zeros((4, 128, 128), np.float32)
    for g, w in enumerate(wins):
        for t in range(128):
            for j in range(w):
                s = t - j
                if s >= 0:
                    bc[g, s, t] += 1.0 / w
                    bc0[g, s, t] += 1.0 / min(t + 1, w)
                else:
                    bp[g, s + 128, t] += 1.0 / w
            bc[g, t, t] -= 1.0
            bc0[g, t, t] -= 1.0
    c["bandc0"] = bc0.transpose(1, 0, 2).reshape(128, 512)
    c["bandc"] = bc.transpose(1, 0, 2).reshape(128, 512)
    c["bandp"] = bp.transpose(1, 0, 2).reshape(128, 512)
    bsA = np.zeros((4, 128, SP), np.float32); bsB = np.zeros((4, 128, SP), np.float32)
    bsC = np.zeros((4, 128, SP), np.float32)
    for g, w in enumerate(wins):
        for t in range(ST):
            for b in range(SB):
                col = t * SB + b
                for j in range(w):
                    r = 15 + t - j
                    if r >= 15:
                        bsC[g, (r - 15) * SB + b, col] += 1.0 / w
                    elif r >= 8:
                        bsB[g, (r - 8) * SB + b, col] += 1.0 / w
                    else:
                        bsA[g, r * SB + b, col] += 1.0 / w
                bsC[g, t * SB + b, col] -= 1.0
    c["bsA"] = bsA.transpose(1, 0, 2).reshape(128, 4 * SP)
    c["bsB"] = bsB.transpose(1, 0, 2).reshape(128, 4 * SP)
    c["bsC"] = bsC.transpose(1, 0, 2).reshape(128, 4 * SP)
    return c


def _pad(a, cols=None):
    out = np.zeros((128, a.shape[1] if cols is None else cols), np.float32)
    out[: a.shape[0], : a.shape[1]] = a
    return out


_CONST_G = ["ident", "ones", "iota16"]
_CONST_1 = ["triu", "negm", "sel127", "tri_s", "negm_s", "negb_s", "selend",
            "onehotB", "onehot0", "bandc0", "bandc", "bandp", "bsA", "bsB", "bsC"]


def _const_pack():
    c = _consts()
    packs = []
    for order in (_CONST_G, _CONST_1):
        offs = {}
        o = 0
        arrs = []
        for k in order:
            offs[k] = (o, c[k].shape[1])
            o += c[k].shape[1]
            arrs.append(c[k])
        packs.append((np.ascontiguousarray(np.concatenate(arrs, axis=1)), offs))
    return packs


def build_program(n_layers=DEPTH, do_phase2=True):
    (cpack, coff), (cpack1, coff1) = _const_pack()
    NCST = cpack.shape[1]
    NCST1 = cpack1.shape[1]
    nc = bass.Bass("TRN2", target_bir_lowering=False)

    def din(name, shape, dt=F32):
        return nc.dram_tensor(name, list(shape), dt, kind="ExternalInput").ap()

    def dout(name, shape, dt=F32):
        return nc.dram_tensor(name, list(shape), dt, kind="ExternalOutput").ap()

    xp = din("xp", [SEQ, D]); xs = din("xs", [SP, D])
    cp = din("cp", [128, D]); cs = din("cs", [SP, D])
    sC = din("sC", [DEPTH, 4, 128, SB, 128]); snat = din("snat", [DEPTH, SB, 4, 128])
    snT = din("snT", [DEPTH, 4, 128, SB]); sm = din("sm", [DEPTH, SP, 4])
    spA = din("spA", [DEPTH, 128, 256]); spB = din("spB", [DEPTH, 112, 256])
    w_ada = din("w_ada", [DEPTH, (6 * D) // WCW, 128, 8 * WCW]); b_ada = din("b_ada", [DEPTH, 128, 6 * D])
    w_in = din("w_in", [DEPTH, (IN_COLS + WCW - 1) // WCW, 128, 8 * WCW]); b_gate = din("b_gate", [DEPTH, 128, 8])
    mh_g = din("mh_g", [DEPTH, 128, 512]); sgu_g = din("sgu_g", [DEPTH, 128, 256])
    sgu_b = din("sgu_b", [DEPTH, 128, 256]); pscale = din("pscale", [DEPTH, 128, 256])
    w_sT = din("w_sT", [DEPTH, 128, 4, 128]); b_sT = din("b_sT", [DEPTH, 128, 4])
    w_sS = din("w_sS", [DEPTH, SP, 4, SP]); b_sS = din("b_sS", [DEPTH, SP, 4])
    w_pool = din("w_pool", [DEPTH, 64, 4, 64]); w_o = din("w_o", [DEPTH, (D + WCW - 1) // WCW, 128, 8 * WCW])
    ln1g = din("ln1g", [DEPTH, 128, D]); ln1b = din("ln1b", [DEPTH, 128, D])
    ln2g = din("ln2g", [DEPTH, 128, D]); ln2b = din("ln2b", [DEPTH, 128, D])
    w_pq = din("w_pq", [DEPTH, 16, 128, 8 * 128]); keysT = din("keysT", [DEPTH, 128, 16, 128])
    puv = [din("puv%d" % l, [NEXP, 2 * D]) for l in range(DEPTH)]
    puvb = [nc.dram_tensor("puvb%d" % l, [NEXP, 2 * D], BF16, kind="Internal").ap() for l in range(DEPTH)]
    cst_d = din("cst", [128, NCST])
    cst1_d = din("cst1", [128, NCST1])

    yp = dout("yp", [SEQ, D]); ys = dout("ys", [SP, D])
    o_pC = dout("pC", [DEPTH, 4, 128, 128]); o_pn = dout("pn", [DEPTH, 4, 128]); o_pm = dout("pm", [DEPTH, 4])
    o_pp = dout("pp", [DEPTH, 15, 256])
    o_nC = dout("nC", [DEPTH, SB, 4, 128, 128]); o_nn = dout("nn", [DEPTH, SB, 4, 128])
    o_nm = dout("nm", [DEPTH, SB, 4]); o_np = dout("npool", [DEPTH, SB, 15, 256])
    o_nv = dout("nv", [DEPTH, SB, ST, 256])

    with ExitStack() as st0:
        kb = KB(nc, st0)
        op = kb.op
        OUT = kb.buf("outs", dma=True)

        def out_dma(dst, src, reads):
            kb.dma("sp", lambda q: q.dma_start(out=dst, in_=src), OUT, reads=reads)

        X = kb.sb("X", [128, NT + 1, D])
        XB = [kb.buf("X%d" % t) for t in range(NT + 1)]
        XL = kb.buf("xload", dma=True)
        CST = Tn(kb, "CST", [128, NCST], dma=True)
        PUVBT = [kb.buf("puvb%d" % i, dma=True) for i in range(DEPTH)]
        EPSB = Tn(kb, "EPSB", [128, 1])
        kb.op("dve", lambda e: e.memset(EPSB[:, :], LN_EPS), writes=[EPSB.b])

        def C(name, P=128, w=None):
            if name in coff:
                o, n = coff[name]
                return CST[:P, o:o + (n if w is None else w)]
            o, n = coff1[name]
            return kb.cst1[:P, o:o + (n if w is None else w)]

        def Cg(name, g, P, blk, w):
            o, n = coff1[name]
            return kb.cst1[:P, o + g * blk: o + g * blk + w]

        with nc.allow_non_contiguous_dma(reason="small strided state/param loads"):
            kb.dma("sp", lambda q: q.dma_start(out=CST[:, :], in_=cst_d), CST.b, writes=[CST.b])
            for t in range(NT):
                kb.dma("sp", lambda q, t=t: q.dma_start(out=X[:, t, :], in_=xp[t * 128:(t + 1) * 128, :]),
                       XL, writes=[XB[t]])
            kb.dma("sp", lambda q: q.dma_start(out=X[:SP, NT, :], in_=xs), XL, writes=[XB[NT]])

            puvb_ = puvb
            for l in range(n_layers):
                for part in ("p", "s"):
                    with ExitStack() as st1:
                        kb.stack = st1
                        kb.sfx = "_a%s%d" % (part, l)
                        _phase1(nc, kb, l, locals(), part)
                        kb.barrier(extra=[OUT])
                        kb.emit()
                        kb.end_phase()
                if do_phase2:
                    with ExitStack() as st2:
                        kb.stack = st2
                        kb.sfx = "_b%d" % l
                        _phase2(nc, kb, l, locals())
                        kb.barrier(extra=[OUT])
                        kb.emit()
                        kb.end_phase()
            kb.stack = st0
            kb.sfx = ""
            for t in range(NT):
                out_dma(yp[t * 128:(t + 1) * 128, :], X[:, t, :], [XB[t]])
            out_dma(ys, X[:SP, NT, :], [XB[NT]])
            kb.emit(final_waits=[(OUT.sem, OUT.dma_total)])
    return nc, (cpack, cpack1)


def _ada(nc, kb, l, E, ADA, P, csrc, off, WCH, PM, hbuf, hT, PT, badac, WCR=None):
    op = kb.op
    C = E["C"]
    w_ada, b_ada = E["w_ada"], E["b_ada"]
    kb.dma("sp", lambda q: q.dma_start(out=hbuf[:P, :], in_=csrc), hbuf.b, writes=[hbuf.b])
    op("act", lambda e: e.activation(out=hbuf[:P, :], in_=hbuf[:P, :], func=AF.Silu), reads=[hbuf.b], writes=[hbuf.b])
    _transpose8(kb, E, hbuf, hT, PT, P)
    r32 = (hT.t.dtype == F32R)
    for c in range(3072 // WCW):
        i = c % 2
        c0 = off + c * WCW
        wch = WCH[c % len(WCH)]
        kb.dma("sp", lambda q: q.dma_start(out=wch[:, :, 0:WCW], in_=w_ada[l, c0 // WCW].rearrange("p (k j) -> p k j", k=8)), wch.b, writes=[wch.b])
        kb.dma("sp", lambda q: q.dma_start(out=badac[i][:P, 0:WCW], in_=b_ada[l, :P, c0:c0 + WCW]), badac[i].b, writes=[badac[i].b])
        wsrc = WCR[c % 2] if r32 else wch
        if r32:
            op("act", lambda e: e.copy(out=wsrc[:, :, 0:WCW], in_=wch[:, :, 0:WCW]), reads=[wch.b], writes=[wsrc.b])
        for k in range(8):
            if r32:
                op("pe", lambda e: e.matmul(PM[i][:, 0:WCW], lhsT=hT[:, k, :], rhs=wsrc[:, k, 0:WCW], start=(k == 0), stop=(k == 7)),
                   reads=[hT.b, wsrc.b], writes=[PM[i].b] if k in (0, 7) else [])
            else:
                op("pe", lambda e: e.matmul(PM[i][:P, 0:WCW], lhsT=hT[:, k, :P], rhs=wch[:, k, 0:WCW], start=(k == 0), stop=(k == 7)),
                   reads=[hT.b, wch.b], writes=[PM[i].b] if k in (0, 7) else [])
        op("dve", lambda e: e.tensor_tensor(out=ADA[:P, c * WCW:(c + 1) * WCW], in0=PM[i][:P, 0:WCW], in1=badac[i][:P, 0:WCW], op=ALU.add),
           reads=[PM[i].b, badac[i].b], writes=[ADA.b])
    op("dve", lambda e: e.tensor_scalar_add(out=ADA[:P, 1024:2048], in0=ADA[:P, 1024:2048], scalar1=1.0), reads=[ADA.b], writes=[ADA.b])


def _transpose8(kb, E, src, dstT, PT, P, srcb=None, op=None):
    op = kb.op if op is None else op
    C = E["C"]
    sb_ = src.b if srcb is None else srcb
    for half in range(2):
        for j in range(4):
            k = half * 4 + j
            op("pe", lambda e, half=half, j=j, k=k: e.transpose(
                out=PT[half][:, j * 128:j * 128 + P], in_=src[:P, k * 128:(k + 1) * 128], identity=C("ident", P, P)),
               reads=[sb_, E["CST"].b], writes=[PT[half].b])
        op("act", lambda e, half=half: e.copy(
            out=dstT[:, half * 4:half * 4 + 4, :P],
            in_=PT[half][:, :].rearrange("p (j c) -> p j c", j=4)[:, :, :P]),
           reads=[PT[half].b], writes=[dstT.b])


def _phase1(nc, kb, l, E, part):
    isS = (part == "s")
    tiles = [NT] if isS else list(range(NT))
    cur = [None]
    cnt = [0]
    convq = [None]

    def run_items(items, n=None):
        n = len(items) if n is None else min(n, len(items))
        for _ in range(n):
            kind, e, fn, owner, reads, writes = items.pop(0)
            if kind == "op":
                kb.op(e, fn, reads=reads, writes=writes, bound=True)
            else:
                kb.dma(e, fn, owner, reads=reads, writes=writes, bound=True)

    def op(e, fn, reads=(), writes=()):
        tok = kb.op(e, fn, reads=reads, writes=writes)
        if cur[0]:
            run_items(cur[0], 1)
        cnt[0] += 1
        if convq[0] and cnt[0] % 16 == 0:
            run_items(convq[0], 1)
        return tok
    C, Cg, CST, X, XB = E["C"], E["Cg"], E["CST"], E["X"], E["XB"]
    EPSB = E["EPSB"]
    out_dma = E["out_dma"]
    w_in, w_o = E["w_in"], E["w_o"]

    kb.cst1 = kb.sb("CST1", [128, E["NCST1"]])
    kb.dma("sp", lambda q: q.dma_start(out=kb.cst1[:, :], in_=E["cst1_d"]), CST.b, writes=[CST.b])
    ADA = Tn(kb, "ADA1", [128, 3072]); ADAs = ADA
    WCH = [Tn(kb, "WCH%d" % i, [128, 8, WCW], dma=True) for i in range(2)]
    WCR = [Tn(kb, "WCR%d" % i, [128, 8, WCW], F32R) for i in range(2)]
    badac = [Tn(kb, "bada%d" % i, [128, 256], dma=True) for i in range(2)]
    nbuf = 1 if isS else 2
    Hs = [Tn(kb, "H%d" % i, [128, D], dma=True) for i in range(nbuf)]
    HTs = [Tn(kb, "HT%d" % i, [128, 8, 128], F32R) for i in range(nbuf)]
    PROJs = [Tn(kb, "PROJ%d" % i, [128, IN_COLS], dma=True) for i in range(nbuf)]
    H, HT, PROJ = Hs[0], HTs[0], PROJs[0]
    Y = Tn(kb, "Y", [128, D])
    PRM = Tn(kb, "PRM", [128, 8 + 512 + 256 * 3 + 2 * D], dma=True)
    WS = BS = WSs = BSs = None
    if isS:
        WSs = Tn(kb, "WSs", [128, 4, SP], dma=True); BSs = Tn(kb, "BSs", [128, 4], dma=True)
    else:
        WS = Tn(kb, "WS", [128, 4, 128], dma=True); BS = Tn(kb, "BS", [128, 4], dma=True)
    WP = Tn(kb, "WP", [64, 4, 64], dma=True)
    PT = [Tn(kb, "PT%d" % i, [128, 512], psum=True) for i in range(2)]
    PM = [Tn(kb, "PM%d" % i, [128, 512], psum=True) for i in range(2)]
    PA = Tn(kb, "PA", [128, 512], psum=True); PB = Tn(kb, "PB", [128, 512], psum=True)
    PC = Tn(kb, "PC", [128, 512], psum=True); PD = Tn(kb, "PD", [128, 512], psum=True)
    SM = Tn(kb, "SM", [128, 64])
    SMs = MREP = CTX = None
    if isS:
        SMs = Tn(kb, "SMs", [128, 4], dma=True)
    else:
        MREP = Tn(kb, "MREP", [128, 4])
        CTX = Tn(kb, "CTX", [128, 4, 129], dma=True)
    DG = Tn(kb, "DG", [128, 128]); DL = Tn(kb, "DL", [128, 128]); WI = Tn(kb, "WI", [128, 128])
    AM = Tn(kb, "AM", [128, 128]); AT = Tn(kb, "AT", [128, 128])
    QT = Tn(kb, "QT", [128, 128]); KT = Tn(kb, "KT", [128, 128])
    VX = Tn(kb, "VX", [128, 129]); TOT = Tn(kb, "TOT", [128, 129]); WV = Tn(kb, "WV", [128, 129])
    HN = Tn(kb, "HN", [128, 128]); SG = Tn(kb, "SG", [128, 128]); ST6 = Tn(kb, "ST6", [128, 2, 6])
    OUTC = None if isS else Tn(kb, "OUTC", [128, 128], dma=True)
    CN = CTS = RA = ZQ = NNAT = NTH = WCB = DECD = DECR = MSO = SPA = SPB = PREV = None
    if isS:
        CN = Tn(kb, "CN", [128, SB, 128], dma=True); CTS = Tn(kb, "CTS", [128, SB, 129])
        RA = Tn(kb, "RA", [128, SB, 128])

    class _V2:
        def __init__(self, ap, b):
            self.t = ap
            self.b = b

        def __getitem__(self, k):
            return self.t[k]
    if isS:
        ZQ = _V2(RA[:, :, :].rearrange("p a b -> p (a b)")[:, 0:SB * SP], RA.b)
        NNAT = Tn(kb, "NNAT", [SB, 4, 128], dma=True); NTH = Tn(kb, "NTH", [128, SB], dma=True)
        WCB = Tn(kb, "WCB", [128, 16]); DECD = Tn(kb, "DECD", [128, 16]); DECR = Tn(kb, "DECR", [128, 16])
        MSO = Tn(kb, "MSO", [SB, 4], dma=True)
        SPA = Tn(kb, "SPA", [128, 256], dma=True); SPB = Tn(kb, "SPB", [128, 256], dma=True)
    else:
        PREV = Tn(kb, "PREV", [128, 256])
    PTT = Tn(kb, "PTT", [64, 4, 128])
    VN = Tn(kb, "VN", [128, 256], dma=True); VTMP = Tn(kb, "VTMP", [128, 256])

    o_bg, o_mh, o_sg, o_sb, o_ps, o_l1g, o_l1b = 0, 8, 520, 776, 1032, 1288, 1288 + D
    for (o, w, src) in [(o_bg, 8, E["b_gate"]), (o_mh, 512, E["mh_g"]), (o_sg, 256, E["sgu_g"]), (o_sb, 256, E["sgu_b"]),
                        (o_ps, 256, E["pscale"]), (o_l1g, D, E["ln1g"]), (o_l1b, D, E["ln1b"])]:
        kb.dma("sp", lambda q, o=o, w=w, src=src: q.dma_start(out=PRM[:, o:o + w], in_=src[l]), PRM.b, writes=[PRM.b])
    kb.dma("sp", lambda q: q.dma_start(out=WP[:, :, :], in_=E["w_pool"][l]), WP.b, writes=[WP.b])
    if isS:
        kb.dma("sp", lambda q: q.dma_start(out=WSs[:SP, :, :], in_=E["w_sS"][l]), WSs.b, writes=[WSs.b])
        kb.dma("sp", lambda q: q.dma_start(out=BSs[:SP, :], in_=E["b_sS"][l]), BSs.b, writes=[BSs.b])
    else:
        kb.dma("sp", lambda q: q.dma_start(out=WS[:, :, :], in_=E["w_sT"][l]), WS.b, writes=[WS.b])
        kb.dma("sp", lambda q: q.dma_start(out=BS[:, :], in_=E["b_sT"][l]), BS.b, writes=[BS.b])
    for g in range(4):
        if isS:
            op("dve", lambda e, g=g: e.tensor_tensor(out=WSs[:SP, g, :], in0=WSs[:SP, g, :], in1=C("tri_s", SP, SP), op=ALU.mult),
               reads=[WSs.b, CST.b], writes=[WSs.b])
        else:
            op("dve", lambda e, g=g: e.tensor_tensor(out=WS[:, g, :], in0=WS[:, g, :], in1=C("triu"), op=ALU.mult),
               reads=[WS.b, CST.b], writes=[WS.b])
    if not isS:
        op("dve", lambda e: e.memset(CTX[:, :, :], 0.0), writes=[CTX.b])
        op("dve", lambda e: e.memset(MREP[:, :], 0.0), writes=[MREP.b])
    op("dve", lambda e: e.memset(VX[:, :], 1.0), writes=[VX.b])

    if isS:
        _ada(nc, kb, l, E, ADA, SP, E["cs"], 0, WCH, PM, H, HT, PT, badac, WCR)
    else:
        _ada(nc, kb, l, E, ADA, 128, E["cp"], 0, WCH, PM, H, HT, PT, badac, WCR)

    w_in_v = w_in[l]
    w_o_v = w_o[l]
    def mk_chunks(n):
        return [(c0, min(WCW, n - c0)) for c0 in range(0, n, WCW)]
    chunks = mk_chunks(IN_COLS)
    wctr = [0]

    def stream_mm(wview, c0, w, lhsT, P, evac, op=op, dma=kb.dma):
        i = wctr[0] % 2
        wctr[0] += 1
        dma("sp", lambda q: q.dma_start(out=WCH[i][:, :, :], in_=wview[c0 // WCW].rearrange("p (k j) -> p k j", k=8)), WCH[i].b, writes=[WCH[i].b])
        wr = WCR[i]
        if wctr[0] % 3 == 0:
            op("dve", lambda e: e.tensor_copy(out=wr[:, :, 0:w], in_=WCH[i][:, :, 0:w]), reads=[WCH[i].b], writes=[wr.b])
        else:
            op("act", lambda e: e.copy(out=wr[:, :, 0:w], in_=WCH[i][:, :, 0:w]), reads=[WCH[i].b], writes=[wr.b])
        for k in range(8):
            op("pe", lambda e, k=k: e.matmul(PM[i][:, 0:w], lhsT=lhsT[:, k, :], rhs=wr[:, k, 0:w],
                                              start=(k == 0), stop=(k == 7)),
               reads=[lhsT.b, wr.b], writes=[PM[i].b] if k in (0, 7) else [])
        evac(PM[i], i)

    def make_A(t):
        items = []

        def iop(e, fn, reads=(), writes=()):
            items.append(("op", e, _bind(fn), None, tuple(reads), tuple(writes)))

        def idma(e, fn, owner, reads=(), writes=()):
            items.append(("dma", e, _bind(fn), owner, tuple(reads), tuple(writes)))
        P = SP if isS else 128
        H, HT, PROJ = Hs[t % nbuf], HTs[t % nbuf], PROJs[t % nbuf]
        xt = X[:P, t, :]
        iop("dve", lambda e: e.tensor_tensor(out=H[:P, :], in0=xt, in1=ADA[:P, 1024:2048], op=ALU.mult), reads=[XB[t], ADA.b], writes=[H.b])
        iop("dve", lambda e: e.tensor_tensor(out=H[:P, :], in0=H[:P, :], in1=ADA[:P, 0:1024], op=ALU.add), reads=[H.b, ADA.b], writes=[H.b])
        _transpose8(kb, E, H, HT, PT, P, op=iop)
        for (c0, w) in chunks:
            stream_mm(w_in_v, c0, w, HT, P,
                      lambda pm, i, c0=c0, w=w: iop("act", lambda e: e.copy(out=PROJ[:P, c0:c0 + w], in_=pm[:P, 0:w]),
                                                    reads=[pm.b], writes=[PROJ.b]), op=iop, dma=idma)
        return items

    conv = []
    if not isS:
        CBs = [Tn(kb, "CB%d" % i, [128, 2 * D], BF16, dma=True) for i in range(2)]
        tab32 = E["puv"][l]
        tabb = E["puvb"][l]
        PUVB = E["PUVBT"][l]
        for blk in range(NEXP // 128):
            cb = CBs[blk % 2]
            conv.append(("dma", "pool", _bind(lambda q: q.dma_start(out=cb[:, :], in_=tab32[blk * 128:(blk + 1) * 128, :])), cb.b, (), (cb.b,)))
            conv.append(("dma", "sp", _bind(lambda q: q.dma_start(out=tabb[blk * 128:(blk + 1) * 128, :], in_=cb[:, :])), PUVB, (cb.b,), (PUVB,)))
    convq[0] = conv
    run_items(make_A(tiles[0]))
    for ti, t in enumerate(tiles):
        is_s = isS
        P = SP if is_s else 128
        ada = ADA
        H, HT, PROJ = Hs[t % nbuf], HTs[t % nbuf], PROJs[t % nbuf]
        xt = X[:P, t, :]
        nxtA = make_A(tiles[ti + 1]) if ti + 1 < len(tiles) else None
        cur[0] = nxtA
        tri = C("tri_s", SP, SP) if is_s else C("triu")
        negm = C("negm_s", SP, SP) if is_s else C("negm")
        selE = C("selend", SP, SP) if is_s else C("sel127")
        if is_s:
            kb.dma("sp", lambda q: q.dma_start(out=SMs[:SP, :], in_=E["sm"][l]), SMs.b, writes=[SMs.b])
            kb.dma("sp", lambda q: q.dma_start(out=NNAT[:, :, :], in_=E["snat"][l]), NNAT.b, writes=[NNAT.b])
        mtok = SMs if is_s else MREP
        op("dve", lambda e: e.tensor_tensor(out=SM[:P, 0:8], in0=PROJ[:P, 2048:2056], in1=PRM[:P, o_bg:o_bg + 8], op=ALU.add),
           reads=[PROJ.b, PRM.b], writes=[SM.b])
        op("dve", lambda e: e.scalar_tensor_tensor(out=SM[:P, 8:12], in0=SM[:P, 4:8], scalar=-1.0, in1=SM[:P, 4:8], op0=ALU.mult, op1=ALU.max),
           reads=[SM.b], writes=[SM.b])
        op("act", lambda e: e.activation(out=SM[:P, 12:16], in_=SM[:P, 8:12], func=AF.Exp, scale=-1.0), reads=[SM.b], writes=[SM.b])
        op("act", lambda e: e.activation(out=SM[:P, 12:16], in_=SM[:P, 12:16], func=AF.Ln, bias=1.0, scale=1.0),
           reads=[SM.b], writes=[SM.b])
        op("dve", lambda e: e.tensor_scalar_min(out=SM[:P, 16:20], in0=SM[:P, 4:8], scalar1=0.0), reads=[SM.b], writes=[SM.b])
        op("dve", lambda e: e.tensor_tensor(out=SM[:P, 16:20], in0=SM[:P, 16:20], in1=SM[:P, 12:16], op=ALU.subtract),
           reads=[SM.b], writes=[SM.b])
        op("pe", lambda e: e.matmul(PA[:P, 0:4], lhsT=tri, rhs=SM[:P, 16:20], start=True, stop=True),
           reads=[CST.b, SM.b], writes=[PA.b])
        op("act", lambda e: e.copy(out=SM[:P, 20:24], in_=PA[:P, 0:4]), reads=[PA.b], writes=[SM.b])
        op("dve", lambda e: e.tensor_tensor(out=SM[:P, 24:28], in0=SM[:P, 0:4], in1=SM[:P, 20:24], op=ALU.subtract),
           reads=[SM.b], writes=[SM.b])
        op("pe", lambda e: e.matmul(PA[:P, 8:12], lhsT=selE, rhs=SM[:P, 20:24], start=True, stop=True),
           reads=[CST.b, SM.b], writes=[PA.b])
        op("act", lambda e: e.copy(out=SM[:P, 28:32], in_=PA[:P, 8:12]), reads=[PA.b], writes=[SM.b])

        for hh in range(4):
            qs = PROJ[:P, hh * 128:(hh + 1) * 128]
            ks = PROJ[:P, 512 + hh * 128:512 + (hh + 1) * 128]
            vs = PROJ[:P, 1024 + hh * 128:1024 + (hh + 1) * 128]
            os_ = PROJ[:P, 1536 + hh * 128:1536 + (hh + 1) * 128]
            col = lambda c, hh=hh: SM[:P, c + hh:c + hh + 1]
            S1 = lambda c: SM[:P, c:c + 1]
            if is_s:
                kb.dma("sp", lambda q, hh=hh: q.dma_start(out=CN[:, :, :], in_=E["sC"][l, hh]), CN.b, writes=[CN.b])
                kb.dma("sp", lambda q, hh=hh: q.dma_start(out=NTH[:, :], in_=E["snT"][l, hh]), NTH.b, writes=[NTH.b])
                for j in range(4):
                    pt = PT[j % 2]
                    for jj in range(4):
                        b = j * 4 + jj
                        op("pe", lambda e, b=b, jj=jj, pt=pt: e.transpose(out=pt[:, jj * 128:(jj + 1) * 128], in_=CN[:, b, :],
                                                                       identity=C("ident")),
                           reads=[CN.b, CST.b], writes=[pt.b])
                    op("act", lambda e, j=j, pt=pt: e.copy(out=CTS[:, j * 4:(j + 1) * 4, 0:128],
                                                           in_=pt[:, :].rearrange("p (j c) -> p j c", j=4)),
                       reads=[pt.b], writes=[CTS.b])
                op("dve", lambda e: e.tensor_copy(out=CTS[:, :, 128:129], in_=NTH[:, :].unsqueeze(2)), reads=[NTH.b], writes=[CTS.b])
            op("dve", lambda e, hh=hh: e.tensor_scalar(out=DG[:P, :P], in0=C("ident", P, P), scalar1=col(24), scalar2=None,
                                                       op0=ALU.mult), reads=[SM.b, CST.b], writes=[DG.b])
            op("pe", lambda e: e.matmul(PB[:P, 0:P], lhsT=C("ones", P, P), rhs=DG[:P, :P], start=True, stop=True),
               reads=[DG.b, CST.b], writes=[PB.b])
            if is_s:
                op("dve", lambda e: e.tensor_tensor(out=DL[:P, :P], in0=PB[:P, 0:P], in1=C("negb_s", SP, SP), op=ALU.add),
                   reads=[PB.b, CST.b], writes=[DL.b])
                op("dve", lambda e: e.tensor_reduce(out=S1(32), in_=DL[:P, :P], axis=AX.X, op=ALU.max), reads=[DL.b], writes=[SM.b])
            else:
                op("dve", lambda e: e.tensor_reduce(out=S1(32), in_=PB[:P, 0:P], axis=AX.X, op=ALU.max), reads=[PB.b], writes=[SM.b])
            op("dve", lambda e, hh=hh: e.scalar_tensor_tensor(out=DL[:P, :P], in0=PB[:P, 0:P], scalar=col(20), in1=negm,
                                                              op0=ALU.add, op1=ALU.add),
               reads=[PB.b, SM.b, CST.b], writes=[DL.b])
            op("dve", lambda e: e.tensor_reduce(out=S1(33), in_=DL[:P, :P], axis=AX.X, op=ALU.max), reads=[DL.b], writes=[SM.b])
            op("dve", lambda e, hh=hh: e.tensor_tensor(out=S1(34), in0=col(20), in1=mtok[:P, hh:hh + 1], op=ALU.add),
               reads=[SM.b, mtok.b], writes=[SM.b])
            op("dve", lambda e: e.tensor_tensor(out=S1(35), in0=S1(34), in1=S1(33), op=ALU.max), reads=[SM.b], writes=[SM.b])
            op("dve", lambda e: e.tensor_scalar(out=S1(36), in0=S1(35), scalar1=-1.0, scalar2=None, op0=ALU.mult),
               reads=[SM.b], writes=[SM.b])
            op("act", lambda e: e.activation(out=WI[:P, :P], in_=DL[:P, :P], func=AF.Exp, bias=S1(36), scale=1.0),
               reads=[DL.b, SM.b], writes=[WI.b])
            op("act", lambda e: e.activation(out=S1(37), in_=S1(34), func=AF.Exp, bias=S1(36), scale=1.0), reads=[SM.b], writes=[SM.b])
            op("act", lambda e: e.activation(out=S1(38), in_=S1(36), func=AF.Exp), reads=[SM.b], writes=[SM.b])
            op("pe", lambda e: e.transpose(out=PC[:, 0:P], in_=qs, identity=C("ident", P, P)), reads=[PROJ.b, CST.b], writes=[PC.b])
            op("pe", lambda e: e.transpose(out=PC[:, 128:128 + P], in_=ks, identity=C("ident", P, P)), reads=[PROJ.b, CST.b], writes=[PC.b])
            op("act", lambda e: e.mul(out=QT[:, :P], in_=PC[:, 0:P], mul=128.0 ** -0.5), reads=[PC.b], writes=[QT.b])
            op("act", lambda e: e.copy(out=KT[:, :P], in_=PC[:, 128:128 + P]), reads=[PC.b], writes=[KT.b])
            op("pe", lambda e: e.matmul(PD[:P, 0:P], lhsT=QT[:, :P], rhs=KT[:, :P], start=True, stop=True),
               reads=[QT.b, KT.b], writes=[PD.b])
            op("dve", lambda e: e.tensor_tensor(out=AM[:P, :P], in0=WI[:P, :P], in1=PD[:P, 0:P], op=ALU.mult),
               reads=[WI.b, PD.b], writes=[AM.b])
            op("pe", lambda e: e.transpose(out=PB[:P, 128:128 + P], in_=AM[:P, :P], identity=C("ident", P, P)),
               reads=[AM.b, CST.b], writes=[PB.b])
            op("act", lambda e: e.copy(out=AT[:P, :P], in_=PB[:P, 128:128 + P]), reads=[PB.b], writes=[AT.b])
            op("pool", lambda e: e.tensor_copy(out=VX[:P, 0:128], in_=vs), reads=[PROJ.b], writes=[VX.b])
            op("pe", lambda e: e.matmul(PD[:P, 128:257], lhsT=AT[:P, :P], rhs=VX[:P, :], start=True, stop=True),
               reads=[AT.b, VX.b], writes=[PD.b])
            if is_s:
                op("pool", lambda e: e.memset(ZQ[:, :], 0.0), writes=[ZQ.b])
                for b in range(SB):
                    op("pool", lambda e, b=b: e.tensor_copy(out=ZQ[:, b * SP + b:(b + 1) * SP:SB], in_=QT[:, b:SP:SB]),
                       reads=[QT.b], writes=[ZQ.b])
                for b in range(SB):
                    op("pe", lambda e, b=b: e.matmul(PC[:P, 256:385], lhsT=ZQ[:, b * SP:(b + 1) * SP], rhs=CTS[:, b, :],
                                                     start=(b == 0), stop=(b == SB - 1)),
                       reads=[ZQ.b, CTS.b], writes=[PC.b] if b in (0, SB - 1) else [])
            else:
                op("pe", lambda e, hh=hh: e.matmul(PC[:P, 256:385], lhsT=QT[:, :P], rhs=CTX[:, hh, :], start=True, stop=True),
                   reads=[QT.b, CTX.b], writes=[PC.b])
            op("act", lambda e: e.activation(out=TOT[:P, :], in_=PC[:P, 256:385], func=AF.Identity, scale=S1(37)),
               reads=[PC.b, SM.b], writes=[TOT.b])
            op("dve", lambda e: e.tensor_tensor(out=TOT[:P, :], in0=TOT[:P, :], in1=PD[:P, 128:257], op=ALU.add),
               reads=[TOT.b, PD.b], writes=[TOT.b])
            op("dve", lambda e: e.scalar_tensor_tensor(out=S1(39), in0=TOT[:P, 128:129], scalar=-1.0, in1=TOT[:P, 128:129], op0=ALU.mult, op1=ALU.max),
               reads=[TOT.b], writes=[SM.b])
            op("dve", lambda e: e.tensor_tensor(out=S1(39), in0=S1(39), in1=S1(38), op=ALU.max), reads=[SM.b], writes=[SM.b])
            op("dve", lambda e: e.reciprocal(out=S1(40), in_=S1(39)), reads=[SM.b], writes=[SM.b])
            op("dve", lambda e: e.tensor_scalar(out=HN[:P, :], in0=TOT[:P, 0:128], scalar1=S1(40), scalar2=None, op0=ALU.mult),
               reads=[TOT.b, SM.b], writes=[HN.b])
            op("dve", lambda e: e.bn_stats(out=ST6[:P, 0, :], in_=HN[:P, :]), reads=[HN.b], writes=[ST6.b])
            op("dve", lambda e: e.bn_aggr(out=SM[:P, 41:43], in_=ST6[:P, 0, :]), reads=[ST6.b], writes=[SM.b])
            op("act", lambda e: e.activation(out=S1(43), in_=S1(42), func=AF.Ln, bias=EPSB[:P, 0:1], scale=1.0), reads=[SM.b, EPSB.b], writes=[SM.b])
            op("act", lambda e: e.activation(out=S1(44), in_=S1(43), func=AF.Exp, scale=-0.5), reads=[SM.b], writes=[SM.b])
            op("dve", lambda e: e.tensor_scalar(out=HN[:P, :], in0=HN[:P, :], scalar1=S1(41), scalar2=S1(44),
                                                op0=ALU.subtract, op1=ALU.mult), reads=[HN.b, SM.b], writes=[HN.b])
            op("dve", lambda e, hh=hh: e.tensor_tensor(out=HN[:P, :], in0=HN[:P, :],
                                                       in1=PRM[:P, o_mh + hh * 128:o_mh + (hh + 1) * 128], op=ALU.mult),
               reads=[HN.b, PRM.b], writes=[HN.b])
            op("act", lambda e: e.activation(out=SG[:P, :], in_=os_, func=AF.Exp, scale=-1.0), reads=[PROJ.b], writes=[SG.b])
            op("dve", lambda e: e.tensor_scalar_add(out=SG[:P, :], in0=SG[:P, :], scalar1=1.0), reads=[SG.b], writes=[SG.b])
            op("dve", lambda e: e.reciprocal(out=SG[:P, :], in_=SG[:P, :]), reads=[SG.b], writes=[SG.b])
            op("dve", lambda e, hh=hh: e.tensor_tensor(out=Y[:P, hh * 128:(hh + 1) * 128], in0=HN[:P, :], in1=SG[:P, :], op=ALU.mult),
               reads=[HN.b, SG.b], writes=[Y.b])
            op("dve", lambda e, hh=hh: e.tensor_tensor(out=S1(45), in0=mtok[:P, hh:hh + 1], in1=S1(32), op=ALU.max),
               reads=[SM.b, mtok.b], writes=[SM.b])
            op("dve", lambda e, hh=hh: e.tensor_tensor(out=S1(45), in0=S1(45), in1=col(28), op=ALU.add), reads=[SM.b], writes=[SM.b])
            op("dve", lambda e, hh=hh: e.tensor_tensor(out=S1(46), in0=col(28), in1=S1(45), op=ALU.subtract), reads=[SM.b], writes=[SM.b])
            op("act", lambda e, hh=hh: e.activation(out=S1(47), in_=col(24), func=AF.Exp, bias=S1(46), scale=1.0),
               reads=[SM.b], writes=[SM.b])
            op("act", lambda e, hh=hh: e.activation(out=S1(48), in_=mtok[:P, hh:hh + 1], func=AF.Exp, bias=S1(46), scale=1.0),
               reads=[SM.b, mtok.b], writes=[SM.b])
            if not is_s:
                op("dve", lambda e: e.tensor_scalar(out=WV[:P, :], in0=VX[:P, :], scalar1=S1(47), scalar2=None, op0=ALU.mult),
                   reads=[VX.b, SM.b], writes=[WV.b])
                op("pe", lambda e: e.matmul(PB[:, 256:385], lhsT=ks, rhs=WV[:P, :], start=True, stop=True),
                   reads=[PROJ.b, WV.b], writes=[PB.b])
                op("dve", lambda e, hh=hh: e.scalar_tensor_tensor(out=CTX[:, hh, :], in0=CTX[:, hh, :], scalar=S1(48), in1=PB[:, 256:385],
                                                                  op0=ALU.mult, op1=ALU.add),
                   reads=[CTX.b, SM.b, PB.b], writes=[CTX.b])
                op("dve", lambda e, hh=hh: e.tensor_copy(out=MREP[:, hh:hh + 1], in_=S1(45)), reads=[SM.b], writes=[MREP.b])
                if t == NT - 1:
                    op("pe", lambda e, hh=hh: e.transpose(out=PA[:, 128:256], in_=CTX[:, hh, 0:128], identity=C("ident")),
                       reads=[CTX.b, CST.b], writes=[PA.b])
                    op("act", lambda e: e.copy(out=OUTC[:, :], in_=PA[:, 128:256]), reads=[PA.b], writes=[OUTC.b])
                    out_dma(E["o_pC"][l, hh], OUTC[:, :], [OUTC.b])
                    out_dma(E["o_pn"][l, hh].rearrange("(k o) -> k o", o=1), CTX[:, hh, 128:129], [CTX.b])
                    if hh == 3:
                        out_dma(E["o_pm"][l:l + 1, :], MREP[0:1, :], [MREP.b])
            else:
                op("dve", lambda e: e.tensor_scalar(out=WCB[:P, :], in0=C("onehotB", SP), scalar1=S1(47), scalar2=None, op0=ALU.mult),
                   reads=[SM.b, CST.b], writes=[WCB.b])
                op("dve", lambda e: e.tensor_tensor(out=RA[:P, :, :], in0=vs.unsqueeze(1).broadcast_to([P, SB, 128]),
                                                    in1=WCB[:P, :].unsqueeze(2).broadcast_to([P, SB, 128]), op=ALU.mult),
                   reads=[PROJ.b, WCB.b], writes=[RA.b])
                op("dve", lambda e: e.tensor_scalar(out=DECD[:P, :], in0=C("onehot0", SP), scalar1=S1(48), scalar2=None, op0=ALU.mult),
                   reads=[SM.b, CST.b], writes=[DECD.b])
                op("pe", lambda e: e.matmul(PA[:, 16:32], lhsT=C("ones", SP, 128), rhs=DECD[:P, :], start=True, stop=True),
                   reads=[DECD.b, CST.b], writes=[PA.b])
                op("act", lambda e: e.copy(out=DECR[:, :], in_=PA[:, 16:32]), reads=[PA.b], writes=[DECR.b])
                for b in range(SB):
                    pq = [PA, PB, PC, PD][b % 4]
                    op("pe", lambda e, b=b, pq=pq: e.matmul(pq[:, 384:512], lhsT=RA[:P, b, :], rhs=ks, start=True, stop=True),
                       reads=[RA.b, PROJ.b], writes=[pq.b])
                    op("dve", lambda e, b=b, pq=pq: e.scalar_tensor_tensor(out=CN[:, b, :], in0=CN[:, b, :], scalar=DECR[:, b:b + 1],
                                                                           in1=pq[:, 384:512], op0=ALU.mult, op1=ALU.add),
                       reads=[CN.b, DECR.b, pq.b], writes=[CN.b])
                out_dma(E["o_nC"][l, :, hh].rearrange("b v k -> v b k"), CN[:, :, :], [CN.b])
                op("pe", lambda e: e.matmul(PA[:SB, 32:160], lhsT=WCB[:P, :], rhs=ks, start=True, stop=True),
                   reads=[WCB.b, PROJ.b], writes=[PA.b])
                op("dve", lambda e, hh=hh: e.scalar_tensor_tensor(out=NNAT[:, hh, :], in0=NNAT[:, hh, :], scalar=SM[:SB, 48:49],
                                                                  in1=PA[:SB, 32:160], op0=ALU.mult, op1=ALU.add),
                   reads=[NNAT.b, SM.b, PA.b], writes=[NNAT.b])
                op("dve", lambda e, hh=hh: e.tensor_copy(out=MSO[:, hh:hh + 1], in_=SM[:SB, 45:46]), reads=[SM.b], writes=[MSO.b])
                if hh == 3:
                    out_dma(E["o_nn"][l], NNAT[:, :, :], [NNAT.b])
                    out_dma(E["o_nm"][l], MSO[:, :], [MSO.b])

        vsv = PROJ[:P, 2312:2568].rearrange("p (g d) -> p g d", g=4)
        op("dve", lambda e: e.tensor_reduce(out=SM[:P, 50:54], in_=vsv, axis=AX.X, op=ALU.add), reads=[PROJ.b], writes=[SM.b])
        op("dve", lambda e: e.tensor_scalar(out=SM[:P, 50:54], in0=SM[:P, 50:54], scalar1=1.0 / 64, scalar2=None, op0=ALU.mult),
           reads=[SM.b], writes=[SM.b])
        op("dve", lambda e: e.tensor_tensor(out=VN[:P, :].rearrange("p (g d) -> p g d", g=4), in0=vsv,
                                            in1=SM[:P, 50:54].unsqueeze(2).broadcast_to([P, 4, 64]), op=ALU.subtract),
           reads=[PROJ.b, SM.b], writes=[VN.b])
        op("pool", lambda e: e.tensor_tensor(out=VTMP[:P, :], in0=VN[:P, :], in1=VN[:P, :], op=ALU.mult), reads=[VN.b], writes=[VTMP.b])
        op("dve", lambda e: e.tensor_reduce(out=SM[:P, 54:58], in_=VTMP[:P, :].rearrange("p (g d) -> p g d", g=4), axis=AX.X, op=ALU.add),
           reads=[VTMP.b], writes=[SM.b])
        op("act", lambda e: e.activation(out=SM[:P, 54:58], in_=SM[:P, 54:58], func=AF.Ln, bias=EPSB[:P, 0:1], scale=1.0 / 64),
           reads=[SM.b, EPSB.b], writes=[SM.b])
        op("act", lambda e: e.activation(out=SM[:P, 58:62], in_=SM[:P, 54:58], func=AF.Exp, scale=-0.5), reads=[SM.b], writes=[SM.b])
        op("dve", lambda e: e.tensor_tensor(out=VN[:P, :].rearrange("p (g d) -> p g d", g=4), in0=VN[:P, :].rearrange("p (g d) -> p g d", g=4),
                                            in1=SM[:P, 58:62].unsqueeze(2).broadcast_to([P, 4, 64]), op=ALU.mult),
           reads=[VN.b, SM.b], writes=[VN.b])
        op("pool", lambda e: e.tensor_tensor(out=VN[:P, :], in0=VN[:P, :], in1=PRM[:P, o_sg:o_sg + 256], op=ALU.mult),
           reads=[VN.b, PRM.b], writes=[VN.b])
        op("pool", lambda e: e.tensor_tensor(out=VN[:P, :], in0=VN[:P, :], in1=PRM[:P, o_sb:o_sb + 256], op=ALU.add),
           reads=[VN.b, PRM.b], writes=[VN.b])
        wsl = WSs if is_s else WS
        bsl = BSs if is_s else BS
        for g in range(4):
            op("pe", lambda e, g=g: e.matmul(PC[:P, g * 64:(g + 1) * 64], lhsT=wsl[:P, g, :P], rhs=VN[:P, g * 64:(g + 1) * 64],
                                             start=True, stop=True), reads=[wsl.b, VN.b], writes=[PC.b])
        for g in range(4):
            op("dve", lambda e, g=g: e.scalar_tensor_tensor(out=Y[:P, 512 + g * 64:512 + (g + 1) * 64], in0=PC[:P, g * 64:(g + 1) * 64],
                                                            scalar=bsl[:P, g:g + 1], in1=PROJ[:P, 2056 + g * 64:2056 + (g + 1) * 64],
                                                            op0=ALU.add, op1=ALU.mult),
               reads=[PC.b, bsl.b, PROJ.b], writes=[Y.b])
        if is_s:
            for tq in range(ST):
                out_dma(E["o_nv"][l][:, tq, :], VN[tq * SB:(tq + 1) * SB, :], [VN.b])

        pin = lambda g: PROJ[:P, 2568 + g * 64:2568 + (g + 1) * 64]
        if is_s:
            kb.dma("sp", lambda q: q.dma_start(out=SPA[:, :], in_=E["spA"][l]), SPA.b, writes=[SPA.b])
            kb.dma("sp", lambda q: q.dma_start(out=SPB[:112, :], in_=E["spB"][l]), SPB.b, writes=[SPB.b])
            for g in range(4):
                op("pe", lambda e, g=g: e.matmul(PA[:64, g * 128:g * 128 + P], lhsT=SPA[:, g * 64:(g + 1) * 64], rhs=Cg("bsA", g, 128, SP, SP),
                                                 start=True, stop=False), reads=[SPA.b, CST.b], writes=[PA.b])
                op("pe", lambda e, g=g: e.matmul(PA[:64, g * 128:g * 128 + P], lhsT=SPB[:112, g * 64:(g + 1) * 64], rhs=Cg("bsB", g, 112, SP, SP),
                                                 start=False, stop=False), reads=[SPB.b, CST.b], writes=[])
                op("pe", lambda e, g=g: e.matmul(PA[:64, g * 128:g * 128 + P], lhsT=pin(g), rhs=Cg("bsC", g, SP, SP, SP),
                                                 start=False, stop=True), reads=[PROJ.b, CST.b], writes=[PA.b])
            npv = E["o_np"][l].rearrange("b r c -> r b c")
            for r in range(4):
                out_dma(npv[r], SPA[64 + r * SB:64 + (r + 1) * SB, :], [SPA.b])
            for r in range(7):
                out_dma(npv[4 + r], SPB[r * SB:(r + 1) * SB, :], [SPB.b])
            for r in range(4):
                out_dma(npv[11 + r], PROJ[r * SB:(r + 1) * SB, 2568:2824], [PROJ.b])
        else:
            for g in range(4):
                band = Cg("bandc0" if t == 0 else "bandc", g, 128, 128, 128)
                op("pe", lambda e, g=g, band=band: e.matmul(PA[:64, g * 128:(g + 1) * 128], lhsT=pin(g), rhs=band, start=True, stop=(t == 0)),
                   reads=[PROJ.b, CST.b], writes=[PA.b])
                if t > 0:
                    op("pe", lambda e, g=g: e.matmul(PA[:64, g * 128:(g + 1) * 128], lhsT=PREV[:, g * 64:(g + 1) * 64],
                                                     rhs=Cg("bandp", g, 128, 128, 128), start=False, stop=True),
                       reads=[PREV.b, CST.b], writes=[PA.b])
            if t < NT - 1:
                op("pool", lambda e: e.tensor_copy(out=PREV[:, :], in_=PROJ[:, 2568:2824]), reads=[PROJ.b], writes=[PREV.b])
            else:
                out_dma(E["o_pp"][l], PROJ[113:128, 2568:2824], [PROJ.b])
        op("act", lambda e: e.copy(out=PTT[:, :, :P], in_=PA[:64, :].rearrange("p (g c) -> p g c", g=4)[:, :, :P]),
           reads=[PA.b], writes=[PTT.b])
        for g in range(4):
            op("pe", lambda e, g=g: e.matmul(PB[:P, g * 64:(g + 1) * 64], lhsT=PTT[:, g, :P], rhs=WP[:, g, :], start=True, stop=True),
               reads=[PTT.b, WP.b], writes=[PB.b])
        op("dve", lambda e: e.tensor_tensor(out=Y[:P, 768:1024], in0=PB[:P, 0:256], in1=PRM[:P, o_ps:o_ps + 256], op=ALU.mult),
           reads=[PB.b, PRM.b], writes=[Y.b])

        _transpose8(kb, E, Y, HT, PT, P, op=op)
        for (c0, w) in mk_chunks(D):
            stream_mm(w_o_v, c0, w, HT, P,
                      lambda pm, i, c0=c0, w=w: op("dve", lambda e: e.tensor_tensor(out=H[:P, c0:c0 + w], in0=pm[:P, 0:w],
                                                                                    in1=ada[:P, 2048 + c0:2048 + c0 + w], op=ALU.mult),
                                                   reads=[pm.b, ada.b], writes=[H.b]))
        _resid_ln(kb, X, XB[t], t, P, H, SM, ST6, PRM, o_l1g, o_l1b, EPSB)
        cur[0] = None
        if nxtA:
            run_items(nxtA)
        if ti == len(tiles) - 1 and conv:
            run_items(conv)


def _resid_ln(kb, X, xb, t, P, Z, SM, ST6, PRM, og, ob, EPSB):
    op = kb.op
    xt = X[:P, t, :]
    op("dve", lambda e: e.scalar_tensor_tensor(out=Z[:P, :], in0=xt, scalar=ALPHA, in1=Z[:P, :], op0=ALU.mult, op1=ALU.add),
       reads=[xb, Z.b], writes=[Z.b])
    op("dve", lambda e: e.bn_stats(out=ST6[:P, 0, :], in_=Z[:P, 0:512]), reads=[Z.b], writes=[ST6.b])
    op("dve", lambda e: e.bn_stats(out=ST6[:P, 1, :], in_=Z[:P, 512:1024]), reads=[Z.b], writes=[ST6.b])
    op("dve", lambda e: e.bn_aggr(out=SM[:P, 41:43], in_=ST6[:P, :, :].rearrange("p a b -> p (a b)")), reads=[ST6.b], writes=[SM.b])
    op("act", lambda e: e.activation(out=SM[:P, 43:44], in_=SM[:P, 42:43], func=AF.Ln, bias=EPSB[:P, 0:1], scale=1.0), reads=[SM.b, EPSB.b], writes=[SM.b])
    op("act", lambda e: e.activation(out=SM[:P, 44:45], in_=SM[:P, 43:44], func=AF.Exp, scale=-0.5), reads=[SM.b], writes=[SM.b])
    op("dve", lambda e: e.tensor_scalar(out=Z[:P, :], in0=Z[:P, :], scalar1=SM[:P, 41:42], scalar2=SM[:P, 44:45],
                                        op0=ALU.subtract, op1=ALU.mult), reads=[Z.b, SM.b], writes=[Z.b])
    op("pool", lambda e: e.tensor_tensor(out=Z[:P, :], in0=Z[:P, :], in1=PRM[:P, og:og + D], op=ALU.mult), reads=[Z.b, PRM.b], writes=[Z.b])
    op("dve", lambda e: e.tensor_tensor(out=xt, in0=Z[:P, :], in1=PRM[:P, ob:ob + D], op=ALU.add), reads=[Z.b, PRM.b], writes=[xb])


def _phase2(nc, kb, l, E):
    C, CST, X, XB = E["C"], E["CST"], E["X"], E["XB"]
    ADA = Tn(kb, "ADA2", [128, 3072]); ADAs = ADA
    WCH = [Tn(kb, "WCHb%d" % i, [128, 8, 128], dma=True) for i in range(2)]
    badac = [Tn(kb, "badab%d" % i, [128, 256], dma=True) for i in range(2)]
    Hs = [Tn(kb, "H2_%d" % i, [128, D], dma=True) for i in range(2)]
    HT = Tn(kb, "H2T", [128, 8, 128])
    PRM = Tn(kb, "PRM2", [128, 2 * D], dma=True)
    KTS = Tn(kb, "KTS", [128, 16, 128], dma=True)
    S0 = Tn(kb, "S0", [128, 2048]); S1_ = Tn(kb, "S1", [128, 2048]); S2 = Tn(kb, "S2", [128, 256])
    TOPS = Tn(kb, "TOPS", [128, 16, 16]); IDXU = Tn(kb, "IDXU", [128, 16, 16], U32); IDXF = Tn(kb, "IDXF", [128, 16, 16])
    CV = Tn(kb, "CV", [128, 8, 16]); CPOS = Tn(kb, "CPOS", [128, 8, 16], U32)
    PAU = Tn(kb, "PAU", [128, 8, 16], U32); PBU = Tn(kb, "PBU", [128, 8, 16], U32)
    PAF = Tn(kb, "PAF", [128, 8, 16]); PBF = Tn(kb, "PBF", [128, 8, 16])
    I1 = Tn(kb, "I1", [128, 128]); I2 = Tn(kb, "I2", [128, 128])
    IDXs = [Tn(kb, "IDX%d" % i, [128, 128], I32) for i in range(2)]
    GATEs = [Tn(kb, "GATE%d" % i, [128, 128]) for i in range(2)]
    ACTV = Tn(kb, "ACTV", [128, 128]); COEF = Tn(kb, "COEF", [128, 128])
    SMf = Tn(kb, "SM2f", [128, 16]); SM = Tn(kb, "SM2", [128, 64]); ST6 = Tn(kb, "ST62", [128, 2, 6])
    NB = NBUF
    UB = [Tn(kb, "UB%d" % i, [128, 2 * D], BF16, dma=True) for i in range(NB)]
    WCAB = Tn(kb, "WCAB", [128, 8, 256], dma=True)
    ACTB = [kb.buf("actv%d" % i) for i in range(NB)]
    COEFB = [kb.buf("coef%d" % i) for i in range(NB)]
    COEF2 = Tn(kb, "COEF2", [128, 128])

    class _View:
        def __init__(self, ap, b):
            self.t = ap
            self.b = b

        def __getitem__(self, k):
            return self.t[k]
    WCA = [WCAB]
    PT = [Tn(kb, "PTb%d" % i, [128, 512], psum=True) for i in range(2)]
    PQ = [Tn(kb, "PQ%d" % i, [128, 512], psum=True) for i in range(2)]
    PM = PQ
    ACCP = [Tn(kb, "ACCP%d" % i, [128, 512], psum=True) for i in range(2)]
    JUNK = Tn(kb, "JUNK", [128, D])
    ACC = JUNK
    DGB = [Tn(kb, "DGB%d" % i, [128, 128], BF16) for i in range(3)]
    PS = [Tn(kb, "PS%d" % i, [128, 512], psum=True) for i in range(2)]

    kb.dma("sp", lambda q: q.dma_start(out=PRM[:, 0:D], in_=E["ln2g"][l]), PRM.b, writes=[PRM.b])
    kb.dma("sp", lambda q: q.dma_start(out=PRM[:, D:2 * D], in_=E["ln2b"][l]), PRM.b, writes=[PRM.b])
    kb.dma("sp", lambda q: q.dma_start(out=KTS[:, :, :], in_=E["keysT"][l]), KTS.b, writes=[KTS.b])
    wpq_v = E["w_pq"][l]

    tab = E["puvb"][l]
    PUVB = E["PUVBT"][l]
    wctr = [0]

    def make_front(t, sset):
        items = []

        def op(e, fn, reads=(), writes=()):
            items.append(("op", e, _bind(fn), None, tuple(reads), tuple(writes)))

        def dma(e, fn, owner, reads=(), writes=()):
            items.append(("dma", e, _bind(fn), owner, tuple(reads), tuple(writes)))
        is_s = (t == NT)
        P = SP if is_s else 128
        ada = ADAs if is_s else ADA
        H = Hs[sset]; IDX = IDXs[sset]; GATE = GATEs[sset]
        QT = S0
        xt = X[:P, t, :]
        op("dve", lambda e: e.tensor_tensor(out=H[:P, :], in0=xt, in1=ada[:P, 1024:2048], op=ALU.mult), reads=[XB[t], ada.b], writes=[H.b])
        op("dve", lambda e: e.tensor_tensor(out=H[:P, :], in0=H[:P, :], in1=ada[:P, 0:1024], op=ALU.add), reads=[H.b, ada.b], writes=[H.b])
        _transpose8(kb, E, H, HT, PT, P, op=op)
        def wdma(c):
            i = c % 2
            dma("sp", lambda q: q.dma_start(out=WCH[i][:, :, :], in_=wpq_v[c].rearrange("p (k j) -> p k j", k=8)), WCH[i].b, writes=[WCH[i].b])
        wdma(0)
        for c in range(16):
            i = c % 2
            j = c % 2
            if c + 1 < 16:
                wdma(c + 1)
            for k in range(8):
                op("pe", lambda e: e.matmul(PQ[j][:, 0:P], lhsT=WCH[i][:, k, :], rhs=HT[:, k, :P], start=(k == 0), stop=(k == 7)),
                   reads=[WCH[i].b, HT.b], writes=[PQ[j].b] if k in (0, 7) else [])
            op("act", lambda e: e.copy(out=QT[:, c * 128:c * 128 + P], in_=PQ[j][:, 0:P]), reads=[PQ[j].b], writes=[QT.b])
        for c4 in range(4):
            ps = PS[c4 % 2]
            for j in range(4):
                c = c4 * 4 + j
                op("pe", lambda e: e.matmul(ps[:P, j * 128:(j + 1) * 128], lhsT=QT[:, c * 128:c * 128 + P], rhs=KTS[:, c, :], start=True, stop=True),
                   reads=[QT.b, KTS.b], writes=[ps.b])
            op("act", lambda e: e.copy(out=S1_[:P, c4 * 512:(c4 + 1) * 512], in_=ps[:P, :]), reads=[ps.b], writes=[S1_.b])
        for c in range(16):
            sc = S1_[:P, c * 128:(c + 1) * 128]
            wk = S2[:P, 0:128]
            op("dve", lambda e: e.max(out=TOPS[:P, c, 0:8], in_=sc), reads=[S1_.b], writes=[TOPS.b])
            op("dve", lambda e: e.max_index(out=IDXU[:P, c, 0:8], in_max=TOPS[:P, c, 0:8], in_values=sc), reads=[S1_.b, TOPS.b], writes=[IDXU.b])
            op("dve", lambda e: e.match_replace(out=wk, in_to_replace=TOPS[:P, c, 0:8], in_values=sc, imm_value=NEG), reads=[S1_.b, TOPS.b], writes=[S2.b])
            op("dve", lambda e: e.max(out=TOPS[:P, c, 8:16], in_=wk), reads=[S2.b], writes=[TOPS.b])
            op("dve", lambda e: e.max_index(out=IDXU[:P, c, 8:16], in_max=TOPS[:P, c, 8:16], in_values=wk), reads=[S2.b, TOPS.b], writes=[IDXU.b])
        op("dve", lambda e: e.tensor_copy(out=IDXF[:P, :, :], in_=IDXU[:P, :, :]), reads=[IDXU.b], writes=[IDXF.b])
        tv = TOPS[:P, :, :].rearrange("p (h two) k -> p h two k", two=2)
        CAND = S0
        op("dve", lambda e: e.tensor_tensor(out=CAND[:P, :].rearrange("p (h a b) -> p h a b", h=8, a=16),
                                            in0=tv[:, :, 0, :].unsqueeze(3).broadcast_to([P, 8, 16, 16]),
                                            in1=tv[:, :, 1, :].unsqueeze(2).broadcast_to([P, 8, 16, 16]), op=ALU.add),
           reads=[TOPS.b], writes=[S0.b])
        for h in range(8):
            cd = CAND[:P, h * 256:(h + 1) * 256]
            wk = S2[:P, 0:256]
            op("dve", lambda e: e.max(out=CV[:P, h, 0:8], in_=cd), reads=[S0.b], writes=[CV.b])
            op("dve", lambda e: e.max_index(out=CPOS[:P, h, 0:8], in_max=CV[:P, h, 0:8], in_values=cd), reads=[S0.b, CV.b], writes=[CPOS.b])
            op("dve", lambda e: e.match_replace(out=wk, in_to_replace=CV[:P, h, 0:8], in_values=cd, imm_value=NEG), reads=[S0.b, CV.b], writes=[S2.b])
            op("dve", lambda e: e.max(out=CV[:P, h, 8:16], in_=wk), reads=[S2.b], writes=[CV.b])
            op("dve", lambda e: e.max_index(out=CPOS[:P, h, 8:16], in_max=CV[:P, h, 8:16], in_values=wk), reads=[S2.b, CV.b], writes=[CPOS.b])
        op("dve", lambda e: e.tensor_single_scalar(out=PAU[:P, :, :], in_=CPOS[:P, :, :], scalar=4, op=ALU.logical_shift_right), reads=[CPOS.b], writes=[PAU.b])
        op("dve", lambda e: e.tensor_single_scalar(out=PBU[:P, :, :], in_=CPOS[:P, :, :], scalar=15, op=ALU.bitwise_and), reads=[CPOS.b], writes=[PBU.b])
        op("dve", lambda e: e.tensor_copy(out=PAF[:P, :, :], in_=PAU[:P, :, :]), reads=[PAU.b], writes=[PAF.b])
        op("dve", lambda e: e.tensor_copy(out=PBF[:P, :, :], in_=PBU[:P, :, :]), reads=[PBU.b], writes=[PBF.b])
        iv = IDXF[:P, :, :].rearrange("p (h two) k -> p h two k", two=2)
        io16 = C("iota16", P).unsqueeze(1).unsqueeze(1).broadcast_to([P, 8, 16, 16])
        for (pf, half, dst) in [(PAF, 0, I1), (PBF, 1, I2)]:
            eq = S1_[:P, :].rearrange("p (h k a) -> p h k a", h=8, k=16)
            op("dve", lambda e: e.tensor_tensor(out=eq, in0=pf[:P, :, :].unsqueeze(3).broadcast_to([P, 8, 16, 16]), in1=io16, op=ALU.is_equal),
               reads=[pf.b, CST.b], writes=[S1_.b])
            op("dve", lambda e: e.tensor_tensor(out=eq, in0=eq, in1=iv[:, :, half, :].unsqueeze(2).broadcast_to([P, 8, 16, 16]), op=ALU.mult),
               reads=[S1_.b, IDXF.b], writes=[S1_.b])
            op("dve", lambda e: e.tensor_reduce(out=dst[:P, :].rearrange("p (h k) -> p h k", h=8), in_=eq, axis=AX.X, op=ALU.add),
               reads=[S1_.b], writes=[dst.b])
        op("dve", lambda e: e.scalar_tensor_tensor(out=I1[:P, :], in0=I1[:P, :], scalar=128.0, in1=I2[:P, :], op0=ALU.mult, op1=ALU.add),
           reads=[I1.b, I2.b], writes=[I1.b])
        op("dve", lambda e: e.tensor_copy(out=IDX[:P, :], in_=I1[:P, :]), reads=[I1.b], writes=[IDX.b])
        gv = GATE[:P, :].rearrange("p (h k) -> p h k", h=8)
        op("dve", lambda e: e.tensor_tensor(out=gv, in0=CV[:P, :, :], in1=CV[:P, :, 0:1].broadcast_to([P, 8, 16]), op=ALU.subtract),
           reads=[CV.b], writes=[GATE.b])
        op("act", lambda e: e.activation(out=GATE[:P, :], in_=GATE[:P, :], func=AF.Exp), reads=[GATE.b], writes=[GATE.b])
        op("dve", lambda e: e.tensor_reduce(out=SMf[:P, 0:8], in_=gv, axis=AX.X, op=ALU.add), reads=[GATE.b], writes=[SMf.b])
        op("dve", lambda e: e.reciprocal(out=SMf[:P, 8:16], in_=SMf[:P, 0:8]), reads=[SMf.b], writes=[SMf.b])
        op("dve", lambda e: e.tensor_tensor(out=gv, in0=gv, in1=SMf[:P, 8:16].unsqueeze(2).broadcast_to([P, 8, 16]), op=ALU.mult),
           reads=[GATE.b, SMf.b], writes=[GATE.b])
        return items

    def run_items(items, n=None):
        n = len(items) if n is None else min(n, len(items))
        for _ in range(n):
            kind, e, fn, owner, reads, writes = items.pop(0)
            if kind == "op":
                kb.op(e, fn, reads=reads, writes=writes, bound=True)
            else:
                kb.dma(e, fn, owner, reads=reads, writes=writes, bound=True)

    def back(t, sset, nxt):
        op = kb.op
        is_s = (t == NT)
        P = SP if is_s else 128
        ada = ADAs if is_s else ADA
        H = Hs[sset]; IDX = IDXs[sset]; GATE = GATEs[sset]
        per = 0 if not nxt else (len(nxt) + 119) // 120

        def axpy(s):
            b = s % NB
            dg = DGB[s % 3]
            op("act", lambda e: e.activation(out=COEF2[:P, s:s + 1], in_=COEF[:P, s:s + 1], func=AF.Identity, scale=GATE[:P, s:s + 1]),
               reads=[COEFB[b], GATE.b], writes=[COEF2.b])
            op("act", lambda e: e.activation(out=dg[:P, :P], in_=C("ident", P, P), func=AF.Identity, scale=COEF2[:P, s:s + 1]),
               reads=[COEF2.b, CST.b], writes=[dg.b])
            for hf in range(2):
                op("pe", lambda e: e.matmul(ACCP[hf][:P, :], lhsT=dg[:P, :P], rhs=UB[b][:P, D + hf * 512:D + (hf + 1) * 512], start=(s == 0), stop=(s == 127)),
                   reads=[dg.b, UB[b].b], writes=[ACCP[hf].b] if s in (0, 127) else [])

        for s_ in range(128):
            b = s_ % NB
            kb.dma("pool", lambda q: q.indirect_dma_start(out=UB[b][:P, :], out_offset=None, in_=tab,
                                                          in_offset=bass.IndirectOffsetOnAxis(ap=IDX[:P, s_:s_ + 1], axis=0)),
                   UB[b].b, reads=[IDX.b, PUVB], writes=[UB[b].b])
            op("dve", lambda e: e.scalar_tensor_tensor(out=JUNK[:P, :], in0=UB[b][:P, 0:D], scalar=1.0, in1=H[:P, :],
                                                       op0=ALU.mult, op1=ALU.mult, accum_out=ACTV[:P, s_:s_ + 1]),
               reads=[UB[b].b, H.b], writes=[JUNK.b, ACTB[b]])
            op("act", lambda e: e.activation(out=COEF[:P, s_:s_ + 1], in_=ACTV[:P, s_:s_ + 1], func=AF.Gelu), reads=[ACTB[b]], writes=[COEFB[b]])
            if s_ >= 1:
                axpy(s_ - 1)
            if nxt:
                run_items(nxt, per)
        axpy(127)
        if nxt:
            run_items(nxt)
        for hf in range(2):
            op("dve", lambda e: e.tensor_tensor(out=ACC[:P, hf * 512:(hf + 1) * 512], in0=ACCP[hf][:P, :], in1=ada[:P, 2048 + hf * 512:2048 + (hf + 1) * 512],
                                                op=ALU.mult), reads=[ACCP[hf].b, ada.b], writes=[ACC.b])
        _resid_ln(kb, X, XB[t], t, P, ACC, SM, ST6, PRM, 0, D, E["EPSB"])

    _ada(nc, kb, l, E, ADA, 128, E["cp"], 3072, WCA, PM, Hs[0], HT, PT, badac)
    run_items(make_front(0, 0))
    for t in range(NT + 1):
        nxt = make_front(t + 1, (t + 1) % 2) if t + 1 < NT else None
        back(t, t % 2, nxt)
        if t + 1 == NT:
            _ada(nc, kb, l, E, ADAs, SP, E["cs"], 3072, WCA, PM, Hs[NT % 2], HT, PT, badac)
            run_items(make_front(NT, NT % 2))


_CACHE = {}


def _chunked(w, cw):
    L, K, n = w.shape
    nch = (n + cw - 1) // cw
    wp = np.zeros((L, K, nch * cw), np.float32)
    wp[:, :, :n] = w
    wp = wp.reshape(L, 8, 128, nch, cw).transpose(0, 3, 2, 1, 4)
    return np.ascontiguousarray(wp.reshape(L, nch, 128, 8 * cw))


def _rep(a, P=128):
    return np.ascontiguousarray(np.broadcast_to(a[:, None, :], (a.shape[0], P, a.shape[1])))


def make_in_maps(inp, cpack):
    f = lambda a: np.ascontiguousarray(np.asarray(a, dtype=np.float32))
    shared = {
        "w_ada": _chunked(f(inp["w_ada"]), WCW), "b_ada": _rep(f(inp["b_ada"])), "w_in": _chunked(f(inp["w_in"]), WCW),
        "b_gate": _rep(f(inp["b_gate"])),
        "mh_g": _rep(f(inp["mh_g"])), "sgu_g": _rep(f(inp["sgu_g"])), "sgu_b": _rep(f(inp["sgu_b"])),
        "pscale": _rep(f(inp["pool_scale"])),
        "w_sT": f(np.asarray(inp["w_s"]).transpose(0, 3, 1, 2)),
        "b_sT": f(np.asarray(inp["b_s"]).transpose(0, 2, 1)),
        "w_pool": f(np.asarray(inp["w_pool"]).transpose(0, 2, 1, 3)),
        "w_o": _chunked(f(inp["w_o"]), WCW), "ln1g": _rep(f(inp["ln1_g"])), "ln1b": _rep(f(inp["ln1_b"])),
        "ln2g": _rep(f(inp["ln2_g"])), "ln2b": _rep(f(inp["ln2_b"])), "w_pq": _chunked(f(inp["w_pq"]), 128),
        "keysT": f(np.asarray(inp["peer_keys"]).transpose(0, 4, 1, 2, 3).reshape(DEPTH, 128, 16, 128)),
        "cst": cpack[0], "cst1": cpack[1],
    }
    ws4 = np.asarray(inp["w_s"])[:, :, :ST, :ST]
    wsS = np.repeat(np.repeat(ws4.transpose(0, 3, 1, 2), SB, axis=1), SB, axis=3)
    shared["w_sS"] = f(wsS)
    bs4 = np.asarray(inp["b_s"])[:, :, :ST]
    shared["b_sS"] = f(np.repeat(bs4.transpose(0, 2, 1), SB, axis=1))
    for l in range(DEPTH):
        shared["puv%d" % l] = np.ascontiguousarray(
            np.concatenate([np.asarray(inp["peer_u"])[l], np.asarray(inp["peer_v"])[l]], axis=1), dtype=np.float32)
    maps = []
    for c in range(NCORES):
        bs = slice(c * SB, (c + 1) * SB)
        m = dict(shared)
        m["xp"] = f(np.asarray(inp["x_prompt"])[c])
        m["xs"] = f(np.asarray(inp["x_sample"])[bs].transpose(1, 0, 2).reshape(SP, D))
        m["cp"] = f(np.broadcast_to(np.asarray(inp["c_prompt"])[c][None, :], (128, D)))
        m["cs"] = f(np.tile(np.asarray(inp["c_sample"])[bs], (ST, 1)))
        sCc = np.asarray(inp["state_mlstm_C"])[:, bs]
        m["sC"] = f(sCc.transpose(0, 2, 3, 1, 4))
        snc = np.asarray(inp["state_mlstm_n"])[:, bs]
        m["snat"] = f(snc)
        m["snT"] = f(snc.transpose(0, 2, 3, 1))
        m["sm"] = f(np.tile(np.asarray(inp["state_mlstm_m"])[:, bs], (1, ST, 1)))
        spc = np.asarray(inp["state_pool"])[:, bs].transpose(0, 2, 1, 3)
        m["spA"] = f(spc[:, 0:8].reshape(DEPTH, 128, 256))
        m["spB"] = f(spc[:, 8:15].reshape(DEPTH, 112, 256))
        maps.append(m)
    return maps


def gather_outputs(results):
    cat = lambda k, ax: np.concatenate([r[k] for r in results], axis=ax)
    yp = np.stack([r["yp"] for r in results], 0)
    ys = np.concatenate([r["ys"].reshape(ST, SB, D).transpose(1, 0, 2) for r in results], 0)
    pC = np.stack([r["pC"] for r in results], 1)
    pn = np.stack([r["pn"] for r in results], 1)
    pm = np.stack([r["pm"] for r in results], 1)
    pp = np.stack([r["pp"] for r in results], 1)
    return (yp, ys, pC, pn, pm, pp, cat("nC", 1), cat("nn", 1), cat("nm", 1), cat("npool", 1), cat("nv", 1))


def kernel(**inputs):
    if "prog" not in _CACHE:
        _CACHE["prog"] = build_program()
    nc, cpack = _CACHE["prog"]
    maps = make_in_maps(inputs, cpack)
    res = run_bass_kernel_spmd(nc, maps, core_ids=list(range(NCORES)))
    outs = gather_outputs(res.results)
    return tuple(np.ascontiguousarray(o, dtype=np.float32) for o in outs)
```

```python
import numpy as np
from contextlib import ExitStack
import concourse.bass as bass
import concourse.mybir as mybir
from concourse.bass_utils import run_bass_kernel_spmd

F32 = mybir.dt.float32
I32 = mybir.dt.int32
U32 = mybir.dt.uint32
F32R = mybir.dt.float32r
BF16 = mybir.dt.bfloat16
ALU = mybir.AluOpType
AF = mybir.ActivationFunctionType
AX = mybir.AxisListType

NCORES = 8
D = 1024
SEQ = 2048
NT = 16
SB = 16
ST = 4
SP = SB * ST
DEPTH = 2
ALPHA = (2 * DEPTH) ** 0.25
LN_EPS = 1e-5
IN_COLS = 2824
NEG = -1.0e30
WCW = 192
NEXP = 16384
SAME_ENGINE_WAITS = True
NBUF = 12


class TB:
    def __init__(self, name, sem=None):
        self.name = name
        self.last_w = None
        self.reads = []
        self.sem = sem
        self.dma_total = 0
        self.dma_dirty = False


class KB:
    ENG = ("pe", "act", "dve", "pool", "sp")

    def __init__(self, nc, stack):
        self.nc = nc
        self.stack = stack
        self.q = {e: [] for e in self.ENG}
        self.cnt = {e: 0 for e in self.ENG}
        self.esem = {e: stack.enter_context(nc.semaphore("es_" + e)) for e in self.ENG}
        self.seen = {e: {} for e in self.ENG}
        self.semobj = {}
        self._sem_owner = {}
        self.stack0 = stack
        self.phase_tbs = []
        self.sem_pool = []
        self.nsem = 0
        self.sfx = ""

    def new_sem(self, name):
        return self.stack.enter_context(self.nc.semaphore(name + self.sfx))

    def buf(self, name, dma=False):
        if not dma:
            return TB(name)
        if self.sem_pool:
            sem, val = self.sem_pool.pop()
        else:
            sem, val = self.stack0.enter_context(self.nc.semaphore("dsem%d" % self.nsem)), 0
            self.nsem += 1
        tb = TB(name, sem)
        tb.dma_total = val
        if self.stack is not self.stack0:
            self.phase_tbs.append(tb)
        return tb

    def end_phase(self):
        for tb in self.phase_tbs:
            self._sem_owner.pop(id(tb.sem), None)
            self.sem_pool.append((tb.sem, tb.dma_total))
        self.phase_tbs = []

    def sb(self, name, shape, dt=F32):
        return self.stack.enter_context(self.nc.sbuf_tensor(name + self.sfx, list(shape), dt))

    def ps(self, name, shape, dt=F32):
        return self.stack.enter_context(self.nc.psum_tensor(name + self.sfx, list(shape), dt))

    def _deps(self, e, reads, writes):
        deps = {}

        def add(tok):
            if tok is None:
                return
            s, v = tok
            k = id(s)
            self.semobj[k] = s
            ow = self._sem_owner.get(k)
            if ow is not None:
                v = ow.dma_total
            if v > deps.get(k, 0):
                deps[k] = v
        for b in reads:
            add(b.last_w)
        for b in writes:
            add(b.last_w)
            for r in b.reads:
                add(r)
        out = []
        own = id(self.esem[e])
        for k, v in deps.items():
            if k == own and (e in ("pe", "sp") or not SAME_ENGINE_WAITS):
                continue
            if self.seen[e].get(k, 0) >= v:
                continue
            self.seen[e][k] = v
            out.append((self.semobj[k], v))
        return out

    def op(self, e, fn, reads=(), writes=(), bound=False):
        waits = self._deps(e, reads, writes)
        for s, v in waits:
            tb = self._sem_owner.get(id(s))
            if tb is not None:
                tb.dma_dirty = True
        self.cnt[e] += 1
        tok = (self.esem[e], self.cnt[e])
        self.q[e].append((waits, fn if bound else _bind(fn), tok[0], 1))
        for b in reads:
            b.reads.append(tok)
        for b in writes:
            b.last_w = tok
            b.reads = []
        return tok

    def dma(self, e, fn, owner, reads=(), writes=(), bound=False):
        self._sem_owner[id(owner.sem)] = owner
        waits = self._deps(e, reads, writes)
        if owner.dma_dirty and owner.dma_total > 0:
            k = id(owner.sem)
            if self.seen[e].get(k, 0) < owner.dma_total:
                self.seen[e][k] = owner.dma_total
                waits.append((owner.sem, owner.dma_total))
            owner.dma_dirty = False
        for s, v in waits:
            tb = self._sem_owner.get(id(s))
            if tb is not None and tb is not owner:
                tb.dma_dirty = True
        owner.dma_total += 16
        tok = (owner.sem, owner.dma_total)
        self.q[e].append((waits, fn if bound else _bind(fn), owner.sem, 16))
        for b in reads:
            b.reads.append(tok)
        for b in writes:
            b.last_w = tok
            b.reads = []
        return tok

    def barrier(self, extra=()):
        toks = [(self.esem[e], self.cnt[e]) for e in self.ENG if self.cnt[e] > 0 and e != "sp"]
        for tb in list(self._sem_owner.values()) + list(extra):
            if tb.dma_total > 0:
                toks.append((tb.sem, tb.dma_total))
        for e in self.ENG:
            waits = []
            for s, v in toks:
                k = id(s)
                if k == id(self.esem[e]):
                    continue
                if self.seen[e].get(k, 0) >= v:
                    continue
                self.seen[e][k] = v
                waits.append((s, v))
            if waits:
                self.q[e].append((waits, None, None, 0))

    def emit(self, final_waits=()):
        nc = self.nc
        engs = {"pe": "tensor", "act": "scalar", "dve": "vector", "pool": "gpsimd", "sp": "sync"}
        with nc.Block() as block:
            for e in self.ENG:
                items = self.q[e]
                fw = list(final_waits) if e == "sp" else []

                def body(eng, items=items, fw=fw):
                    for waits, fn, sem, inc in items:
                        for s, v in waits:
                            eng.wait_ge(s, v)
                        if fn is not None:
                            fn(eng).then_inc(sem, inc)
                    for s, v in fw:
                        eng.wait_ge(s, v)
                getattr(block, engs[e])(body)
        self.q = {e: [] for e in self.ENG}


class _Rec:
    def __init__(self):
        self.call = None

    def __getattr__(self, name):
        def f(*a, **k):
            self.call = (name, a, k)
            return self
        return f


def _bind(fn):
    r = _Rec()
    fn(r)
    assert r.call is not None
    name, a, k = r.call
    return lambda eng: getattr(eng, name)(*a, **k)


class Tn:
    def __init__(self, kb, name, shape, dt=F32, psum=False, dma=False):
        self.t = kb.ps(name, shape, dt) if psum else kb.sb(name, shape, dt)
        self.b = kb.buf(name, dma=dma)

    def __getitem__(self, k):
        return self.t[k]


def _consts():
    c = {}
    i128 = np.arange(128)
    c["ident"] = np.eye(128, dtype=np.float32)
    c["ones"] = np.ones((128, 128), np.float32)
    c["triu"] = (i128[:, None] <= i128[None, :]).astype(np.float32)
    c["negm"] = np.where(i128[None, :] <= i128[:, None], 0.0, NEG).astype(np.float32)
    sel = np.zeros((128, 128), np.float32); sel[127, :] = 1.0
    c["sel127"] = sel
    p = np.arange(SP); tt = p // SB; bb = p % SB
    sameb = bb[:, None] == bb[None, :]
    tri_s = (sameb & (tt[:, None] <= tt[None, :])).astype(np.float32)
    c["tri_s"] = _pad(tri_s)
    c["negm_s"] = _pad(np.where(sameb & (tt[None, :] <= tt[:, None]), 0.0, NEG).astype(np.float32))
    c["negb_s"] = _pad(np.where(sameb, 0.0, NEG).astype(np.float32))
    c["selend"] = _pad(((tt[:, None] == ST - 1) & sameb).astype(np.float32))
    oh = (bb[:, None] == np.arange(SB)[None, :]).astype(np.float32)
    c["onehotB"] = _pad(oh, cols=16)
    oh0 = ((p[:, None] == np.arange(SB)[None, :])).astype(np.float32)
    c["onehot0"] = _pad(oh0, cols=16)
    c["iota16"] = np.broadcast_to(np.arange(16, dtype=np.float32), (128, 16)).copy()
    wins = (2, 4, 8, 16)
    bc0 = np.zeros((4, 128, 128), np.float32); bc = np.zeros((4, 128, 128), np.float32)
    bp = np.zeros((4, 128, 128), np.float32)
    for g, w in enumerate(wins):
        for t in range(128):
            for j in range(w):
                s = t - j
                if s >= 0:
                    bc[g, s, t] += 1.0 / w
                    bc0[g, s, t] += 1.0 / min(t + 1, w)
                else:
                    bp[g, s + 128, t] += 1.0 / w
            bc[g, t, t] -= 1.0
            bc0[g, t, t] -= 1.0
    c["bandc0"] = bc0.transpose(1, 0, 2).reshape(128, 512)
    c["bandc"] = bc.transpose(1, 0, 2).reshape(128, 512)
    c["bandp"] = bp.transpose(1, 0, 2).reshape(128, 512)
    bsA = np.zeros((4, 128, SP), np.float32); bsB = np.zeros((4, 128, SP), np.float32)
    bsC = np.zeros((4, 128, SP), np.float32)
    for g, w in enumerate(wins):
        for t in range(ST):
            for b in range(SB):
                col = t * SB + b
                for j in range(w):
                    r = 15 + t - j
                    if r >= 15:
                        bsC[g, (r - 15) * SB + b, col] += 1.0 / w
                    elif r >= 8:
                        bsB[g, (r - 8) * SB + b, col] += 1.0 / w
                    else:
                        bsA[g, r * SB + b, col] += 1.0 / w
                bsC[g, t * SB + b, col] -= 1.0
    c["bsA"] = bsA.transpose(1, 0, 2).reshape(128, 4 * SP)
    c["bsB"] = bsB.transpose(1, 0, 2).reshape(128, 4 * SP)
    c["bsC"] = bsC.transpose(1, 0, 2).reshape(128, 4 * SP)
    return c


def _pad(a, cols=None):
    out = np.zeros((128, a.shape[1] if cols is None else cols), np.float32)
    out[: a.shape[0], : a.shape[1]] = a
    return out


_CONST_G = ["ident", "ones", "iota16"]
_CONST_1 = ["triu", "negm", "sel127", "tri_s", "negm_s", "negb_s", "selend",
            "onehotB", "onehot0", "bandc0", "bandc", "bandp", "bsA", "bsB", "bsC"]


def _const_pack():
    c = _consts()
    packs = []
    for order in (_CONST_G, _CONST_1):
        offs = {}
        o = 0
        arrs = []
        for k in order:
            offs[k] = (o, c[k].shape[1])
            o += c[k].shape[1]
            arrs.append(c[k])
        packs.append((np.ascontiguousarray(np.concatenate(arrs, axis=1)), offs))
    return packs


def build_program(n_layers=DEPTH, do_phase2=True):
    (cpack, coff), (cpack1, coff1) = _const_pack()
    NCST = cpack.shape[1]
    NCST1 = cpack1.shape[1]
    nc = bass.Bass("TRN2", target_bir_lowering=False)

    def din(name, shape, dt=F32):
        return nc.dram_tensor(name, list(shape), dt, kind="ExternalInput").ap()

    def dout(name, shape, dt=F32):
        return nc.dram_tensor(name, list(shape), dt, kind="ExternalOutput").ap()

    xp = din("xp", [SEQ, D]); xs = din("xs", [SP, D])
    cp = din("cp", [128, D]); cs = din("cs", [SP, D])
    sC = din("sC", [DEPTH, 4, 128, SB, 128]); snat = din("snat", [DEPTH, SB, 4, 128])
    snT = din("snT", [DEPTH, 4, 128, SB]); sm = din("sm", [DEPTH, SP, 4])
    spA = din("spA", [DEPTH, 128, 256]); spB = din("spB", [DEPTH, 112, 256])
    w_ada = din("w_ada", [DEPTH, (6 * D) // WCW, 128, 8 * WCW]); b_ada = din("b_ada", [DEPTH, 128, 6 * D])
    w_in = din("w_in", [DEPTH, (IN_COLS + WCW - 1) // WCW, 128, 8 * WCW]); b_gate = din("b_gate", [DEPTH, 128, 8])
    mh_g = din("mh_g", [DEPTH, 128, 512]); sgu_g = din("sgu_g", [DEPTH, 128, 256])
    sgu_b = din("sgu_b", [DEPTH, 128, 256]); pscale = din("pscale", [DEPTH, 128, 256])
    w_sT = din("w_sT", [DEPTH, 128, 4, 128]); b_sT = din("b_sT", [DEPTH, 128, 4])
    w_sS = din("w_sS", [DEPTH, SP, 4, SP]); b_sS = din("b_sS", [DEPTH, SP, 4])
    w_pool = din("w_pool", [DEPTH, 64, 4, 64]); w_o = din("w_o", [DEPTH, (D + WCW - 1) // WCW, 128, 8 * WCW])
    ln1g = din("ln1g", [DEPTH, 128, D]); ln1b = din("ln1b", [DEPTH, 128, D])
    ln2g = din("ln2g", [DEPTH, 128, D]); ln2b = din("ln2b", [DEPTH, 128, D])
    w_pq = din("w_pq", [DEPTH, 16, 128, 8 * 128]); keysT = din("keysT", [DEPTH, 128, 16, 128])
    puv = [din("puv%d" % l, [NEXP, 2 * D]) for l in range(DEPTH)]
    puvb = [nc.dram_tensor("puvb%d" % l, [NEXP, 2 * D], BF16, kind="Internal").ap() for l in range(DEPTH)]
    cst_d = din("cst", [128, NCST])
    cst1_d = din("cst1", [128, NCST1])

    yp = dout("yp", [SEQ, D]); ys = dout("ys", [SP, D])
    o_pC = dout("pC", [DEPTH, 4, 128, 128]); o_pn = dout("pn", [DEPTH, 4, 128]); o_pm = dout("pm", [DEPTH, 4])
    o_pp = dout("pp", [DEPTH, 15, 256])
    o_nC = dout("nC", [DEPTH, SB, 4, 128, 128]); o_nn = dout("nn", [DEPTH, SB, 4, 128])
    o_nm = dout("nm", [DEPTH, SB, 4]); o_np = dout("npool", [DEPTH, SB, 15, 256])
    o_nv = dout("nv", [DEPTH, SB, ST, 256])

    with ExitStack() as st0:
        kb = KB(nc, st0)
        op = kb.op
        OUT = kb.buf("outs", dma=True)

        def out_dma(dst, src, reads):
            kb.dma("sp", lambda q: q.dma_start(out=dst, in_=src), OUT, reads=reads)

        X = kb.sb("X", [128, NT + 1, D])
        XB = [kb.buf("X%d" % t) for t in range(NT + 1)]
        XL = kb.buf("xload", dma=True)
        CST = Tn(kb, "CST", [128, NCST], dma=True)
        PUVBT = [kb.buf("puvb%d" % i, dma=True) for i in range(DEPTH)]
        EPSB = Tn(kb, "EPSB", [128, 1])
        kb.op("dve", lambda e: e.memset(EPSB[:, :], LN_EPS), writes=[EPSB.b])

        def C(name, P=128, w=None):
            if name in coff:
                o, n = coff[name]
                return CST[:P, o:o + (n if w is None else w)]
            o, n = coff1[name]
            return kb.cst1[:P, o:o + (n if w is None else w)]

        def Cg(name, g, P, blk, w):
            o, n = coff1[name]
            return kb.cst1[:P, o + g * blk: o + g * blk + w]

        with nc.allow_non_contiguous_dma(reason="small strided state/param loads"):
            kb.dma("sp", lambda q: q.dma_start(out=CST[:, :], in_=cst_d), CST.b, writes=[CST.b])
            for t in range(NT):
                kb.dma("sp", lambda q, t=t: q.dma_start(out=X[:, t, :], in_=xp[t * 128:(t + 1) * 128, :]),
                       XL, writes=[XB[t]])
            kb.dma("sp", lambda q: q.dma_start(out=X[:SP, NT, :], in_=xs), XL, writes=[XB[NT]])

            puvb_ = puvb
            for l in range(n_layers):
                for part in ("p", "s"):
                    with ExitStack() as st1:
                        kb.stack = st1
                        kb.sfx = "_a%s%d" % (part, l)
                        _phase1(nc, kb, l, locals(), part)
                        kb.barrier(extra=[OUT])
                        kb.emit()
                        kb.end_phase()
                if do_phase2:
                    with ExitStack() as st2:
                        kb.stack = st2
                        kb.sfx = "_b%d" % l
                        _phase2(nc, kb, l, locals())
                        kb.barrier(extra=[OUT])
                        kb.emit()
                        kb.end_phase()
            kb.stack = st0
            kb.sfx = ""
            for t in range(NT):
                out_dma(yp[t * 128:(t + 1) * 128, :], X[:, t, :], [XB[t]])
            out_dma(ys, X[:SP, NT, :], [XB[NT]])
            kb.emit(final_waits=[(OUT.sem, OUT.dma_total)])
    return nc, (cpack, cpack1)


def _ada(nc, kb, l, E, ADA, P, csrc, off, WCH, PM, hbuf, hT, PT, badac, WCR=None):
    op = kb.op
    C = E["C"]
    w_ada, b_ada = E["w_ada"], E["b_ada"]
    kb.dma("sp", lambda q: q.dma_start(out=hbuf[:P, :], in_=csrc), hbuf.b, writes=[hbuf.b])
    op("act", lambda e: e.activation(out=hbuf[:P, :], in_=hbuf[:P, :], func=AF.Silu), reads=[hbuf.b], writes=[hbuf.b])
    _transpose8(kb, E, hbuf, hT, PT, P)
    r32 = (hT.t.dtype == F32R)
    for c in range(3072 // WCW):
        i = c % 2
        c0 = off + c * WCW
        wch = WCH[c % len(WCH)]
        kb.dma("sp", lambda q: q.dma_start(out=wch[:, :, 0:WCW], in_=w_ada[l, c0 // WCW].rearrange("p (k j) -> p k j", k=8)), wch.b, writes=[wch.b])
        kb.dma("sp", lambda q: q.dma_start(out=badac[i][:P, 0:WCW], in_=b_ada[l, :P, c0:c0 + WCW]), badac[i].b, writes=[badac[i].b])
        wsrc = WCR[c % 2] if r32 else wch
        if r32:
            op("act", lambda e: e.copy(out=wsrc[:, :, 0:WCW], in_=wch[:, :, 0:WCW]), reads=[wch.b], writes=[wsrc.b])
        for k in range(8):
            if r32:
                op("pe", lambda e: e.matmul(PM[i][:, 0:WCW], lhsT=hT[:, k, :], rhs=wsrc[:, k, 0:WCW], start=(k == 0), stop=(k == 7)),
                   reads=[hT.b, wsrc.b], writes=[PM[i].b] if k in (0, 7) else [])
            else:
                op("pe", lambda e: e.matmul(PM[i][:P, 0:WCW], lhsT=hT[:, k, :P], rhs=wch[:, k, 0:WCW], start=(k == 0), stop=(k == 7)),
                   reads=[hT.b, wch.b], writes=[PM[i].b] if k in (0, 7) else [])
        op("dve", lambda e: e.tensor_tensor(out=ADA[:P, c * WCW:(c + 1) * WCW], in0=PM[i][:P, 0:WCW], in1=badac[i][:P, 0:WCW], op=ALU.add),
           reads=[PM[i].b, badac[i].b], writes=[ADA.b])
    op("dve", lambda e: e.tensor_scalar_add(out=ADA[:P, 1024:2048], in0=ADA[:P, 1024:2048], scalar1=1.0), reads=[ADA.b], writes=[ADA.b])


def _transpose8(kb, E, src, dstT, PT, P, srcb=None, op=None):
    op = kb.op if op is None else op
    C = E["C"]
    sb_ = src.b if srcb is None else srcb
    for half in range(2):
        for j in range(4):
            k = half * 4 + j
            op("pe", lambda e, half=half, j=j, k=k: e.transpose(
                out=PT[half][:, j * 128:j * 128 + P], in_=src[:P, k * 128:(k + 1) * 128], identity=C("ident", P, P)),
               reads=[sb_, E["CST"].b], writes=[PT[half].b])
        op("act", lambda e, half=half: e.copy(
            out=dstT[:, half * 4:half * 4 + 4, :P],
            in_=PT[half][:, :].rearrange("p (j c) -> p j c", j=4)[:, :, :P]),
           reads=[PT[half].b], writes=[dstT.b])


def _phase1(nc, kb, l, E, part):
    isS = (part == "s")
    tiles = [NT] if isS else list(range(NT))
    cur = [None]
    cnt = [0]
    convq = [None]

    def run_items(items, n=None):
        n = len(items) if n is None else min(n, len(items))
        for _ in range(n):
            kind, e, fn, owner, reads, writes = items.pop(0)
            if kind == "op":
                kb.op(e, fn, reads=reads, writes=writes, bound=True)
            else:
                kb.dma(e, fn, owner, reads=reads, writes=writes, bound=True)

    def op(e, fn, reads=(), writes=()):
        tok = kb.op(e, fn, reads=reads, writes=writes)
        if cur[0]:
            run_items(cur[0], 1)
        cnt[0] += 1
        if convq[0] and cnt[0] % 16 == 0:
            run_items(convq[0], 1)
        return tok
    C, Cg, CST, X, XB = E["C"], E["Cg"], E["CST"], E["X"], E["XB"]
    EPSB = E["EPSB"]
    out_dma = E["out_dma"]
    w_in, w_o = E["w_in"], E["w_o"]

    kb.cst1 = kb.sb("CST1", [128, E["NCST1"]])
    kb.dma("sp", lambda q: q.dma_start(out=kb.cst1[:, :], in_=E["cst1_d"]), CST.b, writes=[CST.b])
    ADA = Tn(kb, "ADA1", [128, 3072]); ADAs = ADA
    WCH = [Tn(kb, "WCH%d" % i, [128, 8, WCW], dma=True) for i in range(2)]
    WCR = [Tn(kb, "WCR%d" % i, [128, 8, WCW], F32R) for i in range(2)]
    badac = [Tn(kb, "bada%d" % i, [128, 256], dma=True) for i in range(2)]
    nbuf = 1 if isS else 2
    Hs = [Tn(kb, "H%d" % i, [128, D], dma=True) for i in range(nbuf)]
    HTs = [Tn(kb, "HT%d" % i, [128, 8, 128], F32R) for i in range(nbuf)]
    PROJs = [Tn(kb, "PROJ%d" % i, [128, IN_COLS], dma=True) for i in range(nbuf)]
    H, HT, PROJ = Hs[0], HTs[0], PROJs[0]
    Ys = [Tn(kb, "Y%d" % i, [128, D]) for i in range(nbuf)]
    Y = Ys[0]
    SMC = Tn(kb, "SMC", [128, 64]); ST6C = Tn(kb, "ST6C", [128, 2, 6])
    PRM = Tn(kb, "PRM", [128, 8 + 512 + 256 * 3 + 2 * D], dma=True)
    WS = BS = WSs = BSs = None
    if isS:
        WSs = Tn(kb, "WSs", [128, 4, SP], dma=True); BSs = Tn(kb, "BSs", [128, 4], dma=True)
    else:
        WS = Tn(kb, "WS", [128, 4, 128], dma=True); BS = Tn(kb, "BS", [128, 4], dma=True)
    WP = Tn(kb, "WP", [64, 4, 64], dma=True)
    PT = [Tn(kb, "PT%d" % i, [128, 512], psum=True) for i in range(2)]
    PM = [Tn(kb, "PM%d" % i, [128, 512], psum=True) for i in range(2)]
    PA = Tn(kb, "PA", [128, 512], psum=True); PB = Tn(kb, "PB", [128, 512], psum=True)
    PC = Tn(kb, "PC", [128, 512], psum=True); PD = Tn(kb, "PD", [128, 512], psum=True)
    SM = Tn(kb, "SM", [128, 64])
    SMs = MREP = CTX = None
    if isS:
        SMs = Tn(kb, "SMs", [128, 4], dma=True)
    else:
        MREP = Tn(kb, "MREP", [128, 4])
        CTX = Tn(kb, "CTX", [128, 4, 129], dma=True)
    DG = Tn(kb, "DG", [128, 128]); DL = Tn(kb, "DL", [128, 128]); WI = Tn(kb, "WI", [128, 128])
    AM = Tn(kb, "AM", [128, 128]); AT = Tn(kb, "AT", [128, 128])
    QT = Tn(kb, "QT", [128, 128]); KT = Tn(kb, "KT", [128, 128])
    VX = Tn(kb, "VX", [128, 129]); TOT = Tn(kb, "TOT", [128, 129]); WV = Tn(kb, "WV", [128, 129])
    HN = Tn(kb, "HN", [128, 128]); SG = Tn(kb, "SG", [128, 128]); ST6 = Tn(kb, "ST6", [128, 2, 6])
    OUTC = None if isS else Tn(kb, "OUTC", [128, 128], dma=True)
    CN = CTS = RA = ZQ = NNAT = NTH = WCB = DECD = DECR = MSO = SPA = SPB = PREV = None
    if isS:
        CN = Tn(kb, "CN", [128, SB, 128], dma=True); CTS = Tn(kb, "CTS", [128, SB, 129])
        RA = Tn(kb, "RA", [128, SB, 128])

    class _V2:
        def __init__(self, ap, b):
            self.t = ap
            self.b = b

        def __getitem__(self, k):
            return self.t[k]
    if isS:
        ZQ = _V2(RA[:, :, :].rearrange("p a b -> p (a b)")[:, 0:SB * SP], RA.b)
        NNAT = Tn(kb, "NNAT", [SB, 4, 128], dma=True); NTH = Tn(kb, "NTH", [128, SB], dma=True)
        WCB = Tn(kb, "WCB", [128, 16]); DECD = Tn(kb, "DECD", [128, 16]); DECR = Tn(kb, "DECR", [128, 16])
        MSO = Tn(kb, "MSO", [SB, 4], dma=True)
        SPA = Tn(kb, "SPA", [128, 256], dma=True); SPB = Tn(kb, "SPB", [128, 256], dma=True)
    else:
        PREV = Tn(kb, "PREV", [128, 256])
    PTT = Tn(kb, "PTT", [64, 4, 128])
    VN = Tn(kb, "VN", [128, 256], dma=True); VTMP = Tn(kb, "VTMP", [128, 256])

    o_bg, o_mh, o_sg, o_sb, o_ps, o_l1g, o_l1b = 0, 8, 520, 776, 1032, 1288, 1288 + D
    for (o, w, src) in [(o_bg, 8, E["b_gate"]), (o_mh, 512, E["mh_g"]), (o_sg, 256, E["sgu_g"]), (o_sb, 256, E["sgu_b"]),
                        (o_ps, 256, E["pscale"]), (o_l1g, D, E["ln1g"]), (o_l1b, D, E["ln1b"])]:
        kb.dma("sp", lambda q, o=o, w=w, src=src: q.dma_start(out=PRM[:, o:o + w], in_=src[l]), PRM.b, writes=[PRM.b])
    kb.dma("sp", lambda q: q.dma_start(out=WP[:, :, :], in_=E["w_pool"][l]), WP.b, writes=[WP.b])
    if isS:
        kb.dma("sp", lambda q: q.dma_start(out=WSs[:SP, :, :], in_=E["w_sS"][l]), WSs.b, writes=[WSs.b])
        kb.dma("sp", lambda q: q.dma_start(out=BSs[:SP, :], in_=E["b_sS"][l]), BSs.b, writes=[BSs.b])
    else:
        kb.dma("sp", lambda q: q.dma_start(out=WS[:, :, :], in_=E["w_sT"][l]), WS.b, writes=[WS.b])
        kb.dma("sp", lambda q: q.dma_start(out=BS[:, :], in_=E["b_sT"][l]), BS.b, writes=[BS.b])
    for g in range(4):
        if isS:
            op("dve", lambda e, g=g: e.tensor_tensor(out=WSs[:SP, g, :], in0=WSs[:SP, g, :], in1=C("tri_s", SP, SP), op=ALU.mult),
               reads=[WSs.b, CST.b], writes=[WSs.b])
        else:
            op("dve", lambda e, g=g: e.tensor_tensor(out=WS[:, g, :], in0=WS[:, g, :], in1=C("triu"), op=ALU.mult),
               reads=[WS.b, CST.b], writes=[WS.b])
    if not isS:
        op("dve", lambda e: e.memset(CTX[:, :, :], 0.0), writes=[CTX.b])
        op("dve", lambda e: e.memset(MREP[:, :], 0.0), writes=[MREP.b])
    op("dve", lambda e: e.memset(VX[:, :], 1.0), writes=[VX.b])

    if isS:
        _ada(nc, kb, l, E, ADA, SP, E["cs"], 0, WCH, PM, H, HT, PT, badac, WCR)
    else:
        _ada(nc, kb, l, E, ADA, 128, E["cp"], 0, WCH, PM, H, HT, PT, badac, WCR)

    w_in_v = w_in[l]
    w_o_v = w_o[l]
    def mk_chunks(n):
        return [(c0, min(WCW, n - c0)) for c0 in range(0, n, WCW)]
    chunks = mk_chunks(IN_COLS)
    wctr = [0]

    def stream_mm(wview, c0, w, lhsT, P, evac, op=op, dma=kb.dma):
        i = wctr[0] % 2
        wctr[0] += 1
        dma("sp", lambda q: q.dma_start(out=WCH[i][:, :, :], in_=wview[c0 // WCW].rearrange("p (k j) -> p k j", k=8)), WCH[i].b, writes=[WCH[i].b])
        wr = WCR[i]
        if wctr[0] % 3 == 0:
            op("dve", lambda e: e.tensor_copy(out=wr[:, :, 0:w], in_=WCH[i][:, :, 0:w]), reads=[WCH[i].b], writes=[wr.b])
        else:
            op("act", lambda e: e.copy(out=wr[:, :, 0:w], in_=WCH[i][:, :, 0:w]), reads=[WCH[i].b], writes=[wr.b])
        for k in range(8):
            op("pe", lambda e, k=k: e.matmul(PM[i][:, 0:w], lhsT=lhsT[:, k, :], rhs=wr[:, k, 0:w],
                                              start=(k == 0), stop=(k == 7)),
               reads=[lhsT.b, wr.b], writes=[PM[i].b] if k in (0, 7) else [])
        evac(PM[i], i)

    def make_A(t):
        items = []

        def iop(e, fn, reads=(), writes=()):
            items.append(("op", e, _bind(fn), None, tuple(reads), tuple(writes)))

        def idma(e, fn, owner, reads=(), writes=()):
            items.append(("dma", e, _bind(fn), owner, tuple(reads), tuple(writes)))
        P = SP if isS else 128
        H, HT, PROJ = Hs[t % nbuf], HTs[t % nbuf], PROJs[t % nbuf]
        xt = X[:P, t, :]
        iop("dve", lambda e: e.tensor_tensor(out=H[:P, :], in0=xt, in1=ADA[:P, 1024:2048], op=ALU.mult), reads=[XB[t], ADA.b], writes=[H.b])
        iop("dve", lambda e: e.tensor_tensor(out=H[:P, :], in0=H[:P, :], in1=ADA[:P, 0:1024], op=ALU.add), reads=[H.b, ADA.b], writes=[H.b])
        _transpose8(kb, E, H, HT, PT, P, op=iop)
        for (c0, w) in chunks:
            stream_mm(w_in_v, c0, w, HT, P,
                      lambda pm, i, c0=c0, w=w: iop("act", lambda e: e.copy(out=PROJ[:P, c0:c0 + w], in_=pm[:P, 0:w]),
                                                    reads=[pm.b], writes=[PROJ.b]), op=iop, dma=idma)
        return items

    conv = []
    if not isS:
        CBs = [Tn(kb, "CB%d" % i, [128, 2 * D], BF16, dma=True) for i in range(1)]
        tab32 = E["puv"][l]
        tabb = E["puvb"][l]
        PUVB = E["PUVBT"][l]
        for blk in range(NEXP // 128):
            cb = CBs[0]
            conv.append(("dma", "pool", _bind(lambda q: q.dma_start(out=cb[:, :], in_=tab32[blk * 128:(blk + 1) * 128, :])), cb.b, (), (cb.b,)))
            conv.append(("dma", "sp", _bind(lambda q: q.dma_start(out=tabb[blk * 128:(blk + 1) * 128, :], in_=cb[:, :])), PUVB, (cb.b,), (PUVB,)))
    convq[0] = conv
    pendC = [None]
    run_items(make_A(tiles[0]))
    for ti, t in enumerate(tiles):
        is_s = isS
        P = SP if is_s else 128
        ada = ADA
        H, HT, PROJ = Hs[t % nbuf], HTs[t % nbuf], PROJs[t % nbuf]
        xt = X[:P, t, :]
        Y = Ys[t % nbuf]
        nxtA = make_A(tiles[ti + 1]) if ti + 1 < len(tiles) else None
        inter = (pendC[0] or []) + (nxtA or [])
        pendC[0] = None
        cur[0] = inter
        tri = C("tri_s", SP, SP) if is_s else C("triu")
        negm = C("negm_s", SP, SP) if is_s else C("negm")
        selE = C("selend", SP, SP) if is_s else C("sel127")
        if is_s:
            kb.dma("sp", lambda q: q.dma_start(out=SMs[:SP, :], in_=E["sm"][l]), SMs.b, writes=[SMs.b])
            kb.dma("sp", lambda q: q.dma_start(out=NNAT[:, :, :], in_=E["snat"][l]), NNAT.b, writes=[NNAT.b])
        mtok = SMs if is_s else MREP
        op("dve", lambda e: e.tensor_tensor(out=SM[:P, 0:8], in0=PROJ[:P, 2048:2056], in1=PRM[:P, o_bg:o_bg + 8], op=ALU.add),
           reads=[PROJ.b, PRM.b], writes=[SM.b])
        op("dve", lambda e: e.scalar_tensor_tensor(out=SM[:P, 8:12], in0=SM[:P, 4:8], scalar=-1.0, in1=SM[:P, 4:8], op0=ALU.mult, op1=ALU.max),
           reads=[SM.b], writes=[SM.b])
        op("act", lambda e: e.activation(out=SM[:P, 12:16], in_=SM[:P, 8:12], func=AF.Exp, scale=-1.0), reads=[SM.b], writes=[SM.b])
        op("act", lambda e: e.activation(out=SM[:P, 12:16], in_=SM[:P, 12:16], func=AF.Ln, bias=1.0, scale=1.0),
           reads=[SM.b], writes=[SM.b])
        op("dve", lambda e: e.tensor_scalar_min(out=SM[:P, 16:20], in0=SM[:P, 4:8], scalar1=0.0), reads=[SM.b], writes=[SM.b])
        op("dve", lambda e: e.tensor_tensor(out=SM[:P, 16:20], in0=SM[:P, 16:20], in1=SM[:P, 12:16], op=ALU.subtract),
           reads=[SM.b], writes=[SM.b])
        op("pe", lambda e: e.matmul(PA[:P, 0:4], lhsT=tri, rhs=SM[:P, 16:20], start=True, stop=True),
           reads=[CST.b, SM.b], writes=[PA.b])
        op("act", lambda e: e.copy(out=SM[:P, 20:24], in_=PA[:P, 0:4]), reads=[PA.b], writes=[SM.b])
        op("dve", lambda e: e.tensor_tensor(out=SM[:P, 24:28], in0=SM[:P, 0:4], in1=SM[:P, 20:24], op=ALU.subtract),
           reads=[SM.b], writes=[SM.b])
        op("pe", lambda e: e.matmul(PA[:P, 8:12], lhsT=selE, rhs=SM[:P, 20:24], start=True, stop=True),
           reads=[CST.b, SM.b], writes=[PA.b])
        op("act", lambda e: e.copy(out=SM[:P, 28:32], in_=PA[:P, 8:12]), reads=[PA.b], writes=[SM.b])

        for hh in range(4):
            qs = PROJ[:P, hh * 128:(hh + 1) * 128]
            ks = PROJ[:P, 512 + hh * 128:512 + (hh + 1) * 128]
            vs = PROJ[:P, 1024 + hh * 128:1024 + (hh + 1) * 128]
            os_ = PROJ[:P, 1536 + hh * 128:1536 + (hh + 1) * 128]
            col = lambda c, hh=hh: SM[:P, c + hh:c + hh + 1]
            S1 = lambda c: SM[:P, c:c + 1]
            if is_s:
                kb.dma("sp", lambda q, hh=hh: q.dma_start(out=CN[:, :, :], in_=E["sC"][l, hh]), CN.b, writes=[CN.b])
                kb.dma("sp", lambda q, hh=hh: q.dma_start(out=NTH[:, :], in_=E["snT"][l, hh]), NTH.b, writes=[NTH.b])
                for j in range(4):
                    pt = PT[j % 2]
                    for jj in range(4):
                        b = j * 4 + jj
                        op("pe", lambda e, b=b, jj=jj, pt=pt: e.transpose(out=pt[:, jj * 128:(jj + 1) * 128], in_=CN[:, b, :],
                                                                       identity=C("ident")),
                           reads=[CN.b, CST.b], writes=[pt.b])
                    op("act", lambda e, j=j, pt=pt: e.copy(out=CTS[:, j * 4:(j + 1) * 4, 0:128],
                                                           in_=pt[:, :].rearrange("p (j c) -> p j c", j=4)),
                       reads=[pt.b], writes=[CTS.b])
                op("dve", lambda e: e.tensor_copy(out=CTS[:, :, 128:129], in_=NTH[:, :].unsqueeze(2)), reads=[NTH.b], writes=[CTS.b])
            op("dve", lambda e, hh=hh: e.tensor_scalar(out=DG[:P, :P], in0=C("ident", P, P), scalar1=col(24), scalar2=None,
                                                       op0=ALU.mult), reads=[SM.b, CST.b], writes=[DG.b])
            op("pe", lambda e: e.matmul(PB[:P, 0:P], lhsT=C("ones", P, P), rhs=DG[:P, :P], start=True, stop=True),
               reads=[DG.b, CST.b], writes=[PB.b])
            if is_s:
                op("dve", lambda e: e.tensor_tensor(out=DL[:P, :P], in0=PB[:P, 0:P], in1=C("negb_s", SP, SP), op=ALU.add),
                   reads=[PB.b, CST.b], writes=[DL.b])
                op("dve", lambda e: e.tensor_reduce(out=S1(32), in_=DL[:P, :P], axis=AX.X, op=ALU.max), reads=[DL.b], writes=[SM.b])
            else:
                op("dve", lambda e: e.tensor_reduce(out=S1(32), in_=PB[:P, 0:P], axis=AX.X, op=ALU.max), reads=[PB.b], writes=[SM.b])
            op("dve", lambda e, hh=hh: e.scalar_tensor_tensor(out=DL[:P, :P], in0=PB[:P, 0:P], scalar=col(20), in1=negm,
                                                              op0=ALU.add, op1=ALU.add),
               reads=[PB.b, SM.b, CST.b], writes=[DL.b])
            op("dve", lambda e: e.tensor_reduce(out=S1(33), in_=DL[:P, :P], axis=AX.X, op=ALU.max), reads=[DL.b], writes=[SM.b])
            op("dve", lambda e, hh=hh: e.tensor_tensor(out=S1(34), in0=col(20), in1=mtok[:P, hh:hh + 1], op=ALU.add),
               reads=[SM.b, mtok.b], writes=[SM.b])
            op("dve", lambda e: e.tensor_tensor(out=S1(35), in0=S1(34), in1=S1(33), op=ALU.max), reads=[SM.b], writes=[SM.b])
            op("dve", lambda e: e.tensor_scalar(out=S1(36), in0=S1(35), scalar1=-1.0, scalar2=None, op0=ALU.mult),
               reads=[SM.b], writes=[SM.b])
            op("act", lambda e: e.activation(out=WI[:P, :P], in_=DL[:P, :P], func=AF.Exp, bias=S1(36), scale=1.0),
               reads=[DL.b, SM.b], writes=[WI.b])
            op("act", lambda e: e.activation(out=S1(37), in_=S1(34), func=AF.Exp, bias=S1(36), scale=1.0), reads=[SM.b], writes=[SM.b])
            op("act", lambda e: e.activation(out=S1(38), in_=S1(36), func=AF.Exp), reads=[SM.b], writes=[SM.b])
            op("pe", lambda e: e.transpose(out=PC[:, 0:P], in_=qs, identity=C("ident", P, P)), reads=[PROJ.b, CST.b], writes=[PC.b])
            op("pe", lambda e: e.transpose(out=PC[:, 128:128 + P], in_=ks, identity=C("ident", P, P)), reads=[PROJ.b, CST.b], writes=[PC.b])
            op("act", lambda e: e.mul(out=QT[:, :P], in_=PC[:, 0:P], mul=128.0 ** -0.5), reads=[PC.b], writes=[QT.b])
            op("act", lambda e: e.copy(out=KT[:, :P], in_=PC[:, 128:128 + P]), reads=[PC.b], writes=[KT.b])
            op("pe", lambda e: e.matmul(PD[:P, 0:P], lhsT=QT[:, :P], rhs=KT[:, :P], start=True, stop=True),
               reads=[QT.b, KT.b], writes=[PD.b])
            op("dve", lambda e: e.tensor_tensor(out=AM[:P, :P], in0=WI[:P, :P], in1=PD[:P, 0:P], op=ALU.mult),
               reads=[WI.b, PD.b], writes=[AM.b])
            op("pe", lambda e: e.transpose(out=PB[:P, 128:128 + P], in_=AM[:P, :P], identity=C("ident", P, P)),
               reads=[AM.b, CST.b], writes=[PB.b])
            op("act", lambda e: e.copy(out=AT[:P, :P], in_=PB[:P, 128:128 + P]), reads=[PB.b], writes=[AT.b])
            op("pool", lambda e: e.tensor_copy(out=VX[:P, 0:128], in_=vs), reads=[PROJ.b], writes=[VX.b])
            op("pe", lambda e: e.matmul(PD[:P, 128:257], lhsT=AT[:P, :P], rhs=VX[:P, :], start=True, stop=True),
               reads=[AT.b, VX.b], writes=[PD.b])
            if is_s:
                op("pool", lambda e: e.memset(ZQ[:, :], 0.0), writes=[ZQ.b])
                for b in range(SB):
                    op("pool", lambda e, b=b: e.tensor_copy(out=ZQ[:, b * SP + b:(b + 1) * SP:SB], in_=QT[:, b:SP:SB]),
                       reads=[QT.b], writes=[ZQ.b])
                for b in range(SB):
                    op("pe", lambda e, b=b: e.matmul(PC[:P, 256:385], lhsT=ZQ[:, b * SP:(b + 1) * SP], rhs=CTS[:, b, :],
                                                     start=(b == 0), stop=(b == SB - 1)),
                       reads=[ZQ.b, CTS.b], writes=[PC.b] if b in (0, SB - 1) else [])
            else:
                op("pe", lambda e, hh=hh: e.matmul(PC[:P, 256:385], lhsT=QT[:, :P], rhs=CTX[:, hh, :], start=True, stop=True),
                   reads=[QT.b, CTX.b], writes=[PC.b])
            op("act", lambda e: e.activation(out=TOT[:P, :], in_=PC[:P, 256:385], func=AF.Identity, scale=S1(37)),
               reads=[PC.b, SM.b], writes=[TOT.b])
            op("dve", lambda e: e.tensor_tensor(out=TOT[:P, :], in0=TOT[:P, :], in1=PD[:P, 128:257], op=ALU.add),
               reads=[TOT.b, PD.b], writes=[TOT.b])
            op("dve", lambda e: e.scalar_tensor_tensor(out=S1(39), in0=TOT[:P, 128:129], scalar=-1.0, in1=TOT[:P, 128:129], op0=ALU.mult, op1=ALU.max),
               reads=[TOT.b], writes=[SM.b])
            op("dve", lambda e: e.tensor_tensor(out=S1(39), in0=S1(39), in1=S1(38), op=ALU.max), reads=[SM.b], writes=[SM.b])
            op("dve", lambda e: e.reciprocal(out=S1(40), in_=S1(39)), reads=[SM.b], writes=[SM.b])
            op("dve", lambda e: e.tensor_scalar(out=HN[:P, :], in0=TOT[:P, 0:128], scalar1=S1(40), scalar2=None, op0=ALU.mult),
               reads=[TOT.b, SM.b], writes=[HN.b])
            op("dve", lambda e: e.bn_stats(out=ST6[:P, 0, :], in_=HN[:P, :]), reads=[HN.b], writes=[ST6.b])
            op("dve", lambda e: e.bn_aggr(out=SM[:P, 41:43], in_=ST6[:P, 0, :]), reads=[ST6.b], writes=[SM.b])
            op("act", lambda e: e.activation(out=S1(43), in_=S1(42), func=AF.Ln, bias=EPSB[:P, 0:1], scale=1.0), reads=[SM.b, EPSB.b], writes=[SM.b])
            op("act", lambda e: e.activation(out=S1(44), in_=S1(43), func=AF.Exp, scale=-0.5), reads=[SM.b], writes=[SM.b])
            op("dve", lambda e: e.tensor_scalar(out=HN[:P, :], in0=HN[:P, :], scalar1=S1(41), scalar2=S1(44),
                                                op0=ALU.subtract, op1=ALU.mult), reads=[HN.b, SM.b], writes=[HN.b])
            op("dve", lambda e, hh=hh: e.tensor_tensor(out=HN[:P, :], in0=HN[:P, :],
                                                       in1=PRM[:P, o_mh + hh * 128:o_mh + (hh + 1) * 128], op=ALU.mult),
               reads=[HN.b, PRM.b], writes=[HN.b])
            op("act", lambda e: e.activation(out=SG[:P, :], in_=os_, func=AF.Exp, scale=-1.0), reads=[PROJ.b], writes=[SG.b])
            op("dve", lambda e: e.tensor_scalar_add(out=SG[:P, :], in0=SG[:P, :], scalar1=1.0), reads=[SG.b], writes=[SG.b])
            op("dve", lambda e: e.reciprocal(out=SG[:P, :], in_=SG[:P, :]), reads=[SG.b], writes=[SG.b])
            op("dve", lambda e, hh=hh: e.tensor_tensor(out=Y[:P, hh * 128:(hh + 1) * 128], in0=HN[:P, :], in1=SG[:P, :], op=ALU.mult),
               reads=[HN.b, SG.b], writes=[Y.b])
            op("dve", lambda e, hh=hh: e.tensor_tensor(out=S1(45), in0=mtok[:P, hh:hh + 1], in1=S1(32), op=ALU.max),
               reads=[SM.b, mtok.b], writes=[SM.b])
            op("dve", lambda e, hh=hh: e.tensor_tensor(out=S1(45), in0=S1(45), in1=col(28), op=ALU.add), reads=[SM.b], writes=[SM.b])
            op("dve", lambda e, hh=hh: e.tensor_tensor(out=S1(46), in0=col(28), in1=S1(45), op=ALU.subtract), reads=[SM.b], writes=[SM.b])
            op("act", lambda e, hh=hh: e.activation(out=S1(47), in_=col(24), func=AF.Exp, bias=S1(46), scale=1.0),
               reads=[SM.b], writes=[SM.b])
            op("act", lambda e, hh=hh: e.activation(out=S1(48), in_=mtok[:P, hh:hh + 1], func=AF.Exp, bias=S1(46), scale=1.0),
               reads=[SM.b, mtok.b], writes=[SM.b])
            if not is_s:
                op("dve", lambda e: e.tensor_scalar(out=WV[:P, :], in0=VX[:P, :], scalar1=S1(47), scalar2=None, op0=ALU.mult),
                   reads=[VX.b, SM.b], writes=[WV.b])
                op("pe", lambda e: e.matmul(PB[:, 256:385], lhsT=ks, rhs=WV[:P, :], start=True, stop=True),
                   reads=[PROJ.b, WV.b], writes=[PB.b])
                op("dve", lambda e, hh=hh: e.scalar_tensor_tensor(out=CTX[:, hh, :], in0=CTX[:, hh, :], scalar=S1(48), in1=PB[:, 256:385],
                                                                  op0=ALU.mult, op1=ALU.add),
                   reads=[CTX.b, SM.b, PB.b], writes=[CTX.b])
                op("dve", lambda e, hh=hh: e.tensor_copy(out=MREP[:, hh:hh + 1], in_=S1(45)), reads=[SM.b], writes=[MREP.b])
                if t == NT - 1:
                    op("pe", lambda e, hh=hh: e.transpose(out=PA[:, 128:256], in_=CTX[:, hh, 0:128], identity=C("ident")),
                       reads=[CTX.b, CST.b], writes=[PA.b])
                    op("act", lambda e: e.copy(out=OUTC[:, :], in_=PA[:, 128:256]), reads=[PA.b], writes=[OUTC.b])
                    out_dma(E["o_pC"][l, hh], OUTC[:, :], [OUTC.b])
                    out_dma(E["o_pn"][l, hh].rearrange("(k o) -> k o", o=1), CTX[:, hh, 128:129], [CTX.b])
                    if hh == 3:
                        out_dma(E["o_pm"][l:l + 1, :], MREP[0:1, :], [MREP.b])
            else:
                op("dve", lambda e: e.tensor_scalar(out=WCB[:P, :], in0=C("onehotB", SP), scalar1=S1(47), scalar2=None, op0=ALU.mult),
                   reads=[SM.b, CST.b], writes=[WCB.b])
                op("dve", lambda e: e.tensor_tensor(out=RA[:P, :, :], in0=vs.unsqueeze(1).broadcast_to([P, SB, 128]),
                                                    in1=WCB[:P, :].unsqueeze(2).broadcast_to([P, SB, 128]), op=ALU.mult),
                   reads=[PROJ.b, WCB.b], writes=[RA.b])
                op("dve", lambda e: e.tensor_scalar(out=DECD[:P, :], in0=C("onehot0", SP), scalar1=S1(48), scalar2=None, op0=ALU.mult),
                   reads=[SM.b, CST.b], writes=[DECD.b])
                op("pe", lambda e: e.matmul(PA[:, 16:32], lhsT=C("ones", SP, 128), rhs=DECD[:P, :], start=True, stop=True),
                   reads=[DECD.b, CST.b], writes=[PA.b])
                op("act", lambda e: e.copy(out=DECR[:, :], in_=PA[:, 16:32]), reads=[PA.b], writes=[DECR.b])
                for b in range(SB):
                    pq = [PA, PB, PC, PD][b % 4]
                    op("pe", lambda e, b=b, pq=pq: e.matmul(pq[:, 384:512], lhsT=RA[:P, b, :], rhs=ks, start=True, stop=True),
                       reads=[RA.b, PROJ.b], writes=[pq.b])
                    op("dve", lambda e, b=b, pq=pq: e.scalar_tensor_tensor(out=CN[:, b, :], in0=CN[:, b, :], scalar=DECR[:, b:b + 1],
                                                                           in1=pq[:, 384:512], op0=ALU.mult, op1=ALU.add),
                       reads=[CN.b, DECR.b, pq.b], writes=[CN.b])
                out_dma(E["o_nC"][l, :, hh].rearrange("b v k -> v b k"), CN[:, :, :], [CN.b])
                op("pe", lambda e: e.matmul(PA[:SB, 32:160], lhsT=WCB[:P, :], rhs=ks, start=True, stop=True),
                   reads=[WCB.b, PROJ.b], writes=[PA.b])
                op("dve", lambda e, hh=hh: e.scalar_tensor_tensor(out=NNAT[:, hh, :], in0=NNAT[:, hh, :], scalar=SM[:SB, 48:49],
                                                                  in1=PA[:SB, 32:160], op0=ALU.mult, op1=ALU.add),
                   reads=[NNAT.b, SM.b, PA.b], writes=[NNAT.b])
                op("dve", lambda e, hh=hh: e.tensor_copy(out=MSO[:, hh:hh + 1], in_=SM[:SB, 45:46]), reads=[SM.b], writes=[MSO.b])
                if hh == 3:
                    out_dma(E["o_nn"][l], NNAT[:, :, :], [NNAT.b])
                    out_dma(E["o_nm"][l], MSO[:, :], [MSO.b])

        vsv = PROJ[:P, 2312:2568].rearrange("p (g d) -> p g d", g=4)
        op("dve", lambda e: e.tensor_reduce(out=SM[:P, 50:54], in_=vsv, axis=AX.X, op=ALU.add), reads=[PROJ.b], writes=[SM.b])
        op("dve", lambda e: e.tensor_scalar(out=SM[:P, 50:54], in0=SM[:P, 50:54], scalar1=1.0 / 64, scalar2=None, op0=ALU.mult),
           reads=[SM.b], writes=[SM.b])
        op("dve", lambda e: e.tensor_tensor(out=VN[:P, :].rearrange("p (g d) -> p g d", g=4), in0=vsv,
                                            in1=SM[:P, 50:54].unsqueeze(2).broadcast_to([P, 4, 64]), op=ALU.subtract),
           reads=[PROJ.b, SM.b], writes=[VN.b])
        op("pool", lambda e: e.tensor_tensor(out=VTMP[:P, :], in0=VN[:P, :], in1=VN[:P, :], op=ALU.mult), reads=[VN.b], writes=[VTMP.b])
        op("dve", lambda e: e.tensor_reduce(out=SM[:P, 54:58], in_=VTMP[:P, :].rearrange("p (g d) -> p g d", g=4), axis=AX.X, op=ALU.add),
           reads=[VTMP.b], writes=[SM.b])
        op("act", lambda e: e.activation(out=SM[:P, 54:58], in_=SM[:P, 54:58], func=AF.Ln, bias=EPSB[:P, 0:1], scale=1.0 / 64),
           reads=[SM.b, EPSB.b], writes=[SM.b])
        op("act", lambda e: e.activation(out=SM[:P, 58:62], in_=SM[:P, 54:58], func=AF.Exp, scale=-0.5), reads=[SM.b], writes=[SM.b])
        op("dve", lambda e: e.tensor_tensor(out=VN[:P, :].rearrange("p (g d) -> p g d", g=4), in0=VN[:P, :].rearrange("p (g d) -> p g d", g=4),
                                            in1=SM[:P, 58:62].unsqueeze(2).broadcast_to([P, 4, 64]), op=ALU.mult),
           reads=[VN.b, SM.b], writes=[VN.b])
        op("pool", lambda e: e.tensor_tensor(out=VN[:P, :], in0=VN[:P, :], in1=PRM[:P, o_sg:o_sg + 256], op=ALU.mult),
           reads=[VN.b, PRM.b], writes=[VN.b])
        op("pool", lambda e: e.tensor_tensor(out=VN[:P, :], in0=VN[:P, :], in1=PRM[:P, o_sb:o_sb + 256], op=ALU.add),
           reads=[VN.b, PRM.b], writes=[VN.b])
        wsl = WSs if is_s else WS
        bsl = BSs if is_s else BS
        for g in range(4):
            op("pe", lambda e, g=g: e.matmul(PC[:P, g * 64:(g + 1) * 64], lhsT=wsl[:P, g, :P], rhs=VN[:P, g * 64:(g + 1) * 64],
                                             start=True, stop=True), reads=[wsl.b, VN.b], writes=[PC.b])
        for g in range(4):
            op("dve", lambda e, g=g: e.scalar_tensor_tensor(out=Y[:P, 512 + g * 64:512 + (g + 1) * 64], in0=PC[:P, g * 64:(g + 1) * 64],
                                                            scalar=bsl[:P, g:g + 1], in1=PROJ[:P, 2056 + g * 64:2056 + (g + 1) * 64],
                                                            op0=ALU.add, op1=ALU.mult),
               reads=[PC.b, bsl.b, PROJ.b], writes=[Y.b])
        if is_s:
            for tq in range(ST):
                out_dma(E["o_nv"][l][:, tq, :], VN[tq * SB:(tq + 1) * SB, :], [VN.b])

        pin = lambda g: PROJ[:P, 2568 + g * 64:2568 + (g + 1) * 64]
        if is_s:
            kb.dma("sp", lambda q: q.dma_start(out=SPA[:, :], in_=E["spA"][l]), SPA.b, writes=[SPA.b])
            kb.dma("sp", lambda q: q.dma_start(out=SPB[:112, :], in_=E["spB"][l]), SPB.b, writes=[SPB.b])
            for g in range(4):
                op("pe", lambda e, g=g: e.matmul(PA[:64, g * 128:g * 128 + P], lhsT=SPA[:, g * 64:(g + 1) * 64], rhs=Cg("bsA", g, 128, SP, SP),
                                                 start=True, stop=False), reads=[SPA.b, CST.b], writes=[PA.b])
                op("pe", lambda e, g=g: e.matmul(PA[:64, g * 128:g * 128 + P], lhsT=SPB[:112, g * 64:(g + 1) * 64], rhs=Cg("bsB", g, 112, SP, SP),
                                                 start=False, stop=False), reads=[SPB.b, CST.b], writes=[])
                op("pe", lambda e, g=g: e.matmul(PA[:64, g * 128:g * 128 + P], lhsT=pin(g), rhs=Cg("bsC", g, SP, SP, SP),
                                                 start=False, stop=True), reads=[PROJ.b, CST.b], writes=[PA.b])
            npv = E["o_np"][l].rearrange("b r c -> r b c")
            for r in range(4):
                out_dma(npv[r], SPA[64 + r * SB:64 + (r + 1) * SB, :], [SPA.b])
            for r in range(7):
                out_dma(npv[4 + r], SPB[r * SB:(r + 1) * SB, :], [SPB.b])
            for r in range(4):
                out_dma(npv[11 + r], PROJ[r * SB:(r + 1) * SB, 2568:2824], [PROJ.b])
        else:
            for g in range(4):
                band = Cg("bandc0" if t == 0 else "bandc", g, 128, 128, 128)
                op("pe", lambda e, g=g, band=band: e.matmul(PA[:64, g * 128:(g + 1) * 128], lhsT=pin(g), rhs=band, start=True, stop=(t == 0)),
                   reads=[PROJ.b, CST.b], writes=[PA.b])
                if t > 0:
                    op("pe", lambda e, g=g: e.matmul(PA[:64, g * 128:(g + 1) * 128], lhsT=PREV[:, g * 64:(g + 1) * 64],
                                                     rhs=Cg("bandp", g, 128, 128, 128), start=False, stop=True),
                       reads=[PREV.b, CST.b], writes=[PA.b])
            if t < NT - 1:
                op("pool", lambda e: e.tensor_copy(out=PREV[:, :], in_=PROJ[:, 2568:2824]), reads=[PROJ.b], writes=[PREV.b])
            else:
                out_dma(E["o_pp"][l], PROJ[113:128, 2568:2824], [PROJ.b])
        op("act", lambda e: e.copy(out=PTT[:, :, :P], in_=PA[:64, :].rearrange("p (g c) -> p g c", g=4)[:, :, :P]),
           reads=[PA.b], writes=[PTT.b])
        for g in range(4):
            op("pe", lambda e, g=g: e.matmul(PB[:P, g * 64:(g + 1) * 64], lhsT=PTT[:, g, :P], rhs=WP[:, g, :], start=True, stop=True),
               reads=[PTT.b, WP.b], writes=[PB.b])
        op("dve", lambda e: e.tensor_tensor(out=Y[:P, 768:1024], in0=PB[:P, 0:256], in1=PRM[:P, o_ps:o_ps + 256], op=ALU.mult),
           reads=[PB.b, PRM.b], writes=[Y.b])

        def make_C(t, P, Y, H, HT, ada):
            items = []

            def iop(e, fn, reads=(), writes=()):
                items.append(("op", e, _bind(fn), None, tuple(reads), tuple(writes)))

            def idma(e, fn, owner, reads=(), writes=()):
                items.append(("dma", e, _bind(fn), owner, tuple(reads), tuple(writes)))
            _transpose8(kb, E, Y, HT, PT, P, op=iop)
            for (c0, w) in mk_chunks(D):
                stream_mm(w_o_v, c0, w, HT, P,
                          lambda pm, i, c0=c0, w=w: iop("dve", lambda e: e.tensor_tensor(out=H[:P, c0:c0 + w], in0=pm[:P, 0:w],
                                                                                         in1=ada[:P, 2048 + c0:2048 + c0 + w], op=ALU.mult),
                                                        reads=[pm.b, ada.b], writes=[H.b]), op=iop, dma=idma)
            _resid_ln(kb, X, XB[t], t, P, H, SMC, ST6C, PRM, o_l1g, o_l1b, EPSB, op=iop)
            return items

        cur[0] = None
        if inter:
            run_items(inter)
        pendC[0] = make_C(t, P, Y, H, HT, ada)
        if ti == len(tiles) - 1:
            run_items(pendC[0])
            pendC[0] = None
            if conv:
                run_items(conv)


def _resid_ln(kb, X, xb, t, P, Z, SM, ST6, PRM, og, ob, EPSB, op=None):
    op = kb.op if op is None else op
    xt = X[:P, t, :]
    op("dve", lambda e: e.scalar_tensor_tensor(out=Z[:P, :], in0=xt, scalar=ALPHA, in1=Z[:P, :], op0=ALU.mult, op1=ALU.add),
       reads=[xb, Z.b], writes=[Z.b])
    op("dve", lambda e: e.bn_stats(out=ST6[:P, 0, :], in_=Z[:P, 0:512]), reads=[Z.b], writes=[ST6.b])
    op("dve", lambda e: e.bn_stats(out=ST6[:P, 1, :], in_=Z[:P, 512:1024]), reads=[Z.b], writes=[ST6.b])
    op("dve", lambda e: e.bn_aggr(out=SM[:P, 41:43], in_=ST6[:P, :, :].rearrange("p a b -> p (a b)")), reads=[ST6.b], writes=[SM.b])
    op("act", lambda e: e.activation(out=SM[:P, 43:44], in_=SM[:P, 42:43], func=AF.Ln, bias=EPSB[:P, 0:1], scale=1.0), reads=[SM.b, EPSB.b], writes=[SM.b])
    op("act", lambda e: e.activation(out=SM[:P, 44:45], in_=SM[:P, 43:44], func=AF.Exp, scale=-0.5), reads=[SM.b], writes=[SM.b])
    op("dve", lambda e: e.tensor_scalar(out=Z[:P, :], in0=Z[:P, :], scalar1=SM[:P, 41:42], scalar2=SM[:P, 44:45],
                                        op0=ALU.subtract, op1=ALU.mult), reads=[Z.b, SM.b], writes=[Z.b])
    op("pool", lambda e: e.tensor_tensor(out=Z[:P, :], in0=Z[:P, :], in1=PRM[:P, og:og + D], op=ALU.mult), reads=[Z.b, PRM.b], writes=[Z.b])
    op("dve", lambda e: e.tensor_tensor(out=xt, in0=Z[:P, :], in1=PRM[:P, ob:ob + D], op=ALU.add), reads=[Z.b, PRM.b], writes=[xb])


def _phase2(nc, kb, l, E):
    C, CST, X, XB = E["C"], E["CST"], E["X"], E["XB"]
    ADA = Tn(kb, "ADA2", [128, 3072]); ADAs = ADA
    WCH = [Tn(kb, "WCHb%d" % i, [128, 8, 128], dma=True) for i in range(2)]
    badac = [Tn(kb, "badab%d" % i, [128, 256], dma=True) for i in range(2)]
    Hs = [Tn(kb, "H2_%d" % i, [128, D], dma=True) for i in range(2)]
    HT = Tn(kb, "H2T", [128, 8, 128])
    PRM = Tn(kb, "PRM2", [128, 2 * D], dma=True)
    KTS = Tn(kb, "KTS", [128, 16, 128], dma=True)
    S0 = Tn(kb, "S0", [128, 2048]); S1_ = Tn(kb, "S1", [128, 2048]); S2 = Tn(kb, "S2", [128, 256])
    TOPS = Tn(kb, "TOPS", [128, 16, 16]); IDXU = Tn(kb, "IDXU", [128, 16, 16], U32); IDXF = Tn(kb, "IDXF", [128, 16, 16])
    CV = Tn(kb, "CV", [128, 8, 16]); CPOS = Tn(kb, "CPOS", [128, 8, 16], U32)
    PAU = Tn(kb, "PAU", [128, 8, 16], U32); PBU = Tn(kb, "PBU", [128, 8, 16], U32)
    PAF = Tn(kb, "PAF", [128, 8, 16]); PBF = Tn(kb, "PBF", [128, 8, 16])
    I1 = Tn(kb, "I1", [128, 128]); I2 = Tn(kb, "I2", [128, 128])
    IDXs = [Tn(kb, "IDX%d" % i, [128, 128], I32) for i in range(2)]
    GATEs = [Tn(kb, "GATE%d" % i, [128, 128]) for i in range(2)]
    ACTV = Tn(kb, "ACTV", [128, 128]); COEF = Tn(kb, "COEF", [128, 128])
    SMf = Tn(kb, "SM2f", [128, 16]); SM = Tn(kb, "SM2", [128, 64]); ST6 = Tn(kb, "ST62", [128, 2, 6])
    NB = NBUF
    UB = [Tn(kb, "UB%d" % i, [128, 2 * D], BF16, dma=True) for i in range(NB)]
    WCAB = Tn(kb, "WCAB", [128, 8, 256], dma=True)
    ACTB = [kb.buf("actv%d" % i) for i in range(NB)]
    COEFB = [kb.buf("coef%d" % i) for i in range(NB)]
    COEF2 = Tn(kb, "COEF2", [128, 128])

    class _View:
        def __init__(self, ap, b):
            self.t = ap
            self.b = b

        def __getitem__(self, k):
            return self.t[k]
    WCA = [WCAB]
    PT = [Tn(kb, "PTb%d" % i, [128, 512], psum=True) for i in range(2)]
    PQ = [Tn(kb, "PQ%d" % i, [128, 512], psum=True) for i in range(2)]
    PM = PQ
    ACCP = [Tn(kb, "ACCP%d" % i, [128, 512], psum=True) for i in range(2)]
    JUNK = Tn(kb, "JUNK", [128, D])
    ACC = JUNK
    DGB = [Tn(kb, "DGB%d" % i, [128, 128], BF16) for i in range(3)]
    PS = [Tn(kb, "PS%d" % i, [128, 512], psum=True) for i in range(2)]

    kb.dma("sp", lambda q: q.dma_start(out=PRM[:, 0:D], in_=E["ln2g"][l]), PRM.b, writes=[PRM.b])
    kb.dma("sp", lambda q: q.dma_start(out=PRM[:, D:2 * D], in_=E["ln2b"][l]), PRM.b, writes=[PRM.b])
    kb.dma("sp", lambda q: q.dma_start(out=KTS[:, :, :], in_=E["keysT"][l]), KTS.b, writes=[KTS.b])
    wpq_v = E["w_pq"][l]

    tab = E["puvb"][l]
    PUVB = E["PUVBT"][l]
    wctr = [0]

    def make_front(t, sset):
        items = []

        def op(e, fn, reads=(), writes=()):
            items.append(("op", e, _bind(fn), None, tuple(reads), tuple(writes)))

        def dma(e, fn, owner, reads=(), writes=()):
            items.append(("dma", e, _bind(fn), owner, tuple(reads), tuple(writes)))
        is_s = (t == NT)
        P = SP if is_s else 128
        ada = ADAs if is_s else ADA
        H = Hs[sset]; IDX = IDXs[sset]; GATE = GATEs[sset]
        QT = S0
        xt = X[:P, t, :]
        op("dve", lambda e: e.tensor_tensor(out=H[:P, :], in0=xt, in1=ada[:P, 1024:2048], op=ALU.mult), reads=[XB[t], ada.b], writes=[H.b])
        op("dve", lambda e: e.tensor_tensor(out=H[:P, :], in0=H[:P, :], in1=ada[:P, 0:1024], op=ALU.add), reads=[H.b, ada.b], writes=[H.b])
        _transpose8(kb, E, H, HT, PT, P, op=op)
        def wdma(c):
            i = c % 2
            dma("sp", lambda q: q.dma_start(out=WCH[i][:, :, :], in_=wpq_v[c].rearrange("p (k j) -> p k j", k=8)), WCH[i].b, writes=[WCH[i].b])
        wdma(0)
        for c in range(16):
            i = c % 2
            j = c % 2
            if c + 1 < 16:
                wdma(c + 1)
            for k in range(8):
                op("pe", lambda e: e.matmul(PQ[j][:, 0:P], lhsT=WCH[i][:, k, :], rhs=HT[:, k, :P], start=(k == 0), stop=(k == 7)),
                   reads=[WCH[i].b, HT.b], writes=[PQ[j].b] if k in (0, 7) else [])
            op("act", lambda e: e.copy(out=QT[:, c * 128:c * 128 + P], in_=PQ[j][:, 0:P]), reads=[PQ[j].b], writes=[QT.b])
        for c4 in range(4):
            ps = PS[c4 % 2]
            for j in range(4):
                c = c4 * 4 + j
                op("pe", lambda e: e.matmul(ps[:P, j * 128:(j + 1) * 128], lhsT=QT[:, c * 128:c * 128 + P], rhs=KTS[:, c, :], start=True, stop=True),
                   reads=[QT.b, KTS.b], writes=[ps.b])
            op("act", lambda e: e.copy(out=S1_[:P, c4 * 512:(c4 + 1) * 512], in_=ps[:P, :]), reads=[ps.b], writes=[S1_.b])
        for c in range(16):
            sc = S1_[:P, c * 128:(c + 1) * 128]
            wk = S2[:P, 0:128]
            op("dve", lambda e: e.max(out=TOPS[:P, c, 0:8], in_=sc), reads=[S1_.b], writes=[TOPS.b])
            op("dve", lambda e: e.max_index(out=IDXU[:P, c, 0:8], in_max=TOPS[:P, c, 0:8], in_values=sc), reads=[S1_.b, TOPS.b], writes=[IDXU.b])
            op("dve", lambda e: e.match_replace(out=wk, in_to_replace=TOPS[:P, c, 0:8], in_values=sc, imm_value=NEG), reads=[S1_.b, TOPS.b], writes=[S2.b])
            op("dve", lambda e: e.max(out=TOPS[:P, c, 8:16], in_=wk), reads=[S2.b], writes=[TOPS.b])
            op("dve", lambda e: e.max_index(out=IDXU[:P, c, 8:16], in_max=TOPS[:P, c, 8:16], in_values=wk), reads=[S2.b, TOPS.b], writes=[IDXU.b])
        op("dve", lambda e: e.tensor_copy(out=IDXF[:P, :, :], in_=IDXU[:P, :, :]), reads=[IDXU.b], writes=[IDXF.b])
        tv = TOPS[:P, :, :].rearrange("p (h two) k -> p h two k", two=2)
        CAND = S0
        op("dve", lambda e: e.tensor_tensor(out=CAND[:P, :].rearrange("p (h a b) -> p h a b", h=8, a=16),
                                            in0=tv[:, :, 0, :].unsqueeze(3).broadcast_to([P, 8, 16, 16]),
                                            in1=tv[:, :, 1, :].unsqueeze(2).broadcast_to([P, 8, 16, 16]), op=ALU.add),
           reads=[TOPS.b], writes=[S0.b])
        for h in range(8):
            cd = CAND[:P, h * 256:(h + 1) * 256]
            wk = S2[:P, 0:256]
            op("dve", lambda e: e.max(out=CV[:P, h, 0:8], in_=cd), reads=[S0.b], writes=[CV.b])
            op("dve", lambda e: e.max_index(out=CPOS[:P, h, 0:8], in_max=CV[:P, h, 0:8], in_values=cd), reads=[S0.b, CV.b], writes=[CPOS.b])
            op("dve", lambda e: e.match_replace(out=wk, in_to_replace=CV[:P, h, 0:8], in_values=cd, imm_value=NEG), reads=[S0.b, CV.b], writes=[S2.b])
            op("dve", lambda e: e.max(out=CV[:P, h, 8:16], in_=wk), reads=[S2.b], writes=[CV.b])
            op("dve", lambda e: e.max_index(out=CPOS[:P, h, 8:16], in_max=CV[:P, h, 8:16], in_values=wk), reads=[S2.b, CV.b], writes=[CPOS.b])
        op("dve", lambda e: e.tensor_single_scalar(out=PAU[:P, :, :], in_=CPOS[:P, :, :], scalar=4, op=ALU.logical_shift_right), reads=[CPOS.b], writes=[PAU.b])
        op("dve", lambda e: e.tensor_single_scalar(out=PBU[:P, :, :], in_=CPOS[:P, :, :], scalar=15, op=ALU.bitwise_and), reads=[CPOS.b], writes=[PBU.b])
        op("dve", lambda e: e.tensor_copy(out=PAF[:P, :, :], in_=PAU[:P, :, :]), reads=[PAU.b], writes=[PAF.b])
        op("dve", lambda e: e.tensor_copy(out=PBF[:P, :, :], in_=PBU[:P, :, :]), reads=[PBU.b], writes=[PBF.b])
        iv = IDXF[:P, :, :].rearrange("p (h two) k -> p h two k", two=2)
        io16 = C("iota16", P).unsqueeze(1).unsqueeze(1).broadcast_to([P, 8, 16, 16])
        for (pf, half, dst) in [(PAF, 0, I1), (PBF, 1, I2)]:
            eq = S1_[:P, :].rearrange("p (h k a) -> p h k a", h=8, k=16)
            op("dve", lambda e: e.tensor_tensor(out=eq, in0=pf[:P, :, :].unsqueeze(3).broadcast_to([P, 8, 16, 16]), in1=io16, op=ALU.is_equal),
               reads=[pf.b, CST.b], writes=[S1_.b])
            op("dve", lambda e: e.tensor_tensor(out=eq, in0=eq, in1=iv[:, :, half, :].unsqueeze(2).broadcast_to([P, 8, 16, 16]), op=ALU.mult),
               reads=[S1_.b, IDXF.b], writes=[S1_.b])
            op("dve", lambda e: e.tensor_reduce(out=dst[:P, :].rearrange("p (h k) -> p h k", h=8), in_=eq, axis=AX.X, op=ALU.add),
               reads=[S1_.b], writes=[dst.b])
        op("dve", lambda e: e.scalar_tensor_tensor(out=I1[:P, :], in0=I1[:P, :], scalar=128.0, in1=I2[:P, :], op0=ALU.mult, op1=ALU.add),
           reads=[I1.b, I2.b], writes=[I1.b])
        op("dve", lambda e: e.tensor_copy(out=IDX[:P, :], in_=I1[:P, :]), reads=[I1.b], writes=[IDX.b])
        gv = GATE[:P, :].rearrange("p (h k) -> p h k", h=8)
        op("dve", lambda e: e.tensor_tensor(out=gv, in0=CV[:P, :, :], in1=CV[:P, :, 0:1].broadcast_to([P, 8, 16]), op=ALU.subtract),
           reads=[CV.b], writes=[GATE.b])
        op("act", lambda e: e.activation(out=GATE[:P, :], in_=GATE[:P, :], func=AF.Exp), reads=[GATE.b], writes=[GATE.b])
        op("dve", lambda e: e.tensor_reduce(out=SMf[:P, 0:8], in_=gv, axis=AX.X, op=ALU.add), reads=[GATE.b], writes=[SMf.b])
        op("dve", lambda e: e.reciprocal(out=SMf[:P, 8:16], in_=SMf[:P, 0:8]), reads=[SMf.b], writes=[SMf.b])
        op("dve", lambda e: e.tensor_tensor(out=gv, in0=gv, in1=SMf[:P, 8:16].unsqueeze(2).broadcast_to([P, 8, 16]), op=ALU.mult),
           reads=[GATE.b, SMf.b], writes=[GATE.b])
        return items

    def run_items(items, n=None):
        n = len(items) if n is None else min(n, len(items))
        for _ in range(n):
            kind, e, fn, owner, reads, writes = items.pop(0)
            if kind == "op":
                kb.op(e, fn, reads=reads, writes=writes, bound=True)
            else:
                kb.dma(e, fn, owner, reads=reads, writes=writes, bound=True)

    def back(t, sset, nxt):
        op = kb.op
        is_s = (t == NT)
        P = SP if is_s else 128
        ada = ADAs if is_s else ADA
        H = Hs[sset]; IDX = IDXs[sset]; GATE = GATEs[sset]
        per = 0 if not nxt else (len(nxt) + 119) // 120

        def axpy(s):
            b = s % NB
            dg = DGB[s % 3]
            op("act", lambda e: e.activation(out=COEF2[:P, s:s + 1], in_=COEF[:P, s:s + 1], func=AF.Identity, scale=GATE[:P, s:s + 1]),
               reads=[COEFB[b], GATE.b], writes=[COEF2.b])
            op("act", lambda e: e.activation(out=dg[:P, :P], in_=C("ident", P, P), func=AF.Identity, scale=COEF2[:P, s:s + 1]),
               reads=[COEF2.b, CST.b], writes=[dg.b])
            for hf in range(2):
                op("pe", lambda e: e.matmul(ACCP[hf][:P, :], lhsT=dg[:P, :P], rhs=UB[b][:P, D + hf * 512:D + (hf + 1) * 512], start=(s == 0), stop=(s == 127)),
                   reads=[dg.b, UB[b].b], writes=[ACCP[hf].b] if s in (0, 127) else [])

        for s_ in range(128):
            b = s_ % NB
            kb.dma("pool", lambda q: q.indirect_dma_start(out=UB[b][:P, :], out_offset=None, in_=tab,
                                                          in_offset=bass.IndirectOffsetOnAxis(ap=IDX[:P, s_:s_ + 1], axis=0)),
                   UB[b].b, reads=[IDX.b, PUVB], writes=[UB[b].b])
            op("dve", lambda e: e.scalar_tensor_tensor(out=JUNK[:P, :], in0=UB[b][:P, 0:D], scalar=1.0, in1=H[:P, :],
                                                       op0=ALU.mult, op1=ALU.mult, accum_out=ACTV[:P, s_:s_ + 1]),
               reads=[UB[b].b, H.b], writes=[JUNK.b, ACTB[b]])
            op("act", lambda e: e.activation(out=COEF[:P, s_:s_ + 1], in_=ACTV[:P, s_:s_ + 1], func=AF.Gelu), reads=[ACTB[b]], writes=[COEFB[b]])
            if s_ >= 1:
                axpy(s_ - 1)
            if nxt:
                run_items(nxt, per)
        axpy(127)
        if nxt:
            run_items(nxt)
        for hf in range(2):
            op("dve", lambda e: e.tensor_tensor(out=ACC[:P, hf * 512:(hf + 1) * 512], in0=ACCP[hf][:P, :], in1=ada[:P, 2048 + hf * 512:2048 + (hf + 1) * 512],
                                                op=ALU.mult), reads=[ACCP[hf].b, ada.b], writes=[ACC.b])
        _resid_ln(kb, X, XB[t], t, P, ACC, SM, ST6, PRM, 0, D, E["EPSB"])

    _ada(nc, kb, l, E, ADA, 128, E["cp"], 3072, WCA, PM, Hs[0], HT, PT, badac)
    run_items(make_front(0, 0))
    for t in range(NT + 1):
        nxt = make_front(t + 1, (t + 1) % 2) if t + 1 < NT else None
        back(t, t % 2, nxt)
        if t + 1 == NT:
            _ada(nc, kb, l, E, ADAs, SP, E["cs"], 3072, WCA, PM, Hs[NT % 2], HT, PT, badac)
            run_items(make_front(NT, NT % 2))


_CACHE = {}


def _chunked(w, cw):
    L, K, n = w.shape
    nch = (n + cw - 1) // cw
    wp = np.zeros((L, K, nch * cw), np.float32)
    wp[:, :, :n] = w
    wp = wp.reshape(L, 8, 128, nch, cw).transpose(0, 3, 2, 1, 4)
    return np.ascontiguousarray(wp.reshape(L, nch, 128, 8 * cw))


def _rep(a, P=128):
    return np.ascontiguousarray(np.broadcast_to(a[:, None, :], (a.shape[0], P, a.shape[1])))


def make_in_maps(inp, cpack):
    f = lambda a: np.ascontiguousarray(np.asarray(a, dtype=np.float32))
    shared = {
        "w_ada": _chunked(f(inp["w_ada"]), WCW), "b_ada": _rep(f(inp["b_ada"])), "w_in": _chunked(f(inp["w_in"]), WCW),
        "b_gate": _rep(f(inp["b_gate"])),
        "mh_g": _rep(f(inp["mh_g"])), "sgu_g": _rep(f(inp["sgu_g"])), "sgu_b": _rep(f(inp["sgu_b"])),
        "pscale": _rep(f(inp["pool_scale"])),
        "w_sT": f(np.asarray(inp["w_s"]).transpose(0, 3, 1, 2)),
        "b_sT": f(np.asarray(inp["b_s"]).transpose(0, 2, 1)),
        "w_pool": f(np.asarray(inp["w_pool"]).transpose(0, 2, 1, 3)),
        "w_o": _chunked(f(inp["w_o"]), WCW), "ln1g": _rep(f(inp["ln1_g"])), "ln1b": _rep(f(inp["ln1_b"])),
        "ln2g": _rep(f(inp["ln2_g"])), "ln2b": _rep(f(inp["ln2_b"])), "w_pq": _chunked(f(inp["w_pq"]), 128),
        "keysT": f(np.asarray(inp["peer_keys"]).transpose(0, 4, 1, 2, 3).reshape(DEPTH, 128, 16, 128)),
        "cst": cpack[0], "cst1": cpack[1],
    }
    ws4 = np.asarray(inp["w_s"])[:, :, :ST, :ST]
    wsS = np.repeat(np.repeat(ws4.transpose(0, 3, 1, 2), SB, axis=1), SB, axis=3)
    shared["w_sS"] = f(wsS)
    bs4 = np.asarray(inp["b_s"])[:, :, :ST]
    shared["b_sS"] = f(np.repeat(bs4.transpose(0, 2, 1), SB, axis=1))
    for l in range(DEPTH):
        shared["puv%d" % l] = np.ascontiguousarray(
            np.concatenate([np.asarray(inp["peer_u"])[l], np.asarray(inp["peer_v"])[l]], axis=1), dtype=np.float32)
    maps = []
    for c in range(NCORES):
        bs = slice(c * SB, (c + 1) * SB)
        m = dict(shared)
        m["xp"] = f(np.asarray(inp["x_prompt"])[c])
        m["xs"] = f(np.asarray(inp["x_sample"])[bs].transpose(1, 0, 2).reshape(SP, D))
        m["cp"] = f(np.broadcast_to(np.asarray(inp["c_prompt"])[c][None, :], (128, D)))
        m["cs"] = f(np.tile(np.asarray(inp["c_sample"])[bs], (ST, 1)))
        sCc = np.asarray(inp["state_mlstm_C"])[:, bs]
        m["sC"] = f(sCc.transpose(0, 2, 3, 1, 4))
        snc = np.asarray(inp["state_mlstm_n"])[:, bs]
        m["snat"] = f(snc)
        m["snT"] = f(snc.transpose(0, 2, 3, 1))
        m["sm"] = f(np.tile(np.asarray(inp["state_mlstm_m"])[:, bs], (1, ST, 1)))
        spc = np.asarray(inp["state_pool"])[:, bs].transpose(0, 2, 1, 3)
        m["spA"] = f(spc[:, 0:8].reshape(DEPTH, 128, 256))
        m["spB"] = f(spc[:, 8:15].reshape(DEPTH, 112, 256))
        maps.append(m)
    return maps


def gather_outputs(results):
    cat = lambda k, ax: np.concatenate([r[k] for r in results], axis=ax)
    yp = np.stack([r["yp"] for r in results], 0)
    ys = np.concatenate([r["ys"].reshape(ST, SB, D).transpose(1, 0, 2) for r in results], 0)
    pC = np.stack([r["pC"] for r in results], 1)
    pn = np.stack([r["pn"] for r in results], 1)
    pm = np.stack([r["pm"] for r in results], 1)
    pp = np.stack([r["pp"] for r in results], 1)
    return (yp, ys, pC, pn, pm, pp, cat("nC", 1), cat("nn", 1), cat("nm", 1), cat("npool", 1), cat("nv", 1))


def kernel(**inputs):
    if "prog" not in _CACHE:
        _CACHE["prog"] = build_program()
    nc, cpack = _CACHE["prog"]
    maps = make_in_maps(inputs, cpack)
    res = run_bass_kernel_spmd(nc, maps, core_ids=list(range(NCORES)))
    outs = gather_outputs(res.results)
    return tuple(np.ascontiguousarray(o, dtype=np.float32) for o in outs)
```

```python
import numpy as np
from contextlib import ExitStack
import concourse.bass as bass
import concourse.mybir as mybir
from concourse.bass_utils import run_bass_kernel_spmd

F32 = mybir.dt.float32
I32 = mybir.dt.int32
U32 = mybir.dt.uint32
F32R = mybir.dt.float32r
BF16 = mybir.dt.bfloat16
ALU = mybir.AluOpType
AF = mybir.ActivationFunctionType
AX = mybir.AxisListType

NCORES = 8
D = 1024
SEQ = 2048
NT = 16
SB = 16
ST = 4
SP = SB * ST
DEPTH = 2
ALPHA = (2 * DEPTH) ** 0.25
LN_EPS = 1e-5
IN_COLS = 2824
NEG = -1.0e30
WCW = 192
NEXP = 16384
SAME_ENGINE_WAITS = True
NBUF = 12


class TB:
    def __init__(self, name, sem=None):
        self.name = name
        self.last_w = None
        self.reads = []
        self.sem = sem
        self.dma_total = 0
        self.dma_dirty = False


class KB:
    ENG = ("pe", "act", "dve", "pool", "sp")

    def __init__(self, nc, stack):
        self.nc = nc
        self.stack = stack
        self.q = {e: [] for e in self.ENG}
        self.cnt = {e: 0 for e in self.ENG}
        self.esem = {e: stack.enter_context(nc.semaphore("es_" + e)) for e in self.ENG}
        self.seen = {e: {} for e in self.ENG}
        self.semobj = {}
        self._sem_owner = {}
        self.stack0 = stack
        self.phase_tbs = []
        self.sem_pool = []
        self.nsem = 0
        self.sfx = ""

    def new_sem(self, name):
        return self.stack.enter_context(self.nc.semaphore(name + self.sfx))

    def buf(self, name, dma=False):
        if not dma:
            return TB(name)
        if self.sem_pool:
            sem, val = self.sem_pool.pop()
        else:
            sem, val = self.stack0.enter_context(self.nc.semaphore("dsem%d" % self.nsem)), 0
            self.nsem += 1
        tb = TB(name, sem)
        tb.dma_total = val
        if self.stack is not self.stack0:
            self.phase_tbs.append(tb)
        return tb

    def end_phase(self):
        for tb in self.phase_tbs:
            self._sem_owner.pop(id(tb.sem), None)
            self.sem_pool.append((tb.sem, tb.dma_total))
        self.phase_tbs = []

    def sb(self, name, shape, dt=F32):
        return self.stack.enter_context(self.nc.sbuf_tensor(name + self.sfx, list(shape), dt))

    def ps(self, name, shape, dt=F32):
        return self.stack.enter_context(self.nc.psum_tensor(name + self.sfx, list(shape), dt))

    def _deps(self, e, reads, writes):
        deps = {}

        def add(tok):
            if tok is None:
                return
            s, v = tok
            k = id(s)
            self.semobj[k] = s
            ow = self._sem_owner.get(k)
            if ow is not None:
                v = ow.dma_total
            if v > deps.get(k, 0):
                deps[k] = v
        for b in reads:
            add(b.last_w)
        for b in writes:
            add(b.last_w)
            for r in b.reads:
                add(r)
        out = []
        own = id(self.esem[e])
        for k, v in deps.items():
            if k == own and (e in ("pe", "sp") or not SAME_ENGINE_WAITS):
                continue
            if self.seen[e].get(k, 0) >= v:
                continue
            self.seen[e][k] = v
            out.append((self.semobj[k], v))
        return out

    def op(self, e, fn, reads=(), writes=(), bound=False):
        waits = self._deps(e, reads, writes)
        for s, v in waits:
            tb = self._sem_owner.get(id(s))
            if tb is not None:
                tb.dma_dirty = True
        self.cnt[e] += 1
        tok = (self.esem[e], self.cnt[e])
        self.q[e].append((waits, fn if bound else _bind(fn), tok[0], 1))
        for b in reads:
            b.reads.append(tok)
        for b in writes:
            b.last_w = tok
            b.reads = []
        return tok

    def dma(self, e, fn, owner, reads=(), writes=(), bound=False):
        self._sem_owner[id(owner.sem)] = owner
        waits = self._deps(e, reads, writes)
        if owner.dma_dirty and owner.dma_total > 0:
            k = id(owner.sem)
            if self.seen[e].get(k, 0) < owner.dma_total:
                self.seen[e][k] = owner.dma_total
                waits.append((owner.sem, owner.dma_total))
            owner.dma_dirty = False
        for s, v in waits:
            tb = self._sem_owner.get(id(s))
            if tb is not None and tb is not owner:
                tb.dma_dirty = True
        owner.dma_total += 16
        tok = (owner.sem, owner.dma_total)
        self.q[e].append((waits, fn if bound else _bind(fn), owner.sem, 16))
        for b in reads:
            b.reads.append(tok)
        for b in writes:
            b.last_w = tok
            b.reads = []
        return tok

    def barrier(self, extra=()):
        toks = [(self.esem[e], self.cnt[e]) for e in self.ENG if self.cnt[e] > 0 and e != "sp"]
        for tb in list(self._sem_owner.values()) + list(extra):
            if tb.dma_total > 0:
                toks.append((tb.sem, tb.dma_total))
        for e in self.ENG:
            waits = []
            for s, v in toks:
                k = id(s)
                if k == id(self.esem[e]):
                    continue
                if self.seen[e].get(k, 0) >= v:
                    continue
                self.seen[e][k] = v
                waits.append((s, v))
            if waits:
                self.q[e].append((waits, None, None, 0))

    def emit(self, final_waits=()):
        nc = self.nc
        engs = {"pe": "tensor", "act": "scalar", "dve": "vector", "pool": "gpsimd", "sp": "sync"}
        with nc.Block() as block:
            for e in self.ENG:
                items = self.q[e]
                fw = list(final_waits) if e == "sp" else []

                def body(eng, items=items, fw=fw):
                    for waits, fn, sem, inc in items:
                        for s, v in waits:
                            eng.wait_ge(s, v)
                        if fn is not None:
                            fn(eng).then_inc(sem, inc)
                    for s, v in fw:
                        eng.wait_ge(s, v)
                getattr(block, engs[e])(body)
        self.q = {e: [] for e in self.ENG}


class _Rec:
    def __init__(self):
        self.call = None

    def __getattr__(self, name):
        def f(*a, **k):
            self.call = (name, a, k)
            return self
        return f


def _bind(fn):
    r = _Rec()
    fn(r)
    assert r.call is not None
    name, a, k = r.call
    return lambda eng: getattr(eng, name)(*a, **k)


class Tn:
    def __init__(self, kb, name, shape, dt=F32, psum=False, dma=False):
        self.t = kb.ps(name, shape, dt) if psum else kb.sb(name, shape, dt)
        self.b = kb.buf(name, dma=dma)

    def __getitem__(self, k):
        return self.t[k]


def _consts():
    c = {}
    i128 = np.arange(128)
    c["ident"] = np.eye(128, dtype=np.float32)
    c["ones"] = np.ones((128, 128), np.float32)
    c["triu"] = (i128[:, None] <= i128[None, :]).astype(np.float32)
    c["negm"] = np.where(i128[None, :] <= i128[:, None], 0.0, NEG).astype(np.float32)
    sel = np.zeros((128, 128), np.float32); sel[127, :] = 1.0
    c["sel127"] = sel
    p = np.arange(SP); tt = p // SB; bb = p % SB
    sameb = bb[:, None] == bb[None, :]
    tri_s = (sameb & (tt[:, None] <= tt[None, :])).astype(np.float32)
    c["tri_s"] = _pad(tri_s)
    c["negm_s"] = _pad(np.where(sameb & (tt[None, :] <= tt[:, None]), 0.0, NEG).astype(np.float32))
    c["negb_s"] = _pad(np.where(sameb, 0.0, NEG).astype(np.float32))
    c["selend"] = _pad(((tt[:, None] == ST - 1) & sameb).astype(np.float32))
    oh = (bb[:, None] == np.arange(SB)[None, :]).astype(np.float32)
    c["onehotB"] = _pad(oh, cols=16)
    oh0 = ((p[:, None] == np.arange(SB)[None, :])).astype(np.float32)
    c["onehot0"] = _pad(oh0, cols=16)
    c["iota16"] = np.broadcast_to(np.arange(16, dtype=np.float32), (128, 16)).copy()
    wins = (2, 4, 8, 16)
    bc0 = np.zeros((4, 128, 128), np.float32); bc = np.zeros((4, 128, 128), np.float32)
    bp = np.zeros((4, 128, 128), np.float32)
    for g, w in enumerate(wins):
        for t in range(128):
            for j in range(w):
                s = t - j
                if s >= 0:
                    bc[g, s, t] += 1.0 / w
                    bc0[g, s, t] += 1.0 / min(t + 1, w)
                else:
                    bp[g, s + 128, t] += 1.0 / w
            bc[g, t, t] -= 1.0
            bc0[g, t, t] -= 1.0
    c["bandc0"] = bc0.transpose(1, 0, 2).reshape(128, 512)
    c["bandc"] = bc.transpose(1, 0, 2).reshape(128, 512)
    c["bandp"] = bp.transpose(1, 0, 2).reshape(128, 512)
    bsA = np.zeros((4, 128, SP), np.float32); bsB = np.zeros((4, 128, SP), np.float32)
    bsC = np.zeros((4, 128, SP), np.float32)
    for g, w in enumerate(wins):
        for t in range(ST):
            for b in range(SB):
                col = t * SB + b
                for j in range(w):
                    r = 15 + t - j
                    if r >= 15:
                        bsC[g, (r - 15) * SB + b, col] += 1.0 / w
                    elif r >= 8:
                        bsB[g, (r - 8) * SB + b, col] += 1.0 / w
                    else:
                        bsA[g, r * SB + b, col] += 1.0 / w
                bsC[g, t * SB + b, col] -= 1.0
    c["bsA"] = bsA.transpose(1, 0, 2).reshape(128, 4 * SP)
    c["bsB"] = bsB.transpose(1, 0, 2).reshape(128, 4 * SP)
    c["bsC"] = bsC.transpose(1, 0, 2).reshape(128, 4 * SP)
    return c


def _pad(a, cols=None):
    out = np.zeros((128, a.shape[1] if cols is None else cols), np.float32)
    out[: a.shape[0], : a.shape[1]] = a
    return out


_CONST_G = ["ident", "ones", "iota16"]
_CONST_1 = ["triu", "negm", "sel127", "tri_s", "negm_s", "negb_s", "selend",
            "onehotB", "onehot0", "bandc0", "bandc", "bandp", "bsA", "bsB", "bsC"]


def _const_pack():
    c = _consts()
    packs = []
    for order in (_CONST_G, _CONST_1):
        offs = {}
        o = 0
        arrs = []
        for k in order:
            offs[k] = (o, c[k].shape[1])
            o += c[k].shape[1]
            arrs.append(c[k])
        packs.append((np.ascontiguousarray(np.concatenate(arrs, axis=1)), offs))
    return packs


def build_program(n_layers=DEPTH, do_phase2=True):
    (cpack, coff), (cpack1, coff1) = _const_pack()
    NCST = cpack.shape[1]
    NCST1 = cpack1.shape[1]
    nc = bass.Bass("TRN2", target_bir_lowering=False)

    def din(name, shape, dt=F32):
        return nc.dram_tensor(name, list(shape), dt, kind="ExternalInput").ap()

    def dout(name, shape, dt=F32):
        return nc.dram_tensor(name, list(shape), dt, kind="ExternalOutput").ap()

    xp = din("xp", [SEQ, D]); xs = din("xs", [SP, D])
    cp = din("cp", [128, D]); cs = din("cs", [SP, D])
    sC = din("sC", [DEPTH, 4, 128, SB, 128]); snat = din("snat", [DEPTH, SB, 4, 128])
    snT = din("snT", [DEPTH, 4, 128, SB]); sm = din("sm", [DEPTH, SP, 4])
    spA = din("spA", [DEPTH, 128, 256]); spB = din("spB", [DEPTH, 112, 256])
    w_ada = din("w_ada", [DEPTH, (6 * D) // WCW, 128, 8 * WCW]); b_ada = din("b_ada", [DEPTH, 128, 6 * D])
    w_in = din("w_in", [DEPTH, (IN_COLS + WCW - 1) // WCW, 128, 8 * WCW]); b_gate = din("b_gate", [DEPTH, 128, 8])
    mh_g = din("mh_g", [DEPTH, 128, 512]); sgu_g = din("sgu_g", [DEPTH, 128, 256])
    sgu_b = din("sgu_b", [DEPTH, 128, 256]); pscale = din("pscale", [DEPTH, 128, 256])
    w_sT = din("w_sT", [DEPTH, 128, 4, 128]); b_sT = din("b_sT", [DEPTH, 128, 4])
    w_sS = din("w_sS", [DEPTH, SP, 4, SP]); b_sS = din("b_sS", [DEPTH, SP, 4])
    w_pool = din("w_pool", [DEPTH, 64, 4, 64]); w_o = din("w_o", [DEPTH, (D + WCW - 1) // WCW, 128, 8 * WCW])
    ln1g = din("ln1g", [DEPTH, 128, D]); ln1b = din("ln1b", [DEPTH, 128, D])
    ln2g = din("ln2g", [DEPTH, 128, D]); ln2b = din("ln2b", [DEPTH, 128, D])
    w_pq = din("w_pq", [DEPTH, 16, 128, 8 * 128]); keysT = din("keysT", [DEPTH, 128, 16, 128])
    puv = [din("puv%d" % l, [NEXP, 2 * D]) for l in range(DEPTH)]
    puvb = [nc.dram_tensor("puvb%d" % l, [NEXP, 2 * D], BF16, kind="Internal").ap() for l in range(DEPTH)]
    cst_d = din("cst", [128, NCST])
    cst1_d = din("cst1", [128, NCST1])

    yp = dout("yp", [SEQ, D]); ys = dout("ys", [SP, D])
    o_pC = dout("pC", [DEPTH, 4, 128, 128]); o_pn = dout("pn", [DEPTH, 4, 128]); o_pm = dout("pm", [DEPTH, 4])
    o_pp = dout("pp", [DEPTH, 15, 256])
    o_nC = dout("nC", [DEPTH, SB, 4, 128, 128]); o_nn = dout("nn", [DEPTH, SB, 4, 128])
    o_nm = dout("nm", [DEPTH, SB, 4]); o_np = dout("npool", [DEPTH, SB, 15, 256])
    o_nv = dout("nv", [DEPTH, SB, ST, 256])

    with ExitStack() as st0:
        kb = KB(nc, st0)
        op = kb.op
        OUT = kb.buf("outs", dma=True)

        def out_dma(dst, src, reads):
            kb.dma("sp", lambda q: q.dma_start(out=dst, in_=src), OUT, reads=reads)

        X = kb.sb("X", [128, NT + 1, D])
        XB = [kb.buf("X%d" % t) for t in range(NT + 1)]
        XL = kb.buf("xload", dma=True)
        CST = Tn(kb, "CST", [128, NCST], dma=True)
        PUVBT = [kb.buf("puvb%d" % i, dma=True) for i in range(DEPTH)]
        EPSB = Tn(kb, "EPSB", [128, 1])
        kb.op("dve", lambda e: e.memset(EPSB[:, :], LN_EPS), writes=[EPSB.b])

        def C(name, P=128, w=None):
            if name in coff:
                o, n = coff[name]
                return CST[:P, o:o + (n if w is None else w)]
            o, n = coff1[name]
            return kb.cst1[:P, o:o + (n if w is None else w)]

        def Cg(name, g, P, blk, w):
            o, n = coff1[name]
            return kb.cst1[:P, o + g * blk: o + g * blk + w]

        with nc.allow_non_contiguous_dma(reason="small strided state/param loads"):
            kb.dma("sp", lambda q: q.dma_start(out=CST[:, :], in_=cst_d), CST.b, writes=[CST.b])
            for t in range(NT):
                kb.dma("sp", lambda q, t=t: q.dma_start(out=X[:, t, :], in_=xp[t * 128:(t + 1) * 128, :]),
                       XL, writes=[XB[t]])
            kb.dma("sp", lambda q: q.dma_start(out=X[:SP, NT, :], in_=xs), XL, writes=[XB[NT]])

            puvb_ = puvb
            for l in range(n_layers):
                for part in ("p", "s"):
                    with ExitStack() as st1:
                        kb.stack = st1
                        kb.sfx = "_a%s%d" % (part, l)
                        _phase1(nc, kb, l, locals(), part)
                        kb.barrier(extra=[OUT])
                        kb.emit()
                        kb.end_phase()
                if do_phase2:
                    with ExitStack() as st2:
                        kb.stack = st2
                        kb.sfx = "_b%d" % l
                        _phase2(nc, kb, l, locals())
                        kb.barrier(extra=[OUT])
                        kb.emit()
                        kb.end_phase()
            kb.stack = st0
            kb.sfx = ""
            for t in range(NT):
                out_dma(yp[t * 128:(t + 1) * 128, :], X[:, t, :], [XB[t]])
            out_dma(ys, X[:SP, NT, :], [XB[NT]])
            kb.emit(final_waits=[(OUT.sem, OUT.dma_total)])
    return nc, (cpack, cpack1)


def _ada(nc, kb, l, E, ADA, P, csrc, off, WCH, PM, hbuf, hT, PT, badac, WCR=None):
    op = kb.op
    C = E["C"]
    w_ada, b_ada = E["w_ada"], E["b_ada"]
    kb.dma("sp", lambda q: q.dma_start(out=hbuf[:P, :], in_=csrc), hbuf.b, writes=[hbuf.b])
    op("act", lambda e: e.activation(out=hbuf[:P, :], in_=hbuf[:P, :], func=AF.Silu), reads=[hbuf.b], writes=[hbuf.b])
    _transpose8(kb, E, hbuf, hT, PT, P)
    r32 = (hT.t.dtype == F32R)
    for c in range(3072 // WCW):
        i = c % 2
        c0 = off + c * WCW
        wch = WCH[c % len(WCH)]
        kb.dma("sp", lambda q: q.dma_start(out=wch[:, :, 0:WCW], in_=w_ada[l, c0 // WCW].rearrange("p (k j) -> p k j", k=8)), wch.b, writes=[wch.b])
        kb.dma("sp", lambda q: q.dma_start(out=badac[i][:P, 0:WCW], in_=b_ada[l, :P, c0:c0 + WCW]), badac[i].b, writes=[badac[i].b])
        wsrc = WCR[c % 2] if r32 else wch
        if r32:
            op("act", lambda e: e.copy(out=wsrc[:, :, 0:WCW], in_=wch[:, :, 0:WCW]), reads=[wch.b], writes=[wsrc.b])
        for k in range(8):
            if r32:
                op("pe", lambda e: e.matmul(PM[i][:, 0:WCW], lhsT=hT[:, k, :], rhs=wsrc[:, k, 0:WCW], start=(k == 0), stop=(k == 7)),
                   reads=[hT.b, wsrc.b], writes=[PM[i].b] if k in (0, 7) else [])
            else:
                op("pe", lambda e: e.matmul(PM[i][:P, 0:WCW], lhsT=hT[:, k, :P], rhs=wch[:, k, 0:WCW], start=(k == 0), stop=(k == 7)),
                   reads=[hT.b, wch.b], writes=[PM[i].b] if k in (0, 7) else [])
        op("dve", lambda e: e.tensor_tensor(out=ADA[:P, c * WCW:(c + 1) * WCW], in0=PM[i][:P, 0:WCW], in1=badac[i][:P, 0:WCW], op=ALU.add),
           reads=[PM[i].b, badac[i].b], writes=[ADA.b])
    op("dve", lambda e: e.tensor_scalar_add(out=ADA[:P, 1024:2048], in0=ADA[:P, 1024:2048], scalar1=1.0), reads=[ADA.b], writes=[ADA.b])


def _transpose8(kb, E, src, dstT, PT, P, srcb=None, op=None):
    op = kb.op if op is None else op
    C = E["C"]
    sb_ = src.b if srcb is None else srcb
    for half in range(2):
        for j in range(4):
            k = half * 4 + j
            op("pe", lambda e, half=half, j=j, k=k: e.transpose(
                out=PT[half][:, j * 128:j * 128 + P], in_=src[:P, k * 128:(k + 1) * 128], identity=C("ident", P, P)),
               reads=[sb_, E["CST"].b], writes=[PT[half].b])
        op("act", lambda e, half=half: e.copy(
            out=dstT[:, half * 4:half * 4 + 4, :P],
            in_=PT[half][:, :].rearrange("p (j c) -> p j c", j=4)[:, :, :P]),
           reads=[PT[half].b], writes=[dstT.b])


def _phase1(nc, kb, l, E, part):
    isS = (part == "s")
    tiles = [NT] if isS else list(range(NT))
    cur = [None]
    cnt = [0]
    convq = [None]

    def run_items(items, n=None):
        n = len(items) if n is None else min(n, len(items))
        for _ in range(n):
            kind, e, fn, owner, reads, writes = items.pop(0)
            if kind == "op":
                kb.op(e, fn, reads=reads, writes=writes, bound=True)
            else:
                kb.dma(e, fn, owner, reads=reads, writes=writes, bound=True)

    def op(e, fn, reads=(), writes=()):
        tok = kb.op(e, fn, reads=reads, writes=writes)
        if cur[0]:
            run_items(cur[0], 1)
        cnt[0] += 1
        if convq[0] and cnt[0] % 16 == 0:
            run_items(convq[0], 1)
        return tok
    C, Cg, CST, X, XB = E["C"], E["Cg"], E["CST"], E["X"], E["XB"]
    EPSB = E["EPSB"]
    out_dma = E["out_dma"]
    w_in, w_o = E["w_in"], E["w_o"]

    kb.cst1 = kb.sb("CST1", [128, E["NCST1"]])
    kb.dma("sp", lambda q: q.dma_start(out=kb.cst1[:, :], in_=E["cst1_d"]), CST.b, writes=[CST.b])
    ADA = Tn(kb, "ADA1", [128, 3072]); ADAs = ADA
    WCH = [Tn(kb, "WCH%d" % i, [128, 8, WCW], dma=True) for i in range(2)]
    WCR = [Tn(kb, "WCR%d" % i, [128, 8, WCW], F32R) for i in range(2)]
    badac = [Tn(kb, "bada%d" % i, [128, 256], dma=True) for i in range(2)]
    nbuf = 1 if isS else 2
    Hs = [Tn(kb, "H%d" % i, [128, D], dma=True) for i in range(nbuf)]
    HTs = [Tn(kb, "HT%d" % i, [128, 8, 128], F32R) for i in range(nbuf)]
    PROJs = [Tn(kb, "PROJ%d" % i, [128, IN_COLS], dma=True) for i in range(nbuf)]
    H, HT, PROJ = Hs[0], HTs[0], PROJs[0]
    Ys = [Tn(kb, "Y%d" % i, [128, D]) for i in range(nbuf)]
    Y = Ys[0]
    SMC = Tn(kb, "SMC", [128, 64]); ST6C = Tn(kb, "ST6C", [128, 2, 6])
    PRM = Tn(kb, "PRM", [128, 8 + 512 + 256 * 3 + 2 * D], dma=True)
    WS = BS = WSs = BSs = None
    if isS:
        WSs = Tn(kb, "WSs", [128, 4, SP], dma=True); BSs = Tn(kb, "BSs", [128, 4], dma=True)
    else:
        WS = Tn(kb, "WS", [128, 4, 128], dma=True); BS = Tn(kb, "BS", [128, 4], dma=True)
    WP = Tn(kb, "WP", [64, 4, 64], dma=True)
    PT = [Tn(kb, "PT%d" % i, [128, 512], psum=True) for i in range(2)]
    PM = [Tn(kb, "PM%d" % i, [128, 512], psum=True) for i in range(2)]
    PA = Tn(kb, "PA", [128, 512], psum=True); PB = Tn(kb, "PB", [128, 512], psum=True)
    PC = Tn(kb, "PC", [128, 512], psum=True); PD = Tn(kb, "PD", [128, 512], psum=True)
    SM = Tn(kb, "SM", [128, 64])
    SMs = MREP = CTX = None
    if isS:
        SMs = Tn(kb, "SMs", [128, 4], dma=True)
    else:
        MREP = Tn(kb, "MREP", [128, 4])
        CTX = Tn(kb, "CTX", [128, 4, 129], dma=True)
    DG = Tn(kb, "DG", [128, 128]); DL = Tn(kb, "DL", [128, 128]); WI = Tn(kb, "WI", [128, 128])
    AM = Tn(kb, "AM", [128, 128]); AT = Tn(kb, "AT", [128, 128])
    QT = Tn(kb, "QT", [128, 128]); KT = Tn(kb, "KT", [128, 128])
    VX = Tn(kb, "VX", [128, 129]); TOT = Tn(kb, "TOT", [128, 129]); WV = Tn(kb, "WV", [128, 129])
    HN = Tn(kb, "HN", [128, 128]); SG = Tn(kb, "SG", [128, 128]); ST6 = Tn(kb, "ST6", [128, 2, 6])
    OUTC = None if isS else Tn(kb, "OUTC", [128, 128], dma=True)
    CN = CTS = RA = ZQ = NNAT = NTH = WCB = DECD = DECR = MSO = SPA = SPB = PREV = None
    if isS:
        CN = Tn(kb, "CN", [128, SB, 128], dma=True); CTS = Tn(kb, "CTS", [128, SB, 129])
        RA = Tn(kb, "RA", [128, SB, 128])

    class _V2:
        def __init__(self, ap, b):
            self.t = ap
            self.b = b

        def __getitem__(self, k):
            return self.t[k]
    if isS:
        ZQ = _V2(RA[:, :, :].rearrange("p a b -> p (a b)")[:, 0:SB * SP], RA.b)
        NNAT = Tn(kb, "NNAT", [SB, 4, 128], dma=True); NTH = Tn(kb, "NTH", [128, SB], dma=True)
        WCB = Tn(kb, "WCB", [128, 16]); DECD = Tn(kb, "DECD", [128, 16]); DECR = Tn(kb, "DECR", [128, 16])
        MSO = Tn(kb, "MSO", [SB, 4], dma=True)
        SPA = Tn(kb, "SPA", [128, 256], dma=True); SPB = Tn(kb, "SPB", [128, 256], dma=True)
    else:
        PREV = Tn(kb, "PREV", [128, 256])
    PTT = Tn(kb, "PTT", [64, 4, 128])
    VN = Tn(kb, "VN", [128, 256], dma=True); VTMP = Tn(kb, "VTMP", [128, 256])

    o_bg, o_mh, o_sg, o_sb, o_ps, o_l1g, o_l1b = 0, 8, 520, 776, 1032, 1288, 1288 + D
    for (o, w, src) in [(o_bg, 8, E["b_gate"]), (o_mh, 512, E["mh_g"]), (o_sg, 256, E["sgu_g"]), (o_sb, 256, E["sgu_b"]),
                        (o_ps, 256, E["pscale"]), (o_l1g, D, E["ln1g"]), (o_l1b, D, E["ln1b"])]:
        kb.dma("sp", lambda q, o=o, w=w, src=src: q.dma_start(out=PRM[:, o:o + w], in_=src[l]), PRM.b, writes=[PRM.b])
    kb.dma("sp", lambda q: q.dma_start(out=WP[:, :, :], in_=E["w_pool"][l]), WP.b, writes=[WP.b])
    if isS:
        kb.dma("sp", lambda q: q.dma_start(out=WSs[:SP, :, :], in_=E["w_sS"][l]), WSs.b, writes=[WSs.b])
        kb.dma("sp", lambda q: q.dma_start(out=BSs[:SP, :], in_=E["b_sS"][l]), BSs.b, writes=[BSs.b])
    else:
        kb.dma("sp", lambda q: q.dma_start(out=WS[:, :, :], in_=E["w_sT"][l]), WS.b, writes=[WS.b])
        kb.dma("sp", lambda q: q.dma_start(out=BS[:, :], in_=E["b_sT"][l]), BS.b, writes=[BS.b])
    for g in range(4):
        if isS:
            op("dve", lambda e, g=g: e.tensor_tensor(out=WSs[:SP, g, :], in0=WSs[:SP, g, :], in1=C("tri_s", SP, SP), op=ALU.mult),
               reads=[WSs.b, CST.b], writes=[WSs.b])
        else:
            op("dve", lambda e, g=g: e.tensor_tensor(out=WS[:, g, :], in0=WS[:, g, :], in1=C("triu"), op=ALU.mult),
               reads=[WS.b, CST.b], writes=[WS.b])
    if not isS:
        op("dve", lambda e: e.memset(CTX[:, :, :], 0.0), writes=[CTX.b])
        op("dve", lambda e: e.memset(MREP[:, :], 0.0), writes=[MREP.b])
    op("dve", lambda e: e.memset(VX[:, :], 1.0), writes=[VX.b])

    if isS:
        _ada(nc, kb, l, E, ADA, SP, E["cs"], 0, WCH, PM, H, HT, PT, badac, WCR)
    else:
        _ada(nc, kb, l, E, ADA, 128, E["cp"], 0, WCH, PM, H, HT, PT, badac, WCR)

    w_in_v = w_in[l]
    w_o_v = w_o[l]
    def mk_chunks(n):
        return [(c0, min(WCW, n - c0)) for c0 in range(0, n, WCW)]
    chunks = mk_chunks(IN_COLS)
    wctr = [0]

    def stream_mm(wview, c0, w, lhsT, P, evac, op=op, dma=kb.dma):
        i = wctr[0] % 2
        wctr[0] += 1
        dma("sp", lambda q: q.dma_start(out=WCH[i][:, :, :], in_=wview[c0 // WCW].rearrange("p (k j) -> p k j", k=8)), WCH[i].b, writes=[WCH[i].b])
        wr = WCR[i]
        if wctr[0] % 3 == 0:
            op("dve", lambda e: e.tensor_copy(out=wr[:, :, 0:w], in_=WCH[i][:, :, 0:w]), reads=[WCH[i].b], writes=[wr.b])
        else:
            op("act", lambda e: e.copy(out=wr[:, :, 0:w], in_=WCH[i][:, :, 0:w]), reads=[WCH[i].b], writes=[wr.b])
        for k in range(8):
            op("pe", lambda e, k=k: e.matmul(PM[i][:, 0:w], lhsT=lhsT[:, k, :], rhs=wr[:, k, 0:w],
                                              start=(k == 0), stop=(k == 7)),
               reads=[lhsT.b, wr.b], writes=[PM[i].b] if k in (0, 7) else [])
        evac(PM[i], i)

    def make_A(t):
        items = []

        def iop(e, fn, reads=(), writes=()):
            items.append(("op", e, _bind(fn), None, tuple(reads), tuple(writes)))

        def idma(e, fn, owner, reads=(), writes=()):
            items.append(("dma", e, _bind(fn), owner, tuple(reads), tuple(writes)))
        P = SP if isS else 128
        H, HT, PROJ = Hs[t % nbuf], HTs[t % nbuf], PROJs[t % nbuf]
        xt = X[:P, t, :]
        iop("dve", lambda e: e.tensor_tensor(out=H[:P, :], in0=xt, in1=ADA[:P, 1024:2048], op=ALU.mult), reads=[XB[t], ADA.b], writes=[H.b])
        iop("dve", lambda e: e.tensor_tensor(out=H[:P, :], in0=H[:P, :], in1=ADA[:P, 0:1024], op=ALU.add), reads=[H.b, ADA.b], writes=[H.b])
        _transpose8(kb, E, H, HT, PT, P, op=iop)
        for (c0, w) in chunks:
            stream_mm(w_in_v, c0, w, HT, P,
                      lambda pm, i, c0=c0, w=w: iop("act", lambda e: e.copy(out=PROJ[:P, c0:c0 + w], in_=pm[:P, 0:w]),
                                                    reads=[pm.b], writes=[PROJ.b]), op=iop, dma=idma)
        return items

    conv = []
    if not isS:
        CBs = [Tn(kb, "CB%d" % i, [128, 2 * D], BF16, dma=True) for i in range(1)]
        tab32 = E["puv"][l]
        tabb = E["puvb"][l]
        PUVB = E["PUVBT"][l]
        for blk in range(NEXP // 128):
            cb = CBs[0]
            conv.append(("dma", "pool", _bind(lambda q: q.dma_start(out=cb[:, :], in_=tab32[blk * 128:(blk + 1) * 128, :])), cb.b, (), (cb.b,)))
            conv.append(("dma", "sp", _bind(lambda q: q.dma_start(out=tabb[blk * 128:(blk + 1) * 128, :], in_=cb[:, :])), PUVB, (cb.b,), (PUVB,)))
    convq[0] = conv
    pendC = [None]
    run_items(make_A(tiles[0]))
    for ti, t in enumerate(tiles):
        is_s = isS
        P = SP if is_s else 128
        ada = ADA
        H, HT, PROJ = Hs[t % nbuf], HTs[t % nbuf], PROJs[t % nbuf]
        xt = X[:P, t, :]
        Y = Ys[t % nbuf]
        nxtA = make_A(tiles[ti + 1]) if ti + 1 < len(tiles) else None
        inter = (pendC[0] or []) + (nxtA or [])
        pendC[0] = None
        cur[0] = inter
        tri = C("tri_s", SP, SP) if is_s else C("triu")
        negm = C("negm_s", SP, SP) if is_s else C("negm")
        selE = C("selend", SP, SP) if is_s else C("sel127")
        if is_s:
            kb.dma("sp", lambda q: q.dma_start(out=SMs[:SP, :], in_=E["sm"][l]), SMs.b, writes=[SMs.b])
            kb.dma("sp", lambda q: q.dma_start(out=NNAT[:, :, :], in_=E["snat"][l]), NNAT.b, writes=[NNAT.b])
        mtok = SMs if is_s else MREP
        op("dve", lambda e: e.tensor_tensor(out=SM[:P, 0:8], in0=PROJ[:P, 2048:2056], in1=PRM[:P, o_bg:o_bg + 8], op=ALU.add),
           reads=[PROJ.b, PRM.b], writes=[SM.b])
        op("dve", lambda e: e.scalar_tensor_tensor(out=SM[:P, 8:12], in0=SM[:P, 4:8], scalar=-1.0, in1=SM[:P, 4:8], op0=ALU.mult, op1=ALU.max),
           reads=[SM.b], writes=[SM.b])
        op("act", lambda e: e.activation(out=SM[:P, 12:16], in_=SM[:P, 8:12], func=AF.Exp, scale=-1.0), reads=[SM.b], writes=[SM.b])
        op("act", lambda e: e.activation(out=SM[:P, 12:16], in_=SM[:P, 12:16], func=AF.Ln, bias=1.0, scale=1.0),
           reads=[SM.b], writes=[SM.b])
        op("dve", lambda e: e.tensor_scalar_min(out=SM[:P, 16:20], in0=SM[:P, 4:8], scalar1=0.0), reads=[SM.b], writes=[SM.b])
        op("dve", lambda e: e.tensor_tensor(out=SM[:P, 16:20], in0=SM[:P, 16:20], in1=SM[:P, 12:16], op=ALU.subtract),
           reads=[SM.b], writes=[SM.b])
        op("pe", lambda e: e.matmul(PA[:P, 0:4], lhsT=tri, rhs=SM[:P, 16:20], start=True, stop=True),
           reads=[CST.b, SM.b], writes=[PA.b])
        op("act", lambda e: e.copy(out=SM[:P, 20:24], in_=PA[:P, 0:4]), reads=[PA.b], writes=[SM.b])
        op("dve", lambda e: e.tensor_tensor(out=SM[:P, 24:28], in0=SM[:P, 0:4], in1=SM[:P, 20:24], op=ALU.subtract),
           reads=[SM.b], writes=[SM.b])
        op("pe", lambda e: e.matmul(PA[:P, 8:12], lhsT=selE, rhs=SM[:P, 20:24], start=True, stop=True),
           reads=[CST.b, SM.b], writes=[PA.b])
        op("act", lambda e: e.copy(out=SM[:P, 28:32], in_=PA[:P, 8:12]), reads=[PA.b], writes=[SM.b])

        for hh in range(4):
            qs = PROJ[:P, hh * 128:(hh + 1) * 128]
            ks = PROJ[:P, 512 + hh * 128:512 + (hh + 1) * 128]
            vs = PROJ[:P, 1024 + hh * 128:1024 + (hh + 1) * 128]
            os_ = PROJ[:P, 1536 + hh * 128:1536 + (hh + 1) * 128]
            col = lambda c, hh=hh: SM[:P, c + hh:c + hh + 1]
            S1 = lambda c: SM[:P, c:c + 1]
            if is_s:
                kb.dma("sp", lambda q, hh=hh: q.dma_start(out=CN[:, :, :], in_=E["sC"][l, hh]), CN.b, writes=[CN.b])
                kb.dma("sp", lambda q, hh=hh: q.dma_start(out=NTH[:, :], in_=E["snT"][l, hh]), NTH.b, writes=[NTH.b])
                for j in range(4):
                    pt = PT[j % 2]
                    for jj in range(4):
                        b = j * 4 + jj
                        op("pe", lambda e, b=b, jj=jj, pt=pt: e.transpose(out=pt[:, jj * 128:(jj + 1) * 128], in_=CN[:, b, :],
                                                                       identity=C("ident")),
                           reads=[CN.b, CST.b], writes=[pt.b])
                    op("act", lambda e, j=j, pt=pt: e.copy(out=CTS[:, j * 4:(j + 1) * 4, 0:128],
                                                           in_=pt[:, :].rearrange("p (j c) -> p j c", j=4)),
                       reads=[pt.b], writes=[CTS.b])
                op("dve", lambda e: e.tensor_copy(out=CTS[:, :, 128:129], in_=NTH[:, :].unsqueeze(2)), reads=[NTH.b], writes=[CTS.b])
            op("dve", lambda e, hh=hh: e.tensor_scalar(out=DG[:P, :P], in0=C("ident", P, P), scalar1=col(24), scalar2=None,
                                                       op0=ALU.mult), reads=[SM.b, CST.b], writes=[DG.b])
            op("pe", lambda e: e.matmul(PB[:P, 0:P], lhsT=C("ones", P, P), rhs=DG[:P, :P], start=True, stop=True),
               reads=[DG.b, CST.b], writes=[PB.b])
            if is_s:
                op("dve", lambda e: e.tensor_tensor(out=DL[:P, :P], in0=PB[:P, 0:P], in1=C("negb_s", SP, SP), op=ALU.add),
                   reads=[PB.b, CST.b], writes=[DL.b])
                op("dve", lambda e: e.tensor_reduce(out=S1(32), in_=DL[:P, :P], axis=AX.X, op=ALU.max), reads=[DL.b], writes=[SM.b])
            else:
                op("dve", lambda e: e.tensor_reduce(out=S1(32), in_=PB[:P, 0:P], axis=AX.X, op=ALU.max), reads=[PB.b], writes=[SM.b])
            op("dve", lambda e, hh=hh: e.scalar_tensor_tensor(out=DL[:P, :P], in0=PB[:P, 0:P], scalar=col(20), in1=negm,
                                                              op0=ALU.add, op1=ALU.add),
               reads=[PB.b, SM.b, CST.b], writes=[DL.b])
            op("dve", lambda e: e.tensor_reduce(out=S1(33), in_=DL[:P, :P], axis=AX.X, op=ALU.max), reads=[DL.b], writes=[SM.b])
            op("dve", lambda e, hh=hh: e.tensor_tensor(out=S1(34), in0=col(20), in1=mtok[:P, hh:hh + 1], op=ALU.add),
               reads=[SM.b, mtok.b], writes=[SM.b])
            op("dve", lambda e: e.tensor_tensor(out=S1(35), in0=S1(34), in1=S1(33), op=ALU.max), reads=[SM.b], writes=[SM.b])
            op("dve", lambda e: e.tensor_scalar(out=S1(36), in0=S1(35), scalar1=-1.0, scalar2=None, op0=ALU.mult),
               reads=[SM.b], writes=[SM.b])
            op("act", lambda e: e.activation(out=WI[:P, :P], in_=DL[:P, :P], func=AF.Exp, bias=S1(36), scale=1.0),
               reads=[DL.b, SM.b], writes=[WI.b])
            op("act", lambda e: e.activation(out=S1(37), in_=S1(34), func=AF.Exp, bias=S1(36), scale=1.0), reads=[SM.b], writes=[SM.b])
            op("act", lambda e: e.activation(out=S1(38), in_=S1(36), func=AF.Exp), reads=[SM.b], writes=[SM.b])
            op("pe", lambda e: e.transpose(out=PC[:, 0:P], in_=qs, identity=C("ident", P, P)), reads=[PROJ.b, CST.b], writes=[PC.b])
            op("pe", lambda e: e.transpose(out=PC[:, 128:128 + P], in_=ks, identity=C("ident", P, P)), reads=[PROJ.b, CST.b], writes=[PC.b])
            op("act", lambda e: e.mul(out=QT[:, :P], in_=PC[:, 0:P], mul=128.0 ** -0.5), reads=[PC.b], writes=[QT.b])
            op("act", lambda e: e.copy(out=KT[:, :P], in_=PC[:, 128:128 + P]), reads=[PC.b], writes=[KT.b])
            op("pe", lambda e: e.matmul(PD[:P, 0:P], lhsT=QT[:, :P], rhs=KT[:, :P], start=True, stop=True),
               reads=[QT.b, KT.b], writes=[PD.b])
            op("dve", lambda e: e.tensor_tensor(out=AM[:P, :P], in0=WI[:P, :P], in1=PD[:P, 0:P], op=ALU.mult),
               reads=[WI.b, PD.b], writes=[AM.b])
            op("pe", lambda e: e.transpose(out=PB[:P, 128:128 + P], in_=AM[:P, :P], identity=C("ident", P, P)),
               reads=[AM.b, CST.b], writes=[PB.b])
            op("act", lambda e: e.copy(out=AT[:P, :P], in_=PB[:P, 128:128 + P]), reads=[PB.b], writes=[AT.b])
            op("pool", lambda e: e.tensor_copy(out=VX[:P, 0:128], in_=vs), reads=[PROJ.b], writes=[VX.b])
            op("pe", lambda e: e.matmul(PD[:P, 128:257], lhsT=AT[:P, :P], rhs=VX[:P, :], start=True, stop=True),
               reads=[AT.b, VX.b], writes=[PD.b])
            if is_s:
                op("pool", lambda e: e.memset(ZQ[:, :], 0.0), writes=[ZQ.b])
                for b in range(SB):
                    op("pool", lambda e, b=b: e.tensor_copy(out=ZQ[:, b * SP + b:(b + 1) * SP:SB], in_=QT[:, b:SP:SB]),
                       reads=[QT.b], writes=[ZQ.b])
                for b in range(SB):
                    op("pe", lambda e, b=b: e.matmul(PC[:P, 256:385], lhsT=ZQ[:, b * SP:(b + 1) * SP], rhs=CTS[:, b, :],
                                                     start=(b == 0), stop=(b == SB - 1)),
                       reads=[ZQ.b, CTS.b], writes=[PC.b] if b in (0, SB - 1) else [])
            else:
                op("pe", lambda e, hh=hh: e.matmul(PC[:P, 256:385], lhsT=QT[:, :P], rhs=CTX[:, hh, :], start=True, stop=True),
                   reads=[QT.b, CTX.b], writes=[PC.b])
            op("act", lambda e: e.activation(out=TOT[:P, :], in_=PC[:P, 256:385], func=AF.Identity, scale=S1(37)),
               reads=[PC.b, SM.b], writes=[TOT.b])
            op("dve", lambda e: e.tensor_tensor(out=TOT[:P, :], in0=TOT[:P, :], in1=PD[:P, 128:257], op=ALU.add),
               reads=[TOT.b, PD.b], writes=[TOT.b])
            op("dve", lambda e: e.scalar_tensor_tensor(out=S1(39), in0=TOT[:P, 128:129], scalar=-1.0, in1=TOT[:P, 128:129], op0=ALU.mult, op1=ALU.max),
               reads=[TOT.b], writes=[SM.b])
            op("dve", lambda e: e.tensor_tensor(out=S1(39), in0=S1(39), in1=S1(38), op=ALU.max), reads=[SM.b], writes=[SM.b])
            op("dve", lambda e: e.reciprocal(out=S1(40), in_=S1(39)), reads=[SM.b], writes=[SM.b])
            op("dve", lambda e: e.tensor_scalar(out=HN[:P, :], in0=TOT[:P, 0:128], scalar1=S1(40), scalar2=None, op0=ALU.mult),
               reads=[TOT.b, SM.b], writes=[HN.b])
            op("dve", lambda e: e.bn_stats(out=ST6[:P, 0, :], in_=HN[:P, :]), reads=[HN.b], writes=[ST6.b])
            op("dve", lambda e: e.bn_aggr(out=SM[:P, 41:43], in_=ST6[:P, 0, :]), reads=[ST6.b], writes=[SM.b])
            op("act", lambda e: e.activation(out=S1(43), in_=S1(42), func=AF.Ln, bias=EPSB[:P, 0:1], scale=1.0), reads=[SM.b, EPSB.b], writes=[SM.b])
            op("act", lambda e: e.activation(out=S1(44), in_=S1(43), func=AF.Exp, scale=-0.5), reads=[SM.b], writes=[SM.b])
            op("dve", lambda e: e.tensor_scalar(out=HN[:P, :], in0=HN[:P, :], scalar1=S1(41), scalar2=S1(44),
                                                op0=ALU.subtract, op1=ALU.mult), reads=[HN.b, SM.b], writes=[HN.b])
            op("dve", lambda e, hh=hh: e.tensor_tensor(out=HN[:P, :], in0=HN[:P, :],
                                                       in1=PRM[:P, o_mh + hh * 128:o_mh + (hh + 1) * 128], op=ALU.mult),
               reads=[HN.b, PRM.b], writes=[HN.b])
            op("act", lambda e: e.activation(out=SG[:P, :], in_=os_, func=AF.Exp, scale=-1.0), reads=[PROJ.b], writes=[SG.b])
            op("dve", lambda e: e.tensor_scalar_add(out=SG[:P, :], in0=SG[:P, :], scalar1=1.0), reads=[SG.b], writes=[SG.b])
            op("dve", lambda e: e.reciprocal(out=SG[:P, :], in_=SG[:P, :]), reads=[SG.b], writes=[SG.b])
            op("dve", lambda e, hh=hh: e.tensor_tensor(out=Y[:P, hh * 128:(hh + 1) * 128], in0=HN[:P, :], in1=SG[:P, :], op=ALU.mult),
               reads=[HN.b, SG.b], writes=[Y.b])
            op("dve", lambda e, hh=hh: e.tensor_tensor(out=S1(45), in0=mtok[:P, hh:hh + 1], in1=S1(32), op=ALU.max),
               reads=[SM.b, mtok.b], writes=[SM.b])
            op("dve", lambda e, hh=hh: e.tensor_tensor(out=S1(45), in0=S1(45), in1=col(28), op=ALU.add), reads=[SM.b], writes=[SM.b])
            op("dve", lambda e, hh=hh: e.tensor_tensor(out=S1(46), in0=col(28), in1=S1(45), op=ALU.subtract), reads=[SM.b], writes=[SM.b])
            op("act", lambda e, hh=hh: e.activation(out=S1(47), in_=col(24), func=AF.Exp, bias=S1(46), scale=1.0),
               reads=[SM.b], writes=[SM.b])
            op("act", lambda e, hh=hh: e.activation(out=S1(48), in_=mtok[:P, hh:hh + 1], func=AF.Exp, bias=S1(46), scale=1.0),
               reads=[SM.b, mtok.b], writes=[SM.b])
            if not is_s:
                op("dve", lambda e: e.tensor_scalar(out=WV[:P, :], in0=VX[:P, :], scalar1=S1(47), scalar2=None, op0=ALU.mult),
                   reads=[VX.b, SM.b], writes=[WV.b])
                op("pe", lambda e: e.matmul(PB[:, 256:385], lhsT=ks, rhs=WV[:P, :], start=True, stop=True),
                   reads=[PROJ.b, WV.b], writes=[PB.b])
                op("dve", lambda e, hh=hh: e.scalar_tensor_tensor(out=CTX[:, hh, :], in0=CTX[:, hh, :], scalar=S1(48), in1=PB[:, 256:385],
                                                                  op0=ALU.mult, op1=ALU.add),
                   reads=[CTX.b, SM.b, PB.b], writes=[CTX.b])
                op("dve", lambda e, hh=hh: e.tensor_copy(out=MREP[:, hh:hh + 1], in_=S1(45)), reads=[SM.b], writes=[MREP.b])
                if t == NT - 1:
                    op("pe", lambda e, hh=hh: e.transpose(out=PA[:, 128:256], in_=CTX[:, hh, 0:128], identity=C("ident")),
                       reads=[CTX.b, CST.b], writes=[PA.b])
                    op("act", lambda e: e.copy(out=OUTC[:, :], in_=PA[:, 128:256]), reads=[PA.b], writes=[OUTC.b])
                    out_dma(E["o_pC"][l, hh], OUTC[:, :], [OUTC.b])
                    out_dma(E["o_pn"][l, hh].rearrange("(k o) -> k o", o=1), CTX[:, hh, 128:129], [CTX.b])
                    if hh == 3:
                        out_dma(E["o_pm"][l:l + 1, :], MREP[0:1, :], [MREP.b])
            else:
                op("dve", lambda e: e.tensor_scalar(out=WCB[:P, :], in0=C("onehotB", SP), scalar1=S1(47), scalar2=None, op0=ALU.mult),
                   reads=[SM.b, CST.b], writes=[WCB.b])
                op("dve", lambda e: e.tensor_tensor(out=RA[:P, :, :], in0=vs.unsqueeze(1).broadcast_to([P, SB, 128]),
                                                    in1=WCB[:P, :].unsqueeze(2).broadcast_to([P, SB, 128]), op=ALU.mult),
                   reads=[PROJ.b, WCB.b], writes=[RA.b])
                op("dve", lambda e: e.tensor_scalar(out=DECD[:P, :], in0=C("onehot0", SP), scalar1=S1(48), scalar2=None, op0=ALU.mult),
                   reads=[SM.b, CST.b], writes=[DECD.b])
                op("pe", lambda e: e.matmul(PA[:, 16:32], lhsT=C("ones", SP, 128), rhs=DECD[:P, :], start=True, stop=True),
                   reads=[DECD.b, CST.b], writes=[PA.b])
                op("act", lambda e: e.copy(out=DECR[:, :], in_=PA[:, 16:32]), reads=[PA.b], writes=[DECR.b])
                for b in range(SB):
                    pq = [PA, PB, PC, PD][b % 4]
                    op("pe", lambda e, b=b, pq=pq: e.matmul(pq[:, 384:512], lhsT=RA[:P, b, :], rhs=ks, start=True, stop=True),
                       reads=[RA.b, PROJ.b], writes=[pq.b])
                    op("dve", lambda e, b=b, pq=pq: e.scalar_tensor_tensor(out=CN[:, b, :], in0=CN[:, b, :], scalar=DECR[:, b:b + 1],
                                                                           in1=pq[:, 384:512], op0=ALU.mult, op1=ALU.add),
                       reads=[CN.b, DECR.b, pq.b], writes=[CN.b])
                out_dma(E["o_nC"][l, :, hh].rearrange("b v k -> v b k"), CN[:, :, :], [CN.b])
                op("pe", lambda e: e.matmul(PA[:SB, 32:160], lhsT=WCB[:P, :], rhs=ks, start=True, stop=True),
                   reads=[WCB.b, PROJ.b], writes=[PA.b])
                op("dve", lambda e, hh=hh: e.scalar_tensor_tensor(out=NNAT[:, hh, :], in0=NNAT[:, hh, :], scalar=SM[:SB, 48:49],
                                                                  in1=PA[:SB, 32:160], op0=ALU.mult, op1=ALU.add),
                   reads=[NNAT.b, SM.b, PA.b], writes=[NNAT.b])
                op("dve", lambda e, hh=hh: e.tensor_copy(out=MSO[:, hh:hh + 1], in_=SM[:SB, 45:46]), reads=[SM.b], writes=[MSO.b])
                if hh == 3:
                    out_dma(E["o_nn"][l], NNAT[:, :, :], [NNAT.b])
                    out_dma(E["o_nm"][l], MSO[:, :], [MSO.b])

        vsv = PROJ[:P, 2312:2568].rearrange("p (g d) -> p g d", g=4)
        op("dve", lambda e: e.tensor_reduce(out=SM[:P, 50:54], in_=vsv, axis=AX.X, op=ALU.add), reads=[PROJ.b], writes=[SM.b])
        op("dve", lambda e: e.tensor_scalar(out=SM[:P, 50:54], in0=SM[:P, 50:54], scalar1=1.0 / 64, scalar2=None, op0=ALU.mult),
           reads=[SM.b], writes=[SM.b])
        op("dve", lambda e: e.tensor_tensor(out=VN[:P, :].rearrange("p (g d) -> p g d", g=4), in0=vsv,
                                            in1=SM[:P, 50:54].unsqueeze(2).broadcast_to([P, 4, 64]), op=ALU.subtract),
           reads=[PROJ.b, SM.b], writes=[VN.b])
        op("pool", lambda e: e.tensor_tensor(out=VTMP[:P, :], in0=VN[:P, :], in1=VN[:P, :], op=ALU.mult), reads=[VN.b], writes=[VTMP.b])
        op("dve", lambda e: e.tensor_reduce(out=SM[:P, 54:58], in_=VTMP[:P, :].rearrange("p (g d) -> p g d", g=4), axis=AX.X, op=ALU.add),
           reads=[VTMP.b], writes=[SM.b])
        op("act", lambda e: e.activation(out=SM[:P, 54:58], in_=SM[:P, 54:58], func=AF.Ln, bias=EPSB[:P, 0:1], scale=1.0 / 64),
           reads=[SM.b, EPSB.b], writes=[SM.b])
        op("act", lambda e: e.activation(out=SM[:P, 58:62], in_=SM[:P, 54:58], func=AF.Exp, scale=-0.5), reads=[SM.b], writes=[SM.b])
        op("dve", lambda e: e.tensor_tensor(out=VN[:P, :].rearrange("p (g d) -> p g d", g=4), in0=VN[:P, :].rearrange("p (g d) -> p g d", g=4),
                                            in1=SM[:P, 58:62].unsqueeze(2).broadcast_to([P, 4, 64]), op=ALU.mult),
           reads=[VN.b, SM.b], writes=[VN.b])
        op("pool", lambda e: e.tensor_tensor(out=VN[:P, :], in0=VN[:P, :], in1=PRM[:P, o_sg:o_sg + 256], op=ALU.mult),
           reads=[VN.b, PRM.b], writes=[VN.b])
        op("pool", lambda e: e.tensor_tensor(out=VN[:P, :], in0=VN[:P, :], in1=PRM[:P, o_sb:o_sb + 256], op=ALU.add),
           reads=[VN.b, PRM.b], writes=[VN.b])
        wsl = WSs if is_s else WS
        bsl = BSs if is_s else BS
        for g in range(4):
            op("pe", lambda e, g=g: e.matmul(PC[:P, g * 64:(g + 1) * 64], lhsT=wsl[:P, g, :P], rhs=VN[:P, g * 64:(g + 1) * 64],
                                             start=True, stop=True), reads=[wsl.b, VN.b], writes=[PC.b])
        for g in range(4):
            op("dve", lambda e, g=g: e.scalar_tensor_tensor(out=Y[:P, 512 + g * 64:512 + (g + 1) * 64], in0=PC[:P, g * 64:(g + 1) * 64],
                                                            scalar=bsl[:P, g:g + 1], in1=PROJ[:P, 2056 + g * 64:2056 + (g + 1) * 64],
                                                            op0=ALU.add, op1=ALU.mult),
               reads=[PC.b, bsl.b, PROJ.b], writes=[Y.b])
        if is_s:
            for tq in range(ST):
                out_dma(E["o_nv"][l][:, tq, :], VN[tq * SB:(tq + 1) * SB, :], [VN.b])

        pin = lambda g: PROJ[:P, 2568 + g * 64:2568 + (g + 1) * 64]
        if is_s:
            kb.dma("sp", lambda q: q.dma_start(out=SPA[:, :], in_=E["spA"][l]), SPA.b, writes=[SPA.b])
            kb.dma("sp", lambda q: q.dma_start(out=SPB[:112, :], in_=E["spB"][l]), SPB.b, writes=[SPB.b])
            for g in range(4):
                op("pe", lambda e, g=g: e.matmul(PA[:64, g * 128:g * 128 + P], lhsT=SPA[:, g * 64:(g + 1) * 64], rhs=Cg("bsA", g, 128, SP, SP),
                                                 start=True, stop=False), reads=[SPA.b, CST.b], writes=[PA.b])
                op("pe", lambda e, g=g: e.matmul(PA[:64, g * 128:g * 128 + P], lhsT=SPB[:112, g * 64:(g + 1) * 64], rhs=Cg("bsB", g, 112, SP, SP),
                                                 start=False, stop=False), reads=[SPB.b, CST.b], writes=[])
                op("pe", lambda e, g=g: e.matmul(PA[:64, g * 128:g * 128 + P], lhsT=pin(g), rhs=Cg("bsC", g, SP, SP, SP),
                                                 start=False, stop=True), reads=[PROJ.b, CST.b], writes=[PA.b])
            npv = E["o_np"][l].rearrange("b r c -> r b c")
            for r in range(4):
                out_dma(npv[r], SPA[64 + r * SB:64 + (r + 1) * SB, :], [SPA.b])
            for r in range(7):
                out_dma(npv[4 + r], SPB[r * SB:(r + 1) * SB, :], [SPB.b])
            for r in range(4):
                out_dma(npv[11 + r], PROJ[r * SB:(r + 1) * SB, 2568:2824], [PROJ.b])
        else:
            for g in range(4):
                band = Cg("bandc0" if t == 0 else "bandc", g, 128, 128, 128)
                op("pe", lambda e, g=g, band=band: e.matmul(PA[:64, g * 128:(g + 1) * 128], lhsT=pin(g), rhs=band, start=True, stop=(t == 0)),
                   reads=[PROJ.b, CST.b], writes=[PA.b])
                if t > 0:
                    op("pe", lambda e, g=g: e.matmul(PA[:64, g * 128:(g + 1) * 128], lhsT=PREV[:, g * 64:(g + 1) * 64],
                                                     rhs=Cg("bandp", g, 128, 128, 128), start=False, stop=True),
                       reads=[PREV.b, CST.b], writes=[PA.b])
            if t < NT - 1:
                op("pool", lambda e: e.tensor_copy(out=PREV[:, :], in_=PROJ[:, 2568:2824]), reads=[PROJ.b], writes=[PREV.b])
            else:
                out_dma(E["o_pp"][l], PROJ[113:128, 2568:2824], [PROJ.b])
        op("act", lambda e: e.copy(out=PTT[:, :, :P], in_=PA[:64, :].rearrange("p (g c) -> p g c", g=4)[:, :, :P]),
           reads=[PA.b], writes=[PTT.b])
        for g in range(4):
            op("pe", lambda e, g=g: e.matmul(PB[:P, g * 64:(g + 1) * 64], lhsT=PTT[:, g, :P], rhs=WP[:, g, :], start=True, stop=True),
               reads=[PTT.b, WP.b], writes=[PB.b])
        op("dve", lambda e: e.tensor_tensor(out=Y[:P, 768:1024], in0=PB[:P, 0:256], in1=PRM[:P, o_ps:o_ps + 256], op=ALU.mult),
           reads=[PB.b, PRM.b], writes=[Y.b])

        def make_C(t, P, Y, H, HT, ada):
            items = []

            def iop(e, fn, reads=(), writes=()):
                items.append(("op", e, _bind(fn), None, tuple(reads), tuple(writes)))

            def idma(e, fn, owner, reads=(), writes=()):
                items.append(("dma", e, _bind(fn), owner, tuple(reads), tuple(writes)))
            _transpose8(kb, E, Y, HT, PT, P, op=iop)
            for (c0, w) in mk_chunks(D):
                stream_mm(w_o_v, c0, w, HT, P,
                          lambda pm, i, c0=c0, w=w: iop("dve", lambda e: e.tensor_tensor(out=H[:P, c0:c0 + w], in0=pm[:P, 0:w],
                                                                                         in1=ada[:P, 2048 + c0:2048 + c0 + w], op=ALU.mult),
                                                        reads=[pm.b, ada.b], writes=[H.b]), op=iop, dma=idma)
            _resid_ln(kb, X, XB[t], t, P, H, SMC, ST6C, PRM, o_l1g, o_l1b, EPSB, op=iop)
            return items

        cur[0] = None
        if inter:
            run_items(inter)
        pendC[0] = make_C(t, P, Y, H, HT, ada)
        if ti == len(tiles) - 1:
            run_items(pendC[0])
            pendC[0] = None
            if conv:
                run_items(conv)


def _resid_ln(kb, X, xb, t, P, Z, SM, ST6, PRM, og, ob, EPSB, op=None, gain_eng="pool"):
    op = kb.op if op is None else op
    xt = X[:P, t, :]
    op("dve", lambda e: e.scalar_tensor_tensor(out=Z[:P, :], in0=xt, scalar=ALPHA, in1=Z[:P, :], op0=ALU.mult, op1=ALU.add),
       reads=[xb, Z.b], writes=[Z.b])
    op("dve", lambda e: e.bn_stats(out=ST6[:P, 0, :], in_=Z[:P, 0:512]), reads=[Z.b], writes=[ST6.b])
    op("dve", lambda e: e.bn_stats(out=ST6[:P, 1, :], in_=Z[:P, 512:1024]), reads=[Z.b], writes=[ST6.b])
    op("dve", lambda e: e.bn_aggr(out=SM[:P, 41:43], in_=ST6[:P, :, :].rearrange("p a b -> p (a b)")), reads=[ST6.b], writes=[SM.b])
    op("act", lambda e: e.activation(out=SM[:P, 43:44], in_=SM[:P, 42:43], func=AF.Ln, bias=EPSB[:P, 0:1], scale=1.0), reads=[SM.b, EPSB.b], writes=[SM.b])
    op("act", lambda e: e.activation(out=SM[:P, 44:45], in_=SM[:P, 43:44], func=AF.Exp, scale=-0.5), reads=[SM.b], writes=[SM.b])
    op("dve", lambda e: e.tensor_scalar(out=Z[:P, :], in0=Z[:P, :], scalar1=SM[:P, 41:42], scalar2=SM[:P, 44:45],
                                        op0=ALU.subtract, op1=ALU.mult), reads=[Z.b, SM.b], writes=[Z.b])
    op(gain_eng, lambda e: e.tensor_tensor(out=Z[:P, :], in0=Z[:P, :], in1=PRM[:P, og:og + D], op=ALU.mult), reads=[Z.b, PRM.b], writes=[Z.b])
    op("dve", lambda e: e.tensor_tensor(out=xt, in0=Z[:P, :], in1=PRM[:P, ob:ob + D], op=ALU.add), reads=[Z.b, PRM.b], writes=[xb])


def _phase2(nc, kb, l, E):
    C, CST, X, XB = E["C"], E["CST"], E["X"], E["XB"]
    ADA = Tn(kb, "ADA2", [128, 3072]); ADAs = ADA
    WCH = [Tn(kb, "WCHb%d" % i, [128, 8, 128], dma=True) for i in range(2)]
    badac = [Tn(kb, "badab%d" % i, [128, 256], dma=True) for i in range(2)]
    Hs = [Tn(kb, "H2_%d" % i, [128, D], dma=True) for i in range(2)]
    HT = Tn(kb, "H2T", [128, 8, 128])
    PRM = Tn(kb, "PRM2", [128, 2 * D], dma=True)
    KTS = Tn(kb, "KTS", [128, 16, 128], dma=True)
    S0 = Tn(kb, "S0", [128, 2048]); S1_ = Tn(kb, "S1", [128, 2048]); S2 = Tn(kb, "S2", [128, 256])
    TOPS = Tn(kb, "TOPS", [128, 16, 16]); IDXU = Tn(kb, "IDXU", [128, 16, 16], U32); IDXF = Tn(kb, "IDXF", [128, 16, 16])
    CV = Tn(kb, "CV", [128, 8, 16]); CPOS = Tn(kb, "CPOS", [128, 8, 16], U32)
    PAU = Tn(kb, "PAU", [128, 8, 16], U32); PBU = Tn(kb, "PBU", [128, 8, 16], U32)
    PAF = Tn(kb, "PAF", [128, 8, 16]); PBF = Tn(kb, "PBF", [128, 8, 16])
    I1 = Tn(kb, "I1", [128, 128]); I2 = Tn(kb, "I2", [128, 128])
    IDXs = [Tn(kb, "IDX%d" % i, [128, 128], I32) for i in range(2)]
    GATEs = [Tn(kb, "GATE%d" % i, [128, 128]) for i in range(2)]
    ACTV = Tn(kb, "ACTV", [128, 128]); COEF = Tn(kb, "COEF", [128, 128])
    SMf = Tn(kb, "SM2f", [128, 16]); SM = Tn(kb, "SM2", [128, 64]); ST6 = Tn(kb, "ST62", [128, 2, 6])
    NB = NBUF
    UB = [Tn(kb, "UB%d" % i, [128, 2 * D], BF16, dma=True) for i in range(NB)]
    WCAB = Tn(kb, "WCAB", [128, 8, 256], dma=True)
    ACTB = [kb.buf("actv%d" % i) for i in range(NB)]
    COEFB = [kb.buf("coef%d" % i) for i in range(NB)]
    COEF2 = Tn(kb, "COEF2", [128, 128])

    class _View:
        def __init__(self, ap, b):
            self.t = ap
            self.b = b

        def __getitem__(self, k):
            return self.t[k]
    WCA = [WCAB]
    PT = [Tn(kb, "PTb%d" % i, [128, 512], psum=True) for i in range(2)]
    PQ = [Tn(kb, "PQ%d" % i, [128, 512], psum=True) for i in range(2)]
    PM = PQ
    ACCP = [Tn(kb, "ACCP%d" % i, [128, 512], psum=True) for i in range(2)]
    JUNK = Tn(kb, "JUNK", [128, D])
    ACC = JUNK
    DGB = [Tn(kb, "DGB%d" % i, [128, 128], BF16) for i in range(3)]
    PS = [Tn(kb, "PS%d" % i, [128, 512], psum=True) for i in range(2)]

    kb.dma("sp", lambda q: q.dma_start(out=PRM[:, 0:D], in_=E["ln2g"][l]), PRM.b, writes=[PRM.b])
    kb.dma("sp", lambda q: q.dma_start(out=PRM[:, D:2 * D], in_=E["ln2b"][l]), PRM.b, writes=[PRM.b])
    kb.dma("sp", lambda q: q.dma_start(out=KTS[:, :, :], in_=E["keysT"][l]), KTS.b, writes=[KTS.b])
    wpq_v = E["w_pq"][l]

    tab = E["puvb"][l]
    PUVB = E["PUVBT"][l]
    wctr = [0]

    def make_front(t, sset):
        items = []

        def op(e, fn, reads=(), writes=()):
            items.append(("op", e, _bind(fn), None, tuple(reads), tuple(writes)))

        def dma(e, fn, owner, reads=(), writes=()):
            items.append(("dma", e, _bind(fn), owner, tuple(reads), tuple(writes)))
        is_s = (t == NT)
        P = SP if is_s else 128
        ada = ADAs if is_s else ADA
        H = Hs[sset]; IDX = IDXs[sset]; GATE = GATEs[sset]
        QT = S0
        xt = X[:P, t, :]
        op("dve", lambda e: e.tensor_tensor(out=H[:P, :], in0=xt, in1=ada[:P, 1024:2048], op=ALU.mult), reads=[XB[t], ada.b], writes=[H.b])
        op("dve", lambda e: e.tensor_tensor(out=H[:P, :], in0=H[:P, :], in1=ada[:P, 0:1024], op=ALU.add), reads=[H.b, ada.b], writes=[H.b])
        _transpose8(kb, E, H, HT, PT, P, op=op)
        def wdma(c):
            i = c % 2
            dma("sp", lambda q: q.dma_start(out=WCH[i][:, :, :], in_=wpq_v[c].rearrange("p (k j) -> p k j", k=8)), WCH[i].b, writes=[WCH[i].b])
        wdma(0)
        for c in range(16):
            i = c % 2
            j = c % 2
            if c + 1 < 16:
                wdma(c + 1)
            for k in range(8):
                op("pe", lambda e: e.matmul(PQ[j][:, 0:P], lhsT=WCH[i][:, k, :], rhs=HT[:, k, :P], start=(k == 0), stop=(k == 7)),
                   reads=[WCH[i].b, HT.b], writes=[PQ[j].b] if k in (0, 7) else [])
            op("act", lambda e: e.copy(out=QT[:, c * 128:c * 128 + P], in_=PQ[j][:, 0:P]), reads=[PQ[j].b], writes=[QT.b])
        for c4 in range(4):
            ps = PS[c4 % 2]
            for j in range(4):
                c = c4 * 4 + j
                op("pe", lambda e: e.matmul(ps[:P, j * 128:(j + 1) * 128], lhsT=QT[:, c * 128:c * 128 + P], rhs=KTS[:, c, :], start=True, stop=True),
                   reads=[QT.b, KTS.b], writes=[ps.b])
            op("act", lambda e: e.copy(out=S1_[:P, c4 * 512:(c4 + 1) * 512], in_=ps[:P, :]), reads=[ps.b], writes=[S1_.b])
        for c in range(16):
            sc = S1_[:P, c * 128:(c + 1) * 128]
            wk = S2[:P, 0:128]
            op("dve", lambda e: e.max(out=TOPS[:P, c, 0:8], in_=sc), reads=[S1_.b], writes=[TOPS.b])
            op("dve", lambda e: e.max_index(out=IDXU[:P, c, 0:8], in_max=TOPS[:P, c, 0:8], in_values=sc), reads=[S1_.b, TOPS.b], writes=[IDXU.b])
            op("dve", lambda e: e.match_replace(out=wk, in_to_replace=TOPS[:P, c, 0:8], in_values=sc, imm_value=NEG), reads=[S1_.b, TOPS.b], writes=[S2.b])
            op("dve", lambda e: e.max(out=TOPS[:P, c, 8:16], in_=wk), reads=[S2.b], writes=[TOPS.b])
            op("dve", lambda e: e.max_index(out=IDXU[:P, c, 8:16], in_max=TOPS[:P, c, 8:16], in_values=wk), reads=[S2.b, TOPS.b], writes=[IDXU.b])
        op("dve", lambda e: e.tensor_copy(out=IDXF[:P, :, :], in_=IDXU[:P, :, :]), reads=[IDXU.b], writes=[IDXF.b])
        tv = TOPS[:P, :, :].rearrange("p (h two) k -> p h two k", two=2)
        CAND = S0
        op("dve", lambda e: e.tensor_tensor(out=CAND[:P, :].rearrange("p (h a b) -> p h a b", h=8, a=16),
                                            in0=tv[:, :, 0, :].unsqueeze(3).broadcast_to([P, 8, 16, 16]),
                                            in1=tv[:, :, 1, :].unsqueeze(2).broadcast_to([P, 8, 16, 16]), op=ALU.add),
           reads=[TOPS.b], writes=[S0.b])
        for h in range(8):
            cd = CAND[:P, h * 256:(h + 1) * 256]
            wk = S2[:P, 0:256]
            op("dve", lambda e: e.max(out=CV[:P, h, 0:8], in_=cd), reads=[S0.b], writes=[CV.b])
            op("dve", lambda e: e.max_index(out=CPOS[:P, h, 0:8], in_max=CV[:P, h, 0:8], in_values=cd), reads=[S0.b, CV.b], writes=[CPOS.b])
            op("dve", lambda e: e.match_replace(out=wk, in_to_replace=CV[:P, h, 0:8], in_values=cd, imm_value=NEG), reads=[S0.b, CV.b], writes=[S2.b])
            op("dve", lambda e: e.max(out=CV[:P, h, 8:16], in_=wk), reads=[S2.b], writes=[CV.b])
            op("dve", lambda e: e.max_index(out=CPOS[:P, h, 8:16], in_max=CV[:P, h, 8:16], in_values=wk), reads=[S2.b, CV.b], writes=[CPOS.b])
        op("dve", lambda e: e.tensor_single_scalar(out=PAU[:P, :, :], in_=CPOS[:P, :, :], scalar=4, op=ALU.logical_shift_right), reads=[CPOS.b], writes=[PAU.b])
        op("dve", lambda e: e.tensor_single_scalar(out=PBU[:P, :, :], in_=CPOS[:P, :, :], scalar=15, op=ALU.bitwise_and), reads=[CPOS.b], writes=[PBU.b])
        op("dve", lambda e: e.tensor_copy(out=PAF[:P, :, :], in_=PAU[:P, :, :]), reads=[PAU.b], writes=[PAF.b])
        op("dve", lambda e: e.tensor_copy(out=PBF[:P, :, :], in_=PBU[:P, :, :]), reads=[PBU.b], writes=[PBF.b])
        iv = IDXF[:P, :, :].rearrange("p (h two) k -> p h two k", two=2)
        io16 = C("iota16", P).unsqueeze(1).unsqueeze(1).broadcast_to([P, 8, 16, 16])
        for (pf, half, dst) in [(PAF, 0, I1), (PBF, 1, I2)]:
            eq = S1_[:P, :].rearrange("p (h k a) -> p h k a", h=8, k=16)
            op("dve", lambda e: e.tensor_tensor(out=eq, in0=pf[:P, :, :].unsqueeze(3).broadcast_to([P, 8, 16, 16]), in1=io16, op=ALU.is_equal),
               reads=[pf.b, CST.b], writes=[S1_.b])
            op("dve", lambda e: e.tensor_tensor(out=eq, in0=eq, in1=iv[:, :, half, :].unsqueeze(2).broadcast_to([P, 8, 16, 16]), op=ALU.mult),
               reads=[S1_.b, IDXF.b], writes=[S1_.b])
            op("dve", lambda e: e.tensor_reduce(out=dst[:P, :].rearrange("p (h k) -> p h k", h=8), in_=eq, axis=AX.X, op=ALU.add),
               reads=[S1_.b], writes=[dst.b])
        op("dve", lambda e: e.scalar_tensor_tensor(out=I1[:P, :], in0=I1[:P, :], scalar=128.0, in1=I2[:P, :], op0=ALU.mult, op1=ALU.add),
           reads=[I1.b, I2.b], writes=[I1.b])
        op("dve", lambda e: e.tensor_copy(out=IDX[:P, :], in_=I1[:P, :]), reads=[I1.b], writes=[IDX.b])
        gv = GATE[:P, :].rearrange("p (h k) -> p h k", h=8)
        op("dve", lambda e: e.tensor_tensor(out=gv, in0=CV[:P, :, :], in1=CV[:P, :, 0:1].broadcast_to([P, 8, 16]), op=ALU.subtract),
           reads=[CV.b], writes=[GATE.b])
        op("act", lambda e: e.activation(out=GATE[:P, :], in_=GATE[:P, :], func=AF.Exp), reads=[GATE.b], writes=[GATE.b])
        op("dve", lambda e: e.tensor_reduce(out=SMf[:P, 0:8], in_=gv, axis=AX.X, op=ALU.add), reads=[GATE.b], writes=[SMf.b])
        op("dve", lambda e: e.reciprocal(out=SMf[:P, 8:16], in_=SMf[:P, 0:8]), reads=[SMf.b], writes=[SMf.b])
        op("dve", lambda e: e.tensor_tensor(out=gv, in0=gv, in1=SMf[:P, 8:16].unsqueeze(2).broadcast_to([P, 8, 16]), op=ALU.mult),
           reads=[GATE.b, SMf.b], writes=[GATE.b])
        return items

    def run_items(items, n=None):
        n = len(items) if n is None else min(n, len(items))
        for _ in range(n):
            kind, e, fn, owner, reads, writes = items.pop(0)
            if kind == "op":
                kb.op(e, fn, reads=reads, writes=writes, bound=True)
            else:
                kb.dma(e, fn, owner, reads=reads, writes=writes, bound=True)

    def back(t, sset, nxt):
        op = kb.op
        is_s = (t == NT)
        P = SP if is_s else 128
        ada = ADAs if is_s else ADA
        H = Hs[sset]; IDX = IDXs[sset]; GATE = GATEs[sset]
        per = 0 if not nxt else (len(nxt) + 119) // 120

        def axpy(s):
            b = s % NB
            dg = DGB[s % 3]
            op("act", lambda e: e.activation(out=COEF2[:P, s:s + 1], in_=COEF[:P, s:s + 1], func=AF.Identity, scale=GATE[:P, s:s + 1]),
               reads=[COEFB[b], GATE.b], writes=[COEF2.b])
            op("act", lambda e: e.activation(out=dg[:P, :P], in_=C("ident", P, P), func=AF.Identity, scale=COEF2[:P, s:s + 1]),
               reads=[COEF2.b, CST.b], writes=[dg.b])
            for hf in range(2):
                op("pe", lambda e: e.matmul(ACCP[hf][:P, :], lhsT=dg[:P, :P], rhs=UB[b][:P, D + hf * 512:D + (hf + 1) * 512], start=(s == 0), stop=(s == 127)),
                   reads=[dg.b, UB[b].b], writes=[ACCP[hf].b] if s in (0, 127) else [])

        for s_ in range(128):
            b = s_ % NB
            kb.dma("pool", lambda q: q.indirect_dma_start(out=UB[b][:P, :], out_offset=None, in_=tab,
                                                          in_offset=bass.IndirectOffsetOnAxis(ap=IDX[:P, s_:s_ + 1], axis=0)),
                   UB[b].b, reads=[IDX.b, PUVB], writes=[UB[b].b])
            op("dve", lambda e: e.scalar_tensor_tensor(out=JUNK[:P, :], in0=UB[b][:P, 0:D], scalar=1.0, in1=H[:P, :],
                                                       op0=ALU.mult, op1=ALU.mult, accum_out=ACTV[:P, s_:s_ + 1]),
               reads=[UB[b].b, H.b], writes=[JUNK.b, ACTB[b]])
            op("act", lambda e: e.activation(out=COEF[:P, s_:s_ + 1], in_=ACTV[:P, s_:s_ + 1], func=AF.Gelu), reads=[ACTB[b]], writes=[COEFB[b]])
            if s_ >= 1:
                axpy(s_ - 1)
            if nxt:
                run_items(nxt, per)
        axpy(127)
        if nxt:
            run_items(nxt)
        for hf in range(2):
            op("dve", lambda e: e.tensor_tensor(out=ACC[:P, hf * 512:(hf + 1) * 512], in0=ACCP[hf][:P, :], in1=ada[:P, 2048 + hf * 512:2048 + (hf + 1) * 512],
                                                op=ALU.mult), reads=[ACCP[hf].b, ada.b], writes=[ACC.b])
        _resid_ln(kb, X, XB[t], t, P, ACC, SM, ST6, PRM, 0, D, E["EPSB"], gain_eng="dve")

    _ada(nc, kb, l, E, ADA, 128, E["cp"], 3072, WCA, PM, Hs[0], HT, PT, badac)
    run_items(make_front(0, 0))
    for t in range(NT + 1):
        nxt = make_front(t + 1, (t + 1) % 2) if t + 1 < NT else None
        back(t, t % 2, nxt)
        if t + 1 == NT:
            _ada(nc, kb, l, E, ADAs, SP, E["cs"], 3072, WCA, PM, Hs[NT % 2], HT, PT, badac)
            run_items(make_front(NT, NT % 2))


_CACHE = {}


def _chunked(w, cw):
    L, K, n = w.shape
    nch = (n + cw - 1) // cw
    wp = np.zeros((L, K, nch * cw), np.float32)
    wp[:, :, :n] = w
    wp = wp.reshape(L, 8, 128, nch, cw).transpose(0, 3, 2, 1, 4)
    return np.ascontiguousarray(wp.reshape(L, nch, 128, 8 * cw))


def _rep(a, P=128):
    return np.ascontiguousarray(np.broadcast_to(a[:, None, :], (a.shape[0], P, a.shape[1])))


def make_in_maps(inp, cpack):
    f = lambda a: np.ascontiguousarray(np.asarray(a, dtype=np.float32))
    shared = {
        "w_ada": _chunked(f(inp["w_ada"]), WCW), "b_ada": _rep(f(inp["b_ada"])), "w_in": _chunked(f(inp["w_in"]), WCW),
        "b_gate": _rep(f(inp["b_gate"])),
        "mh_g": _rep(f(inp["mh_g"])), "sgu_g": _rep(f(inp["sgu_g"])), "sgu_b": _rep(f(inp["sgu_b"])),
        "pscale": _rep(f(inp["pool_scale"])),
        "w_sT": f(np.asarray(inp["w_s"]).transpose(0, 3, 1, 2)),
        "b_sT": f(np.asarray(inp["b_s"]).transpose(0, 2, 1)),
        "w_pool": f(np.asarray(inp["w_pool"]).transpose(0, 2, 1, 3)),
        "w_o": _chunked(f(inp["w_o"]), WCW), "ln1g": _rep(f(inp["ln1_g"])), "ln1b": _rep(f(inp["ln1_b"])),
        "ln2g": _rep(f(inp["ln2_g"])), "ln2b": _rep(f(inp["ln2_b"])), "w_pq": _chunked(f(inp["w_pq"]), 128),
        "keysT": f(np.asarray(inp["peer_keys"]).transpose(0, 4, 1, 2, 3).reshape(DEPTH, 128, 16, 128)),
        "cst": cpack[0], "cst1": cpack[1],
    }
    ws4 = np.asarray(inp["w_s"])[:, :, :ST, :ST]
    wsS = np.repeat(np.repeat(ws4.transpose(0, 3, 1, 2), SB, axis=1), SB, axis=3)
    shared["w_sS"] = f(wsS)
    bs4 = np.asarray(inp["b_s"])[:, :, :ST]
    shared["b_sS"] = f(np.repeat(bs4.transpose(0, 2, 1), SB, axis=1))
    for l in range(DEPTH):
        shared["puv%d" % l] = np.ascontiguousarray(
            np.concatenate([np.asarray(inp["peer_u"])[l], np.asarray(inp["peer_v"])[l]], axis=1), dtype=np.float32)
    maps = []
    for c in range(NCORES):
        bs = slice(c * SB, (c + 1) * SB)
        m = dict(shared)
        m["xp"] = f(np.asarray(inp["x_prompt"])[c])
        m["xs"] = f(np.asarray(inp["x_sample"])[bs].transpose(1, 0, 2).reshape(SP, D))
        m["cp"] = f(np.broadcast_to(np.asarray(inp["c_prompt"])[c][None, :], (128, D)))
        m["cs"] = f(np.tile(np.asarray(inp["c_sample"])[bs], (ST, 1)))
        sCc = np.asarray(inp["state_mlstm_C"])[:, bs]
        m["sC"] = f(sCc.transpose(0, 2, 3, 1, 4))
        snc = np.asarray(inp["state_mlstm_n"])[:, bs]
        m["snat"] = f(snc)
        m["snT"] = f(snc.transpose(0, 2, 3, 1))
        m["sm"] = f(np.tile(np.asarray(inp["state_mlstm_m"])[:, bs], (1, ST, 1)))
        spc = np.asarray(inp["state_pool"])[:, bs].transpose(0, 2, 1, 3)
        m["spA"] = f(spc[:, 0:8].reshape(DEPTH, 128, 256))
        m["spB"] = f(spc[:, 8:15].reshape(DEPTH, 112, 256))
        maps.append(m)
    return maps


def gather_outputs(results):
    cat = lambda k, ax: np.concatenate([r[k] for r in results], axis=ax)
    yp = np.stack([r["yp"] for r in results], 0)
    ys = np.concatenate([r["ys"].reshape(ST, SB, D).transpose(1, 0, 2) for r in results], 0)
    pC = np.stack([r["pC"] for r in results], 1)
    pn = np.stack([r["pn"] for r in results], 1)
    pm = np.stack([r["pm"] for r in results], 1)
    pp = np.stack([r["pp"] for r in results], 1)
    return (yp, ys, pC, pn, pm, pp, cat("nC", 1), cat("nn", 1), cat("nm", 1), cat("npool", 1), cat("nv", 1))


def kernel(**inputs):
    if "prog" not in _CACHE:
        _CACHE["prog"] = build_program()
    nc, cpack = _CACHE["prog"]
    maps = make_in_maps(inputs, cpack)
    res = run_bass_kernel_spmd(nc, maps, core_ids=list(range(NCORES)))
    outs = gather_outputs(res.results)
    return tuple(np.ascontiguousarray(o, dtype=np.float32) for o in outs)
```
